# Optimizing a Trainium2 kernel written in Bass

```python
import jax, jax.numpy as jnp
from jax import lax
import numpy as np

D_MODEL = 1024
BATCH = 8
SEQ = 8192
DEPTH = 1

MLSTM_HEADS = 4
MLSTM_HEAD_DIM = 128
MLSTM_WIDTH = MLSTM_HEADS * MLSTM_HEAD_DIM
MLSTM_CHUNK = 128
N_GATES = 4 * MLSTM_HEADS
MLA_HEADS = 4
QK_NOPE = 128
QK_ROPE = 64
V_HEAD = 128
Q_LORA = 256
KV_LORA = 128
MLA_WIDTH = MLA_HEADS * V_HEAD
Q_BLOCK = 128
ROPE_THETA = 10000.0
D_MIX = MLSTM_WIDTH + MLA_WIDTH
OFF_Q = 0
OFF_K = OFF_Q + MLSTM_WIDTH
OFF_V = OFF_K + MLSTM_WIDTH
OFF_O = OFF_V + MLSTM_WIDTH
OFF_G = OFF_O + MLSTM_WIDTH
OFF_CQ = OFF_G + N_GATES
OFF_CKV = OFF_CQ + Q_LORA
OFF_KR = OFF_CKV + KV_LORA
IN_COLS = OFF_KR + QK_ROPE
D_FF = 2816
CONV_WIDTH = 3
PLE_DIM = 256
EPS = 1e-6

kernel_name = 'hybrid_mlstm_mla_convffn_block'


def rms_norm(x, g):
    xf = x.astype(jnp.float32)
    y = xf * lax.rsqrt(jnp.mean(xf * xf, axis=-1, keepdims=True) + EPS)
    return (y * g.astype(jnp.float32)).astype(x.dtype)


def dwconv3_centred(x, w, b):
    xp = jnp.pad(x, ((0, 0), (1, 1), (0, 0)))
    return xp[:, :-2] * w[0] + xp[:, 1:-1] * w[1] + xp[:, 2:] * w[2] + b


def apply_rope(x, cos, sin):
    xf = x.astype(jnp.float32)
    x1, x2 = jnp.split(xf, 2, axis=-1)
    return jnp.concatenate([x1 * cos - x2 * sin, x2 * cos + x1 * sin], axis=-1).astype(x.dtype)


def mlstm_chunkwise(q, k, v, i_pre, f_pre):
    b, nh, s, dh = q.shape
    L = MLSTM_CHUNK
    nc = s // L
    qc = q.reshape(b, nh, nc, L, dh)
    kc = k.reshape(b, nh, nc, L, dh)
    vc = v.reshape(b, nh, nc, L, dh)
    log_f = jax.nn.log_sigmoid(f_pre).reshape(b, nh, nc, L)
    log_i = i_pre.reshape(b, nh, nc, L)
    g = jnp.cumsum(log_f, axis=-1)
    g_last = g[..., -1]
    a = g_last[..., None] - g + log_i
    m_loc = jnp.max(a, axis=-1)
    w = jnp.exp(a - m_loc[..., None])
    c_loc = jnp.einsum('bhcsd,bhcse->bhcde', vc * w[..., None], kc)
    n_loc = jnp.einsum('bhcs,bhcse->bhce', w, kc)

    def step(carry, inp):
        c_st, n_st, m_st = carry
        gl, ml, cl, nl = inp
        m_new = jnp.maximum(gl + m_st, ml)
        s_old = jnp.exp(gl + m_st - m_new)
        s_new = jnp.exp(ml - m_new)
        c_next = s_old[..., None, None] * c_st + s_new[..., None, None] * cl
        n_next = s_old[..., None] * n_st + s_new[..., None] * nl
        return (c_next, n_next, m_new), (c_st, n_st, m_st)

    init = (jnp.zeros((b, nh, dh, dh), q.dtype), jnp.zeros((b, nh, dh), q.dtype),
            jnp.zeros((b, nh), q.dtype))
    xs = (jnp.moveaxis(g_last, 2, 0), jnp.moveaxis(m_loc, 2, 0),
          jnp.moveaxis(c_loc, 2, 0), jnp.moveaxis(n_loc, 2, 0))
    _, (c_prev, n_prev, m_prev) = lax.scan(step, init, xs)
    c_prev = jnp.moveaxis(c_prev, 0, 2)
    n_prev = jnp.moveaxis(n_prev, 0, 2)
    m_prev = jnp.moveaxis(m_prev, 0, 2)

    lower_tri = jnp.tril(jnp.ones((L, L), dtype=bool))
    d = jnp.where(lower_tri, g[..., :, None] - g[..., None, :] + log_i[..., None, :], -jnp.inf)
    b_inter = g + m_prev[..., None]
    m_t = jnp.maximum(jnp.max(d, axis=-1), b_inter)
    scores = jnp.einsum('bhctd,bhcsd->bhcts', qc, kc) * jnp.exp(d - m_t[..., None])
    inter = jnp.exp(b_inter - m_t)
    num = (jnp.einsum('bhcts,bhcsd->bhctd', scores, vc)
           + inter[..., None] * jnp.einsum('bhcde,bhcte->bhctd', c_prev, qc))
    den = jnp.sum(scores, axis=-1) + inter * jnp.einsum('bhce,bhcte->bhct', n_prev, qc)
    out = num / jnp.maximum(jnp.abs(den), jnp.exp(-m_t))[..., None]
    return out.reshape(b, nh, s, dh)


def mlstm_group(q_pre, k_pre, v_pre, o_pre, gates, conv_w, conv_b, norm_g):
    bsz, s, _ = v_pre.shape
    qk = jax.nn.silu(dwconv3_centred(jnp.concatenate([q_pre, k_pre], axis=-1), conv_w, conv_b))
    q, k = jnp.split(qk, 2, axis=-1)

    def heads(t):
        return t.reshape(bsz, s, MLSTM_HEADS, MLSTM_HEAD_DIM).transpose(0, 2, 1, 3).astype(jnp.float32)

    q = heads(q)
    k = heads(k) * (MLSTM_HEAD_DIM ** -0.5)
    v = heads(v_pre)
    gt = gates.astype(jnp.float32).reshape(bsz, s, 4, MLSTM_HEADS).transpose(2, 0, 3, 1)
    h_fwd = mlstm_chunkwise(q, k, v, gt[0], gt[1])
    flip = lambda t: jnp.flip(t, axis=2)
    h_bwd = flip(mlstm_chunkwise(flip(q), flip(k), flip(v), flip(gt[2]), flip(gt[3])))
    h = (h_fwd + h_bwd).transpose(0, 2, 1, 3)
    h = rms_norm(h, norm_g.reshape(MLSTM_HEADS, MLSTM_HEAD_DIM))
    h = h.reshape(bsz, s, MLSTM_WIDTH).astype(v_pre.dtype)
    return jax.nn.sigmoid(o_pre) * h


def mla_group(c_q, c_kv, k_rope_pre, q_norm_g, w_uq, kv_norm_g, w_ukv, cos, sin):
    bsz, s, _ = c_q.shape
    q = (rms_norm(c_q, q_norm_g) @ w_uq).reshape(bsz, s, MLA_HEADS, QK_NOPE + QK_ROPE)
    q_nope, q_rope = q[..., :QK_NOPE], q[..., QK_NOPE:]
    q_rope = apply_rope(q_rope, cos[:, None, :], sin[:, None, :])
    kv = (rms_norm(c_kv, kv_norm_g) @ w_ukv).reshape(bsz, s, MLA_HEADS, QK_NOPE + V_HEAD)
    k_nope, v = kv[..., :QK_NOPE], kv[..., QK_NOPE:]
    k_rope = apply_rope(k_rope_pre, cos, sin)
    scale = (QK_NOPE + QK_ROPE) ** -0.5
    nb = s // Q_BLOCK

    def to_blocks(t):
        return jnp.moveaxis(t.reshape(bsz, nb, Q_BLOCK, *t.shape[2:]), 1, 0)

    def attend(blk):
        qn, qr = blk
        sc = (jnp.einsum('bqhd,bkhd->bhqk', qn, k_nope)
              + jnp.einsum('bqhr,bkr->bhqk', qr, k_rope))
        pr = jax.nn.softmax(sc.astype(jnp.float32) * scale, axis=-1)
        return jnp.einsum('bhqk,bkhd->bqhd', pr.astype(v.dtype), v)

    o = lax.map(attend, (to_blocks(q_nope), to_blocks(q_rope)))
    return jnp.moveaxis(o, 0, 1).reshape(bsz, s, MLA_WIDTH)


def conv_gated_mlp(xn, w_up, conv_w, conv_b, w_down):
    u = dwconv3_centred(xn @ w_up, conv_w, conv_b)
    gate, val = jnp.split(u, 2, axis=-1)
    return (jax.nn.silu(gate) * val) @ w_down


def setup_inputs(seed: int = 0) -> dict:
    key = jax.random.key(seed)
    ks = jax.random.split(key, 24)
    f32 = jnp.float32

    def nrm(k, shape, scale):
        return jax.random.normal(k, shape, f32) * scale

    def gain(k, shape):
        return 1.0 + 0.02 * jax.random.normal(k, shape, f32)

    gate_i = nrm(ks[4], (DEPTH, 2, MLSTM_HEADS), 0.1)
    gate_f = jnp.linspace(3.0, 6.0, MLSTM_HEADS, dtype=f32) + nrm(ks[5], (DEPTH, 2, MLSTM_HEADS), 0.1)
    b_gates = jnp.stack([gate_i, gate_f], axis=2).reshape(DEPTH, N_GATES)
    return {
        'x': nrm(ks[0], (BATCH, SEQ, D_MODEL), 1.0),
        'p': nrm(ks[1], (DEPTH, BATCH, SEQ, PLE_DIM), 1.0),
        'ln_mix_g': gain(ks[2], (DEPTH, D_MODEL)),
        'w_in': nrm(ks[3], (DEPTH, D_MODEL, IN_COLS), D_MODEL ** -0.5),
        'b_gates': b_gates,
        'conv_qk_w': nrm(ks[6], (DEPTH, CONV_WIDTH, 2 * MLSTM_WIDTH), CONV_WIDTH ** -0.5),
        'conv_qk_b': nrm(ks[7], (DEPTH, 2 * MLSTM_WIDTH), 0.02),
        'mlstm_norm_g': gain(ks[8], (DEPTH, MLSTM_WIDTH)),
        'q_norm_g': gain(ks[9], (DEPTH, Q_LORA)),
        'w_uq': nrm(ks[10], (DEPTH, Q_LORA, MLA_HEADS * (QK_NOPE + QK_ROPE)), Q_LORA ** -0.5),
        'kv_norm_g': gain(ks[11], (DEPTH, KV_LORA)),
        'w_ukv': nrm(ks[12], (DEPTH, KV_LORA, MLA_HEADS * (QK_NOPE + V_HEAD)), KV_LORA ** -0.5),
        'w_out': nrm(ks[13], (DEPTH, D_MIX, D_MODEL), D_MIX ** -0.5),
        'ln_ffn_g': gain(ks[14], (DEPTH, D_MODEL)),
        'w_up': nrm(ks[15], (DEPTH, D_MODEL, 2 * D_FF), D_MODEL ** -0.5),
        'conv_ffn_w': nrm(ks[16], (DEPTH, CONV_WIDTH, 2 * D_FF), CONV_WIDTH ** -0.5),
        'conv_ffn_b': nrm(ks[17], (DEPTH, 2 * D_FF), 0.02),
        'w_down': nrm(ks[18], (DEPTH, D_FF, D_MODEL), D_FF ** -0.5),
        'ple_norm_g': gain(ks[19], (DEPTH, D_MODEL)),
        'w_ple_gate': nrm(ks[20], (DEPTH, D_MODEL, D_MODEL), D_MODEL ** -0.5),
        'w_ple_proj': nrm(ks[21], (DEPTH, PLE_DIM, D_MODEL), PLE_DIM ** -0.5),
        'ple_post_g': gain(ks[22], (DEPTH, D_MODEL)),
        'final_g': gain(ks[23], (D_MODEL,)),
    }


def reference(x, p, ln_mix_g, w_in, b_gates, conv_qk_w, conv_qk_b, mlstm_norm_g,
              q_norm_g, w_uq, kv_norm_g, w_ukv, w_out, ln_ffn_g, w_up, conv_ffn_w,
              conv_ffn_b, w_down, ple_norm_g, w_ple_gate, w_ple_proj, ple_post_g, final_g):
    s = x.shape[1]
    pos = jnp.arange(s, dtype=jnp.float32)
    inv_freq = ROPE_THETA ** (-jnp.arange(0, QK_ROPE, 2, dtype=jnp.float32) / QK_ROPE)
    ang = pos[:, None] * inv_freq[None, :]
    cos, sin = jnp.cos(ang), jnp.sin(ang)
    h = x
    for l in range(DEPTH):
        xn = rms_norm(h, ln_mix_g[l])
        proj = xn @ w_in[l]
        y_a = mlstm_group(proj[..., OFF_Q:OFF_K], proj[..., OFF_K:OFF_V],
                          proj[..., OFF_V:OFF_O], proj[..., OFF_O:OFF_G],
                          proj[..., OFF_G:OFF_CQ] + b_gates[l],
                          conv_qk_w[l], conv_qk_b[l], mlstm_norm_g[l])
        y_b = mla_group(proj[..., OFF_CQ:OFF_CKV], proj[..., OFF_CKV:OFF_KR],
                        proj[..., OFF_KR:IN_COLS], q_norm_g[l], w_uq[l],
                        kv_norm_g[l], w_ukv[l], cos, sin)
        h = h + jnp.concatenate([y_a, y_b], axis=-1) @ w_out[l]
        h = h + conv_gated_mlp(rms_norm(h, ln_ffn_g[l]), w_up[l], conv_ffn_w[l],
                               conv_ffn_b[l], w_down[l])
        gate = jax.nn.sigmoid(rms_norm(h, ple_norm_g[l]) @ w_ple_gate[l])
        h = h + gate * rms_norm(p[l] @ w_ple_proj[l], ple_post_g[l])
    return rms_norm(h, final_g)
```

```python
import contextlib
import numpy as np
import concourse.bass as bass
import concourse.mybir as mybir
from concourse.bass_utils import run_bass_kernel_spmd

F32 = mybir.dt.float32
BF16 = mybir.dt.bfloat16
AF = mybir.ActivationFunctionType
ALU = mybir.AluOpType
AX = mybir.AxisListType

D = 1024
NH = 4
IN_COLS = 2512
D_FF = 2816
EPS = 1e-6
SEM_MAX = 30000


class Buf:
    __slots__ = ("name", "lw", "rd")

    def __init__(self, name=""):
        self.name = name
        self.lw = None
        self.rd = []


class Chan:
    __slots__ = ("sem", "count", "last")

    def __init__(self, sem):
        self.sem = sem
        self.count = 0
        self.last = None


class Op:
    __slots__ = ("eng", "fn", "deps", "sig", "signo", "chan", "cval", "done")


class Sched:
    ENGS = ("pe", "act", "dve", "pool", "sp")

    def __init__(self, nc, stack):
        self.nc = nc
        self.stack = stack
        self.ops = []
        self.last_on = {e: None for e in self.ENGS}
        self.pending_bar = {e: [] for e in self.ENGS}
        self.chans = []
        self.cnt = {e: 0 for e in self.ENGS}
        self.sems = {e: [] for e in self.ENGS}
        self.waited = {e: {} for e in self.ENGS}

    def chan(self, name):
        c = Chan(self.stack.enter_context(self.nc.semaphore(name)))
        self.chans.append(c)
        return c

    def add(self, eng, fn, r=(), w=(), chan=None):
        op = Op()
        op.eng, op.fn, op.deps, op.sig, op.signo, op.chan, op.cval = eng, fn, {}, False, 0, chan, 0
        op.done = False
        for b in r:
            if b.lw is not None:
                op.deps[b.lw] = True
        for b in w:
            if b.lw is not None:
                op.deps.setdefault(b.lw, False)
            for q in b.rd:
                op.deps.setdefault(q, False)
        for b in r:
            b.rd.append(op)
        for b in w:
            b.lw = op
            b.rd = []
        if self.pending_bar[eng]:
            for d in self.pending_bar[eng]:
                op.deps[d] = True
            self.pending_bar[eng] = []
        if chan is not None:
            if chan.last is not None:
                op.deps[chan.last] = True
            chan.count += 16
            op.cval = chan.count
            chan.last = op
        op.deps.pop(op, None)
        self.ops.append(op)
        self.last_on[eng] = op
        return op

    def barrier(self):
        lasts = [o for o in self.last_on.values() if o is not None]
        lasts += [c.last for c in self.chans if c.last is not None]
        for e in self.ENGS:
            self.pending_bar[e] = list(lasts)

    def emit(self):
        nc = self.nc
        fin = self.add("sp", lambda e: e.nop())
        for c in self.chans:
            if c.last is not None and not c.last.done:
                fin.deps[c.last] = True
        for e in self.ENGS:
            self.pending_bar[e] = []
        for op in self.ops:
            for d in [d for d in op.deps if d.done]:
                del op.deps[d]
            for d, raw in op.deps.items():
                if d.chan is not None:
                    continue
                if d.eng == op.eng and (op.eng == "pe" or not raw):
                    continue
                d.sig = True
        cnt = self.cnt
        for op in self.ops:
            if op.chan is None and op.sig:
                cnt[op.eng] += 1
                op.signo = cnt[op.eng]
        sems = self.sems
        for e in self.ENGS:
            n = cnt[e] // SEM_MAX + 1
            while len(sems[e]) < n:
                sems[e].append(self.stack.enter_context(nc.semaphore(f"s_{e}{len(sems[e])}")))
        per = {e: [o for o in self.ops if o.eng == e] for e in self.ENGS}
        handles = {"pe": "tensor", "act": "scalar", "dve": "vector", "pool": "gpsimd", "sp": "sync"}

        def run(e, eng):
            waited = self.waited[e]
            for op in per[e]:
                for d, raw in op.deps.items():
                    if d.chan is not None:
                        key, val, sem = ("c", id(d.chan)), d.cval, d.chan.sem
                    else:
                        if d.eng == e and (e == "pe" or not raw):
                            continue
                        j = (d.signo - 1) // SEM_MAX
                        key, val, sem = (d.eng, j), d.signo - j * SEM_MAX, sems[d.eng][j]
                    if waited.get(key, 0) >= val:
                        continue
                    waited[key] = val
                    eng.wait_ge(sem, val)
                ins = op.fn(eng)
                if op.chan is not None:
                    ins.then_inc(op.chan.sem, 16)
                elif op.sig:
                    j = (op.signo - 1) // SEM_MAX
                    ins.then_inc(sems[e][j], 1)

        with nc.Block() as block:
            for e in self.ENGS:
                if per[e]:
                    getattr(block, handles[e])(lambda eng, e=e: run(e, eng))
        for op in self.ops:
            op.done = True
            op.fn = None
            op.deps = {}
        self.ops = []
        self.last_on = {e: None for e in self.ENGS}


def _act(out, in_, func, **kw):
    return lambda e: e.activation(out, in_, func, **kw)


def _ts(out, in0, s1, s2, op0, op1=None):
    if op1 is None:
        return lambda e: e.tensor_scalar(out, in0, s1, None, op0)
    return lambda e: e.tensor_scalar(out, in0, s1, s2, op0, op1)


def _stt(out, in0, sc, in1, op0, op1):
    return lambda e: e.scalar_tensor_tensor(out, in0, sc, in1, op0, op1)


def _tt(out, in0, in1, op):
    return lambda e: e.tensor_tensor(out, in0, in1, op)


def _cp(out, in_):
    return lambda e: e.tensor_copy(out, in_)


def _mm(out, lhsT, rhs, start, stop):
    return lambda e: e.matmul(out, lhsT, rhs, start=start, stop=stop)


def _tr(out, in_, ident):
    return lambda e: e.transpose(out, in_, ident)


def _dma(out, in_, **kw):
    return lambda e: e.dma_start(out=out, in_=in_, **kw)


def _memset(ap, v):
    return lambda e: e.memset(ap, v)


class K:
    def __init__(self, S, dbg=()):
        self.S = S
        self.dbg = dbg
        self.nc = bass.Bass("TRN2", target_bir_lowering=False)
        self.stack = contextlib.ExitStack()
        self.sc = Sched(self.nc, self.stack)

    def sb(self, st, name, shape, dt=F32):
        return st.enter_context(self.nc.sbuf_tensor(name, list(shape), dt))

    def ps(self, st, name, shape, dt=F32):
        return st.enter_context(self.nc.psum_tensor(name, list(shape), dt))

    def dram_in(self, name, shape, dt=F32):
        return self.nc.dram_tensor(name, list(shape), dt, kind="ExternalInput").ap()

    def dram_out(self, name, shape, dt=F32):
        return self.nc.dram_tensor(name, list(shape), dt, kind="ExternalOutput").ap()

    def dram_tmp(self, name, shape, dt=F32):
        if name in self.dbg:
            return self.nc.dram_tensor(name, list(shape), dt, kind="ExternalOutput").ap()
        return self.nc.dram_tensor(name, list(shape), dt).ap()


class Slot:
    def __init__(self, t, chan=None):
        self.t = t
        self.b = Buf()
        self.c = chan


def build(S, dbg=()):
    k = K(S, dbg)
    nc, sc = k.nc, k.sc
    NT, NB = S // 512, S // 128
    top = k.stack

    x = k.dram_in("x", [S, D])
    p_in = k.dram_in("p", [S, 256])
    ln_mix_g = k.dram_in("ln_mix_g", [D])
    w_in = k.dram_in("w_in", [D, IN_COLS])
    b_gates = k.dram_in("b_gates", [16])
    conv_qk_w = k.dram_in("conv_qk_w", [3, 1024])
    conv_qk_b = k.dram_in("conv_qk_b", [1024])
    mlstm_norm_g = k.dram_in("mlstm_norm_g", [512])
    q_norm_g = k.dram_in("q_norm_g", [256])
    w_uq = k.dram_in("w_uq", [256, 768])
    kv_norm_g = k.dram_in("kv_norm_g", [128])
    w_ukv = k.dram_in("w_ukv", [128, 1024])
    w_out = k.dram_in("w_out", [1024, 1024])
    ln_ffn_g = k.dram_in("ln_ffn_g", [D])
    w_up = k.dram_in("w_up", [D, 2 * D_FF])
    conv_ffn_w = k.dram_in("conv_ffn_w", [3, 2 * D_FF])
    conv_ffn_b = k.dram_in("conv_ffn_b", [2 * D_FF])
    w_down = k.dram_in("w_down", [D_FF, D])
    ple_norm_g = k.dram_in("ple_norm_g", [D])
    w_ple_gate = k.dram_in("w_ple_gate", [D, D])
    w_ple_proj = k.dram_in("w_ple_proj", [256, D])
    ple_post_g = k.dram_in("ple_post_g", [D])
    final_g = k.dram_in("final_g", [D])
    out = k.dram_out("out", [S, D])

    qkT = k.dram_tmp("qkT", [1024, S], BF16)
    vo_s = k.dram_tmp("vo_s", [S, 1024])
    g3_s = k.dram_tmp("g3_s", [S, 464])

    ident = k.sb(top, "ident", [128, 128], BF16)
    ones_f = k.sb(top, "ones_f", [128, 128], F32)
    mhalf = k.sb(top, "mhalf", [128, 16], F32)
    cb = Buf("consts")
    sc.add("pool", _memset(ones_f[:], 1.0), w=[cb])
    sc.add("pool", _memset(mhalf[:], -0.5), w=[cb])
    sc.add("pool", lambda e: e.affine_select(ident[:], ones_f[:], [[-1, 128]], ALU.is_equal, 0.0,
                                             base=0, channel_multiplier=1), r=[cb], w=[cb])
    sc.emit()

    ld_ch = [sc.chan(f"ldw{i}") for i in range(2)]
    cnt = {"w": 0, "e": 0}

    def alt():
        cnt["e"] += 1
        return "dve" if cnt["e"] % 2 else "act"

    def scale_cast(eng, out_ap, in_ap, sc_ap=None):
        if eng == "act":
            if sc_ap is None:
                return _act(out_ap, in_ap, AF.Copy)
            return _act(out_ap, in_ap, AF.Copy, scale=sc_ap)
        if sc_ap is None:
            return _cp(out_ap, in_ap)
        return _ts(out_ap, in_ap, sc_ap, None, ALU.mult)

    def load_cols(st, name, src, G):
        t = k.sb(st, name, [128, G])
        b = Buf(name)
        ch = sc.chan("c_" + name)
        sc.add("sp", _dma(t[:], src.rearrange("(g p) -> p g", p=128), allow_slow_non_contiguous=True),
               w=[b], chan=ch)
        return t, b

    def load_bcast(st, name, src, n):
        t = k.sb(st, name, [128, n])
        b = Buf(name)
        ch = sc.chan("c_" + name)
        sc.add("sp", _dma(t[:], bass.AP(src.tensor, src.offset, [[0, 128], [1, n]])), w=[b], chan=ch)
        return t, b

    def load_w(st, stage, name, src, Kdim, cols, gain=None):
        kcn = Kdim // 128
        t = k.sb(st, name, [128, kcn, cols], BF16)
        b = Buf(name)
        for kc in range(kcn):
            sw = stage[0].t.shape[1]
            for c0 in range(0, cols, sw):
                w = min(sw, cols - c0)
                s = stage[cnt["w"] % 2]
                cnt["w"] += 1
                sc.add("sp", _dma(s.t[:, 0:w], src[kc * 128:(kc + 1) * 128, c0:c0 + w]), w=[s.b], chan=s.c)
                g = None if gain is None else gain[0][:, kc:kc + 1]
                rr = [s.b] + ([] if gain is None else [gain[1]])
                e = alt()
                sc.add(e, scale_cast(e, t[:, kc, c0:c0 + w], s.t[:, 0:w], g), r=rr, w=[b])
        return t, b

    with contextlib.ExitStack() as st:
        stage = [Slot(k.sb(st, f"wstage{i}", [128, 2816]), ld_ch[i]) for i in range(2)]
        gmix = load_cols(st, "gmix", ln_mix_g, 8)
        Win, Winb = load_w(st, stage, "Win", w_in, D, IN_COLS, gmix)
        cw = k.sb(st, "cw", [128, 8, 3])
        cwb = Buf("cw")
        for tap in range(3):
            sc.add("sp", _dma(cw[:, :, tap], conv_qk_w[tap].rearrange("(g p) -> p g", p=128),
                              allow_slow_non_contiguous=True), w=[Buf()], chan=sc.chan(f"c_cw{tap}"))
        cbias = load_cols(st, "cbias", conv_qk_b, 8)
        bg = load_bcast(st, "bg", b_gates, 16)
        sc.barrier()

        XT = [Slot(k.sb(st, f"xt{i}", [128, 4, D]), sc.chan(f"c_xt{i}")) for i in range(2)]
        XB = Slot(k.sb(st, "xb", [128, 4, D], BF16))
        XN = [Slot(k.sb(st, f"xn{i}", [128, 8, 512], BF16)) for i in range(2)]
        junk = Slot(k.sb(st, "junk", [128, D], BF16))
        ss = Slot(k.sb(st, "ss", [128, 4]))
        ss2 = Slot(k.sb(st, "ss2", [128, 4]))
        rstd = Slot(k.sb(st, "rstd", [128, 4]))
        PRE = [Slot(k.sb(st, f"pre{g}", [128, 514])) for g in range(8)]
        ACC = [Slot(k.sb(st, f"acc{i}", [128, 512])) for i in range(2)]
        QKB = [Slot(k.sb(st, f"qkb{i}", [128, 512], BF16), sc.chan(f"c_qkb{i}")) for i in range(3)]
        VO = [Slot(k.sb(st, f"vo{i}", [128, 1024]), sc.chan(f"c_vo{i}")) for i in range(2)]
        G3 = [Slot(k.sb(st, f"g3{i}", [128, 464]), sc.chan(f"c_g3{i}")) for i in range(2)]
        TB = [Slot(k.ps(st, f"tb{i}", [128, 1024], BF16)) for i in range(2)]
        MB = [Slot(k.ps(st, f"mb{i}", [128, 512])) for i in range(6)]
        last8 = Slot(k.sb(st, "last8", [128, 8]))
        last8b = Slot(k.sb(st, "last8b", [128, 8], BF16), sc.chan("c_last8"))
        for g in range(8):
            sc.add("pool", _memset(PRE[g].t[:, 0:2], 0.0), w=[PRE[g].b])
        mbi = 0
        qi = 0
        voi = 0
        for i in range(NT):
            t0 = i * 512
            xt = XT[i % 2]
            xn = XN[i % 2]
            sc.add("sp", _dma(xt.t[:], x[t0:t0 + 512, :].rearrange("(b p) d -> p b d", p=128)), w=[xt.b], chan=xt.c)
            for b in range(4):
                sc.add("act", _act(junk.t[:], xt.t[:, b, :], AF.Square, scale=1.0 / 32.0, accum_out=ss.t[:, b:b + 1]),
                       r=[xt.b], w=[junk.b, ss.b])
            sc.add("dve", _ts(ss2.t[:], ss.t[:], EPS, None, ALU.add), r=[ss.b], w=[ss2.b])
            sc.add("pool", _tt(rstd.t[:], ss2.t[:], mhalf[:, 0:4], ALU.pow), r=[ss2.b], w=[rstd.b])
            for b in range(4):
                e = "dve" if b % 2 else "act"
                sc.add(e, scale_cast(e, XB.t[:, b, :], xt.t[:, b, :], rstd.t[:, b:b + 1]), r=[xt.b, rstd.b], w=[XB.b])
            for j in range(4):
                tb = TB[j % 2]
                for kk in range(2):
                    kc = 2 * j + kk
                    for b in range(4):
                        sc.add("pe", _tr(tb.t[:, kk * 512 + b * 128:kk * 512 + (b + 1) * 128],
                                         XB.t[:, b, kc * 128:(kc + 1) * 128], ident[:]), r=[XB.b], w=[tb.b])
                e = "dve" if j % 2 else "act"
                sc.add(e, scale_cast(e, xn.t[:, 2 * j:2 * j + 2, :], tb.t[:].rearrange("p (a b) -> p a b", a=2)),
                       r=[tb.b], w=[xn.b])
            for g in range(8):
                pm = MB[mbi % 6]
                mbi += 1
                for kc in range(8):
                    sc.add("pe", _mm(pm.t[:], Win[:, kc, g * 128:(g + 1) * 128], xn.t[:, kc, :], kc == 0, kc == 7),
                           r=[xn.b, Winb], w=[pm.b])
                pre = PRE[g]
                acc = ACC[g % 2]
                qb = QKB[qi % 3]
                qi += 1
                sc.add("act", _act(pre.t[:, 2:514], pm.t[:], AF.Copy), r=[pm.b], w=[pre.b])
                sc.add("dve", _ts(acc.t[:], pre.t[:, 2:514], cw[:, g, 2:3], None, ALU.mult), r=[pre.b], w=[acc.b])
                sc.add("dve", _stt(acc.t[:], pre.t[:, 1:513], cw[:, g, 1:2], acc.t[:], ALU.mult, ALU.add),
                       r=[pre.b, acc.b], w=[acc.b])
                sc.add("dve", _stt(acc.t[:], pre.t[:, 0:512], cw[:, g, 0:1], acc.t[:], ALU.mult, ALU.add),
                       r=[pre.b, acc.b], w=[acc.b])
                sc.add("act", _act(qb.t[:], acc.t[:], AF.Silu, bias=cbias[0][:, g:g + 1]), r=[acc.b], w=[qb.b])
                if i == 0:
                    sc.add("pool", _dma(qkT[g * 128:(g + 1) * 128, 0:511], qb.t[:, 1:512]), r=[qb.b], chan=qb.c)
                else:
                    sc.add("pool", _dma(qkT[g * 128:(g + 1) * 128, t0 - 1:t0 + 511], qb.t[:]), r=[qb.b], chan=qb.c)
                sc.add("dve", _cp(pre.t[:, 0:2], pre.t[:, 512:514]), r=[pre.b], w=[pre.b])
            for b in range(4):
                vo = VO[voi % 2]
                g3 = G3[voi % 2]
                voi += 1
                for part, (c0, c1) in enumerate(((1024, 1536), (1536, 2048), (2048, 2512))):
                    pm = MB[mbi % 6]
                    mbi += 1
                    for kc in range(8):
                        sc.add("pe", _mm(pm.t[:, 0:c1 - c0], xn.t[:, kc, b * 128:(b + 1) * 128], Win[:, kc, c0:c1],
                                         kc == 0, kc == 7), r=[xn.b, Winb], w=[pm.b])
                    if part == 0:
                        sc.add("dve", _cp(vo.t[:, 0:512], pm.t[:]), r=[pm.b], w=[vo.b])
                    elif part == 1:
                        sc.add("act", _act(vo.t[:, 512:1024], pm.t[:], AF.Tanh, scale=0.5), r=[pm.b], w=[vo.b])
                        sc.add("dve", _ts(vo.t[:, 512:1024], vo.t[:, 512:1024], 0.5, 0.5, ALU.mult, ALU.add),
                               r=[vo.b], w=[vo.b])
                    else:
                        sc.add("act", _act(g3.t[:], pm.t[:, 0:464], AF.Copy), r=[pm.b], w=[g3.b])
                        sc.add("dve", _tt(g3.t[:, 0:16], g3.t[:, 0:16], bg[0][:], ALU.add), r=[g3.b], w=[g3.b])
                r0 = t0 + b * 128
                sc.add("pool", _dma(vo_s[r0:r0 + 128, :], vo.t[:]), r=[vo.b], chan=vo.c)
                sc.add("pool", _dma(g3_s[r0:r0 + 128, :], g3.t[:]), r=[g3.b], chan=g3.c)
        prb = [PRE[g].b for g in range(8)]
        for g in range(8):
            sc.add("dve", _ts(last8.t[:, g:g + 1], PRE[g].t[:, 0:1], cw[:, g, 0:1], None, ALU.mult), r=[PRE[g].b], w=[last8.b])
            sc.add("dve", _stt(last8.t[:, g:g + 1], PRE[g].t[:, 1:2], cw[:, g, 1:2], last8.t[:, g:g + 1], ALU.mult, ALU.add),
                   r=[PRE[g].b, last8.b], w=[last8.b])
        sc.add("dve", _tt(last8.t[:], last8.t[:], cbias[0][:], ALU.add), r=[last8.b], w=[last8.b])
        sc.add("act", _act(last8b.t[:], last8.t[:], AF.Silu), r=[last8.b], w=[last8b.b])
        sc.add("pool", _dma(qkT.rearrange("(g p) s -> p g s", p=128)[:, :, S - 1], last8b.t[:],
                            allow_slow_non_contiguous=True), r=[last8b.b], chan=last8b.c)
        sc.emit()

    hf_s = k.dram_tmp("hf_s", [S, 512])
    yT_s = k.dram_tmp("yT_s", [1024, S], BF16)

    def bc(ap, m):
        a = [list(d) for d in ap.ap]
        return bass.AP(ap.tensor, ap.offset, a + [[0, m]])

    def bc_mid(ap, m):
        a = [list(d) for d in ap.ap]
        return bass.AP(ap.tensor, ap.offset, [a[0], [0, m]] + a[1:])

    with contextlib.ExitStack() as st:
        maskF = k.sb(st, "maskF", [128, 128])
        maskB = k.sb(st, "maskB", [128, 128])
        mb_ = Buf("masks")
        sc.add("pool", lambda e: e.affine_select(maskF[:], ones_f[:], [[1, 128]], ALU.is_ge, 0.0,
                                                 base=0, channel_multiplier=-1), w=[mb_])
        sc.add("pool", lambda e: e.affine_select(maskB[:], ones_f[:], [[-1, 128]], ALU.is_ge, 0.0,
                                                 base=0, channel_multiplier=1), w=[mb_])
        normg = load_bcast(st, "normg", mlstm_norm_g, 512)
        SL = [Slot(k.sb(st, f"sl{i}", [128, 8, 512], BF16), sc.chan(f"c_sl{i}")) for i in range(2)]
        VOT = [Slot(k.sb(st, f"vot{i}", [128, 1024]), sc.chan(f"c_vot{i}")) for i in range(2)]
        GT = [Slot(k.sb(st, f"gt{i}", [128, 16]), sc.chan(f"c_gt{i}")) for i in range(2)]
        HF = [Slot(k.sb(st, f"hf{i}", [128, 512]), sc.chan(f"c_hf{i}")) for i in range(2)]
        e1 = Slot(k.sb(st, "e1", [128, 4]))
        lsp = Slot(k.sb(st, "lsp", [128, 4]))
        tmpa = Slot(k.sb(st, "tmpa", [128, 4]))
        AA = [Slot(k.sb(st, f"aa{i}", [128, 4])) for i in range(2)]
        EG = [Slot(k.sb(st, f"eg{i}", [128, 8])) for i in range(2)]
        V1 = [Slot(k.sb(st, f"v1{i}", [128, 4, 130], BF16)) for i in range(2)]
        KTOK = [Slot(k.sb(st, f"ktok{i}", [128, 4, 128], BF16)) for i in range(2)]
        MM = [Slot(k.sb(st, f"mm{i}", [128, 4, 128], BF16)) for i in range(2)]
        C1 = Slot(k.sb(st, "c1", [128, 4, 130]))
        C1b = Slot(k.sb(st, "c1b", [128, 4, 130], BF16))
        tmpC = Slot(k.sb(st, "tmpc", [128, 4, 130]))
        den = Slot(k.sb(st, "den", [128, 4]))
        rr_ = Slot(k.sb(st, "rr", [128, 4]))
        hsum = Slot(k.sb(st, "hsum", [128, 512]))
        sq = Slot(k.sb(st, "sq", [128, 512]))
        ssn = Slot(k.sb(st, "ssn", [128, 4]))
        rs = Slot(k.sb(st, "rs", [128, 4]))
        yb = Slot(k.sb(st, "yb", [128, 512], BF16))
        YT = [Slot(k.sb(st, f"yt{i}", [128, 4, 512], BF16), sc.chan(f"c_yt{i}")) for i in range(2)]
        TBK = Slot(k.ps(st, "tbk", [128, 1024], BF16))
        SPS = [Slot(k.ps(st, f"sps{i}", [128, 512])) for i in range(2)]
        UPS = Slot(k.ps(st, "ups", [128, 1024]))
        DPS = Slot(k.ps(st, "dps", [128, 1024]))
        GPS = Slot(k.ps(st, "gps", [128, 512]))
        qkT3 = qkT.rearrange("(g p) s -> p g s", p=128)
        yT3 = yT_s.rearrange("(g p) s -> p g s", p=128)
        state = {"slab": None, "n": 0}

        def pre(c, dirn, n):
            cg = c // 4
            if state["slab"] != (dirn, cg):
                state["slab"] = (dirn, cg)
                state["n"] += 1
                sl = SL[state["n"] % 2]
                sc.add("sp", _dma(sl.t[:], qkT3[:, :, cg * 512:(cg + 1) * 512]), w=[sl.b], chan=sl.c)
            sl = SL[state["n"] % 2]
            vot, gt, aa, eg, v1, ktok, mm, sps = VOT[n % 2], GT[n % 2], AA[n % 2], EG[n % 2], V1[n % 2], KTOK[n % 2], MM[n % 2], SPS[n % 2]
            r0 = c * 128
            sc.add("sp", _dma(vot.t[:], vo_s[r0:r0 + 128, :]), w=[vot.b], chan=vot.c)
            sc.add("sp", _dma(gt.t[:], g3_s[r0:r0 + 128, 0:16]), w=[gt.b], chan=gt.c)
            io, fo = dirn * 8, dirn * 8 + 4
            tri = maskF if dirn == 0 else maskB
            sc.add("act", _act(e1.t[:], gt.t[:, fo:fo + 4], AF.Exp, scale=-1.0), r=[gt.b], w=[e1.b])
            sc.add("act", _act(lsp.t[:], e1.t[:], AF.Ln, bias=1.0), r=[e1.b], w=[lsp.b])
            sc.add("pe", _mm(GPS.t[:, 0:4], tri[:], lsp.t[:], True, True), r=[lsp.b, mb_], w=[GPS.b])
            sc.add("pe", _mm(GPS.t[:, 4:8], ones_f[:], lsp.t[:], True, True), r=[lsp.b], w=[GPS.b])
            sc.add("dve", _tt(tmpa.t[:], gt.t[:, io:io + 4], GPS.t[:, 0:4], ALU.add), r=[gt.b, GPS.b], w=[tmpa.b])
            sc.add("act", _act(aa.t[:], tmpa.t[:], AF.Exp), r=[tmpa.b], w=[aa.b])
            sc.add("act", _act(eg.t[:], GPS.t[:, 0:8], AF.Exp, scale=-1.0), r=[GPS.b], w=[eg.b])
            sc.add("dve", _tt(v1.t[:, :, 0:128], vot.t[:, 0:512].rearrange("p (h d) -> p h d", h=4), bc(aa.t[:], 128), ALU.mult),
                   r=[vot.b, aa.b], w=[v1.b])
            sc.add("dve", _cp(v1.t[:, :, 128], aa.t[:]), r=[aa.b], w=[v1.b])
            c4 = (c % 4) * 128
            for h in range(4):
                sc.add("pe", _tr(TBK.t[:, h * 128:(h + 1) * 128], sl.t[:, 4 + h, c4:c4 + 128], ident[:]), r=[sl.b], w=[TBK.b])
            sc.add("act", _act(ktok.t[:], TBK.t[:, 0:512].rearrange("p (h d) -> p h d", h=4), AF.Copy, scale=128.0 ** -0.5),
                   r=[TBK.b], w=[ktok.b])
            for h in range(4):
                sc.add("pe", _mm(sps.t[:, h * 128:(h + 1) * 128], sl.t[:, 4 + h, c4:c4 + 128], sl.t[:, h, c4:c4 + 128], True, True),
                       r=[sl.b], w=[sps.b])
            sc.add("dve", _stt(mm.t[:], sps.t[:].rearrange("p (h d) -> p h d", h=4), 128.0 ** -0.5, bc_mid(tri[:], 4), ALU.mult, ALU.mult),
                   r=[sps.b, mb_], w=[mm.b])
            return sl, c4

        def main(c, dirn, n, sl, c4):
            vot, eg, v1, ktok, mm = VOT[n % 2], EG[n % 2], V1[n % 2], KTOK[n % 2], MM[n % 2]
            hf = HF[n % 2]
            r0 = c * 128
            if dirn == 1:
                sc.add("sp", _dma(hf.t[:], hf_s[r0:r0 + 128, :]), w=[hf.b], chan=hf.c)
            for h in range(4):
                sc.add("pe", _mm(UPS.t[:, h * 256:h * 256 + 129], mm.t[:, h, :], v1.t[:, h, 0:129], True, False), r=[mm.b, v1.b], w=[UPS.b])
                sc.add("pe", _mm(UPS.t[:, h * 256:h * 256 + 129], sl.t[:, h, c4:c4 + 128], C1b.t[:, h, 0:129], False, True),
                       r=[sl.b, C1b.b], w=[UPS.b])
            for h in range(4):
                sc.add("pe", _mm(DPS.t[:, h * 256:h * 256 + 129], ktok.t[:, h, :], v1.t[:, h, 0:129], True, True), r=[ktok.b, v1.b], w=[DPS.b])
            U3 = UPS.t[:].rearrange("p (h d) -> p h d", h=4)
            D3 = DPS.t[:].rearrange("p (h d) -> p h d", h=4)
            sc.add("dve", _tt(tmpC.t[:, :, 0:129], D3[:, :, 0:129], C1.t[:, :, 0:129], ALU.add), r=[DPS.b, C1.b], w=[tmpC.b])
            sc.add("dve", _tt(C1.t[:, :, 0:129], tmpC.t[:, :, 0:129], bc(eg.t[:, 4:8], 129), ALU.mult), r=[tmpC.b, eg.b], w=[C1.b])
            sc.add("act", _act(C1b.t[:, :, 0:129], C1.t[:, :, 0:129], AF.Copy), r=[C1.b], w=[C1b.b])
            sc.add("dve", _tt(den.t[:], U3[:, :, 128], eg.t[:, 0:4], ALU.mult), r=[UPS.b, eg.b], w=[den.b])
            sc.add("act", _act(den.t[:], den.t[:], AF.Abs), r=[den.b], w=[den.b])
            sc.add("dve", _ts(den.t[:], den.t[:], 1.0, None, ALU.max), r=[den.b], w=[den.b])
            sc.add("dve", lambda e: e.reciprocal(rr_.t[:], den.t[:]), r=[den.b], w=[rr_.b])
            sc.add("dve", _tt(rr_.t[:], rr_.t[:], eg.t[:, 0:4], ALU.mult), r=[rr_.b, eg.b], w=[rr_.b])
            if dirn == 0:
                sc.add("dve", _tt(hf.t[:].rearrange("p (h d) -> p h d", h=4), U3[:, :, 0:128], bc(rr_.t[:], 128), ALU.mult),
                       r=[UPS.b, rr_.b], w=[hf.b])
                sc.add("pool", _dma(hf_s[r0:r0 + 128, :], hf.t[:]), r=[hf.b], chan=hf.c)
                return
            sc.add("dve", _tt(hsum.t[:].rearrange("p (h d) -> p h d", h=4), U3[:, :, 0:128], bc(rr_.t[:], 128), ALU.mult),
                   r=[UPS.b, rr_.b], w=[hsum.b])
            sc.add("dve", _tt(hsum.t[:], hsum.t[:], hf.t[:], ALU.add), r=[hsum.b, hf.b], w=[hsum.b])
            sc.add("act", _act(sq.t[:], hsum.t[:], AF.Square, scale=128.0 ** -0.5), r=[hsum.b], w=[sq.b])
            sc.add("dve", lambda e: e.tensor_reduce(ssn.t[:], sq.t[:].rearrange("p (h d) -> p h d", h=4), AX.X, ALU.add),
                   r=[sq.b], w=[ssn.b])
            sc.add("dve", _ts(ssn.t[:], ssn.t[:], EPS, None, ALU.add), r=[ssn.b], w=[ssn.b])
            sc.add("pool", _tt(rs.t[:], ssn.t[:], mhalf[:, 0:4], ALU.pow), r=[ssn.b], w=[rs.b])
            sc.add("dve", _tt(hsum.t[:].rearrange("p (h d) -> p h d", h=4), hsum.t[:].rearrange("p (h d) -> p h d", h=4),
                              bc(rs.t[:], 128), ALU.mult), r=[hsum.b, rs.b], w=[hsum.b])
            sc.add("dve", _tt(hsum.t[:], hsum.t[:], normg[0][:], ALU.mult), r=[hsum.b, normg[1]], w=[hsum.b])
            sc.add("dve", _tt(yb.t[:], hsum.t[:], vot.t[:, 512:1024], ALU.mult), r=[hsum.b, vot.b], w=[yb.b])
            yt = YT[(c // 4) % 2]
            for h in range(4):
                sc.add("pe", _tr(TBK.t[:, 512 + h * 128:512 + (h + 1) * 128], yb.t[:, h * 128:(h + 1) * 128], ident[:]), r=[yb.b], w=[TBK.b])
            sc.add("act", _act(yt.t[:, :, c4:c4 + 128], TBK.t[:, 512:1024].rearrange("p (h d) -> p h d", h=4), AF.Copy),
                   r=[TBK.b], w=[yt.b])
            if c % 4 == 0:
                cg = c // 4
                sc.add("pool", _dma(yT3[:, 0:4, cg * 512:(cg + 1) * 512], yt.t[:]), r=[yt.b], chan=yt.c)

        n = 0
        for dirn in range(2):
            order = list(range(NB)) if dirn == 0 else list(range(NB - 1, -1, -1))
            if dirn == 1:
                sc.barrier()
            sc.add("pool", _memset(C1.t[:], 0.0), w=[C1.b])
            sc.add("pool", _memset(C1b.t[:], 0.0), w=[C1b.b])
            nxt = pre(order[0], dirn, n)
            for j, c in enumerate(order):
                cur = nxt
                if j + 1 < NB:
                    nxt = pre(order[j + 1], dirn, n + 1)
                main(c, dirn, n, *cur)
                n += 1
        sc.emit()

    KT_s = k.dram_tmp("KT_s", [512, S], BF16)
    KR_s = k.dram_tmp("KR_s", [64, S], BF16)
    V_s = k.dram_tmp("V_s", [S, 512], BF16)
    QN_s = k.dram_tmp("QN_s", [512, S], BF16)
    QR_s = k.dram_tmp("QR_s", [4, 65, S], BF16)
    kmax_s = k.dram_tmp("kmax_s", [128, 4])
    TWO_PI = 6.283185307179586

    with contextlib.ExitStack() as st:
        stage = [Slot(k.sb(st, f"wstagec{i}", [128, 1024]), ld_ch[i]) for i in range(2)]
        gq = load_cols(st, "gq", q_norm_g, 2)
        gkv = load_cols(st, "gkv", kv_norm_g, 1)
        Wuq, Wuqb = load_w(st, stage, "Wuq", w_uq, 256, 768, gq)
        Wkv, Wkvb = load_w(st, stage, "Wkv", w_ukv, 128, 1024, gkv)
        Wkv4 = Wkv[:, 0, :].rearrange("p (h t d) -> p h t d", h=4, t=2)
        cos2 = k.sb(st, "cos2", [128, NB, 64])
        sin1 = k.sb(st, "sin1", [128, NB, 32])
        tb_ = Buf("ropetab")
        pos = k.sb(st, "pos", [128, NB])
        invf = k.sb(st, "invf", [128, 32])
        ang = k.sb(st, "ang", [128, NB, 32])
        angi = k.sb(st, "angi", [128, NB, 32], mybir.dt.int32)
        angf = k.sb(st, "angf", [128, NB, 32])
        msk = k.sb(st, "msk", [128, NB, 32])
        sc.add("pool", lambda e: e.iota(pos[:], [[128, NB]], base=0, channel_multiplier=1, allow_small_or_imprecise_dtypes=True), w=[tb_])
        sc.add("pool", lambda e: e.iota(invf[:], [[1, 32]], base=0, channel_multiplier=0, allow_small_or_imprecise_dtypes=True), r=[tb_], w=[tb_])
        sc.add("act", _act(invf[:], invf[:], AF.Exp, scale=-float(np.log(10000.0)) / 32.0), r=[tb_], w=[tb_])
        sc.add("dve", _tt(ang[:], bc(pos[:], 32), bc_mid(invf[:], NB), ALU.mult), r=[tb_], w=[tb_])
        sc.add("dve", _ts(ang[:], ang[:], 1.0 / TWO_PI, None, ALU.mult), r=[tb_], w=[tb_])
        for which in range(2):
            if which == 1:
                sc.add("dve", _ts(ang[:], ang[:], 0.25, None, ALU.add), r=[tb_], w=[tb_])
            sc.add("dve", _cp(angi[:], ang[:]), r=[tb_], w=[tb_])
            sc.add("dve", _cp(angf[:], angi[:]), r=[tb_], w=[tb_])
            sc.add("dve", _tt(angf[:], ang[:], angf[:], ALU.subtract), r=[tb_], w=[tb_])
            sc.add("dve", _ts(msk[:], angf[:], 0.5, None, ALU.is_gt), r=[tb_], w=[tb_])
            sc.add("dve", _tt(angf[:], angf[:], msk[:], ALU.subtract), r=[tb_], w=[tb_])
            sc.add("dve", _ts(msk[:], angf[:], -0.5, None, ALU.is_lt), r=[tb_], w=[tb_])
            sc.add("dve", _tt(angf[:], angf[:], msk[:], ALU.add), r=[tb_], w=[tb_])
            if which == 0:
                sc.add("act", _act(sin1[:], angf[:], AF.Sin, scale=TWO_PI * (1.0 - 1e-6)), r=[tb_], w=[tb_])
            else:
                sc.add("act", _act(cos2[:, :, 0:32], angf[:], AF.Sin, scale=TWO_PI * (1.0 - 1e-6)), r=[tb_], w=[tb_])
                sc.add("act", _act(cos2[:, :, 32:64], angf[:], AF.Sin, scale=TWO_PI * (1.0 - 1e-6)), r=[tb_], w=[tb_])
        sc.barrier()

        G3T = [Slot(k.sb(st, f"g3t{i}", [128, 4, 464]), sc.chan(f"c_g3t{i}")) for i in range(2)]
        junkc = Slot(k.sb(st, "junkc", [128, 3072], BF16))
        ssq = Slot(k.sb(st, "ssq", [128, 8]))
        ssq2 = Slot(k.sb(st, "ssq2", [128, 8]))
        rst = Slot(k.sb(st, "rst", [128, 8]))
        cqn = Slot(k.sb(st, "cqn", [128, 4, 256], BF16))
        ckvn = Slot(k.sb(st, "ckvn", [128, 4, 128], BF16))
        tA = Slot(k.sb(st, "tA", [128, 4, 64]))
        tB = Slot(k.sb(st, "tB", [128, 4, 64]))
        krb = Slot(k.sb(st, "krb", [128, 4, 64], BF16))
        sqr = Slot(k.sb(st, "sqr", [128, 4, 64]))
        kr2 = Slot(k.sb(st, "kr2", [128, 4]))
        cqT = Slot(k.sb(st, "cqT", [128, 2, 512], BF16))
        ckvT = Slot(k.sb(st, "ckvT", [128, 512], BF16))
        krT = Slot(k.sb(st, "krT", [64, 512], BF16), sc.chan("c_krT"))
        KTS = [Slot(k.sb(st, f"kts{i}", [128, 512], BF16), sc.chan(f"c_kts{i}")) for i in range(2)]
        VS = [Slot(k.sb(st, f"vs{i}", [128, 512], BF16), sc.chan(f"c_vs{i}")) for i in range(2)]
        sqk = Slot(k.sb(st, "sqk", [128, 512]))
        kn2 = Slot(k.sb(st, "kn2", [128, 4, 4]))
        kmax = Slot(k.sb(st, "kmax", [128, 4]))
        kmt = Slot(k.sb(st, "kmt", [128, 4]))
        q_sb = Slot(k.sb(st, "q_sb", [128, 4, 768]))
        qtA = Slot(k.sb(st, "qtA", [128, 4, 4, 64]))
        qtB = Slot(k.sb(st, "qtB", [128, 4, 4, 64]))
        qbn = Slot(k.sb(st, "qbn", [128, 4, 4, 128], BF16))
        qbr = Slot(k.sb(st, "qbr", [128, 4, 4, 66], BF16))
        qn2 = Slot(k.sb(st, "qn2", [128, 16]))
        qn1 = Slot(k.sb(st, "qn1", [128, 16]))
        QS = [Slot(k.sb(st, f"qs{i}", [128, 2, 512], BF16), sc.chan(f"c_qs{i}")) for i in range(2)]
        QRS = [Slot(k.sb(st, f"qrs{i}", [65, 2, 512], BF16), sc.chan(f"c_qrs{i}")) for i in range(2)]
        PB = [Slot(k.ps(st, f"pb{i}", [128, 512])) for i in range(8)]
        pbi = {"i": 0}

        def bank():
            pbi["i"] += 1
            return PB[pbi["i"] % 8]

        def bfv(slot):
            return slot.t[:].bitcast(BF16)

        sc.add("pool", _memset(kmax.t[:], 0.0), w=[kmax.b])
        sc.add("pool", _memset(qbr.t[:], 0.0), w=[qbr.b])
        q4 = q_sb.t[:].rearrange("p b (h d) -> p b h d", h=4)
        kti = 0
        for i in range(NT):
            t0 = i * 512
            g3 = G3T[i % 2]
            sc.add("sp", _dma(g3.t[:], g3_s[t0:t0 + 512, :].rearrange("(b p) c -> p b c", p=128)), w=[g3.b], chan=g3.c)
            for b in range(4):
                sc.add("act", _act(junkc.t[:, 0:256], g3.t[:, b, 16:272], AF.Square, scale=1.0 / 16.0, accum_out=ssq.t[:, b:b + 1]),
                       r=[g3.b], w=[junkc.b, ssq.b])
                sc.add("act", _act(junkc.t[:, 0:128], g3.t[:, b, 272:400], AF.Square, scale=128.0 ** -0.5, accum_out=ssq.t[:, 4 + b:5 + b]),
                       r=[g3.b], w=[junkc.b, ssq.b])
            sc.add("dve", _ts(ssq2.t[:], ssq.t[:], EPS, None, ALU.add), r=[ssq.b], w=[ssq2.b])
            sc.add("pool", _tt(rst.t[:], ssq2.t[:], mhalf[:, 0:8], ALU.pow), r=[ssq2.b], w=[rst.b])
            sc.add("dve", _tt(cqn.t[:], g3.t[:, :, 16:272], bc(rst.t[:, 0:4], 256), ALU.mult), r=[g3.b, rst.b], w=[cqn.b])
            sc.add("dve", _tt(ckvn.t[:], g3.t[:, :, 272:400], bc(rst.t[:, 4:8], 128), ALU.mult), r=[g3.b, rst.b], w=[ckvn.b])
            xk = g3.t[:, :, 400:464]
            cs, sn = cos2[:, 4 * i:4 * i + 4, :], sin1[:, 4 * i:4 * i + 4, :]
            sc.add("dve", _tt(tA.t[:], xk, cs, ALU.mult), r=[g3.b], w=[tA.b])
            sc.add("dve", _tt(tB.t[:, :, 0:32], g3.t[:, :, 432:464], sn, ALU.mult), r=[g3.b], w=[tB.b])
            sc.add("dve", _tt(tB.t[:, :, 32:64], g3.t[:, :, 400:432], sn, ALU.mult), r=[g3.b], w=[tB.b])
            sc.add("dve", _tt(krb.t[:, :, 0:32], tA.t[:, :, 0:32], tB.t[:, :, 0:32], ALU.subtract), r=[tA.b, tB.b], w=[krb.b])
            sc.add("dve", _tt(krb.t[:, :, 32:64], tA.t[:, :, 32:64], tB.t[:, :, 32:64], ALU.add), r=[tA.b, tB.b], w=[krb.b])
            sc.add("act", _act(sqr.t[:], xk, AF.Square), r=[g3.b], w=[sqr.b])
            sc.add("dve", lambda e: e.tensor_reduce(kr2.t[:], sqr.t[:], AX.X, ALU.add), r=[sqr.b], w=[kr2.b])
            pa, pb2 = bank(), bank()
            for b in range(4):
                for kc in range(2):
                    sc.add("pe", _tr(bfv(pa)[:, kc * 512 + b * 128:kc * 512 + (b + 1) * 128], cqn.t[:, b, kc * 128:(kc + 1) * 128], ident[:]),
                           r=[cqn.b], w=[pa.b])
                sc.add("pe", _tr(bfv(pb2)[:, b * 128:(b + 1) * 128], ckvn.t[:, b, :], ident[:]), r=[ckvn.b], w=[pb2.b])
                sc.add("pe", _tr(bfv(pb2)[0:64, 512 + b * 128:512 + (b + 1) * 128], krb.t[:, b, :], ident[:]), r=[krb.b], w=[pb2.b])
            sc.add("act", _act(cqT.t[:], bfv(pa).rearrange("p (a b) -> p a b", a=2), AF.Copy), r=[pa.b], w=[cqT.b])
            sc.add("dve", _cp(ckvT.t[:], bfv(pb2)[:, 0:512]), r=[pb2.b], w=[ckvT.b])
            sc.add("act", _act(krT.t[:], bfv(pb2)[0:64, 512:1024], AF.Copy), r=[pb2.b], w=[krT.b])
            sc.add("pool", _dma(KR_s[:, t0:t0 + 512], krT.t[:]), r=[krT.b], chan=krT.c)
            for h in range(4):
                pk = bank()
                sc.add("pe", _mm(pk.t[:], Wkv[:, 0, h * 256:h * 256 + 128], ckvT.t[:], True, True), r=[ckvT.b, Wkvb], w=[pk.b])
                kts = KTS[kti % 2]
                kti += 1
                e = "act" if h % 2 else "dve"
                sc.add(e, scale_cast(e, kts.t[:], pk.t[:]), r=[pk.b], w=[kts.b])
                sc.add("pool", _dma(KT_s[h * 128:(h + 1) * 128, t0:t0 + 512], kts.t[:]), r=[kts.b], chan=kts.c)
            for b in range(4):
                tok = slice(b * 128, (b + 1) * 128)
                pv = bank()
                sc.add("pe", _mm(pv.t[:].rearrange("p (h d) -> p h d", h=4), ckvT.t[:, tok], Wkv4[:, :, 1, :], True, True),
                       r=[ckvT.b, Wkvb], w=[pv.b])
                vs = VS[b % 2]
                sc.add("act", _act(vs.t[:], pv.t[:], AF.Copy), r=[pv.b], w=[vs.b])
                sc.add("pool", _dma(V_s[t0 + b * 128:t0 + (b + 1) * 128, :], vs.t[:]), r=[vs.b], chan=vs.c)
                pk = bank()
                sc.add("pe", _mm(pk.t[:].rearrange("p (h d) -> p h d", h=4), ckvT.t[:, tok], Wkv4[:, :, 0, :], True, True),
                       r=[ckvT.b, Wkvb], w=[pk.b])
                sc.add("act", _act(sqk.t[:], pk.t[:], AF.Square), r=[pk.b], w=[sqk.b])
                sc.add("dve", lambda e, b=b: e.tensor_reduce(kn2.t[:, b, :], sqk.t[:].rearrange("p (h d) -> p h d", h=4), AX.X, ALU.add),
                       r=[sqk.b], w=[kn2.b])
                pq0, pq1 = bank(), bank()
                for kc in range(2):
                    sc.add("pe", _mm(pq0.t[:], cqT.t[:, kc, tok], Wuq[:, kc, 0:512], kc == 0, kc == 1), r=[cqT.b, Wuqb], w=[pq0.b])
                for kc in range(2):
                    sc.add("pe", _mm(pq1.t[:, 0:256], cqT.t[:, kc, tok], Wuq[:, kc, 512:768], kc == 0, kc == 1), r=[cqT.b, Wuqb], w=[pq1.b])
                sc.add("act", _act(q_sb.t[:, b, 0:512], pq0.t[:], AF.Copy), r=[pq0.b], w=[q_sb.b])
                sc.add("dve", _cp(q_sb.t[:, b, 512:768], pq1.t[:, 0:256]), r=[pq1.b], w=[q_sb.b])
            sc.add("dve", _tt(kn2.t[:], kn2.t[:], bc(kr2.t[:], 4), ALU.add), r=[kn2.b, kr2.b], w=[kn2.b])
            sc.add("dve", lambda e: e.tensor_reduce(kmt.t[:], kn2.t[:].rearrange("p b h -> p h b"), AX.X, ALU.max), r=[kn2.b], w=[kmt.b])
            sc.add("dve", _tt(kmax.t[:], kmax.t[:], kmt.t[:], ALU.max), r=[kmax.b, kmt.b], w=[kmax.b])
            cs4 = bass.AP(cs.tensor, cs.offset, [list(cs.ap[0]), list(cs.ap[1]), [0, 4], list(cs.ap[2])])
            sn4 = bass.AP(sn.tensor, sn.offset, [list(sn.ap[0]), list(sn.ap[1]), [0, 4], list(sn.ap[2])])
            sc.add("dve", _tt(qtA.t[:], q4[:, :, :, 128:192], cs4, ALU.mult), r=[q_sb.b], w=[qtA.b])
            sc.add("dve", _tt(qtB.t[:, :, :, 0:32], q4[:, :, :, 160:192], sn4, ALU.mult), r=[q_sb.b], w=[qtB.b])
            sc.add("dve", _tt(qtB.t[:, :, :, 32:64], q4[:, :, :, 128:160], sn4, ALU.mult), r=[q_sb.b], w=[qtB.b])
            sc.add("dve", _tt(qbr.t[:, :, :, 0:32], qtA.t[:, :, :, 0:32], qtB.t[:, :, :, 0:32], ALU.subtract), r=[qtA.b, qtB.b], w=[qbr.b])
            sc.add("dve", _tt(qbr.t[:, :, :, 32:64], qtA.t[:, :, :, 32:64], qtB.t[:, :, :, 32:64], ALU.add), r=[qtA.b, qtB.b], w=[qbr.b])
            sc.add("dve", _cp(qbn.t[:], q4[:, :, :, 0:128]), r=[q_sb.b], w=[qbn.b])
            sc.add("act", _act(junkc.t[:], q_sb.t[:].rearrange("p b c -> p (b c)"), AF.Square), r=[q_sb.b], w=[junkc.b])
            sc.add("dve", lambda e: e.tensor_reduce(qn2.t[:], junkc.t[:].rearrange("p (g d) -> p g d", g=16), AX.X, ALU.add),
                   r=[junkc.b], w=[qn2.b])
            sc.add("pool", _tt(qn1.t[:], qn2.t[:], mhalf[:, 0:16], ALU.pow), r=[qn2.b], w=[qn1.b])
            sc.add("dve", _tt(qn1.t[:], qn1.t[:], qn2.t[:], ALU.mult), r=[qn1.b, qn2.b], w=[qn1.b])
            sc.add("dve", _ts(qbr.t[:, :, :, 64], qn1.t[:].rearrange("p (b h) -> p b h", b=4), -1.01, None, ALU.mult),
                   r=[qn1.b], w=[qbr.b])
            for hp in range(2):
                pn, pr = bank(), bank()
                for hh in range(2):
                    h = 2 * hp + hh
                    for b in range(4):
                        sc.add("pe", _tr(bfv(pn)[:, hh * 512 + b * 128:hh * 512 + (b + 1) * 128], qbn.t[:, b, h, :], ident[:]), r=[qbn.b], w=[pn.b])
                        sc.add("pe", _tr(bfv(pr)[0:65, hh * 512 + b * 128:hh * 512 + (b + 1) * 128], qbr.t[:, b, h, 0:65], ident[:]), r=[qbr.b], w=[pr.b])
                qs, qrs = QS[hp], QRS[hp]
                sc.add("act", _act(qs.t[:], bfv(pn).rearrange("p (a b) -> p a b", a=2), AF.Copy), r=[pn.b], w=[qs.b])
                sc.add("dve", _cp(qrs.t[:], bfv(pr)[0:65, :].rearrange("p (a b) -> p a b", a=2)), r=[pr.b], w=[qrs.b])
                sc.add("pool", _dma(QN_s.rearrange("(h p) s -> p h s", p=128)[:, 2 * hp:2 * hp + 2, t0:t0 + 512], qs.t[:]), r=[qs.b], chan=qs.c)
                sc.add("pool", _dma(QR_s.rearrange("h p s -> p h s")[:, 2 * hp:2 * hp + 2, t0:t0 + 512], qrs.t[:]), r=[qrs.b], chan=qrs.c)
        kmo = Slot(k.sb(st, "kmo", [128, 4]), sc.chan("c_kmo"))
        sc.add("dve", _cp(kmo.t[:], kmax.t[:]), r=[kmax.b], w=[kmo.b])
        sc.add("pool", _dma(kmax_s[:, :], kmo.t[:]), r=[kmo.b], chan=kmo.c)
        sc.emit()

    with contextlib.ExitStack() as st:
        KT = Slot(k.sb(st, "KT", [128, 4, S], BF16), sc.chan("c_KT"))
        KR = Slot(k.sb(st, "KR", [65, S], BF16), sc.chan("c_KR"))
        VR = Slot(k.sb(st, "VR", [128, NB, 512], BF16), sc.chan("c_VR"))
        kml = Slot(k.sb(st, "kml", [128, 4]), sc.chan("c_kml"))
        km1 = Slot(k.sb(st, "km1", [1, 4]))
        kmx = Slot(k.sb(st, "kmx", [128, 4]))
        phalf = Slot(k.sb(st, "phalf", [128, 4]))
        SPB = [Slot(k.ps(st, f"spb{i}", [128, 512])) for i in range(4)]
        OPB = [Slot(k.ps(st, f"opb{i}", [128, 512])) for i in range(2)]
        RSB = Slot(k.ps(st, "rsb", [128, 512]))
        PT = [Slot(k.sb(st, f"pt{i}", [128, 512], BF16)) for i in range(4)]
        ACCD = [Slot(k.sb(st, f"accd{i}", [128, 512])) for i in range(2)]
        ACCP = [Slot(k.sb(st, f"accp{i}", [128, 512])) for i in range(2)]
        QN = [Slot(k.sb(st, f"qnt{i}", [128, 512], BF16), sc.chan(f"c_qn{i}")) for i in range(2)]
        QR = [Slot(k.sb(st, f"qrt{i}", [65, 512], BF16), sc.chan(f"c_qr{i}")) for i in range(2)]
        rinv = Slot(k.sb(st, "rinv", [128, 512]))
        YO = [Slot(k.sb(st, f"yo{i}", [128, 512], BF16), sc.chan(f"c_yo{i}")) for i in range(2)]
        sc.add("sp", _dma(KT.t[:], KT_s.rearrange("(h p) s -> p h s", p=128)), w=[KT.b], chan=KT.c)
        sc.add("pool", _memset(KR.t[64:65, :], 1.0), w=[KR.b])
        sc.add("sp", _dma(KR.t[0:64, :], KR_s[:, :]), w=[KR.b], chan=KR.c)
        sc.add("sp", _dma(VR.t[:], V_s.rearrange("(c p) d -> p c d", p=128)), w=[VR.b], chan=VR.c)
        sc.add("sp", _dma(kml.t[:], kmax_s[:, :]), w=[kml.b], chan=kml.c)
        sc.add("pool", _memset(phalf.t[:], 0.5), w=[phalf.b])
        sc.add("pool", lambda e: e.tensor_reduce(km1.t[:], kml.t[:], AX.C, ALU.max), r=[kml.b], w=[km1.b])
        sc.add("pe", _mm(RSB.t[:, 0:4], ones_f[0:1, :], km1.t[:], True, True), r=[km1.b], w=[RSB.b])
        sc.add("dve", _cp(kmx.t[:], RSB.t[:, 0:4]), r=[RSB.b], w=[kmx.b])
        sc.add("pool", _tt(kmx.t[:], kmx.t[:], phalf.t[:], ALU.pow), r=[kmx.b, phalf.b], w=[kmx.b])
        scale = 192.0 ** -0.5
        it = 0
        pti = 0
        for h in range(4):
            for j in range(NT):
                qn, qr = QN[it % 2], QR[it % 2]
                accd, accp, opb, yo = ACCD[it % 2], ACCP[it % 2], OPB[it % 2], YO[it % 2]
                it += 1
                sc.add("sp", _dma(qn.t[:], QN_s[h * 128:(h + 1) * 128, j * 512:(j + 1) * 512]), w=[qn.b], chan=qn.c)
                sc.add("sp", _dma(qr.t[:], QR_s[h, :, j * 512:(j + 1) * 512]), w=[qr.b], chan=qr.c)
                sc.add("dve", _ts(qr.t[64:65, :], qr.t[64:65, :], kmx.t[64:65, h:h + 1], None, ALU.mult), r=[qr.b, kmx.b], w=[qr.b])

                def qk(kc):
                    sp_ = SPB[kc % 4]
                    ks = slice(kc * 128, (kc + 1) * 128)
                    sc.add("pe", _mm(sp_.t[:], KT.t[:, h, ks], qn.t[:], True, False), r=[KT.b, qn.b], w=[sp_.b])
                    sc.add("pe", _mm(sp_.t[:], KR.t[0:65, ks], qr.t[0:65, :], False, True), r=[KR.b, qr.b], w=[sp_.b])

                qk(0)
                if NB > 1:
                    qk(1)
                for kc in range(NB):
                    if kc + 2 < NB:
                        qk(kc + 2)
                    sp_ = SPB[kc % 4]
                    pt = PT[pti % 4]
                    pti += 1
                    sc.add("act", _act(pt.t[:], sp_.t[:], AF.Exp, scale=scale), r=[sp_.b], w=[pt.b])
                    eng, acc = ("dve", accd) if kc % 2 == 0 else ("pool", accp)
                    if kc < 2:
                        sc.add(eng, _cp(acc.t[:], pt.t[:]), r=[pt.b], w=[acc.b])
                    else:
                        sc.add(eng, _tt(acc.t[:], acc.t[:], pt.t[:], ALU.add), r=[pt.b, acc.b], w=[acc.b])
                    sc.add("pe", _mm(opb.t[:], VR.t[:, kc, h * 128:(h + 1) * 128], pt.t[:], kc == 0, kc == NB - 1),
                           r=[VR.b, pt.b], w=[opb.b])
                if NB > 1:
                    sc.add("dve", _tt(accd.t[:], accd.t[:], accp.t[:], ALU.add), r=[accd.b, accp.b], w=[accd.b])
                sc.add("pe", _mm(RSB.t[:], ones_f[:], accd.t[:], True, True), r=[accd.b], w=[RSB.b])
                sc.add("dve", lambda e: e.reciprocal(rinv.t[:], RSB.t[:]), r=[RSB.b], w=[rinv.b])
                sc.add("dve", _tt(yo.t[:], opb.t[:], rinv.t[:], ALU.mult), r=[opb.b, rinv.b], w=[yo.b])
                sc.add("pool", _dma(yT_s[512 + h * 128:512 + (h + 1) * 128, j * 512:(j + 1) * 512], yo.t[:]), r=[yo.b], chan=yo.c)
        sc.emit()

    h1_s = k.dram_tmp("h1_s", [S, D])
    xn2T_s = k.dram_tmp("xn2T_s", [D, S + 2], BF16)
    h2_s = k.dram_tmp("h2_s", [S, D])
    yT3 = yT_s.rearrange("(g p) s -> p g s", p=128)
    xn2T3 = xn2T_s.rearrange("(g p) s -> p g s", p=128)

    def rms_transpose(xt, ss, ss2, rstd, junk, XB, xn, TBs, gain_scale=1.0 / 32.0):
        for b in range(4):
            sc.add("act", _act(junk.t[:], xt.t[:, b, :], AF.Square, scale=gain_scale, accum_out=ss.t[:, b:b + 1]),
                   r=[xt.b], w=[junk.b, ss.b])
        sc.add("dve", _ts(ss2.t[:], ss.t[:], EPS, None, ALU.add), r=[ss.b], w=[ss2.b])
        sc.add("pool", _tt(rstd.t[:], ss2.t[:], mhalf[:, 0:4], ALU.pow), r=[ss2.b], w=[rstd.b])
        for b in range(4):
            e = "dve" if b % 2 else "act"
            sc.add(e, scale_cast(e, XB.t[:, b, :], xt.t[:, b, :], rstd.t[:, b:b + 1]), r=[xt.b, rstd.b], w=[XB.b])
        for j in range(4):
            tb = TBs[j % 2]
            for kk in range(2):
                kc = 2 * j + kk
                for b in range(4):
                    sc.add("pe", _tr(tb.t[:, kk * 512 + b * 128:kk * 512 + (b + 1) * 128],
                                     XB.t[:, b, kc * 128:(kc + 1) * 128], ident[:]), r=[XB.b], w=[tb.b])
            e = "dve" if j % 2 else "act"
            sc.add(e, scale_cast(e, xn.t[:, 2 * j:2 * j + 2, :], tb.t[:].rearrange("p (a b) -> p a b", a=2)),
                   r=[tb.b], w=[xn.b])

    with contextlib.ExitStack() as st:
        stage = [Slot(k.sb(st, f"wstaged{i}", [128, 1024]), ld_ch[i]) for i in range(2)]
        Wout, Woutb = load_w(st, stage, "Wout", w_out, D, D)
        zt = Slot(k.sb(st, "zt", [128, 8, 2], BF16), sc.chan("c_zt"))
        sc.add("pool", _memset(zt.t[:], 0.0), w=[zt.b])
        sc.add("pool", _dma(xn2T3[:, :, 0:1], zt.t[:, :, 0:1], allow_slow_non_contiguous=True), r=[zt.b], chan=zt.c)
        sc.add("pool", _dma(xn2T3[:, :, S + 1:S + 2], zt.t[:, :, 1:2], allow_slow_non_contiguous=True), r=[zt.b], chan=zt.c)
        YTT = [Slot(k.sb(st, f"ytt{i}", [128, 8, 512], BF16), sc.chan(f"c_ytt{i}")) for i in range(2)]
        XT = [Slot(k.sb(st, f"xtd{i}", [128, 4, D]), sc.chan(f"c_xtd{i}")) for i in range(2)]
        XB = Slot(k.sb(st, "xbd", [128, 4, D], BF16))
        XN = [Slot(k.sb(st, f"xnd{i}", [128, 8, 512], BF16), sc.chan(f"c_xnd{i}")) for i in range(2)]
        junk = Slot(k.sb(st, "junkd", [128, D], BF16))
        ss = Slot(k.sb(st, "ssd", [128, 4]))
        ss2 = Slot(k.sb(st, "ss2d", [128, 4]))
        rstd = Slot(k.sb(st, "rstdd", [128, 4]))
        TBs = [Slot(k.ps(st, f"tbd{i}", [128, 1024], BF16)) for i in range(2)]
        MB = [Slot(k.ps(st, f"mbd{i}", [128, 512])) for i in range(6)]
        mbi = 0
        for i in range(NT):
            t0 = i * 512
            ytt, xt, xn = YTT[i % 2], XT[i % 2], XN[i % 2]
            sc.add("sp", _dma(ytt.t[:], yT3[:, :, t0:t0 + 512]), w=[ytt.b], chan=ytt.c)
            sc.add("sp", _dma(xt.t[:], x[t0:t0 + 512, :].rearrange("(b p) d -> p b d", p=128)), w=[xt.b], chan=xt.c)
            for b in range(4):
                for half in range(2):
                    pm = MB[mbi % 6]
                    mbi += 1
                    for kc in range(8):
                        sc.add("pe", _mm(pm.t[:], ytt.t[:, kc, b * 128:(b + 1) * 128], Wout[:, kc, half * 512:(half + 1) * 512],
                                         kc == 0, kc == 7), r=[ytt.b, Woutb], w=[pm.b])
                    sc.add("dve", _tt(xt.t[:, b, half * 512:(half + 1) * 512], pm.t[:], xt.t[:, b, half * 512:(half + 1) * 512], ALU.add),
                           r=[pm.b, xt.b], w=[xt.b])
            sc.add("pool", _dma(h1_s[t0:t0 + 512, :].rearrange("(b p) d -> p b d", p=128), xt.t[:]), r=[xt.b], chan=xt.c)
            rms_transpose(xt, ss, ss2, rstd, junk, XB, xn, TBs)
            sc.add("pool", _dma(xn2T3[:, :, 1 + t0:1 + t0 + 512], xn.t[:]), r=[xn.b], chan=xn.c)
        sc.emit()

    TT_ = 256
    with contextlib.ExitStack() as st:
        stage = [Slot(k.sb(st, f"wstagee{i}", [128, 1408]), ld_ch[i]) for i in range(2)]
        gffn = load_cols(st, "gffn", ln_ffn_g, 8)
        Wup, Wupb = load_w(st, stage, "Wup", w_up, D, 2 * D_FF, gffn)
        Wdn, Wdnb = load_w(st, stage, "Wdn", w_down, D_FF, D)
        fw = k.sb(st, "fw", [128, 44, 3])
        fwb = Buf("fw")
        for tap in range(3):
            sc.add("sp", _dma(fw[:, :, tap], conv_ffn_w[tap].rearrange("(g p) -> p g", p=128),
                              allow_slow_non_contiguous=True), w=[Buf()], chan=sc.chan(f"c_fw{tap}"))
        fb = load_cols(st, "fb", conv_ffn_b, 44)
        sc.barrier()
        XS = [Slot(k.sb(st, f"xs{i}", [128, 8, TT_ + 2], BF16), sc.chan(f"c_xs{i}")) for i in range(2)]
        H1 = [Slot(k.sb(st, f"h1t{i}", [128, 2, D]), sc.chan(f"c_h1t{i}")) for i in range(2)]
        AT = [Slot(k.sb(st, f"at{i}", [128, 22, TT_], BF16)) for i in range(2)]
        CG = [Slot(k.sb(st, f"cg{i}", [128, TT_])) for i in range(2)]
        CV = [Slot(k.sb(st, f"cv{i}", [128, TT_])) for i in range(2)]
        SG = [Slot(k.sb(st, f"sg{i}", [128, TT_])) for i in range(2)]
        MB = [Slot(k.ps(st, f"mbe{i}", [128, 512])) for i in range(8)]
        mbi = 0
        for i in range(S // TT_):
            t0 = i * TT_
            xs, h1, at = XS[i % 2], H1[i % 2], AT[i % 2]
            sc.add("sp", _dma(xs.t[:], xn2T3[:, :, t0:t0 + TT_ + 2]), w=[xs.b], chan=xs.c)
            sc.add("sp", _dma(h1.t[:], h1_s[t0:t0 + TT_, :].rearrange("(b p) d -> p b d", p=128)), w=[h1.b], chan=h1.c)
            for g in range(22):
                res = []
                for which, (gi, dst) in enumerate(((g, CG[g % 2]), (22 + g, CV[g % 2]))):
                    pm = MB[mbi % 8]
                    mbi += 1
                    for kc in range(8):
                        sc.add("pe", _mm(pm.t[:, 0:TT_ + 2], Wup[:, kc, gi * 128:(gi + 1) * 128], xs.t[:, kc, :], kc == 0, kc == 7),
                               r=[xs.b, Wupb], w=[pm.b])
                    sc.add("act", _act(dst.t[:], pm.t[:, 0:TT_], AF.Identity, scale=fw[:, gi, 0:1], bias=fb[0][:, gi:gi + 1]),
                           r=[pm.b], w=[dst.b])
                    sc.add("dve", _stt(dst.t[:], pm.t[:, 1:TT_ + 1], fw[:, gi, 1:2], dst.t[:], ALU.mult, ALU.add), r=[pm.b, dst.b], w=[dst.b])
                    sc.add("dve", _stt(dst.t[:], pm.t[:, 2:TT_ + 2], fw[:, gi, 2:3], dst.t[:], ALU.mult, ALU.add), r=[pm.b, dst.b], w=[dst.b])
                cg, cv, sg = CG[g % 2], CV[g % 2], SG[g % 2]
                sc.add("act", _act(sg.t[:], cg.t[:], AF.Silu), r=[cg.b], w=[sg.b])
                sc.add("pool", _tt(at.t[:, g, :], sg.t[:], cv.t[:], ALU.mult), r=[sg.b, cv.b], w=[at.b])
            for b in range(TT_ // 128):
                for half in range(2):
                    pm = MB[mbi % 8]
                    mbi += 1
                    for g in range(22):
                        sc.add("pe", _mm(pm.t[:], at.t[:, g, b * 128:(b + 1) * 128], Wdn[:, g, half * 512:(half + 1) * 512], g == 0, g == 21),
                               r=[at.b, Wdnb], w=[pm.b])
                    sc.add("dve", _tt(h1.t[:, b, half * 512:(half + 1) * 512], pm.t[:], h1.t[:, b, half * 512:(half + 1) * 512], ALU.add),
                           r=[pm.b, h1.b], w=[h1.b])
            sc.add("pool", _dma(h2_s[t0:t0 + TT_, :].rearrange("(b p) d -> p b d", p=128), h1.t[:]), r=[h1.b], chan=h1.c)
        sc.emit()

    with contextlib.ExitStack() as st:
        stage = [Slot(k.sb(st, f"wstagef{i}", [128, 1024]), ld_ch[i]) for i in range(2)]
        gple = load_cols(st, "gple", ple_norm_g, 8)
        Wg, Wgb = load_w(st, stage, "Wg", w_ple_gate, D, D, gple)
        Wp, Wpb = load_w(st, stage, "Wp", w_ple_proj, 256, D)
        postg = load_bcast(st, "postg", ple_post_g, D)
        fing = load_bcast(st, "fing", final_g, D)
        sc.barrier()
        XT = [Slot(k.sb(st, f"xtf{i}", [128, 4, D]), sc.chan(f"c_xtf{i}")) for i in range(2)]
        PTL = [Slot(k.sb(st, f"ptl{i}", [128, 4, 256]), sc.chan(f"c_ptl{i}")) for i in range(2)]
        XB = Slot(k.sb(st, "xbf", [128, 4, D], BF16))
        XN = Slot(k.sb(st, "xnf", [128, 8, 512], BF16))
        PBf = Slot(k.sb(st, "pbf", [128, 4, 256], BF16))
        PTT = Slot(k.sb(st, "ptt", [128, 2, 512], BF16))
        junk = Slot(k.sb(st, "junkf", [128, D], BF16))
        ss = Slot(k.sb(st, "ssf", [128, 4]))
        ss2 = Slot(k.sb(st, "ss2f", [128, 4]))
        rstd = Slot(k.sb(st, "rstdf", [128, 4]))
        ssb = Slot(k.sb(st, "ssb", [128, 2]))
        ssb2 = Slot(k.sb(st, "ssb2", [128, 2]))
        rsb2 = Slot(k.sb(st, "rsb2", [128, 2]))
        SGM = [Slot(k.sb(st, f"sgm{i}", [128, D])) for i in range(2)]
        PJ = [Slot(k.sb(st, f"pj{i}", [128, D])) for i in range(2)]
        OT = [Slot(k.sb(st, f"ot{i}", [128, D]), sc.chan(f"c_ot{i}")) for i in range(2)]
        TBs = [Slot(k.ps(st, f"tbf{i}", [128, 1024], BF16)) for i in range(2)]
        MB = [Slot(k.ps(st, f"mbf{i}", [128, 512])) for i in range(6)]
        mbi = 0
        bi_ = 0
        for i in range(NT):
            t0 = i * 512
            xt, ptl = XT[i % 2], PTL[i % 2]
            sc.add("sp", _dma(xt.t[:], h2_s[t0:t0 + 512, :].rearrange("(b p) d -> p b d", p=128)), w=[xt.b], chan=xt.c)
            sc.add("sp", _dma(ptl.t[:], p_in[t0:t0 + 512, :].rearrange("(b p) d -> p b d", p=128)), w=[ptl.b], chan=ptl.c)
            rms_transpose(xt, ss, ss2, rstd, junk, XB, XN, TBs)
            sc.add("dve", _cp(PBf.t[:], ptl.t[:]), r=[ptl.b], w=[PBf.b])
            tb = TBs[0]
            for kc in range(2):
                for b in range(4):
                    sc.add("pe", _tr(tb.t[:, kc * 512 + b * 128:kc * 512 + (b + 1) * 128], PBf.t[:, b, kc * 128:(kc + 1) * 128], ident[:]),
                           r=[PBf.b], w=[tb.b])
            sc.add("act", _act(PTT.t[:], tb.t[:].rearrange("p (a b) -> p a b", a=2), AF.Copy), r=[tb.b], w=[PTT.b])
            for b in range(4):
                sgm, pj, ot = SGM[bi_ % 2], PJ[bi_ % 2], OT[bi_ % 2]
                bi_ += 1
                tok = slice(b * 128, (b + 1) * 128)
                for half in range(2):
                    hs = slice(half * 512, (half + 1) * 512)
                    pm = MB[mbi % 6]
                    mbi += 1
                    for kc in range(8):
                        sc.add("pe", _mm(pm.t[:], XN.t[:, kc, tok], Wg[:, kc, hs], kc == 0, kc == 7), r=[XN.b, Wgb], w=[pm.b])
                    sc.add("act", _act(sgm.t[:, hs], pm.t[:], AF.Sigmoid), r=[pm.b], w=[sgm.b])
                    pm = MB[mbi % 6]
                    mbi += 1
                    for kc in range(2):
                        sc.add("pe", _mm(pm.t[:], PTT.t[:, kc, tok], Wp[:, kc, hs], kc == 0, kc == 1), r=[PTT.b, Wpb], w=[pm.b])
                    sc.add("act", _act(pj.t[:, hs], pm.t[:], AF.Copy), r=[pm.b], w=[pj.b])
                sc.add("act", _act(junk.t[:], pj.t[:], AF.Square, scale=1.0 / 32.0, accum_out=ssb.t[:, 0:1]), r=[pj.b], w=[junk.b, ssb.b])
                sc.add("dve", _ts(ssb2.t[:, 0:1], ssb.t[:, 0:1], EPS, None, ALU.add), r=[ssb.b], w=[ssb2.b])
                sc.add("pool", _tt(rsb2.t[:, 0:1], ssb2.t[:, 0:1], mhalf[:, 0:1], ALU.pow), r=[ssb2.b], w=[rsb2.b])
                sc.add("dve", _tt(pj.t[:], pj.t[:], postg[0][:], ALU.mult), r=[pj.b, postg[1]], w=[pj.b])
                sc.add("dve", _stt(pj.t[:], pj.t[:], rsb2.t[:, 0:1], sgm.t[:], ALU.mult, ALU.mult), r=[pj.b, rsb2.b, sgm.b], w=[pj.b])
                sc.add("dve", _tt(pj.t[:], pj.t[:], xt.t[:, b, :], ALU.add), r=[pj.b, xt.b], w=[pj.b])
                sc.add("act", _act(junk.t[:], pj.t[:], AF.Square, scale=1.0 / 32.0, accum_out=ssb.t[:, 1:2]), r=[pj.b], w=[junk.b, ssb.b])
                sc.add("dve", _ts(ssb2.t[:, 1:2], ssb.t[:, 1:2], EPS, None, ALU.add), r=[ssb.b], w=[ssb2.b])
                sc.add("pool", _tt(rsb2.t[:, 1:2], ssb2.t[:, 1:2], mhalf[:, 0:1], ALU.pow), r=[ssb2.b], w=[rsb2.b])
                sc.add("dve", _stt(ot.t[:], pj.t[:], rsb2.t[:, 1:2], fing[0][:], ALU.mult, ALU.mult), r=[pj.b, rsb2.b, fing[1]], w=[ot.b])
                sc.add("pool", _dma(out[t0 + b * 128:t0 + (b + 1) * 128, :], ot.t[:]), r=[ot.b], chan=ot.c)
        sc.emit()

    k.final_wait = None
    return k


def finish(k):
    return k.nc


_W_NAMES = ["ln_mix_g", "w_in", "b_gates", "conv_qk_w", "conv_qk_b", "mlstm_norm_g", "q_norm_g", "w_uq", "kv_norm_g",
            "w_ukv", "w_out", "ln_ffn_g", "w_up", "conv_ffn_w", "conv_ffn_b", "w_down", "ple_norm_g", "w_ple_gate",
            "w_ple_proj", "ple_post_g"]


def kernel(**inputs):
    x = np.asarray(inputs["x"])
    p = np.asarray(inputs["p"])
    B, S, _ = x.shape
    nc = finish(build(S))
    shared = {n: np.ascontiguousarray(np.asarray(inputs[n])[0], dtype=np.float32) for n in _W_NAMES}
    shared["final_g"] = np.ascontiguousarray(np.asarray(inputs["final_g"]), dtype=np.float32)
    in_maps = []
    for b in range(B):
        m = dict(shared)
        m["x"] = np.ascontiguousarray(x[b], dtype=np.float32)
        m["p"] = np.ascontiguousarray(p[0, b], dtype=np.float32)
        in_maps.append(m)
    res = run_bass_kernel_spmd(nc, in_maps, core_ids=list(range(B)))
    return np.stack([np.asarray(r["out"]) for r in res.results], axis=0).astype(np.float32)
```

```python
import contextlib
import numpy as np
import concourse.bass as bass
import concourse.mybir as mybir
from concourse.bass_utils import run_bass_kernel_spmd

F32 = mybir.dt.float32
BF16 = mybir.dt.bfloat16
AF = mybir.ActivationFunctionType
ALU = mybir.AluOpType
AX = mybir.AxisListType

D = 1024
NH = 4
IN_COLS = 2512
D_FF = 2816
EPS = 1e-6
SEM_MAX = 30000


class Buf:
    __slots__ = ("name", "lw", "rd")

    def __init__(self, name=""):
        self.name = name
        self.lw = None
        self.rd = []


class Chan:
    __slots__ = ("sem", "count", "last")

    def __init__(self, sem):
        self.sem = sem
        self.count = 0
        self.last = None


class Op:
    __slots__ = ("eng", "fn", "deps", "sig", "signo", "chan", "cval", "done")


class Sched:
    ENGS = ("pe", "act", "dve", "pool", "sp")

    def __init__(self, nc, stack):
        self.nc = nc
        self.stack = stack
        self.ops = []
        self.last_on = {e: None for e in self.ENGS}
        self.pending_bar = {e: [] for e in self.ENGS}
        self.chans = []
        self.cnt = {e: 0 for e in self.ENGS}
        self.sems = {e: [] for e in self.ENGS}
        self.waited = {e: {} for e in self.ENGS}

    def chan(self, name):
        c = Chan(self.stack.enter_context(self.nc.semaphore(name)))
        self.chans.append(c)
        return c

    def add(self, eng, fn, r=(), w=(), chan=None):
        op = Op()
        op.eng, op.fn, op.deps, op.sig, op.signo, op.chan, op.cval = eng, fn, {}, False, 0, chan, 0
        op.done = False
        for b in r:
            if b.lw is not None:
                op.deps[b.lw] = True
        for b in w:
            if b.lw is not None:
                op.deps.setdefault(b.lw, False)
            for q in b.rd:
                op.deps.setdefault(q, False)
        for b in r:
            b.rd.append(op)
        for b in w:
            b.lw = op
            b.rd = []
        if self.pending_bar[eng]:
            for d in self.pending_bar[eng]:
                op.deps[d] = True
            self.pending_bar[eng] = []
        if chan is not None:
            if chan.last is not None:
                op.deps[chan.last] = True
            chan.count += 16
            op.cval = chan.count
            chan.last = op
        op.deps.pop(op, None)
        self.ops.append(op)
        self.last_on[eng] = op
        return op

    def barrier(self):
        lasts = [o for o in self.last_on.values() if o is not None]
        lasts += [c.last for c in self.chans if c.last is not None]
        for e in self.ENGS:
            self.pending_bar[e] = list(lasts)

    def emit(self):
        nc = self.nc
        fin = self.add("sp", lambda e: e.nop())
        for c in self.chans:
            if c.last is not None and not c.last.done:
                fin.deps[c.last] = True
        for e in self.ENGS:
            self.pending_bar[e] = []
        for op in self.ops:
            for d in [d for d in op.deps if d.done]:
                del op.deps[d]
            for d, raw in op.deps.items():
                if d.chan is not None:
                    continue
                if d.eng == op.eng and (op.eng == "pe" or not raw):
                    continue
                d.sig = True
        cnt = self.cnt
        for op in self.ops:
            if op.chan is None and op.sig:
                cnt[op.eng] += 1
                op.signo = cnt[op.eng]
        sems = self.sems
        for e in self.ENGS:
            n = cnt[e] // SEM_MAX + 1
            while len(sems[e]) < n:
                sems[e].append(self.stack.enter_context(nc.semaphore(f"s_{e}{len(sems[e])}")))
        per = {e: [o for o in self.ops if o.eng == e] for e in self.ENGS}
        handles = {"pe": "tensor", "act": "scalar", "dve": "vector", "pool": "gpsimd", "sp": "sync"}

        def run(e, eng):
            waited = self.waited[e]
            for op in per[e]:
                for d, raw in op.deps.items():
                    if d.chan is not None:
                        key, val, sem = ("c", id(d.chan)), d.cval, d.chan.sem
                    else:
                        if d.eng == e and (e == "pe" or not raw):
                            continue
                        j = (d.signo - 1) // SEM_MAX
                        key, val, sem = (d.eng, j), d.signo - j * SEM_MAX, sems[d.eng][j]
                    if waited.get(key, 0) >= val:
                        continue
                    waited[key] = val
                    eng.wait_ge(sem, val)
                ins = op.fn(eng)
                if op.chan is not None:
                    ins.then_inc(op.chan.sem, 16)
                elif op.sig:
                    j = (op.signo - 1) // SEM_MAX
                    ins.then_inc(sems[e][j], 1)

        with nc.Block() as block:
            for e in self.ENGS:
                if per[e]:
                    getattr(block, handles[e])(lambda eng, e=e: run(e, eng))
        for op in self.ops:
            op.done = True
            op.fn = None
            op.deps = {}
        self.ops = []
        self.last_on = {e: None for e in self.ENGS}


def _act(out, in_, func, **kw):
    return lambda e: e.activation(out, in_, func, **kw)


def _ts(out, in0, s1, s2, op0, op1=None):
    if op1 is None:
        return lambda e: e.tensor_scalar(out, in0, s1, None, op0)
    return lambda e: e.tensor_scalar(out, in0, s1, s2, op0, op1)


def _stt(out, in0, sc, in1, op0, op1):
    return lambda e: e.scalar_tensor_tensor(out, in0, sc, in1, op0, op1)


def _tt(out, in0, in1, op):
    return lambda e: e.tensor_tensor(out, in0, in1, op)


def _cp(out, in_):
    return lambda e: e.tensor_copy(out, in_)


def _mm(out, lhsT, rhs, start, stop):
    return lambda e: e.matmul(out, lhsT, rhs, start=start, stop=stop)


def _tr(out, in_, ident):
    return lambda e: e.transpose(out, in_, ident)


def _dma(out, in_, **kw):
    return lambda e: e.dma_start(out=out, in_=in_, **kw)


def _memset(ap, v):
    return lambda e: e.memset(ap, v)


class K:
    def __init__(self, S, dbg=()):
        self.S = S
        self.dbg = dbg
        self.nc = bass.Bass("TRN2", target_bir_lowering=False)
        self.stack = contextlib.ExitStack()
        self.sc = Sched(self.nc, self.stack)

    def sb(self, st, name, shape, dt=F32):
        return st.enter_context(self.nc.sbuf_tensor(name, list(shape), dt))

    def ps(self, st, name, shape, dt=F32):
        return st.enter_context(self.nc.psum_tensor(name, list(shape), dt))

    def dram_in(self, name, shape, dt=F32):
        return self.nc.dram_tensor(name, list(shape), dt, kind="ExternalInput").ap()

    def dram_out(self, name, shape, dt=F32):
        return self.nc.dram_tensor(name, list(shape), dt, kind="ExternalOutput").ap()

    def dram_tmp(self, name, shape, dt=F32):
        if name in self.dbg:
            return self.nc.dram_tensor(name, list(shape), dt, kind="ExternalOutput").ap()
        return self.nc.dram_tensor(name, list(shape), dt).ap()


class Slot:
    def __init__(self, t, chan=None):
        self.t = t
        self.b = Buf()
        self.c = chan


def build(S, dbg=()):
    k = K(S, dbg)
    nc, sc = k.nc, k.sc
    NT, NB = S // 512, S // 128
    top = k.stack

    x = k.dram_in("x", [S, D])
    p_in = k.dram_in("p", [S, 256])
    ln_mix_g = k.dram_in("ln_mix_g", [D])
    w_in = k.dram_in("w_in", [D, IN_COLS])
    b_gates = k.dram_in("b_gates", [16])
    conv_qk_w = k.dram_in("conv_qk_w", [3, 1024])
    conv_qk_b = k.dram_in("conv_qk_b", [1024])
    mlstm_norm_g = k.dram_in("mlstm_norm_g", [512])
    q_norm_g = k.dram_in("q_norm_g", [256])
    w_uq = k.dram_in("w_uq", [256, 768])
    kv_norm_g = k.dram_in("kv_norm_g", [128])
    w_ukv = k.dram_in("w_ukv", [128, 1024])
    w_out = k.dram_in("w_out", [1024, 1024])
    ln_ffn_g = k.dram_in("ln_ffn_g", [D])
    w_up = k.dram_in("w_up", [D, 2 * D_FF])
    conv_ffn_w = k.dram_in("conv_ffn_w", [3, 2 * D_FF])
    conv_ffn_b = k.dram_in("conv_ffn_b", [2 * D_FF])
    w_down = k.dram_in("w_down", [D_FF, D])
    ple_norm_g = k.dram_in("ple_norm_g", [D])
    w_ple_gate = k.dram_in("w_ple_gate", [D, D])
    w_ple_proj = k.dram_in("w_ple_proj", [256, D])
    ple_post_g = k.dram_in("ple_post_g", [D])
    final_g = k.dram_in("final_g", [D])
    out = k.dram_out("out", [S, D])

    qkT = k.dram_tmp("qkT", [1024, S], BF16)
    vo_s = k.dram_tmp("vo_s", [S, 1024])
    g3_s = k.dram_tmp("g3_s", [S, 464])

    ident = k.sb(top, "ident", [128, 128], BF16)
    ones_f = k.sb(top, "ones_f", [128, 128], F32)
    mhalf = k.sb(top, "mhalf", [128, 16], F32)
    cb = Buf("consts")
    sc.add("pool", _memset(ones_f[:], 1.0), w=[cb])
    sc.add("pool", _memset(mhalf[:], -0.5), w=[cb])
    sc.add("pool", lambda e: e.affine_select(ident[:], ones_f[:], [[-1, 128]], ALU.is_equal, 0.0,
                                             base=0, channel_multiplier=1), r=[cb], w=[cb])
    sc.emit()

    ld_ch = [sc.chan(f"ldw{i}") for i in range(2)]
    cnt = {"w": 0, "e": 0}

    def alt():
        cnt["e"] += 1
        return "dve" if cnt["e"] % 2 else "act"

    def scale_cast(eng, out_ap, in_ap, sc_ap=None):
        if eng == "act":
            if sc_ap is None:
                return _act(out_ap, in_ap, AF.Copy)
            return _act(out_ap, in_ap, AF.Copy, scale=sc_ap)
        if sc_ap is None:
            return _cp(out_ap, in_ap)
        return _ts(out_ap, in_ap, sc_ap, None, ALU.mult)

    def load_cols(st, name, src, G):
        t = k.sb(st, name, [128, G])
        b = Buf(name)
        ch = sc.chan("c_" + name)
        sc.add("sp", _dma(t[:], src.rearrange("(g p) -> p g", p=128), allow_slow_non_contiguous=True),
               w=[b], chan=ch)
        return t, b

    def load_bcast(st, name, src, n):
        t = k.sb(st, name, [128, n])
        b = Buf(name)
        ch = sc.chan("c_" + name)
        sc.add("sp", _dma(t[:], bass.AP(src.tensor, src.offset, [[0, 128], [1, n]])), w=[b], chan=ch)
        return t, b

    def load_w(st, stage, name, src, Kdim, cols, gain=None):
        kcn = Kdim // 128
        t = k.sb(st, name, [128, kcn, cols], BF16)
        b = Buf(name)
        for kc in range(kcn):
            sw = stage[0].t.shape[1]
            for c0 in range(0, cols, sw):
                w = min(sw, cols - c0)
                s = stage[cnt["w"] % 2]
                cnt["w"] += 1
                sc.add("sp", _dma(s.t[:, 0:w], src[kc * 128:(kc + 1) * 128, c0:c0 + w]), w=[s.b], chan=s.c)
                g = None if gain is None else gain[0][:, kc:kc + 1]
                rr = [s.b] + ([] if gain is None else [gain[1]])
                e = alt()
                sc.add(e, scale_cast(e, t[:, kc, c0:c0 + w], s.t[:, 0:w], g), r=rr, w=[b])
        return t, b

    with contextlib.ExitStack() as st:
        stage = [Slot(k.sb(st, f"wstage{i}", [128, 2816]), ld_ch[i]) for i in range(2)]
        gmix = load_cols(st, "gmix", ln_mix_g, 8)
        Win, Winb = load_w(st, stage, "Win", w_in, D, IN_COLS, gmix)
        cw = k.sb(st, "cw", [128, 8, 3])
        cwb = Buf("cw")
        for tap in range(3):
            sc.add("sp", _dma(cw[:, :, tap], conv_qk_w[tap].rearrange("(g p) -> p g", p=128),
                              allow_slow_non_contiguous=True), w=[Buf()], chan=sc.chan(f"c_cw{tap}"))
        cbias = load_cols(st, "cbias", conv_qk_b, 8)
        bg = load_bcast(st, "bg", b_gates, 16)
        sc.barrier()

        XT = [Slot(k.sb(st, f"xt{i}", [128, 4, D]), sc.chan(f"c_xt{i}")) for i in range(2)]
        XB = Slot(k.sb(st, "xb", [128, 4, D], BF16))
        XN = [Slot(k.sb(st, f"xn{i}", [128, 8, 512], BF16)) for i in range(2)]
        junk = Slot(k.sb(st, "junk", [128, D], BF16))
        ss = Slot(k.sb(st, "ss", [128, 4]))
        ss2 = Slot(k.sb(st, "ss2", [128, 4]))
        rstd = Slot(k.sb(st, "rstd", [128, 4]))
        PRE = [Slot(k.sb(st, f"pre{g}", [128, 514])) for g in range(8)]
        ACC = [Slot(k.sb(st, f"acc{i}", [128, 512])) for i in range(2)]
        QKB = [Slot(k.sb(st, f"qkb{i}", [128, 512], BF16), sc.chan(f"c_qkb{i}")) for i in range(3)]
        VO = [Slot(k.sb(st, f"vo{i}", [128, 1024]), sc.chan(f"c_vo{i}")) for i in range(2)]
        G3 = [Slot(k.sb(st, f"g3{i}", [128, 464]), sc.chan(f"c_g3{i}")) for i in range(2)]
        TB = [Slot(k.ps(st, f"tb{i}", [128, 1024], BF16)) for i in range(2)]
        MB = [Slot(k.ps(st, f"mb{i}", [128, 512])) for i in range(6)]
        last8 = Slot(k.sb(st, "last8", [128, 8]))
        last8b = Slot(k.sb(st, "last8b", [128, 8], BF16), sc.chan("c_last8"))
        for g in range(8):
            sc.add("pool", _memset(PRE[g].t[:, 0:2], 0.0), w=[PRE[g].b])
        mbi = 0
        qi = 0
        voi = 0
        for i in range(NT):
            t0 = i * 512
            xt = XT[i % 2]
            xn = XN[i % 2]
            sc.add("sp", _dma(xt.t[:], x[t0:t0 + 512, :].rearrange("(b p) d -> p b d", p=128)), w=[xt.b], chan=xt.c)
            for b in range(4):
                sc.add("act", _act(junk.t[:], xt.t[:, b, :], AF.Square, scale=1.0 / 32.0, accum_out=ss.t[:, b:b + 1]),
                       r=[xt.b], w=[junk.b, ss.b])
            sc.add("dve", _ts(ss2.t[:], ss.t[:], EPS, None, ALU.add), r=[ss.b], w=[ss2.b])
            sc.add("pool", _tt(rstd.t[:], ss2.t[:], mhalf[:, 0:4], ALU.pow), r=[ss2.b], w=[rstd.b])
            for b in range(4):
                e = "dve" if b % 2 else "act"
                sc.add(e, scale_cast(e, XB.t[:, b, :], xt.t[:, b, :], rstd.t[:, b:b + 1]), r=[xt.b, rstd.b], w=[XB.b])
            for j in range(4):
                tb = TB[j % 2]
                for kk in range(2):
                    kc = 2 * j + kk
                    for b in range(4):
                        sc.add("pe", _tr(tb.t[:, kk * 512 + b * 128:kk * 512 + (b + 1) * 128],
                                         XB.t[:, b, kc * 128:(kc + 1) * 128], ident[:]), r=[XB.b], w=[tb.b])
                e = "dve" if j % 2 else "act"
                sc.add(e, scale_cast(e, xn.t[:, 2 * j:2 * j + 2, :], tb.t[:].rearrange("p (a b) -> p a b", a=2)),
                       r=[tb.b], w=[xn.b])
            for g in range(8):
                pm = MB[mbi % 6]
                mbi += 1
                for kc in range(8):
                    sc.add("pe", _mm(pm.t[:], Win[:, kc, g * 128:(g + 1) * 128], xn.t[:, kc, :], kc == 0, kc == 7),
                           r=[xn.b, Winb], w=[pm.b])
                pre = PRE[g]
                acc = ACC[g % 2]
                qb = QKB[qi % 3]
                qi += 1
                sc.add("act", _act(pre.t[:, 2:514], pm.t[:], AF.Copy), r=[pm.b], w=[pre.b])
                sc.add("dve", _ts(acc.t[:], pre.t[:, 2:514], cw[:, g, 2:3], None, ALU.mult), r=[pre.b], w=[acc.b])
                sc.add("dve", _stt(acc.t[:], pre.t[:, 1:513], cw[:, g, 1:2], acc.t[:], ALU.mult, ALU.add),
                       r=[pre.b, acc.b], w=[acc.b])
                sc.add("dve", _stt(acc.t[:], pre.t[:, 0:512], cw[:, g, 0:1], acc.t[:], ALU.mult, ALU.add),
                       r=[pre.b, acc.b], w=[acc.b])
                sc.add("act", _act(qb.t[:], acc.t[:], AF.Silu, bias=cbias[0][:, g:g + 1]), r=[acc.b], w=[qb.b])
                if i == 0:
                    sc.add("pool", _dma(qkT[g * 128:(g + 1) * 128, 0:511], qb.t[:, 1:512]), r=[qb.b], chan=qb.c)
                else:
                    sc.add("pool", _dma(qkT[g * 128:(g + 1) * 128, t0 - 1:t0 + 511], qb.t[:]), r=[qb.b], chan=qb.c)
                sc.add("dve", _cp(pre.t[:, 0:2], pre.t[:, 512:514]), r=[pre.b], w=[pre.b])
            for b in range(4):
                vo = VO[voi % 2]
                g3 = G3[voi % 2]
                voi += 1
                for part, (c0, c1) in enumerate(((1024, 1536), (1536, 2048), (2048, 2512))):
                    pm = MB[mbi % 6]
                    mbi += 1
                    for kc in range(8):
                        sc.add("pe", _mm(pm.t[:, 0:c1 - c0], xn.t[:, kc, b * 128:(b + 1) * 128], Win[:, kc, c0:c1],
                                         kc == 0, kc == 7), r=[xn.b, Winb], w=[pm.b])
                    if part == 0:
                        sc.add("dve", _cp(vo.t[:, 0:512], pm.t[:]), r=[pm.b], w=[vo.b])
                    elif part == 1:
                        sc.add("act", _act(vo.t[:, 512:1024], pm.t[:], AF.Tanh, scale=0.5), r=[pm.b], w=[vo.b])
                        sc.add("dve", _ts(vo.t[:, 512:1024], vo.t[:, 512:1024], 0.5, 0.5, ALU.mult, ALU.add),
                               r=[vo.b], w=[vo.b])
                    else:
                        sc.add("act", _act(g3.t[:], pm.t[:, 0:464], AF.Copy), r=[pm.b], w=[g3.b])
                        sc.add("dve", _tt(g3.t[:, 0:16], g3.t[:, 0:16], bg[0][:], ALU.add), r=[g3.b], w=[g3.b])
                r0 = t0 + b * 128
                sc.add("pool", _dma(vo_s[r0:r0 + 128, :], vo.t[:]), r=[vo.b], chan=vo.c)
                sc.add("pool", _dma(g3_s[r0:r0 + 128, :], g3.t[:]), r=[g3.b], chan=g3.c)
        prb = [PRE[g].b for g in range(8)]
        for g in range(8):
            sc.add("dve", _ts(last8.t[:, g:g + 1], PRE[g].t[:, 0:1], cw[:, g, 0:1], None, ALU.mult), r=[PRE[g].b], w=[last8.b])
            sc.add("dve", _stt(last8.t[:, g:g + 1], PRE[g].t[:, 1:2], cw[:, g, 1:2], last8.t[:, g:g + 1], ALU.mult, ALU.add),
                   r=[PRE[g].b, last8.b], w=[last8.b])
        sc.add("dve", _tt(last8.t[:], last8.t[:], cbias[0][:], ALU.add), r=[last8.b], w=[last8.b])
        sc.add("act", _act(last8b.t[:], last8.t[:], AF.Silu), r=[last8.b], w=[last8b.b])
        sc.add("pool", _dma(qkT.rearrange("(g p) s -> p g s", p=128)[:, :, S - 1], last8b.t[:],
                            allow_slow_non_contiguous=True), r=[last8b.b], chan=last8b.c)
        sc.emit()

    hf_s = k.dram_tmp("hf_s", [S, 512])
    yT_s = k.dram_tmp("yT_s", [1024, S], BF16)

    def bc(ap, m):
        a = [list(d) for d in ap.ap]
        return bass.AP(ap.tensor, ap.offset, a + [[0, m]])

    def bc_mid(ap, m):
        a = [list(d) for d in ap.ap]
        return bass.AP(ap.tensor, ap.offset, [a[0], [0, m]] + a[1:])

    with contextlib.ExitStack() as st:
        maskF = k.sb(st, "maskF", [128, 128])
        maskB = k.sb(st, "maskB", [128, 128])
        mb_ = Buf("masks")
        sc.add("pool", lambda e: e.affine_select(maskF[:], ones_f[:], [[1, 128]], ALU.is_ge, 0.0,
                                                 base=0, channel_multiplier=-1), w=[mb_])
        sc.add("pool", lambda e: e.affine_select(maskB[:], ones_f[:], [[-1, 128]], ALU.is_ge, 0.0,
                                                 base=0, channel_multiplier=1), w=[mb_])
        normg = load_bcast(st, "normg", mlstm_norm_g, 512)
        SL = [Slot(k.sb(st, f"sl{i}", [128, 8, 512], BF16), sc.chan(f"c_sl{i}")) for i in range(2)]
        VOT = [Slot(k.sb(st, f"vot{i}", [128, 1024]), sc.chan(f"c_vot{i}")) for i in range(2)]
        GT = [Slot(k.sb(st, f"gt{i}", [128, 16]), sc.chan(f"c_gt{i}")) for i in range(2)]
        HF = [Slot(k.sb(st, f"hf{i}", [128, 512]), sc.chan(f"c_hf{i}")) for i in range(2)]
        e1 = Slot(k.sb(st, "e1", [128, 4]))
        lsp = Slot(k.sb(st, "lsp", [128, 4]))
        tmpa = Slot(k.sb(st, "tmpa", [128, 4]))
        AA = [Slot(k.sb(st, f"aa{i}", [128, 4])) for i in range(2)]
        EG = [Slot(k.sb(st, f"eg{i}", [128, 8])) for i in range(2)]
        V1 = [Slot(k.sb(st, f"v1{i}", [128, 4, 130], BF16)) for i in range(2)]
        KTOK = [Slot(k.sb(st, f"ktok{i}", [128, 4, 128], BF16)) for i in range(2)]
        MM = [Slot(k.sb(st, f"mm{i}", [128, 4, 128], BF16)) for i in range(2)]
        C1 = Slot(k.sb(st, "c1", [128, 4, 130]))
        C1b = Slot(k.sb(st, "c1b", [128, 4, 130], BF16))
        tmpC = Slot(k.sb(st, "tmpc", [128, 4, 130]))
        den = Slot(k.sb(st, "den", [128, 4]))
        rr_ = Slot(k.sb(st, "rr", [128, 4]))
        hsum = Slot(k.sb(st, "hsum", [128, 512]))
        sq = Slot(k.sb(st, "sq", [128, 512]))
        ssn = Slot(k.sb(st, "ssn", [128, 4]))
        rs = Slot(k.sb(st, "rs", [128, 4]))
        yb = Slot(k.sb(st, "yb", [128, 512], BF16))
        YT = [Slot(k.sb(st, f"yt{i}", [128, 4, 512], BF16), sc.chan(f"c_yt{i}")) for i in range(2)]
        TBK = Slot(k.ps(st, "tbk", [128, 1024], BF16))
        SPS = [Slot(k.ps(st, f"sps{i}", [128, 512])) for i in range(2)]
        UPS = Slot(k.ps(st, "ups", [128, 1024]))
        DPS = Slot(k.ps(st, "dps", [128, 1024]))
        GPS = Slot(k.ps(st, "gps", [128, 512]))
        qkT3 = qkT.rearrange("(g p) s -> p g s", p=128)
        yT3 = yT_s.rearrange("(g p) s -> p g s", p=128)
        state = {"slab": None, "n": 0}

        def pre(c, dirn, n):
            cg = c // 4
            if state["slab"] != (dirn, cg):
                state["slab"] = (dirn, cg)
                state["n"] += 1
                sl = SL[state["n"] % 2]
                sc.add("sp", _dma(sl.t[:], qkT3[:, :, cg * 512:(cg + 1) * 512]), w=[sl.b], chan=sl.c)
            sl = SL[state["n"] % 2]
            vot, gt, aa, eg, v1, ktok, mm, sps = VOT[n % 2], GT[n % 2], AA[n % 2], EG[n % 2], V1[n % 2], KTOK[n % 2], MM[n % 2], SPS[n % 2]
            r0 = c * 128
            sc.add("sp", _dma(vot.t[:], vo_s[r0:r0 + 128, :]), w=[vot.b], chan=vot.c)
            sc.add("sp", _dma(gt.t[:], g3_s[r0:r0 + 128, 0:16]), w=[gt.b], chan=gt.c)
            io, fo = dirn * 8, dirn * 8 + 4
            tri = maskF if dirn == 0 else maskB
            sc.add("act", _act(e1.t[:], gt.t[:, fo:fo + 4], AF.Exp, scale=-1.0), r=[gt.b], w=[e1.b])
            sc.add("act", _act(lsp.t[:], e1.t[:], AF.Ln, bias=1.0), r=[e1.b], w=[lsp.b])
            sc.add("pe", _mm(GPS.t[:, 0:4], tri[:], lsp.t[:], True, True), r=[lsp.b, mb_], w=[GPS.b])
            sc.add("pe", _mm(GPS.t[:, 4:8], ones_f[:], lsp.t[:], True, True), r=[lsp.b], w=[GPS.b])
            sc.add("dve", _tt(tmpa.t[:], gt.t[:, io:io + 4], GPS.t[:, 0:4], ALU.add), r=[gt.b, GPS.b], w=[tmpa.b])
            sc.add("act", _act(aa.t[:], tmpa.t[:], AF.Exp), r=[tmpa.b], w=[aa.b])
            sc.add("act", _act(eg.t[:], GPS.t[:, 0:8], AF.Exp, scale=-1.0), r=[GPS.b], w=[eg.b])
            sc.add("dve", _tt(v1.t[:, :, 0:128], vot.t[:, 0:512].rearrange("p (h d) -> p h d", h=4), bc(aa.t[:], 128), ALU.mult),
                   r=[vot.b, aa.b], w=[v1.b])
            sc.add("dve", _cp(v1.t[:, :, 128], aa.t[:]), r=[aa.b], w=[v1.b])
            c4 = (c % 4) * 128
            for h in range(4):
                sc.add("pe", _tr(TBK.t[:, h * 128:(h + 1) * 128], sl.t[:, 4 + h, c4:c4 + 128], ident[:]), r=[sl.b], w=[TBK.b])
            sc.add("act", _act(ktok.t[:], TBK.t[:, 0:512].rearrange("p (h d) -> p h d", h=4), AF.Copy, scale=128.0 ** -0.5),
                   r=[TBK.b], w=[ktok.b])
            for h in range(4):
                sc.add("pe", _mm(sps.t[:, h * 128:(h + 1) * 128], sl.t[:, 4 + h, c4:c4 + 128], sl.t[:, h, c4:c4 + 128], True, True),
                       r=[sl.b], w=[sps.b])
            sc.add("dve", _stt(mm.t[:], sps.t[:].rearrange("p (h d) -> p h d", h=4), 128.0 ** -0.5, bc_mid(tri[:], 4), ALU.mult, ALU.mult),
                   r=[sps.b, mb_], w=[mm.b])
            return sl, c4

        def main(c, dirn, n, sl, c4):
            vot, eg, v1, ktok, mm = VOT[n % 2], EG[n % 2], V1[n % 2], KTOK[n % 2], MM[n % 2]
            hf = HF[n % 2]
            r0 = c * 128
            if dirn == 1:
                sc.add("sp", _dma(hf.t[:], hf_s[r0:r0 + 128, :]), w=[hf.b], chan=hf.c)
            for h in range(4):
                sc.add("pe", _mm(UPS.t[:, h * 256:h * 256 + 129], mm.t[:, h, :], v1.t[:, h, 0:129], True, False), r=[mm.b, v1.b], w=[UPS.b])
                sc.add("pe", _mm(UPS.t[:, h * 256:h * 256 + 129], sl.t[:, h, c4:c4 + 128], C1b.t[:, h, 0:129], False, True),
                       r=[sl.b, C1b.b], w=[UPS.b])
            for h in range(4):
                sc.add("pe", _mm(DPS.t[:, h * 256:h * 256 + 129], ktok.t[:, h, :], v1.t[:, h, 0:129], True, True), r=[ktok.b, v1.b], w=[DPS.b])
            U3 = UPS.t[:].rearrange("p (h d) -> p h d", h=4)
            D3 = DPS.t[:].rearrange("p (h d) -> p h d", h=4)
            sc.add("dve", _tt(tmpC.t[:, :, 0:129], D3[:, :, 0:129], C1.t[:, :, 0:129], ALU.add), r=[DPS.b, C1.b], w=[tmpC.b])
            sc.add("dve", _tt(C1.t[:, :, 0:129], tmpC.t[:, :, 0:129], bc(eg.t[:, 4:8], 129), ALU.mult), r=[tmpC.b, eg.b], w=[C1.b])
            sc.add("act", _act(C1b.t[:, :, 0:129], C1.t[:, :, 0:129], AF.Copy), r=[C1.b], w=[C1b.b])
            sc.add("dve", _tt(den.t[:], U3[:, :, 128], eg.t[:, 0:4], ALU.mult), r=[UPS.b, eg.b], w=[den.b])
            sc.add("act", _act(den.t[:], den.t[:], AF.Abs), r=[den.b], w=[den.b])
            sc.add("dve", _ts(den.t[:], den.t[:], 1.0, None, ALU.max), r=[den.b], w=[den.b])
            sc.add("dve", lambda e: e.reciprocal(rr_.t[:], den.t[:]), r=[den.b], w=[rr_.b])
            sc.add("dve", _tt(rr_.t[:], rr_.t[:], eg.t[:, 0:4], ALU.mult), r=[rr_.b, eg.b], w=[rr_.b])
            if dirn == 0:
                sc.add("dve", _tt(hf.t[:].rearrange("p (h d) -> p h d", h=4), U3[:, :, 0:128], bc(rr_.t[:], 128), ALU.mult),
                       r=[UPS.b, rr_.b], w=[hf.b])
                sc.add("pool", _dma(hf_s[r0:r0 + 128, :], hf.t[:]), r=[hf.b], chan=hf.c)
                return
            sc.add("dve", _tt(hsum.t[:].rearrange("p (h d) -> p h d", h=4), U3[:, :, 0:128], bc(rr_.t[:], 128), ALU.mult),
                   r=[UPS.b, rr_.b], w=[hsum.b])
            sc.add("dve", _tt(hsum.t[:], hsum.t[:], hf.t[:], ALU.add), r=[hsum.b, hf.b], w=[hsum.b])
            sc.add("act", _act(sq.t[:], hsum.t[:], AF.Square, scale=128.0 ** -0.5), r=[hsum.b], w=[sq.b])
            sc.add("dve", lambda e: e.tensor_reduce(ssn.t[:], sq.t[:].rearrange("p (h d) -> p h d", h=4), AX.X, ALU.add),
                   r=[sq.b], w=[ssn.b])
            sc.add("dve", _ts(ssn.t[:], ssn.t[:], EPS, None, ALU.add), r=[ssn.b], w=[ssn.b])
            sc.add("pool", _tt(rs.t[:], ssn.t[:], mhalf[:, 0:4], ALU.pow), r=[ssn.b], w=[rs.b])
            sc.add("dve", _tt(hsum.t[:].rearrange("p (h d) -> p h d", h=4), hsum.t[:].rearrange("p (h d) -> p h d", h=4),
                              bc(rs.t[:], 128), ALU.mult), r=[hsum.b, rs.b], w=[hsum.b])
            sc.add("dve", _tt(hsum.t[:], hsum.t[:], normg[0][:], ALU.mult), r=[hsum.b, normg[1]], w=[hsum.b])
            sc.add("dve", _tt(yb.t[:], hsum.t[:], vot.t[:, 512:1024], ALU.mult), r=[hsum.b, vot.b], w=[yb.b])
            yt = YT[(c // 4) % 2]
            for h in range(4):
                sc.add("pe", _tr(TBK.t[:, 512 + h * 128:512 + (h + 1) * 128], yb.t[:, h * 128:(h + 1) * 128], ident[:]), r=[yb.b], w=[TBK.b])
            sc.add("act", _act(yt.t[:, :, c4:c4 + 128], TBK.t[:, 512:1024].rearrange("p (h d) -> p h d", h=4), AF.Copy),
                   r=[TBK.b], w=[yt.b])
            if c % 4 == 0:
                cg = c // 4
                sc.add("pool", _dma(yT3[:, 0:4, cg * 512:(cg + 1) * 512], yt.t[:]), r=[yt.b], chan=yt.c)

        n = 0
        for dirn in range(2):
            order = list(range(NB)) if dirn == 0 else list(range(NB - 1, -1, -1))
            if dirn == 1:
                sc.barrier()
            sc.add("pool", _memset(C1.t[:], 0.0), w=[C1.b])
            sc.add("pool", _memset(C1b.t[:], 0.0), w=[C1b.b])
            nxt = pre(order[0], dirn, n)
            for j, c in enumerate(order):
                cur = nxt
                if j + 1 < NB:
                    nxt = pre(order[j + 1], dirn, n + 1)
                main(c, dirn, n, *cur)
                n += 1
        sc.emit()

    KT_s = k.dram_tmp("KT_s", [512, S], BF16)
    KR_s = k.dram_tmp("KR_s", [64, S], BF16)
    V_s = k.dram_tmp("V_s", [S, 512], BF16)
    QN_s = k.dram_tmp("QN_s", [512, S], BF16)
    QR_s = k.dram_tmp("QR_s", [4, 65, S], BF16)
    kmax_s = k.dram_tmp("kmax_s", [128, 4])
    TWO_PI = 6.283185307179586

    with contextlib.ExitStack() as st:
        stage = [Slot(k.sb(st, f"wstagec{i}", [128, 1024]), ld_ch[i]) for i in range(2)]
        gq = load_cols(st, "gq", q_norm_g, 2)
        gkv = load_cols(st, "gkv", kv_norm_g, 1)
        Wuq, Wuqb = load_w(st, stage, "Wuq", w_uq, 256, 768, gq)
        Wkv, Wkvb = load_w(st, stage, "Wkv", w_ukv, 128, 1024, gkv)
        Wkv4 = Wkv[:, 0, :].rearrange("p (h t d) -> p h t d", h=4, t=2)
        cos2 = k.sb(st, "cos2", [128, NB, 64])
        sin1 = k.sb(st, "sin1", [128, NB, 32])
        tb_ = Buf("ropetab")
        pos = k.sb(st, "pos", [128, NB])
        invf = k.sb(st, "invf", [128, 32])
        ang = k.sb(st, "ang", [128, NB, 32])
        angi = k.sb(st, "angi", [128, NB, 32], mybir.dt.int32)
        angf = k.sb(st, "angf", [128, NB, 32])
        msk = k.sb(st, "msk", [128, NB, 32])
        sc.add("pool", lambda e: e.iota(pos[:], [[128, NB]], base=0, channel_multiplier=1, allow_small_or_imprecise_dtypes=True), w=[tb_])
        sc.add("pool", lambda e: e.iota(invf[:], [[1, 32]], base=0, channel_multiplier=0, allow_small_or_imprecise_dtypes=True), r=[tb_], w=[tb_])
        sc.add("act", _act(invf[:], invf[:], AF.Exp, scale=-float(np.log(10000.0)) / 32.0), r=[tb_], w=[tb_])
        sc.add("dve", _tt(ang[:], bc(pos[:], 32), bc_mid(invf[:], NB), ALU.mult), r=[tb_], w=[tb_])
        sc.add("dve", _ts(ang[:], ang[:], 1.0 / TWO_PI, None, ALU.mult), r=[tb_], w=[tb_])
        for which in range(2):
            if which == 1:
                sc.add("dve", _ts(ang[:], ang[:], 0.25, None, ALU.add), r=[tb_], w=[tb_])
            sc.add("dve", _cp(angi[:], ang[:]), r=[tb_], w=[tb_])
            sc.add("dve", _cp(angf[:], angi[:]), r=[tb_], w=[tb_])
            sc.add("dve", _tt(angf[:], ang[:], angf[:], ALU.subtract), r=[tb_], w=[tb_])
            sc.add("dve", _ts(msk[:], angf[:], 0.5, None, ALU.is_gt), r=[tb_], w=[tb_])
            sc.add("dve", _tt(angf[:], angf[:], msk[:], ALU.subtract), r=[tb_], w=[tb_])
            sc.add("dve", _ts(msk[:], angf[:], -0.5, None, ALU.is_lt), r=[tb_], w=[tb_])
            sc.add("dve", _tt(angf[:], angf[:], msk[:], ALU.add), r=[tb_], w=[tb_])
            if which == 0:
                sc.add("act", _act(sin1[:], angf[:], AF.Sin, scale=TWO_PI * (1.0 - 1e-6)), r=[tb_], w=[tb_])
            else:
                sc.add("act", _act(cos2[:, :, 0:32], angf[:], AF.Sin, scale=TWO_PI * (1.0 - 1e-6)), r=[tb_], w=[tb_])
                sc.add("act", _act(cos2[:, :, 32:64], angf[:], AF.Sin, scale=TWO_PI * (1.0 - 1e-6)), r=[tb_], w=[tb_])
        sc.barrier()

        G3T = [Slot(k.sb(st, f"g3t{i}", [128, 4, 464]), sc.chan(f"c_g3t{i}")) for i in range(2)]
        junkc = Slot(k.sb(st, "junkc", [128, 3072], BF16))
        ssq = Slot(k.sb(st, "ssq", [128, 8]))
        ssq2 = Slot(k.sb(st, "ssq2", [128, 8]))
        rst = Slot(k.sb(st, "rst", [128, 8]))
        cqn = Slot(k.sb(st, "cqn", [128, 4, 256], BF16))
        ckvn = Slot(k.sb(st, "ckvn", [128, 4, 128], BF16))
        tA = Slot(k.sb(st, "tA", [128, 4, 64]))
        tB = Slot(k.sb(st, "tB", [128, 4, 64]))
        krb = Slot(k.sb(st, "krb", [128, 4, 64], BF16))
        sqr = Slot(k.sb(st, "sqr", [128, 4, 64]))
        kr2 = Slot(k.sb(st, "kr2", [128, 4]))
        cqT = Slot(k.sb(st, "cqT", [128, 2, 512], BF16))
        ckvT = Slot(k.sb(st, "ckvT", [128, 512], BF16))
        krT = Slot(k.sb(st, "krT", [64, 512], BF16), sc.chan("c_krT"))
        KTS = [Slot(k.sb(st, f"kts{i}", [128, 512], BF16), sc.chan(f"c_kts{i}")) for i in range(2)]
        VS = [Slot(k.sb(st, f"vs{i}", [128, 512], BF16), sc.chan(f"c_vs{i}")) for i in range(2)]
        sqk = Slot(k.sb(st, "sqk", [128, 512]))
        kn2 = Slot(k.sb(st, "kn2", [128, 4, 4]))
        kmax = Slot(k.sb(st, "kmax", [128, 4]))
        kmt = Slot(k.sb(st, "kmt", [128, 4]))
        q_sb = Slot(k.sb(st, "q_sb", [128, 4, 768]))
        qtA = Slot(k.sb(st, "qtA", [128, 4, 4, 64]))
        qtB = Slot(k.sb(st, "qtB", [128, 4, 4, 64]))
        qbn = Slot(k.sb(st, "qbn", [128, 4, 4, 128], BF16))
        qbr = Slot(k.sb(st, "qbr", [128, 4, 4, 66], BF16))
        qn2 = Slot(k.sb(st, "qn2", [128, 16]))
        qn1 = Slot(k.sb(st, "qn1", [128, 16]))
        QS = [Slot(k.sb(st, f"qs{i}", [128, 2, 512], BF16), sc.chan(f"c_qs{i}")) for i in range(2)]
        QRS = [Slot(k.sb(st, f"qrs{i}", [65, 2, 512], BF16), sc.chan(f"c_qrs{i}")) for i in range(2)]
        PB = [Slot(k.ps(st, f"pb{i}", [128, 512])) for i in range(8)]
        pbi = {"i": 0}

        def bank():
            pbi["i"] += 1
            return PB[pbi["i"] % 8]

        def bfv(slot):
            return slot.t[:].bitcast(BF16)

        sc.add("pool", _memset(kmax.t[:], 0.0), w=[kmax.b])
        sc.add("pool", _memset(qbr.t[:], 0.0), w=[qbr.b])
        q4 = q_sb.t[:].rearrange("p b (h d) -> p b h d", h=4)
        kti = 0
        for i in range(NT):
            t0 = i * 512
            g3 = G3T[i % 2]
            sc.add("sp", _dma(g3.t[:], g3_s[t0:t0 + 512, :].rearrange("(b p) c -> p b c", p=128)), w=[g3.b], chan=g3.c)
            for b in range(4):
                sc.add("act", _act(junkc.t[:, 0:256], g3.t[:, b, 16:272], AF.Square, scale=1.0 / 16.0, accum_out=ssq.t[:, b:b + 1]),
                       r=[g3.b], w=[junkc.b, ssq.b])
                sc.add("act", _act(junkc.t[:, 0:128], g3.t[:, b, 272:400], AF.Square, scale=128.0 ** -0.5, accum_out=ssq.t[:, 4 + b:5 + b]),
                       r=[g3.b], w=[junkc.b, ssq.b])
            sc.add("dve", _ts(ssq2.t[:], ssq.t[:], EPS, None, ALU.add), r=[ssq.b], w=[ssq2.b])
            sc.add("pool", _tt(rst.t[:], ssq2.t[:], mhalf[:, 0:8], ALU.pow), r=[ssq2.b], w=[rst.b])
            sc.add("dve", _tt(cqn.t[:], g3.t[:, :, 16:272], bc(rst.t[:, 0:4], 256), ALU.mult), r=[g3.b, rst.b], w=[cqn.b])
            sc.add("dve", _tt(ckvn.t[:], g3.t[:, :, 272:400], bc(rst.t[:, 4:8], 128), ALU.mult), r=[g3.b, rst.b], w=[ckvn.b])
            xk = g3.t[:, :, 400:464]
            cs, sn = cos2[:, 4 * i:4 * i + 4, :], sin1[:, 4 * i:4 * i + 4, :]
            sc.add("dve", _tt(tA.t[:], xk, cs, ALU.mult), r=[g3.b], w=[tA.b])
            sc.add("dve", _tt(tB.t[:, :, 0:32], g3.t[:, :, 432:464], sn, ALU.mult), r=[g3.b], w=[tB.b])
            sc.add("dve", _tt(tB.t[:, :, 32:64], g3.t[:, :, 400:432], sn, ALU.mult), r=[g3.b], w=[tB.b])
            sc.add("dve", _tt(krb.t[:, :, 0:32], tA.t[:, :, 0:32], tB.t[:, :, 0:32], ALU.subtract), r=[tA.b, tB.b], w=[krb.b])
            sc.add("dve", _tt(krb.t[:, :, 32:64], tA.t[:, :, 32:64], tB.t[:, :, 32:64], ALU.add), r=[tA.b, tB.b], w=[krb.b])
            sc.add("act", _act(sqr.t[:], xk, AF.Square), r=[g3.b], w=[sqr.b])
            sc.add("dve", lambda e: e.tensor_reduce(kr2.t[:], sqr.t[:], AX.X, ALU.add), r=[sqr.b], w=[kr2.b])
            pa, pb2 = bank(), bank()
            for b in range(4):
                for kc in range(2):
                    sc.add("pe", _tr(bfv(pa)[:, kc * 512 + b * 128:kc * 512 + (b + 1) * 128], cqn.t[:, b, kc * 128:(kc + 1) * 128], ident[:]),
                           r=[cqn.b], w=[pa.b])
                sc.add("pe", _tr(bfv(pb2)[:, b * 128:(b + 1) * 128], ckvn.t[:, b, :], ident[:]), r=[ckvn.b], w=[pb2.b])
                sc.add("pe", _tr(bfv(pb2)[0:64, 512 + b * 128:512 + (b + 1) * 128], krb.t[:, b, :], ident[:]), r=[krb.b], w=[pb2.b])
            sc.add("act", _act(cqT.t[:], bfv(pa).rearrange("p (a b) -> p a b", a=2), AF.Copy), r=[pa.b], w=[cqT.b])
            sc.add("dve", _cp(ckvT.t[:], bfv(pb2)[:, 0:512]), r=[pb2.b], w=[ckvT.b])
            sc.add("act", _act(krT.t[:], bfv(pb2)[0:64, 512:1024], AF.Copy), r=[pb2.b], w=[krT.b])
            sc.add("pool", _dma(KR_s[:, t0:t0 + 512], krT.t[:]), r=[krT.b], chan=krT.c)
            for h in range(4):
                pk = bank()
                sc.add("pe", _mm(pk.t[:], Wkv[:, 0, h * 256:h * 256 + 128], ckvT.t[:], True, True), r=[ckvT.b, Wkvb], w=[pk.b])
                kts = KTS[kti % 2]
                kti += 1
                e = "act" if h % 2 else "dve"
                sc.add(e, scale_cast(e, kts.t[:], pk.t[:]), r=[pk.b], w=[kts.b])
                sc.add("pool", _dma(KT_s[h * 128:(h + 1) * 128, t0:t0 + 512], kts.t[:]), r=[kts.b], chan=kts.c)
            for b in range(4):
                tok = slice(b * 128, (b + 1) * 128)
                pv = bank()
                sc.add("pe", _mm(pv.t[:].rearrange("p (h d) -> p h d", h=4), ckvT.t[:, tok], Wkv4[:, :, 1, :], True, True),
                       r=[ckvT.b, Wkvb], w=[pv.b])
                vs = VS[b % 2]
                sc.add("act", _act(vs.t[:], pv.t[:], AF.Copy), r=[pv.b], w=[vs.b])
                sc.add("pool", _dma(V_s[t0 + b * 128:t0 + (b + 1) * 128, :], vs.t[:]), r=[vs.b], chan=vs.c)
                pk = bank()
                sc.add("pe", _mm(pk.t[:].rearrange("p (h d) -> p h d", h=4), ckvT.t[:, tok], Wkv4[:, :, 0, :], True, True),
                       r=[ckvT.b, Wkvb], w=[pk.b])
                sc.add("act", _act(sqk.t[:], pk.t[:], AF.Square), r=[pk.b], w=[sqk.b])
                sc.add("dve", lambda e, b=b: e.tensor_reduce(kn2.t[:, b, :], sqk.t[:].rearrange("p (h d) -> p h d", h=4), AX.X, ALU.add),
                       r=[sqk.b], w=[kn2.b])
                pq0, pq1 = bank(), bank()
                for kc in range(2):
                    sc.add("pe", _mm(pq0.t[:], cqT.t[:, kc, tok], Wuq[:, kc, 0:512], kc == 0, kc == 1), r=[cqT.b, Wuqb], w=[pq0.b])
                for kc in range(2):
                    sc.add("pe", _mm(pq1.t[:, 0:256], cqT.t[:, kc, tok], Wuq[:, kc, 512:768], kc == 0, kc == 1), r=[cqT.b, Wuqb], w=[pq1.b])
                sc.add("act", _act(q_sb.t[:, b, 0:512], pq0.t[:], AF.Copy), r=[pq0.b], w=[q_sb.b])
                sc.add("dve", _cp(q_sb.t[:, b, 512:768], pq1.t[:, 0:256]), r=[pq1.b], w=[q_sb.b])
            sc.add("dve", _tt(kn2.t[:], kn2.t[:], bc(kr2.t[:], 4), ALU.add), r=[kn2.b, kr2.b], w=[kn2.b])
            sc.add("dve", lambda e: e.tensor_reduce(kmt.t[:], kn2.t[:].rearrange("p b h -> p h b"), AX.X, ALU.max), r=[kn2.b], w=[kmt.b])
            sc.add("dve", _tt(kmax.t[:], kmax.t[:], kmt.t[:], ALU.max), r=[kmax.b, kmt.b], w=[kmax.b])
            cs4 = bass.AP(cs.tensor, cs.offset, [list(cs.ap[0]), list(cs.ap[1]), [0, 4], list(cs.ap[2])])
            sn4 = bass.AP(sn.tensor, sn.offset, [list(sn.ap[0]), list(sn.ap[1]), [0, 4], list(sn.ap[2])])
            sc.add("dve", _tt(qtA.t[:], q4[:, :, :, 128:192], cs4, ALU.mult), r=[q_sb.b], w=[qtA.b])
            sc.add("dve", _tt(qtB.t[:, :, :, 0:32], q4[:, :, :, 160:192], sn4, ALU.mult), r=[q_sb.b], w=[qtB.b])
            sc.add("dve", _tt(qtB.t[:, :, :, 32:64], q4[:, :, :, 128:160], sn4, ALU.mult), r=[q_sb.b], w=[qtB.b])
            sc.add("dve", _tt(qbr.t[:, :, :, 0:32], qtA.t[:, :, :, 0:32], qtB.t[:, :, :, 0:32], ALU.subtract), r=[qtA.b, qtB.b], w=[qbr.b])
            sc.add("dve", _tt(qbr.t[:, :, :, 32:64], qtA.t[:, :, :, 32:64], qtB.t[:, :, :, 32:64], ALU.add), r=[qtA.b, qtB.b], w=[qbr.b])
            sc.add("dve", _cp(qbn.t[:], q4[:, :, :, 0:128]), r=[q_sb.b], w=[qbn.b])
            sc.add("act", _act(junkc.t[:], q_sb.t[:].rearrange("p b c -> p (b c)"), AF.Square), r=[q_sb.b], w=[junkc.b])
            sc.add("dve", lambda e: e.tensor_reduce(qn2.t[:], junkc.t[:].rearrange("p (g d) -> p g d", g=16), AX.X, ALU.add),
                   r=[junkc.b], w=[qn2.b])
            sc.add("pool", _tt(qn1.t[:], qn2.t[:], mhalf[:, 0:16], ALU.pow), r=[qn2.b], w=[qn1.b])
            sc.add("dve", _tt(qn1.t[:], qn1.t[:], qn2.t[:], ALU.mult), r=[qn1.b, qn2.b], w=[qn1.b])
            sc.add("dve", _ts(qbr.t[:, :, :, 64], qn1.t[:].rearrange("p (b h) -> p b h", b=4), -1.01, None, ALU.mult),
                   r=[qn1.b], w=[qbr.b])
            for hp in range(2):
                pn, pr = bank(), bank()
                for hh in range(2):
                    h = 2 * hp + hh
                    for b in range(4):
                        sc.add("pe", _tr(bfv(pn)[:, hh * 512 + b * 128:hh * 512 + (b + 1) * 128], qbn.t[:, b, h, :], ident[:]), r=[qbn.b], w=[pn.b])
                        sc.add("pe", _tr(bfv(pr)[0:65, hh * 512 + b * 128:hh * 512 + (b + 1) * 128], qbr.t[:, b, h, 0:65], ident[:]), r=[qbr.b], w=[pr.b])
                qs, qrs = QS[hp], QRS[hp]
                sc.add("act", _act(qs.t[:], bfv(pn).rearrange("p (a b) -> p a b", a=2), AF.Copy), r=[pn.b], w=[qs.b])
                sc.add("dve", _cp(qrs.t[:], bfv(pr)[0:65, :].rearrange("p (a b) -> p a b", a=2)), r=[pr.b], w=[qrs.b])
                sc.add("pool", _dma(QN_s.rearrange("(h p) s -> p h s", p=128)[:, 2 * hp:2 * hp + 2, t0:t0 + 512], qs.t[:]), r=[qs.b], chan=qs.c)
                sc.add("pool", _dma(QR_s.rearrange("h p s -> p h s")[:, 2 * hp:2 * hp + 2, t0:t0 + 512], qrs.t[:]), r=[qrs.b], chan=qrs.c)
        kmo = Slot(k.sb(st, "kmo", [128, 4]), sc.chan("c_kmo"))
        sc.add("dve", _cp(kmo.t[:], kmax.t[:]), r=[kmax.b], w=[kmo.b])
        sc.add("pool", _dma(kmax_s[:, :], kmo.t[:]), r=[kmo.b], chan=kmo.c)
        sc.emit()

    with contextlib.ExitStack() as st:
        KT = Slot(k.sb(st, "KT", [128, 4, S], BF16), sc.chan("c_KT"))
        KR = Slot(k.sb(st, "KR", [65, S], BF16), sc.chan("c_KR"))
        VR = Slot(k.sb(st, "VR", [128, NB, 512], BF16), sc.chan("c_VR"))
        kml = Slot(k.sb(st, "kml", [128, 4]), sc.chan("c_kml"))
        km1 = Slot(k.sb(st, "km1", [1, 4]))
        kmx = Slot(k.sb(st, "kmx", [128, 4]))
        phalf = Slot(k.sb(st, "phalf", [128, 4]))
        SPB = [Slot(k.ps(st, f"spb{i}", [128, 512])) for i in range(3)]
        OPB = [Slot(k.ps(st, f"opb{i}", [128, 512])) for i in range(2)]
        RSB = Slot(k.ps(st, "rsb", [128, 512]))
        NPT = 10
        PT = [Slot(k.sb(st, f"pt{i}", [128, 512], BF16)) for i in range(NPT)]
        RS = [Slot(k.ps(st, f"rs{i}", [128, 512])) for i in range(1)]
        rs_sb = Slot(k.sb(st, "rs_sb", [128, 512]))
        ones_b = Slot(k.sb(st, "ones_b", [128, 32], BF16))
        inv32 = Slot(k.sb(st, "inv32", [128, 128]))
        sc.add("pool", _memset(ones_b.t[:], 1.0), w=[ones_b.b])
        sc.add("pool", _memset(inv32.t[:], 1.0 / 32.0), w=[inv32.b])
        QN = [Slot(k.sb(st, f"qnt{i}", [128, 512], BF16), sc.chan(f"c_qn{i}")) for i in range(2)]
        QR = [Slot(k.sb(st, f"qrt{i}", [65, 512], BF16), sc.chan(f"c_qr{i}")) for i in range(2)]
        rinv = Slot(k.sb(st, "rinv", [128, 512]))
        YO = [Slot(k.sb(st, f"yo{i}", [128, 512], BF16), sc.chan(f"c_yo{i}")) for i in range(2)]
        sc.add("sp", _dma(KT.t[:], KT_s.rearrange("(h p) s -> p h s", p=128)), w=[KT.b], chan=KT.c)
        sc.add("pool", _memset(KR.t[64:65, :], 1.0), w=[KR.b])
        sc.add("sp", _dma(KR.t[0:64, :], KR_s[:, :]), w=[KR.b], chan=KR.c)
        sc.add("sp", _dma(VR.t[:], V_s.rearrange("(c p) d -> p c d", p=128)), w=[VR.b], chan=VR.c)
        sc.add("sp", _dma(kml.t[:], kmax_s[:, :]), w=[kml.b], chan=kml.c)
        sc.add("pool", _memset(phalf.t[:], 0.5), w=[phalf.b])
        sc.add("pool", lambda e: e.tensor_reduce(km1.t[:], kml.t[:], AX.C, ALU.max), r=[kml.b], w=[km1.b])
        sc.add("pe", _mm(RSB.t[:, 0:4], ones_f[0:1, :], km1.t[:], True, True), r=[km1.b], w=[RSB.b])
        sc.add("dve", _cp(kmx.t[:], RSB.t[:, 0:4]), r=[RSB.b], w=[kmx.b])
        sc.add("pool", _tt(kmx.t[:], kmx.t[:], phalf.t[:], ALU.pow), r=[kmx.b, phalf.b], w=[kmx.b])
        scale = 192.0 ** -0.5
        it = 0
        pti = 0
        for h in range(4):
            for j in range(NT):
                qn, qr = QN[it % 2], QR[it % 2]
                opb, yo, rs = OPB[it % 2], YO[it % 2], RS[0]
                it += 1
                sc.add("sp", _dma(qn.t[:], QN_s[h * 128:(h + 1) * 128, j * 512:(j + 1) * 512]), w=[qn.b], chan=qn.c)
                sc.add("sp", _dma(qr.t[:], QR_s[h, :, j * 512:(j + 1) * 512]), w=[qr.b], chan=qr.c)
                sc.add("dve", _ts(qr.t[64:65, :], qr.t[64:65, :], kmx.t[64:65, h:h + 1], None, ALU.mult), r=[qr.b, kmx.b], w=[qr.b])

                def qk(kc):
                    sp_ = SPB[kc % 3]
                    ks = slice(kc * 128, (kc + 1) * 128)
                    sc.add("pe", _mm(sp_.t[:], KT.t[:, h, ks], qn.t[:], True, False), r=[KT.b, qn.b], w=[sp_.b])
                    sc.add("pe", _mm(sp_.t[:], KR.t[0:65, ks], qr.t[0:65, :], False, True), r=[KR.b, qr.b], w=[sp_.b])

                qk(0)
                if NB > 1:
                    qk(1)
                grp = []
                for kc in range(NB):
                    if kc + 2 < NB:
                        qk(kc + 2)
                    sp_ = SPB[kc % 3]
                    pt = PT[pti % NPT]
                    pti += 1
                    sc.add("act", _act(pt.t[:], sp_.t[:], AF.Exp, scale=scale), r=[sp_.b], w=[pt.b])
                    sc.add("pe", _mm(opb.t[:], VR.t[:, kc, h * 128:(h + 1) * 128], pt.t[:], kc == 0, kc == NB - 1),
                           r=[VR.b, pt.b], w=[opb.b])
                    grp.append(pt)
                    if len(grp) == 4:
                        for r_, ptr in enumerate(grp):
                            sc.add("pe", lambda e, r_=r_, ptr=ptr, kc=kc: e.matmul(rs.t[32 * r_:32 * r_ + 32, :], ones_b.t[:, 0:32], ptr.t[:],
                                                                               start=(kc == 3), stop=(kc == NB - 1),
                                                                               tile_position=(0, 32 * r_)),
                                   r=[ptr.b, ones_b.b], w=[rs.b])
                        grp = []
                sc.add("dve", _cp(rs_sb.t[:], rs.t[:]), r=[rs.b], w=[rs_sb.b])
                sc.add("pe", _mm(RSB.t[:], inv32.t[:], rs_sb.t[:], True, True), r=[rs_sb.b, inv32.b], w=[RSB.b])
                sc.add("dve", lambda e: e.reciprocal(rinv.t[:], RSB.t[:]), r=[RSB.b], w=[rinv.b])
                sc.add("dve", _tt(yo.t[:], opb.t[:], rinv.t[:], ALU.mult), r=[opb.b, rinv.b], w=[yo.b])
                sc.add("pool", _dma(yT_s[512 + h * 128:512 + (h + 1) * 128, j * 512:(j + 1) * 512], yo.t[:]), r=[yo.b], chan=yo.c)
        sc.emit()

    h1_s = k.dram_tmp("h1_s", [S, D])
    xn2T_s = k.dram_tmp("xn2T_s", [D, S + 2], BF16)
    h2_s = k.dram_tmp("h2_s", [S, D])
    yT3 = yT_s.rearrange("(g p) s -> p g s", p=128)
    xn2T3 = xn2T_s.rearrange("(g p) s -> p g s", p=128)

    def rms_transpose(xt, ss, ss2, rstd, junk, XB, xn, TBs, gain_scale=1.0 / 32.0):
        for b in range(4):
            sc.add("act", _act(junk.t[:], xt.t[:, b, :], AF.Square, scale=gain_scale, accum_out=ss.t[:, b:b + 1]),
                   r=[xt.b], w=[junk.b, ss.b])
        sc.add("dve", _ts(ss2.t[:], ss.t[:], EPS, None, ALU.add), r=[ss.b], w=[ss2.b])
        sc.add("pool", _tt(rstd.t[:], ss2.t[:], mhalf[:, 0:4], ALU.pow), r=[ss2.b], w=[rstd.b])
        for b in range(4):
            e = "dve" if b % 2 else "act"
            sc.add(e, scale_cast(e, XB.t[:, b, :], xt.t[:, b, :], rstd.t[:, b:b + 1]), r=[xt.b, rstd.b], w=[XB.b])
        for j in range(4):
            tb = TBs[j % 2]
            for kk in range(2):
                kc = 2 * j + kk
                for b in range(4):
                    sc.add("pe", _tr(tb.t[:, kk * 512 + b * 128:kk * 512 + (b + 1) * 128],
                                     XB.t[:, b, kc * 128:(kc + 1) * 128], ident[:]), r=[XB.b], w=[tb.b])
            e = "dve" if j % 2 else "act"
            sc.add(e, scale_cast(e, xn.t[:, 2 * j:2 * j + 2, :], tb.t[:].rearrange("p (a b) -> p a b", a=2)),
                   r=[tb.b], w=[xn.b])

    with contextlib.ExitStack() as st:
        stage = [Slot(k.sb(st, f"wstaged{i}", [128, 1024]), ld_ch[i]) for i in range(2)]
        Wout, Woutb = load_w(st, stage, "Wout", w_out, D, D)
        zt = Slot(k.sb(st, "zt", [128, 8, 2], BF16), sc.chan("c_zt"))
        sc.add("pool", _memset(zt.t[:], 0.0), w=[zt.b])
        sc.add("pool", _dma(xn2T3[:, :, 0:1], zt.t[:, :, 0:1], allow_slow_non_contiguous=True), r=[zt.b], chan=zt.c)
        sc.add("pool", _dma(xn2T3[:, :, S + 1:S + 2], zt.t[:, :, 1:2], allow_slow_non_contiguous=True), r=[zt.b], chan=zt.c)
        YTT = [Slot(k.sb(st, f"ytt{i}", [128, 8, 512], BF16), sc.chan(f"c_ytt{i}")) for i in range(2)]
        XT = [Slot(k.sb(st, f"xtd{i}", [128, 4, D]), sc.chan(f"c_xtd{i}")) for i in range(2)]
        XB = Slot(k.sb(st, "xbd", [128, 4, D], BF16))
        XN = [Slot(k.sb(st, f"xnd{i}", [128, 8, 512], BF16), sc.chan(f"c_xnd{i}")) for i in range(2)]
        junk = Slot(k.sb(st, "junkd", [128, D], BF16))
        ss = Slot(k.sb(st, "ssd", [128, 4]))
        ss2 = Slot(k.sb(st, "ss2d", [128, 4]))
        rstd = Slot(k.sb(st, "rstdd", [128, 4]))
        TBs = [Slot(k.ps(st, f"tbd{i}", [128, 1024], BF16)) for i in range(2)]
        MB = [Slot(k.ps(st, f"mbd{i}", [128, 512])) for i in range(6)]
        mbi = 0
        for i in range(NT):
            t0 = i * 512
            ytt, xt, xn = YTT[i % 2], XT[i % 2], XN[i % 2]
            sc.add("sp", _dma(ytt.t[:], yT3[:, :, t0:t0 + 512]), w=[ytt.b], chan=ytt.c)
            sc.add("sp", _dma(xt.t[:], x[t0:t0 + 512, :].rearrange("(b p) d -> p b d", p=128)), w=[xt.b], chan=xt.c)
            for b in range(4):
                for half in range(2):
                    pm = MB[mbi % 6]
                    mbi += 1
                    for kc in range(8):
                        sc.add("pe", _mm(pm.t[:], ytt.t[:, kc, b * 128:(b + 1) * 128], Wout[:, kc, half * 512:(half + 1) * 512],
                                         kc == 0, kc == 7), r=[ytt.b, Woutb], w=[pm.b])
                    sc.add("dve", _tt(xt.t[:, b, half * 512:(half + 1) * 512], pm.t[:], xt.t[:, b, half * 512:(half + 1) * 512], ALU.add),
                           r=[pm.b, xt.b], w=[xt.b])
            sc.add("pool", _dma(h1_s[t0:t0 + 512, :].rearrange("(b p) d -> p b d", p=128), xt.t[:]), r=[xt.b], chan=xt.c)
            rms_transpose(xt, ss, ss2, rstd, junk, XB, xn, TBs)
            sc.add("pool", _dma(xn2T3[:, :, 1 + t0:1 + t0 + 512], xn.t[:]), r=[xn.b], chan=xn.c)
        sc.emit()

    TT_ = 256
    with contextlib.ExitStack() as st:
        stage = [Slot(k.sb(st, f"wstagee{i}", [128, 1408]), ld_ch[i]) for i in range(2)]
        gffn = load_cols(st, "gffn", ln_ffn_g, 8)
        Wup, Wupb = load_w(st, stage, "Wup", w_up, D, 2 * D_FF, gffn)
        Wdn, Wdnb = load_w(st, stage, "Wdn", w_down, D_FF, D)
        fw = k.sb(st, "fw", [128, 44, 3])
        fwb = Buf("fw")
        for tap in range(3):
            sc.add("sp", _dma(fw[:, :, tap], conv_ffn_w[tap].rearrange("(g p) -> p g", p=128),
                              allow_slow_non_contiguous=True), w=[Buf()], chan=sc.chan(f"c_fw{tap}"))
        fb = load_cols(st, "fb", conv_ffn_b, 44)
        sc.barrier()
        XS = [Slot(k.sb(st, f"xs{i}", [128, 8, TT_ + 2], BF16), sc.chan(f"c_xs{i}")) for i in range(2)]
        H1 = [Slot(k.sb(st, f"h1t{i}", [128, 2, D]), sc.chan(f"c_h1t{i}")) for i in range(2)]
        AT = [Slot(k.sb(st, f"at{i}", [128, 22, TT_], BF16)) for i in range(2)]
        CG = [Slot(k.sb(st, f"cg{i}", [128, TT_])) for i in range(2)]
        CV = [Slot(k.sb(st, f"cv{i}", [128, TT_])) for i in range(2)]
        SG = [Slot(k.sb(st, f"sg{i}", [128, TT_])) for i in range(2)]
        MB = [Slot(k.ps(st, f"mbe{i}", [128, 512])) for i in range(8)]
        mbi = 0
        for i in range(S // TT_):
            t0 = i * TT_
            xs, h1, at = XS[i % 2], H1[i % 2], AT[i % 2]
            sc.add("sp", _dma(xs.t[:], xn2T3[:, :, t0:t0 + TT_ + 2]), w=[xs.b], chan=xs.c)
            sc.add("sp", _dma(h1.t[:], h1_s[t0:t0 + TT_, :].rearrange("(b p) d -> p b d", p=128)), w=[h1.b], chan=h1.c)
            for g in range(22):
                res = []
                for which, (gi, dst) in enumerate(((g, CG[g % 2]), (22 + g, CV[g % 2]))):
                    pm = MB[mbi % 8]
                    mbi += 1
                    for kc in range(8):
                        sc.add("pe", _mm(pm.t[:, 0:TT_ + 2], Wup[:, kc, gi * 128:(gi + 1) * 128], xs.t[:, kc, :], kc == 0, kc == 7),
                               r=[xs.b, Wupb], w=[pm.b])
                    sc.add("act", _act(dst.t[:], pm.t[:, 0:TT_], AF.Identity, scale=fw[:, gi, 0:1], bias=fb[0][:, gi:gi + 1]),
                           r=[pm.b], w=[dst.b])
                    sc.add("dve", _stt(dst.t[:], pm.t[:, 1:TT_ + 1], fw[:, gi, 1:2], dst.t[:], ALU.mult, ALU.add), r=[pm.b, dst.b], w=[dst.b])
                    sc.add("dve", _stt(dst.t[:], pm.t[:, 2:TT_ + 2], fw[:, gi, 2:3], dst.t[:], ALU.mult, ALU.add), r=[pm.b, dst.b], w=[dst.b])
                cg, cv, sg = CG[g % 2], CV[g % 2], SG[g % 2]
                sc.add("act", _act(sg.t[:], cg.t[:], AF.Silu), r=[cg.b], w=[sg.b])
                sc.add("pool", _tt(at.t[:, g, :], sg.t[:], cv.t[:], ALU.mult), r=[sg.b, cv.b], w=[at.b])
            for b in range(TT_ // 128):
                for half in range(2):
                    pm = MB[mbi % 8]
                    mbi += 1
                    for g in range(22):
                        sc.add("pe", _mm(pm.t[:], at.t[:, g, b * 128:(b + 1) * 128], Wdn[:, g, half * 512:(half + 1) * 512], g == 0, g == 21),
                               r=[at.b, Wdnb], w=[pm.b])
                    sc.add("dve", _tt(h1.t[:, b, half * 512:(half + 1) * 512], pm.t[:], h1.t[:, b, half * 512:(half + 1) * 512], ALU.add),
                           r=[pm.b, h1.b], w=[h1.b])
            sc.add("pool", _dma(h2_s[t0:t0 + TT_, :].rearrange("(b p) d -> p b d", p=128), h1.t[:]), r=[h1.b], chan=h1.c)
        sc.emit()

    with contextlib.ExitStack() as st:
        stage = [Slot(k.sb(st, f"wstagef{i}", [128, 1024]), ld_ch[i]) for i in range(2)]
        gple = load_cols(st, "gple", ple_norm_g, 8)
        Wg, Wgb = load_w(st, stage, "Wg", w_ple_gate, D, D, gple)
        Wp, Wpb = load_w(st, stage, "Wp", w_ple_proj, 256, D)
        postg = load_bcast(st, "postg", ple_post_g, D)
        fing = load_bcast(st, "fing", final_g, D)
        sc.barrier()
        XT = [Slot(k.sb(st, f"xtf{i}", [128, 4, D]), sc.chan(f"c_xtf{i}")) for i in range(2)]
        PTL = [Slot(k.sb(st, f"ptl{i}", [128, 4, 256]), sc.chan(f"c_ptl{i}")) for i in range(2)]
        XB = Slot(k.sb(st, "xbf", [128, 4, D], BF16))
        XN = Slot(k.sb(st, "xnf", [128, 8, 512], BF16))
        PBf = Slot(k.sb(st, "pbf", [128, 4, 256], BF16))
        PTT = Slot(k.sb(st, "ptt", [128, 2, 512], BF16))
        junk = Slot(k.sb(st, "junkf", [128, D], BF16))
        ss = Slot(k.sb(st, "ssf", [128, 4]))
        ss2 = Slot(k.sb(st, "ss2f", [128, 4]))
        rstd = Slot(k.sb(st, "rstdf", [128, 4]))
        ssb = Slot(k.sb(st, "ssb", [128, 2]))
        ssb2 = Slot(k.sb(st, "ssb2", [128, 2]))
        rsb2 = Slot(k.sb(st, "rsb2", [128, 2]))
        SGM = [Slot(k.sb(st, f"sgm{i}", [128, D])) for i in range(2)]
        PJ = [Slot(k.sb(st, f"pj{i}", [128, D])) for i in range(2)]
        OT = [Slot(k.sb(st, f"ot{i}", [128, D]), sc.chan(f"c_ot{i}")) for i in range(2)]
        TBs = [Slot(k.ps(st, f"tbf{i}", [128, 1024], BF16)) for i in range(2)]
        MB = [Slot(k.ps(st, f"mbf{i}", [128, 512])) for i in range(6)]
        mbi = 0
        bi_ = 0
        for i in range(NT):
            t0 = i * 512
            xt, ptl = XT[i % 2], PTL[i % 2]
            sc.add("sp", _dma(xt.t[:], h2_s[t0:t0 + 512, :].rearrange("(b p) d -> p b d", p=128)), w=[xt.b], chan=xt.c)
            sc.add("sp", _dma(ptl.t[:], p_in[t0:t0 + 512, :].rearrange("(b p) d -> p b d", p=128)), w=[ptl.b], chan=ptl.c)
            rms_transpose(xt, ss, ss2, rstd, junk, XB, XN, TBs)
            sc.add("dve", _cp(PBf.t[:], ptl.t[:]), r=[ptl.b], w=[PBf.b])
            tb = TBs[0]
            for kc in range(2):
                for b in range(4):
                    sc.add("pe", _tr(tb.t[:, kc * 512 + b * 128:kc * 512 + (b + 1) * 128], PBf.t[:, b, kc * 128:(kc + 1) * 128], ident[:]),
                           r=[PBf.b], w=[tb.b])
            sc.add("act", _act(PTT.t[:], tb.t[:].rearrange("p (a b) -> p a b", a=2), AF.Copy), r=[tb.b], w=[PTT.b])
            for b in range(4):
                sgm, pj, ot = SGM[bi_ % 2], PJ[bi_ % 2], OT[bi_ % 2]
                bi_ += 1
                tok = slice(b * 128, (b + 1) * 128)
                for half in range(2):
                    hs = slice(half * 512, (half + 1) * 512)
                    pm = MB[mbi % 6]
                    mbi += 1
                    for kc in range(8):
                        sc.add("pe", _mm(pm.t[:], XN.t[:, kc, tok], Wg[:, kc, hs], kc == 0, kc == 7), r=[XN.b, Wgb], w=[pm.b])
                    sc.add("act", _act(sgm.t[:, hs], pm.t[:], AF.Sigmoid), r=[pm.b], w=[sgm.b])
                    pm = MB[mbi % 6]
                    mbi += 1
                    for kc in range(2):
                        sc.add("pe", _mm(pm.t[:], PTT.t[:, kc, tok], Wp[:, kc, hs], kc == 0, kc == 1), r=[PTT.b, Wpb], w=[pm.b])
                    sc.add("act", _act(pj.t[:, hs], pm.t[:], AF.Copy), r=[pm.b], w=[pj.b])
                sc.add("act", _act(junk.t[:], pj.t[:], AF.Square, scale=1.0 / 32.0, accum_out=ssb.t[:, 0:1]), r=[pj.b], w=[junk.b, ssb.b])
                sc.add("dve", _ts(ssb2.t[:, 0:1], ssb.t[:, 0:1], EPS, None, ALU.add), r=[ssb.b], w=[ssb2.b])
                sc.add("pool", _tt(rsb2.t[:, 0:1], ssb2.t[:, 0:1], mhalf[:, 0:1], ALU.pow), r=[ssb2.b], w=[rsb2.b])
                sc.add("dve", _tt(pj.t[:], pj.t[:], postg[0][:], ALU.mult), r=[pj.b, postg[1]], w=[pj.b])
                sc.add("dve", _stt(pj.t[:], pj.t[:], rsb2.t[:, 0:1], sgm.t[:], ALU.mult, ALU.mult), r=[pj.b, rsb2.b, sgm.b], w=[pj.b])
                sc.add("dve", _tt(pj.t[:], pj.t[:], xt.t[:, b, :], ALU.add), r=[pj.b, xt.b], w=[pj.b])
                sc.add("act", _act(junk.t[:], pj.t[:], AF.Square, scale=1.0 / 32.0, accum_out=ssb.t[:, 1:2]), r=[pj.b], w=[junk.b, ssb.b])
                sc.add("dve", _ts(ssb2.t[:, 1:2], ssb.t[:, 1:2], EPS, None, ALU.add), r=[ssb.b], w=[ssb2.b])
                sc.add("pool", _tt(rsb2.t[:, 1:2], ssb2.t[:, 1:2], mhalf[:, 0:1], ALU.pow), r=[ssb2.b], w=[rsb2.b])
                sc.add("dve", _stt(ot.t[:], pj.t[:], rsb2.t[:, 1:2], fing[0][:], ALU.mult, ALU.mult), r=[pj.b, rsb2.b, fing[1]], w=[ot.b])
                sc.add("pool", _dma(out[t0 + b * 128:t0 + (b + 1) * 128, :], ot.t[:]), r=[ot.b], chan=ot.c)
        sc.emit()

    k.final_wait = None
    return k


def finish(k):
    return k.nc


_W_NAMES = ["ln_mix_g", "w_in", "b_gates", "conv_qk_w", "conv_qk_b", "mlstm_norm_g", "q_norm_g", "w_uq", "kv_norm_g",
            "w_ukv", "w_out", "ln_ffn_g", "w_up", "conv_ffn_w", "conv_ffn_b", "w_down", "ple_norm_g", "w_ple_gate",
            "w_ple_proj", "ple_post_g"]


def kernel(**inputs):
    x = np.asarray(inputs["x"])
    p = np.asarray(inputs["p"])
    B, S, _ = x.shape
    nc = finish(build(S))
    shared = {n: np.ascontiguousarray(np.asarray(inputs[n])[0], dtype=np.float32) for n in _W_NAMES}
    shared["final_g"] = np.ascontiguousarray(np.asarray(inputs["final_g"]), dtype=np.float32)
    in_maps = []
    for b in range(B):
        m = dict(shared)
        m["x"] = np.ascontiguousarray(x[b], dtype=np.float32)
        m["p"] = np.ascontiguousarray(p[0, b], dtype=np.float32)
        in_maps.append(m)
    res = run_bass_kernel_spmd(nc, in_maps, core_ids=list(range(B)))
    return np.stack([np.asarray(r["out"]) for r in res.results], axis=0).astype(np.float32)
```

```python
import contextlib
import numpy as np
import concourse.bass as bass
import concourse.mybir as mybir
from concourse.bass_utils import run_bass_kernel_spmd

F32 = mybir.dt.float32
BF16 = mybir.dt.bfloat16
AF = mybir.ActivationFunctionType
ALU = mybir.AluOpType
AX = mybir.AxisListType

D = 1024
NH = 4
IN_COLS = 2512
D_FF = 2816
EPS = 1e-6
SEM_MAX = 30000


class Buf:
    __slots__ = ("name", "lw", "rd")

    def __init__(self, name=""):
        self.name = name
        self.lw = None
        self.rd = []


class Chan:
    __slots__ = ("sem", "count", "last")

    def __init__(self, sem):
        self.sem = sem
        self.count = 0
        self.last = None


class Op:
    __slots__ = ("eng", "fn", "deps", "sig", "signo", "chan", "cval", "done")


class Sched:
    ENGS = ("pe", "act", "dve", "pool", "sp")

    def __init__(self, nc, stack):
        self.nc = nc
        self.stack = stack
        self.ops = []
        self.last_on = {e: None for e in self.ENGS}
        self.pending_bar = {e: [] for e in self.ENGS}
        self.chans = []
        self.cnt = {e: 0 for e in self.ENGS}
        self.sems = {e: [] for e in self.ENGS}
        self.waited = {e: {} for e in self.ENGS}

    def chan(self, name):
        c = Chan(self.stack.enter_context(self.nc.semaphore(name)))
        self.chans.append(c)
        return c

    def add(self, eng, fn, r=(), w=(), chan=None):
        op = Op()
        op.eng, op.fn, op.deps, op.sig, op.signo, op.chan, op.cval = eng, fn, {}, False, 0, chan, 0
        op.done = False
        for b in r:
            if b.lw is not None:
                op.deps[b.lw] = True
        for b in w:
            if b.lw is not None:
                op.deps.setdefault(b.lw, False)
            for q in b.rd:
                op.deps.setdefault(q, False)
        for b in r:
            b.rd.append(op)
        for b in w:
            b.lw = op
            b.rd = []
        if self.pending_bar[eng]:
            for d in self.pending_bar[eng]:
                op.deps[d] = True
            self.pending_bar[eng] = []
        if chan is not None:
            if chan.last is not None:
                op.deps[chan.last] = True
            chan.count += 16
            op.cval = chan.count
            chan.last = op
        op.deps.pop(op, None)
        self.ops.append(op)
        self.last_on[eng] = op
        return op

    def barrier(self):
        lasts = [o for o in self.last_on.values() if o is not None]
        lasts += [c.last for c in self.chans if c.last is not None]
        for e in self.ENGS:
            self.pending_bar[e] = list(lasts)

    def emit(self):
        nc = self.nc
        fin = self.add("sp", lambda e: e.nop())
        for c in self.chans:
            if c.last is not None and not c.last.done:
                fin.deps[c.last] = True
        for e in self.ENGS:
            self.pending_bar[e] = []
        for op in self.ops:
            for d in [d for d in op.deps if d.done]:
                del op.deps[d]
            for d, raw in op.deps.items():
                if d.chan is not None:
                    continue
                if d.eng == op.eng and (op.eng == "pe" or not raw):
                    continue
                d.sig = True
        cnt = self.cnt
        for op in self.ops:
            if op.chan is None and op.sig:
                cnt[op.eng] += 1
                op.signo = cnt[op.eng]
        sems = self.sems
        for e in self.ENGS:
            n = cnt[e] // SEM_MAX + 1
            while len(sems[e]) < n:
                sems[e].append(self.stack.enter_context(nc.semaphore(f"s_{e}{len(sems[e])}")))
        per = {e: [o for o in self.ops if o.eng == e] for e in self.ENGS}
        handles = {"pe": "tensor", "act": "scalar", "dve": "vector", "pool": "gpsimd", "sp": "sync"}

        def run(e, eng):
            waited = self.waited[e]
            for op in per[e]:
                for d, raw in op.deps.items():
                    if d.chan is not None:
                        key, val, sem = ("c", id(d.chan)), d.cval, d.chan.sem
                    else:
                        if d.eng == e and (e == "pe" or not raw):
                            continue
                        j = (d.signo - 1) // SEM_MAX
                        key, val, sem = (d.eng, j), d.signo - j * SEM_MAX, sems[d.eng][j]
                    if waited.get(key, 0) >= val:
                        continue
                    waited[key] = val
                    eng.wait_ge(sem, val)
                ins = op.fn(eng)
                if op.chan is not None:
                    ins.then_inc(op.chan.sem, 16)
                elif op.sig:
                    j = (op.signo - 1) // SEM_MAX
                    ins.then_inc(sems[e][j], 1)

        with nc.Block() as block:
            for e in self.ENGS:
                if per[e]:
                    getattr(block, handles[e])(lambda eng, e=e: run(e, eng))
        for op in self.ops:
            op.done = True
            op.fn = None
            op.deps = {}
        self.ops = []
        self.last_on = {e: None for e in self.ENGS}


def _act(out, in_, func, **kw):
    return lambda e: e.activation(out, in_, func, **kw)


def _ts(out, in0, s1, s2, op0, op1=None):
    if op1 is None:
        return lambda e: e.tensor_scalar(out, in0, s1, None, op0)
    return lambda e: e.tensor_scalar(out, in0, s1, s2, op0, op1)


def _stt(out, in0, sc, in1, op0, op1):
    return lambda e: e.scalar_tensor_tensor(out, in0, sc, in1, op0, op1)


def _tt(out, in0, in1, op):
    return lambda e: e.tensor_tensor(out, in0, in1, op)


def _cp(out, in_):
    return lambda e: e.tensor_copy(out, in_)


def _mm(out, lhsT, rhs, start, stop):
    return lambda e: e.matmul(out, lhsT, rhs, start=start, stop=stop)


def _tr(out, in_, ident):
    return lambda e: e.transpose(out, in_, ident)


def _dma(out, in_, **kw):
    return lambda e: e.dma_start(out=out, in_=in_, **kw)


def _memset(ap, v):
    return lambda e: e.memset(ap, v)


class K:
    def __init__(self, S, dbg=()):
        self.S = S
        self.dbg = dbg
        self.nc = bass.Bass("TRN2", target_bir_lowering=False)
        self.stack = contextlib.ExitStack()
        self.sc = Sched(self.nc, self.stack)

    def sb(self, st, name, shape, dt=F32):
        return st.enter_context(self.nc.sbuf_tensor(name, list(shape), dt))

    def ps(self, st, name, shape, dt=F32):
        return st.enter_context(self.nc.psum_tensor(name, list(shape), dt))

    def dram_in(self, name, shape, dt=F32):
        return self.nc.dram_tensor(name, list(shape), dt, kind="ExternalInput").ap()

    def dram_out(self, name, shape, dt=F32):
        return self.nc.dram_tensor(name, list(shape), dt, kind="ExternalOutput").ap()

    def dram_tmp(self, name, shape, dt=F32):
        if name in self.dbg:
            return self.nc.dram_tensor(name, list(shape), dt, kind="ExternalOutput").ap()
        return self.nc.dram_tensor(name, list(shape), dt).ap()


class Slot:
    def __init__(self, t, chan=None):
        self.t = t
        self.b = Buf()
        self.c = chan


def build(S, dbg=()):
    k = K(S, dbg)
    nc, sc = k.nc, k.sc
    NT, NB = S // 512, S // 128
    top = k.stack

    x = k.dram_in("x", [S, D])
    p_in = k.dram_in("p", [S, 256])
    ln_mix_g = k.dram_in("ln_mix_g", [D])
    w_in = k.dram_in("w_in", [D, IN_COLS])
    b_gates = k.dram_in("b_gates", [16])
    conv_qk_w = k.dram_in("conv_qk_w", [3, 1024])
    conv_qk_b = k.dram_in("conv_qk_b", [1024])
    mlstm_norm_g = k.dram_in("mlstm_norm_g", [512])
    q_norm_g = k.dram_in("q_norm_g", [256])
    w_uq = k.dram_in("w_uq", [256, 768])
    kv_norm_g = k.dram_in("kv_norm_g", [128])
    w_ukv = k.dram_in("w_ukv", [128, 1024])
    w_out = k.dram_in("w_out", [1024, 1024])
    ln_ffn_g = k.dram_in("ln_ffn_g", [D])
    w_up = k.dram_in("w_up", [D, 2 * D_FF])
    conv_ffn_w = k.dram_in("conv_ffn_w", [3, 2 * D_FF])
    conv_ffn_b = k.dram_in("conv_ffn_b", [2 * D_FF])
    w_down = k.dram_in("w_down", [D_FF, D])
    ple_norm_g = k.dram_in("ple_norm_g", [D])
    w_ple_gate = k.dram_in("w_ple_gate", [D, D])
    w_ple_proj = k.dram_in("w_ple_proj", [256, D])
    ple_post_g = k.dram_in("ple_post_g", [D])
    final_g = k.dram_in("final_g", [D])
    out = k.dram_out("out", [S, D])

    qkT = k.dram_tmp("qkT", [1024, S], BF16)
    vo_s = k.dram_tmp("vo_s", [S, 1024])
    g3_s = k.dram_tmp("g3_s", [S, 464])

    ident = k.sb(top, "ident", [128, 128], BF16)
    ones_f = k.sb(top, "ones_f", [128, 128], F32)
    mhalf = k.sb(top, "mhalf", [128, 16], F32)
    cb = Buf("consts")
    sc.add("pool", _memset(ones_f[:], 1.0), w=[cb])
    sc.add("pool", _memset(mhalf[:], -0.5), w=[cb])
    sc.add("pool", lambda e: e.affine_select(ident[:], ones_f[:], [[-1, 128]], ALU.is_equal, 0.0,
                                             base=0, channel_multiplier=1), r=[cb], w=[cb])
    sc.emit()

    ld_ch = [sc.chan(f"ldw{i}") for i in range(2)]
    cnt = {"w": 0, "e": 0}

    def alt():
        cnt["e"] += 1
        return "dve" if cnt["e"] % 2 else "act"

    def scale_cast(eng, out_ap, in_ap, sc_ap=None):
        if eng == "act":
            if sc_ap is None:
                return _act(out_ap, in_ap, AF.Copy)
            return _act(out_ap, in_ap, AF.Copy, scale=sc_ap)
        if sc_ap is None:
            return _cp(out_ap, in_ap)
        return _ts(out_ap, in_ap, sc_ap, None, ALU.mult)

    def load_cols(st, name, src, G):
        t = k.sb(st, name, [128, G])
        b = Buf(name)
        ch = sc.chan("c_" + name)
        sc.add("sp", _dma(t[:], src.rearrange("(g p) -> p g", p=128), allow_slow_non_contiguous=True),
               w=[b], chan=ch)
        return t, b

    def load_bcast(st, name, src, n):
        t = k.sb(st, name, [128, n])
        b = Buf(name)
        ch = sc.chan("c_" + name)
        sc.add("sp", _dma(t[:], bass.AP(src.tensor, src.offset, [[0, 128], [1, n]])), w=[b], chan=ch)
        return t, b

    def load_w(st, stage, name, src, Kdim, cols, gain=None):
        kcn = Kdim // 128
        t = k.sb(st, name, [128, kcn, cols], BF16)
        b = Buf(name)
        for kc in range(kcn):
            sw = stage[0].t.shape[1]
            for c0 in range(0, cols, sw):
                w = min(sw, cols - c0)
                s = stage[cnt["w"] % 2]
                cnt["w"] += 1
                sc.add("sp", _dma(s.t[:, 0:w], src[kc * 128:(kc + 1) * 128, c0:c0 + w]), w=[s.b], chan=s.c)
                g = None if gain is None else gain[0][:, kc:kc + 1]
                rr = [s.b] + ([] if gain is None else [gain[1]])
                e = alt()
                sc.add(e, scale_cast(e, t[:, kc, c0:c0 + w], s.t[:, 0:w], g), r=rr, w=[b])
        return t, b

    with contextlib.ExitStack() as st:
        stage = [Slot(k.sb(st, f"wstage{i}", [128, 2816]), ld_ch[i]) for i in range(2)]
        gmix = load_cols(st, "gmix", ln_mix_g, 8)
        Win, Winb = load_w(st, stage, "Win", w_in, D, IN_COLS, gmix)
        cw = k.sb(st, "cw", [128, 8, 3])
        cwb = Buf("cw")
        for tap in range(3):
            sc.add("sp", _dma(cw[:, :, tap], conv_qk_w[tap].rearrange("(g p) -> p g", p=128),
                              allow_slow_non_contiguous=True), w=[Buf()], chan=sc.chan(f"c_cw{tap}"))
        cbias = load_cols(st, "cbias", conv_qk_b, 8)
        bg = load_bcast(st, "bg", b_gates, 16)
        sc.barrier()

        XT = [Slot(k.sb(st, f"xt{i}", [128, 4, D]), sc.chan(f"c_xt{i}")) for i in range(2)]
        XB = Slot(k.sb(st, "xb", [128, 4, D], BF16))
        XN = [Slot(k.sb(st, f"xn{i}", [128, 8, 512], BF16)) for i in range(2)]
        junk = Slot(k.sb(st, "junk", [128, D], BF16))
        ss = Slot(k.sb(st, "ss", [128, 4]))
        ss2 = Slot(k.sb(st, "ss2", [128, 4]))
        rstd = Slot(k.sb(st, "rstd", [128, 4]))
        PRE = [Slot(k.sb(st, f"pre{g}", [128, 514])) for g in range(8)]
        ACC = [Slot(k.sb(st, f"acc{i}", [128, 512])) for i in range(2)]
        QKB = [Slot(k.sb(st, f"qkb{i}", [128, 512], BF16), sc.chan(f"c_qkb{i}")) for i in range(3)]
        VO = [Slot(k.sb(st, f"vo{i}", [128, 1024]), sc.chan(f"c_vo{i}")) for i in range(2)]
        G3 = [Slot(k.sb(st, f"g3{i}", [128, 464]), sc.chan(f"c_g3{i}")) for i in range(2)]
        TB = [Slot(k.ps(st, f"tb{i}", [128, 1024], BF16)) for i in range(2)]
        MB = [Slot(k.ps(st, f"mb{i}", [128, 512])) for i in range(6)]
        last8 = Slot(k.sb(st, "last8", [128, 8]))
        last8b = Slot(k.sb(st, "last8b", [128, 8], BF16), sc.chan("c_last8"))
        for g in range(8):
            sc.add("pool", _memset(PRE[g].t[:, 0:2], 0.0), w=[PRE[g].b])
        cntA = {"mbi": 0, "qi": 0, "voi": 0}

        def stage1(i):
            t0 = i * 512
            xt = XT[i % 2]
            xn = XN[i % 2]
            sc.add("sp", _dma(xt.t[:], x[t0:t0 + 512, :].rearrange("(b p) d -> p b d", p=128)), w=[xt.b], chan=xt.c)
            for b in range(4):
                sc.add("act", _act(junk.t[:], xt.t[:, b, :], AF.Square, scale=1.0 / 32.0, accum_out=ss.t[:, b:b + 1]),
                       r=[xt.b], w=[junk.b, ss.b])
            sc.add("dve", _ts(ss2.t[:], ss.t[:], EPS, None, ALU.add), r=[ss.b], w=[ss2.b])
            sc.add("pool", _tt(rstd.t[:], ss2.t[:], mhalf[:, 0:4], ALU.pow), r=[ss2.b], w=[rstd.b])
            for b in range(4):
                e = "dve" if b % 2 else "act"
                sc.add(e, scale_cast(e, XB.t[:, b, :], xt.t[:, b, :], rstd.t[:, b:b + 1]), r=[xt.b, rstd.b], w=[XB.b])
            for j in range(4):
                tb = TB[j % 2]
                for kk in range(2):
                    kc = 2 * j + kk
                    for b in range(4):
                        sc.add("pe", _tr(tb.t[:, kk * 512 + b * 128:kk * 512 + (b + 1) * 128],
                                         XB.t[:, b, kc * 128:(kc + 1) * 128], ident[:]), r=[XB.b], w=[tb.b])
                e = "dve" if j % 2 else "act"
                sc.add(e, scale_cast(e, xn.t[:, 2 * j:2 * j + 2, :], tb.t[:].rearrange("p (a b) -> p a b", a=2)),
                       r=[tb.b], w=[xn.b])

        def stage2(i):
            t0 = i * 512
            xn = XN[i % 2]
            mbi, qi, voi = cntA["mbi"], cntA["qi"], cntA["voi"]
            for g in range(8):
                pm = MB[mbi % 6]
                mbi += 1
                for kc in range(8):
                    sc.add("pe", _mm(pm.t[:], Win[:, kc, g * 128:(g + 1) * 128], xn.t[:, kc, :], kc == 0, kc == 7),
                           r=[xn.b, Winb], w=[pm.b])
                pre = PRE[g]
                acc = ACC[g % 2]
                qb = QKB[qi % 3]
                qi += 1
                sc.add("act", _act(pre.t[:, 2:514], pm.t[:], AF.Copy), r=[pm.b], w=[pre.b])
                sc.add("dve", _ts(acc.t[:], pre.t[:, 2:514], cw[:, g, 2:3], None, ALU.mult), r=[pre.b], w=[acc.b])
                sc.add("dve", _stt(acc.t[:], pre.t[:, 1:513], cw[:, g, 1:2], acc.t[:], ALU.mult, ALU.add),
                       r=[pre.b, acc.b], w=[acc.b])
                sc.add("dve", _stt(acc.t[:], pre.t[:, 0:512], cw[:, g, 0:1], acc.t[:], ALU.mult, ALU.add),
                       r=[pre.b, acc.b], w=[acc.b])
                sc.add("act", _act(qb.t[:], acc.t[:], AF.Silu, bias=cbias[0][:, g:g + 1]), r=[acc.b], w=[qb.b])
                if i == 0:
                    sc.add("pool", _dma(qkT[g * 128:(g + 1) * 128, 0:511], qb.t[:, 1:512]), r=[qb.b], chan=qb.c)
                else:
                    sc.add("pool", _dma(qkT[g * 128:(g + 1) * 128, t0 - 1:t0 + 511], qb.t[:]), r=[qb.b], chan=qb.c)
                sc.add("dve", _cp(pre.t[:, 0:2], pre.t[:, 512:514]), r=[pre.b], w=[pre.b])
            for b in range(4):
                vo = VO[voi % 2]
                g3 = G3[voi % 2]
                voi += 1
                for part, (c0, c1) in enumerate(((1024, 1536), (1536, 2048), (2048, 2512))):
                    pm = MB[mbi % 6]
                    mbi += 1
                    for kc in range(8):
                        sc.add("pe", _mm(pm.t[:, 0:c1 - c0], xn.t[:, kc, b * 128:(b + 1) * 128], Win[:, kc, c0:c1],
                                         kc == 0, kc == 7), r=[xn.b, Winb], w=[pm.b])
                    if part == 0:
                        sc.add("dve", _cp(vo.t[:, 0:512], pm.t[:]), r=[pm.b], w=[vo.b])
                    elif part == 1:
                        sc.add("act", _act(vo.t[:, 512:1024], pm.t[:], AF.Tanh, scale=0.5), r=[pm.b], w=[vo.b])
                        sc.add("dve", _ts(vo.t[:, 512:1024], vo.t[:, 512:1024], 0.5, 0.5, ALU.mult, ALU.add),
                               r=[vo.b], w=[vo.b])
                    else:
                        sc.add("act", _act(g3.t[:], pm.t[:, 0:464], AF.Copy), r=[pm.b], w=[g3.b])
                        sc.add("dve", _tt(g3.t[:, 0:16], g3.t[:, 0:16], bg[0][:], ALU.add), r=[g3.b], w=[g3.b])
                r0 = t0 + b * 128
                sc.add("pool", _dma(vo_s[r0:r0 + 128, :], vo.t[:]), r=[vo.b], chan=vo.c)
                sc.add("pool", _dma(g3_s[r0:r0 + 128, :], g3.t[:]), r=[g3.b], chan=g3.c)
            cntA["mbi"], cntA["qi"], cntA["voi"] = mbi, qi, voi

        stage1(0)
        for i in range(NT):
            if i + 1 < NT:
                stage1(i + 1)
            stage2(i)
        prb = [PRE[g].b for g in range(8)]
        for g in range(8):
            sc.add("dve", _ts(last8.t[:, g:g + 1], PRE[g].t[:, 0:1], cw[:, g, 0:1], None, ALU.mult), r=[PRE[g].b], w=[last8.b])
            sc.add("dve", _stt(last8.t[:, g:g + 1], PRE[g].t[:, 1:2], cw[:, g, 1:2], last8.t[:, g:g + 1], ALU.mult, ALU.add),
                   r=[PRE[g].b, last8.b], w=[last8.b])
        sc.add("dve", _tt(last8.t[:], last8.t[:], cbias[0][:], ALU.add), r=[last8.b], w=[last8.b])
        sc.add("act", _act(last8b.t[:], last8.t[:], AF.Silu), r=[last8.b], w=[last8b.b])
        sc.add("pool", _dma(qkT.rearrange("(g p) s -> p g s", p=128)[:, :, S - 1], last8b.t[:],
                            allow_slow_non_contiguous=True), r=[last8b.b], chan=last8b.c)
        sc.emit()

    hf_s = k.dram_tmp("hf_s", [S, 512])
    hb_s = k.dram_tmp("hb_s", [S, 512])
    yT_s = k.dram_tmp("yT_s", [1024, S], BF16)

    def bc(ap, m):
        a = [list(d) for d in ap.ap]
        return bass.AP(ap.tensor, ap.offset, a + [[0, m]])

    def bc_mid(ap, m):
        a = [list(d) for d in ap.ap]
        return bass.AP(ap.tensor, ap.offset, [a[0], [0, m]] + a[1:])

    with contextlib.ExitStack() as st:
        maskF = k.sb(st, "maskF", [128, 128])
        maskB = k.sb(st, "maskB", [128, 128])
        mb_ = Buf("masks")
        sc.add("pool", lambda e: e.affine_select(maskF[:], ones_f[:], [[1, 128]], ALU.is_ge, 0.0,
                                                 base=0, channel_multiplier=-1), w=[mb_])
        sc.add("pool", lambda e: e.affine_select(maskB[:], ones_f[:], [[-1, 128]], ALU.is_ge, 0.0,
                                                 base=0, channel_multiplier=1), w=[mb_])
        hdst = (hf_s, hb_s)
        tris = (maskF, maskB)

        def ring(name, shape, dt=F32, chan=False, n=2):
            return [[Slot(k.sb(st, f"{name}{d}_{i}", shape, dt), sc.chan(f"c_{name}{d}_{i}") if chan else None) for i in range(n)]
                    for d in range(2)]

        SL = ring("sl", [128, 8, 512], BF16, True)
        VOT = ring("vot", [128, 512], F32, True)
        GT = ring("gt", [128, 16], F32, True)
        HO = ring("ho", [128, 512], F32, True)
        AA = ring("aa", [128, 4])
        EG = ring("eg", [128, 8])
        V1 = ring("v1", [128, 4, 130], BF16)
        KTOK = ring("ktok", [128, 4, 128], BF16)
        MM = ring("mm", [128, 4, 128], BF16)
        e1 = [Slot(k.sb(st, f"e1_{d}", [128, 4])) for d in range(2)]
        lsp = [Slot(k.sb(st, f"lsp_{d}", [128, 4])) for d in range(2)]
        tmpa = [Slot(k.sb(st, f"tmpa_{d}", [128, 4])) for d in range(2)]
        C1 = [Slot(k.sb(st, f"c1_{d}", [128, 4, 130])) for d in range(2)]
        C1b = [Slot(k.sb(st, f"c1b_{d}", [128, 4, 130], BF16)) for d in range(2)]
        tmpC = [Slot(k.sb(st, f"tmpc_{d}", [128, 4, 130])) for d in range(2)]
        den = [Slot(k.sb(st, f"den_{d}", [128, 4])) for d in range(2)]
        rr_ = [Slot(k.sb(st, f"rr_{d}", [128, 4])) for d in range(2)]
        TBK = Slot(k.ps(st, "tbk", [128, 1024], BF16))
        SPS = [Slot(k.ps(st, f"sps{i}", [128, 512])) for i in range(2)]
        UPS = Slot(k.ps(st, "ups", [128, 1024]))
        DPS = Slot(k.ps(st, "dps", [128, 1024]))
        GPS = Slot(k.ps(st, "gps", [128, 512]))
        qkT3 = qkT.rearrange("(g p) s -> p g s", p=128)
        slab = [{"cg": None, "n": 0} for _ in range(2)]

        def pre(c, d, n):
            cg = c // 4
            if slab[d]["cg"] != cg:
                slab[d]["cg"] = cg
                slab[d]["n"] += 1
                sl = SL[d][slab[d]["n"] % 2]
                sc.add("sp", _dma(sl.t[:], qkT3[:, :, cg * 512:(cg + 1) * 512]), w=[sl.b], chan=sl.c)
            sl = SL[d][slab[d]["n"] % 2]
            vot, gt, aa, eg, v1, ktok, mm = (VOT[d][n % 2], GT[d][n % 2], AA[d][n % 2], EG[d][n % 2], V1[d][n % 2],
                                             KTOK[d][n % 2], MM[d][n % 2])
            sps = SPS[d]
            r0 = c * 128
            sc.add("sp", _dma(vot.t[:], vo_s[r0:r0 + 128, 0:512]), w=[vot.b], chan=vot.c)
            sc.add("sp", _dma(gt.t[:], g3_s[r0:r0 + 128, 0:16]), w=[gt.b], chan=gt.c)
            io, fo = d * 8, d * 8 + 4
            tri = tris[d]
            sc.add("act", _act(e1[d].t[:], gt.t[:, fo:fo + 4], AF.Exp, scale=-1.0), r=[gt.b], w=[e1[d].b])
            sc.add("act", _act(lsp[d].t[:], e1[d].t[:], AF.Ln, bias=1.0), r=[e1[d].b], w=[lsp[d].b])
            gcol = d * 8
            sc.add("pe", _mm(GPS.t[:, gcol:gcol + 4], tri[:], lsp[d].t[:], True, True), r=[lsp[d].b, mb_], w=[GPS.b])
            sc.add("pe", _mm(GPS.t[:, gcol + 4:gcol + 8], ones_f[:], lsp[d].t[:], True, True), r=[lsp[d].b], w=[GPS.b])
            sc.add("dve", _tt(tmpa[d].t[:], gt.t[:, io:io + 4], GPS.t[:, gcol:gcol + 4], ALU.add), r=[gt.b, GPS.b], w=[tmpa[d].b])
            sc.add("act", _act(aa.t[:], tmpa[d].t[:], AF.Exp), r=[tmpa[d].b], w=[aa.b])
            sc.add("act", _act(eg.t[:], GPS.t[:, gcol:gcol + 8], AF.Exp, scale=-1.0), r=[GPS.b], w=[eg.b])
            sc.add("dve", _tt(v1.t[:, :, 0:128], vot.t[:].rearrange("p (h d) -> p h d", h=4), bc(aa.t[:], 128), ALU.mult),
                   r=[vot.b, aa.b], w=[v1.b])
            sc.add("dve", _cp(v1.t[:, :, 128], aa.t[:]), r=[aa.b], w=[v1.b])
            c4 = (c % 4) * 128
            tcol = d * 512
            for h in range(4):
                sc.add("pe", _tr(TBK.t[:, tcol + h * 128:tcol + (h + 1) * 128], sl.t[:, 4 + h, c4:c4 + 128], ident[:]), r=[sl.b], w=[TBK.b])
            sc.add("act", _act(ktok.t[:], TBK.t[:, tcol:tcol + 512].rearrange("p (h d) -> p h d", h=4), AF.Copy, scale=128.0 ** -0.5),
                   r=[TBK.b], w=[ktok.b])
            for h in range(4):
                sc.add("pe", _mm(sps.t[:, h * 128:(h + 1) * 128], sl.t[:, 4 + h, c4:c4 + 128], sl.t[:, h, c4:c4 + 128], True, True),
                       r=[sl.b], w=[sps.b])
            sc.add("dve", _stt(mm.t[:], sps.t[:].rearrange("p (h d) -> p h d", h=4), 128.0 ** -0.5, bc_mid(tri[:], 4), ALU.mult, ALU.mult),
                   r=[sps.b, mb_], w=[mm.b])
            return sl, c4

        def main(c, d, n, sl, c4):
            eg, v1, ktok, mm, ho = EG[d][n % 2], V1[d][n % 2], KTOK[d][n % 2], MM[d][n % 2], HO[d][n % 2]
            c1, c1b, tc_, dn, rr = C1[d], C1b[d], tmpC[d], den[d], rr_[d]
            r0 = c * 128
            for h in range(4):
                sc.add("pe", _mm(UPS.t[:, h * 256:h * 256 + 129], mm.t[:, h, :], v1.t[:, h, 0:129], True, False), r=[mm.b, v1.b], w=[UPS.b])
                sc.add("pe", _mm(UPS.t[:, h * 256:h * 256 + 129], sl.t[:, h, c4:c4 + 128], c1b.t[:, h, 0:129], False, True),
                       r=[sl.b, c1b.b], w=[UPS.b])
            for h in range(4):
                sc.add("pe", _mm(DPS.t[:, h * 256:h * 256 + 129], ktok.t[:, h, :], v1.t[:, h, 0:129], True, True), r=[ktok.b, v1.b], w=[DPS.b])
            U3 = UPS.t[:].rearrange("p (h d) -> p h d", h=4)
            D3 = DPS.t[:].rearrange("p (h d) -> p h d", h=4)
            sc.add("dve", _tt(tc_.t[:, :, 0:129], D3[:, :, 0:129], c1.t[:, :, 0:129], ALU.add), r=[DPS.b, c1.b], w=[tc_.b])
            sc.add("pool", _tt(c1.t[:, :, 0:129], tc_.t[:, :, 0:129], bc(eg.t[:, 4:8], 129), ALU.mult), r=[tc_.b, eg.b], w=[c1.b])
            sc.add("act", _act(c1b.t[:, :, 0:129], c1.t[:, :, 0:129], AF.Copy), r=[c1.b], w=[c1b.b])
            sc.add("dve", _tt(dn.t[:], U3[:, :, 128], eg.t[:, 0:4], ALU.mult), r=[UPS.b, eg.b], w=[dn.b])
            sc.add("act", _act(dn.t[:], dn.t[:], AF.Abs), r=[dn.b], w=[dn.b])
            sc.add("dve", _ts(dn.t[:], dn.t[:], 1.0, None, ALU.max), r=[dn.b], w=[dn.b])
            sc.add("dve", lambda e: e.reciprocal(rr.t[:], dn.t[:]), r=[dn.b], w=[rr.b])
            sc.add("dve", _tt(rr.t[:], rr.t[:], eg.t[:, 0:4], ALU.mult), r=[rr.b, eg.b], w=[rr.b])
            sc.add("dve", _tt(ho.t[:].rearrange("p (h d) -> p h d", h=4), U3[:, :, 0:128], bc(rr.t[:], 128), ALU.mult),
                   r=[UPS.b, rr.b], w=[ho.b])
            sc.add("pool", _dma(hdst[d][r0:r0 + 128, :], ho.t[:]), r=[ho.b], chan=ho.c)

        orders = (list(range(NB)), list(range(NB - 1, -1, -1)))
        nxt = [None, None]
        for d in range(2):
            sc.add("pool", _memset(C1[d].t[:], 0.0), w=[C1[d].b])
            sc.add("pool", _memset(C1b[d].t[:], 0.0), w=[C1b[d].b])
            nxt[d] = pre(orders[d][0], d, 0)
        for j in range(NB):
            cur = list(nxt)
            if j + 1 < NB:
                for d in range(2):
                    nxt[d] = pre(orders[d][j + 1], d, j + 1)
            for d in range(2):
                main(orders[d][j], d, j, *cur[d])
        sc.emit()

    KT_s = k.dram_tmp("KT_s", [512, S], BF16)
    KR_s = k.dram_tmp("KR_s", [64, S], BF16)
    V_s = k.dram_tmp("V_s", [S, 512], BF16)
    QN_s = k.dram_tmp("QN_s", [512, S], BF16)
    QR_s = k.dram_tmp("QR_s", [4, 65, S], BF16)
    kmax_s = k.dram_tmp("kmax_s", [128, 4])
    TWO_PI = 6.283185307179586

    with contextlib.ExitStack() as st:
        stage = [Slot(k.sb(st, f"wstagec{i}", [128, 1024]), ld_ch[i]) for i in range(2)]
        gq = load_cols(st, "gq", q_norm_g, 2)
        gkv = load_cols(st, "gkv", kv_norm_g, 1)
        Wuq, Wuqb = load_w(st, stage, "Wuq", w_uq, 256, 768, gq)
        Wkv, Wkvb = load_w(st, stage, "Wkv", w_ukv, 128, 1024, gkv)
        Wkv4 = Wkv[:, 0, :].rearrange("p (h t d) -> p h t d", h=4, t=2)
        cos2 = k.sb(st, "cos2", [128, NB, 64])
        sin1 = k.sb(st, "sin1", [128, NB, 32])
        tb_ = Buf("ropetab")
        pos = k.sb(st, "pos", [128, NB])
        invf = k.sb(st, "invf", [128, 32])
        ang = k.sb(st, "ang", [128, NB, 32])
        angi = k.sb(st, "angi", [128, NB, 32], mybir.dt.int32)
        angf = k.sb(st, "angf", [128, NB, 32])
        msk = k.sb(st, "msk", [128, NB, 32])
        sc.add("pool", lambda e: e.iota(pos[:], [[128, NB]], base=0, channel_multiplier=1, allow_small_or_imprecise_dtypes=True), w=[tb_])
        sc.add("pool", lambda e: e.iota(invf[:], [[1, 32]], base=0, channel_multiplier=0, allow_small_or_imprecise_dtypes=True), r=[tb_], w=[tb_])
        sc.add("act", _act(invf[:], invf[:], AF.Exp, scale=-float(np.log(10000.0)) / 32.0), r=[tb_], w=[tb_])
        sc.add("dve", _tt(ang[:], bc(pos[:], 32), bc_mid(invf[:], NB), ALU.mult), r=[tb_], w=[tb_])
        sc.add("dve", _ts(ang[:], ang[:], 1.0 / TWO_PI, None, ALU.mult), r=[tb_], w=[tb_])
        for which in range(2):
            if which == 1:
                sc.add("dve", _ts(ang[:], ang[:], 0.25, None, ALU.add), r=[tb_], w=[tb_])
            sc.add("dve", _cp(angi[:], ang[:]), r=[tb_], w=[tb_])
            sc.add("dve", _cp(angf[:], angi[:]), r=[tb_], w=[tb_])
            sc.add("dve", _tt(angf[:], ang[:], angf[:], ALU.subtract), r=[tb_], w=[tb_])
            sc.add("dve", _ts(msk[:], angf[:], 0.5, None, ALU.is_gt), r=[tb_], w=[tb_])
            sc.add("dve", _tt(angf[:], angf[:], msk[:], ALU.subtract), r=[tb_], w=[tb_])
            sc.add("dve", _ts(msk[:], angf[:], -0.5, None, ALU.is_lt), r=[tb_], w=[tb_])
            sc.add("dve", _tt(angf[:], angf[:], msk[:], ALU.add), r=[tb_], w=[tb_])
            if which == 0:
                sc.add("act", _act(sin1[:], angf[:], AF.Sin, scale=TWO_PI * (1.0 - 1e-6)), r=[tb_], w=[tb_])
            else:
                sc.add("act", _act(cos2[:, :, 0:32], angf[:], AF.Sin, scale=TWO_PI * (1.0 - 1e-6)), r=[tb_], w=[tb_])
                sc.add("act", _act(cos2[:, :, 32:64], angf[:], AF.Sin, scale=TWO_PI * (1.0 - 1e-6)), r=[tb_], w=[tb_])
        sc.barrier()

        G3T = [Slot(k.sb(st, f"g3t{i}", [128, 4, 464]), sc.chan(f"c_g3t{i}")) for i in range(2)]
        junkc = Slot(k.sb(st, "junkc", [128, 3072], BF16))
        ssq = Slot(k.sb(st, "ssq", [128, 8]))
        ssq2 = Slot(k.sb(st, "ssq2", [128, 8]))
        rst = Slot(k.sb(st, "rst", [128, 8]))
        cqn = Slot(k.sb(st, "cqn", [128, 4, 256], BF16))
        ckvn = Slot(k.sb(st, "ckvn", [128, 4, 128], BF16))
        tA = Slot(k.sb(st, "tA", [128, 4, 64]))
        tB = Slot(k.sb(st, "tB", [128, 4, 64]))
        krb = Slot(k.sb(st, "krb", [128, 4, 64], BF16))
        sqr = Slot(k.sb(st, "sqr", [128, 4, 64]))
        kr2 = Slot(k.sb(st, "kr2", [128, 4]))
        cqT = Slot(k.sb(st, "cqT", [128, 2, 512], BF16))
        ckvT = Slot(k.sb(st, "ckvT", [128, 512], BF16))
        krT = Slot(k.sb(st, "krT", [64, 512], BF16), sc.chan("c_krT"))
        KTS = [Slot(k.sb(st, f"kts{i}", [128, 512], BF16), sc.chan(f"c_kts{i}")) for i in range(2)]
        VS = [Slot(k.sb(st, f"vs{i}", [128, 512], BF16), sc.chan(f"c_vs{i}")) for i in range(2)]
        sqk = Slot(k.sb(st, "sqk", [128, 512]))
        kn2 = Slot(k.sb(st, "kn2", [128, 4, 4]))
        kmax = Slot(k.sb(st, "kmax", [128, 4]))
        kmt = Slot(k.sb(st, "kmt", [128, 4]))
        q_sb = Slot(k.sb(st, "q_sb", [128, 4, 768]))
        qtA = Slot(k.sb(st, "qtA", [128, 4, 4, 64]))
        qtB = Slot(k.sb(st, "qtB", [128, 4, 4, 64]))
        qbn = Slot(k.sb(st, "qbn", [128, 4, 4, 128], BF16))
        qbr = Slot(k.sb(st, "qbr", [128, 4, 4, 66], BF16))
        qn2 = Slot(k.sb(st, "qn2", [128, 16]))
        qn1 = Slot(k.sb(st, "qn1", [128, 16]))
        QS = [Slot(k.sb(st, f"qs{i}", [128, 2, 512], BF16), sc.chan(f"c_qs{i}")) for i in range(2)]
        QRS = [Slot(k.sb(st, f"qrs{i}", [65, 2, 512], BF16), sc.chan(f"c_qrs{i}")) for i in range(2)]
        PB = [Slot(k.ps(st, f"pb{i}", [128, 512])) for i in range(8)]
        pbi = {"i": 0}

        def bank():
            pbi["i"] += 1
            return PB[pbi["i"] % 8]

        def bfv(slot):
            return slot.t[:].bitcast(BF16)

        sc.add("pool", _memset(kmax.t[:], 0.0), w=[kmax.b])
        sc.add("pool", _memset(qbr.t[:], 0.0), w=[qbr.b])
        q4 = q_sb.t[:].rearrange("p b (h d) -> p b h d", h=4)
        kti = 0
        for i in range(NT):
            t0 = i * 512
            g3 = G3T[i % 2]
            sc.add("sp", _dma(g3.t[:], g3_s[t0:t0 + 512, :].rearrange("(b p) c -> p b c", p=128)), w=[g3.b], chan=g3.c)
            for b in range(4):
                sc.add("act", _act(junkc.t[:, 0:256], g3.t[:, b, 16:272], AF.Square, scale=1.0 / 16.0, accum_out=ssq.t[:, b:b + 1]),
                       r=[g3.b], w=[junkc.b, ssq.b])
                sc.add("act", _act(junkc.t[:, 0:128], g3.t[:, b, 272:400], AF.Square, scale=128.0 ** -0.5, accum_out=ssq.t[:, 4 + b:5 + b]),
                       r=[g3.b], w=[junkc.b, ssq.b])
            sc.add("dve", _ts(ssq2.t[:], ssq.t[:], EPS, None, ALU.add), r=[ssq.b], w=[ssq2.b])
            sc.add("pool", _tt(rst.t[:], ssq2.t[:], mhalf[:, 0:8], ALU.pow), r=[ssq2.b], w=[rst.b])
            sc.add("dve", _tt(cqn.t[:], g3.t[:, :, 16:272], bc(rst.t[:, 0:4], 256), ALU.mult), r=[g3.b, rst.b], w=[cqn.b])
            sc.add("dve", _tt(ckvn.t[:], g3.t[:, :, 272:400], bc(rst.t[:, 4:8], 128), ALU.mult), r=[g3.b, rst.b], w=[ckvn.b])
            xk = g3.t[:, :, 400:464]
            cs, sn = cos2[:, 4 * i:4 * i + 4, :], sin1[:, 4 * i:4 * i + 4, :]
            sc.add("dve", _tt(tA.t[:], xk, cs, ALU.mult), r=[g3.b], w=[tA.b])
            sc.add("dve", _tt(tB.t[:, :, 0:32], g3.t[:, :, 432:464], sn, ALU.mult), r=[g3.b], w=[tB.b])
            sc.add("dve", _tt(tB.t[:, :, 32:64], g3.t[:, :, 400:432], sn, ALU.mult), r=[g3.b], w=[tB.b])
            sc.add("dve", _tt(krb.t[:, :, 0:32], tA.t[:, :, 0:32], tB.t[:, :, 0:32], ALU.subtract), r=[tA.b, tB.b], w=[krb.b])
            sc.add("dve", _tt(krb.t[:, :, 32:64], tA.t[:, :, 32:64], tB.t[:, :, 32:64], ALU.add), r=[tA.b, tB.b], w=[krb.b])
            sc.add("act", _act(sqr.t[:], xk, AF.Square), r=[g3.b], w=[sqr.b])
            sc.add("dve", lambda e: e.tensor_reduce(kr2.t[:], sqr.t[:], AX.X, ALU.add), r=[sqr.b], w=[kr2.b])
            pa, pb2 = bank(), bank()
            for b in range(4):
                for kc in range(2):
                    sc.add("pe", _tr(bfv(pa)[:, kc * 512 + b * 128:kc * 512 + (b + 1) * 128], cqn.t[:, b, kc * 128:(kc + 1) * 128], ident[:]),
                           r=[cqn.b], w=[pa.b])
                sc.add("pe", _tr(bfv(pb2)[:, b * 128:(b + 1) * 128], ckvn.t[:, b, :], ident[:]), r=[ckvn.b], w=[pb2.b])
                sc.add("pe", _tr(bfv(pb2)[0:64, 512 + b * 128:512 + (b + 1) * 128], krb.t[:, b, :], ident[:]), r=[krb.b], w=[pb2.b])
            sc.add("act", _act(cqT.t[:], bfv(pa).rearrange("p (a b) -> p a b", a=2), AF.Copy), r=[pa.b], w=[cqT.b])
            sc.add("dve", _cp(ckvT.t[:], bfv(pb2)[:, 0:512]), r=[pb2.b], w=[ckvT.b])
            sc.add("act", _act(krT.t[:], bfv(pb2)[0:64, 512:1024], AF.Copy), r=[pb2.b], w=[krT.b])
            sc.add("pool", _dma(KR_s[:, t0:t0 + 512], krT.t[:]), r=[krT.b], chan=krT.c)
            for h in range(4):
                pk = bank()
                sc.add("pe", _mm(pk.t[:], Wkv[:, 0, h * 256:h * 256 + 128], ckvT.t[:], True, True), r=[ckvT.b, Wkvb], w=[pk.b])
                kts = KTS[kti % 2]
                kti += 1
                e = "act" if h % 2 else "dve"
                sc.add(e, scale_cast(e, kts.t[:], pk.t[:]), r=[pk.b], w=[kts.b])
                sc.add("pool", _dma(KT_s[h * 128:(h + 1) * 128, t0:t0 + 512], kts.t[:]), r=[kts.b], chan=kts.c)
            for b in range(4):
                tok = slice(b * 128, (b + 1) * 128)
                pv = bank()
                sc.add("pe", _mm(pv.t[:].rearrange("p (h d) -> p h d", h=4), ckvT.t[:, tok], Wkv4[:, :, 1, :], True, True),
                       r=[ckvT.b, Wkvb], w=[pv.b])
                vs = VS[b % 2]
                sc.add("act", _act(vs.t[:], pv.t[:], AF.Copy), r=[pv.b], w=[vs.b])
                sc.add("pool", _dma(V_s[t0 + b * 128:t0 + (b + 1) * 128, :], vs.t[:]), r=[vs.b], chan=vs.c)
                pk = bank()
                sc.add("pe", _mm(pk.t[:].rearrange("p (h d) -> p h d", h=4), ckvT.t[:, tok], Wkv4[:, :, 0, :], True, True),
                       r=[ckvT.b, Wkvb], w=[pk.b])
                sc.add("act", _act(sqk.t[:], pk.t[:], AF.Square), r=[pk.b], w=[sqk.b])
                sc.add("dve", lambda e, b=b: e.tensor_reduce(kn2.t[:, b, :], sqk.t[:].rearrange("p (h d) -> p h d", h=4), AX.X, ALU.add),
                       r=[sqk.b], w=[kn2.b])
                pq0, pq1 = bank(), bank()
                for kc in range(2):
                    sc.add("pe", _mm(pq0.t[:], cqT.t[:, kc, tok], Wuq[:, kc, 0:512], kc == 0, kc == 1), r=[cqT.b, Wuqb], w=[pq0.b])
                for kc in range(2):
                    sc.add("pe", _mm(pq1.t[:, 0:256], cqT.t[:, kc, tok], Wuq[:, kc, 512:768], kc == 0, kc == 1), r=[cqT.b, Wuqb], w=[pq1.b])
                sc.add("act", _act(q_sb.t[:, b, 0:512], pq0.t[:], AF.Copy), r=[pq0.b], w=[q_sb.b])
                sc.add("dve", _cp(q_sb.t[:, b, 512:768], pq1.t[:, 0:256]), r=[pq1.b], w=[q_sb.b])
            sc.add("dve", _tt(kn2.t[:], kn2.t[:], bc(kr2.t[:], 4), ALU.add), r=[kn2.b, kr2.b], w=[kn2.b])
            sc.add("dve", lambda e: e.tensor_reduce(kmt.t[:], kn2.t[:].rearrange("p b h -> p h b"), AX.X, ALU.max), r=[kn2.b], w=[kmt.b])
            sc.add("dve", _tt(kmax.t[:], kmax.t[:], kmt.t[:], ALU.max), r=[kmax.b, kmt.b], w=[kmax.b])
            cs4 = bass.AP(cs.tensor, cs.offset, [list(cs.ap[0]), list(cs.ap[1]), [0, 4], list(cs.ap[2])])
            sn4 = bass.AP(sn.tensor, sn.offset, [list(sn.ap[0]), list(sn.ap[1]), [0, 4], list(sn.ap[2])])
            sc.add("dve", _tt(qtA.t[:], q4[:, :, :, 128:192], cs4, ALU.mult), r=[q_sb.b], w=[qtA.b])
            sc.add("dve", _tt(qtB.t[:, :, :, 0:32], q4[:, :, :, 160:192], sn4, ALU.mult), r=[q_sb.b], w=[qtB.b])
            sc.add("dve", _tt(qtB.t[:, :, :, 32:64], q4[:, :, :, 128:160], sn4, ALU.mult), r=[q_sb.b], w=[qtB.b])
            sc.add("dve", _tt(qbr.t[:, :, :, 0:32], qtA.t[:, :, :, 0:32], qtB.t[:, :, :, 0:32], ALU.subtract), r=[qtA.b, qtB.b], w=[qbr.b])
            sc.add("dve", _tt(qbr.t[:, :, :, 32:64], qtA.t[:, :, :, 32:64], qtB.t[:, :, :, 32:64], ALU.add), r=[qtA.b, qtB.b], w=[qbr.b])
            sc.add("dve", _cp(qbn.t[:], q4[:, :, :, 0:128]), r=[q_sb.b], w=[qbn.b])
            sc.add("act", _act(junkc.t[:], q_sb.t[:].rearrange("p b c -> p (b c)"), AF.Square), r=[q_sb.b], w=[junkc.b])
            sc.add("dve", lambda e: e.tensor_reduce(qn2.t[:], junkc.t[:].rearrange("p (g d) -> p g d", g=16), AX.X, ALU.add),
                   r=[junkc.b], w=[qn2.b])
            sc.add("pool", _tt(qn1.t[:], qn2.t[:], mhalf[:, 0:16], ALU.pow), r=[qn2.b], w=[qn1.b])
            sc.add("dve", _tt(qn1.t[:], qn1.t[:], qn2.t[:], ALU.mult), r=[qn1.b, qn2.b], w=[qn1.b])
            sc.add("dve", _ts(qbr.t[:, :, :, 64], qn1.t[:].rearrange("p (b h) -> p b h", b=4), -1.01, None, ALU.mult),
                   r=[qn1.b], w=[qbr.b])
            for hp in range(2):
                pn, pr = bank(), bank()
                for hh in range(2):
                    h = 2 * hp + hh
                    for b in range(4):
                        sc.add("pe", _tr(bfv(pn)[:, hh * 512 + b * 128:hh * 512 + (b + 1) * 128], qbn.t[:, b, h, :], ident[:]), r=[qbn.b], w=[pn.b])
                        sc.add("pe", _tr(bfv(pr)[0:65, hh * 512 + b * 128:hh * 512 + (b + 1) * 128], qbr.t[:, b, h, 0:65], ident[:]), r=[qbr.b], w=[pr.b])
                qs, qrs = QS[hp], QRS[hp]
                sc.add("act", _act(qs.t[:], bfv(pn).rearrange("p (a b) -> p a b", a=2), AF.Copy), r=[pn.b], w=[qs.b])
                sc.add("dve", _cp(qrs.t[:], bfv(pr)[0:65, :].rearrange("p (a b) -> p a b", a=2)), r=[pr.b], w=[qrs.b])
                sc.add("pool", _dma(QN_s.rearrange("(h p) s -> p h s", p=128)[:, 2 * hp:2 * hp + 2, t0:t0 + 512], qs.t[:]), r=[qs.b], chan=qs.c)
                sc.add("pool", _dma(QR_s.rearrange("h p s -> p h s")[:, 2 * hp:2 * hp + 2, t0:t0 + 512], qrs.t[:]), r=[qrs.b], chan=qrs.c)
        kmo = Slot(k.sb(st, "kmo", [128, 4]), sc.chan("c_kmo"))
        sc.add("dve", _cp(kmo.t[:], kmax.t[:]), r=[kmax.b], w=[kmo.b])
        sc.add("pool", _dma(kmax_s[:, :], kmo.t[:]), r=[kmo.b], chan=kmo.c)
        sc.emit()

    with contextlib.ExitStack() as st:
        KT = Slot(k.sb(st, "KT", [128, 4, S], BF16), sc.chan("c_KT"))
        KR = Slot(k.sb(st, "KR", [65, S], BF16), sc.chan("c_KR"))
        VR = Slot(k.sb(st, "VR", [128, NB, 512], BF16), sc.chan("c_VR"))
        kml = Slot(k.sb(st, "kml", [128, 4]), sc.chan("c_kml"))
        km1 = Slot(k.sb(st, "km1", [1, 4]))
        kmx = Slot(k.sb(st, "kmx", [128, 4]))
        phalf = Slot(k.sb(st, "phalf", [128, 4]))
        SPB = [Slot(k.ps(st, f"spb{i}", [128, 512])) for i in range(3)]
        OPB = [Slot(k.ps(st, f"opb{i}", [128, 512])) for i in range(2)]
        RSB = Slot(k.ps(st, "rsb", [128, 512]))
        NPT = 10
        PT = [Slot(k.sb(st, f"pt{i}", [128, 512], BF16)) for i in range(NPT)]
        RS = [Slot(k.ps(st, f"rs{i}", [128, 512])) for i in range(1)]
        rs_sb = Slot(k.sb(st, "rs_sb", [128, 512]))
        ones_b = Slot(k.sb(st, "ones_b", [128, 32], BF16))
        inv32 = Slot(k.sb(st, "inv32", [128, 128]))
        sc.add("pool", _memset(ones_b.t[:], 1.0), w=[ones_b.b])
        sc.add("pool", _memset(inv32.t[:], 1.0 / 32.0), w=[inv32.b])
        QN = [Slot(k.sb(st, f"qnt{i}", [128, 512], BF16), sc.chan(f"c_qn{i}")) for i in range(2)]
        QR = [Slot(k.sb(st, f"qrt{i}", [65, 512], BF16), sc.chan(f"c_qr{i}")) for i in range(2)]
        rinv = Slot(k.sb(st, "rinv", [128, 512]))
        YO = [Slot(k.sb(st, f"yo{i}", [128, 512], BF16), sc.chan(f"c_yo{i}")) for i in range(2)]
        sc.add("sp", _dma(KT.t[:], KT_s.rearrange("(h p) s -> p h s", p=128)), w=[KT.b], chan=KT.c)
        sc.add("pool", _memset(KR.t[64:65, :], 1.0), w=[KR.b])
        sc.add("sp", _dma(KR.t[0:64, :], KR_s[:, :]), w=[KR.b], chan=KR.c)
        sc.add("sp", _dma(VR.t[:], V_s.rearrange("(c p) d -> p c d", p=128)), w=[VR.b], chan=VR.c)
        sc.add("sp", _dma(kml.t[:], kmax_s[:, :]), w=[kml.b], chan=kml.c)
        sc.add("pool", _memset(phalf.t[:], 0.5), w=[phalf.b])
        sc.add("pool", lambda e: e.tensor_reduce(km1.t[:], kml.t[:], AX.C, ALU.max), r=[kml.b], w=[km1.b])
        sc.add("pe", _mm(RSB.t[:, 0:4], ones_f[0:1, :], km1.t[:], True, True), r=[km1.b], w=[RSB.b])
        sc.add("dve", _cp(kmx.t[:], RSB.t[:, 0:4]), r=[RSB.b], w=[kmx.b])
        sc.add("pool", _tt(kmx.t[:], kmx.t[:], phalf.t[:], ALU.pow), r=[kmx.b, phalf.b], w=[kmx.b])
        scale = 192.0 ** -0.5
        it = 0
        pti = 0
        for h in range(4):
            for j in range(NT):
                qn, qr = QN[it % 2], QR[it % 2]
                opb, yo, rs = OPB[it % 2], YO[it % 2], RS[0]
                it += 1
                sc.add("sp", _dma(qn.t[:], QN_s[h * 128:(h + 1) * 128, j * 512:(j + 1) * 512]), w=[qn.b], chan=qn.c)
                sc.add("sp", _dma(qr.t[:], QR_s[h, :, j * 512:(j + 1) * 512]), w=[qr.b], chan=qr.c)
                sc.add("dve", _ts(qr.t[64:65, :], qr.t[64:65, :], kmx.t[64:65, h:h + 1], None, ALU.mult), r=[qr.b, kmx.b], w=[qr.b])

                def qk(kc):
                    sp_ = SPB[kc % 3]
                    ks = slice(kc * 128, (kc + 1) * 128)
                    sc.add("pe", _mm(sp_.t[:], KT.t[:, h, ks], qn.t[:], True, False), r=[KT.b, qn.b], w=[sp_.b])
                    sc.add("pe", _mm(sp_.t[:], KR.t[0:65, ks], qr.t[0:65, :], False, True), r=[KR.b, qr.b], w=[sp_.b])

                qk(0)
                if NB > 1:
                    qk(1)
                grp = []
                for kc in range(NB):
                    if kc + 2 < NB:
                        qk(kc + 2)
                    sp_ = SPB[kc % 3]
                    pt = PT[pti % NPT]
                    pti += 1
                    sc.add("act", _act(pt.t[:], sp_.t[:], AF.Exp, scale=scale), r=[sp_.b], w=[pt.b])
                    sc.add("pe", _mm(opb.t[:], VR.t[:, kc, h * 128:(h + 1) * 128], pt.t[:], kc == 0, kc == NB - 1),
                           r=[VR.b, pt.b], w=[opb.b])
                    grp.append(pt)
                    if len(grp) == 4:
                        for r_, ptr in enumerate(grp):
                            sc.add("pe", lambda e, r_=r_, ptr=ptr, kc=kc: e.matmul(rs.t[32 * r_:32 * r_ + 32, :], ones_b.t[:, 0:32], ptr.t[:],
                                                                               start=(kc == 3), stop=(kc == NB - 1),
                                                                               tile_position=(0, 32 * r_)),
                                   r=[ptr.b, ones_b.b], w=[rs.b])
                        grp = []
                sc.add("dve", _cp(rs_sb.t[:], rs.t[:]), r=[rs.b], w=[rs_sb.b])
                sc.add("pe", _mm(RSB.t[:], inv32.t[:], rs_sb.t[:], True, True), r=[rs_sb.b, inv32.b], w=[RSB.b])
                sc.add("dve", lambda e: e.reciprocal(rinv.t[:], RSB.t[:]), r=[RSB.b], w=[rinv.b])
                sc.add("dve", _tt(yo.t[:], opb.t[:], rinv.t[:], ALU.mult), r=[opb.b, rinv.b], w=[yo.b])
                sc.add("pool", _dma(yT_s[512 + h * 128:512 + (h + 1) * 128, j * 512:(j + 1) * 512], yo.t[:]), r=[yo.b], chan=yo.c)
        sc.emit()

    h1_s = k.dram_tmp("h1_s", [S, D])
    xn2T_s = k.dram_tmp("xn2T_s", [D, S + 2], BF16)
    h2_s = k.dram_tmp("h2_s", [S, D])
    yT3 = yT_s.rearrange("(g p) s -> p g s", p=128)
    xn2T3 = xn2T_s.rearrange("(g p) s -> p g s", p=128)

    def rms_transpose(xt, ss, ss2, rstd, junk, XB, xn, TBs, gain_scale=1.0 / 32.0):
        for b in range(4):
            sc.add("act", _act(junk.t[:], xt.t[:, b, :], AF.Square, scale=gain_scale, accum_out=ss.t[:, b:b + 1]),
                   r=[xt.b], w=[junk.b, ss.b])
        sc.add("dve", _ts(ss2.t[:], ss.t[:], EPS, None, ALU.add), r=[ss.b], w=[ss2.b])
        sc.add("pool", _tt(rstd.t[:], ss2.t[:], mhalf[:, 0:4], ALU.pow), r=[ss2.b], w=[rstd.b])
        for b in range(4):
            e = "dve" if b % 2 else "act"
            sc.add(e, scale_cast(e, XB.t[:, b, :], xt.t[:, b, :], rstd.t[:, b:b + 1]), r=[xt.b, rstd.b], w=[XB.b])
        for j in range(4):
            tb = TBs[j % 2]
            for kk in range(2):
                kc = 2 * j + kk
                for b in range(4):
                    sc.add("pe", _tr(tb.t[:, kk * 512 + b * 128:kk * 512 + (b + 1) * 128],
                                     XB.t[:, b, kc * 128:(kc + 1) * 128], ident[:]), r=[XB.b], w=[tb.b])
            e = "dve" if j % 2 else "act"
            sc.add(e, scale_cast(e, xn.t[:, 2 * j:2 * j + 2, :], tb.t[:].rearrange("p (a b) -> p a b", a=2)),
                   r=[tb.b], w=[xn.b])

    with contextlib.ExitStack() as st:
        stage = [Slot(k.sb(st, f"wstaged{i}", [128, 1024]), ld_ch[i]) for i in range(2)]
        Wout, Woutb = load_w(st, stage, "Wout", w_out, D, D)
        normg = load_bcast(st, "normg", mlstm_norm_g, 512)
        zt = Slot(k.sb(st, "zt", [128, 8, 2], BF16), sc.chan("c_zt"))
        sc.add("pool", _memset(zt.t[:], 0.0), w=[zt.b])
        sc.add("pool", _dma(xn2T3[:, :, 0:1], zt.t[:, :, 0:1], allow_slow_non_contiguous=True), r=[zt.b], chan=zt.c)
        sc.add("pool", _dma(xn2T3[:, :, S + 1:S + 2], zt.t[:, :, 1:2], allow_slow_non_contiguous=True), r=[zt.b], chan=zt.c)
        HFT = [Slot(k.sb(st, f"hft{i}", [128, 4, 512]), sc.chan(f"c_hft{i}")) for i in range(2)]
        HBT = [Slot(k.sb(st, f"hbt{i}", [128, 4, 512]), sc.chan(f"c_hbt{i}")) for i in range(2)]
        SOT = [Slot(k.sb(st, f"sot{i}", [128, 4, 512]), sc.chan(f"c_sot{i}")) for i in range(2)]
        sqd = Slot(k.sb(st, "sqd", [128, 4, 512], BF16))
        ssn = Slot(k.sb(st, "ssnd", [128, 16]))
        rsn = Slot(k.sb(st, "rsnd", [128, 16]))
        YB = [Slot(k.sb(st, f"ybd{i}", [128, 4, 512], BF16)) for i in range(2)]
        YAT = [Slot(k.sb(st, f"yat{i}", [128, 4, 512], BF16)) for i in range(2)]
        YTT = [Slot(k.sb(st, f"ytt{i}", [128, 4, 512], BF16), sc.chan(f"c_ytt{i}")) for i in range(2)]
        XT = [Slot(k.sb(st, f"xtd{i}", [128, 4, D]), sc.chan(f"c_xtd{i}")) for i in range(3)]
        XB = Slot(k.sb(st, "xbd", [128, 4, D], BF16))
        XN = [Slot(k.sb(st, f"xnd{i}", [128, 8, 512], BF16), sc.chan(f"c_xnd{i}")) for i in range(2)]
        junk = Slot(k.sb(st, "junkd", [128, D], BF16))
        ss = Slot(k.sb(st, "ssd", [128, 4]))
        ss2 = Slot(k.sb(st, "ss2d", [128, 4]))
        rstd = Slot(k.sb(st, "rstdd", [128, 4]))
        TBs = [Slot(k.ps(st, f"tbd{i}", [128, 1024], BF16)) for i in range(2)]
        MB = [Slot(k.ps(st, f"mbd{i}", [128, 512])) for i in range(6)]
        cntD = {"mbi": 0}

        def combine(i):
            t0 = i * 512
            hft, hbt, sot, yb, ytt, xt = HFT[i % 2], HBT[i % 2], SOT[i % 2], YB[i % 2], YTT[i % 2], XT[i % 3]
            tv = lambda ap: ap[t0:t0 + 512, :].rearrange("(b p) d -> p b d", p=128)
            sc.add("sp", _dma(hft.t[:], tv(hf_s)), w=[hft.b], chan=hft.c)
            sc.add("sp", _dma(hbt.t[:], tv(hb_s)), w=[hbt.b], chan=hbt.c)
            sc.add("sp", _dma(sot.t[:], vo_s[t0:t0 + 512, 512:1024].rearrange("(b p) d -> p b d", p=128)), w=[sot.b], chan=sot.c)
            sc.add("sp", _dma(ytt.t[:], yT3[:, 4:8, t0:t0 + 512]), w=[ytt.b], chan=ytt.c)
            sc.add("sp", _dma(xt.t[:], tv(x)), w=[xt.b], chan=xt.c)
            sc.add("dve", _tt(hft.t[:], hft.t[:], hbt.t[:], ALU.add), r=[hft.b, hbt.b], w=[hft.b])
            sc.add("act", _act(sqd.t[:], hft.t[:], AF.Square, scale=128.0 ** -0.5), r=[hft.b], w=[sqd.b])
            sc.add("dve", lambda e: e.tensor_reduce(ssn.t[:], sqd.t[:].rearrange("p b (h d) -> p (b h) d", h=4), AX.X, ALU.add),
                   r=[sqd.b], w=[ssn.b])
            sc.add("dve", _ts(ssn.t[:], ssn.t[:], EPS, None, ALU.add), r=[ssn.b], w=[ssn.b])
            sc.add("pool", _tt(rsn.t[:], ssn.t[:], mhalf[:, 0:16], ALU.pow), r=[ssn.b], w=[rsn.b])
            h16 = hft.t[:].rearrange("p b (h d) -> p (b h) d", h=4)
            sc.add("dve", _tt(h16, h16, bc(rsn.t[:], 128), ALU.mult), r=[hft.b, rsn.b], w=[hft.b])
            sc.add("pool", _tt(sot.t[:], sot.t[:], bc_mid(normg[0][:], 4), ALU.mult), r=[sot.b, normg[1]], w=[sot.b])
            sc.add("dve", _tt(yb.t[:], hft.t[:], sot.t[:], ALU.mult), r=[hft.b, sot.b], w=[yb.b])

        def normelt(i):
            t0 = i * 512
            xt = XT[i % 3]
            sc.add("pool", _dma(h1_s[t0:t0 + 512, :].rearrange("(b p) d -> p b d", p=128), xt.t[:]), r=[xt.b], chan=xt.c)
            for b in range(4):
                sc.add("act", _act(junk.t[:], xt.t[:, b, :], AF.Square, scale=1.0 / 32.0, accum_out=ss.t[:, b:b + 1]),
                       r=[xt.b], w=[junk.b, ss.b])
            sc.add("dve", _ts(ss2.t[:], ss.t[:], EPS, None, ALU.add), r=[ss.b], w=[ss2.b])
            sc.add("pool", _tt(rstd.t[:], ss2.t[:], mhalf[:, 0:4], ALU.pow), r=[ss2.b], w=[rstd.b])
            for b in range(4):
                e = "dve" if b % 2 else "act"
                sc.add(e, scale_cast(e, XB.t[:, b, :], xt.t[:, b, :], rstd.t[:, b:b + 1]), r=[xt.b, rstd.b], w=[XB.b])

        def mmstage(i):
            yb, yat, ytt, xt = YB[i % 2], YAT[i % 2], YTT[i % 2], XT[i % 3]
            for hp in range(2):
                tb = TBs[hp]
                for hh in range(2):
                    h = 2 * hp + hh
                    for b in range(4):
                        sc.add("pe", _tr(tb.t[:, hh * 512 + b * 128:hh * 512 + (b + 1) * 128], yb.t[:, b, h * 128:(h + 1) * 128], ident[:]),
                               r=[yb.b], w=[tb.b])
                e = "dve" if hp else "act"
                sc.add(e, scale_cast(e, yat.t[:, 2 * hp:2 * hp + 2, :], tb.t[:].rearrange("p (a b) -> p a b", a=2)), r=[tb.b], w=[yat.b])
            mbi = cntD["mbi"]
            for b in range(4):
                for half in range(2):
                    pm = MB[mbi % 6]
                    mbi += 1
                    for kc in range(8):
                        src = yat if kc < 4 else ytt
                        sc.add("pe", _mm(pm.t[:], src.t[:, kc % 4, b * 128:(b + 1) * 128], Wout[:, kc, half * 512:(half + 1) * 512],
                                         kc == 0, kc == 7), r=[src.b, Woutb], w=[pm.b])
                    sc.add("dve", _tt(xt.t[:, b, half * 512:(half + 1) * 512], pm.t[:], xt.t[:, b, half * 512:(half + 1) * 512], ALU.add),
                           r=[pm.b, xt.b], w=[xt.b])
            cntD["mbi"] = mbi

        def trstage(i):
            t0 = i * 512
            xn = XN[i % 2]
            for j in range(4):
                tb = TBs[j % 2]
                for kk in range(2):
                    kc = 2 * j + kk
                    for b in range(4):
                        sc.add("pe", _tr(tb.t[:, kk * 512 + b * 128:kk * 512 + (b + 1) * 128],
                                         XB.t[:, b, kc * 128:(kc + 1) * 128], ident[:]), r=[XB.b], w=[tb.b])
                e = "dve" if j % 2 else "act"
                sc.add(e, scale_cast(e, xn.t[:, 2 * j:2 * j + 2, :], tb.t[:].rearrange("p (a b) -> p a b", a=2)),
                       r=[tb.b], w=[xn.b])
            sc.add("pool", _dma(xn2T3[:, :, 1 + t0:1 + t0 + 512], xn.t[:]), r=[xn.b], chan=xn.c)

        combine(0)
        for i in range(NT + 1):
            if i + 1 < NT:
                combine(i + 1)
            if i >= 1:
                normelt(i - 1)
            if i < NT:
                mmstage(i)
            if i >= 1:
                trstage(i - 1)
        sc.emit()

    TT_ = 256
    with contextlib.ExitStack() as st:
        stage = [Slot(k.sb(st, f"wstagee{i}", [128, 1408]), ld_ch[i]) for i in range(2)]
        gffn = load_cols(st, "gffn", ln_ffn_g, 8)
        Wup, Wupb = load_w(st, stage, "Wup", w_up, D, 2 * D_FF, gffn)
        Wdn, Wdnb = load_w(st, stage, "Wdn", w_down, D_FF, D)
        fw = k.sb(st, "fw", [128, 44, 3])
        fwb = Buf("fw")
        for tap in range(3):
            sc.add("sp", _dma(fw[:, :, tap], conv_ffn_w[tap].rearrange("(g p) -> p g", p=128),
                              allow_slow_non_contiguous=True), w=[Buf()], chan=sc.chan(f"c_fw{tap}"))
        fb = load_cols(st, "fb", conv_ffn_b, 44)
        sc.barrier()
        XS = [Slot(k.sb(st, f"xs{i}", [128, 8, TT_ + 2], BF16), sc.chan(f"c_xs{i}")) for i in range(2)]
        H1 = [Slot(k.sb(st, f"h1t{i}", [128, 2, D]), sc.chan(f"c_h1t{i}")) for i in range(2)]
        AT = [Slot(k.sb(st, f"at{i}", [128, 22, TT_], BF16)) for i in range(2)]
        CG = [Slot(k.sb(st, f"cg{i}", [128, TT_])) for i in range(2)]
        CV = [Slot(k.sb(st, f"cv{i}", [128, TT_])) for i in range(2)]
        SG = [Slot(k.sb(st, f"sg{i}", [128, TT_])) for i in range(2)]
        MB = [Slot(k.ps(st, f"mbe{i}", [128, 512])) for i in range(8)]
        mbi = 0
        for i in range(S // TT_):
            t0 = i * TT_
            xs, h1, at = XS[i % 2], H1[i % 2], AT[i % 2]
            sc.add("sp", _dma(xs.t[:], xn2T3[:, :, t0:t0 + TT_ + 2]), w=[xs.b], chan=xs.c)
            sc.add("sp", _dma(h1.t[:], h1_s[t0:t0 + TT_, :].rearrange("(b p) d -> p b d", p=128)), w=[h1.b], chan=h1.c)
            for g in range(22):
                res = []
                for which, (gi, dst) in enumerate(((g, CG[g % 2]), (22 + g, CV[g % 2]))):
                    pm = MB[mbi % 8]
                    mbi += 1
                    for kc in range(8):
                        sc.add("pe", _mm(pm.t[:, 0:TT_ + 2], Wup[:, kc, gi * 128:(gi + 1) * 128], xs.t[:, kc, :], kc == 0, kc == 7),
                               r=[xs.b, Wupb], w=[pm.b])
                    sc.add("act", _act(dst.t[:], pm.t[:, 0:TT_], AF.Identity, scale=fw[:, gi, 0:1], bias=fb[0][:, gi:gi + 1]),
                           r=[pm.b], w=[dst.b])
                    sc.add("dve", _stt(dst.t[:], pm.t[:, 1:TT_ + 1], fw[:, gi, 1:2], dst.t[:], ALU.mult, ALU.add), r=[pm.b, dst.b], w=[dst.b])
                    sc.add("dve", _stt(dst.t[:], pm.t[:, 2:TT_ + 2], fw[:, gi, 2:3], dst.t[:], ALU.mult, ALU.add), r=[pm.b, dst.b], w=[dst.b])
                cg, cv, sg = CG[g % 2], CV[g % 2], SG[g % 2]
                sc.add("act", _act(sg.t[:], cg.t[:], AF.Silu), r=[cg.b], w=[sg.b])
                sc.add("pool", _tt(at.t[:, g, :], sg.t[:], cv.t[:], ALU.mult), r=[sg.b, cv.b], w=[at.b])
            for b in range(TT_ // 128):
                for half in range(2):
                    pm = MB[mbi % 8]
                    mbi += 1
                    for g in range(22):
                        sc.add("pe", _mm(pm.t[:], at.t[:, g, b * 128:(b + 1) * 128], Wdn[:, g, half * 512:(half + 1) * 512], g == 0, g == 21),
                               r=[at.b, Wdnb], w=[pm.b])
                    sc.add("dve", _tt(h1.t[:, b, half * 512:(half + 1) * 512], pm.t[:], h1.t[:, b, half * 512:(half + 1) * 512], ALU.add),
                           r=[pm.b, h1.b], w=[h1.b])
            sc.add("pool", _dma(h2_s[t0:t0 + TT_, :].rearrange("(b p) d -> p b d", p=128), h1.t[:]), r=[h1.b], chan=h1.c)
        sc.emit()

    with contextlib.ExitStack() as st:
        stage = [Slot(k.sb(st, f"wstagef{i}", [128, 1024]), ld_ch[i]) for i in range(2)]
        gple = load_cols(st, "gple", ple_norm_g, 8)
        Wg, Wgb = load_w(st, stage, "Wg", w_ple_gate, D, D, gple)
        Wp, Wpb = load_w(st, stage, "Wp", w_ple_proj, 256, D)
        postg = load_bcast(st, "postg", ple_post_g, D)
        fing = load_bcast(st, "fing", final_g, D)
        sc.barrier()
        XT = [Slot(k.sb(st, f"xtf{i}", [128, 4, D]), sc.chan(f"c_xtf{i}")) for i in range(2)]
        PTL = [Slot(k.sb(st, f"ptl{i}", [128, 4, 256]), sc.chan(f"c_ptl{i}")) for i in range(2)]
        XB = Slot(k.sb(st, "xbf", [128, 4, D], BF16))
        XNF = [Slot(k.sb(st, f"xnf{i}", [128, 8, 512], BF16)) for i in range(2)]
        PBf = Slot(k.sb(st, "pbf", [128, 4, 256], BF16))
        PTTF = [Slot(k.sb(st, f"ptt{i}", [128, 2, 512], BF16)) for i in range(2)]
        junk = Slot(k.sb(st, "junkf", [128, D], BF16))
        ss = Slot(k.sb(st, "ssf", [128, 4]))
        ss2 = Slot(k.sb(st, "ss2f", [128, 4]))
        rstd = Slot(k.sb(st, "rstdf", [128, 4]))
        ssb = Slot(k.sb(st, "ssb", [128, 2]))
        ssb2 = Slot(k.sb(st, "ssb2", [128, 2]))
        rsb2 = Slot(k.sb(st, "rsb2", [128, 2]))
        SGM = [Slot(k.sb(st, f"sgm{i}", [128, D])) for i in range(2)]
        PJ = [Slot(k.sb(st, f"pj{i}", [128, D])) for i in range(2)]
        OT = [Slot(k.sb(st, f"ot{i}", [128, D]), sc.chan(f"c_ot{i}")) for i in range(2)]
        TBs = [Slot(k.ps(st, f"tbf{i}", [128, 1024], BF16)) for i in range(2)]
        MB = [Slot(k.ps(st, f"mbf{i}", [128, 512])) for i in range(6)]
        cntF = {"mbi": 0, "bi": 0}

        def f_stage1(i):
            t0 = i * 512
            xt, ptl, XN, PTT = XT[i % 2], PTL[i % 2], XNF[i % 2], PTTF[i % 2]
            sc.add("sp", _dma(xt.t[:], h2_s[t0:t0 + 512, :].rearrange("(b p) d -> p b d", p=128)), w=[xt.b], chan=xt.c)
            sc.add("sp", _dma(ptl.t[:], p_in[t0:t0 + 512, :].rearrange("(b p) d -> p b d", p=128)), w=[ptl.b], chan=ptl.c)
            rms_transpose(xt, ss, ss2, rstd, junk, XB, XN, TBs)
            sc.add("dve", _cp(PBf.t[:], ptl.t[:]), r=[ptl.b], w=[PBf.b])
            tb = TBs[0]
            for kc in range(2):
                for b in range(4):
                    sc.add("pe", _tr(tb.t[:, kc * 512 + b * 128:kc * 512 + (b + 1) * 128], PBf.t[:, b, kc * 128:(kc + 1) * 128], ident[:]),
                           r=[PBf.b], w=[tb.b])
            sc.add("act", _act(PTT.t[:], tb.t[:].rearrange("p (a b) -> p a b", a=2), AF.Copy), r=[tb.b], w=[PTT.b])

        def f_stage2(i):
            t0 = i * 512
            xt, XN, PTT = XT[i % 2], XNF[i % 2], PTTF[i % 2]
            mbi, bi_ = cntF["mbi"], cntF["bi"]
            for b in range(4):
                sgm, pj, ot = SGM[bi_ % 2], PJ[bi_ % 2], OT[bi_ % 2]
                bi_ += 1
                tok = slice(b * 128, (b + 1) * 128)
                for half in range(2):
                    hs = slice(half * 512, (half + 1) * 512)
                    pm = MB[mbi % 6]
                    mbi += 1
                    for kc in range(8):
                        sc.add("pe", _mm(pm.t[:], XN.t[:, kc, tok], Wg[:, kc, hs], kc == 0, kc == 7), r=[XN.b, Wgb], w=[pm.b])
                    sc.add("act", _act(sgm.t[:, hs], pm.t[:], AF.Sigmoid), r=[pm.b], w=[sgm.b])
                    pm = MB[mbi % 6]
                    mbi += 1
                    for kc in range(2):
                        sc.add("pe", _mm(pm.t[:], PTT.t[:, kc, tok], Wp[:, kc, hs], kc == 0, kc == 1), r=[PTT.b, Wpb], w=[pm.b])
                    sc.add("act", _act(pj.t[:, hs], pm.t[:], AF.Copy), r=[pm.b], w=[pj.b])
                sc.add("act", _act(junk.t[:], pj.t[:], AF.Square, scale=1.0 / 32.0, accum_out=ssb.t[:, 0:1]), r=[pj.b], w=[junk.b, ssb.b])
                sc.add("dve", _ts(ssb2.t[:, 0:1], ssb.t[:, 0:1], EPS, None, ALU.add), r=[ssb.b], w=[ssb2.b])
                sc.add("pool", _tt(rsb2.t[:, 0:1], ssb2.t[:, 0:1], mhalf[:, 0:1], ALU.pow), r=[ssb2.b], w=[rsb2.b])
                sc.add("pool", _tt(pj.t[:], pj.t[:], postg[0][:], ALU.mult), r=[pj.b, postg[1]], w=[pj.b])
                sc.add("dve", _stt(pj.t[:], pj.t[:], rsb2.t[:, 0:1], sgm.t[:], ALU.mult, ALU.mult), r=[pj.b, rsb2.b, sgm.b], w=[pj.b])
                sc.add("dve", _tt(pj.t[:], pj.t[:], xt.t[:, b, :], ALU.add), r=[pj.b, xt.b], w=[pj.b])
                sc.add("act", _act(junk.t[:], pj.t[:], AF.Square, scale=1.0 / 32.0, accum_out=ssb.t[:, 1:2]), r=[pj.b], w=[junk.b, ssb.b])
                sc.add("dve", _ts(ssb2.t[:, 1:2], ssb.t[:, 1:2], EPS, None, ALU.add), r=[ssb.b], w=[ssb2.b])
                sc.add("pool", _tt(rsb2.t[:, 1:2], ssb2.t[:, 1:2], mhalf[:, 0:1], ALU.pow), r=[ssb2.b], w=[rsb2.b])
                sc.add("dve", _stt(ot.t[:], pj.t[:], rsb2.t[:, 1:2], fing[0][:], ALU.mult, ALU.mult), r=[pj.b, rsb2.b, fing[1]], w=[ot.b])
                sc.add("pool", _dma(out[t0 + b * 128:t0 + (b + 1) * 128, :], ot.t[:]), r=[ot.b], chan=ot.c)
            cntF["mbi"], cntF["bi"] = mbi, bi_

        f_stage1(0)
        for i in range(NT):
            if i + 1 < NT:
                f_stage1(i + 1)
            f_stage2(i)
        sc.emit()

    k.final_wait = None
    return k


def finish(k):
    return k.nc


_W_NAMES = ["ln_mix_g", "w_in", "b_gates", "conv_qk_w", "conv_qk_b", "mlstm_norm_g", "q_norm_g", "w_uq", "kv_norm_g",
            "w_ukv", "w_out", "ln_ffn_g", "w_up", "conv_ffn_w", "conv_ffn_b", "w_down", "ple_norm_g", "w_ple_gate",
            "w_ple_proj", "ple_post_g"]


def kernel(**inputs):
    x = np.asarray(inputs["x"])
    p = np.asarray(inputs["p"])
    B, S, _ = x.shape
    nc = finish(build(S))
    shared = {n: np.ascontiguousarray(np.asarray(inputs[n])[0], dtype=np.float32) for n in _W_NAMES}
    shared["final_g"] = np.ascontiguousarray(np.asarray(inputs["final_g"]), dtype=np.float32)
    in_maps = []
    for b in range(B):
        m = dict(shared)
        m["x"] = np.ascontiguousarray(x[b], dtype=np.float32)
        m["p"] = np.ascontiguousarray(p[0, b], dtype=np.float32)
        in_maps.append(m)
    res = run_bass_kernel_spmd(nc, in_maps, core_ids=list(range(B)))
    return np.stack([np.asarray(r["out"]) for r in res.results], axis=0).astype(np.float32)
```

```python
import contextlib
import numpy as np
import concourse.bass as bass
import concourse.mybir as mybir
from concourse.bass_utils import run_bass_kernel_spmd

F32 = mybir.dt.float32
BF16 = mybir.dt.bfloat16
AF = mybir.ActivationFunctionType
ALU = mybir.AluOpType
AX = mybir.AxisListType

D = 1024
NH = 4
IN_COLS = 2512
D_FF = 2816
EPS = 1e-6
SEM_MAX = 30000


class Buf:
    __slots__ = ("name", "lw", "rd")

    def __init__(self, name=""):
        self.name = name
        self.lw = None
        self.rd = []


class Chan:
    __slots__ = ("sem", "count", "last")

    def __init__(self, sem):
        self.sem = sem
        self.count = 0
        self.last = None


class Op:
    __slots__ = ("eng", "fn", "deps", "sig", "signo", "chan", "cval", "done")


class Sched:
    ENGS = ("pe", "act", "dve", "pool", "sp")

    def __init__(self, nc, stack):
        self.nc = nc
        self.stack = stack
        self.ops = []
        self.last_on = {e: None for e in self.ENGS}
        self.pending_bar = {e: [] for e in self.ENGS}
        self.chans = []
        self.free_chans = []
        self.phase_chans = []
        self.cnt = {e: 0 for e in self.ENGS}
        self.sems = {e: [] for e in self.ENGS}
        self.waited = {e: {} for e in self.ENGS}

    def chan(self, name, keep=False):
        if self.free_chans and not keep:
            c = self.free_chans.pop()
        else:
            c = Chan(self.stack.enter_context(self.nc.semaphore(name)))
            self.chans.append(c)
        if not keep:
            self.phase_chans.append(c)
        return c

    def add(self, eng, fn, r=(), w=(), chan=None):
        op = Op()
        op.eng, op.fn, op.deps, op.sig, op.signo, op.chan, op.cval = eng, fn, {}, False, 0, chan, 0
        op.done = False
        for b in r:
            if b.lw is not None:
                op.deps[b.lw] = True
        for b in w:
            if b.lw is not None:
                op.deps.setdefault(b.lw, False)
            for q in b.rd:
                op.deps.setdefault(q, False)
        for b in r:
            b.rd.append(op)
        for b in w:
            b.lw = op
            b.rd = []
        if self.pending_bar[eng]:
            for d in self.pending_bar[eng]:
                op.deps[d] = True
            self.pending_bar[eng] = []
        if chan is not None:
            if chan.last is not None:
                op.deps[chan.last] = True
            chan.count += 16
            op.cval = chan.count
            chan.last = op
        op.deps.pop(op, None)
        self.ops.append(op)
        self.last_on[eng] = op
        return op

    def barrier(self):
        lasts = [o for o in self.last_on.values() if o is not None]
        lasts += [c.last for c in self.chans if c.last is not None]
        for e in self.ENGS:
            self.pending_bar[e] = list(lasts)

    def emit(self):
        nc = self.nc
        fin = self.add("sp", lambda e: e.nop())
        for c in self.chans:
            if c.last is not None and not c.last.done:
                fin.deps[c.last] = True
        for e in self.ENGS:
            self.pending_bar[e] = []
        for op in self.ops:
            for d in [d for d in op.deps if d.done]:
                del op.deps[d]
            for d, raw in op.deps.items():
                if d.chan is not None:
                    continue
                if d.eng == op.eng and (op.eng == "pe" or not raw):
                    continue
                d.sig = True
        cnt = self.cnt
        for op in self.ops:
            if op.chan is None and op.sig:
                cnt[op.eng] += 1
                op.signo = cnt[op.eng]
        sems = self.sems
        for e in self.ENGS:
            n = cnt[e] // SEM_MAX + 1
            while len(sems[e]) < n:
                sems[e].append(self.stack.enter_context(nc.semaphore(f"s_{e}{len(sems[e])}")))
        per = {e: [o for o in self.ops if o.eng == e] for e in self.ENGS}
        handles = {"pe": "tensor", "act": "scalar", "dve": "vector", "pool": "gpsimd", "sp": "sync"}

        def run(e, eng):
            waited = self.waited[e]
            for op in per[e]:
                for d, raw in op.deps.items():
                    if d.chan is not None:
                        key, val, sem = ("c", id(d.chan)), d.cval, d.chan.sem
                    else:
                        if d.eng == e and (e == "pe" or not raw):
                            continue
                        j = (d.signo - 1) // SEM_MAX
                        key, val, sem = (d.eng, j), d.signo - j * SEM_MAX, sems[d.eng][j]
                    if waited.get(key, 0) >= val:
                        continue
                    waited[key] = val
                    eng.wait_ge(sem, val)
                ins = op.fn(eng)
                if op.chan is not None:
                    ins.then_inc(op.chan.sem, 16)
                elif op.sig:
                    j = (op.signo - 1) // SEM_MAX
                    ins.then_inc(sems[e][j], 1)

        with nc.Block() as block:
            for e in self.ENGS:
                if per[e]:
                    getattr(block, handles[e])(lambda eng, e=e: run(e, eng))
        for op in self.ops:
            op.done = True
            op.fn = None
            op.deps = {}
        self.ops = []
        self.last_on = {e: None for e in self.ENGS}
        for c in self.phase_chans:
            c.last = None
        self.free_chans.extend(self.phase_chans)
        self.phase_chans = []


def _act(out, in_, func, **kw):
    return lambda e: e.activation(out, in_, func, **kw)


def _ts(out, in0, s1, s2, op0, op1=None):
    if op1 is None:
        return lambda e: e.tensor_scalar(out, in0, s1, None, op0)
    return lambda e: e.tensor_scalar(out, in0, s1, s2, op0, op1)


def _stt(out, in0, sc, in1, op0, op1):
    return lambda e: e.scalar_tensor_tensor(out, in0, sc, in1, op0, op1)


def _tt(out, in0, in1, op):
    return lambda e: e.tensor_tensor(out, in0, in1, op)


def _cp(out, in_):
    return lambda e: e.tensor_copy(out, in_)


def _mm(out, lhsT, rhs, start, stop):
    return lambda e: e.matmul(out, lhsT, rhs, start=start, stop=stop)


def _tr(out, in_, ident):
    return lambda e: e.transpose(out, in_, ident)


def _dma(out, in_, **kw):
    return lambda e: e.dma_start(out=out, in_=in_, **kw)


def _memset(ap, v):
    return lambda e: e.memset(ap, v)


class K:
    def __init__(self, S, dbg=()):
        self.S = S
        self.dbg = dbg
        self.nc = bass.Bass("TRN2", target_bir_lowering=False)
        self.stack = contextlib.ExitStack()
        self.sc = Sched(self.nc, self.stack)

    def sb(self, st, name, shape, dt=F32):
        return st.enter_context(self.nc.sbuf_tensor(name, list(shape), dt))

    def ps(self, st, name, shape, dt=F32):
        return st.enter_context(self.nc.psum_tensor(name, list(shape), dt))

    def dram_in(self, name, shape, dt=F32):
        return self.nc.dram_tensor(name, list(shape), dt, kind="ExternalInput").ap()

    def dram_out(self, name, shape, dt=F32):
        return self.nc.dram_tensor(name, list(shape), dt, kind="ExternalOutput").ap()

    def dram_tmp(self, name, shape, dt=F32):
        if name in self.dbg:
            return self.nc.dram_tensor(name, list(shape), dt, kind="ExternalOutput").ap()
        return self.nc.dram_tensor(name, list(shape), dt).ap()


class Slot:
    def __init__(self, t, chan=None):
        self.t = t
        self.b = Buf()
        self.c = chan


def build(S, dbg=()):
    k = K(S, dbg)
    nc, sc = k.nc, k.sc
    NT, NB = S // 512, S // 128
    top = k.stack

    x = k.dram_in("x", [S, D])
    p_in = k.dram_in("p", [S, 256])
    ln_mix_g = k.dram_in("ln_mix_g", [D])
    w_in = k.dram_in("w_in", [D, IN_COLS])
    b_gates = k.dram_in("b_gates", [16])
    conv_qk_w = k.dram_in("conv_qk_w", [3, 1024])
    conv_qk_b = k.dram_in("conv_qk_b", [1024])
    mlstm_norm_g = k.dram_in("mlstm_norm_g", [512])
    q_norm_g = k.dram_in("q_norm_g", [256])
    w_uq = k.dram_in("w_uq", [256, 768])
    kv_norm_g = k.dram_in("kv_norm_g", [128])
    w_ukv = k.dram_in("w_ukv", [128, 1024])
    w_out = k.dram_in("w_out", [1024, 1024])
    ln_ffn_g = k.dram_in("ln_ffn_g", [D])
    w_up = k.dram_in("w_up", [D, 2 * D_FF])
    conv_ffn_w = k.dram_in("conv_ffn_w", [3, 2 * D_FF])
    conv_ffn_b = k.dram_in("conv_ffn_b", [2 * D_FF])
    w_down = k.dram_in("w_down", [D_FF, D])
    ple_norm_g = k.dram_in("ple_norm_g", [D])
    w_ple_gate = k.dram_in("w_ple_gate", [D, D])
    w_ple_proj = k.dram_in("w_ple_proj", [256, D])
    ple_post_g = k.dram_in("ple_post_g", [D])
    final_g = k.dram_in("final_g", [D])
    out = k.dram_out("out", [S, D])

    qkT = k.dram_tmp("qkT", [1024, S], BF16)
    vo_s = k.dram_tmp("vo_s", [S, 512])
    so_s = k.dram_tmp("so_s", [S, 512], BF16)
    g3_s = k.dram_tmp("g3_s", [S, 464])

    ident = k.sb(top, "ident", [128, 128], BF16)
    ones_f = k.sb(top, "ones_f", [128, 128], F32)
    mhalf = k.sb(top, "mhalf", [128, 16], F32)
    cb = Buf("consts")
    sc.add("pool", _memset(ones_f[:], 1.0), w=[cb])
    sc.add("pool", _memset(mhalf[:], -0.5), w=[cb])
    sc.add("pool", lambda e: e.affine_select(ident[:], ones_f[:], [[-1, 128]], ALU.is_equal, 0.0,
                                             base=0, channel_multiplier=1), r=[cb], w=[cb])
    sc.emit()

    ld_ch = [sc.chan(f"ldw{i}", keep=True) for i in range(2)]
    cnt = {"w": 0, "e": 0}

    def alt():
        cnt["e"] += 1
        return "dve" if cnt["e"] % 2 else "act"

    def scale_cast(eng, out_ap, in_ap, sc_ap=None):
        if eng == "act":
            if sc_ap is None:
                return _act(out_ap, in_ap, AF.Copy)
            return _act(out_ap, in_ap, AF.Copy, scale=sc_ap)
        if sc_ap is None:
            return _cp(out_ap, in_ap)
        return _ts(out_ap, in_ap, sc_ap, None, ALU.mult)

    def load_cols(st, name, src, G):
        t = k.sb(st, name, [128, G])
        b = Buf(name)
        ch = sc.chan("c_" + name)
        sc.add("sp", _dma(t[:], src.rearrange("(g p) -> p g", p=128), allow_slow_non_contiguous=True),
               w=[b], chan=ch)
        return t, b

    def load_bcast(st, name, src, n):
        t = k.sb(st, name, [128, n])
        b = Buf(name)
        ch = sc.chan("c_" + name)
        sc.add("sp", _dma(t[:], bass.AP(src.tensor, src.offset, [[0, 128], [1, n]])), w=[b], chan=ch)
        return t, b

    def load_w(st, stage, name, src, Kdim, cols, gain=None):
        kcn = Kdim // 128
        t = k.sb(st, name, [128, kcn, cols], BF16)
        b = Buf(name)
        for kc in range(kcn):
            sw = stage[0].t.shape[1]
            for c0 in range(0, cols, sw):
                w = min(sw, cols - c0)
                s = stage[cnt["w"] % 2]
                cnt["w"] += 1
                sc.add("sp", _dma(s.t[:, 0:w], src[kc * 128:(kc + 1) * 128, c0:c0 + w]), w=[s.b], chan=s.c)
                g = None if gain is None else gain[0][:, kc:kc + 1]
                rr = [s.b] + ([] if gain is None else [gain[1]])
                e = alt()
                sc.add(e, scale_cast(e, t[:, kc, c0:c0 + w], s.t[:, 0:w], g), r=rr, w=[b])
        return t, b

    with contextlib.ExitStack() as st:
        stage = [Slot(k.sb(st, f"wstage{i}", [128, 2816]), ld_ch[i]) for i in range(2)]
        gmix = load_cols(st, "gmix", ln_mix_g, 8)
        Win, Winb = load_w(st, stage, "Win", w_in, D, IN_COLS, gmix)
        cw = k.sb(st, "cw", [128, 8, 3])
        cwb = Buf("cw")
        for tap in range(3):
            sc.add("sp", _dma(cw[:, :, tap], conv_qk_w[tap].rearrange("(g p) -> p g", p=128),
                              allow_slow_non_contiguous=True), w=[Buf()], chan=sc.chan(f"c_cw{tap}"))
        cbias = load_cols(st, "cbias", conv_qk_b, 8)
        bg = load_bcast(st, "bg", b_gates, 16)
        sc.barrier()

        XT = [Slot(k.sb(st, f"xt{i}", [128, 4, D]), sc.chan(f"c_xt{i}")) for i in range(3)]
        XBA = [Slot(k.sb(st, f"xb{i}", [128, 4, D], BF16)) for i in range(2)]
        XN = [Slot(k.sb(st, f"xn{i}", [128, 8, 512], BF16)) for i in range(2)]
        junk = Slot(k.sb(st, "junk", [128, D], BF16))
        ss = Slot(k.sb(st, "ss", [128, 4]))
        ss2 = Slot(k.sb(st, "ss2", [128, 4]))
        rstd = Slot(k.sb(st, "rstd", [128, 4]))
        PRE = [Slot(k.sb(st, f"pre{g}", [128, 514])) for g in range(8)]
        ACC = [Slot(k.sb(st, f"acc{i}", [128, 512])) for i in range(2)]
        QKB = [Slot(k.sb(st, f"qkb{i}", [128, 512], BF16), sc.chan(f"c_qkb{i}")) for i in range(3)]
        VO = [Slot(k.sb(st, f"vo{i}", [128, 1024]), sc.chan(f"c_vo{i}")) for i in range(2)]
        G3 = [Slot(k.sb(st, f"g3{i}", [128, 464]), sc.chan(f"c_g3{i}")) for i in range(2)]
        SOB = [Slot(k.sb(st, f"sob{i}", [128, 512], BF16), sc.chan(f"c_sob{i}")) for i in range(2)]
        TB = [Slot(k.ps(st, f"tb{i}", [128, 1024], BF16)) for i in range(2)]
        MB = [Slot(k.ps(st, f"mb{i}", [128, 512])) for i in range(6)]
        last8 = Slot(k.sb(st, "last8", [128, 8]))
        last8b = Slot(k.sb(st, "last8b", [128, 8], BF16), sc.chan("c_last8"))
        for g in range(8):
            sc.add("pool", _memset(PRE[g].t[:, 0:2], 0.0), w=[PRE[g].b])
        cntA = {"mbi": 0, "qi": 0, "voi": 0}

        def stage1a(i):
            t0 = i * 512
            xt, XB = XT[i % 3], XBA[i % 2]
            sc.add("sp", _dma(xt.t[:], x[t0:t0 + 512, :].rearrange("(b p) d -> p b d", p=128)), w=[xt.b], chan=xt.c)
            for b in range(4):
                sc.add("act", _act(junk.t[:], xt.t[:, b, :], AF.Square, scale=1.0 / 32.0, accum_out=ss.t[:, b:b + 1]),
                       r=[xt.b], w=[junk.b, ss.b])
            sc.add("dve", _ts(ss2.t[:], ss.t[:], EPS, None, ALU.add), r=[ss.b], w=[ss2.b])
            sc.add("pool", _tt(rstd.t[:], ss2.t[:], mhalf[:, 0:4], ALU.pow), r=[ss2.b], w=[rstd.b])
            for b in range(4):
                e = "dve" if b % 2 else "act"
                sc.add(e, scale_cast(e, XB.t[:, b, :], xt.t[:, b, :], rstd.t[:, b:b + 1]), r=[xt.b, rstd.b], w=[XB.b])

        def stage1b(i):
            xn, XB = XN[i % 2], XBA[i % 2]
            for j in range(4):
                tb = TB[j % 2]
                for kk in range(2):
                    kc = 2 * j + kk
                    for b in range(4):
                        sc.add("pe", _tr(tb.t[:, kk * 512 + b * 128:kk * 512 + (b + 1) * 128],
                                         XB.t[:, b, kc * 128:(kc + 1) * 128], ident[:]), r=[XB.b], w=[tb.b])
                e = "dve" if j % 2 else "act"
                sc.add(e, scale_cast(e, xn.t[:, 2 * j:2 * j + 2, :], tb.t[:].rearrange("p (a b) -> p a b", a=2)),
                       r=[tb.b], w=[xn.b])

        def stage2fm(i):
            t0 = i * 512
            xn = XN[i % 2]
            mbi, qi, voi = cntA["mbi"], cntA["qi"], cntA["voi"]
            for g in range(8):
                pm = MB[mbi % 6]
                mbi += 1
                for kc in range(8):
                    sc.add("pe", _mm(pm.t[:], Win[:, kc, g * 128:(g + 1) * 128], xn.t[:, kc, :], kc == 0, kc == 7),
                           r=[xn.b, Winb], w=[pm.b])
                pre = PRE[g]
                acc = ACC[g % 2]
                qb = QKB[qi % 3]
                qi += 1
                sc.add("act", _act(pre.t[:, 2:514], pm.t[:], AF.Copy), r=[pm.b], w=[pre.b])
                sc.add("dve", _ts(acc.t[:], pre.t[:, 2:514], cw[:, g, 2:3], None, ALU.mult), r=[pre.b], w=[acc.b])
                sc.add("dve", _stt(acc.t[:], pre.t[:, 1:513], cw[:, g, 1:2], acc.t[:], ALU.mult, ALU.add),
                       r=[pre.b, acc.b], w=[acc.b])
                sc.add("dve", _stt(acc.t[:], pre.t[:, 0:512], cw[:, g, 0:1], acc.t[:], ALU.mult, ALU.add),
                       r=[pre.b, acc.b], w=[acc.b])
                sc.add("act", _act(qb.t[:], acc.t[:], AF.Silu, bias=cbias[0][:, g:g + 1]), r=[acc.b], w=[qb.b])
                if i == 0:
                    sc.add("pool", _dma(qkT[g * 128:(g + 1) * 128, 0:511], qb.t[:, 1:512]), r=[qb.b], chan=qb.c)
                else:
                    sc.add("pool", _dma(qkT[g * 128:(g + 1) * 128, t0 - 1:t0 + 511], qb.t[:]), r=[qb.b], chan=qb.c)
                sc.add("dve", _cp(pre.t[:, 0:2], pre.t[:, 512:514]), r=[pre.b], w=[pre.b])
            cntA["mbi"], cntA["qi"], cntA["voi"] = mbi, qi, voi

        def stage2tm(i):
            t0 = i * 512
            xn = XN[i % 2]
            mbi, qi, voi = cntA["mbi"], cntA["qi"], cntA["voi"]
            for b in range(4):
                vo = VO[voi % 2]
                g3 = G3[voi % 2]
                sob = SOB[voi % 2]
                voi += 1
                for part, (c0, c1) in enumerate(((1024, 1536), (1536, 2048), (2048, 2512))):
                    pm = MB[mbi % 6]
                    mbi += 1
                    for kc in range(8):
                        sc.add("pe", _mm(pm.t[:, 0:c1 - c0], xn.t[:, kc, b * 128:(b + 1) * 128], Win[:, kc, c0:c1],
                                         kc == 0, kc == 7), r=[xn.b, Winb], w=[pm.b])
                    if part == 0:
                        sc.add("dve", _cp(vo.t[:, 0:512], pm.t[:]), r=[pm.b], w=[vo.b])
                    elif part == 1:
                        sc.add("act", _act(vo.t[:, 512:1024], pm.t[:], AF.Tanh, scale=0.5), r=[pm.b], w=[vo.b])
                        sc.add("dve", _ts(sob.t[:], vo.t[:, 512:1024], 0.5, 0.5, ALU.mult, ALU.add),
                               r=[vo.b], w=[sob.b])
                    else:
                        sc.add("act", _act(g3.t[:], pm.t[:, 0:464], AF.Copy), r=[pm.b], w=[g3.b])
                        sc.add("dve", _tt(g3.t[:, 0:16], g3.t[:, 0:16], bg[0][:], ALU.add), r=[g3.b], w=[g3.b])
                r0 = t0 + b * 128
                sc.add("pool", _dma(vo_s[r0:r0 + 128, :], vo.t[:, 0:512]), r=[vo.b], chan=vo.c)
                sc.add("pool", _dma(so_s[r0:r0 + 128, :], sob.t[:]), r=[sob.b], chan=sob.c)
                sc.add("pool", _dma(g3_s[r0:r0 + 128, :], g3.t[:]), r=[g3.b], chan=g3.c)
            cntA["mbi"], cntA["qi"], cntA["voi"] = mbi, qi, voi

        stage1a(0)
        stage1b(0)
        if NT > 1:
            stage1a(1)
        for i in range(NT):
            stage2fm(i)
            if i + 1 < NT:
                stage1b(i + 1)
            if i + 2 < NT:
                stage1a(i + 2)
            stage2tm(i)
        prb = [PRE[g].b for g in range(8)]
        for g in range(8):
            sc.add("dve", _ts(last8.t[:, g:g + 1], PRE[g].t[:, 0:1], cw[:, g, 0:1], None, ALU.mult), r=[PRE[g].b], w=[last8.b])
            sc.add("dve", _stt(last8.t[:, g:g + 1], PRE[g].t[:, 1:2], cw[:, g, 1:2], last8.t[:, g:g + 1], ALU.mult, ALU.add),
                   r=[PRE[g].b, last8.b], w=[last8.b])
        sc.add("dve", _tt(last8.t[:], last8.t[:], cbias[0][:], ALU.add), r=[last8.b], w=[last8.b])
        sc.add("act", _act(last8b.t[:], last8.t[:], AF.Silu), r=[last8.b], w=[last8b.b])
        sc.add("pool", _dma(qkT.rearrange("(g p) s -> p g s", p=128)[:, :, S - 1], last8b.t[:],
                            allow_slow_non_contiguous=True), r=[last8b.b], chan=last8b.c)
        sc.emit()

    hf_s = k.dram_tmp("hf_s", [S, 512], BF16)
    hb_s = k.dram_tmp("hb_s", [S, 512], BF16)
    yT_s = k.dram_tmp("yT_s", [1024, S], BF16)

    def bc(ap, m):
        a = [list(d) for d in ap.ap]
        return bass.AP(ap.tensor, ap.offset, a + [[0, m]])

    def bc_mid(ap, m):
        a = [list(d) for d in ap.ap]
        return bass.AP(ap.tensor, ap.offset, [a[0], [0, m]] + a[1:])

    with contextlib.ExitStack() as st:
        maskF = k.sb(st, "maskF", [128, 128])
        maskB = k.sb(st, "maskB", [128, 128])
        mb_ = Buf("masks")
        sc.add("pool", lambda e: e.affine_select(maskF[:], ones_f[:], [[1, 128]], ALU.is_ge, 0.0,
                                                 base=0, channel_multiplier=-1), w=[mb_])
        sc.add("pool", lambda e: e.affine_select(maskB[:], ones_f[:], [[-1, 128]], ALU.is_ge, 0.0,
                                                 base=0, channel_multiplier=1), w=[mb_])
        hdst = (hf_s, hb_s)
        tris = (maskF, maskB)

        def ring(name, shape, dt=F32, chan=False, n=2):
            return [[Slot(k.sb(st, f"{name}{d}_{i}", shape, dt), sc.chan(f"c_{name}{d}_{i}") if chan else None) for i in range(n)]
                    for d in range(2)]

        SL = ring("sl", [128, 8, 512], BF16, True)
        VOT = ring("vot", [128, 512], F32, True)
        GT = ring("gt", [128, 16], F32, True)
        HO = ring("ho", [128, 512], BF16, True)
        AA = ring("aa", [128, 4])
        EG = ring("eg", [128, 8])
        V1 = ring("v1", [128, 4, 130], BF16)
        KTOK = ring("ktok", [128, 4, 128], BF16)
        MM = ring("mm", [128, 4, 128], BF16)
        e1 = [Slot(k.sb(st, f"e1_{d}", [128, 4])) for d in range(2)]
        lsp = [Slot(k.sb(st, f"lsp_{d}", [128, 4])) for d in range(2)]
        tmpa = [Slot(k.sb(st, f"tmpa_{d}", [128, 4])) for d in range(2)]
        C1 = [Slot(k.sb(st, f"c1_{d}", [128, 4, 130])) for d in range(2)]
        C1b = [Slot(k.sb(st, f"c1b_{d}", [128, 4, 130], BF16)) for d in range(2)]
        tmpC = [Slot(k.sb(st, f"tmpc_{d}", [128, 4, 130])) for d in range(2)]
        den = [Slot(k.sb(st, f"den_{d}", [128, 4])) for d in range(2)]
        rr_ = [Slot(k.sb(st, f"rr_{d}", [128, 4])) for d in range(2)]
        TBK = Slot(k.ps(st, "tbk", [128, 1024], BF16))
        SPS = [Slot(k.ps(st, f"sps{i}", [128, 512])) for i in range(2)]
        UPS = Slot(k.ps(st, "ups", [128, 1024]))
        DPS = Slot(k.ps(st, "dps", [128, 1024]))
        GPS = Slot(k.ps(st, "gps", [128, 512]))
        qkT3 = qkT.rearrange("(g p) s -> p g s", p=128)
        slab = [{"cg": None, "n": 0} for _ in range(2)]

        def pre(c, d, n):
            cg = c // 4
            if slab[d]["cg"] != cg:
                slab[d]["cg"] = cg
                slab[d]["n"] += 1
                sl = SL[d][slab[d]["n"] % 2]
                sc.add("sp", _dma(sl.t[:], qkT3[:, :, cg * 512:(cg + 1) * 512]), w=[sl.b], chan=sl.c)
            sl = SL[d][slab[d]["n"] % 2]
            vot, gt, aa, eg, v1, ktok, mm = (VOT[d][n % 2], GT[d][n % 2], AA[d][n % 2], EG[d][n % 2], V1[d][n % 2],
                                             KTOK[d][n % 2], MM[d][n % 2])
            sps = SPS[d]
            r0 = c * 128
            sc.add("sp", _dma(vot.t[:], vo_s[r0:r0 + 128, :]), w=[vot.b], chan=vot.c)
            sc.add("sp", _dma(gt.t[:], g3_s[r0:r0 + 128, 0:16]), w=[gt.b], chan=gt.c)
            io, fo = d * 8, d * 8 + 4
            tri = tris[d]
            sc.add("act", _act(e1[d].t[:], gt.t[:, fo:fo + 4], AF.Exp, scale=-1.0), r=[gt.b], w=[e1[d].b])
            sc.add("act", _act(lsp[d].t[:], e1[d].t[:], AF.Ln, bias=1.0), r=[e1[d].b], w=[lsp[d].b])
            gcol = d * 8
            sc.add("pe", _mm(GPS.t[:, gcol:gcol + 4], tri[:], lsp[d].t[:], True, True), r=[lsp[d].b, mb_], w=[GPS.b])
            sc.add("pe", _mm(GPS.t[:, gcol + 4:gcol + 8], ones_f[:], lsp[d].t[:], True, True), r=[lsp[d].b], w=[GPS.b])
            sc.add("dve", _tt(tmpa[d].t[:], gt.t[:, io:io + 4], GPS.t[:, gcol:gcol + 4], ALU.add), r=[gt.b, GPS.b], w=[tmpa[d].b])
            sc.add("act", _act(aa.t[:], tmpa[d].t[:], AF.Exp), r=[tmpa[d].b], w=[aa.b])
            sc.add("act", _act(eg.t[:], GPS.t[:, gcol:gcol + 8], AF.Exp, scale=-1.0), r=[GPS.b], w=[eg.b])
            sc.add("dve", _tt(v1.t[:, :, 0:128], vot.t[:].rearrange("p (h d) -> p h d", h=4), bc(aa.t[:], 128), ALU.mult),
                   r=[vot.b, aa.b], w=[v1.b])
            sc.add("dve", _cp(v1.t[:, :, 128], aa.t[:]), r=[aa.b], w=[v1.b])
            c4 = (c % 4) * 128
            tcol = d * 512
            for h in range(4):
                sc.add("pe", _tr(TBK.t[:, tcol + h * 128:tcol + (h + 1) * 128], sl.t[:, 4 + h, c4:c4 + 128], ident[:]), r=[sl.b], w=[TBK.b])
            sc.add("act", _act(ktok.t[:], TBK.t[:, tcol:tcol + 512].rearrange("p (h d) -> p h d", h=4), AF.Copy, scale=128.0 ** -0.5),
                   r=[TBK.b], w=[ktok.b])
            for h in range(4):
                sc.add("pe", _mm(sps.t[:, h * 128:(h + 1) * 128], sl.t[:, 4 + h, c4:c4 + 128], sl.t[:, h, c4:c4 + 128], True, True),
                       r=[sl.b], w=[sps.b])
            sc.add("dve", _stt(mm.t[:], sps.t[:].rearrange("p (h d) -> p h d", h=4), 128.0 ** -0.5, bc_mid(tri[:], 4), ALU.mult, ALU.mult),
                   r=[sps.b, mb_], w=[mm.b])
            return sl, c4

        def main(c, d, n, sl, c4):
            eg, v1, ktok, mm, ho = EG[d][n % 2], V1[d][n % 2], KTOK[d][n % 2], MM[d][n % 2], HO[d][n % 2]
            c1, c1b, tc_, dn, rr = C1[d], C1b[d], tmpC[d], den[d], rr_[d]
            r0 = c * 128
            for h in range(4):
                sc.add("pe", _mm(UPS.t[:, h * 256:h * 256 + 129], mm.t[:, h, :], v1.t[:, h, 0:129], True, False), r=[mm.b, v1.b], w=[UPS.b])
                sc.add("pe", _mm(UPS.t[:, h * 256:h * 256 + 129], sl.t[:, h, c4:c4 + 128], c1b.t[:, h, 0:129], False, True),
                       r=[sl.b, c1b.b], w=[UPS.b])
            for h in range(4):
                sc.add("pe", _mm(DPS.t[:, h * 256:h * 256 + 129], ktok.t[:, h, :], v1.t[:, h, 0:129], True, True), r=[ktok.b, v1.b], w=[DPS.b])
            U3 = UPS.t[:].rearrange("p (h d) -> p h d", h=4)
            D3 = DPS.t[:].rearrange("p (h d) -> p h d", h=4)
            sc.add("dve", _tt(tc_.t[:, :, 0:129], D3[:, :, 0:129], c1.t[:, :, 0:129], ALU.add), r=[DPS.b, c1.b], w=[tc_.b])
            sc.add("pool", _tt(c1.t[:, :, 0:129], tc_.t[:, :, 0:129], bc(eg.t[:, 4:8], 129), ALU.mult), r=[tc_.b, eg.b], w=[c1.b])
            sc.add("act", _act(c1b.t[:, :, 0:129], c1.t[:, :, 0:129], AF.Copy), r=[c1.b], w=[c1b.b])
            sc.add("dve", _tt(dn.t[:], U3[:, :, 128], eg.t[:, 0:4], ALU.mult), r=[UPS.b, eg.b], w=[dn.b])
            sc.add("act", _act(dn.t[:], dn.t[:], AF.Abs), r=[dn.b], w=[dn.b])
            sc.add("dve", _ts(dn.t[:], dn.t[:], 1.0, None, ALU.max), r=[dn.b], w=[dn.b])
            sc.add("dve", lambda e: e.reciprocal(rr.t[:], dn.t[:]), r=[dn.b], w=[rr.b])
            sc.add("dve", _tt(rr.t[:], rr.t[:], eg.t[:, 0:4], ALU.mult), r=[rr.b, eg.b], w=[rr.b])
            sc.add("dve", _tt(ho.t[:].rearrange("p (h d) -> p h d", h=4), U3[:, :, 0:128], bc(rr.t[:], 128), ALU.mult),
                   r=[UPS.b, rr.b], w=[ho.b])
            sc.add("pool", _dma(hdst[d][r0:r0 + 128, :], ho.t[:]), r=[ho.b], chan=ho.c)

        orders = (list(range(NB)), list(range(NB - 1, -1, -1)))
        nxt = [None, None]
        for d in range(2):
            sc.add("pool", _memset(C1[d].t[:], 0.0), w=[C1[d].b])
            sc.add("pool", _memset(C1b[d].t[:], 0.0), w=[C1b[d].b])
            nxt[d] = pre(orders[d][0], d, 0)
        for j in range(NB):
            cur = list(nxt)
            if j + 1 < NB:
                for d in range(2):
                    nxt[d] = pre(orders[d][j + 1], d, j + 1)
            for d in range(2):
                main(orders[d][j], d, j, *cur[d])
        sc.emit()

    KT_s = k.dram_tmp("KT_s", [512, S], BF16)
    KR_s = k.dram_tmp("KR_s", [64, S], BF16)
    V_s = k.dram_tmp("V_s", [S, 512], BF16)
    QN_s = k.dram_tmp("QN_s", [512, S], BF16)
    QR_s = k.dram_tmp("QR_s", [4, 65, S], BF16)
    kmax_s = k.dram_tmp("kmax_s", [128, 4])
    TWO_PI = 6.283185307179586

    with contextlib.ExitStack() as st:
        stage = [Slot(k.sb(st, f"wstagec{i}", [128, 1024]), ld_ch[i]) for i in range(2)]
        gq = load_cols(st, "gq", q_norm_g, 2)
        gkv = load_cols(st, "gkv", kv_norm_g, 1)
        Wuq, Wuqb = load_w(st, stage, "Wuq", w_uq, 256, 768, gq)
        Wkv, Wkvb = load_w(st, stage, "Wkv", w_ukv, 128, 1024, gkv)
        Wkv4 = Wkv[:, 0, :].rearrange("p (h t d) -> p h t d", h=4, t=2)
        cos2 = k.sb(st, "cos2", [128, NB, 64])
        sin1 = k.sb(st, "sin1", [128, NB, 32])
        tb_ = Buf("ropetab")
        pos = k.sb(st, "pos", [128, NB])
        invf = k.sb(st, "invf", [128, 32])
        ang = k.sb(st, "ang", [128, NB, 32])
        angi = k.sb(st, "angi", [128, NB, 32], mybir.dt.int32)
        angf = k.sb(st, "angf", [128, NB, 32])
        msk = k.sb(st, "msk", [128, NB, 32])
        sc.add("pool", lambda e: e.iota(pos[:], [[128, NB]], base=0, channel_multiplier=1, allow_small_or_imprecise_dtypes=True), w=[tb_])
        sc.add("pool", lambda e: e.iota(invf[:], [[1, 32]], base=0, channel_multiplier=0, allow_small_or_imprecise_dtypes=True), r=[tb_], w=[tb_])
        sc.add("act", _act(invf[:], invf[:], AF.Exp, scale=-float(np.log(10000.0)) / 32.0), r=[tb_], w=[tb_])
        sc.add("dve", _tt(ang[:], bc(pos[:], 32), bc_mid(invf[:], NB), ALU.mult), r=[tb_], w=[tb_])
        sc.add("dve", _ts(ang[:], ang[:], 1.0 / TWO_PI, None, ALU.mult), r=[tb_], w=[tb_])
        for which in range(2):
            if which == 1:
                sc.add("dve", _ts(ang[:], ang[:], 0.25, None, ALU.add), r=[tb_], w=[tb_])
            sc.add("dve", _cp(angi[:], ang[:]), r=[tb_], w=[tb_])
            sc.add("dve", _cp(angf[:], angi[:]), r=[tb_], w=[tb_])
            sc.add("dve", _tt(angf[:], ang[:], angf[:], ALU.subtract), r=[tb_], w=[tb_])
            sc.add("dve", _ts(msk[:], angf[:], 0.5, None, ALU.is_gt), r=[tb_], w=[tb_])
            sc.add("dve", _tt(angf[:], angf[:], msk[:], ALU.subtract), r=[tb_], w=[tb_])
            sc.add("dve", _ts(msk[:], angf[:], -0.5, None, ALU.is_lt), r=[tb_], w=[tb_])
            sc.add("dve", _tt(angf[:], angf[:], msk[:], ALU.add), r=[tb_], w=[tb_])
            if which == 0:
                sc.add("act", _act(sin1[:], angf[:], AF.Sin, scale=TWO_PI * (1.0 - 1e-6)), r=[tb_], w=[tb_])
            else:
                sc.add("act", _act(cos2[:, :, 0:32], angf[:], AF.Sin, scale=TWO_PI * (1.0 - 1e-6)), r=[tb_], w=[tb_])
                sc.add("act", _act(cos2[:, :, 32:64], angf[:], AF.Sin, scale=TWO_PI * (1.0 - 1e-6)), r=[tb_], w=[tb_])
        sc.barrier()

        G3T = [Slot(k.sb(st, f"g3t{i}", [128, 4, 464]), sc.chan(f"c_g3t{i}")) for i in range(2)]
        junkc = Slot(k.sb(st, "junkc", [128, 3072], BF16))
        ssq = Slot(k.sb(st, "ssq", [128, 8]))
        ssq2 = Slot(k.sb(st, "ssq2", [128, 8]))
        rst = Slot(k.sb(st, "rst", [128, 8]))
        cqn = Slot(k.sb(st, "cqn", [128, 4, 256], BF16))
        ckvn = Slot(k.sb(st, "ckvn", [128, 4, 128], BF16))
        tA = Slot(k.sb(st, "tA", [128, 4, 64]))
        tB = Slot(k.sb(st, "tB", [128, 4, 64]))
        krb = Slot(k.sb(st, "krb", [128, 4, 64], BF16))
        sqr = Slot(k.sb(st, "sqr", [128, 4, 64]))
        kr2 = Slot(k.sb(st, "kr2", [128, 4]))
        cqT = Slot(k.sb(st, "cqT", [128, 2, 512], BF16))
        ckvT = Slot(k.sb(st, "ckvT", [128, 512], BF16))
        krT = Slot(k.sb(st, "krT", [64, 512], BF16), sc.chan("c_krT"))
        KTS = [Slot(k.sb(st, f"kts{i}", [128, 512], BF16), sc.chan(f"c_kts{i}")) for i in range(2)]
        VS = [Slot(k.sb(st, f"vs{i}", [128, 512], BF16), sc.chan(f"c_vs{i}")) for i in range(2)]
        sqk = Slot(k.sb(st, "sqk", [128, 512]))
        kn2 = Slot(k.sb(st, "kn2", [128, 4, 4]))
        kmax = Slot(k.sb(st, "kmax", [128, 4]))
        kmt = Slot(k.sb(st, "kmt", [128, 4]))
        q_sb = Slot(k.sb(st, "q_sb", [128, 4, 768]))
        qtA = Slot(k.sb(st, "qtA", [128, 4, 4, 64]))
        qtB = Slot(k.sb(st, "qtB", [128, 4, 4, 64]))
        qbn = Slot(k.sb(st, "qbn", [128, 4, 4, 128], BF16))
        qbr = Slot(k.sb(st, "qbr", [128, 4, 4, 66], BF16))
        qn2 = Slot(k.sb(st, "qn2", [128, 16]))
        qn1 = Slot(k.sb(st, "qn1", [128, 16]))
        QS = [Slot(k.sb(st, f"qs{i}", [128, 2, 512], BF16), sc.chan(f"c_qs{i}")) for i in range(2)]
        QRS = [Slot(k.sb(st, f"qrs{i}", [65, 2, 512], BF16), sc.chan(f"c_qrs{i}")) for i in range(2)]
        PB = [Slot(k.ps(st, f"pb{i}", [128, 512])) for i in range(8)]
        pbi = {"i": 0}

        def bank():
            pbi["i"] += 1
            return PB[pbi["i"] % 8]

        def bfv(slot):
            return slot.t[:].bitcast(BF16)

        sc.add("pool", _memset(kmax.t[:], 0.0), w=[kmax.b])
        sc.add("pool", _memset(qbr.t[:], 0.0), w=[qbr.b])
        q4 = q_sb.t[:].rearrange("p b (h d) -> p b h d", h=4)
        kti = 0
        for i in range(NT):
            t0 = i * 512
            g3 = G3T[i % 2]
            sc.add("sp", _dma(g3.t[:], g3_s[t0:t0 + 512, :].rearrange("(b p) c -> p b c", p=128)), w=[g3.b], chan=g3.c)
            for b in range(4):
                sc.add("act", _act(junkc.t[:, 0:256], g3.t[:, b, 16:272], AF.Square, scale=1.0 / 16.0, accum_out=ssq.t[:, b:b + 1]),
                       r=[g3.b], w=[junkc.b, ssq.b])
                sc.add("act", _act(junkc.t[:, 0:128], g3.t[:, b, 272:400], AF.Square, scale=128.0 ** -0.5, accum_out=ssq.t[:, 4 + b:5 + b]),
                       r=[g3.b], w=[junkc.b, ssq.b])
            sc.add("dve", _ts(ssq2.t[:], ssq.t[:], EPS, None, ALU.add), r=[ssq.b], w=[ssq2.b])
            sc.add("pool", _tt(rst.t[:], ssq2.t[:], mhalf[:, 0:8], ALU.pow), r=[ssq2.b], w=[rst.b])
            sc.add("dve", _tt(cqn.t[:], g3.t[:, :, 16:272], bc(rst.t[:, 0:4], 256), ALU.mult), r=[g3.b, rst.b], w=[cqn.b])
            sc.add("dve", _tt(ckvn.t[:], g3.t[:, :, 272:400], bc(rst.t[:, 4:8], 128), ALU.mult), r=[g3.b, rst.b], w=[ckvn.b])
            xk = g3.t[:, :, 400:464]
            cs, sn = cos2[:, 4 * i:4 * i + 4, :], sin1[:, 4 * i:4 * i + 4, :]
            sc.add("dve", _tt(tA.t[:], xk, cs, ALU.mult), r=[g3.b], w=[tA.b])
            sc.add("dve", _tt(tB.t[:, :, 0:32], g3.t[:, :, 432:464], sn, ALU.mult), r=[g3.b], w=[tB.b])
            sc.add("dve", _tt(tB.t[:, :, 32:64], g3.t[:, :, 400:432], sn, ALU.mult), r=[g3.b], w=[tB.b])
            sc.add("dve", _tt(krb.t[:, :, 0:32], tA.t[:, :, 0:32], tB.t[:, :, 0:32], ALU.subtract), r=[tA.b, tB.b], w=[krb.b])
            sc.add("dve", _tt(krb.t[:, :, 32:64], tA.t[:, :, 32:64], tB.t[:, :, 32:64], ALU.add), r=[tA.b, tB.b], w=[krb.b])
            sc.add("act", _act(sqr.t[:], xk, AF.Square), r=[g3.b], w=[sqr.b])
            sc.add("dve", lambda e: e.tensor_reduce(kr2.t[:], sqr.t[:], AX.X, ALU.add), r=[sqr.b], w=[kr2.b])
            pa, pb2 = bank(), bank()
            for b in range(4):
                for kc in range(2):
                    sc.add("pe", _tr(bfv(pa)[:, kc * 512 + b * 128:kc * 512 + (b + 1) * 128], cqn.t[:, b, kc * 128:(kc + 1) * 128], ident[:]),
                           r=[cqn.b], w=[pa.b])
                sc.add("pe", _tr(bfv(pb2)[:, b * 128:(b + 1) * 128], ckvn.t[:, b, :], ident[:]), r=[ckvn.b], w=[pb2.b])
                sc.add("pe", _tr(bfv(pb2)[0:64, 512 + b * 128:512 + (b + 1) * 128], krb.t[:, b, :], ident[:]), r=[krb.b], w=[pb2.b])
            sc.add("act", _act(cqT.t[:], bfv(pa).rearrange("p (a b) -> p a b", a=2), AF.Copy), r=[pa.b], w=[cqT.b])
            sc.add("dve", _cp(ckvT.t[:], bfv(pb2)[:, 0:512]), r=[pb2.b], w=[ckvT.b])
            sc.add("act", _act(krT.t[:], bfv(pb2)[0:64, 512:1024], AF.Copy), r=[pb2.b], w=[krT.b])
            sc.add("pool", _dma(KR_s[:, t0:t0 + 512], krT.t[:]), r=[krT.b], chan=krT.c)
            for h in range(4):
                pk = bank()
                sc.add("pe", _mm(pk.t[:], Wkv[:, 0, h * 256:h * 256 + 128], ckvT.t[:], True, True), r=[ckvT.b, Wkvb], w=[pk.b])
                kts = KTS[kti % 2]
                kti += 1
                e = "act" if h % 2 else "dve"
                sc.add(e, scale_cast(e, kts.t[:], pk.t[:]), r=[pk.b], w=[kts.b])
                sc.add("pool", _dma(KT_s[h * 128:(h + 1) * 128, t0:t0 + 512], kts.t[:]), r=[kts.b], chan=kts.c)
            for b in range(4):
                tok = slice(b * 128, (b + 1) * 128)
                pv = bank()
                sc.add("pe", _mm(pv.t[:].rearrange("p (h d) -> p h d", h=4), ckvT.t[:, tok], Wkv4[:, :, 1, :], True, True),
                       r=[ckvT.b, Wkvb], w=[pv.b])
                vs = VS[b % 2]
                sc.add("act", _act(vs.t[:], pv.t[:], AF.Copy), r=[pv.b], w=[vs.b])
                sc.add("pool", _dma(V_s[t0 + b * 128:t0 + (b + 1) * 128, :], vs.t[:]), r=[vs.b], chan=vs.c)
                pk = bank()
                sc.add("pe", _mm(pk.t[:].rearrange("p (h d) -> p h d", h=4), ckvT.t[:, tok], Wkv4[:, :, 0, :], True, True),
                       r=[ckvT.b, Wkvb], w=[pk.b])
                sc.add("act", _act(sqk.t[:], pk.t[:], AF.Square), r=[pk.b], w=[sqk.b])
                sc.add("dve", lambda e, b=b: e.tensor_reduce(kn2.t[:, b, :], sqk.t[:].rearrange("p (h d) -> p h d", h=4), AX.X, ALU.add),
                       r=[sqk.b], w=[kn2.b])
                pq0, pq1 = bank(), bank()
                for kc in range(2):
                    sc.add("pe", _mm(pq0.t[:], cqT.t[:, kc, tok], Wuq[:, kc, 0:512], kc == 0, kc == 1), r=[cqT.b, Wuqb], w=[pq0.b])
                for kc in range(2):
                    sc.add("pe", _mm(pq1.t[:, 0:256], cqT.t[:, kc, tok], Wuq[:, kc, 512:768], kc == 0, kc == 1), r=[cqT.b, Wuqb], w=[pq1.b])
                sc.add("act", _act(q_sb.t[:, b, 0:512], pq0.t[:], AF.Copy), r=[pq0.b], w=[q_sb.b])
                sc.add("dve", _cp(q_sb.t[:, b, 512:768], pq1.t[:, 0:256]), r=[pq1.b], w=[q_sb.b])
            sc.add("dve", _tt(kn2.t[:], kn2.t[:], bc(kr2.t[:], 4), ALU.add), r=[kn2.b, kr2.b], w=[kn2.b])
            sc.add("dve", lambda e: e.tensor_reduce(kmt.t[:], kn2.t[:].rearrange("p b h -> p h b"), AX.X, ALU.max), r=[kn2.b], w=[kmt.b])
            sc.add("dve", _tt(kmax.t[:], kmax.t[:], kmt.t[:], ALU.max), r=[kmax.b, kmt.b], w=[kmax.b])
            cs4 = bass.AP(cs.tensor, cs.offset, [list(cs.ap[0]), list(cs.ap[1]), [0, 4], list(cs.ap[2])])
            sn4 = bass.AP(sn.tensor, sn.offset, [list(sn.ap[0]), list(sn.ap[1]), [0, 4], list(sn.ap[2])])
            sc.add("dve", _tt(qtA.t[:], q4[:, :, :, 128:192], cs4, ALU.mult), r=[q_sb.b], w=[qtA.b])
            sc.add("dve", _tt(qtB.t[:, :, :, 0:32], q4[:, :, :, 160:192], sn4, ALU.mult), r=[q_sb.b], w=[qtB.b])
            sc.add("dve", _tt(qtB.t[:, :, :, 32:64], q4[:, :, :, 128:160], sn4, ALU.mult), r=[q_sb.b], w=[qtB.b])
            sc.add("dve", _tt(qbr.t[:, :, :, 0:32], qtA.t[:, :, :, 0:32], qtB.t[:, :, :, 0:32], ALU.subtract), r=[qtA.b, qtB.b], w=[qbr.b])
            sc.add("dve", _tt(qbr.t[:, :, :, 32:64], qtA.t[:, :, :, 32:64], qtB.t[:, :, :, 32:64], ALU.add), r=[qtA.b, qtB.b], w=[qbr.b])
            sc.add("dve", _cp(qbn.t[:], q4[:, :, :, 0:128]), r=[q_sb.b], w=[qbn.b])
            sc.add("act", _act(junkc.t[:], q_sb.t[:].rearrange("p b c -> p (b c)"), AF.Square), r=[q_sb.b], w=[junkc.b])
            sc.add("dve", lambda e: e.tensor_reduce(qn2.t[:], junkc.t[:].rearrange("p (g d) -> p g d", g=16), AX.X, ALU.add),
                   r=[junkc.b], w=[qn2.b])
            sc.add("pool", _tt(qn1.t[:], qn2.t[:], mhalf[:, 0:16], ALU.pow), r=[qn2.b], w=[qn1.b])
            sc.add("dve", _tt(qn1.t[:], qn1.t[:], qn2.t[:], ALU.mult), r=[qn1.b, qn2.b], w=[qn1.b])
            sc.add("dve", _ts(qbr.t[:, :, :, 64], qn1.t[:].rearrange("p (b h) -> p b h", b=4), -1.01, None, ALU.mult),
                   r=[qn1.b], w=[qbr.b])
            for hp in range(2):
                pn, pr = bank(), bank()
                for hh in range(2):
                    h = 2 * hp + hh
                    for b in range(4):
                        sc.add("pe", _tr(bfv(pn)[:, hh * 512 + b * 128:hh * 512 + (b + 1) * 128], qbn.t[:, b, h, :], ident[:]), r=[qbn.b], w=[pn.b])
                        sc.add("pe", _tr(bfv(pr)[0:65, hh * 512 + b * 128:hh * 512 + (b + 1) * 128], qbr.t[:, b, h, 0:65], ident[:]), r=[qbr.b], w=[pr.b])
                qs, qrs = QS[hp], QRS[hp]
                sc.add("act", _act(qs.t[:], bfv(pn).rearrange("p (a b) -> p a b", a=2), AF.Copy), r=[pn.b], w=[qs.b])
                sc.add("dve", _cp(qrs.t[:], bfv(pr)[0:65, :].rearrange("p (a b) -> p a b", a=2)), r=[pr.b], w=[qrs.b])
                sc.add("pool", _dma(QN_s.rearrange("(h p) s -> p h s", p=128)[:, 2 * hp:2 * hp + 2, t0:t0 + 512], qs.t[:]), r=[qs.b], chan=qs.c)
                sc.add("pool", _dma(QR_s.rearrange("h p s -> p h s")[:, 2 * hp:2 * hp + 2, t0:t0 + 512], qrs.t[:]), r=[qrs.b], chan=qrs.c)
        kmo = Slot(k.sb(st, "kmo", [128, 4]), sc.chan("c_kmo"))
        sc.add("dve", _cp(kmo.t[:], kmax.t[:]), r=[kmax.b], w=[kmo.b])
        sc.add("pool", _dma(kmax_s[:, :], kmo.t[:]), r=[kmo.b], chan=kmo.c)
        sc.emit()

    with contextlib.ExitStack() as st:
        KT = Slot(k.sb(st, "KT", [128, 4, S], BF16), sc.chan("c_KT"))
        KR = Slot(k.sb(st, "KR", [65, S], BF16), sc.chan("c_KR"))
        VR = Slot(k.sb(st, "VR", [128, NB, 512], BF16), sc.chan("c_VR"))
        kml = Slot(k.sb(st, "kml", [128, 4]), sc.chan("c_kml"))
        km1 = Slot(k.sb(st, "km1", [1, 4]))
        kmx = Slot(k.sb(st, "kmx", [128, 4]))
        phalf = Slot(k.sb(st, "phalf", [128, 4]))
        SPB = [Slot(k.ps(st, f"spb{i}", [128, 512])) for i in range(3)]
        OPB = [Slot(k.ps(st, f"opb{i}", [128, 512])) for i in range(2)]
        RSB = Slot(k.ps(st, "rsb", [128, 512]))
        NPT = 10
        PT = [Slot(k.sb(st, f"pt{i}", [128, 512], BF16)) for i in range(NPT)]
        RS = [Slot(k.ps(st, f"rs{i}", [128, 512])) for i in range(1)]
        rs_sb = Slot(k.sb(st, "rs_sb", [128, 512]))
        ones_b = Slot(k.sb(st, "ones_b", [128, 32], BF16))
        inv32 = Slot(k.sb(st, "inv32", [128, 128]))
        sc.add("pool", _memset(ones_b.t[:], 1.0), w=[ones_b.b])
        sc.add("pool", _memset(inv32.t[:], 1.0 / 32.0), w=[inv32.b])
        QN = [Slot(k.sb(st, f"qnt{i}", [128, 512], BF16), sc.chan(f"c_qn{i}")) for i in range(2)]
        QR = [Slot(k.sb(st, f"qrt{i}", [65, 512], BF16), sc.chan(f"c_qr{i}")) for i in range(2)]
        rinv = Slot(k.sb(st, "rinv", [128, 512]))
        YO = [Slot(k.sb(st, f"yo{i}", [128, 512], BF16), sc.chan(f"c_yo{i}")) for i in range(2)]
        sc.add("sp", _dma(KT.t[:], KT_s.rearrange("(h p) s -> p h s", p=128)), w=[KT.b], chan=KT.c)
        sc.add("pool", _memset(KR.t[64:65, :], 1.0), w=[KR.b])
        sc.add("sp", _dma(KR.t[0:64, :], KR_s[:, :]), w=[KR.b], chan=KR.c)
        sc.add("sp", _dma(VR.t[:], V_s.rearrange("(c p) d -> p c d", p=128)), w=[VR.b], chan=VR.c)
        sc.add("sp", _dma(kml.t[:], kmax_s[:, :]), w=[kml.b], chan=kml.c)
        sc.add("pool", _memset(phalf.t[:], 0.5), w=[phalf.b])
        sc.add("pool", lambda e: e.tensor_reduce(km1.t[:], kml.t[:], AX.C, ALU.max), r=[kml.b], w=[km1.b])
        sc.add("pe", _mm(RSB.t[:, 0:4], ones_f[0:1, :], km1.t[:], True, True), r=[km1.b], w=[RSB.b])
        sc.add("dve", _cp(kmx.t[:], RSB.t[:, 0:4]), r=[RSB.b], w=[kmx.b])
        sc.add("pool", _tt(kmx.t[:], kmx.t[:], phalf.t[:], ALU.pow), r=[kmx.b, phalf.b], w=[kmx.b])
        scale = 192.0 ** -0.5
        it = 0
        pti = 0
        for h in range(4):
            for j in range(NT):
                qn, qr = QN[it % 2], QR[it % 2]
                opb, yo, rs = OPB[it % 2], YO[it % 2], RS[0]
                it += 1
                sc.add("sp", _dma(qn.t[:], QN_s[h * 128:(h + 1) * 128, j * 512:(j + 1) * 512]), w=[qn.b], chan=qn.c)
                sc.add("sp", _dma(qr.t[:], QR_s[h, :, j * 512:(j + 1) * 512]), w=[qr.b], chan=qr.c)
                sc.add("dve", _ts(qr.t[64:65, :], qr.t[64:65, :], kmx.t[64:65, h:h + 1], None, ALU.mult), r=[qr.b, kmx.b], w=[qr.b])

                def qk(kc):
                    sp_ = SPB[kc % 3]
                    ks = slice(kc * 128, (kc + 1) * 128)
                    sc.add("pe", _mm(sp_.t[:], KT.t[:, h, ks], qn.t[:], True, False), r=[KT.b, qn.b], w=[sp_.b])
                    sc.add("pe", _mm(sp_.t[:], KR.t[0:65, ks], qr.t[0:65, :], False, True), r=[KR.b, qr.b], w=[sp_.b])

                qk(0)
                if NB > 1:
                    qk(1)
                grp = []
                for kc in range(NB):
                    if kc + 2 < NB:
                        qk(kc + 2)
                    sp_ = SPB[kc % 3]
                    pt = PT[pti % NPT]
                    pti += 1
                    sc.add("act", _act(pt.t[:], sp_.t[:], AF.Exp, scale=scale), r=[sp_.b], w=[pt.b])
                    sc.add("pe", _mm(opb.t[:], VR.t[:, kc, h * 128:(h + 1) * 128], pt.t[:], kc == 0, kc == NB - 1),
                           r=[VR.b, pt.b], w=[opb.b])
                    grp.append(pt)
                    if len(grp) == 4:
                        for r_, ptr in enumerate(grp):
                            sc.add("pe", lambda e, r_=r_, ptr=ptr, kc=kc: e.matmul(rs.t[32 * r_:32 * r_ + 32, :], ones_b.t[:, 0:32], ptr.t[:],
                                                                               start=(kc == 3), stop=(kc == NB - 1),
                                                                               tile_position=(0, 32 * r_)),
                                   r=[ptr.b, ones_b.b], w=[rs.b])
                        grp = []
                sc.add("dve", _cp(rs_sb.t[:], rs.t[:]), r=[rs.b], w=[rs_sb.b])
                sc.add("pe", _mm(RSB.t[:], inv32.t[:], rs_sb.t[:], True, True), r=[rs_sb.b, inv32.b], w=[RSB.b])
                sc.add("dve", lambda e: e.reciprocal(rinv.t[:], RSB.t[:]), r=[RSB.b], w=[rinv.b])
                sc.add("dve", _tt(yo.t[:], opb.t[:], rinv.t[:], ALU.mult), r=[opb.b, rinv.b], w=[yo.b])
                sc.add("pool", _dma(yT_s[512 + h * 128:512 + (h + 1) * 128, j * 512:(j + 1) * 512], yo.t[:]), r=[yo.b], chan=yo.c)
        sc.emit()

    h1_s = k.dram_tmp("h1_s", [S, D])
    xn2T_s = k.dram_tmp("xn2T_s", [D, S + 2], BF16)
    h2_s = k.dram_tmp("h2_s", [S, D])
    yT3 = yT_s.rearrange("(g p) s -> p g s", p=128)
    xn2T3 = xn2T_s.rearrange("(g p) s -> p g s", p=128)

    def rms_transpose(xt, ss, ss2, rstd, junk, XB, xn, TBs, gain_scale=1.0 / 32.0):
        for b in range(4):
            sc.add("act", _act(junk.t[:], xt.t[:, b, :], AF.Square, scale=gain_scale, accum_out=ss.t[:, b:b + 1]),
                   r=[xt.b], w=[junk.b, ss.b])
        sc.add("dve", _ts(ss2.t[:], ss.t[:], EPS, None, ALU.add), r=[ss.b], w=[ss2.b])
        sc.add("pool", _tt(rstd.t[:], ss2.t[:], mhalf[:, 0:4], ALU.pow), r=[ss2.b], w=[rstd.b])
        for b in range(4):
            e = "dve" if b % 2 else "act"
            sc.add(e, scale_cast(e, XB.t[:, b, :], xt.t[:, b, :], rstd.t[:, b:b + 1]), r=[xt.b, rstd.b], w=[XB.b])
        for j in range(4):
            tb = TBs[j % 2]
            for kk in range(2):
                kc = 2 * j + kk
                for b in range(4):
                    sc.add("pe", _tr(tb.t[:, kk * 512 + b * 128:kk * 512 + (b + 1) * 128],
                                     XB.t[:, b, kc * 128:(kc + 1) * 128], ident[:]), r=[XB.b], w=[tb.b])
            e = "dve" if j % 2 else "act"
            sc.add(e, scale_cast(e, xn.t[:, 2 * j:2 * j + 2, :], tb.t[:].rearrange("p (a b) -> p a b", a=2)),
                   r=[tb.b], w=[xn.b])

    with contextlib.ExitStack() as st:
        stage = [Slot(k.sb(st, f"wstaged{i}", [128, 1024]), ld_ch[i]) for i in range(2)]
        Wout, Woutb = load_w(st, stage, "Wout", w_out, D, D)
        normg = load_bcast(st, "normg", mlstm_norm_g, 512)
        zt = Slot(k.sb(st, "zt", [128, 8, 2], BF16), sc.chan("c_zt"))
        sc.add("pool", _memset(zt.t[:], 0.0), w=[zt.b])
        sc.add("pool", _dma(xn2T3[:, :, 0:1], zt.t[:, :, 0:1], allow_slow_non_contiguous=True), r=[zt.b], chan=zt.c)
        sc.add("pool", _dma(xn2T3[:, :, S + 1:S + 2], zt.t[:, :, 1:2], allow_slow_non_contiguous=True), r=[zt.b], chan=zt.c)
        HFT = [Slot(k.sb(st, f"hft{i}", [128, 4, 512], BF16), sc.chan(f"c_hft{i}")) for i in range(2)]
        HBT = [Slot(k.sb(st, f"hbt{i}", [128, 4, 512], BF16), sc.chan(f"c_hbt{i}")) for i in range(2)]
        SOT = [Slot(k.sb(st, f"sot{i}", [128, 4, 512], BF16), sc.chan(f"c_sot{i}")) for i in range(2)]
        HS = Slot(k.sb(st, "hsd", [128, 4, 512]))
        SG = Slot(k.sb(st, "sgd", [128, 4, 512]))
        sqd = Slot(k.sb(st, "sqd", [128, 4, 512], BF16))
        ssn = Slot(k.sb(st, "ssnd", [128, 16]))
        rsn = Slot(k.sb(st, "rsnd", [128, 16]))
        YB = [Slot(k.sb(st, f"ybd{i}", [128, 4, 512], BF16)) for i in range(2)]
        YAT = [Slot(k.sb(st, f"yat{i}", [128, 4, 512], BF16)) for i in range(2)]
        YTT = [Slot(k.sb(st, f"ytt{i}", [128, 4, 512], BF16), sc.chan(f"c_ytt{i}")) for i in range(2)]
        XT = [Slot(k.sb(st, f"xtd{i}", [128, 4, D]), sc.chan(f"c_xtd{i}")) for i in range(3)]
        XB = Slot(k.sb(st, "xbd", [128, 4, D], BF16))
        XN = [Slot(k.sb(st, f"xnd{i}", [128, 8, 512], BF16), sc.chan(f"c_xnd{i}")) for i in range(2)]
        junk = Slot(k.sb(st, "junkd", [128, D], BF16))
        ss = Slot(k.sb(st, "ssd", [128, 4]))
        ss2 = Slot(k.sb(st, "ss2d", [128, 4]))
        rstd = Slot(k.sb(st, "rstdd", [128, 4]))
        TBs = [Slot(k.ps(st, f"tbd{i}", [128, 1024], BF16)) for i in range(2)]
        MB = [Slot(k.ps(st, f"mbd{i}", [128, 512])) for i in range(6)]
        cntD = {"mbi": 0}

        def combine(i):
            t0 = i * 512
            hft, hbt, sot, yb, ytt, xt = HFT[i % 2], HBT[i % 2], SOT[i % 2], YB[i % 2], YTT[i % 2], XT[i % 3]
            tv = lambda ap: ap[t0:t0 + 512, :].rearrange("(b p) d -> p b d", p=128)
            sc.add("sp", _dma(hft.t[:], tv(hf_s)), w=[hft.b], chan=hft.c)
            sc.add("sp", _dma(hbt.t[:], tv(hb_s)), w=[hbt.b], chan=hbt.c)
            sc.add("sp", _dma(sot.t[:], tv(so_s)), w=[sot.b], chan=sot.c)
            sc.add("sp", _dma(ytt.t[:], yT3[:, 4:8, t0:t0 + 512]), w=[ytt.b], chan=ytt.c)
            sc.add("sp", _dma(xt.t[:], tv(x)), w=[xt.b], chan=xt.c)
            sc.add("dve", _tt(HS.t[:], hft.t[:], hbt.t[:], ALU.add), r=[hft.b, hbt.b], w=[HS.b])
            sc.add("act", _act(sqd.t[:], HS.t[:], AF.Square, scale=128.0 ** -0.5), r=[HS.b], w=[sqd.b])
            sc.add("dve", lambda e: e.tensor_reduce(ssn.t[:], sqd.t[:].rearrange("p b (h d) -> p (b h) d", h=4), AX.X, ALU.add),
                   r=[sqd.b], w=[ssn.b])
            sc.add("dve", _ts(ssn.t[:], ssn.t[:], EPS, None, ALU.add), r=[ssn.b], w=[ssn.b])
            sc.add("pool", _tt(rsn.t[:], ssn.t[:], mhalf[:, 0:16], ALU.pow), r=[ssn.b], w=[rsn.b])
            h16 = HS.t[:].rearrange("p b (h d) -> p (b h) d", h=4)
            sc.add("dve", _tt(h16, h16, bc(rsn.t[:], 128), ALU.mult), r=[HS.b, rsn.b], w=[HS.b])
            sc.add("pool", _tt(SG.t[:], sot.t[:], bc_mid(normg[0][:], 4), ALU.mult), r=[sot.b, normg[1]], w=[SG.b])
            sc.add("dve", _tt(yb.t[:], HS.t[:], SG.t[:], ALU.mult), r=[HS.b, SG.b], w=[yb.b])

        def normelt(i):
            t0 = i * 512
            xt = XT[i % 3]
            sc.add("sp", _dma(h1_s[t0:t0 + 512, :].rearrange("(b p) d -> p b d", p=128), xt.t[:]), r=[xt.b], chan=xt.c)
            for b in range(4):
                sc.add("act", _act(junk.t[:], xt.t[:, b, :], AF.Square, scale=1.0 / 32.0, accum_out=ss.t[:, b:b + 1]),
                       r=[xt.b], w=[junk.b, ss.b])
            sc.add("dve", _ts(ss2.t[:], ss.t[:], EPS, None, ALU.add), r=[ss.b], w=[ss2.b])
            sc.add("pool", _tt(rstd.t[:], ss2.t[:], mhalf[:, 0:4], ALU.pow), r=[ss2.b], w=[rstd.b])
            for b in range(4):
                e = "dve" if b % 2 else "act"
                sc.add(e, scale_cast(e, XB.t[:, b, :], xt.t[:, b, :], rstd.t[:, b:b + 1]), r=[xt.b, rstd.b], w=[XB.b])

        def mmstage(i):
            yb, yat, ytt, xt = YB[i % 2], YAT[i % 2], YTT[i % 2], XT[i % 3]
            for hp in range(2):
                tb = TBs[hp]
                for hh in range(2):
                    h = 2 * hp + hh
                    for b in range(4):
                        sc.add("pe", _tr(tb.t[:, hh * 512 + b * 128:hh * 512 + (b + 1) * 128], yb.t[:, b, h * 128:(h + 1) * 128], ident[:]),
                               r=[yb.b], w=[tb.b])
                e = "dve" if hp else "act"
                sc.add(e, scale_cast(e, yat.t[:, 2 * hp:2 * hp + 2, :], tb.t[:].rearrange("p (a b) -> p a b", a=2)), r=[tb.b], w=[yat.b])
            mbi = cntD["mbi"]
            for b in range(4):
                for half in range(2):
                    pm = MB[mbi % 6]
                    mbi += 1
                    for kc in range(8):
                        src = yat if kc < 4 else ytt
                        sc.add("pe", _mm(pm.t[:], src.t[:, kc % 4, b * 128:(b + 1) * 128], Wout[:, kc, half * 512:(half + 1) * 512],
                                         kc == 0, kc == 7), r=[src.b, Woutb], w=[pm.b])
                    sc.add("dve", _tt(xt.t[:, b, half * 512:(half + 1) * 512], pm.t[:], xt.t[:, b, half * 512:(half + 1) * 512], ALU.add),
                           r=[pm.b, xt.b], w=[xt.b])
            cntD["mbi"] = mbi

        def trstage(i):
            t0 = i * 512
            xn = XN[i % 2]
            for j in range(4):
                tb = TBs[j % 2]
                for kk in range(2):
                    kc = 2 * j + kk
                    for b in range(4):
                        sc.add("pe", _tr(tb.t[:, kk * 512 + b * 128:kk * 512 + (b + 1) * 128],
                                         XB.t[:, b, kc * 128:(kc + 1) * 128], ident[:]), r=[XB.b], w=[tb.b])
                e = "dve" if j % 2 else "act"
                sc.add(e, scale_cast(e, xn.t[:, 2 * j:2 * j + 2, :], tb.t[:].rearrange("p (a b) -> p a b", a=2)),
                       r=[tb.b], w=[xn.b])
            sc.add("sp", _dma(xn2T3[:, :, 1 + t0:1 + t0 + 512], xn.t[:]), r=[xn.b], chan=xn.c)

        combine(0)
        for i in range(NT + 1):
            if i + 1 < NT:
                combine(i + 1)
            if i >= 1:
                normelt(i - 1)
            if i < NT:
                mmstage(i)
            if i >= 1:
                trstage(i - 1)
        sc.emit()

    TT_ = 256
    with contextlib.ExitStack() as st:
        stage = [Slot(k.sb(st, f"wstagee{i}", [128, 1408]), ld_ch[i]) for i in range(2)]
        gffn = load_cols(st, "gffn", ln_ffn_g, 8)
        Wup, Wupb = load_w(st, stage, "Wup", w_up, D, 2 * D_FF, gffn)
        Wdn, Wdnb = load_w(st, stage, "Wdn", w_down, D_FF, D)
        fw = k.sb(st, "fw", [128, 44, 3])
        fwb = Buf("fw")
        for tap in range(3):
            sc.add("sp", _dma(fw[:, :, tap], conv_ffn_w[tap].rearrange("(g p) -> p g", p=128),
                              allow_slow_non_contiguous=True), w=[Buf()], chan=sc.chan(f"c_fw{tap}"))
        fb = load_cols(st, "fb", conv_ffn_b, 44)
        sc.barrier()
        XS = [Slot(k.sb(st, f"xs{i}", [128, 8, TT_ + 2], BF16), sc.chan(f"c_xs{i}")) for i in range(2)]
        H1 = [Slot(k.sb(st, f"h1t{i}", [128, 2, D]), sc.chan(f"c_h1t{i}")) for i in range(2)]
        AT = [Slot(k.sb(st, f"at{i}", [128, 22, TT_], BF16)) for i in range(2)]
        CG = [Slot(k.sb(st, f"cg{i}", [128, TT_])) for i in range(2)]
        CV = [Slot(k.sb(st, f"cv{i}", [128, TT_])) for i in range(2)]
        SG = [Slot(k.sb(st, f"sg{i}", [128, TT_])) for i in range(2)]
        MB = [Slot(k.ps(st, f"mbe{i}", [128, 512])) for i in range(8)]
        mbi = 0
        for i in range(S // TT_):
            t0 = i * TT_
            xs, h1, at = XS[i % 2], H1[i % 2], AT[i % 2]
            sc.add("sp", _dma(xs.t[:], xn2T3[:, :, t0:t0 + TT_ + 2]), w=[xs.b], chan=xs.c)
            sc.add("sp", _dma(h1.t[:], h1_s[t0:t0 + TT_, :].rearrange("(b p) d -> p b d", p=128)), w=[h1.b], chan=h1.c)
            for g in range(22):
                res = []
                for which, (gi, dst) in enumerate(((g, CG[g % 2]), (22 + g, CV[g % 2]))):
                    pm = MB[mbi % 8]
                    mbi += 1
                    for kc in range(8):
                        sc.add("pe", _mm(pm.t[:, 0:TT_ + 2], Wup[:, kc, gi * 128:(gi + 1) * 128], xs.t[:, kc, :], kc == 0, kc == 7),
                               r=[xs.b, Wupb], w=[pm.b])
                    sc.add("act", _act(dst.t[:], pm.t[:, 0:TT_], AF.Identity, scale=fw[:, gi, 0:1], bias=fb[0][:, gi:gi + 1]),
                           r=[pm.b], w=[dst.b])
                    sc.add("dve", _stt(dst.t[:], pm.t[:, 1:TT_ + 1], fw[:, gi, 1:2], dst.t[:], ALU.mult, ALU.add), r=[pm.b, dst.b], w=[dst.b])
                    sc.add("dve", _stt(dst.t[:], pm.t[:, 2:TT_ + 2], fw[:, gi, 2:3], dst.t[:], ALU.mult, ALU.add), r=[pm.b, dst.b], w=[dst.b])
                cg, cv, sg = CG[g % 2], CV[g % 2], SG[g % 2]
                sc.add("act", _act(sg.t[:], cg.t[:], AF.Silu), r=[cg.b], w=[sg.b])
                sc.add("pool", _tt(at.t[:, g, :], sg.t[:], cv.t[:], ALU.mult), r=[sg.b, cv.b], w=[at.b])
            for b in range(TT_ // 128):
                for half in range(2):
                    pm = MB[mbi % 8]
                    mbi += 1
                    for g in range(22):
                        sc.add("pe", _mm(pm.t[:], at.t[:, g, b * 128:(b + 1) * 128], Wdn[:, g, half * 512:(half + 1) * 512], g == 0, g == 21),
                               r=[at.b, Wdnb], w=[pm.b])
                    sc.add("dve", _tt(h1.t[:, b, half * 512:(half + 1) * 512], pm.t[:], h1.t[:, b, half * 512:(half + 1) * 512], ALU.add),
                           r=[pm.b, h1.b], w=[h1.b])
            sc.add("pool", _dma(h2_s[t0:t0 + TT_, :].rearrange("(b p) d -> p b d", p=128), h1.t[:]), r=[h1.b], chan=h1.c)
        sc.emit()

    with contextlib.ExitStack() as st:
        stage = [Slot(k.sb(st, f"wstagef{i}", [128, 1024]), ld_ch[i]) for i in range(2)]
        gple = load_cols(st, "gple", ple_norm_g, 8)
        Wg, Wgb = load_w(st, stage, "Wg", w_ple_gate, D, D, gple)
        Wp, Wpb = load_w(st, stage, "Wp", w_ple_proj, 256, D)
        postg = load_bcast(st, "postg", ple_post_g, D)
        fing = load_bcast(st, "fing", final_g, D)
        sc.barrier()
        XT = [Slot(k.sb(st, f"xtf{i}", [128, 4, D]), sc.chan(f"c_xtf{i}")) for i in range(3)]
        PTL = [Slot(k.sb(st, f"ptl{i}", [128, 4, 256]), sc.chan(f"c_ptl{i}")) for i in range(2)]
        XBF = [Slot(k.sb(st, f"xbf{i}", [128, 4, D], BF16)) for i in range(2)]
        PBF = [Slot(k.sb(st, f"pbf{i}", [128, 4, 256], BF16)) for i in range(2)]
        XNF = [Slot(k.sb(st, f"xnf{i}", [128, 8, 512], BF16)) for i in range(2)]
        PTTF = [Slot(k.sb(st, f"ptt{i}", [128, 2, 512], BF16)) for i in range(2)]
        junk = Slot(k.sb(st, "junkf", [128, D], BF16))
        ss = Slot(k.sb(st, "ssf", [128, 4]))
        ss2 = Slot(k.sb(st, "ss2f", [128, 4]))
        rstd = Slot(k.sb(st, "rstdf", [128, 4]))
        ssb = Slot(k.sb(st, "ssb", [128, 2]))
        ssb2 = Slot(k.sb(st, "ssb2", [128, 2]))
        rsb2 = Slot(k.sb(st, "rsb2", [128, 2]))
        SGM = [Slot(k.sb(st, f"sgm{i}", [128, D])) for i in range(2)]
        PJ = [Slot(k.sb(st, f"pj{i}", [128, D])) for i in range(2)]
        OT = [Slot(k.sb(st, f"ot{i}", [128, D]), sc.chan(f"c_ot{i}")) for i in range(3)]
        TBs = [Slot(k.ps(st, f"tbf{i}", [128, 1024], BF16)) for i in range(2)]
        MB = [Slot(k.ps(st, f"mbf{i}", [128, 512])) for i in range(6)]
        cntF = {"mbi": 0, "bi": 0}

        def f_stage1a(i):
            t0 = i * 512
            xt, ptl, XB, PBf = XT[i % 3], PTL[i % 2], XBF[i % 2], PBF[i % 2]
            sc.add("sp", _dma(xt.t[:], h2_s[t0:t0 + 512, :].rearrange("(b p) d -> p b d", p=128)), w=[xt.b], chan=xt.c)
            sc.add("sp", _dma(ptl.t[:], p_in[t0:t0 + 512, :].rearrange("(b p) d -> p b d", p=128)), w=[ptl.b], chan=ptl.c)
            for b in range(4):
                sc.add("act", _act(junk.t[:], xt.t[:, b, :], AF.Square, scale=1.0 / 32.0, accum_out=ss.t[:, b:b + 1]),
                       r=[xt.b], w=[junk.b, ss.b])
            sc.add("dve", _ts(ss2.t[:], ss.t[:], EPS, None, ALU.add), r=[ss.b], w=[ss2.b])
            sc.add("pool", _tt(rstd.t[:], ss2.t[:], mhalf[:, 0:4], ALU.pow), r=[ss2.b], w=[rstd.b])
            for b in range(4):
                e = "dve" if b % 2 else "act"
                sc.add(e, scale_cast(e, XB.t[:, b, :], xt.t[:, b, :], rstd.t[:, b:b + 1]), r=[xt.b, rstd.b], w=[XB.b])
            sc.add("pool", _cp(PBf.t[:], ptl.t[:]), r=[ptl.b], w=[PBf.b])

        def f_stage1b(i):
            XN, PTT, XB, PBf = XNF[i % 2], PTTF[i % 2], XBF[i % 2], PBF[i % 2]
            for j in range(4):
                tb = TBs[j % 2]
                for kk in range(2):
                    kc = 2 * j + kk
                    for b in range(4):
                        sc.add("pe", _tr(tb.t[:, kk * 512 + b * 128:kk * 512 + (b + 1) * 128],
                                         XB.t[:, b, kc * 128:(kc + 1) * 128], ident[:]), r=[XB.b], w=[tb.b])
                e = "dve" if j % 2 else "act"
                sc.add(e, scale_cast(e, XN.t[:, 2 * j:2 * j + 2, :], tb.t[:].rearrange("p (a b) -> p a b", a=2)),
                       r=[tb.b], w=[XN.b])
            tb = TBs[0]
            for kc in range(2):
                for b in range(4):
                    sc.add("pe", _tr(tb.t[:, kc * 512 + b * 128:kc * 512 + (b + 1) * 128], PBf.t[:, b, kc * 128:(kc + 1) * 128], ident[:]),
                           r=[PBf.b], w=[tb.b])
            sc.add("act", _act(PTT.t[:], tb.t[:].rearrange("p (a b) -> p a b", a=2), AF.Copy), r=[tb.b], w=[PTT.b])

        def f_stage2(i, blocks):
            t0 = i * 512
            xt, XN, PTT = XT[i % 3], XNF[i % 2], PTTF[i % 2]
            mbi, bi_ = cntF["mbi"], cntF["bi"]
            for b in blocks:
                sgm, pj, ot = SGM[bi_ % 2], PJ[bi_ % 2], OT[bi_ % 3]
                bi_ += 1
                tok = slice(b * 128, (b + 1) * 128)
                for half in range(2):
                    hs = slice(half * 512, (half + 1) * 512)
                    pm = MB[mbi % 6]
                    mbi += 1
                    for kc in range(8):
                        sc.add("pe", _mm(pm.t[:], XN.t[:, kc, tok], Wg[:, kc, hs], kc == 0, kc == 7), r=[XN.b, Wgb], w=[pm.b])
                    sc.add("act", _act(sgm.t[:, hs], pm.t[:], AF.Sigmoid), r=[pm.b], w=[sgm.b])
                    pm = MB[mbi % 6]
                    mbi += 1
                    for kc in range(2):
                        sc.add("pe", _mm(pm.t[:], PTT.t[:, kc, tok], Wp[:, kc, hs], kc == 0, kc == 1), r=[PTT.b, Wpb], w=[pm.b])
                    sc.add("act", _act(pj.t[:, hs], pm.t[:], AF.Copy), r=[pm.b], w=[pj.b])
                sc.add("act", _act(junk.t[:], pj.t[:], AF.Square, scale=1.0 / 32.0, accum_out=ssb.t[:, 0:1]), r=[pj.b], w=[junk.b, ssb.b])
                sc.add("dve", _ts(ssb2.t[:, 0:1], ssb.t[:, 0:1], EPS, None, ALU.add), r=[ssb.b], w=[ssb2.b])
                sc.add("pool", _tt(rsb2.t[:, 0:1], ssb2.t[:, 0:1], mhalf[:, 0:1], ALU.pow), r=[ssb2.b], w=[rsb2.b])
                sc.add("pool", _tt(pj.t[:], pj.t[:], postg[0][:], ALU.mult), r=[pj.b, postg[1]], w=[pj.b])
                sc.add("dve", _stt(pj.t[:], pj.t[:], rsb2.t[:, 0:1], sgm.t[:], ALU.mult, ALU.mult), r=[pj.b, rsb2.b, sgm.b], w=[pj.b])
                sc.add("dve", _tt(pj.t[:], pj.t[:], xt.t[:, b, :], ALU.add), r=[pj.b, xt.b], w=[pj.b])
                sc.add("act", _act(junk.t[:], pj.t[:], AF.Square, scale=1.0 / 32.0, accum_out=ssb.t[:, 1:2]), r=[pj.b], w=[junk.b, ssb.b])
                sc.add("dve", _ts(ssb2.t[:, 1:2], ssb.t[:, 1:2], EPS, None, ALU.add), r=[ssb.b], w=[ssb2.b])
                sc.add("pool", _tt(rsb2.t[:, 1:2], ssb2.t[:, 1:2], mhalf[:, 0:1], ALU.pow), r=[ssb2.b], w=[rsb2.b])
                sc.add("dve", _stt(ot.t[:], pj.t[:], rsb2.t[:, 1:2], fing[0][:], ALU.mult, ALU.mult), r=[pj.b, rsb2.b, fing[1]], w=[ot.b])
                sc.add("sp", _dma(out[t0 + b * 128:t0 + (b + 1) * 128, :], ot.t[:]), r=[ot.b], chan=ot.c)
            cntF["mbi"], cntF["bi"] = mbi, bi_

        f_stage1a(0)
        f_stage1b(0)
        if NT > 1:
            f_stage1a(1)
        for i in range(NT):
            f_stage2(i, (0, 1))
            if i + 1 < NT:
                f_stage1b(i + 1)
            if i + 2 < NT:
                f_stage1a(i + 2)
            f_stage2(i, (2, 3))
        sc.emit()

    k.final_wait = None
    return k


def finish(k):
    return k.nc


_W_NAMES = ["ln_mix_g", "w_in", "b_gates", "conv_qk_w", "conv_qk_b", "mlstm_norm_g", "q_norm_g", "w_uq", "kv_norm_g",
            "w_ukv", "w_out", "ln_ffn_g", "w_up", "conv_ffn_w", "conv_ffn_b", "w_down", "ple_norm_g", "w_ple_gate",
            "w_ple_proj", "ple_post_g"]


def kernel(**inputs):
    x = np.asarray(inputs["x"])
    p = np.asarray(inputs["p"])
    B, S, _ = x.shape
    nc = finish(build(S))
    shared = {n: np.ascontiguousarray(np.asarray(inputs[n])[0], dtype=np.float32) for n in _W_NAMES}
    shared["final_g"] = np.ascontiguousarray(np.asarray(inputs["final_g"]), dtype=np.float32)
    in_maps = []
    for b in range(B):
        m = dict(shared)
        m["x"] = np.ascontiguousarray(x[b], dtype=np.float32)
        m["p"] = np.ascontiguousarray(p[0, b], dtype=np.float32)
        in_maps.append(m)
    res = run_bass_kernel_spmd(nc, in_maps, core_ids=list(range(B)))
    return np.stack([np.asarray(r["out"]) for r in res.results], axis=0).astype(np.float32)
```

```python
import contextlib
import numpy as np
import concourse.bass as bass
import concourse.mybir as mybir
from concourse.bass_utils import run_bass_kernel_spmd

F32 = mybir.dt.float32
BF16 = mybir.dt.bfloat16
AF = mybir.ActivationFunctionType
ALU = mybir.AluOpType
AX = mybir.AxisListType

D = 1024
NH = 4
IN_COLS = 2512
D_FF = 2816
EPS = 1e-6
SEM_MAX = 30000


class Buf:
    __slots__ = ("name", "lw", "rd")

    def __init__(self, name=""):
        self.name = name
        self.lw = None
        self.rd = []


class Chan:
    __slots__ = ("sem", "count", "last")

    def __init__(self, sem):
        self.sem = sem
        self.count = 0
        self.last = None


class Op:
    __slots__ = ("eng", "fn", "deps", "sig", "signo", "chan", "cval", "done")


class Sched:
    ENGS = ("pe", "act", "dve", "pool", "sp")

    def __init__(self, nc, stack):
        self.nc = nc
        self.stack = stack
        self.ops = []
        self.last_on = {e: None for e in self.ENGS}
        self.pending_bar = {e: [] for e in self.ENGS}
        self.chans = []
        self.free_chans = []
        self.phase_chans = []
        self.cnt = {e: 0 for e in self.ENGS}
        self.sems = {e: [] for e in self.ENGS}
        self.waited = {e: {} for e in self.ENGS}

    def chan(self, name, keep=False):
        if self.free_chans and not keep:
            c = self.free_chans.pop()
        else:
            c = Chan(self.stack.enter_context(self.nc.semaphore(name)))
            self.chans.append(c)
        if not keep:
            self.phase_chans.append(c)
        return c

    def add(self, eng, fn, r=(), w=(), chan=None):
        op = Op()
        op.eng, op.fn, op.deps, op.sig, op.signo, op.chan, op.cval = eng, fn, {}, False, 0, chan, 0
        op.done = False
        for b in r:
            if b.lw is not None:
                op.deps[b.lw] = True
        for b in w:
            if b.lw is not None:
                op.deps.setdefault(b.lw, False)
            for q in b.rd:
                op.deps.setdefault(q, False)
        for b in r:
            b.rd.append(op)
        for b in w:
            b.lw = op
            b.rd = []
        if self.pending_bar[eng]:
            for d in self.pending_bar[eng]:
                op.deps[d] = True
            self.pending_bar[eng] = []
        if chan is not None:
            if chan.last is not None:
                op.deps[chan.last] = True
            chan.count += 16
            op.cval = chan.count
            chan.last = op
        op.deps.pop(op, None)
        self.ops.append(op)
        self.last_on[eng] = op
        return op

    def barrier(self):
        lasts = [o for o in self.last_on.values() if o is not None]
        lasts += [c.last for c in self.chans if c.last is not None]
        for e in self.ENGS:
            self.pending_bar[e] = list(lasts)

    def emit(self):
        nc = self.nc
        fin = self.add("sp", lambda e: e.nop())
        for c in self.chans:
            if c.last is not None and not c.last.done:
                fin.deps[c.last] = True
        for e in self.ENGS:
            self.pending_bar[e] = []
        for op in self.ops:
            for d in [d for d in op.deps if d.done]:
                del op.deps[d]
            for d, raw in op.deps.items():
                if d.chan is not None:
                    continue
                if d.eng == op.eng and (op.eng == "pe" or not raw):
                    continue
                d.sig = True
        cnt = self.cnt
        for op in self.ops:
            if op.chan is None and op.sig:
                cnt[op.eng] += 1
                op.signo = cnt[op.eng]
        sems = self.sems
        for e in self.ENGS:
            n = cnt[e] // SEM_MAX + 1
            while len(sems[e]) < n:
                sems[e].append(self.stack.enter_context(nc.semaphore(f"s_{e}{len(sems[e])}")))
        per = {e: [o for o in self.ops if o.eng == e] for e in self.ENGS}
        handles = {"pe": "tensor", "act": "scalar", "dve": "vector", "pool": "gpsimd", "sp": "sync"}

        def run(e, eng):
            waited = self.waited[e]
            for op in per[e]:
                for d, raw in op.deps.items():
                    if d.chan is not None:
                        key, val, sem = ("c", id(d.chan)), d.cval, d.chan.sem
                    else:
                        if d.eng == e and (e == "pe" or not raw):
                            continue
                        j = (d.signo - 1) // SEM_MAX
                        key, val, sem = (d.eng, j), d.signo - j * SEM_MAX, sems[d.eng][j]
                    if waited.get(key, 0) >= val:
                        continue
                    waited[key] = val
                    eng.wait_ge(sem, val)
                ins = op.fn(eng)
                if op.chan is not None:
                    ins.then_inc(op.chan.sem, 16)
                elif op.sig:
                    j = (op.signo - 1) // SEM_MAX
                    ins.then_inc(sems[e][j], 1)

        with nc.Block() as block:
            for e in self.ENGS:
                if per[e]:
                    getattr(block, handles[e])(lambda eng, e=e: run(e, eng))
        for op in self.ops:
            op.done = True
            op.fn = None
            op.deps = {}
        self.ops = []
        self.last_on = {e: None for e in self.ENGS}
        for c in self.phase_chans:
            c.last = None
        self.free_chans.extend(self.phase_chans)
        self.phase_chans = []


def _act(out, in_, func, **kw):
    return lambda e: e.activation(out, in_, func, **kw)


def _ts(out, in0, s1, s2, op0, op1=None):
    if op1 is None:
        return lambda e: e.tensor_scalar(out, in0, s1, None, op0)
    return lambda e: e.tensor_scalar(out, in0, s1, s2, op0, op1)


def _stt(out, in0, sc, in1, op0, op1):
    return lambda e: e.scalar_tensor_tensor(out, in0, sc, in1, op0, op1)


def _tt(out, in0, in1, op):
    return lambda e: e.tensor_tensor(out, in0, in1, op)


def _cp(out, in_):
    return lambda e: e.tensor_copy(out, in_)


def _mm(out, lhsT, rhs, start, stop):
    return lambda e: e.matmul(out, lhsT, rhs, start=start, stop=stop)


def _tr(out, in_, ident):
    return lambda e: e.transpose(out, in_, ident)


def _dma(out, in_, **kw):
    return lambda e: e.dma_start(out=out, in_=in_, **kw)


def _memset(ap, v):
    return lambda e: e.memset(ap, v)


class K:
    def __init__(self, S, dbg=()):
        self.S = S
        self.dbg = dbg
        self.nc = bass.Bass("TRN2", target_bir_lowering=False)
        self.stack = contextlib.ExitStack()
        self.sc = Sched(self.nc, self.stack)

    def sb(self, st, name, shape, dt=F32):
        return st.enter_context(self.nc.sbuf_tensor(name, list(shape), dt))

    def ps(self, st, name, shape, dt=F32):
        return st.enter_context(self.nc.psum_tensor(name, list(shape), dt))

    def dram_in(self, name, shape, dt=F32):
        return self.nc.dram_tensor(name, list(shape), dt, kind="ExternalInput").ap()

    def dram_out(self, name, shape, dt=F32):
        return self.nc.dram_tensor(name, list(shape), dt, kind="ExternalOutput").ap()

    def dram_tmp(self, name, shape, dt=F32):
        if name in self.dbg:
            return self.nc.dram_tensor(name, list(shape), dt, kind="ExternalOutput").ap()
        return self.nc.dram_tensor(name, list(shape), dt).ap()


class Slot:
    def __init__(self, t, chan=None):
        self.t = t
        self.b = Buf()
        self.c = chan


def build(S, dbg=()):
    k = K(S, dbg)
    nc, sc = k.nc, k.sc
    NT, NB = S // 512, S // 128
    top = k.stack

    x = k.dram_in("x", [S, D])
    p_in = k.dram_in("p", [S, 256])
    ln_mix_g = k.dram_in("ln_mix_g", [D])
    w_in = k.dram_in("w_in", [D, IN_COLS])
    b_gates = k.dram_in("b_gates", [16])
    conv_qk_w = k.dram_in("conv_qk_w", [3, 1024])
    conv_qk_b = k.dram_in("conv_qk_b", [1024])
    mlstm_norm_g = k.dram_in("mlstm_norm_g", [512])
    q_norm_g = k.dram_in("q_norm_g", [256])
    w_uq = k.dram_in("w_uq", [256, 768])
    kv_norm_g = k.dram_in("kv_norm_g", [128])
    w_ukv = k.dram_in("w_ukv", [128, 1024])
    w_out = k.dram_in("w_out", [1024, 1024])
    ln_ffn_g = k.dram_in("ln_ffn_g", [D])
    w_up = k.dram_in("w_up", [D, 2 * D_FF])
    conv_ffn_w = k.dram_in("conv_ffn_w", [3, 2 * D_FF])
    conv_ffn_b = k.dram_in("conv_ffn_b", [2 * D_FF])
    w_down = k.dram_in("w_down", [D_FF, D])
    ple_norm_g = k.dram_in("ple_norm_g", [D])
    w_ple_gate = k.dram_in("w_ple_gate", [D, D])
    w_ple_proj = k.dram_in("w_ple_proj", [256, D])
    ple_post_g = k.dram_in("ple_post_g", [D])
    final_g = k.dram_in("final_g", [D])
    out = k.dram_out("out", [S, D])

    qkT = k.dram_tmp("qkT", [1024, S], BF16)
    vo_s = k.dram_tmp("vo_s", [S, 512])
    so_s = k.dram_tmp("so_s", [S, 512], BF16)
    g3_s = k.dram_tmp("g3_s", [S, 464])

    ident = k.sb(top, "ident", [128, 128], BF16)
    ones_f = k.sb(top, "ones_f", [128, 128], F32)
    mhalf = k.sb(top, "mhalf", [128, 16], F32)
    cb = Buf("consts")
    sc.add("pool", _memset(ones_f[:], 1.0), w=[cb])
    sc.add("pool", _memset(mhalf[:], -0.5), w=[cb])
    sc.add("pool", lambda e: e.affine_select(ident[:], ones_f[:], [[-1, 128]], ALU.is_equal, 0.0,
                                             base=0, channel_multiplier=1), r=[cb], w=[cb])
    sc.emit()

    ld_ch = [sc.chan(f"ldw{i}", keep=True) for i in range(2)]
    cnt = {"w": 0, "e": 0}

    def alt():
        cnt["e"] += 1
        return "dve" if cnt["e"] % 2 else "act"

    def scale_cast(eng, out_ap, in_ap, sc_ap=None):
        if eng == "act":
            if sc_ap is None:
                return _act(out_ap, in_ap, AF.Copy)
            return _act(out_ap, in_ap, AF.Copy, scale=sc_ap)
        if sc_ap is None:
            return _cp(out_ap, in_ap)
        return _ts(out_ap, in_ap, sc_ap, None, ALU.mult)

    def load_cols(st, name, src, G):
        t = k.sb(st, name, [128, G])
        b = Buf(name)
        ch = sc.chan("c_" + name)
        sc.add("sp", _dma(t[:], src.rearrange("(g p) -> p g", p=128), allow_slow_non_contiguous=True),
               w=[b], chan=ch)
        return t, b

    def load_bcast(st, name, src, n):
        t = k.sb(st, name, [128, n])
        b = Buf(name)
        ch = sc.chan("c_" + name)
        sc.add("sp", _dma(t[:], bass.AP(src.tensor, src.offset, [[0, 128], [1, n]])), w=[b], chan=ch)
        return t, b

    def load_w(st, stage, name, src, Kdim, cols, gain=None):
        kcn = Kdim // 128
        t = k.sb(st, name, [128, kcn, cols], BF16)
        b = Buf(name)
        for kc in range(kcn):
            sw = stage[0].t.shape[1]
            for c0 in range(0, cols, sw):
                w = min(sw, cols - c0)
                s = stage[cnt["w"] % 2]
                cnt["w"] += 1
                sc.add("sp", _dma(s.t[:, 0:w], src[kc * 128:(kc + 1) * 128, c0:c0 + w]), w=[s.b], chan=s.c)
                g = None if gain is None else gain[0][:, kc:kc + 1]
                rr = [s.b] + ([] if gain is None else [gain[1]])
                e = alt()
                sc.add(e, scale_cast(e, t[:, kc, c0:c0 + w], s.t[:, 0:w], g), r=rr, w=[b])
        return t, b

    with contextlib.ExitStack() as st:
        stage = [Slot(k.sb(st, f"wstage{i}", [128, 2816]), ld_ch[i]) for i in range(2)]
        gmix = load_cols(st, "gmix", ln_mix_g, 8)
        Win, Winb = load_w(st, stage, "Win", w_in, D, IN_COLS, gmix)
        cw = k.sb(st, "cw", [128, 8, 3])
        cwb = Buf("cw")
        for tap in range(3):
            sc.add("sp", _dma(cw[:, :, tap], conv_qk_w[tap].rearrange("(g p) -> p g", p=128),
                              allow_slow_non_contiguous=True), w=[Buf()], chan=sc.chan(f"c_cw{tap}"))
        cbias = load_cols(st, "cbias", conv_qk_b, 8)
        bg = load_bcast(st, "bg", b_gates, 16)
        sc.barrier()

        XT = [Slot(k.sb(st, f"xt{i}", [128, 4, D]), sc.chan(f"c_xt{i}")) for i in range(3)]
        XBA = [Slot(k.sb(st, f"xb{i}", [128, 4, D], BF16)) for i in range(2)]
        XN = [Slot(k.sb(st, f"xn{i}", [128, 8, 512], BF16)) for i in range(2)]
        junk = Slot(k.sb(st, "junk", [128, D], BF16))
        ss = Slot(k.sb(st, "ss", [128, 4]))
        ss2 = Slot(k.sb(st, "ss2", [128, 4]))
        rstd = Slot(k.sb(st, "rstd", [128, 4]))
        PRE = [Slot(k.sb(st, f"pre{g}", [128, 514])) for g in range(8)]
        ACC = [Slot(k.sb(st, f"acc{i}", [128, 512])) for i in range(2)]
        QKB = [Slot(k.sb(st, f"qkb{i}", [128, 512], BF16), sc.chan(f"c_qkb{i}")) for i in range(3)]
        VO = [Slot(k.sb(st, f"vo{i}", [128, 1024]), sc.chan(f"c_vo{i}")) for i in range(2)]
        G3 = [Slot(k.sb(st, f"g3{i}", [128, 464]), sc.chan(f"c_g3{i}")) for i in range(2)]
        SOB = [Slot(k.sb(st, f"sob{i}", [128, 512], BF16), sc.chan(f"c_sob{i}")) for i in range(2)]
        TB = [Slot(k.ps(st, f"tb{i}", [128, 1024], BF16)) for i in range(2)]
        MB = [Slot(k.ps(st, f"mb{i}", [128, 512])) for i in range(6)]
        last8 = Slot(k.sb(st, "last8", [128, 8]))
        last8b = Slot(k.sb(st, "last8b", [128, 8], BF16), sc.chan("c_last8"))
        for g in range(8):
            sc.add("pool", _memset(PRE[g].t[:, 0:2], 0.0), w=[PRE[g].b])
        cntA = {"mbi": 0, "qi": 0, "voi": 0}

        def stage1a(i):
            t0 = i * 512
            xt, XB = XT[i % 3], XBA[i % 2]
            sc.add("sp", _dma(xt.t[:], x[t0:t0 + 512, :].rearrange("(b p) d -> p b d", p=128)), w=[xt.b], chan=xt.c)
            for b in range(4):
                sc.add("act", _act(junk.t[:], xt.t[:, b, :], AF.Square, scale=1.0 / 32.0, accum_out=ss.t[:, b:b + 1]),
                       r=[xt.b], w=[junk.b, ss.b])
            sc.add("dve", _ts(ss2.t[:], ss.t[:], EPS, None, ALU.add), r=[ss.b], w=[ss2.b])
            sc.add("pool", _tt(rstd.t[:], ss2.t[:], mhalf[:, 0:4], ALU.pow), r=[ss2.b], w=[rstd.b])
            for b in range(4):
                e = "dve" if b % 2 else "act"
                sc.add(e, scale_cast(e, XB.t[:, b, :], xt.t[:, b, :], rstd.t[:, b:b + 1]), r=[xt.b, rstd.b], w=[XB.b])

        def stage1b(i):
            xn, XB = XN[i % 2], XBA[i % 2]
            for j in range(4):
                tb = TB[j % 2]
                for kk in range(2):
                    kc = 2 * j + kk
                    for b in range(4):
                        sc.add("pe", _tr(tb.t[:, kk * 512 + b * 128:kk * 512 + (b + 1) * 128],
                                         XB.t[:, b, kc * 128:(kc + 1) * 128], ident[:]), r=[XB.b], w=[tb.b])
                e = "dve" if j % 2 else "act"
                sc.add(e, scale_cast(e, xn.t[:, 2 * j:2 * j + 2, :], tb.t[:].rearrange("p (a b) -> p a b", a=2)),
                       r=[tb.b], w=[xn.b])

        def stage2fm(i):
            t0 = i * 512
            xn = XN[i % 2]
            mbi, qi, voi = cntA["mbi"], cntA["qi"], cntA["voi"]
            def tail(g, qi):
                pre = PRE[g]
                acc = ACC[g % 2]
                qb = QKB[qi % 3]
                sc.add("dve", _ts(acc.t[:], pre.t[:, 2:514], cw[:, g, 2:3], None, ALU.mult), r=[pre.b], w=[acc.b])
                sc.add("dve", _stt(acc.t[:], pre.t[:, 1:513], cw[:, g, 1:2], acc.t[:], ALU.mult, ALU.add),
                       r=[pre.b, acc.b], w=[acc.b])
                sc.add("dve", _stt(acc.t[:], pre.t[:, 0:512], cw[:, g, 0:1], acc.t[:], ALU.mult, ALU.add),
                       r=[pre.b, acc.b], w=[acc.b])
                sc.add("act", _act(qb.t[:], acc.t[:], AF.Silu, bias=cbias[0][:, g:g + 1]), r=[acc.b], w=[qb.b])
                if i == 0:
                    sc.add("pool", _dma(qkT[g * 128:(g + 1) * 128, 0:511], qb.t[:, 1:512]), r=[qb.b], chan=qb.c)
                else:
                    sc.add("pool", _dma(qkT[g * 128:(g + 1) * 128, t0 - 1:t0 + 511], qb.t[:]), r=[qb.b], chan=qb.c)
                sc.add("dve", _cp(pre.t[:, 0:2], pre.t[:, 512:514]), r=[pre.b], w=[pre.b])

            for g in range(8):
                pm = MB[mbi % 6]
                mbi += 1
                for kc in range(8):
                    sc.add("pe", _mm(pm.t[:], Win[:, kc, g * 128:(g + 1) * 128], xn.t[:, kc, :], kc == 0, kc == 7),
                           r=[xn.b, Winb], w=[pm.b])
                sc.add("act", _act(PRE[g].t[:, 2:514], pm.t[:], AF.Copy), r=[pm.b], w=[PRE[g].b])
                if g >= 1:
                    tail(g - 1, qi)
                    qi += 1
            tail(7, qi)
            qi += 1
            cntA["mbi"], cntA["qi"], cntA["voi"] = mbi, qi, voi

        def stage2tm(i):
            t0 = i * 512
            xn = XN[i % 2]
            mbi, qi, voi = cntA["mbi"], cntA["qi"], cntA["voi"]
            for b in range(4):
                vo = VO[voi % 2]
                g3 = G3[voi % 2]
                sob = SOB[voi % 2]
                voi += 1
                for part, (c0, c1) in enumerate(((1024, 1536), (1536, 2048), (2048, 2512))):
                    pm = MB[mbi % 6]
                    mbi += 1
                    for kc in range(8):
                        sc.add("pe", _mm(pm.t[:, 0:c1 - c0], xn.t[:, kc, b * 128:(b + 1) * 128], Win[:, kc, c0:c1],
                                         kc == 0, kc == 7), r=[xn.b, Winb], w=[pm.b])
                    if part == 0:
                        sc.add("dve", _cp(vo.t[:, 0:512], pm.t[:]), r=[pm.b], w=[vo.b])
                    elif part == 1:
                        sc.add("act", _act(vo.t[:, 512:1024], pm.t[:], AF.Tanh, scale=0.5), r=[pm.b], w=[vo.b])
                        sc.add("dve", _ts(sob.t[:], vo.t[:, 512:1024], 0.5, 0.5, ALU.mult, ALU.add),
                               r=[vo.b], w=[sob.b])
                    else:
                        sc.add("act", _act(g3.t[:], pm.t[:, 0:464], AF.Copy), r=[pm.b], w=[g3.b])
                        sc.add("dve", _tt(g3.t[:, 0:16], g3.t[:, 0:16], bg[0][:], ALU.add), r=[g3.b], w=[g3.b])
                r0 = t0 + b * 128
                sc.add("pool", _dma(vo_s[r0:r0 + 128, :], vo.t[:, 0:512]), r=[vo.b], chan=vo.c)
                sc.add("pool", _dma(so_s[r0:r0 + 128, :], sob.t[:]), r=[sob.b], chan=sob.c)
                sc.add("pool", _dma(g3_s[r0:r0 + 128, :], g3.t[:]), r=[g3.b], chan=g3.c)
            cntA["mbi"], cntA["qi"], cntA["voi"] = mbi, qi, voi

        stage1a(0)
        stage1b(0)
        if NT > 1:
            stage1a(1)
        for i in range(NT):
            stage2fm(i)
            if i + 1 < NT:
                stage1b(i + 1)
            if i + 2 < NT:
                stage1a(i + 2)
            stage2tm(i)
        prb = [PRE[g].b for g in range(8)]
        for g in range(8):
            sc.add("dve", _ts(last8.t[:, g:g + 1], PRE[g].t[:, 0:1], cw[:, g, 0:1], None, ALU.mult), r=[PRE[g].b], w=[last8.b])
            sc.add("dve", _stt(last8.t[:, g:g + 1], PRE[g].t[:, 1:2], cw[:, g, 1:2], last8.t[:, g:g + 1], ALU.mult, ALU.add),
                   r=[PRE[g].b, last8.b], w=[last8.b])
        sc.add("dve", _tt(last8.t[:], last8.t[:], cbias[0][:], ALU.add), r=[last8.b], w=[last8.b])
        sc.add("act", _act(last8b.t[:], last8.t[:], AF.Silu), r=[last8.b], w=[last8b.b])
        sc.add("pool", _dma(qkT.rearrange("(g p) s -> p g s", p=128)[:, :, S - 1], last8b.t[:],
                            allow_slow_non_contiguous=True), r=[last8b.b], chan=last8b.c)
        sc.emit()

    hf_s = k.dram_tmp("hf_s", [S, 512], BF16)
    hb_s = k.dram_tmp("hb_s", [S, 512], BF16)
    yT_s = k.dram_tmp("yT_s", [1024, S], BF16)

    def bc(ap, m):
        a = [list(d) for d in ap.ap]
        return bass.AP(ap.tensor, ap.offset, a + [[0, m]])

    def bc_mid(ap, m):
        a = [list(d) for d in ap.ap]
        return bass.AP(ap.tensor, ap.offset, [a[0], [0, m]] + a[1:])

    with contextlib.ExitStack() as st:
        maskF = k.sb(st, "maskF", [128, 128])
        maskB = k.sb(st, "maskB", [128, 128])
        mb_ = Buf("masks")
        sc.add("pool", lambda e: e.affine_select(maskF[:], ones_f[:], [[1, 128]], ALU.is_ge, 0.0,
                                                 base=0, channel_multiplier=-1), w=[mb_])
        sc.add("pool", lambda e: e.affine_select(maskB[:], ones_f[:], [[-1, 128]], ALU.is_ge, 0.0,
                                                 base=0, channel_multiplier=1), w=[mb_])
        hdst = (hf_s, hb_s)
        tris = (maskF, maskB)

        def ring(name, shape, dt=F32, chan=False, n=2):
            return [[Slot(k.sb(st, f"{name}{d}_{i}", shape, dt), sc.chan(f"c_{name}{d}_{i}") if chan else None) for i in range(n)]
                    for d in range(2)]

        SL = ring("sl", [128, 8, 512], BF16, True)
        VOT = ring("vot", [128, 512], F32, True)
        GT = ring("gt", [128, 16], F32, True)
        HO = ring("ho", [128, 512], BF16, True)
        AA = ring("aa", [128, 4])
        EG = ring("eg", [128, 8])
        V1 = ring("v1", [128, 4, 130], BF16)
        KTOK = ring("ktok", [128, 4, 128], BF16)
        MM = ring("mm", [128, 4, 128], BF16)
        e1 = [Slot(k.sb(st, f"e1_{d}", [128, 4])) for d in range(2)]
        lsp = [Slot(k.sb(st, f"lsp_{d}", [128, 4])) for d in range(2)]
        tmpa = [Slot(k.sb(st, f"tmpa_{d}", [128, 4])) for d in range(2)]
        C1 = [Slot(k.sb(st, f"c1_{d}", [128, 4, 130])) for d in range(2)]
        C1b = [Slot(k.sb(st, f"c1b_{d}", [128, 4, 130], BF16)) for d in range(2)]
        tmpC = [Slot(k.sb(st, f"tmpc_{d}", [128, 4, 130])) for d in range(2)]
        den = [Slot(k.sb(st, f"den_{d}", [128, 4])) for d in range(2)]
        rr_ = [Slot(k.sb(st, f"rr_{d}", [128, 4])) for d in range(2)]
        TBK = Slot(k.ps(st, "tbk", [128, 1024], BF16))
        SPS = [Slot(k.ps(st, f"sps{i}", [128, 512])) for i in range(2)]
        UPS = Slot(k.ps(st, "ups", [128, 1024]))
        DPS = Slot(k.ps(st, "dps", [128, 1024]))
        GPS = Slot(k.ps(st, "gps", [128, 512]))
        qkT3 = qkT.rearrange("(g p) s -> p g s", p=128)
        slab = [{"cg": None, "n": 0} for _ in range(2)]

        def pre(c, d, n):
            cg = c // 4
            if slab[d]["cg"] != cg:
                slab[d]["cg"] = cg
                slab[d]["n"] += 1
                sl = SL[d][slab[d]["n"] % 2]
                sc.add("sp", _dma(sl.t[:], qkT3[:, :, cg * 512:(cg + 1) * 512]), w=[sl.b], chan=sl.c)
            sl = SL[d][slab[d]["n"] % 2]
            vot, gt, aa, eg, v1, ktok, mm = (VOT[d][n % 2], GT[d][n % 2], AA[d][n % 2], EG[d][n % 2], V1[d][n % 2],
                                             KTOK[d][n % 2], MM[d][n % 2])
            sps = SPS[d]
            r0 = c * 128
            sc.add("sp", _dma(vot.t[:], vo_s[r0:r0 + 128, :]), w=[vot.b], chan=vot.c)
            sc.add("sp", _dma(gt.t[:], g3_s[r0:r0 + 128, 0:16]), w=[gt.b], chan=gt.c)
            io, fo = d * 8, d * 8 + 4
            tri = tris[d]
            sc.add("act", _act(e1[d].t[:], gt.t[:, fo:fo + 4], AF.Exp, scale=-1.0), r=[gt.b], w=[e1[d].b])
            sc.add("act", _act(lsp[d].t[:], e1[d].t[:], AF.Ln, bias=1.0), r=[e1[d].b], w=[lsp[d].b])
            gcol = d * 8
            sc.add("pe", _mm(GPS.t[:, gcol:gcol + 4], tri[:], lsp[d].t[:], True, True), r=[lsp[d].b, mb_], w=[GPS.b])
            sc.add("pe", _mm(GPS.t[:, gcol + 4:gcol + 8], ones_f[:], lsp[d].t[:], True, True), r=[lsp[d].b], w=[GPS.b])
            sc.add("dve", _tt(tmpa[d].t[:], gt.t[:, io:io + 4], GPS.t[:, gcol:gcol + 4], ALU.add), r=[gt.b, GPS.b], w=[tmpa[d].b])
            sc.add("act", _act(aa.t[:], tmpa[d].t[:], AF.Exp), r=[tmpa[d].b], w=[aa.b])
            sc.add("act", _act(eg.t[:], GPS.t[:, gcol:gcol + 8], AF.Exp, scale=-1.0), r=[GPS.b], w=[eg.b])
            sc.add("dve", _tt(v1.t[:, :, 0:128], vot.t[:].rearrange("p (h d) -> p h d", h=4), bc(aa.t[:], 128), ALU.mult),
                   r=[vot.b, aa.b], w=[v1.b])
            sc.add("dve", _cp(v1.t[:, :, 128], aa.t[:]), r=[aa.b], w=[v1.b])
            c4 = (c % 4) * 128
            tcol = d * 512
            for h in range(4):
                sc.add("pe", _tr(TBK.t[:, tcol + h * 128:tcol + (h + 1) * 128], sl.t[:, 4 + h, c4:c4 + 128], ident[:]), r=[sl.b], w=[TBK.b])
            sc.add("act", _act(ktok.t[:], TBK.t[:, tcol:tcol + 512].rearrange("p (h d) -> p h d", h=4), AF.Copy, scale=128.0 ** -0.5),
                   r=[TBK.b], w=[ktok.b])
            for h in range(4):
                sc.add("pe", _mm(sps.t[:, h * 128:(h + 1) * 128], sl.t[:, 4 + h, c4:c4 + 128], sl.t[:, h, c4:c4 + 128], True, True),
                       r=[sl.b], w=[sps.b])
            sc.add("dve", _stt(mm.t[:], sps.t[:].rearrange("p (h d) -> p h d", h=4), 128.0 ** -0.5, bc_mid(tri[:], 4), ALU.mult, ALU.mult),
                   r=[sps.b, mb_], w=[mm.b])
            return sl, c4

        def main(c, d, n, sl, c4):
            eg, v1, ktok, mm, ho = EG[d][n % 2], V1[d][n % 2], KTOK[d][n % 2], MM[d][n % 2], HO[d][n % 2]
            c1, c1b, tc_, dn, rr = C1[d], C1b[d], tmpC[d], den[d], rr_[d]
            r0 = c * 128
            for h in range(4):
                sc.add("pe", _mm(UPS.t[:, h * 256:h * 256 + 129], mm.t[:, h, :], v1.t[:, h, 0:129], True, False), r=[mm.b, v1.b], w=[UPS.b])
                sc.add("pe", _mm(UPS.t[:, h * 256:h * 256 + 129], sl.t[:, h, c4:c4 + 128], c1b.t[:, h, 0:129], False, True),
                       r=[sl.b, c1b.b], w=[UPS.b])
            for h in range(4):
                sc.add("pe", _mm(DPS.t[:, h * 256:h * 256 + 129], ktok.t[:, h, :], v1.t[:, h, 0:129], True, True), r=[ktok.b, v1.b], w=[DPS.b])
            U3 = UPS.t[:].rearrange("p (h d) -> p h d", h=4)
            D3 = DPS.t[:].rearrange("p (h d) -> p h d", h=4)
            sc.add("dve", _tt(tc_.t[:, :, 0:129], D3[:, :, 0:129], c1.t[:, :, 0:129], ALU.add), r=[DPS.b, c1.b], w=[tc_.b])
            sc.add("pool", _tt(c1.t[:, :, 0:129], tc_.t[:, :, 0:129], bc(eg.t[:, 4:8], 129), ALU.mult), r=[tc_.b, eg.b], w=[c1.b])
            sc.add("act", _act(c1b.t[:, :, 0:129], c1.t[:, :, 0:129], AF.Copy), r=[c1.b], w=[c1b.b])
            sc.add("dve", _tt(dn.t[:], U3[:, :, 128], eg.t[:, 0:4], ALU.mult), r=[UPS.b, eg.b], w=[dn.b])
            sc.add("act", _act(dn.t[:], dn.t[:], AF.Abs), r=[dn.b], w=[dn.b])
            sc.add("dve", _ts(dn.t[:], dn.t[:], 1.0, None, ALU.max), r=[dn.b], w=[dn.b])
            sc.add("dve", lambda e: e.reciprocal(rr.t[:], dn.t[:]), r=[dn.b], w=[rr.b])
            sc.add("dve", _tt(rr.t[:], rr.t[:], eg.t[:, 0:4], ALU.mult), r=[rr.b, eg.b], w=[rr.b])
            sc.add("dve", _tt(ho.t[:].rearrange("p (h d) -> p h d", h=4), U3[:, :, 0:128], bc(rr.t[:], 128), ALU.mult),
                   r=[UPS.b, rr.b], w=[ho.b])
            sc.add("pool", _dma(hdst[d][r0:r0 + 128, :], ho.t[:]), r=[ho.b], chan=ho.c)

        orders = (list(range(NB)), list(range(NB - 1, -1, -1)))
        nxt = [None, None]
        for d in range(2):
            sc.add("pool", _memset(C1[d].t[:], 0.0), w=[C1[d].b])
            sc.add("pool", _memset(C1b[d].t[:], 0.0), w=[C1b[d].b])
            nxt[d] = pre(orders[d][0], d, 0)
        for j in range(NB):
            cur = list(nxt)
            if j + 1 < NB:
                for d in range(2):
                    nxt[d] = pre(orders[d][j + 1], d, j + 1)
            for d in range(2):
                main(orders[d][j], d, j, *cur[d])
        sc.emit()

    KT_s = k.dram_tmp("KT_s", [512, S], BF16)
    KR_s = k.dram_tmp("KR_s", [64, S], BF16)
    V_s = k.dram_tmp("V_s", [S, 512], BF16)
    QN_s = k.dram_tmp("QN_s", [512, S], BF16)
    QR_s = k.dram_tmp("QR_s", [4, 65, S], BF16)
    kmax_s = k.dram_tmp("kmax_s", [128, 4])
    TWO_PI = 6.283185307179586

    with contextlib.ExitStack() as st:
        stage = [Slot(k.sb(st, f"wstagec{i}", [128, 1024]), ld_ch[i]) for i in range(2)]
        gq = load_cols(st, "gq", q_norm_g, 2)
        gkv = load_cols(st, "gkv", kv_norm_g, 1)
        Wuq, Wuqb = load_w(st, stage, "Wuq", w_uq, 256, 768, gq)
        Wkv, Wkvb = load_w(st, stage, "Wkv", w_ukv, 128, 1024, gkv)
        Wkv4 = Wkv[:, 0, :].rearrange("p (h t d) -> p h t d", h=4, t=2)
        cos2 = k.sb(st, "cos2", [128, NB, 64])
        sin1 = k.sb(st, "sin1", [128, NB, 32])
        tb_ = Buf("ropetab")
        pos = k.sb(st, "pos", [128, NB])
        invf = k.sb(st, "invf", [128, 32])
        ang = k.sb(st, "ang", [128, NB, 32])
        angi = k.sb(st, "angi", [128, NB, 32], mybir.dt.int32)
        angf = k.sb(st, "angf", [128, NB, 32])
        msk = k.sb(st, "msk", [128, NB, 32])
        sc.add("pool", lambda e: e.iota(pos[:], [[128, NB]], base=0, channel_multiplier=1, allow_small_or_imprecise_dtypes=True), w=[tb_])
        sc.add("pool", lambda e: e.iota(invf[:], [[1, 32]], base=0, channel_multiplier=0, allow_small_or_imprecise_dtypes=True), r=[tb_], w=[tb_])
        sc.add("act", _act(invf[:], invf[:], AF.Exp, scale=-float(np.log(10000.0)) / 32.0), r=[tb_], w=[tb_])
        sc.add("dve", _tt(ang[:], bc(pos[:], 32), bc_mid(invf[:], NB), ALU.mult), r=[tb_], w=[tb_])
        sc.add("dve", _ts(ang[:], ang[:], 1.0 / TWO_PI, None, ALU.mult), r=[tb_], w=[tb_])
        for which in range(2):
            if which == 1:
                sc.add("dve", _ts(ang[:], ang[:], 0.25, None, ALU.add), r=[tb_], w=[tb_])
            sc.add("dve", _cp(angi[:], ang[:]), r=[tb_], w=[tb_])
            sc.add("dve", _cp(angf[:], angi[:]), r=[tb_], w=[tb_])
            sc.add("dve", _tt(angf[:], ang[:], angf[:], ALU.subtract), r=[tb_], w=[tb_])
            sc.add("dve", _ts(msk[:], angf[:], 0.5, None, ALU.is_gt), r=[tb_], w=[tb_])
            sc.add("dve", _tt(angf[:], angf[:], msk[:], ALU.subtract), r=[tb_], w=[tb_])
            sc.add("dve", _ts(msk[:], angf[:], -0.5, None, ALU.is_lt), r=[tb_], w=[tb_])
            sc.add("dve", _tt(angf[:], angf[:], msk[:], ALU.add), r=[tb_], w=[tb_])
            if which == 0:
                sc.add("act", _act(sin1[:], angf[:], AF.Sin, scale=TWO_PI * (1.0 - 1e-6)), r=[tb_], w=[tb_])
            else:
                sc.add("act", _act(cos2[:, :, 0:32], angf[:], AF.Sin, scale=TWO_PI * (1.0 - 1e-6)), r=[tb_], w=[tb_])
                sc.add("act", _act(cos2[:, :, 32:64], angf[:], AF.Sin, scale=TWO_PI * (1.0 - 1e-6)), r=[tb_], w=[tb_])
        sc.barrier()

        G3T = [Slot(k.sb(st, f"g3t{i}", [128, 4, 464]), sc.chan(f"c_g3t{i}")) for i in range(2)]
        junkc = Slot(k.sb(st, "junkc", [128, 3072], BF16))
        ssq = Slot(k.sb(st, "ssq", [128, 8]))
        ssq2 = Slot(k.sb(st, "ssq2", [128, 8]))
        rst = Slot(k.sb(st, "rst", [128, 8]))
        cqn = Slot(k.sb(st, "cqn", [128, 4, 256], BF16))
        ckvn = Slot(k.sb(st, "ckvn", [128, 4, 128], BF16))
        tA = Slot(k.sb(st, "tA", [128, 4, 64]))
        tB = Slot(k.sb(st, "tB", [128, 4, 64]))
        krb = Slot(k.sb(st, "krb", [128, 4, 64], BF16))
        sqr = Slot(k.sb(st, "sqr", [128, 4, 64]))
        kr2 = Slot(k.sb(st, "kr2", [128, 4]))
        cqT = Slot(k.sb(st, "cqT", [128, 2, 512], BF16))
        ckvT = Slot(k.sb(st, "ckvT", [128, 512], BF16))
        krT = Slot(k.sb(st, "krT", [64, 512], BF16), sc.chan("c_krT"))
        KTS = [Slot(k.sb(st, f"kts{i}", [128, 512], BF16), sc.chan(f"c_kts{i}")) for i in range(2)]
        VS = [Slot(k.sb(st, f"vs{i}", [128, 512], BF16), sc.chan(f"c_vs{i}")) for i in range(2)]
        sqk = Slot(k.sb(st, "sqk", [128, 512]))
        kn2 = Slot(k.sb(st, "kn2", [128, 4, 4]))
        kmax = Slot(k.sb(st, "kmax", [128, 4]))
        kmt = Slot(k.sb(st, "kmt", [128, 4]))
        q_sb = Slot(k.sb(st, "q_sb", [128, 4, 768]))
        qtA = Slot(k.sb(st, "qtA", [128, 4, 4, 64]))
        qtB = Slot(k.sb(st, "qtB", [128, 4, 4, 64]))
        qbn = Slot(k.sb(st, "qbn", [128, 4, 4, 128], BF16))
        qbr = Slot(k.sb(st, "qbr", [128, 4, 4, 66], BF16))
        qn2 = Slot(k.sb(st, "qn2", [128, 16]))
        qn1 = Slot(k.sb(st, "qn1", [128, 16]))
        QS = [Slot(k.sb(st, f"qs{i}", [128, 2, 512], BF16), sc.chan(f"c_qs{i}")) for i in range(2)]
        QRS = [Slot(k.sb(st, f"qrs{i}", [65, 2, 512], BF16), sc.chan(f"c_qrs{i}")) for i in range(2)]
        PB = [Slot(k.ps(st, f"pb{i}", [128, 512])) for i in range(8)]
        pbi = {"i": 0}

        def bank():
            pbi["i"] += 1
            return PB[pbi["i"] % 8]

        def bfv(slot):
            return slot.t[:].bitcast(BF16)

        sc.add("pool", _memset(kmax.t[:], 0.0), w=[kmax.b])
        sc.add("pool", _memset(qbr.t[:], 0.0), w=[qbr.b])
        q4 = q_sb.t[:].rearrange("p b (h d) -> p b h d", h=4)
        kti = 0
        for i in range(NT):
            t0 = i * 512
            g3 = G3T[i % 2]
            sc.add("sp", _dma(g3.t[:], g3_s[t0:t0 + 512, :].rearrange("(b p) c -> p b c", p=128)), w=[g3.b], chan=g3.c)
            for b in range(4):
                sc.add("act", _act(junkc.t[:, 0:256], g3.t[:, b, 16:272], AF.Square, scale=1.0 / 16.0, accum_out=ssq.t[:, b:b + 1]),
                       r=[g3.b], w=[junkc.b, ssq.b])
                sc.add("act", _act(junkc.t[:, 0:128], g3.t[:, b, 272:400], AF.Square, scale=128.0 ** -0.5, accum_out=ssq.t[:, 4 + b:5 + b]),
                       r=[g3.b], w=[junkc.b, ssq.b])
            sc.add("dve", _ts(ssq2.t[:], ssq.t[:], EPS, None, ALU.add), r=[ssq.b], w=[ssq2.b])
            sc.add("pool", _tt(rst.t[:], ssq2.t[:], mhalf[:, 0:8], ALU.pow), r=[ssq2.b], w=[rst.b])
            sc.add("dve", _tt(cqn.t[:], g3.t[:, :, 16:272], bc(rst.t[:, 0:4], 256), ALU.mult), r=[g3.b, rst.b], w=[cqn.b])
            sc.add("dve", _tt(ckvn.t[:], g3.t[:, :, 272:400], bc(rst.t[:, 4:8], 128), ALU.mult), r=[g3.b, rst.b], w=[ckvn.b])
            xk = g3.t[:, :, 400:464]
            cs, sn = cos2[:, 4 * i:4 * i + 4, :], sin1[:, 4 * i:4 * i + 4, :]
            sc.add("dve", _tt(tA.t[:], xk, cs, ALU.mult), r=[g3.b], w=[tA.b])
            sc.add("dve", _tt(tB.t[:, :, 0:32], g3.t[:, :, 432:464], sn, ALU.mult), r=[g3.b], w=[tB.b])
            sc.add("dve", _tt(tB.t[:, :, 32:64], g3.t[:, :, 400:432], sn, ALU.mult), r=[g3.b], w=[tB.b])
            sc.add("dve", _tt(krb.t[:, :, 0:32], tA.t[:, :, 0:32], tB.t[:, :, 0:32], ALU.subtract), r=[tA.b, tB.b], w=[krb.b])
            sc.add("dve", _tt(krb.t[:, :, 32:64], tA.t[:, :, 32:64], tB.t[:, :, 32:64], ALU.add), r=[tA.b, tB.b], w=[krb.b])
            sc.add("act", _act(sqr.t[:], xk, AF.Square), r=[g3.b], w=[sqr.b])
            sc.add("dve", lambda e: e.tensor_reduce(kr2.t[:], sqr.t[:], AX.X, ALU.add), r=[sqr.b], w=[kr2.b])
            pa, pb2 = bank(), bank()
            for b in range(4):
                for kc in range(2):
                    sc.add("pe", _tr(bfv(pa)[:, kc * 512 + b * 128:kc * 512 + (b + 1) * 128], cqn.t[:, b, kc * 128:(kc + 1) * 128], ident[:]),
                           r=[cqn.b], w=[pa.b])
                sc.add("pe", _tr(bfv(pb2)[:, b * 128:(b + 1) * 128], ckvn.t[:, b, :], ident[:]), r=[ckvn.b], w=[pb2.b])
                sc.add("pe", _tr(bfv(pb2)[0:64, 512 + b * 128:512 + (b + 1) * 128], krb.t[:, b, :], ident[:]), r=[krb.b], w=[pb2.b])
            sc.add("act", _act(cqT.t[:], bfv(pa).rearrange("p (a b) -> p a b", a=2), AF.Copy), r=[pa.b], w=[cqT.b])
            sc.add("dve", _cp(ckvT.t[:], bfv(pb2)[:, 0:512]), r=[pb2.b], w=[ckvT.b])
            sc.add("act", _act(krT.t[:], bfv(pb2)[0:64, 512:1024], AF.Copy), r=[pb2.b], w=[krT.b])
            sc.add("pool", _dma(KR_s[:, t0:t0 + 512], krT.t[:]), r=[krT.b], chan=krT.c)
            for h in range(4):
                pk = bank()
                sc.add("pe", _mm(pk.t[:], Wkv[:, 0, h * 256:h * 256 + 128], ckvT.t[:], True, True), r=[ckvT.b, Wkvb], w=[pk.b])
                kts = KTS[kti % 2]
                kti += 1
                e = "act" if h % 2 else "dve"
                sc.add(e, scale_cast(e, kts.t[:], pk.t[:]), r=[pk.b], w=[kts.b])
                sc.add("pool", _dma(KT_s[h * 128:(h + 1) * 128, t0:t0 + 512], kts.t[:]), r=[kts.b], chan=kts.c)
            for b in range(4):
                tok = slice(b * 128, (b + 1) * 128)
                pv = bank()
                sc.add("pe", _mm(pv.t[:].rearrange("p (h d) -> p h d", h=4), ckvT.t[:, tok], Wkv4[:, :, 1, :], True, True),
                       r=[ckvT.b, Wkvb], w=[pv.b])
                vs = VS[b % 2]
                sc.add("act", _act(vs.t[:], pv.t[:], AF.Copy), r=[pv.b], w=[vs.b])
                sc.add("pool", _dma(V_s[t0 + b * 128:t0 + (b + 1) * 128, :], vs.t[:]), r=[vs.b], chan=vs.c)
                pk = bank()
                sc.add("pe", _mm(pk.t[:].rearrange("p (h d) -> p h d", h=4), ckvT.t[:, tok], Wkv4[:, :, 0, :], True, True),
                       r=[ckvT.b, Wkvb], w=[pk.b])
                sc.add("act", _act(sqk.t[:], pk.t[:], AF.Square), r=[pk.b], w=[sqk.b])
                sc.add("dve", lambda e, b=b: e.tensor_reduce(kn2.t[:, b, :], sqk.t[:].rearrange("p (h d) -> p h d", h=4), AX.X, ALU.add),
                       r=[sqk.b], w=[kn2.b])
                pq0, pq1 = bank(), bank()
                for kc in range(2):
                    sc.add("pe", _mm(pq0.t[:], cqT.t[:, kc, tok], Wuq[:, kc, 0:512], kc == 0, kc == 1), r=[cqT.b, Wuqb], w=[pq0.b])
                for kc in range(2):
                    sc.add("pe", _mm(pq1.t[:, 0:256], cqT.t[:, kc, tok], Wuq[:, kc, 512:768], kc == 0, kc == 1), r=[cqT.b, Wuqb], w=[pq1.b])
                sc.add("act", _act(q_sb.t[:, b, 0:512], pq0.t[:], AF.Copy), r=[pq0.b], w=[q_sb.b])
                sc.add("dve", _cp(q_sb.t[:, b, 512:768], pq1.t[:, 0:256]), r=[pq1.b], w=[q_sb.b])
            sc.add("dve", _tt(kn2.t[:], kn2.t[:], bc(kr2.t[:], 4), ALU.add), r=[kn2.b, kr2.b], w=[kn2.b])
            sc.add("dve", lambda e: e.tensor_reduce(kmt.t[:], kn2.t[:].rearrange("p b h -> p h b"), AX.X, ALU.max), r=[kn2.b], w=[kmt.b])
            sc.add("dve", _tt(kmax.t[:], kmax.t[:], kmt.t[:], ALU.max), r=[kmax.b, kmt.b], w=[kmax.b])
            cs4 = bass.AP(cs.tensor, cs.offset, [list(cs.ap[0]), list(cs.ap[1]), [0, 4], list(cs.ap[2])])
            sn4 = bass.AP(sn.tensor, sn.offset, [list(sn.ap[0]), list(sn.ap[1]), [0, 4], list(sn.ap[2])])
            sc.add("dve", _tt(qtA.t[:], q4[:, :, :, 128:192], cs4, ALU.mult), r=[q_sb.b], w=[qtA.b])
            sc.add("dve", _tt(qtB.t[:, :, :, 0:32], q4[:, :, :, 160:192], sn4, ALU.mult), r=[q_sb.b], w=[qtB.b])
            sc.add("dve", _tt(qtB.t[:, :, :, 32:64], q4[:, :, :, 128:160], sn4, ALU.mult), r=[q_sb.b], w=[qtB.b])
            sc.add("dve", _tt(qbr.t[:, :, :, 0:32], qtA.t[:, :, :, 0:32], qtB.t[:, :, :, 0:32], ALU.subtract), r=[qtA.b, qtB.b], w=[qbr.b])
            sc.add("dve", _tt(qbr.t[:, :, :, 32:64], qtA.t[:, :, :, 32:64], qtB.t[:, :, :, 32:64], ALU.add), r=[qtA.b, qtB.b], w=[qbr.b])
            sc.add("dve", _cp(qbn.t[:], q4[:, :, :, 0:128]), r=[q_sb.b], w=[qbn.b])
            sc.add("act", _act(junkc.t[:], q_sb.t[:].rearrange("p b c -> p (b c)"), AF.Square), r=[q_sb.b], w=[junkc.b])
            sc.add("dve", lambda e: e.tensor_reduce(qn2.t[:], junkc.t[:].rearrange("p (g d) -> p g d", g=16), AX.X, ALU.add),
                   r=[junkc.b], w=[qn2.b])
            sc.add("pool", _tt(qn1.t[:], qn2.t[:], mhalf[:, 0:16], ALU.pow), r=[qn2.b], w=[qn1.b])
            sc.add("dve", _tt(qn1.t[:], qn1.t[:], qn2.t[:], ALU.mult), r=[qn1.b, qn2.b], w=[qn1.b])
            sc.add("dve", _ts(qbr.t[:, :, :, 64], qn1.t[:].rearrange("p (b h) -> p b h", b=4), -1.01, None, ALU.mult),
                   r=[qn1.b], w=[qbr.b])
            for hp in range(2):
                pn, pr = bank(), bank()
                for hh in range(2):
                    h = 2 * hp + hh
                    for b in range(4):
                        sc.add("pe", _tr(bfv(pn)[:, hh * 512 + b * 128:hh * 512 + (b + 1) * 128], qbn.t[:, b, h, :], ident[:]), r=[qbn.b], w=[pn.b])
                        sc.add("pe", _tr(bfv(pr)[0:65, hh * 512 + b * 128:hh * 512 + (b + 1) * 128], qbr.t[:, b, h, 0:65], ident[:]), r=[qbr.b], w=[pr.b])
                qs, qrs = QS[hp], QRS[hp]
                sc.add("act", _act(qs.t[:], bfv(pn).rearrange("p (a b) -> p a b", a=2), AF.Copy), r=[pn.b], w=[qs.b])
                sc.add("dve", _cp(qrs.t[:], bfv(pr)[0:65, :].rearrange("p (a b) -> p a b", a=2)), r=[pr.b], w=[qrs.b])
                sc.add("pool", _dma(QN_s.rearrange("(h p) s -> p h s", p=128)[:, 2 * hp:2 * hp + 2, t0:t0 + 512], qs.t[:]), r=[qs.b], chan=qs.c)
                sc.add("pool", _dma(QR_s.rearrange("h p s -> p h s")[:, 2 * hp:2 * hp + 2, t0:t0 + 512], qrs.t[:]), r=[qrs.b], chan=qrs.c)
        kmo = Slot(k.sb(st, "kmo", [128, 4]), sc.chan("c_kmo"))
        sc.add("dve", _cp(kmo.t[:], kmax.t[:]), r=[kmax.b], w=[kmo.b])
        sc.add("pool", _dma(kmax_s[:, :], kmo.t[:]), r=[kmo.b], chan=kmo.c)
        sc.emit()

    with contextlib.ExitStack() as st:
        KT = Slot(k.sb(st, "KT", [128, 4, S], BF16), sc.chan("c_KT"))
        KR = Slot(k.sb(st, "KR", [65, S], BF16), sc.chan("c_KR"))
        VR = Slot(k.sb(st, "VR", [128, NB, 512], BF16), sc.chan("c_VR"))
        kml = Slot(k.sb(st, "kml", [128, 4]), sc.chan("c_kml"))
        km1 = Slot(k.sb(st, "km1", [1, 4]))
        kmx = Slot(k.sb(st, "kmx", [128, 4]))
        phalf = Slot(k.sb(st, "phalf", [128, 4]))
        SPB = [Slot(k.ps(st, f"spb{i}", [128, 512])) for i in range(3)]
        OPB = [Slot(k.ps(st, f"opb{i}", [128, 512])) for i in range(2)]
        RSB = Slot(k.ps(st, "rsb", [128, 512]))
        NPT = 10
        PT = [Slot(k.sb(st, f"pt{i}", [128, 512], BF16)) for i in range(NPT)]
        RS = [Slot(k.ps(st, f"rs{i}", [128, 512])) for i in range(1)]
        rs_sb = Slot(k.sb(st, "rs_sb", [128, 512]))
        ones_b = Slot(k.sb(st, "ones_b", [128, 32], BF16))
        inv32 = Slot(k.sb(st, "inv32", [128, 128]))
        sc.add("pool", _memset(ones_b.t[:], 1.0), w=[ones_b.b])
        sc.add("pool", _memset(inv32.t[:], 1.0 / 32.0), w=[inv32.b])
        QN = [Slot(k.sb(st, f"qnt{i}", [128, 512], BF16), sc.chan(f"c_qn{i}")) for i in range(2)]
        QR = [Slot(k.sb(st, f"qrt{i}", [65, 512], BF16), sc.chan(f"c_qr{i}")) for i in range(2)]
        rinv = Slot(k.sb(st, "rinv", [128, 512]))
        YO = [Slot(k.sb(st, f"yo{i}", [128, 512], BF16), sc.chan(f"c_yo{i}")) for i in range(2)]
        sc.add("sp", _dma(KT.t[:], KT_s.rearrange("(h p) s -> p h s", p=128)), w=[KT.b], chan=KT.c)
        sc.add("pool", _memset(KR.t[64:65, :], 1.0), w=[KR.b])
        sc.add("sp", _dma(KR.t[0:64, :], KR_s[:, :]), w=[KR.b], chan=KR.c)
        sc.add("sp", _dma(VR.t[:], V_s.rearrange("(c p) d -> p c d", p=128)), w=[VR.b], chan=VR.c)
        sc.add("sp", _dma(kml.t[:], kmax_s[:, :]), w=[kml.b], chan=kml.c)
        sc.add("pool", _memset(phalf.t[:], 0.5), w=[phalf.b])
        sc.add("pool", lambda e: e.tensor_reduce(km1.t[:], kml.t[:], AX.C, ALU.max), r=[kml.b], w=[km1.b])
        sc.add("pe", _mm(RSB.t[:, 0:4], ones_f[0:1, :], km1.t[:], True, True), r=[km1.b], w=[RSB.b])
        sc.add("dve", _cp(kmx.t[:], RSB.t[:, 0:4]), r=[RSB.b], w=[kmx.b])
        sc.add("pool", _tt(kmx.t[:], kmx.t[:], phalf.t[:], ALU.pow), r=[kmx.b, phalf.b], w=[kmx.b])
        scale = 192.0 ** -0.5
        it = 0
        pti = 0
        for h in range(4):
            for j in range(NT):
                qn, qr = QN[it % 2], QR[it % 2]
                opb, yo, rs = OPB[it % 2], YO[it % 2], RS[0]
                it += 1
                sc.add("sp", _dma(qn.t[:], QN_s[h * 128:(h + 1) * 128, j * 512:(j + 1) * 512]), w=[qn.b], chan=qn.c)
                sc.add("sp", _dma(qr.t[:], QR_s[h, :, j * 512:(j + 1) * 512]), w=[qr.b], chan=qr.c)
                sc.add("dve", _ts(qr.t[64:65, :], qr.t[64:65, :], kmx.t[64:65, h:h + 1], None, ALU.mult), r=[qr.b, kmx.b], w=[qr.b])

                def qk(kc):
                    sp_ = SPB[kc % 3]
                    ks = slice(kc * 128, (kc + 1) * 128)
                    sc.add("pe", _mm(sp_.t[:], KT.t[:, h, ks], qn.t[:], True, False), r=[KT.b, qn.b], w=[sp_.b])
                    sc.add("pe", _mm(sp_.t[:], KR.t[0:65, ks], qr.t[0:65, :], False, True), r=[KR.b, qr.b], w=[sp_.b])

                qk(0)
                if NB > 1:
                    qk(1)
                grp = []
                for kc in range(NB):
                    if kc + 2 < NB:
                        qk(kc + 2)
                    sp_ = SPB[kc % 3]
                    pt = PT[pti % NPT]
                    pti += 1
                    sc.add("act", _act(pt.t[:], sp_.t[:], AF.Exp, scale=scale), r=[sp_.b], w=[pt.b])
                    sc.add("pe", _mm(opb.t[:], VR.t[:, kc, h * 128:(h + 1) * 128], pt.t[:], kc == 0, kc == NB - 1),
                           r=[VR.b, pt.b], w=[opb.b])
                    grp.append(pt)
                    if len(grp) == 4:
                        for r_, ptr in enumerate(grp):
                            sc.add("pe", lambda e, r_=r_, ptr=ptr, kc=kc: e.matmul(rs.t[32 * r_:32 * r_ + 32, :], ones_b.t[:, 0:32], ptr.t[:],
                                                                               start=(kc == 3), stop=(kc == NB - 1),
                                                                               tile_position=(0, 32 * r_)),
                                   r=[ptr.b, ones_b.b], w=[rs.b])
                        grp = []
                sc.add("dve", _cp(rs_sb.t[:], rs.t[:]), r=[rs.b], w=[rs_sb.b])
                sc.add("pe", _mm(RSB.t[:], inv32.t[:], rs_sb.t[:], True, True), r=[rs_sb.b, inv32.b], w=[RSB.b])
                sc.add("dve", lambda e: e.reciprocal(rinv.t[:], RSB.t[:]), r=[RSB.b], w=[rinv.b])
                sc.add("dve", _tt(yo.t[:], opb.t[:], rinv.t[:], ALU.mult), r=[opb.b, rinv.b], w=[yo.b])
                sc.add("pool", _dma(yT_s[512 + h * 128:512 + (h + 1) * 128, j * 512:(j + 1) * 512], yo.t[:]), r=[yo.b], chan=yo.c)
        sc.emit()

    h1_s = k.dram_tmp("h1_s", [S, D])
    xn2T_s = k.dram_tmp("xn2T_s", [D, S + 2], BF16)
    h2_s = k.dram_tmp("h2_s", [S, D])
    yT3 = yT_s.rearrange("(g p) s -> p g s", p=128)
    xn2T3 = xn2T_s.rearrange("(g p) s -> p g s", p=128)

    def rms_transpose(xt, ss, ss2, rstd, junk, XB, xn, TBs, gain_scale=1.0 / 32.0):
        for b in range(4):
            sc.add("act", _act(junk.t[:], xt.t[:, b, :], AF.Square, scale=gain_scale, accum_out=ss.t[:, b:b + 1]),
                   r=[xt.b], w=[junk.b, ss.b])
        sc.add("dve", _ts(ss2.t[:], ss.t[:], EPS, None, ALU.add), r=[ss.b], w=[ss2.b])
        sc.add("pool", _tt(rstd.t[:], ss2.t[:], mhalf[:, 0:4], ALU.pow), r=[ss2.b], w=[rstd.b])
        for b in range(4):
            e = "dve" if b % 2 else "act"
            sc.add(e, scale_cast(e, XB.t[:, b, :], xt.t[:, b, :], rstd.t[:, b:b + 1]), r=[xt.b, rstd.b], w=[XB.b])
        for j in range(4):
            tb = TBs[j % 2]
            for kk in range(2):
                kc = 2 * j + kk
                for b in range(4):
                    sc.add("pe", _tr(tb.t[:, kk * 512 + b * 128:kk * 512 + (b + 1) * 128],
                                     XB.t[:, b, kc * 128:(kc + 1) * 128], ident[:]), r=[XB.b], w=[tb.b])
            e = "dve" if j % 2 else "act"
            sc.add(e, scale_cast(e, xn.t[:, 2 * j:2 * j + 2, :], tb.t[:].rearrange("p (a b) -> p a b", a=2)),
                   r=[tb.b], w=[xn.b])

    with contextlib.ExitStack() as st:
        stage = [Slot(k.sb(st, f"wstaged{i}", [128, 1024]), ld_ch[i]) for i in range(2)]
        Wout, Woutb = load_w(st, stage, "Wout", w_out, D, D)
        normg = load_bcast(st, "normg", mlstm_norm_g, 512)
        zt = Slot(k.sb(st, "zt", [128, 8, 2], BF16), sc.chan("c_zt"))
        sc.add("pool", _memset(zt.t[:], 0.0), w=[zt.b])
        sc.add("pool", _dma(xn2T3[:, :, 0:1], zt.t[:, :, 0:1], allow_slow_non_contiguous=True), r=[zt.b], chan=zt.c)
        sc.add("pool", _dma(xn2T3[:, :, S + 1:S + 2], zt.t[:, :, 1:2], allow_slow_non_contiguous=True), r=[zt.b], chan=zt.c)
        HFT = [Slot(k.sb(st, f"hft{i}", [128, 4, 512], BF16), sc.chan(f"c_hft{i}")) for i in range(2)]
        HBT = [Slot(k.sb(st, f"hbt{i}", [128, 4, 512], BF16), sc.chan(f"c_hbt{i}")) for i in range(2)]
        SOT = [Slot(k.sb(st, f"sot{i}", [128, 4, 512], BF16), sc.chan(f"c_sot{i}")) for i in range(2)]
        HS = Slot(k.sb(st, "hsd", [128, 4, 512]))
        SG = Slot(k.sb(st, "sgd", [128, 4, 512]))
        sqd = Slot(k.sb(st, "sqd", [128, 4, 512], BF16))
        ssn = Slot(k.sb(st, "ssnd", [128, 16]))
        rsn = Slot(k.sb(st, "rsnd", [128, 16]))
        YB = [Slot(k.sb(st, f"ybd{i}", [128, 4, 512], BF16)) for i in range(2)]
        YAT = [Slot(k.sb(st, f"yat{i}", [128, 4, 512], BF16)) for i in range(2)]
        YTT = [Slot(k.sb(st, f"ytt{i}", [128, 4, 512], BF16), sc.chan(f"c_ytt{i}")) for i in range(2)]
        XT = [Slot(k.sb(st, f"xtd{i}", [128, 4, D]), sc.chan(f"c_xtd{i}")) for i in range(3)]
        XB = Slot(k.sb(st, "xbd", [128, 4, D], BF16))
        XN = [Slot(k.sb(st, f"xnd{i}", [128, 8, 512], BF16), sc.chan(f"c_xnd{i}")) for i in range(2)]
        junk = Slot(k.sb(st, "junkd", [128, D], BF16))
        ss = Slot(k.sb(st, "ssd", [128, 4]))
        ss2 = Slot(k.sb(st, "ss2d", [128, 4]))
        rstd = Slot(k.sb(st, "rstdd", [128, 4]))
        TBs = [Slot(k.ps(st, f"tbd{i}", [128, 1024], BF16)) for i in range(2)]
        MB = [Slot(k.ps(st, f"mbd{i}", [128, 512])) for i in range(6)]
        cntD = {"mbi": 0}

        def combine(i):
            t0 = i * 512
            hft, hbt, sot, yb, ytt, xt = HFT[i % 2], HBT[i % 2], SOT[i % 2], YB[i % 2], YTT[i % 2], XT[i % 3]
            tv = lambda ap: ap[t0:t0 + 512, :].rearrange("(b p) d -> p b d", p=128)
            sc.add("sp", _dma(hft.t[:], tv(hf_s)), w=[hft.b], chan=hft.c)
            sc.add("sp", _dma(hbt.t[:], tv(hb_s)), w=[hbt.b], chan=hbt.c)
            sc.add("sp", _dma(sot.t[:], tv(so_s)), w=[sot.b], chan=sot.c)
            sc.add("sp", _dma(ytt.t[:], yT3[:, 4:8, t0:t0 + 512]), w=[ytt.b], chan=ytt.c)
            sc.add("sp", _dma(xt.t[:], tv(x)), w=[xt.b], chan=xt.c)
            sc.add("dve", _tt(HS.t[:], hft.t[:], hbt.t[:], ALU.add), r=[hft.b, hbt.b], w=[HS.b])
            sc.add("act", _act(sqd.t[:], HS.t[:], AF.Square, scale=128.0 ** -0.5), r=[HS.b], w=[sqd.b])
            sc.add("dve", lambda e: e.tensor_reduce(ssn.t[:], sqd.t[:].rearrange("p b (h d) -> p (b h) d", h=4), AX.X, ALU.add),
                   r=[sqd.b], w=[ssn.b])
            sc.add("dve", _ts(ssn.t[:], ssn.t[:], EPS, None, ALU.add), r=[ssn.b], w=[ssn.b])
            sc.add("pool", _tt(rsn.t[:], ssn.t[:], mhalf[:, 0:16], ALU.pow), r=[ssn.b], w=[rsn.b])
            h16 = HS.t[:].rearrange("p b (h d) -> p (b h) d", h=4)
            sc.add("dve", _tt(h16, h16, bc(rsn.t[:], 128), ALU.mult), r=[HS.b, rsn.b], w=[HS.b])
            sc.add("pool", _tt(SG.t[:], sot.t[:], bc_mid(normg[0][:], 4), ALU.mult), r=[sot.b, normg[1]], w=[SG.b])
            sc.add("dve", _tt(yb.t[:], HS.t[:], SG.t[:], ALU.mult), r=[HS.b, SG.b], w=[yb.b])

        def normelt(i):
            t0 = i * 512
            xt = XT[i % 3]
            sc.add("sp", _dma(h1_s[t0:t0 + 512, :].rearrange("(b p) d -> p b d", p=128), xt.t[:]), r=[xt.b], chan=xt.c)
            for b in range(4):
                sc.add("act", _act(junk.t[:], xt.t[:, b, :], AF.Square, scale=1.0 / 32.0, accum_out=ss.t[:, b:b + 1]),
                       r=[xt.b], w=[junk.b, ss.b])
            sc.add("dve", _ts(ss2.t[:], ss.t[:], EPS, None, ALU.add), r=[ss.b], w=[ss2.b])
            sc.add("pool", _tt(rstd.t[:], ss2.t[:], mhalf[:, 0:4], ALU.pow), r=[ss2.b], w=[rstd.b])
            for b in range(4):
                e = "dve" if b % 2 else "act"
                sc.add(e, scale_cast(e, XB.t[:, b, :], xt.t[:, b, :], rstd.t[:, b:b + 1]), r=[xt.b, rstd.b], w=[XB.b])

        def mmstage(i):
            yb, yat, ytt, xt = YB[i % 2], YAT[i % 2], YTT[i % 2], XT[i % 3]
            for hp in range(2):
                tb = TBs[hp]
                for hh in range(2):
                    h = 2 * hp + hh
                    for b in range(4):
                        sc.add("pe", _tr(tb.t[:, hh * 512 + b * 128:hh * 512 + (b + 1) * 128], yb.t[:, b, h * 128:(h + 1) * 128], ident[:]),
                               r=[yb.b], w=[tb.b])
                e = "dve" if hp else "act"
                sc.add(e, scale_cast(e, yat.t[:, 2 * hp:2 * hp + 2, :], tb.t[:].rearrange("p (a b) -> p a b", a=2)), r=[tb.b], w=[yat.b])
            mbi = cntD["mbi"]
            for b in range(4):
                for half in range(2):
                    pm = MB[mbi % 6]
                    mbi += 1
                    for kc in range(8):
                        src = yat if kc < 4 else ytt
                        sc.add("pe", _mm(pm.t[:], src.t[:, kc % 4, b * 128:(b + 1) * 128], Wout[:, kc, half * 512:(half + 1) * 512],
                                         kc == 0, kc == 7), r=[src.b, Woutb], w=[pm.b])
                    sc.add("dve", _tt(xt.t[:, b, half * 512:(half + 1) * 512], pm.t[:], xt.t[:, b, half * 512:(half + 1) * 512], ALU.add),
                           r=[pm.b, xt.b], w=[xt.b])
            cntD["mbi"] = mbi

        def trstage(i):
            t0 = i * 512
            xn = XN[i % 2]
            for j in range(4):
                tb = TBs[j % 2]
                for kk in range(2):
                    kc = 2 * j + kk
                    for b in range(4):
                        sc.add("pe", _tr(tb.t[:, kk * 512 + b * 128:kk * 512 + (b + 1) * 128],
                                         XB.t[:, b, kc * 128:(kc + 1) * 128], ident[:]), r=[XB.b], w=[tb.b])
                e = "dve" if j % 2 else "act"
                sc.add(e, scale_cast(e, xn.t[:, 2 * j:2 * j + 2, :], tb.t[:].rearrange("p (a b) -> p a b", a=2)),
                       r=[tb.b], w=[xn.b])
            sc.add("sp", _dma(xn2T3[:, :, 1 + t0:1 + t0 + 512], xn.t[:]), r=[xn.b], chan=xn.c)

        combine(0)
        for i in range(NT + 1):
            if i + 1 < NT:
                combine(i + 1)
            if i >= 1:
                normelt(i - 1)
            if i < NT:
                mmstage(i)
            if i >= 1:
                trstage(i - 1)
        sc.emit()

    TT_ = 256
    with contextlib.ExitStack() as st:
        stage = [Slot(k.sb(st, f"wstagee{i}", [128, 1408]), ld_ch[i]) for i in range(2)]
        gffn = load_cols(st, "gffn", ln_ffn_g, 8)
        Wup, Wupb = load_w(st, stage, "Wup", w_up, D, 2 * D_FF, gffn)
        Wdn, Wdnb = load_w(st, stage, "Wdn", w_down, D_FF, D)
        fw = k.sb(st, "fw", [128, 44, 3])
        fwb = Buf("fw")
        for tap in range(3):
            sc.add("sp", _dma(fw[:, :, tap], conv_ffn_w[tap].rearrange("(g p) -> p g", p=128),
                              allow_slow_non_contiguous=True), w=[Buf()], chan=sc.chan(f"c_fw{tap}"))
        fb = load_cols(st, "fb", conv_ffn_b, 44)
        sc.barrier()
        XS = [Slot(k.sb(st, f"xs{i}", [128, 8, TT_ + 2], BF16), sc.chan(f"c_xs{i}")) for i in range(2)]
        H1 = [Slot(k.sb(st, f"h1t{i}", [128, 2, D]), sc.chan(f"c_h1t{i}")) for i in range(2)]
        AT = [Slot(k.sb(st, f"at{i}", [128, 22, TT_], BF16)) for i in range(2)]
        CG = [Slot(k.sb(st, f"cg{i}", [128, TT_])) for i in range(2)]
        CV = [Slot(k.sb(st, f"cv{i}", [128, TT_])) for i in range(2)]
        SG = [Slot(k.sb(st, f"sg{i}", [128, TT_])) for i in range(2)]
        MB = [Slot(k.ps(st, f"mbe{i}", [128, 512])) for i in range(8)]
        mbi = 0
        for i in range(S // TT_):
            t0 = i * TT_
            xs, h1, at = XS[i % 2], H1[i % 2], AT[i % 2]
            sc.add("sp", _dma(xs.t[:], xn2T3[:, :, t0:t0 + TT_ + 2]), w=[xs.b], chan=xs.c)
            sc.add("sp", _dma(h1.t[:], h1_s[t0:t0 + TT_, :].rearrange("(b p) d -> p b d", p=128)), w=[h1.b], chan=h1.c)
            for g in range(22):
                res = []
                for which, (gi, dst) in enumerate(((g, CG[g % 2]), (22 + g, CV[g % 2]))):
                    pm = MB[mbi % 8]
                    mbi += 1
                    for kc in range(8):
                        sc.add("pe", _mm(pm.t[:, 0:TT_ + 2], Wup[:, kc, gi * 128:(gi + 1) * 128], xs.t[:, kc, :], kc == 0, kc == 7),
                               r=[xs.b, Wupb], w=[pm.b])
                    sc.add("act", _act(dst.t[:], pm.t[:, 0:TT_], AF.Identity, scale=fw[:, gi, 0:1], bias=fb[0][:, gi:gi + 1]),
                           r=[pm.b], w=[dst.b])
                    sc.add("dve", _stt(dst.t[:], pm.t[:, 1:TT_ + 1], fw[:, gi, 1:2], dst.t[:], ALU.mult, ALU.add), r=[pm.b, dst.b], w=[dst.b])
                    sc.add("dve", _stt(dst.t[:], pm.t[:, 2:TT_ + 2], fw[:, gi, 2:3], dst.t[:], ALU.mult, ALU.add), r=[pm.b, dst.b], w=[dst.b])
                cg, cv, sg = CG[g % 2], CV[g % 2], SG[g % 2]
                sc.add("act", _act(sg.t[:], cg.t[:], AF.Silu), r=[cg.b], w=[sg.b])
                sc.add("pool", _tt(at.t[:, g, :], sg.t[:], cv.t[:], ALU.mult), r=[sg.b, cv.b], w=[at.b])
            for b in range(TT_ // 128):
                for half in range(2):
                    pm = MB[mbi % 8]
                    mbi += 1
                    for g in range(22):
                        sc.add("pe", _mm(pm.t[:], at.t[:, g, b * 128:(b + 1) * 128], Wdn[:, g, half * 512:(half + 1) * 512], g == 0, g == 21),
                               r=[at.b, Wdnb], w=[pm.b])
                    sc.add("dve", _tt(h1.t[:, b, half * 512:(half + 1) * 512], pm.t[:], h1.t[:, b, half * 512:(half + 1) * 512], ALU.add),
                           r=[pm.b, h1.b], w=[h1.b])
            sc.add("pool", _dma(h2_s[t0:t0 + TT_, :].rearrange("(b p) d -> p b d", p=128), h1.t[:]), r=[h1.b], chan=h1.c)
        sc.emit()

    with contextlib.ExitStack() as st:
        stage = [Slot(k.sb(st, f"wstagef{i}", [128, 1024]), ld_ch[i]) for i in range(2)]
        gple = load_cols(st, "gple", ple_norm_g, 8)
        Wg, Wgb = load_w(st, stage, "Wg", w_ple_gate, D, D, gple)
        Wp, Wpb = load_w(st, stage, "Wp", w_ple_proj, 256, D)
        postg = load_bcast(st, "postg", ple_post_g, D)
        fing = load_bcast(st, "fing", final_g, D)
        sc.barrier()
        XT = [Slot(k.sb(st, f"xtf{i}", [128, 4, D]), sc.chan(f"c_xtf{i}")) for i in range(3)]
        PTL = [Slot(k.sb(st, f"ptl{i}", [128, 4, 256]), sc.chan(f"c_ptl{i}")) for i in range(2)]
        XBF = [Slot(k.sb(st, f"xbf{i}", [128, 4, D], BF16)) for i in range(2)]
        PBF = [Slot(k.sb(st, f"pbf{i}", [128, 4, 256], BF16)) for i in range(2)]
        XNF = [Slot(k.sb(st, f"xnf{i}", [128, 8, 512], BF16)) for i in range(2)]
        PTTF = [Slot(k.sb(st, f"ptt{i}", [128, 2, 512], BF16)) for i in range(2)]
        junk = Slot(k.sb(st, "junkf", [128, D], BF16))
        ss = Slot(k.sb(st, "ssf", [128, 4]))
        ss2 = Slot(k.sb(st, "ss2f", [128, 4]))
        rstd = Slot(k.sb(st, "rstdf", [128, 4]))
        ssb = Slot(k.sb(st, "ssb", [128, 2]))
        ssb2 = Slot(k.sb(st, "ssb2", [128, 2]))
        rsb2 = Slot(k.sb(st, "rsb2", [128, 2]))
        SGM = [Slot(k.sb(st, f"sgm{i}", [128, D])) for i in range(2)]
        PJ = [Slot(k.sb(st, f"pj{i}", [128, D])) for i in range(2)]
        OT = [Slot(k.sb(st, f"ot{i}", [128, D]), sc.chan(f"c_ot{i}")) for i in range(3)]
        TBs = [Slot(k.ps(st, f"tbf{i}", [128, 1024], BF16)) for i in range(2)]
        MB = [Slot(k.ps(st, f"mbf{i}", [128, 512])) for i in range(6)]
        cntF = {"mbi": 0, "bi": 0}

        def f_stage1a(i):
            t0 = i * 512
            xt, ptl, XB, PBf = XT[i % 3], PTL[i % 2], XBF[i % 2], PBF[i % 2]
            sc.add("sp", _dma(xt.t[:], h2_s[t0:t0 + 512, :].rearrange("(b p) d -> p b d", p=128)), w=[xt.b], chan=xt.c)
            sc.add("sp", _dma(ptl.t[:], p_in[t0:t0 + 512, :].rearrange("(b p) d -> p b d", p=128)), w=[ptl.b], chan=ptl.c)
            for b in range(4):
                sc.add("act", _act(junk.t[:], xt.t[:, b, :], AF.Square, scale=1.0 / 32.0, accum_out=ss.t[:, b:b + 1]),
                       r=[xt.b], w=[junk.b, ss.b])
            sc.add("dve", _ts(ss2.t[:], ss.t[:], EPS, None, ALU.add), r=[ss.b], w=[ss2.b])
            sc.add("pool", _tt(rstd.t[:], ss2.t[:], mhalf[:, 0:4], ALU.pow), r=[ss2.b], w=[rstd.b])
            for b in range(4):
                e = "dve" if b % 2 else "act"
                sc.add(e, scale_cast(e, XB.t[:, b, :], xt.t[:, b, :], rstd.t[:, b:b + 1]), r=[xt.b, rstd.b], w=[XB.b])
            sc.add("act", _act(PBf.t[:], ptl.t[:], AF.Copy), r=[ptl.b], w=[PBf.b])

        def f_stage1b(i):
            XN, PTT, XB, PBf = XNF[i % 2], PTTF[i % 2], XBF[i % 2], PBF[i % 2]
            for j in range(4):
                tb = TBs[j % 2]
                for kk in range(2):
                    kc = 2 * j + kk
                    for b in range(4):
                        sc.add("pe", _tr(tb.t[:, kk * 512 + b * 128:kk * 512 + (b + 1) * 128],
                                         XB.t[:, b, kc * 128:(kc + 1) * 128], ident[:]), r=[XB.b], w=[tb.b])
                e = "dve" if j % 2 else "act"
                sc.add(e, scale_cast(e, XN.t[:, 2 * j:2 * j + 2, :], tb.t[:].rearrange("p (a b) -> p a b", a=2)),
                       r=[tb.b], w=[XN.b])
            tb = TBs[0]
            for kc in range(2):
                for b in range(4):
                    sc.add("pe", _tr(tb.t[:, kc * 512 + b * 128:kc * 512 + (b + 1) * 128], PBf.t[:, b, kc * 128:(kc + 1) * 128], ident[:]),
                           r=[PBf.b], w=[tb.b])
            sc.add("act", _act(PTT.t[:], tb.t[:].rearrange("p (a b) -> p a b", a=2), AF.Copy), r=[tb.b], w=[PTT.b])

        RN = 4
        SGM3 = SGM + [Slot(k.sb(st, f"sgm{i}", [128, D])) for i in range(2, RN)]
        PJ3 = PJ + [Slot(k.sb(st, f"pj{i}", [128, D])) for i in range(2, RN)]
        SSB = [Slot(k.sb(st, f"ssbr{i}", [128, 2])) for i in range(RN)]
        RSB2 = [Slot(k.sb(st, f"rsbr{i}", [128, 2])) for i in range(RN)]
        junk2 = Slot(k.sb(st, "junkf2", [128, D], BF16))

        def blk(n):
            i, b = divmod(n, 4)
            return i, b, XT[i % 3], SGM3[n % RN], PJ3[n % RN], SSB[n % RN], RSB2[n % RN], OT[n % 3]

        def f_a12(n):
            i, b, xt, sgm, pj, ssb_, rsb_, ot = blk(n)
            XN, PTT = XNF[i % 2], PTTF[i % 2]
            mbi = cntF["mbi"]
            tok = slice(b * 128, (b + 1) * 128)
            for half in range(2):
                hs = slice(half * 512, (half + 1) * 512)
                pm = MB[mbi % 6]
                mbi += 1
                for kc in range(8):
                    sc.add("pe", _mm(pm.t[:], XN.t[:, kc, tok], Wg[:, kc, hs], kc == 0, kc == 7), r=[XN.b, Wgb], w=[pm.b])
                sc.add("act", _act(sgm.t[:, hs], pm.t[:], AF.Sigmoid), r=[pm.b], w=[sgm.b])
                pm = MB[mbi % 6]
                mbi += 1
                for kc in range(2):
                    sc.add("pe", _mm(pm.t[:], PTT.t[:, kc, tok], Wp[:, kc, hs], kc == 0, kc == 1), r=[PTT.b, Wpb], w=[pm.b])
                sc.add("act", _act(pj.t[:, hs], pm.t[:], AF.Copy), r=[pm.b], w=[pj.b])
            cntF["mbi"] = mbi
            sc.add("act", _act(junk.t[:], pj.t[:], AF.Square, scale=1.0 / 32.0, accum_out=ssb_.t[:, 0:1]), r=[pj.b], w=[junk.b, ssb_.b])

        def f_d12(n):
            i, b, xt, sgm, pj, ssb_, rsb_, ot = blk(n)
            sc.add("dve", _ts(ssb_.t[:, 0:1], ssb_.t[:, 0:1], EPS, None, ALU.add), r=[ssb_.b], w=[ssb_.b])
            sc.add("pool", _tt(rsb_.t[:, 0:1], ssb_.t[:, 0:1], mhalf[:, 0:1], ALU.pow), r=[ssb_.b], w=[rsb_.b])
            sc.add("dve", _stt(sgm.t[:], sgm.t[:], rsb_.t[:, 0:1], postg[0][:], ALU.mult, ALU.mult), r=[sgm.b, rsb_.b, postg[1]], w=[sgm.b])
            sc.add("dve", _tt(pj.t[:], pj.t[:], sgm.t[:], ALU.mult), r=[pj.b, sgm.b], w=[pj.b])
            sc.add("dve", _tt(pj.t[:], pj.t[:], xt.t[:, b, :], ALU.add), r=[pj.b, xt.b], w=[pj.b])

        def f_a3(n):
            i, b, xt, sgm, pj, ssb_, rsb_, ot = blk(n)
            sc.add("act", _act(junk2.t[:], pj.t[:], AF.Square, scale=1.0 / 32.0, accum_out=ssb_.t[:, 1:2]), r=[pj.b], w=[junk2.b, ssb_.b])

        def f_d3(n):
            i, b, xt, sgm, pj, ssb_, rsb_, ot = blk(n)
            t0 = i * 512
            sc.add("dve", _ts(ssb_.t[:, 1:2], ssb_.t[:, 1:2], EPS, None, ALU.add), r=[ssb_.b], w=[ssb_.b])
            sc.add("pool", _tt(rsb_.t[:, 1:2], ssb_.t[:, 1:2], mhalf[:, 0:1], ALU.pow), r=[ssb_.b], w=[rsb_.b])
            sc.add("dve", _stt(ot.t[:], pj.t[:], rsb_.t[:, 1:2], fing[0][:], ALU.mult, ALU.mult), r=[pj.b, rsb_.b, fing[1]], w=[ot.b])
            sc.add("sp", _dma(out[t0 + b * 128:t0 + (b + 1) * 128, :], ot.t[:]), r=[ot.b], chan=ot.c)

        f_stage1a(0)
        f_stage1b(0)
        if NT > 1:
            f_stage1a(1)
        NBLK = NT * 4
        for n in range(-2, NBLK + 1):
            if 0 <= n + 2 < NBLK:
                f_a12(n + 2)
                i2, b2 = divmod(n + 2, 4)
                if b2 == 1:
                    if i2 + 1 < NT:
                        f_stage1b(i2 + 1)
                    if i2 + 2 < NT:
                        f_stage1a(i2 + 2)
            if 0 <= n + 1 < NBLK:
                f_d12(n + 1)
            if 0 <= n < NBLK:
                f_a3(n)
            if 0 <= n - 1 < NBLK:
                f_d3(n - 1)
        sc.emit()

    k.final_wait = None
    return k


def finish(k):
    return k.nc


_W_NAMES = ["ln_mix_g", "w_in", "b_gates", "conv_qk_w", "conv_qk_b", "mlstm_norm_g", "q_norm_g", "w_uq", "kv_norm_g",
            "w_ukv", "w_out", "ln_ffn_g", "w_up", "conv_ffn_w", "conv_ffn_b", "w_down", "ple_norm_g", "w_ple_gate",
            "w_ple_proj", "ple_post_g"]


def kernel(**inputs):
    x = np.asarray(inputs["x"])
    p = np.asarray(inputs["p"])
    B, S, _ = x.shape
    nc = finish(build(S))
    shared = {n: np.ascontiguousarray(np.asarray(inputs[n])[0], dtype=np.float32) for n in _W_NAMES}
    shared["final_g"] = np.ascontiguousarray(np.asarray(inputs["final_g"]), dtype=np.float32)
    in_maps = []
    for b in range(B):
        m = dict(shared)
        m["x"] = np.ascontiguousarray(x[b], dtype=np.float32)
        m["p"] = np.ascontiguousarray(p[0, b], dtype=np.float32)
        in_maps.append(m)
    res = run_bass_kernel_spmd(nc, in_maps, core_ids=list(range(B)))
    return np.stack([np.asarray(r["out"]) for r in res.results], axis=0).astype(np.float32)
```

```python
import contextlib
import numpy as np
import concourse.bass as bass
import concourse.mybir as mybir
from concourse.bass_utils import run_bass_kernel_spmd

F32 = mybir.dt.float32
BF16 = mybir.dt.bfloat16
AF = mybir.ActivationFunctionType
ALU = mybir.AluOpType
AX = mybir.AxisListType

D = 1024
NH = 4
IN_COLS = 2512
D_FF = 2816
EPS = 1e-6
SEM_MAX = 30000


class Buf:
    __slots__ = ("name", "lw", "rd")

    def __init__(self, name=""):
        self.name = name
        self.lw = None
        self.rd = []


class Chan:
    __slots__ = ("sem", "count", "last")

    def __init__(self, sem):
        self.sem = sem
        self.count = 0
        self.last = None


class Op:
    __slots__ = ("eng", "fn", "deps", "sig", "signo", "chan", "cval", "done")


class Sched:
    ENGS = ("pe", "act", "dve", "pool", "sp")

    def __init__(self, nc, stack):
        self.nc = nc
        self.stack = stack
        self.ops = []
        self.last_on = {e: None for e in self.ENGS}
        self.pending_bar = {e: [] for e in self.ENGS}
        self.chans = []
        self.free_chans = []
        self.phase_chans = []
        self.cnt = {e: 0 for e in self.ENGS}
        self.sems = {e: [] for e in self.ENGS}
        self.waited = {e: {} for e in self.ENGS}

    def chan(self, name, keep=False):
        if self.free_chans and not keep:
            c = self.free_chans.pop()
        else:
            c = Chan(self.stack.enter_context(self.nc.semaphore(name)))
            self.chans.append(c)
        if not keep:
            self.phase_chans.append(c)
        return c

    def add(self, eng, fn, r=(), w=(), chan=None):
        op = Op()
        op.eng, op.fn, op.deps, op.sig, op.signo, op.chan, op.cval = eng, fn, {}, False, 0, chan, 0
        op.done = False
        for b in r:
            if b.lw is not None:
                op.deps[b.lw] = True
        for b in w:
            if b.lw is not None:
                op.deps.setdefault(b.lw, False)
            for q in b.rd:
                op.deps.setdefault(q, False)
        for b in r:
            b.rd.append(op)
        for b in w:
            b.lw = op
            b.rd = []
        if self.pending_bar[eng]:
            for d in self.pending_bar[eng]:
                op.deps[d] = True
            self.pending_bar[eng] = []
        if chan is not None:
            if chan.last is not None:
                op.deps[chan.last] = True
            chan.count += 16
            op.cval = chan.count
            chan.last = op
        op.deps.pop(op, None)
        self.ops.append(op)
        self.last_on[eng] = op
        return op

    def barrier(self):
        lasts = [o for o in self.last_on.values() if o is not None]
        lasts += [c.last for c in self.chans if c.last is not None]
        for e in self.ENGS:
            self.pending_bar[e] = list(lasts)

    def emit(self):
        nc = self.nc
        fin = self.add("sp", lambda e: e.nop())
        for c in self.chans:
            if c.last is not None and not c.last.done:
                fin.deps[c.last] = True
        for e in self.ENGS:
            self.pending_bar[e] = []
        for op in self.ops:
            for d in [d for d in op.deps if d.done]:
                del op.deps[d]
            for d, raw in op.deps.items():
                if d.chan is not None:
                    continue
                if d.eng == op.eng and (op.eng == "pe" or not raw):
                    continue
                d.sig = True
        cnt = self.cnt
        for op in self.ops:
            if op.chan is None and op.sig:
                cnt[op.eng] += 1
                op.signo = cnt[op.eng]
        sems = self.sems
        for e in self.ENGS:
            n = cnt[e] // SEM_MAX + 1
            while len(sems[e]) < n:
                sems[e].append(self.stack.enter_context(nc.semaphore(f"s_{e}{len(sems[e])}")))
        per = {e: [o for o in self.ops if o.eng == e] for e in self.ENGS}
        handles = {"pe": "tensor", "act": "scalar", "dve": "vector", "pool": "gpsimd", "sp": "sync"}

        def run(e, eng):
            waited = self.waited[e]
            for op in per[e]:
                for d, raw in op.deps.items():
                    if d.chan is not None:
                        key, val, sem = ("c", id(d.chan)), d.cval, d.chan.sem
                    else:
                        if d.eng == e and (e == "pe" or not raw):
                            continue
                        j = (d.signo - 1) // SEM_MAX
                        key, val, sem = (d.eng, j), d.signo - j * SEM_MAX, sems[d.eng][j]
                    if waited.get(key, 0) >= val:
                        continue
                    waited[key] = val
                    eng.wait_ge(sem, val)
                ins = op.fn(eng)
                if op.chan is not None:
                    ins.then_inc(op.chan.sem, 16)
                elif op.sig:
                    j = (op.signo - 1) // SEM_MAX
                    ins.then_inc(sems[e][j], 1)

        with nc.Block() as block:
            for e in self.ENGS:
                if per[e]:
                    getattr(block, handles[e])(lambda eng, e=e: run(e, eng))
        for op in self.ops:
            op.done = True
            op.fn = None
            op.deps = {}
        self.ops = []
        self.last_on = {e: None for e in self.ENGS}
        for c in self.phase_chans:
            c.last = None
        self.free_chans.extend(self.phase_chans)
        self.phase_chans = []


def _act(out, in_, func, **kw):
    return lambda e: e.activation(out, in_, func, **kw)


def _ts(out, in0, s1, s2, op0, op1=None):
    if op1 is None:
        return lambda e: e.tensor_scalar(out, in0, s1, None, op0)
    return lambda e: e.tensor_scalar(out, in0, s1, s2, op0, op1)


def _stt(out, in0, sc, in1, op0, op1):
    return lambda e: e.scalar_tensor_tensor(out, in0, sc, in1, op0, op1)


def _tt(out, in0, in1, op):
    return lambda e: e.tensor_tensor(out, in0, in1, op)


def _cp(out, in_):
    return lambda e: e.tensor_copy(out, in_)


def _mm(out, lhsT, rhs, start, stop):
    return lambda e: e.matmul(out, lhsT, rhs, start=start, stop=stop)


def _tr(out, in_, ident):
    return lambda e: e.transpose(out, in_, ident)


def _dma(out, in_, **kw):
    return lambda e: e.dma_start(out=out, in_=in_, **kw)


def _memset(ap, v):
    return lambda e: e.memset(ap, v)


class K:
    def __init__(self, S, dbg=()):
        self.S = S
        self.dbg = dbg
        self.nc = bass.Bass("TRN2", target_bir_lowering=False)
        self.stack = contextlib.ExitStack()
        self.sc = Sched(self.nc, self.stack)

    def sb(self, st, name, shape, dt=F32):
        return st.enter_context(self.nc.sbuf_tensor(name, list(shape), dt))

    def ps(self, st, name, shape, dt=F32):
        return st.enter_context(self.nc.psum_tensor(name, list(shape), dt))

    def dram_in(self, name, shape, dt=F32):
        return self.nc.dram_tensor(name, list(shape), dt, kind="ExternalInput").ap()

    def dram_out(self, name, shape, dt=F32):
        return self.nc.dram_tensor(name, list(shape), dt, kind="ExternalOutput").ap()

    def dram_tmp(self, name, shape, dt=F32):
        if name in self.dbg:
            return self.nc.dram_tensor(name, list(shape), dt, kind="ExternalOutput").ap()
        return self.nc.dram_tensor(name, list(shape), dt).ap()


class Slot:
    def __init__(self, t, chan=None):
        self.t = t
        self.b = Buf()
        self.c = chan


def build(S, dbg=()):
    k = K(S, dbg)
    nc, sc = k.nc, k.sc
    NT, NB = S // 512, S // 128
    top = k.stack

    x = k.dram_in("x", [S, D])
    p_in = k.dram_in("p", [S, 256])
    ln_mix_g = k.dram_in("ln_mix_g", [D])
    w_in = k.dram_in("w_in", [D, IN_COLS])
    b_gates = k.dram_in("b_gates", [16])
    conv_qk_w = k.dram_in("conv_qk_w", [3, 1024])
    conv_qk_b = k.dram_in("conv_qk_b", [1024])
    mlstm_norm_g = k.dram_in("mlstm_norm_g", [512])
    q_norm_g = k.dram_in("q_norm_g", [256])
    w_uq = k.dram_in("w_uq", [256, 768])
    kv_norm_g = k.dram_in("kv_norm_g", [128])
    w_ukv = k.dram_in("w_ukv", [128, 1024])
    w_out = k.dram_in("w_out", [1024, 1024])
    ln_ffn_g = k.dram_in("ln_ffn_g", [D])
    w_up = k.dram_in("w_up", [D, 2 * D_FF])
    conv_ffn_w = k.dram_in("conv_ffn_w", [3, 2 * D_FF])
    conv_ffn_b = k.dram_in("conv_ffn_b", [2 * D_FF])
    w_down = k.dram_in("w_down", [D_FF, D])
    ple_norm_g = k.dram_in("ple_norm_g", [D])
    w_ple_gate = k.dram_in("w_ple_gate", [D, D])
    w_ple_proj = k.dram_in("w_ple_proj", [256, D])
    ple_post_g = k.dram_in("ple_post_g", [D])
    final_g = k.dram_in("final_g", [D])
    out = k.dram_out("out", [S, D])

    qkT = k.dram_tmp("qkT", [1024, S], BF16)
    vo_s = k.dram_tmp("vo_s", [S, 512])
    so_s = k.dram_tmp("so_s", [S, 512], BF16)
    g3_s = k.dram_tmp("g3_s", [S, 464])

    ident = k.sb(top, "ident", [128, 128], BF16)
    ones_f = k.sb(top, "ones_f", [128, 128], F32)
    mhalf = k.sb(top, "mhalf", [128, 16], F32)
    cb = Buf("consts")
    sc.add("pool", _memset(ones_f[:], 1.0), w=[cb])
    sc.add("pool", _memset(mhalf[:], -0.5), w=[cb])
    sc.add("pool", lambda e: e.affine_select(ident[:], ones_f[:], [[-1, 128]], ALU.is_equal, 0.0,
                                             base=0, channel_multiplier=1), r=[cb], w=[cb])
    sc.emit()

    ld_ch = [sc.chan(f"ldw{i}", keep=True) for i in range(2)]
    cnt = {"w": 0, "e": 0}

    def alt():
        cnt["e"] += 1
        return "dve" if cnt["e"] % 2 else "act"

    def scale_cast(eng, out_ap, in_ap, sc_ap=None):
        if eng == "act":
            if sc_ap is None:
                return _act(out_ap, in_ap, AF.Copy)
            return _act(out_ap, in_ap, AF.Copy, scale=sc_ap)
        if sc_ap is None:
            return _cp(out_ap, in_ap)
        return _ts(out_ap, in_ap, sc_ap, None, ALU.mult)

    def load_cols(st, name, src, G):
        t = k.sb(st, name, [128, G])
        b = Buf(name)
        ch = sc.chan("c_" + name)
        sc.add("sp", _dma(t[:], src.rearrange("(g p) -> p g", p=128), allow_slow_non_contiguous=True),
               w=[b], chan=ch)
        return t, b

    def load_bcast(st, name, src, n):
        t = k.sb(st, name, [128, n])
        b = Buf(name)
        ch = sc.chan("c_" + name)
        sc.add("sp", _dma(t[:], bass.AP(src.tensor, src.offset, [[0, 128], [1, n]])), w=[b], chan=ch)
        return t, b

    def load_w(st, stage, name, src, Kdim, cols, gain=None):
        kcn = Kdim // 128
        t = k.sb(st, name, [128, kcn, cols], BF16)
        b = Buf(name)
        for kc in range(kcn):
            sw = stage[0].t.shape[1]
            for c0 in range(0, cols, sw):
                w = min(sw, cols - c0)
                s = stage[cnt["w"] % 2]
                cnt["w"] += 1
                sc.add("sp", _dma(s.t[:, 0:w], src[kc * 128:(kc + 1) * 128, c0:c0 + w]), w=[s.b], chan=s.c)
                g = None if gain is None else gain[0][:, kc:kc + 1]
                rr = [s.b] + ([] if gain is None else [gain[1]])
                e = alt()
                sc.add(e, scale_cast(e, t[:, kc, c0:c0 + w], s.t[:, 0:w], g), r=rr, w=[b])
        return t, b

    with contextlib.ExitStack() as st:
        stage = [Slot(k.sb(st, f"wstage{i}", [128, 2816]), ld_ch[i]) for i in range(2)]
        gmix = load_cols(st, "gmix", ln_mix_g, 8)
        Win, Winb = load_w(st, stage, "Win", w_in, D, IN_COLS, gmix)
        cw = k.sb(st, "cw", [128, 8, 3])
        cwb = Buf("cw")
        for tap in range(3):
            sc.add("sp", _dma(cw[:, :, tap], conv_qk_w[tap].rearrange("(g p) -> p g", p=128),
                              allow_slow_non_contiguous=True), w=[Buf()], chan=sc.chan(f"c_cw{tap}"))
        cbias = load_cols(st, "cbias", conv_qk_b, 8)
        bg = load_bcast(st, "bg", b_gates, 16)
        sc.barrier()

        XT = [Slot(k.sb(st, f"xt{i}", [128, 4, D]), sc.chan(f"c_xt{i}")) for i in range(3)]
        XBA = [Slot(k.sb(st, f"xb{i}", [128, 4, D], BF16)) for i in range(2)]
        XN = [Slot(k.sb(st, f"xn{i}", [128, 8, 512], BF16)) for i in range(2)]
        junk = Slot(k.sb(st, "junk", [128, D], BF16))
        ss = Slot(k.sb(st, "ss", [128, 4]))
        ss2 = Slot(k.sb(st, "ss2", [128, 4]))
        rstd = Slot(k.sb(st, "rstd", [128, 4]))
        PRE = [Slot(k.sb(st, f"pre{g}", [128, 514])) for g in range(8)]
        ACC = [Slot(k.sb(st, f"acc{i}", [128, 512])) for i in range(2)]
        QKB = [Slot(k.sb(st, f"qkb{i}", [128, 512], BF16), sc.chan(f"c_qkb{i}")) for i in range(3)]
        VO = [Slot(k.sb(st, f"vo{i}", [128, 1024]), sc.chan(f"c_vo{i}")) for i in range(2)]
        G3 = [Slot(k.sb(st, f"g3{i}", [128, 464]), sc.chan(f"c_g3{i}")) for i in range(2)]
        SOB = [Slot(k.sb(st, f"sob{i}", [128, 512], BF16), sc.chan(f"c_sob{i}")) for i in range(2)]
        TB = [Slot(k.ps(st, f"tb{i}", [128, 1024], BF16)) for i in range(2)]
        MB = [Slot(k.ps(st, f"mb{i}", [128, 512])) for i in range(6)]
        last8 = Slot(k.sb(st, "last8", [128, 8]))
        last8b = Slot(k.sb(st, "last8b", [128, 8], BF16), sc.chan("c_last8"))
        for g in range(8):
            sc.add("pool", _memset(PRE[g].t[:, 0:2], 0.0), w=[PRE[g].b])
        cntA = {"mbi": 0, "qi": 0, "voi": 0}

        def stage1a(i):
            t0 = i * 512
            xt, XB = XT[i % 3], XBA[i % 2]
            sc.add("sp", _dma(xt.t[:], x[t0:t0 + 512, :].rearrange("(b p) d -> p b d", p=128)), w=[xt.b], chan=xt.c)
            for b in range(4):
                sc.add("act", _act(junk.t[:], xt.t[:, b, :], AF.Square, scale=1.0 / 32.0, accum_out=ss.t[:, b:b + 1]),
                       r=[xt.b], w=[junk.b, ss.b])
            sc.add("dve", _ts(ss2.t[:], ss.t[:], EPS, None, ALU.add), r=[ss.b], w=[ss2.b])
            sc.add("pool", _tt(rstd.t[:], ss2.t[:], mhalf[:, 0:4], ALU.pow), r=[ss2.b], w=[rstd.b])
            for b in range(4):
                e = "dve" if b % 2 else "act"
                sc.add(e, scale_cast(e, XB.t[:, b, :], xt.t[:, b, :], rstd.t[:, b:b + 1]), r=[xt.b, rstd.b], w=[XB.b])

        def stage1b(i):
            xn, XB = XN[i % 2], XBA[i % 2]
            for j in range(4):
                tb = TB[j % 2]
                for kk in range(2):
                    kc = 2 * j + kk
                    for b in range(4):
                        sc.add("pe", _tr(tb.t[:, kk * 512 + b * 128:kk * 512 + (b + 1) * 128],
                                         XB.t[:, b, kc * 128:(kc + 1) * 128], ident[:]), r=[XB.b], w=[tb.b])
                e = "dve" if j % 2 else "act"
                sc.add(e, scale_cast(e, xn.t[:, 2 * j:2 * j + 2, :], tb.t[:].rearrange("p (a b) -> p a b", a=2)),
                       r=[tb.b], w=[xn.b])

        def stage2fm(i):
            t0 = i * 512
            xn = XN[i % 2]
            mbi, qi, voi = cntA["mbi"], cntA["qi"], cntA["voi"]
            def tail(g, qi):
                pre = PRE[g]
                acc = ACC[g % 2]
                qb = QKB[qi % 3]
                sc.add("dve", _ts(acc.t[:], pre.t[:, 2:514], cw[:, g, 2:3], None, ALU.mult), r=[pre.b], w=[acc.b])
                sc.add("dve", _stt(acc.t[:], pre.t[:, 1:513], cw[:, g, 1:2], acc.t[:], ALU.mult, ALU.add),
                       r=[pre.b, acc.b], w=[acc.b])
                sc.add("dve", _stt(acc.t[:], pre.t[:, 0:512], cw[:, g, 0:1], acc.t[:], ALU.mult, ALU.add),
                       r=[pre.b, acc.b], w=[acc.b])
                sc.add("act", _act(qb.t[:], acc.t[:], AF.Silu, bias=cbias[0][:, g:g + 1]), r=[acc.b], w=[qb.b])
                if i == 0:
                    sc.add("pool", _dma(qkT[g * 128:(g + 1) * 128, 0:511], qb.t[:, 1:512]), r=[qb.b], chan=qb.c)
                else:
                    sc.add("pool", _dma(qkT[g * 128:(g + 1) * 128, t0 - 1:t0 + 511], qb.t[:]), r=[qb.b], chan=qb.c)
                sc.add("dve", _cp(pre.t[:, 0:2], pre.t[:, 512:514]), r=[pre.b], w=[pre.b])

            for g in range(8):
                pm = MB[mbi % 6]
                mbi += 1
                for kc in range(8):
                    sc.add("pe", _mm(pm.t[:], Win[:, kc, g * 128:(g + 1) * 128], xn.t[:, kc, :], kc == 0, kc == 7),
                           r=[xn.b, Winb], w=[pm.b])
                sc.add("act", _act(PRE[g].t[:, 2:514], pm.t[:], AF.Copy), r=[pm.b], w=[PRE[g].b])
                if g >= 1:
                    tail(g - 1, qi)
                    qi += 1
            tail(7, qi)
            qi += 1
            cntA["mbi"], cntA["qi"], cntA["voi"] = mbi, qi, voi

        def stage2tm(i):
            t0 = i * 512
            xn = XN[i % 2]
            mbi, qi, voi = cntA["mbi"], cntA["qi"], cntA["voi"]
            for b in range(4):
                vo = VO[voi % 2]
                g3 = G3[voi % 2]
                sob = SOB[voi % 2]
                voi += 1
                for part, (c0, c1) in enumerate(((1024, 1536), (1536, 2048), (2048, 2512))):
                    pm = MB[mbi % 6]
                    mbi += 1
                    for kc in range(8):
                        sc.add("pe", _mm(pm.t[:, 0:c1 - c0], xn.t[:, kc, b * 128:(b + 1) * 128], Win[:, kc, c0:c1],
                                         kc == 0, kc == 7), r=[xn.b, Winb], w=[pm.b])
                    if part == 0:
                        sc.add("dve", _cp(vo.t[:, 0:512], pm.t[:]), r=[pm.b], w=[vo.b])
                    elif part == 1:
                        sc.add("act", _act(vo.t[:, 512:1024], pm.t[:], AF.Tanh, scale=0.5), r=[pm.b], w=[vo.b])
                        sc.add("dve", _ts(sob.t[:], vo.t[:, 512:1024], 0.5, 0.5, ALU.mult, ALU.add),
                               r=[vo.b], w=[sob.b])
                    else:
                        sc.add("act", _act(g3.t[:], pm.t[:, 0:464], AF.Copy), r=[pm.b], w=[g3.b])
                        sc.add("dve", _tt(g3.t[:, 0:16], g3.t[:, 0:16], bg[0][:], ALU.add), r=[g3.b], w=[g3.b])
                r0 = t0 + b * 128
                sc.add("pool", _dma(vo_s[r0:r0 + 128, :], vo.t[:, 0:512]), r=[vo.b], chan=vo.c)
                sc.add("pool", _dma(so_s[r0:r0 + 128, :], sob.t[:]), r=[sob.b], chan=sob.c)
                sc.add("pool", _dma(g3_s[r0:r0 + 128, :], g3.t[:]), r=[g3.b], chan=g3.c)
            cntA["mbi"], cntA["qi"], cntA["voi"] = mbi, qi, voi

        stage1a(0)
        stage1b(0)
        if NT > 1:
            stage1a(1)
        for i in range(NT):
            stage2fm(i)
            if i + 1 < NT:
                stage1b(i + 1)
            if i + 2 < NT:
                stage1a(i + 2)
            stage2tm(i)
        prb = [PRE[g].b for g in range(8)]
        for g in range(8):
            sc.add("dve", _ts(last8.t[:, g:g + 1], PRE[g].t[:, 0:1], cw[:, g, 0:1], None, ALU.mult), r=[PRE[g].b], w=[last8.b])
            sc.add("dve", _stt(last8.t[:, g:g + 1], PRE[g].t[:, 1:2], cw[:, g, 1:2], last8.t[:, g:g + 1], ALU.mult, ALU.add),
                   r=[PRE[g].b, last8.b], w=[last8.b])
        sc.add("dve", _tt(last8.t[:], last8.t[:], cbias[0][:], ALU.add), r=[last8.b], w=[last8.b])
        sc.add("act", _act(last8b.t[:], last8.t[:], AF.Silu), r=[last8.b], w=[last8b.b])
        sc.add("pool", _dma(qkT.rearrange("(g p) s -> p g s", p=128)[:, :, S - 1], last8b.t[:],
                            allow_slow_non_contiguous=True), r=[last8b.b], chan=last8b.c)
        sc.emit()

    hf_s = k.dram_tmp("hf_s", [S, 512], BF16)
    hb_s = k.dram_tmp("hb_s", [S, 512], BF16)
    yT_s = k.dram_tmp("yT_s", [1024, S], BF16)

    def bc(ap, m):
        a = [list(d) for d in ap.ap]
        return bass.AP(ap.tensor, ap.offset, a + [[0, m]])

    def bc_mid(ap, m):
        a = [list(d) for d in ap.ap]
        return bass.AP(ap.tensor, ap.offset, [a[0], [0, m]] + a[1:])

    with contextlib.ExitStack() as st:
        maskF = k.sb(st, "maskF", [128, 128])
        maskB = k.sb(st, "maskB", [128, 128])
        mb_ = Buf("masks")
        sc.add("pool", lambda e: e.affine_select(maskF[:], ones_f[:], [[1, 128]], ALU.is_ge, 0.0,
                                                 base=0, channel_multiplier=-1), w=[mb_])
        sc.add("pool", lambda e: e.affine_select(maskB[:], ones_f[:], [[-1, 128]], ALU.is_ge, 0.0,
                                                 base=0, channel_multiplier=1), w=[mb_])
        hdst = (hf_s, hb_s)
        tris = (maskF, maskB)

        def ring(name, shape, dt=F32, chan=False, n=2):
            return [[Slot(k.sb(st, f"{name}{d}_{i}", shape, dt), sc.chan(f"c_{name}{d}_{i}") if chan else None) for i in range(n)]
                    for d in range(2)]

        SL = ring("sl", [128, 8, 512], BF16, True)
        VOT = ring("vot", [128, 512], F32, True)
        GT = ring("gt", [128, 16], F32, True)
        HO = ring("ho", [128, 512], BF16, True)
        AA = ring("aa", [128, 4])
        EG = ring("eg", [128, 8])
        V1 = ring("v1", [128, 4, 130], BF16)
        KTOK = ring("ktok", [128, 4, 128], BF16)
        MM = ring("mm", [128, 4, 128], BF16)
        e1 = [Slot(k.sb(st, f"e1_{d}", [128, 4])) for d in range(2)]
        lsp = [Slot(k.sb(st, f"lsp_{d}", [128, 4])) for d in range(2)]
        tmpa = [Slot(k.sb(st, f"tmpa_{d}", [128, 4])) for d in range(2)]
        C1 = [Slot(k.sb(st, f"c1_{d}", [128, 4, 130])) for d in range(2)]
        C1b = [Slot(k.sb(st, f"c1b_{d}", [128, 4, 130], BF16)) for d in range(2)]
        tmpC = [Slot(k.sb(st, f"tmpc_{d}", [128, 4, 130])) for d in range(2)]
        den = [Slot(k.sb(st, f"den_{d}", [128, 4])) for d in range(2)]
        rr_ = [Slot(k.sb(st, f"rr_{d}", [128, 4])) for d in range(2)]
        TBK = Slot(k.ps(st, "tbk", [128, 1024], BF16))
        SPS = [Slot(k.ps(st, f"sps{i}", [128, 512])) for i in range(2)]
        UPS = Slot(k.ps(st, "ups", [128, 1024]))
        DPS = Slot(k.ps(st, "dps", [128, 1024]))
        GPS = Slot(k.ps(st, "gps", [128, 512]))
        qkT3 = qkT.rearrange("(g p) s -> p g s", p=128)
        slab = [{"cg": None, "n": 0} for _ in range(2)]

        def pre(c, d, n):
            cg = c // 4
            if slab[d]["cg"] != cg:
                slab[d]["cg"] = cg
                slab[d]["n"] += 1
                sl = SL[d][slab[d]["n"] % 2]
                sc.add("sp", _dma(sl.t[:], qkT3[:, :, cg * 512:(cg + 1) * 512]), w=[sl.b], chan=sl.c)
            sl = SL[d][slab[d]["n"] % 2]
            vot, gt, aa, eg, v1, ktok, mm = (VOT[d][n % 2], GT[d][n % 2], AA[d][n % 2], EG[d][n % 2], V1[d][n % 2],
                                             KTOK[d][n % 2], MM[d][n % 2])
            sps = SPS[d]
            r0 = c * 128
            sc.add("sp", _dma(vot.t[:], vo_s[r0:r0 + 128, :]), w=[vot.b], chan=vot.c)
            sc.add("sp", _dma(gt.t[:], g3_s[r0:r0 + 128, 0:16]), w=[gt.b], chan=gt.c)
            io, fo = d * 8, d * 8 + 4
            tri = tris[d]
            sc.add("act", _act(e1[d].t[:], gt.t[:, fo:fo + 4], AF.Exp, scale=-1.0), r=[gt.b], w=[e1[d].b])
            sc.add("act", _act(lsp[d].t[:], e1[d].t[:], AF.Ln, bias=1.0), r=[e1[d].b], w=[lsp[d].b])
            gcol = d * 8
            sc.add("pe", _mm(GPS.t[:, gcol:gcol + 4], tri[:], lsp[d].t[:], True, True), r=[lsp[d].b, mb_], w=[GPS.b])
            sc.add("pe", _mm(GPS.t[:, gcol + 4:gcol + 8], ones_f[:], lsp[d].t[:], True, True), r=[lsp[d].b], w=[GPS.b])
            sc.add("dve", _tt(tmpa[d].t[:], gt.t[:, io:io + 4], GPS.t[:, gcol:gcol + 4], ALU.add), r=[gt.b, GPS.b], w=[tmpa[d].b])
            sc.add("act", _act(aa.t[:], tmpa[d].t[:], AF.Exp), r=[tmpa[d].b], w=[aa.b])
            sc.add("act", _act(eg.t[:], GPS.t[:, gcol:gcol + 8], AF.Exp, scale=-1.0), r=[GPS.b], w=[eg.b])
            sc.add("dve", _tt(v1.t[:, :, 0:128], vot.t[:].rearrange("p (h d) -> p h d", h=4), bc(aa.t[:], 128), ALU.mult),
                   r=[vot.b, aa.b], w=[v1.b])
            sc.add("dve", _cp(v1.t[:, :, 128], aa.t[:]), r=[aa.b], w=[v1.b])
            c4 = (c % 4) * 128
            tcol = d * 512
            for h in range(4):
                sc.add("pe", _tr(TBK.t[:, tcol + h * 128:tcol + (h + 1) * 128], sl.t[:, 4 + h, c4:c4 + 128], ident[:]), r=[sl.b], w=[TBK.b])
            sc.add("act", _act(ktok.t[:], TBK.t[:, tcol:tcol + 512].rearrange("p (h d) -> p h d", h=4), AF.Copy, scale=128.0 ** -0.5),
                   r=[TBK.b], w=[ktok.b])
            for h in range(4):
                sc.add("pe", _mm(sps.t[:, h * 128:(h + 1) * 128], sl.t[:, 4 + h, c4:c4 + 128], sl.t[:, h, c4:c4 + 128], True, True),
                       r=[sl.b], w=[sps.b])
            sc.add("dve", _stt(mm.t[:], sps.t[:].rearrange("p (h d) -> p h d", h=4), 128.0 ** -0.5, bc_mid(tri[:], 4), ALU.mult, ALU.mult),
                   r=[sps.b, mb_], w=[mm.b])
            return sl, c4

        def main(c, d, n, sl, c4):
            eg, v1, ktok, mm, ho = EG[d][n % 2], V1[d][n % 2], KTOK[d][n % 2], MM[d][n % 2], HO[d][n % 2]
            c1, c1b, tc_, dn, rr = C1[d], C1b[d], tmpC[d], den[d], rr_[d]
            r0 = c * 128
            for h in range(4):
                sc.add("pe", _mm(UPS.t[:, h * 256:h * 256 + 129], mm.t[:, h, :], v1.t[:, h, 0:129], True, False), r=[mm.b, v1.b], w=[UPS.b])
                sc.add("pe", _mm(UPS.t[:, h * 256:h * 256 + 129], sl.t[:, h, c4:c4 + 128], c1b.t[:, h, 0:129], False, True),
                       r=[sl.b, c1b.b], w=[UPS.b])
            for h in range(4):
                sc.add("pe", _mm(DPS.t[:, h * 256:h * 256 + 129], ktok.t[:, h, :], v1.t[:, h, 0:129], True, True), r=[ktok.b, v1.b], w=[DPS.b])
            U3 = UPS.t[:].rearrange("p (h d) -> p h d", h=4)
            D3 = DPS.t[:].rearrange("p (h d) -> p h d", h=4)
            sc.add("dve", _tt(tc_.t[:, :, 0:129], D3[:, :, 0:129], c1.t[:, :, 0:129], ALU.add), r=[DPS.b, c1.b], w=[tc_.b])
            sc.add("pool", _tt(c1.t[:, :, 0:129], tc_.t[:, :, 0:129], bc(eg.t[:, 4:8], 129), ALU.mult), r=[tc_.b, eg.b], w=[c1.b])
            sc.add("act", _act(c1b.t[:, :, 0:129], c1.t[:, :, 0:129], AF.Copy), r=[c1.b], w=[c1b.b])
            sc.add("dve", _tt(dn.t[:], U3[:, :, 128], eg.t[:, 0:4], ALU.mult), r=[UPS.b, eg.b], w=[dn.b])
            sc.add("act", _act(dn.t[:], dn.t[:], AF.Abs), r=[dn.b], w=[dn.b])
            sc.add("dve", _ts(dn.t[:], dn.t[:], 1.0, None, ALU.max), r=[dn.b], w=[dn.b])
            sc.add("dve", lambda e: e.reciprocal(rr.t[:], dn.t[:]), r=[dn.b], w=[rr.b])
            sc.add("dve", _tt(rr.t[:], rr.t[:], eg.t[:, 0:4], ALU.mult), r=[rr.b, eg.b], w=[rr.b])
            sc.add("dve", _tt(ho.t[:].rearrange("p (h d) -> p h d", h=4), U3[:, :, 0:128], bc(rr.t[:], 128), ALU.mult),
                   r=[UPS.b, rr.b], w=[ho.b])
            sc.add("pool", _dma(hdst[d][r0:r0 + 128, :], ho.t[:]), r=[ho.b], chan=ho.c)

        orders = (list(range(NB)), list(range(NB - 1, -1, -1)))
        nxt = [None, None]
        for d in range(2):
            sc.add("pool", _memset(C1[d].t[:], 0.0), w=[C1[d].b])
            sc.add("pool", _memset(C1b[d].t[:], 0.0), w=[C1b[d].b])
            nxt[d] = pre(orders[d][0], d, 0)
        for j in range(NB):
            cur = list(nxt)
            if j + 1 < NB:
                for d in range(2):
                    nxt[d] = pre(orders[d][j + 1], d, j + 1)
            for d in range(2):
                main(orders[d][j], d, j, *cur[d])
        sc.emit()

    KT_s = k.dram_tmp("KT_s", [512, S], BF16)
    KR_s = k.dram_tmp("KR_s", [64, S], BF16)
    V_s = k.dram_tmp("V_s", [S, 512], BF16)
    QN_s = k.dram_tmp("QN_s", [512, S], BF16)
    QR_s = k.dram_tmp("QR_s", [4, 65, S], BF16)
    kmax_s = k.dram_tmp("kmax_s", [128, 4])
    TWO_PI = 6.283185307179586

    with contextlib.ExitStack() as st:
        stage = [Slot(k.sb(st, f"wstagec{i}", [128, 1024]), ld_ch[i]) for i in range(2)]
        gq = load_cols(st, "gq", q_norm_g, 2)
        gkv = load_cols(st, "gkv", kv_norm_g, 1)
        Wuq, Wuqb = load_w(st, stage, "Wuq", w_uq, 256, 768, gq)
        Wkv, Wkvb = load_w(st, stage, "Wkv", w_ukv, 128, 1024, gkv)
        Wkv4 = Wkv[:, 0, :].rearrange("p (h t d) -> p h t d", h=4, t=2)
        cos2 = k.sb(st, "cos2", [128, NB, 64])
        sin1 = k.sb(st, "sin1", [128, NB, 32])
        tb_ = Buf("ropetab")
        pos = k.sb(st, "pos", [128, NB])
        invf = k.sb(st, "invf", [128, 32])
        ang = k.sb(st, "ang", [128, NB, 32])
        angi = k.sb(st, "angi", [128, NB, 32], mybir.dt.int32)
        angf = k.sb(st, "angf", [128, NB, 32])
        msk = k.sb(st, "msk", [128, NB, 32])
        sc.add("pool", lambda e: e.iota(pos[:], [[128, NB]], base=0, channel_multiplier=1, allow_small_or_imprecise_dtypes=True), w=[tb_])
        sc.add("pool", lambda e: e.iota(invf[:], [[1, 32]], base=0, channel_multiplier=0, allow_small_or_imprecise_dtypes=True), r=[tb_], w=[tb_])
        sc.add("act", _act(invf[:], invf[:], AF.Exp, scale=-float(np.log(10000.0)) / 32.0), r=[tb_], w=[tb_])
        sc.add("dve", _tt(ang[:], bc(pos[:], 32), bc_mid(invf[:], NB), ALU.mult), r=[tb_], w=[tb_])
        sc.add("dve", _ts(ang[:], ang[:], 1.0 / TWO_PI, None, ALU.mult), r=[tb_], w=[tb_])
        for which in range(2):
            if which == 1:
                sc.add("dve", _ts(ang[:], ang[:], 0.25, None, ALU.add), r=[tb_], w=[tb_])
            sc.add("dve", _cp(angi[:], ang[:]), r=[tb_], w=[tb_])
            sc.add("dve", _cp(angf[:], angi[:]), r=[tb_], w=[tb_])
            sc.add("dve", _tt(angf[:], ang[:], angf[:], ALU.subtract), r=[tb_], w=[tb_])
            sc.add("dve", _ts(msk[:], angf[:], 0.5, None, ALU.is_gt), r=[tb_], w=[tb_])
            sc.add("dve", _tt(angf[:], angf[:], msk[:], ALU.subtract), r=[tb_], w=[tb_])
            sc.add("dve", _ts(msk[:], angf[:], -0.5, None, ALU.is_lt), r=[tb_], w=[tb_])
            sc.add("dve", _tt(angf[:], angf[:], msk[:], ALU.add), r=[tb_], w=[tb_])
            if which == 0:
                sc.add("act", _act(sin1[:], angf[:], AF.Sin, scale=TWO_PI * (1.0 - 1e-6)), r=[tb_], w=[tb_])
            else:
                sc.add("act", _act(cos2[:, :, 0:32], angf[:], AF.Sin, scale=TWO_PI * (1.0 - 1e-6)), r=[tb_], w=[tb_])
                sc.add("act", _act(cos2[:, :, 32:64], angf[:], AF.Sin, scale=TWO_PI * (1.0 - 1e-6)), r=[tb_], w=[tb_])
        sc.barrier()

        G3T = [Slot(k.sb(st, f"g3t{i}", [128, 4, 464]), sc.chan(f"c_g3t{i}")) for i in range(2)]
        junkc = Slot(k.sb(st, "junkc", [128, 3072], BF16))
        ssq = Slot(k.sb(st, "ssq", [128, 8]))
        ssq2 = Slot(k.sb(st, "ssq2", [128, 8]))
        rst = Slot(k.sb(st, "rst", [128, 8]))
        cqn = Slot(k.sb(st, "cqn", [128, 4, 256], BF16))
        ckvn = Slot(k.sb(st, "ckvn", [128, 4, 128], BF16))
        tA = Slot(k.sb(st, "tA", [128, 4, 64]))
        tB = Slot(k.sb(st, "tB", [128, 4, 64]))
        krb = Slot(k.sb(st, "krb", [128, 4, 64], BF16))
        sqr = Slot(k.sb(st, "sqr", [128, 4, 64]))
        KR2 = [Slot(k.sb(st, f"kr2_{i}", [128, 4])) for i in range(2)]
        CQT = [Slot(k.sb(st, f"cqT{i}", [128, 2, 512], BF16)) for i in range(2)]
        CKVT = [Slot(k.sb(st, f"ckvT{i}", [128, 512], BF16)) for i in range(2)]
        krT = Slot(k.sb(st, "krT", [64, 512], BF16), sc.chan("c_krT"))
        KTS = [Slot(k.sb(st, f"kts{i}", [128, 512], BF16), sc.chan(f"c_kts{i}")) for i in range(2)]
        VS = [Slot(k.sb(st, f"vs{i}", [128, 512], BF16), sc.chan(f"c_vs{i}")) for i in range(2)]
        sqk = Slot(k.sb(st, "sqk", [128, 512]))
        kn2 = Slot(k.sb(st, "kn2", [128, 4, 4]))
        kmax = Slot(k.sb(st, "kmax", [128, 4]))
        kmt = Slot(k.sb(st, "kmt", [128, 4]))
        q_sb = Slot(k.sb(st, "q_sb", [128, 4, 768]))
        qtA = Slot(k.sb(st, "qtA", [128, 4, 4, 64]))
        qtB = Slot(k.sb(st, "qtB", [128, 4, 4, 64]))
        qbn = Slot(k.sb(st, "qbn", [128, 4, 4, 128], BF16))
        qbr = Slot(k.sb(st, "qbr", [128, 4, 4, 66], BF16))
        qn2 = Slot(k.sb(st, "qn2", [128, 16]))
        qn1 = Slot(k.sb(st, "qn1", [128, 16]))
        QS = [Slot(k.sb(st, f"qs{i}", [128, 2, 512], BF16), sc.chan(f"c_qs{i}")) for i in range(2)]
        QRS = [Slot(k.sb(st, f"qrs{i}", [65, 2, 512], BF16), sc.chan(f"c_qrs{i}")) for i in range(2)]
        PB = [Slot(k.ps(st, f"pb{i}", [128, 512])) for i in range(8)]
        pbi = {"i": 0}

        def bank():
            pbi["i"] += 1
            return PB[pbi["i"] % 8]

        def bfv(slot):
            return slot.t[:].bitcast(BF16)

        sc.add("pool", _memset(kmax.t[:], 0.0), w=[kmax.b])
        sc.add("pool", _memset(qbr.t[:], 0.0), w=[qbr.b])
        q4 = q_sb.t[:].rearrange("p b (h d) -> p b h d", h=4)
        cntC = {"kti": 0}

        def c_x(i):
            t0 = i * 512
            g3 = G3T[i % 2]
            kr2, cqT, ckvT = KR2[i % 2], CQT[i % 2], CKVT[i % 2]
            kti = cntC["kti"]
            sc.add("sp", _dma(g3.t[:], g3_s[t0:t0 + 512, :].rearrange("(b p) c -> p b c", p=128)), w=[g3.b], chan=g3.c)
            for b in range(4):
                sc.add("act", _act(junkc.t[:, 0:256], g3.t[:, b, 16:272], AF.Square, scale=1.0 / 16.0, accum_out=ssq.t[:, b:b + 1]),
                       r=[g3.b], w=[junkc.b, ssq.b])
                sc.add("act", _act(junkc.t[:, 0:128], g3.t[:, b, 272:400], AF.Square, scale=128.0 ** -0.5, accum_out=ssq.t[:, 4 + b:5 + b]),
                       r=[g3.b], w=[junkc.b, ssq.b])
            sc.add("dve", _ts(ssq2.t[:], ssq.t[:], EPS, None, ALU.add), r=[ssq.b], w=[ssq2.b])
            sc.add("pool", _tt(rst.t[:], ssq2.t[:], mhalf[:, 0:8], ALU.pow), r=[ssq2.b], w=[rst.b])
            sc.add("dve", _tt(cqn.t[:], g3.t[:, :, 16:272], bc(rst.t[:, 0:4], 256), ALU.mult), r=[g3.b, rst.b], w=[cqn.b])
            sc.add("dve", _tt(ckvn.t[:], g3.t[:, :, 272:400], bc(rst.t[:, 4:8], 128), ALU.mult), r=[g3.b, rst.b], w=[ckvn.b])
            xk = g3.t[:, :, 400:464]
            cs, sn = cos2[:, 4 * i:4 * i + 4, :], sin1[:, 4 * i:4 * i + 4, :]
            sc.add("dve", _tt(tA.t[:], xk, cs, ALU.mult), r=[g3.b], w=[tA.b])
            sc.add("dve", _tt(tB.t[:, :, 0:32], g3.t[:, :, 432:464], sn, ALU.mult), r=[g3.b], w=[tB.b])
            sc.add("dve", _tt(tB.t[:, :, 32:64], g3.t[:, :, 400:432], sn, ALU.mult), r=[g3.b], w=[tB.b])
            sc.add("dve", _tt(krb.t[:, :, 0:32], tA.t[:, :, 0:32], tB.t[:, :, 0:32], ALU.subtract), r=[tA.b, tB.b], w=[krb.b])
            sc.add("dve", _tt(krb.t[:, :, 32:64], tA.t[:, :, 32:64], tB.t[:, :, 32:64], ALU.add), r=[tA.b, tB.b], w=[krb.b])
            sc.add("act", _act(sqr.t[:], xk, AF.Square), r=[g3.b], w=[sqr.b])
            sc.add("dve", lambda e: e.tensor_reduce(kr2.t[:], sqr.t[:], AX.X, ALU.add), r=[sqr.b], w=[kr2.b])
            pa, pb2 = bank(), bank()
            for b in range(4):
                for kc in range(2):
                    sc.add("pe", _tr(bfv(pa)[:, kc * 512 + b * 128:kc * 512 + (b + 1) * 128], cqn.t[:, b, kc * 128:(kc + 1) * 128], ident[:]),
                           r=[cqn.b], w=[pa.b])
                sc.add("pe", _tr(bfv(pb2)[:, b * 128:(b + 1) * 128], ckvn.t[:, b, :], ident[:]), r=[ckvn.b], w=[pb2.b])
                sc.add("pe", _tr(bfv(pb2)[0:64, 512 + b * 128:512 + (b + 1) * 128], krb.t[:, b, :], ident[:]), r=[krb.b], w=[pb2.b])
            sc.add("act", _act(cqT.t[:], bfv(pa).rearrange("p (a b) -> p a b", a=2), AF.Copy), r=[pa.b], w=[cqT.b])
            sc.add("dve", _cp(ckvT.t[:], bfv(pb2)[:, 0:512]), r=[pb2.b], w=[ckvT.b])
            sc.add("act", _act(krT.t[:], bfv(pb2)[0:64, 512:1024], AF.Copy), r=[pb2.b], w=[krT.b])
            sc.add("pool", _dma(KR_s[:, t0:t0 + 512], krT.t[:]), r=[krT.b], chan=krT.c)
            for h in range(4):
                pk = bank()
                sc.add("pe", _mm(pk.t[:], Wkv[:, 0, h * 256:h * 256 + 128], ckvT.t[:], True, True), r=[ckvT.b, Wkvb], w=[pk.b])
                kts = KTS[kti % 2]
                kti += 1
                e = "act" if h % 2 else "dve"
                sc.add(e, scale_cast(e, kts.t[:], pk.t[:]), r=[pk.b], w=[kts.b])
                sc.add("pool", _dma(KT_s[h * 128:(h + 1) * 128, t0:t0 + 512], kts.t[:]), r=[kts.b], chan=kts.c)
            cntC["kti"] = kti

        def c_y(i):
            t0 = i * 512
            kr2, cqT, ckvT = KR2[i % 2], CQT[i % 2], CKVT[i % 2]
            cs, sn = cos2[:, 4 * i:4 * i + 4, :], sin1[:, 4 * i:4 * i + 4, :]
            for b in range(4):
                tok = slice(b * 128, (b + 1) * 128)
                pv = bank()
                sc.add("pe", _mm(pv.t[:].rearrange("p (h d) -> p h d", h=4), ckvT.t[:, tok], Wkv4[:, :, 1, :], True, True),
                       r=[ckvT.b, Wkvb], w=[pv.b])
                vs = VS[b % 2]
                sc.add("act", _act(vs.t[:], pv.t[:], AF.Copy), r=[pv.b], w=[vs.b])
                sc.add("pool", _dma(V_s[t0 + b * 128:t0 + (b + 1) * 128, :], vs.t[:]), r=[vs.b], chan=vs.c)
                pk = bank()
                sc.add("pe", _mm(pk.t[:].rearrange("p (h d) -> p h d", h=4), ckvT.t[:, tok], Wkv4[:, :, 0, :], True, True),
                       r=[ckvT.b, Wkvb], w=[pk.b])
                sc.add("act", _act(sqk.t[:], pk.t[:], AF.Square), r=[pk.b], w=[sqk.b])
                sc.add("dve", lambda e, b=b: e.tensor_reduce(kn2.t[:, b, :], sqk.t[:].rearrange("p (h d) -> p h d", h=4), AX.X, ALU.add),
                       r=[sqk.b], w=[kn2.b])
                pq0, pq1 = bank(), bank()
                for kc in range(2):
                    sc.add("pe", _mm(pq0.t[:], cqT.t[:, kc, tok], Wuq[:, kc, 0:512], kc == 0, kc == 1), r=[cqT.b, Wuqb], w=[pq0.b])
                for kc in range(2):
                    sc.add("pe", _mm(pq1.t[:, 0:256], cqT.t[:, kc, tok], Wuq[:, kc, 512:768], kc == 0, kc == 1), r=[cqT.b, Wuqb], w=[pq1.b])
                sc.add("act", _act(q_sb.t[:, b, 0:512], pq0.t[:], AF.Copy), r=[pq0.b], w=[q_sb.b])
                sc.add("dve", _cp(q_sb.t[:, b, 512:768], pq1.t[:, 0:256]), r=[pq1.b], w=[q_sb.b])
            sc.add("dve", _tt(kn2.t[:], kn2.t[:], bc(kr2.t[:], 4), ALU.add), r=[kn2.b, kr2.b], w=[kn2.b])
            sc.add("dve", lambda e: e.tensor_reduce(kmt.t[:], kn2.t[:].rearrange("p b h -> p h b"), AX.X, ALU.max), r=[kn2.b], w=[kmt.b])
            sc.add("dve", _tt(kmax.t[:], kmax.t[:], kmt.t[:], ALU.max), r=[kmax.b, kmt.b], w=[kmax.b])
            cs4 = bass.AP(cs.tensor, cs.offset, [list(cs.ap[0]), list(cs.ap[1]), [0, 4], list(cs.ap[2])])
            sn4 = bass.AP(sn.tensor, sn.offset, [list(sn.ap[0]), list(sn.ap[1]), [0, 4], list(sn.ap[2])])
            sc.add("dve", _tt(qtA.t[:], q4[:, :, :, 128:192], cs4, ALU.mult), r=[q_sb.b], w=[qtA.b])
            sc.add("dve", _tt(qtB.t[:, :, :, 0:32], q4[:, :, :, 160:192], sn4, ALU.mult), r=[q_sb.b], w=[qtB.b])
            sc.add("dve", _tt(qtB.t[:, :, :, 32:64], q4[:, :, :, 128:160], sn4, ALU.mult), r=[q_sb.b], w=[qtB.b])
            sc.add("dve", _tt(qbr.t[:, :, :, 0:32], qtA.t[:, :, :, 0:32], qtB.t[:, :, :, 0:32], ALU.subtract), r=[qtA.b, qtB.b], w=[qbr.b])
            sc.add("dve", _tt(qbr.t[:, :, :, 32:64], qtA.t[:, :, :, 32:64], qtB.t[:, :, :, 32:64], ALU.add), r=[qtA.b, qtB.b], w=[qbr.b])
            sc.add("dve", _cp(qbn.t[:], q4[:, :, :, 0:128]), r=[q_sb.b], w=[qbn.b])
            sc.add("act", _act(junkc.t[:], q_sb.t[:].rearrange("p b c -> p (b c)"), AF.Square), r=[q_sb.b], w=[junkc.b])
            sc.add("dve", lambda e: e.tensor_reduce(qn2.t[:], junkc.t[:].rearrange("p (g d) -> p g d", g=16), AX.X, ALU.add),
                   r=[junkc.b], w=[qn2.b])
            sc.add("pool", _tt(qn1.t[:], qn2.t[:], mhalf[:, 0:16], ALU.pow), r=[qn2.b], w=[qn1.b])
            sc.add("dve", _tt(qn1.t[:], qn1.t[:], qn2.t[:], ALU.mult), r=[qn1.b, qn2.b], w=[qn1.b])
            sc.add("dve", _ts(qbr.t[:, :, :, 64], qn1.t[:].rearrange("p (b h) -> p b h", b=4), -1.01, None, ALU.mult),
                   r=[qn1.b], w=[qbr.b])
            for hp in range(2):
                pn, pr = bank(), bank()
                for hh in range(2):
                    h = 2 * hp + hh
                    for b in range(4):
                        sc.add("pe", _tr(bfv(pn)[:, hh * 512 + b * 128:hh * 512 + (b + 1) * 128], qbn.t[:, b, h, :], ident[:]), r=[qbn.b], w=[pn.b])
                        sc.add("pe", _tr(bfv(pr)[0:65, hh * 512 + b * 128:hh * 512 + (b + 1) * 128], qbr.t[:, b, h, 0:65], ident[:]), r=[qbr.b], w=[pr.b])
                qs, qrs = QS[hp], QRS[hp]
                sc.add("act", _act(qs.t[:], bfv(pn).rearrange("p (a b) -> p a b", a=2), AF.Copy), r=[pn.b], w=[qs.b])
                sc.add("dve", _cp(qrs.t[:], bfv(pr)[0:65, :].rearrange("p (a b) -> p a b", a=2)), r=[pr.b], w=[qrs.b])
                sc.add("pool", _dma(QN_s.rearrange("(h p) s -> p h s", p=128)[:, 2 * hp:2 * hp + 2, t0:t0 + 512], qs.t[:]), r=[qs.b], chan=qs.c)
                sc.add("pool", _dma(QR_s.rearrange("h p s -> p h s")[:, 2 * hp:2 * hp + 2, t0:t0 + 512], qrs.t[:]), r=[qrs.b], chan=qrs.c)
        c_x(0)
        for i in range(NT):
            if i + 1 < NT:
                c_x(i + 1)
            c_y(i)
        kmo = Slot(k.sb(st, "kmo", [128, 4]), sc.chan("c_kmo"))
        sc.add("dve", _cp(kmo.t[:], kmax.t[:]), r=[kmax.b], w=[kmo.b])
        sc.add("pool", _dma(kmax_s[:, :], kmo.t[:]), r=[kmo.b], chan=kmo.c)
        sc.emit()

    with contextlib.ExitStack() as st:
        KT = Slot(k.sb(st, "KT", [128, 4, S], BF16), sc.chan("c_KT"))
        KR = Slot(k.sb(st, "KR", [65, S], BF16), sc.chan("c_KR"))
        VR = Slot(k.sb(st, "VR", [128, NB, 512], BF16), sc.chan("c_VR"))
        kml = Slot(k.sb(st, "kml", [128, 4]), sc.chan("c_kml"))
        km1 = Slot(k.sb(st, "km1", [1, 4]))
        kmx = Slot(k.sb(st, "kmx", [128, 4]))
        phalf = Slot(k.sb(st, "phalf", [128, 4]))
        SPB = [Slot(k.ps(st, f"spb{i}", [128, 512])) for i in range(3)]
        OPB = [Slot(k.ps(st, f"opb{i}", [128, 512])) for i in range(2)]
        RSB = Slot(k.ps(st, "rsb", [128, 512]))
        NPT = 10
        PT = [Slot(k.sb(st, f"pt{i}", [128, 512], BF16)) for i in range(NPT)]
        RS = [Slot(k.ps(st, f"rs{i}", [128, 512])) for i in range(1)]
        rs_sb = Slot(k.sb(st, "rs_sb", [128, 512]))
        ones_b = Slot(k.sb(st, "ones_b", [128, 32], BF16))
        inv32 = Slot(k.sb(st, "inv32", [128, 128]))
        sc.add("pool", _memset(ones_b.t[:], 1.0), w=[ones_b.b])
        sc.add("pool", _memset(inv32.t[:], 1.0 / 32.0), w=[inv32.b])
        QN = [Slot(k.sb(st, f"qnt{i}", [128, 512], BF16), sc.chan(f"c_qn{i}")) for i in range(2)]
        QR = [Slot(k.sb(st, f"qrt{i}", [65, 512], BF16), sc.chan(f"c_qr{i}")) for i in range(2)]
        rinv = Slot(k.sb(st, "rinv", [128, 512]))
        YO = [Slot(k.sb(st, f"yo{i}", [128, 512], BF16), sc.chan(f"c_yo{i}")) for i in range(2)]
        sc.add("sp", _dma(KT.t[:], KT_s.rearrange("(h p) s -> p h s", p=128)), w=[KT.b], chan=KT.c)
        sc.add("pool", _memset(KR.t[64:65, :], 1.0), w=[KR.b])
        sc.add("sp", _dma(KR.t[0:64, :], KR_s[:, :]), w=[KR.b], chan=KR.c)
        sc.add("sp", _dma(VR.t[:], V_s.rearrange("(c p) d -> p c d", p=128)), w=[VR.b], chan=VR.c)
        sc.add("sp", _dma(kml.t[:], kmax_s[:, :]), w=[kml.b], chan=kml.c)
        sc.add("pool", _memset(phalf.t[:], 0.5), w=[phalf.b])
        sc.add("pool", lambda e: e.tensor_reduce(km1.t[:], kml.t[:], AX.C, ALU.max), r=[kml.b], w=[km1.b])
        sc.add("pe", _mm(RSB.t[:, 0:4], ones_f[0:1, :], km1.t[:], True, True), r=[km1.b], w=[RSB.b])
        sc.add("dve", _cp(kmx.t[:], RSB.t[:, 0:4]), r=[RSB.b], w=[kmx.b])
        sc.add("pool", _tt(kmx.t[:], kmx.t[:], phalf.t[:], ALU.pow), r=[kmx.b, phalf.b], w=[kmx.b])
        scale = 192.0 ** -0.5
        it = 0
        pti = 0
        for h in range(4):
            for j in range(NT):
                qn, qr = QN[it % 2], QR[it % 2]
                opb, yo, rs = OPB[it % 2], YO[it % 2], RS[0]
                it += 1
                sc.add("sp", _dma(qn.t[:], QN_s[h * 128:(h + 1) * 128, j * 512:(j + 1) * 512]), w=[qn.b], chan=qn.c)
                sc.add("sp", _dma(qr.t[:], QR_s[h, :, j * 512:(j + 1) * 512]), w=[qr.b], chan=qr.c)
                sc.add("dve", _ts(qr.t[64:65, :], qr.t[64:65, :], kmx.t[64:65, h:h + 1], None, ALU.mult), r=[qr.b, kmx.b], w=[qr.b])

                def qk(kc):
                    sp_ = SPB[kc % 3]
                    ks = slice(kc * 128, (kc + 1) * 128)
                    sc.add("pe", _mm(sp_.t[:], KT.t[:, h, ks], qn.t[:], True, False), r=[KT.b, qn.b], w=[sp_.b])
                    sc.add("pe", _mm(sp_.t[:], KR.t[0:65, ks], qr.t[0:65, :], False, True), r=[KR.b, qr.b], w=[sp_.b])

                qk(0)
                if NB > 1:
                    qk(1)
                grp = []
                for kc in range(NB):
                    if kc + 2 < NB:
                        qk(kc + 2)
                    sp_ = SPB[kc % 3]
                    pt = PT[pti % NPT]
                    pti += 1
                    sc.add("act", _act(pt.t[:], sp_.t[:], AF.Exp, scale=scale), r=[sp_.b], w=[pt.b])
                    sc.add("pe", _mm(opb.t[:], VR.t[:, kc, h * 128:(h + 1) * 128], pt.t[:], kc == 0, kc == NB - 1),
                           r=[VR.b, pt.b], w=[opb.b])
                    grp.append(pt)
                    if len(grp) == 4:
                        for r_, ptr in enumerate(grp):
                            sc.add("pe", lambda e, r_=r_, ptr=ptr, kc=kc: e.matmul(rs.t[32 * r_:32 * r_ + 32, :], ones_b.t[:, 0:32], ptr.t[:],
                                                                               start=(kc == 3), stop=(kc == NB - 1),
                                                                               tile_position=(0, 32 * r_)),
                                   r=[ptr.b, ones_b.b], w=[rs.b])
                        grp = []
                sc.add("dve", _cp(rs_sb.t[:], rs.t[:]), r=[rs.b], w=[rs_sb.b])
                sc.add("pe", _mm(RSB.t[:], inv32.t[:], rs_sb.t[:], True, True), r=[rs_sb.b, inv32.b], w=[RSB.b])
                sc.add("dve", lambda e: e.reciprocal(rinv.t[:], RSB.t[:]), r=[RSB.b], w=[rinv.b])
                sc.add("dve", _tt(yo.t[:], opb.t[:], rinv.t[:], ALU.mult), r=[opb.b, rinv.b], w=[yo.b])
                sc.add("pool", _dma(yT_s[512 + h * 128:512 + (h + 1) * 128, j * 512:(j + 1) * 512], yo.t[:]), r=[yo.b], chan=yo.c)
        sc.emit()

    h1_s = k.dram_tmp("h1_s", [S, D])
    xn2T_s = k.dram_tmp("xn2T_s", [D, S + 2], BF16)
    h2_s = k.dram_tmp("h2_s", [S, D])
    yT3 = yT_s.rearrange("(g p) s -> p g s", p=128)
    xn2T3 = xn2T_s.rearrange("(g p) s -> p g s", p=128)

    def rms_transpose(xt, ss, ss2, rstd, junk, XB, xn, TBs, gain_scale=1.0 / 32.0):
        for b in range(4):
            sc.add("act", _act(junk.t[:], xt.t[:, b, :], AF.Square, scale=gain_scale, accum_out=ss.t[:, b:b + 1]),
                   r=[xt.b], w=[junk.b, ss.b])
        sc.add("dve", _ts(ss2.t[:], ss.t[:], EPS, None, ALU.add), r=[ss.b], w=[ss2.b])
        sc.add("pool", _tt(rstd.t[:], ss2.t[:], mhalf[:, 0:4], ALU.pow), r=[ss2.b], w=[rstd.b])
        for b in range(4):
            e = "dve" if b % 2 else "act"
            sc.add(e, scale_cast(e, XB.t[:, b, :], xt.t[:, b, :], rstd.t[:, b:b + 1]), r=[xt.b, rstd.b], w=[XB.b])
        for j in range(4):
            tb = TBs[j % 2]
            for kk in range(2):
                kc = 2 * j + kk
                for b in range(4):
                    sc.add("pe", _tr(tb.t[:, kk * 512 + b * 128:kk * 512 + (b + 1) * 128],
                                     XB.t[:, b, kc * 128:(kc + 1) * 128], ident[:]), r=[XB.b], w=[tb.b])
            e = "dve" if j % 2 else "act"
            sc.add(e, scale_cast(e, xn.t[:, 2 * j:2 * j + 2, :], tb.t[:].rearrange("p (a b) -> p a b", a=2)),
                   r=[tb.b], w=[xn.b])

    with contextlib.ExitStack() as st:
        stage = [Slot(k.sb(st, f"wstaged{i}", [128, 1024]), ld_ch[i]) for i in range(2)]
        Wout, Woutb = load_w(st, stage, "Wout", w_out, D, D)
        normg = load_bcast(st, "normg", mlstm_norm_g, 512)
        zt = Slot(k.sb(st, "zt", [128, 8, 2], BF16), sc.chan("c_zt"))
        sc.add("pool", _memset(zt.t[:], 0.0), w=[zt.b])
        sc.add("pool", _dma(xn2T3[:, :, 0:1], zt.t[:, :, 0:1], allow_slow_non_contiguous=True), r=[zt.b], chan=zt.c)
        sc.add("pool", _dma(xn2T3[:, :, S + 1:S + 2], zt.t[:, :, 1:2], allow_slow_non_contiguous=True), r=[zt.b], chan=zt.c)
        HFT = [Slot(k.sb(st, f"hft{i}", [128, 4, 512], BF16), sc.chan(f"c_hft{i}")) for i in range(2)]
        HBT = [Slot(k.sb(st, f"hbt{i}", [128, 4, 512], BF16), sc.chan(f"c_hbt{i}")) for i in range(2)]
        SOT = [Slot(k.sb(st, f"sot{i}", [128, 4, 512], BF16), sc.chan(f"c_sot{i}")) for i in range(2)]
        HS = Slot(k.sb(st, "hsd", [128, 4, 512]))
        SG = Slot(k.sb(st, "sgd", [128, 4, 512]))
        sqd = Slot(k.sb(st, "sqd", [128, 4, 512], BF16))
        ssn = Slot(k.sb(st, "ssnd", [128, 16]))
        rsn = Slot(k.sb(st, "rsnd", [128, 16]))
        YB = [Slot(k.sb(st, f"ybd{i}", [128, 4, 512], BF16)) for i in range(2)]
        YAT = [Slot(k.sb(st, f"yat{i}", [128, 4, 512], BF16)) for i in range(2)]
        YTT = [Slot(k.sb(st, f"ytt{i}", [128, 4, 512], BF16), sc.chan(f"c_ytt{i}")) for i in range(2)]
        XT = [Slot(k.sb(st, f"xtd{i}", [128, 4, D]), sc.chan(f"c_xtd{i}")) for i in range(3)]
        XB = Slot(k.sb(st, "xbd", [128, 4, D], BF16))
        XN = [Slot(k.sb(st, f"xnd{i}", [128, 8, 512], BF16), sc.chan(f"c_xnd{i}")) for i in range(2)]
        junk = Slot(k.sb(st, "junkd", [128, D], BF16))
        ss = Slot(k.sb(st, "ssd", [128, 4]))
        ss2 = Slot(k.sb(st, "ss2d", [128, 4]))
        rstd = Slot(k.sb(st, "rstdd", [128, 4]))
        TBs = [Slot(k.ps(st, f"tbd{i}", [128, 1024], BF16)) for i in range(2)]
        MB = [Slot(k.ps(st, f"mbd{i}", [128, 512])) for i in range(6)]
        cntD = {"mbi": 0}

        def loads(i):
            t0 = i * 512
            hft, hbt, sot, ytt, xt = HFT[i % 2], HBT[i % 2], SOT[i % 2], YTT[i % 2], XT[i % 3]
            tv = lambda ap: ap[t0:t0 + 512, :].rearrange("(b p) d -> p b d", p=128)
            sc.add("sp", _dma(hft.t[:], tv(hf_s)), w=[hft.b], chan=hft.c)
            sc.add("sp", _dma(hbt.t[:], tv(hb_s)), w=[hbt.b], chan=hbt.c)
            sc.add("sp", _dma(sot.t[:], tv(so_s)), w=[sot.b], chan=sot.c)
            sc.add("sp", _dma(ytt.t[:], yT3[:, 4:8, t0:t0 + 512]), w=[ytt.b], chan=ytt.c)
            sc.add("sp", _dma(xt.t[:], tv(x)), w=[xt.b], chan=xt.c)

        def comb1(i):
            hft, hbt, sot = HFT[i % 2], HBT[i % 2], SOT[i % 2]
            sc.add("dve", _tt(HS.t[:], hft.t[:], hbt.t[:], ALU.add), r=[hft.b, hbt.b], w=[HS.b])
            sc.add("act", _act(sqd.t[:], HS.t[:], AF.Square, scale=128.0 ** -0.5), r=[HS.b], w=[sqd.b])
            sc.add("pool", _tt(SG.t[:], sot.t[:], bc_mid(normg[0][:], 4), ALU.mult), r=[sot.b, normg[1]], w=[SG.b])
            sc.add("dve", lambda e: e.tensor_reduce(ssn.t[:], sqd.t[:].rearrange("p b (h d) -> p (b h) d", h=4), AX.X, ALU.add),
                   r=[sqd.b], w=[ssn.b])
            sc.add("dve", _ts(ssn.t[:], ssn.t[:], EPS, None, ALU.add), r=[ssn.b], w=[ssn.b])
            sc.add("pool", _tt(rsn.t[:], ssn.t[:], mhalf[:, 0:16], ALU.pow), r=[ssn.b], w=[rsn.b])

        def comb2(i):
            yb = YB[i % 2]
            h16 = HS.t[:].rearrange("p b (h d) -> p (b h) d", h=4)
            sc.add("dve", _tt(h16, h16, bc(rsn.t[:], 128), ALU.mult), r=[HS.b, rsn.b], w=[HS.b])
            sc.add("dve", _tt(yb.t[:], HS.t[:], SG.t[:], ALU.mult), r=[HS.b, SG.b], w=[yb.b])

        def norm1(i):
            t0 = i * 512
            xt = XT[i % 3]
            sc.add("sp", _dma(h1_s[t0:t0 + 512, :].rearrange("(b p) d -> p b d", p=128), xt.t[:]), r=[xt.b], chan=xt.c)
            for b in range(4):
                sc.add("act", _act(junk.t[:], xt.t[:, b, :], AF.Square, scale=1.0 / 32.0, accum_out=ss.t[:, b:b + 1]),
                       r=[xt.b], w=[junk.b, ss.b])
            sc.add("dve", _ts(ss2.t[:], ss.t[:], EPS, None, ALU.add), r=[ss.b], w=[ss2.b])
            sc.add("pool", _tt(rstd.t[:], ss2.t[:], mhalf[:, 0:4], ALU.pow), r=[ss2.b], w=[rstd.b])

        def norm2(i):
            xt = XT[i % 3]
            for b in range(4):
                e = "dve" if b % 2 else "act"
                sc.add(e, scale_cast(e, XB.t[:, b, :], xt.t[:, b, :], rstd.t[:, b:b + 1]), r=[xt.b, rstd.b], w=[XB.b])

        def mmstage(i):
            yb, yat, ytt, xt = YB[i % 2], YAT[i % 2], YTT[i % 2], XT[i % 3]
            for hp in range(2):
                tb = TBs[hp]
                for hh in range(2):
                    h = 2 * hp + hh
                    for b in range(4):
                        sc.add("pe", _tr(tb.t[:, hh * 512 + b * 128:hh * 512 + (b + 1) * 128], yb.t[:, b, h * 128:(h + 1) * 128], ident[:]),
                               r=[yb.b], w=[tb.b])
                e = "dve" if hp else "act"
                sc.add(e, scale_cast(e, yat.t[:, 2 * hp:2 * hp + 2, :], tb.t[:].rearrange("p (a b) -> p a b", a=2)), r=[tb.b], w=[yat.b])
            mbi = cntD["mbi"]
            for b in range(4):
                for half in range(2):
                    pm = MB[mbi % 6]
                    mbi += 1
                    for kc in range(8):
                        src = yat if kc < 4 else ytt
                        sc.add("pe", _mm(pm.t[:], src.t[:, kc % 4, b * 128:(b + 1) * 128], Wout[:, kc, half * 512:(half + 1) * 512],
                                         kc == 0, kc == 7), r=[src.b, Woutb], w=[pm.b])
                    sc.add("dve", _tt(xt.t[:, b, half * 512:(half + 1) * 512], pm.t[:], xt.t[:, b, half * 512:(half + 1) * 512], ALU.add),
                           r=[pm.b, xt.b], w=[xt.b])
            cntD["mbi"] = mbi

        def trstage(i):
            t0 = i * 512
            xn = XN[i % 2]
            for j in range(4):
                tb = TBs[j % 2]
                for kk in range(2):
                    kc = 2 * j + kk
                    for b in range(4):
                        sc.add("pe", _tr(tb.t[:, kk * 512 + b * 128:kk * 512 + (b + 1) * 128],
                                         XB.t[:, b, kc * 128:(kc + 1) * 128], ident[:]), r=[XB.b], w=[tb.b])
                e = "dve" if j % 2 else "act"
                sc.add(e, scale_cast(e, xn.t[:, 2 * j:2 * j + 2, :], tb.t[:].rearrange("p (a b) -> p a b", a=2)),
                       r=[tb.b], w=[xn.b])
            sc.add("sp", _dma(xn2T3[:, :, 1 + t0:1 + t0 + 512], xn.t[:]), r=[xn.b], chan=xn.c)

        loads(0)
        if NT > 1:
            loads(1)
        comb1(0)
        comb2(0)
        for i in range(NT + 1):
            if i >= 1:
                norm1(i - 1)
            if i + 1 < NT:
                comb1(i + 1)
            if i < NT:
                mmstage(i)
            if i >= 1:
                norm2(i - 1)
                trstage(i - 1)
            if i + 1 < NT:
                comb2(i + 1)
            if i + 2 < NT:
                loads(i + 2)
        sc.emit()

    TT_ = 256
    with contextlib.ExitStack() as st:
        stage = [Slot(k.sb(st, f"wstagee{i}", [128, 1408]), ld_ch[i]) for i in range(2)]
        gffn = load_cols(st, "gffn", ln_ffn_g, 8)
        Wup, Wupb = load_w(st, stage, "Wup", w_up, D, 2 * D_FF, gffn)
        Wdn, Wdnb = load_w(st, stage, "Wdn", w_down, D_FF, D)
        fw = k.sb(st, "fw", [128, 44, 3])
        fwb = Buf("fw")
        for tap in range(3):
            sc.add("sp", _dma(fw[:, :, tap], conv_ffn_w[tap].rearrange("(g p) -> p g", p=128),
                              allow_slow_non_contiguous=True), w=[Buf()], chan=sc.chan(f"c_fw{tap}"))
        fb = load_cols(st, "fb", conv_ffn_b, 44)
        sc.barrier()
        XS = [Slot(k.sb(st, f"xs{i}", [128, 8, TT_ + 2], BF16), sc.chan(f"c_xs{i}")) for i in range(2)]
        H1 = [Slot(k.sb(st, f"h1t{i}", [128, 2, D]), sc.chan(f"c_h1t{i}")) for i in range(2)]
        AT = [Slot(k.sb(st, f"at{i}", [128, 22, TT_], BF16)) for i in range(2)]
        CG = [Slot(k.sb(st, f"cg{i}", [128, TT_])) for i in range(2)]
        CV = [Slot(k.sb(st, f"cv{i}", [128, TT_])) for i in range(2)]
        SG = [Slot(k.sb(st, f"sg{i}", [128, TT_])) for i in range(2)]
        MB = [Slot(k.ps(st, f"mbe{i}", [128, 512])) for i in range(8)]
        mbi = 0
        for i in range(S // TT_):
            t0 = i * TT_
            xs, h1, at = XS[i % 2], H1[i % 2], AT[i % 2]
            sc.add("sp", _dma(xs.t[:], xn2T3[:, :, t0:t0 + TT_ + 2]), w=[xs.b], chan=xs.c)
            sc.add("sp", _dma(h1.t[:], h1_s[t0:t0 + TT_, :].rearrange("(b p) d -> p b d", p=128)), w=[h1.b], chan=h1.c)
            for g in range(22):
                res = []
                for which, (gi, dst) in enumerate(((g, CG[g % 2]), (22 + g, CV[g % 2]))):
                    pm = MB[mbi % 8]
                    mbi += 1
                    for kc in range(8):
                        sc.add("pe", _mm(pm.t[:, 0:TT_ + 2], Wup[:, kc, gi * 128:(gi + 1) * 128], xs.t[:, kc, :], kc == 0, kc == 7),
                               r=[xs.b, Wupb], w=[pm.b])
                    sc.add("act", _act(dst.t[:], pm.t[:, 0:TT_], AF.Identity, scale=fw[:, gi, 0:1], bias=fb[0][:, gi:gi + 1]),
                           r=[pm.b], w=[dst.b])
                    sc.add("dve", _stt(dst.t[:], pm.t[:, 1:TT_ + 1], fw[:, gi, 1:2], dst.t[:], ALU.mult, ALU.add), r=[pm.b, dst.b], w=[dst.b])
                    sc.add("dve", _stt(dst.t[:], pm.t[:, 2:TT_ + 2], fw[:, gi, 2:3], dst.t[:], ALU.mult, ALU.add), r=[pm.b, dst.b], w=[dst.b])
                cg, cv, sg = CG[g % 2], CV[g % 2], SG[g % 2]
                sc.add("act", _act(sg.t[:], cg.t[:], AF.Silu), r=[cg.b], w=[sg.b])
                sc.add("pool", _tt(at.t[:, g, :], sg.t[:], cv.t[:], ALU.mult), r=[sg.b, cv.b], w=[at.b])
            for b in range(TT_ // 128):
                for half in range(2):
                    pm = MB[mbi % 8]
                    mbi += 1
                    for g in range(22):
                        sc.add("pe", _mm(pm.t[:], at.t[:, g, b * 128:(b + 1) * 128], Wdn[:, g, half * 512:(half + 1) * 512], g == 0, g == 21),
                               r=[at.b, Wdnb], w=[pm.b])
                    sc.add("dve", _tt(h1.t[:, b, half * 512:(half + 1) * 512], pm.t[:], h1.t[:, b, half * 512:(half + 1) * 512], ALU.add),
                           r=[pm.b, h1.b], w=[h1.b])
            sc.add("pool", _dma(h2_s[t0:t0 + TT_, :].rearrange("(b p) d -> p b d", p=128), h1.t[:]), r=[h1.b], chan=h1.c)
        sc.emit()

    with contextlib.ExitStack() as st:
        stage = [Slot(k.sb(st, f"wstagef{i}", [128, 1024]), ld_ch[i]) for i in range(2)]
        gple = load_cols(st, "gple", ple_norm_g, 8)
        Wg, Wgb = load_w(st, stage, "Wg", w_ple_gate, D, D, gple)
        Wp, Wpb = load_w(st, stage, "Wp", w_ple_proj, 256, D)
        postg = load_bcast(st, "postg", ple_post_g, D)
        fing = load_bcast(st, "fing", final_g, D)
        sc.barrier()
        XT = [Slot(k.sb(st, f"xtf{i}", [128, 4, D]), sc.chan(f"c_xtf{i}")) for i in range(3)]
        PTL = [Slot(k.sb(st, f"ptl{i}", [128, 4, 256]), sc.chan(f"c_ptl{i}")) for i in range(2)]
        XBF = [Slot(k.sb(st, f"xbf{i}", [128, 4, D], BF16)) for i in range(2)]
        PBF = [Slot(k.sb(st, f"pbf{i}", [128, 4, 256], BF16)) for i in range(2)]
        XNF = [Slot(k.sb(st, f"xnf{i}", [128, 8, 512], BF16)) for i in range(2)]
        PTTF = [Slot(k.sb(st, f"ptt{i}", [128, 2, 512], BF16)) for i in range(2)]
        junk = Slot(k.sb(st, "junkf", [128, D], BF16))
        ss = Slot(k.sb(st, "ssf", [128, 4]))
        ss2 = Slot(k.sb(st, "ss2f", [128, 4]))
        rstd = Slot(k.sb(st, "rstdf", [128, 4]))
        ssb = Slot(k.sb(st, "ssb", [128, 2]))
        ssb2 = Slot(k.sb(st, "ssb2", [128, 2]))
        rsb2 = Slot(k.sb(st, "rsb2", [128, 2]))
        SGM = [Slot(k.sb(st, f"sgm{i}", [128, D])) for i in range(2)]
        PJ = [Slot(k.sb(st, f"pj{i}", [128, D])) for i in range(2)]
        OT = [Slot(k.sb(st, f"ot{i}", [128, D]), sc.chan(f"c_ot{i}")) for i in range(3)]
        TBs = [Slot(k.ps(st, f"tbf{i}", [128, 1024], BF16)) for i in range(2)]
        MB = [Slot(k.ps(st, f"mbf{i}", [128, 512])) for i in range(6)]
        cntF = {"mbi": 0, "bi": 0}

        def f_stage1a(i):
            t0 = i * 512
            xt, ptl, XB, PBf = XT[i % 3], PTL[i % 2], XBF[i % 2], PBF[i % 2]
            sc.add("sp", _dma(xt.t[:], h2_s[t0:t0 + 512, :].rearrange("(b p) d -> p b d", p=128)), w=[xt.b], chan=xt.c)
            sc.add("sp", _dma(ptl.t[:], p_in[t0:t0 + 512, :].rearrange("(b p) d -> p b d", p=128)), w=[ptl.b], chan=ptl.c)
            for b in range(4):
                sc.add("act", _act(junk.t[:], xt.t[:, b, :], AF.Square, scale=1.0 / 32.0, accum_out=ss.t[:, b:b + 1]),
                       r=[xt.b], w=[junk.b, ss.b])
            sc.add("dve", _ts(ss2.t[:], ss.t[:], EPS, None, ALU.add), r=[ss.b], w=[ss2.b])
            sc.add("pool", _tt(rstd.t[:], ss2.t[:], mhalf[:, 0:4], ALU.pow), r=[ss2.b], w=[rstd.b])
            for b in range(4):
                e = "dve" if b % 2 else "act"
                sc.add(e, scale_cast(e, XB.t[:, b, :], xt.t[:, b, :], rstd.t[:, b:b + 1]), r=[xt.b, rstd.b], w=[XB.b])
            sc.add("act", _act(PBf.t[:], ptl.t[:], AF.Copy), r=[ptl.b], w=[PBf.b])

        def f_stage1b(i):
            XN, PTT, XB, PBf = XNF[i % 2], PTTF[i % 2], XBF[i % 2], PBF[i % 2]
            for j in range(4):
                tb = TBs[j % 2]
                for kk in range(2):
                    kc = 2 * j + kk
                    for b in range(4):
                        sc.add("pe", _tr(tb.t[:, kk * 512 + b * 128:kk * 512 + (b + 1) * 128],
                                         XB.t[:, b, kc * 128:(kc + 1) * 128], ident[:]), r=[XB.b], w=[tb.b])
                e = "dve" if j % 2 else "act"
                sc.add(e, scale_cast(e, XN.t[:, 2 * j:2 * j + 2, :], tb.t[:].rearrange("p (a b) -> p a b", a=2)),
                       r=[tb.b], w=[XN.b])
            tb = TBs[0]
            for kc in range(2):
                for b in range(4):
                    sc.add("pe", _tr(tb.t[:, kc * 512 + b * 128:kc * 512 + (b + 1) * 128], PBf.t[:, b, kc * 128:(kc + 1) * 128], ident[:]),
                           r=[PBf.b], w=[tb.b])
            sc.add("act", _act(PTT.t[:], tb.t[:].rearrange("p (a b) -> p a b", a=2), AF.Copy), r=[tb.b], w=[PTT.b])

        RN = 4
        SGM3 = SGM + [Slot(k.sb(st, f"sgm{i}", [128, D])) for i in range(2, RN)]
        PJ3 = PJ + [Slot(k.sb(st, f"pj{i}", [128, D])) for i in range(2, RN)]
        SSB = [Slot(k.sb(st, f"ssbr{i}", [128, 2])) for i in range(RN)]
        RSB2 = [Slot(k.sb(st, f"rsbr{i}", [128, 2])) for i in range(RN)]
        junk2 = Slot(k.sb(st, "junkf2", [128, D], BF16))

        def blk(n):
            i, b = divmod(n, 4)
            return i, b, XT[i % 3], SGM3[n % RN], PJ3[n % RN], SSB[n % RN], RSB2[n % RN], OT[n % 3]

        def f_a12(n):
            i, b, xt, sgm, pj, ssb_, rsb_, ot = blk(n)
            XN, PTT = XNF[i % 2], PTTF[i % 2]
            mbi = cntF["mbi"]
            tok = slice(b * 128, (b + 1) * 128)
            for half in range(2):
                hs = slice(half * 512, (half + 1) * 512)
                pm = MB[mbi % 6]
                mbi += 1
                for kc in range(8):
                    sc.add("pe", _mm(pm.t[:], XN.t[:, kc, tok], Wg[:, kc, hs], kc == 0, kc == 7), r=[XN.b, Wgb], w=[pm.b])
                sc.add("act", _act(sgm.t[:, hs], pm.t[:], AF.Sigmoid), r=[pm.b], w=[sgm.b])
                pm = MB[mbi % 6]
                mbi += 1
                for kc in range(2):
                    sc.add("pe", _mm(pm.t[:], PTT.t[:, kc, tok], Wp[:, kc, hs], kc == 0, kc == 1), r=[PTT.b, Wpb], w=[pm.b])
                sc.add("act", _act(pj.t[:, hs], pm.t[:], AF.Copy), r=[pm.b], w=[pj.b])
            cntF["mbi"] = mbi
            sc.add("act", _act(junk.t[:], pj.t[:], AF.Square, scale=1.0 / 32.0, accum_out=ssb_.t[:, 0:1]), r=[pj.b], w=[junk.b, ssb_.b])

        def f_d12(n):
            i, b, xt, sgm, pj, ssb_, rsb_, ot = blk(n)
            sc.add("dve", _ts(ssb_.t[:, 0:1], ssb_.t[:, 0:1], EPS, None, ALU.add), r=[ssb_.b], w=[ssb_.b])
            sc.add("pool", _tt(rsb_.t[:, 0:1], ssb_.t[:, 0:1], mhalf[:, 0:1], ALU.pow), r=[ssb_.b], w=[rsb_.b])
            sc.add("dve", _stt(sgm.t[:], sgm.t[:], rsb_.t[:, 0:1], postg[0][:], ALU.mult, ALU.mult), r=[sgm.b, rsb_.b, postg[1]], w=[sgm.b])
            sc.add("dve", _tt(pj.t[:], pj.t[:], sgm.t[:], ALU.mult), r=[pj.b, sgm.b], w=[pj.b])
            sc.add("dve", _tt(pj.t[:], pj.t[:], xt.t[:, b, :], ALU.add), r=[pj.b, xt.b], w=[pj.b])

        def f_a3(n):
            i, b, xt, sgm, pj, ssb_, rsb_, ot = blk(n)
            sc.add("act", _act(junk2.t[:], pj.t[:], AF.Square, scale=1.0 / 32.0, accum_out=ssb_.t[:, 1:2]), r=[pj.b], w=[junk2.b, ssb_.b])

        def f_d3(n):
            i, b, xt, sgm, pj, ssb_, rsb_, ot = blk(n)
            t0 = i * 512
            sc.add("dve", _ts(ssb_.t[:, 1:2], ssb_.t[:, 1:2], EPS, None, ALU.add), r=[ssb_.b], w=[ssb_.b])
            sc.add("pool", _tt(rsb_.t[:, 1:2], ssb_.t[:, 1:2], mhalf[:, 0:1], ALU.pow), r=[ssb_.b], w=[rsb_.b])
            sc.add("dve", _stt(ot.t[:], pj.t[:], rsb_.t[:, 1:2], fing[0][:], ALU.mult, ALU.mult), r=[pj.b, rsb_.b, fing[1]], w=[ot.b])
            sc.add("sp", _dma(out[t0 + b * 128:t0 + (b + 1) * 128, :], ot.t[:]), r=[ot.b], chan=ot.c)

        f_stage1a(0)
        f_stage1b(0)
        if NT > 1:
            f_stage1a(1)
        NBLK = NT * 4
        for n in range(-2, NBLK + 1):
            if 0 <= n + 2 < NBLK:
                f_a12(n + 2)
                i2, b2 = divmod(n + 2, 4)
                if b2 == 1:
                    if i2 + 1 < NT:
                        f_stage1b(i2 + 1)
                    if i2 + 2 < NT:
                        f_stage1a(i2 + 2)
            if 0 <= n + 1 < NBLK:
                f_d12(n + 1)
            if 0 <= n < NBLK:
                f_a3(n)
            if 0 <= n - 1 < NBLK:
                f_d3(n - 1)
        sc.emit()

    k.final_wait = None
    return k


def finish(k):
    return k.nc


_W_NAMES = ["ln_mix_g", "w_in", "b_gates", "conv_qk_w", "conv_qk_b", "mlstm_norm_g", "q_norm_g", "w_uq", "kv_norm_g",
            "w_ukv", "w_out", "ln_ffn_g", "w_up", "conv_ffn_w", "conv_ffn_b", "w_down", "ple_norm_g", "w_ple_gate",
            "w_ple_proj", "ple_post_g"]


def kernel(**inputs):
    x = np.asarray(inputs["x"])
    p = np.asarray(inputs["p"])
    B, S, _ = x.shape
    nc = finish(build(S))
    shared = {n: np.ascontiguousarray(np.asarray(inputs[n])[0], dtype=np.float32) for n in _W_NAMES}
    shared["final_g"] = np.ascontiguousarray(np.asarray(inputs["final_g"]), dtype=np.float32)
    in_maps = []
    for b in range(B):
        m = dict(shared)
        m["x"] = np.ascontiguousarray(x[b], dtype=np.float32)
        m["p"] = np.ascontiguousarray(p[0, b], dtype=np.float32)
        in_maps.append(m)
    res = run_bass_kernel_spmd(nc, in_maps, core_ids=list(range(B)))
    return np.stack([np.asarray(r["out"]) for r in res.results], axis=0).astype(np.float32)
```

```python
import contextlib
import numpy as np
import concourse.bass as bass
import concourse.mybir as mybir
from concourse.bass_utils import run_bass_kernel_spmd

F32 = mybir.dt.float32
BF16 = mybir.dt.bfloat16
AF = mybir.ActivationFunctionType
ALU = mybir.AluOpType
AX = mybir.AxisListType

D = 1024
NH = 4
IN_COLS = 2512
D_FF = 2816
EPS = 1e-6
SEM_MAX = 30000


class Buf:
    __slots__ = ("name", "lw", "rd")

    def __init__(self, name=""):
        self.name = name
        self.lw = None
        self.rd = []


class Chan:
    __slots__ = ("sem", "count", "last")

    def __init__(self, sem):
        self.sem = sem
        self.count = 0
        self.last = None


class Op:
    __slots__ = ("eng", "fn", "deps", "sig", "signo", "chan", "cval", "done")


class Sched:
    ENGS = ("pe", "act", "dve", "pool", "sp")

    def __init__(self, nc, stack):
        self.nc = nc
        self.stack = stack
        self.ops = []
        self.last_on = {e: None for e in self.ENGS}
        self.pending_bar = {e: [] for e in self.ENGS}
        self.chans = []
        self.free_chans = []
        self.phase_chans = []
        self.cnt = {e: 0 for e in self.ENGS}
        self.sems = {e: [] for e in self.ENGS}
        self.waited = {e: {} for e in self.ENGS}

    def chan(self, name, keep=False):
        if self.free_chans and not keep:
            c = self.free_chans.pop()
        else:
            c = Chan(self.stack.enter_context(self.nc.semaphore(name)))
            self.chans.append(c)
        if not keep:
            self.phase_chans.append(c)
        return c

    def add(self, eng, fn, r=(), w=(), chan=None):
        op = Op()
        op.eng, op.fn, op.deps, op.sig, op.signo, op.chan, op.cval = eng, fn, {}, False, 0, chan, 0
        op.done = False
        for b in r:
            if b.lw is not None:
                op.deps[b.lw] = True
        for b in w:
            if b.lw is not None:
                op.deps.setdefault(b.lw, False)
            for q in b.rd:
                op.deps.setdefault(q, False)
        for b in r:
            b.rd.append(op)
        for b in w:
            b.lw = op
            b.rd = []
        if self.pending_bar[eng]:
            for d in self.pending_bar[eng]:
                op.deps[d] = True
            self.pending_bar[eng] = []
        if chan is not None:
            if chan.last is not None:
                op.deps[chan.last] = True
            chan.count += 16
            op.cval = chan.count
            chan.last = op
        op.deps.pop(op, None)
        self.ops.append(op)
        self.last_on[eng] = op
        return op

    def barrier(self):
        lasts = [o for o in self.last_on.values() if o is not None]
        lasts += [c.last for c in self.chans if c.last is not None]
        for e in self.ENGS:
            self.pending_bar[e] = list(lasts)

    def emit(self):
        nc = self.nc
        fin = self.add("sp", lambda e: e.nop())
        for c in self.chans:
            if c.last is not None and not c.last.done:
                fin.deps[c.last] = True
        for e in self.ENGS:
            self.pending_bar[e] = []
        for op in self.ops:
            for d in [d for d in op.deps if d.done]:
                del op.deps[d]
            for d, raw in op.deps.items():
                if d.chan is not None:
                    continue
                if d.eng == op.eng and (op.eng == "pe" or not raw):
                    continue
                d.sig = True
        cnt = self.cnt
        for op in self.ops:
            if op.chan is None and op.sig:
                cnt[op.eng] += 1
                op.signo = cnt[op.eng]
        sems = self.sems
        for e in self.ENGS:
            n = cnt[e] // SEM_MAX + 1
            while len(sems[e]) < n:
                sems[e].append(self.stack.enter_context(nc.semaphore(f"s_{e}{len(sems[e])}")))
        per = {e: [o for o in self.ops if o.eng == e] for e in self.ENGS}
        handles = {"pe": "tensor", "act": "scalar", "dve": "vector", "pool": "gpsimd", "sp": "sync"}

        def run(e, eng):
            waited = self.waited[e]
            for op in per[e]:
                for d, raw in op.deps.items():
                    if d.chan is not None:
                        key, val, sem = ("c", id(d.chan)), d.cval, d.chan.sem
                    else:
                        if d.eng == e and (e == "pe" or not raw):
                            continue
                        j = (d.signo - 1) // SEM_MAX
                        key, val, sem = (d.eng, j), d.signo - j * SEM_MAX, sems[d.eng][j]
                    if waited.get(key, 0) >= val:
                        continue
                    waited[key] = val
                    eng.wait_ge(sem, val)
                ins = op.fn(eng)
                if op.chan is not None:
                    ins.then_inc(op.chan.sem, 16)
                elif op.sig:
                    j = (op.signo - 1) // SEM_MAX
                    ins.then_inc(sems[e][j], 1)

        with nc.Block() as block:
            for e in self.ENGS:
                if per[e]:
                    getattr(block, handles[e])(lambda eng, e=e: run(e, eng))
        for op in self.ops:
            op.done = True
            op.fn = None
            op.deps = {}
        self.ops = []
        self.last_on = {e: None for e in self.ENGS}
        for c in self.phase_chans:
            c.last = None
        self.free_chans.extend(self.phase_chans)
        self.phase_chans = []


def _act(out, in_, func, **kw):
    return lambda e: e.activation(out, in_, func, **kw)


def _ts(out, in0, s1, s2, op0, op1=None):
    if op1 is None:
        return lambda e: e.tensor_scalar(out, in0, s1, None, op0)
    return lambda e: e.tensor_scalar(out, in0, s1, s2, op0, op1)


def _stt(out, in0, sc, in1, op0, op1):
    return lambda e: e.scalar_tensor_tensor(out, in0, sc, in1, op0, op1)


def _tt(out, in0, in1, op):
    return lambda e: e.tensor_tensor(out, in0, in1, op)


def _cp(out, in_):
    return lambda e: e.tensor_copy(out, in_)


def _mm(out, lhsT, rhs, start, stop):
    return lambda e: e.matmul(out, lhsT, rhs, start=start, stop=stop)


def _tr(out, in_, ident):
    return lambda e: e.transpose(out, in_, ident)


def _dma(out, in_, **kw):
    return lambda e: e.dma_start(out=out, in_=in_, **kw)


def _memset(ap, v):
    return lambda e: e.memset(ap, v)


class K:
    def __init__(self, S, dbg=()):
        self.S = S
        self.dbg = dbg
        self.nc = bass.Bass("TRN2", target_bir_lowering=False)
        self.stack = contextlib.ExitStack()
        self.sc = Sched(self.nc, self.stack)

    def sb(self, st, name, shape, dt=F32):
        return st.enter_context(self.nc.sbuf_tensor(name, list(shape), dt))

    def ps(self, st, name, shape, dt=F32):
        return st.enter_context(self.nc.psum_tensor(name, list(shape), dt))

    def dram_in(self, name, shape, dt=F32):
        return self.nc.dram_tensor(name, list(shape), dt, kind="ExternalInput").ap()

    def dram_out(self, name, shape, dt=F32):
        return self.nc.dram_tensor(name, list(shape), dt, kind="ExternalOutput").ap()

    def dram_tmp(self, name, shape, dt=F32):
        if name in self.dbg:
            return self.nc.dram_tensor(name, list(shape), dt, kind="ExternalOutput").ap()
        return self.nc.dram_tensor(name, list(shape), dt).ap()


class Slot:
    def __init__(self, t, chan=None):
        self.t = t
        self.b = Buf()
        self.c = chan


def build(S, dbg=()):
    k = K(S, dbg)
    nc, sc = k.nc, k.sc
    NT, NB = S // 512, S // 128
    top = k.stack

    x = k.dram_in("x", [S, D])
    p_in = k.dram_in("p", [S, 256])
    ln_mix_g = k.dram_in("ln_mix_g", [D])
    w_in = k.dram_in("w_in", [D, IN_COLS])
    b_gates = k.dram_in("b_gates", [16])
    conv_qk_w = k.dram_in("conv_qk_w", [3, 1024])
    conv_qk_b = k.dram_in("conv_qk_b", [1024])
    mlstm_norm_g = k.dram_in("mlstm_norm_g", [512])
    q_norm_g = k.dram_in("q_norm_g", [256])
    w_uq = k.dram_in("w_uq", [256, 768])
    kv_norm_g = k.dram_in("kv_norm_g", [128])
    w_ukv = k.dram_in("w_ukv", [128, 1024])
    w_out = k.dram_in("w_out", [1024, 1024])
    ln_ffn_g = k.dram_in("ln_ffn_g", [D])
    w_up = k.dram_in("w_up", [D, 2 * D_FF])
    conv_ffn_w = k.dram_in("conv_ffn_w", [3, 2 * D_FF])
    conv_ffn_b = k.dram_in("conv_ffn_b", [2 * D_FF])
    w_down = k.dram_in("w_down", [D_FF, D])
    ple_norm_g = k.dram_in("ple_norm_g", [D])
    w_ple_gate = k.dram_in("w_ple_gate", [D, D])
    w_ple_proj = k.dram_in("w_ple_proj", [256, D])
    ple_post_g = k.dram_in("ple_post_g", [D])
    final_g = k.dram_in("final_g", [D])
    out = k.dram_out("out", [S, D])

    qkT = k.dram_tmp("qkT", [1024, S], BF16)
    vo_s = k.dram_tmp("vo_s", [S, 512])
    so_s = k.dram_tmp("so_s", [S, 512], BF16)
    g3_s = k.dram_tmp("g3_s", [S, 464])

    ident = k.sb(top, "ident", [128, 128], BF16)
    ones_f = k.sb(top, "ones_f", [128, 128], F32)
    mhalf = k.sb(top, "mhalf", [128, 16], F32)
    cb = Buf("consts")
    sc.add("pool", _memset(ones_f[:], 1.0), w=[cb])
    sc.add("pool", _memset(mhalf[:], -0.5), w=[cb])
    sc.add("pool", lambda e: e.affine_select(ident[:], ones_f[:], [[-1, 128]], ALU.is_equal, 0.0,
                                             base=0, channel_multiplier=1), r=[cb], w=[cb])
    sc.emit()

    ld_ch = [sc.chan(f"ldw{i}", keep=True) for i in range(2)]
    cnt = {"w": 0, "e": 0}

    def alt():
        cnt["e"] += 1
        return "dve" if cnt["e"] % 2 else "act"

    def scale_cast(eng, out_ap, in_ap, sc_ap=None):
        if eng == "act":
            if sc_ap is None:
                return _act(out_ap, in_ap, AF.Copy)
            return _act(out_ap, in_ap, AF.Copy, scale=sc_ap)
        if sc_ap is None:
            return _cp(out_ap, in_ap)
        return _ts(out_ap, in_ap, sc_ap, None, ALU.mult)

    def load_cols(st, name, src, G):
        t = k.sb(st, name, [128, G])
        b = Buf(name)
        ch = sc.chan("c_" + name)
        sc.add("sp", _dma(t[:], src.rearrange("(g p) -> p g", p=128), allow_slow_non_contiguous=True),
               w=[b], chan=ch)
        return t, b

    def load_bcast(st, name, src, n):
        t = k.sb(st, name, [128, n])
        b = Buf(name)
        ch = sc.chan("c_" + name)
        sc.add("sp", _dma(t[:], bass.AP(src.tensor, src.offset, [[0, 128], [1, n]])), w=[b], chan=ch)
        return t, b

    def load_w(st, stage, name, src, Kdim, cols, gain=None):
        kcn = Kdim // 128
        t = k.sb(st, name, [128, kcn, cols], BF16)
        b = Buf(name)
        for kc in range(kcn):
            sw = stage[0].t.shape[1]
            for c0 in range(0, cols, sw):
                w = min(sw, cols - c0)
                s = stage[cnt["w"] % 2]
                cnt["w"] += 1
                sc.add("sp", _dma(s.t[:, 0:w], src[kc * 128:(kc + 1) * 128, c0:c0 + w]), w=[s.b], chan=s.c)
                g = None if gain is None else gain[0][:, kc:kc + 1]
                rr = [s.b] + ([] if gain is None else [gain[1]])
                e = alt()
                sc.add(e, scale_cast(e, t[:, kc, c0:c0 + w], s.t[:, 0:w], g), r=rr, w=[b])
        return t, b

    with contextlib.ExitStack() as st:
        stage = [Slot(k.sb(st, f"wstage{i}", [128, 2816]), ld_ch[i]) for i in range(2)]
        gmix = load_cols(st, "gmix", ln_mix_g, 8)
        Win, Winb = load_w(st, stage, "Win", w_in, D, IN_COLS, gmix)
        cw = k.sb(st, "cw", [128, 8, 3])
        cwb = Buf("cw")
        for tap in range(3):
            sc.add("sp", _dma(cw[:, :, tap], conv_qk_w[tap].rearrange("(g p) -> p g", p=128),
                              allow_slow_non_contiguous=True), w=[Buf()], chan=sc.chan(f"c_cw{tap}"))
        cbias = load_cols(st, "cbias", conv_qk_b, 8)
        bg = load_bcast(st, "bg", b_gates, 16)
        sc.barrier()

        XT = [Slot(k.sb(st, f"xt{i}", [128, 4, D]), sc.chan(f"c_xt{i}")) for i in range(3)]
        XBA = [Slot(k.sb(st, f"xb{i}", [128, 4, D], BF16)) for i in range(2)]
        XN = [Slot(k.sb(st, f"xn{i}", [128, 8, 512], BF16)) for i in range(2)]
        junk = Slot(k.sb(st, "junk", [128, D], BF16))
        ss = Slot(k.sb(st, "ss", [128, 4]))
        ss2 = Slot(k.sb(st, "ss2", [128, 4]))
        rstd = Slot(k.sb(st, "rstd", [128, 4]))
        PRE = [Slot(k.sb(st, f"pre{g}", [128, 514])) for g in range(8)]
        ACC = [Slot(k.sb(st, f"acc{i}", [128, 512])) for i in range(2)]
        QKB = [Slot(k.sb(st, f"qkb{i}", [128, 512], BF16), sc.chan(f"c_qkb{i}")) for i in range(3)]
        VO = [Slot(k.sb(st, f"vo{i}", [128, 1024]), sc.chan(f"c_vo{i}")) for i in range(2)]
        G3 = [Slot(k.sb(st, f"g3{i}", [128, 464]), sc.chan(f"c_g3{i}")) for i in range(2)]
        SOB = [Slot(k.sb(st, f"sob{i}", [128, 512], BF16), sc.chan(f"c_sob{i}")) for i in range(2)]
        TB = [Slot(k.ps(st, f"tb{i}", [128, 1024], BF16)) for i in range(2)]
        MB = [Slot(k.ps(st, f"mb{i}", [128, 512])) for i in range(6)]
        last8 = Slot(k.sb(st, "last8", [128, 8]))
        last8b = Slot(k.sb(st, "last8b", [128, 8], BF16), sc.chan("c_last8"))
        for g in range(8):
            sc.add("pool", _memset(PRE[g].t[:, 0:2], 0.0), w=[PRE[g].b])
        cntA = {"mbi": 0, "qi": 0, "voi": 0}

        def stage1a(i):
            t0 = i * 512
            xt, XB = XT[i % 3], XBA[i % 2]
            sc.add("sp", _dma(xt.t[:], x[t0:t0 + 512, :].rearrange("(b p) d -> p b d", p=128)), w=[xt.b], chan=xt.c)
            for b in range(4):
                sc.add("act", _act(junk.t[:], xt.t[:, b, :], AF.Square, scale=1.0 / 32.0, accum_out=ss.t[:, b:b + 1]),
                       r=[xt.b], w=[junk.b, ss.b])
            sc.add("dve", _ts(ss2.t[:], ss.t[:], EPS, None, ALU.add), r=[ss.b], w=[ss2.b])
            sc.add("pool", _tt(rstd.t[:], ss2.t[:], mhalf[:, 0:4], ALU.pow), r=[ss2.b], w=[rstd.b])
            for b in range(4):
                e = "dve" if b % 2 else "act"
                sc.add(e, scale_cast(e, XB.t[:, b, :], xt.t[:, b, :], rstd.t[:, b:b + 1]), r=[xt.b, rstd.b], w=[XB.b])

        def stage1b(i):
            xn, XB = XN[i % 2], XBA[i % 2]
            for j in range(4):
                tb = TB[j % 2]
                for kk in range(2):
                    kc = 2 * j + kk
                    for b in range(4):
                        sc.add("pe", _tr(tb.t[:, kk * 512 + b * 128:kk * 512 + (b + 1) * 128],
                                         XB.t[:, b, kc * 128:(kc + 1) * 128], ident[:]), r=[XB.b], w=[tb.b])
                e = "dve" if j % 2 else "act"
                sc.add(e, scale_cast(e, xn.t[:, 2 * j:2 * j + 2, :], tb.t[:].rearrange("p (a b) -> p a b", a=2)),
                       r=[tb.b], w=[xn.b])

        def stage2fm(i):
            t0 = i * 512
            xn = XN[i % 2]
            mbi, qi, voi = cntA["mbi"], cntA["qi"], cntA["voi"]
            def tail(g, qi):
                pre = PRE[g]
                acc = ACC[g % 2]
                qb = QKB[qi % 3]
                sc.add("dve", _ts(acc.t[:], pre.t[:, 2:514], cw[:, g, 2:3], None, ALU.mult), r=[pre.b], w=[acc.b])
                sc.add("dve", _stt(acc.t[:], pre.t[:, 1:513], cw[:, g, 1:2], acc.t[:], ALU.mult, ALU.add),
                       r=[pre.b, acc.b], w=[acc.b])
                sc.add("dve", _stt(acc.t[:], pre.t[:, 0:512], cw[:, g, 0:1], acc.t[:], ALU.mult, ALU.add),
                       r=[pre.b, acc.b], w=[acc.b])
                sc.add("act", _act(qb.t[:], acc.t[:], AF.Silu, bias=cbias[0][:, g:g + 1]), r=[acc.b], w=[qb.b])
                if i == 0:
                    sc.add("pool", _dma(qkT[g * 128:(g + 1) * 128, 0:511], qb.t[:, 1:512]), r=[qb.b], chan=qb.c)
                else:
                    sc.add("pool", _dma(qkT[g * 128:(g + 1) * 128, t0 - 1:t0 + 511], qb.t[:]), r=[qb.b], chan=qb.c)
                sc.add("dve", _cp(pre.t[:, 0:2], pre.t[:, 512:514]), r=[pre.b], w=[pre.b])

            for g in range(8):
                pm = MB[mbi % 6]
                mbi += 1
                for kc in range(8):
                    sc.add("pe", _mm(pm.t[:], Win[:, kc, g * 128:(g + 1) * 128], xn.t[:, kc, :], kc == 0, kc == 7),
                           r=[xn.b, Winb], w=[pm.b])
                sc.add("act", _act(PRE[g].t[:, 2:514], pm.t[:], AF.Copy), r=[pm.b], w=[PRE[g].b])
                if g >= 1:
                    tail(g - 1, qi)
                    qi += 1
            tail(7, qi)
            qi += 1
            cntA["mbi"], cntA["qi"], cntA["voi"] = mbi, qi, voi

        def stage2tm(i):
            t0 = i * 512
            xn = XN[i % 2]
            mbi, qi, voi = cntA["mbi"], cntA["qi"], cntA["voi"]
            for b in range(4):
                vo = VO[voi % 2]
                g3 = G3[voi % 2]
                sob = SOB[voi % 2]
                voi += 1
                for part, (c0, c1) in enumerate(((1024, 1536), (1536, 2048), (2048, 2512))):
                    pm = MB[mbi % 6]
                    mbi += 1
                    for kc in range(8):
                        sc.add("pe", _mm(pm.t[:, 0:c1 - c0], xn.t[:, kc, b * 128:(b + 1) * 128], Win[:, kc, c0:c1],
                                         kc == 0, kc == 7), r=[xn.b, Winb], w=[pm.b])
                    if part == 0:
                        sc.add("dve", _cp(vo.t[:, 0:512], pm.t[:]), r=[pm.b], w=[vo.b])
                    elif part == 1:
                        sc.add("act", _act(vo.t[:, 512:1024], pm.t[:], AF.Tanh, scale=0.5), r=[pm.b], w=[vo.b])
                        sc.add("dve", _ts(sob.t[:], vo.t[:, 512:1024], 0.5, 0.5, ALU.mult, ALU.add),
                               r=[vo.b], w=[sob.b])
                    else:
                        sc.add("act", _act(g3.t[:], pm.t[:, 0:464], AF.Copy), r=[pm.b], w=[g3.b])
                        sc.add("dve", _tt(g3.t[:, 0:16], g3.t[:, 0:16], bg[0][:], ALU.add), r=[g3.b], w=[g3.b])
                r0 = t0 + b * 128
                sc.add("pool", _dma(vo_s[r0:r0 + 128, :], vo.t[:, 0:512]), r=[vo.b], chan=vo.c)
                sc.add("pool", _dma(so_s[r0:r0 + 128, :], sob.t[:]), r=[sob.b], chan=sob.c)
                sc.add("pool", _dma(g3_s[r0:r0 + 128, :], g3.t[:]), r=[g3.b], chan=g3.c)
            cntA["mbi"], cntA["qi"], cntA["voi"] = mbi, qi, voi

        stage1a(0)
        stage1b(0)
        if NT > 1:
            stage1a(1)
        for i in range(NT):
            stage2fm(i)
            if i + 1 < NT:
                stage1b(i + 1)
            if i + 2 < NT:
                stage1a(i + 2)
            stage2tm(i)
        prb = [PRE[g].b for g in range(8)]
        for g in range(8):
            sc.add("dve", _ts(last8.t[:, g:g + 1], PRE[g].t[:, 0:1], cw[:, g, 0:1], None, ALU.mult), r=[PRE[g].b], w=[last8.b])
            sc.add("dve", _stt(last8.t[:, g:g + 1], PRE[g].t[:, 1:2], cw[:, g, 1:2], last8.t[:, g:g + 1], ALU.mult, ALU.add),
                   r=[PRE[g].b, last8.b], w=[last8.b])
        sc.add("dve", _tt(last8.t[:], last8.t[:], cbias[0][:], ALU.add), r=[last8.b], w=[last8.b])
        sc.add("act", _act(last8b.t[:], last8.t[:], AF.Silu), r=[last8.b], w=[last8b.b])
        sc.add("pool", _dma(qkT.rearrange("(g p) s -> p g s", p=128)[:, :, S - 1], last8b.t[:],
                            allow_slow_non_contiguous=True), r=[last8b.b], chan=last8b.c)
        sc.emit()

    hf_s = k.dram_tmp("hf_s", [S, 512], BF16)
    hb_s = k.dram_tmp("hb_s", [S, 512], BF16)
    yT_s = k.dram_tmp("yT_s", [1024, S], BF16)

    def bc(ap, m):
        a = [list(d) for d in ap.ap]
        return bass.AP(ap.tensor, ap.offset, a + [[0, m]])

    def bc_mid(ap, m):
        a = [list(d) for d in ap.ap]
        return bass.AP(ap.tensor, ap.offset, [a[0], [0, m]] + a[1:])

    with contextlib.ExitStack() as st:
        maskF = k.sb(st, "maskF", [128, 128])
        maskB = k.sb(st, "maskB", [128, 128])
        mb_ = Buf("masks")
        sc.add("pool", lambda e: e.affine_select(maskF[:], ones_f[:], [[1, 128]], ALU.is_ge, 0.0,
                                                 base=0, channel_multiplier=-1), w=[mb_])
        sc.add("pool", lambda e: e.affine_select(maskB[:], ones_f[:], [[-1, 128]], ALU.is_ge, 0.0,
                                                 base=0, channel_multiplier=1), w=[mb_])
        hdst = (hf_s, hb_s)
        tris = (maskF, maskB)

        def ring(name, shape, dt=F32, chan=False, n=2):
            return [[Slot(k.sb(st, f"{name}{d}_{i}", shape, dt), sc.chan(f"c_{name}{d}_{i}") if chan else None) for i in range(n)]
                    for d in range(2)]

        SL = ring("sl", [128, 8, 512], BF16, True)
        VOT = ring("vot", [128, 512], F32, True)
        GT = ring("gt", [128, 16], F32, True)
        HO = ring("ho", [128, 512], BF16, True)
        AA = ring("aa", [128, 4])
        EG = ring("eg", [128, 8])
        V1 = ring("v1", [128, 4, 130], BF16)
        KTOK = ring("ktok", [128, 4, 128], BF16)
        MM = ring("mm", [128, 4, 128], BF16)
        e1 = [Slot(k.sb(st, f"e1_{d}", [128, 4])) for d in range(2)]
        lsp = [Slot(k.sb(st, f"lsp_{d}", [128, 4])) for d in range(2)]
        tmpa = [Slot(k.sb(st, f"tmpa_{d}", [128, 4])) for d in range(2)]
        C1 = [Slot(k.sb(st, f"c1_{d}", [128, 4, 130])) for d in range(2)]
        C1b = [Slot(k.sb(st, f"c1b_{d}", [128, 4, 130], BF16)) for d in range(2)]
        tmpC = [Slot(k.sb(st, f"tmpc_{d}", [128, 4, 130])) for d in range(2)]
        den = [Slot(k.sb(st, f"den_{d}", [128, 4])) for d in range(2)]
        rr_ = [Slot(k.sb(st, f"rr_{d}", [128, 4])) for d in range(2)]
        TBK = Slot(k.ps(st, "tbk", [128, 1024], BF16))
        SPS = [Slot(k.ps(st, f"sps{i}", [128, 512])) for i in range(2)]
        UPS = Slot(k.ps(st, "ups", [128, 1024]))
        DPS = Slot(k.ps(st, "dps", [128, 1024]))
        GPS = Slot(k.ps(st, "gps", [128, 512]))
        qkT3 = qkT.rearrange("(g p) s -> p g s", p=128)
        slab = [{"cg": None, "n": 0} for _ in range(2)]

        def pre(c, d, n):
            cg = c // 4
            if slab[d]["cg"] != cg:
                slab[d]["cg"] = cg
                slab[d]["n"] += 1
                sl = SL[d][slab[d]["n"] % 2]
                sc.add("sp", _dma(sl.t[:], qkT3[:, :, cg * 512:(cg + 1) * 512]), w=[sl.b], chan=sl.c)
            sl = SL[d][slab[d]["n"] % 2]
            vot, gt, aa, eg, v1, ktok, mm = (VOT[d][n % 2], GT[d][n % 2], AA[d][n % 2], EG[d][n % 2], V1[d][n % 2],
                                             KTOK[d][n % 2], MM[d][n % 2])
            sps = SPS[d]
            r0 = c * 128
            sc.add("sp", _dma(vot.t[:], vo_s[r0:r0 + 128, :]), w=[vot.b], chan=vot.c)
            sc.add("sp", _dma(gt.t[:], g3_s[r0:r0 + 128, 0:16]), w=[gt.b], chan=gt.c)
            io, fo = d * 8, d * 8 + 4
            tri = tris[d]
            sc.add("act", _act(e1[d].t[:], gt.t[:, fo:fo + 4], AF.Exp, scale=-1.0), r=[gt.b], w=[e1[d].b])
            sc.add("act", _act(lsp[d].t[:], e1[d].t[:], AF.Ln, bias=1.0), r=[e1[d].b], w=[lsp[d].b])
            gcol = d * 8
            sc.add("pe", _mm(GPS.t[:, gcol:gcol + 4], tri[:], lsp[d].t[:], True, True), r=[lsp[d].b, mb_], w=[GPS.b])
            sc.add("pe", _mm(GPS.t[:, gcol + 4:gcol + 8], ones_f[:], lsp[d].t[:], True, True), r=[lsp[d].b], w=[GPS.b])
            sc.add("dve", _tt(tmpa[d].t[:], gt.t[:, io:io + 4], GPS.t[:, gcol:gcol + 4], ALU.add), r=[gt.b, GPS.b], w=[tmpa[d].b])
            sc.add("act", _act(aa.t[:], tmpa[d].t[:], AF.Exp), r=[tmpa[d].b], w=[aa.b])
            sc.add("act", _act(eg.t[:], GPS.t[:, gcol:gcol + 8], AF.Exp, scale=-1.0), r=[GPS.b], w=[eg.b])
            sc.add("dve", _tt(v1.t[:, :, 0:128], vot.t[:].rearrange("p (h d) -> p h d", h=4), bc(aa.t[:], 128), ALU.mult),
                   r=[vot.b, aa.b], w=[v1.b])
            sc.add("dve", _cp(v1.t[:, :, 128], aa.t[:]), r=[aa.b], w=[v1.b])
            c4 = (c % 4) * 128
            tcol = d * 512
            for h in range(4):
                sc.add("pe", _tr(TBK.t[:, tcol + h * 128:tcol + (h + 1) * 128], sl.t[:, 4 + h, c4:c4 + 128], ident[:]), r=[sl.b], w=[TBK.b])
            sc.add("act", _act(ktok.t[:], TBK.t[:, tcol:tcol + 512].rearrange("p (h d) -> p h d", h=4), AF.Copy, scale=128.0 ** -0.5),
                   r=[TBK.b], w=[ktok.b])
            for h in range(4):
                sc.add("pe", _mm(sps.t[:, h * 128:(h + 1) * 128], sl.t[:, 4 + h, c4:c4 + 128], sl.t[:, h, c4:c4 + 128], True, True),
                       r=[sl.b], w=[sps.b])
            sc.add("dve", _stt(mm.t[:], sps.t[:].rearrange("p (h d) -> p h d", h=4), 128.0 ** -0.5, bc_mid(tri[:], 4), ALU.mult, ALU.mult),
                   r=[sps.b, mb_], w=[mm.b])
            return sl, c4

        def main(c, d, n, sl, c4):
            eg, v1, ktok, mm, ho = EG[d][n % 2], V1[d][n % 2], KTOK[d][n % 2], MM[d][n % 2], HO[d][n % 2]
            c1, c1b, tc_, dn, rr = C1[d], C1b[d], tmpC[d], den[d], rr_[d]
            r0 = c * 128
            U3 = UPS.t[:].rearrange("p (h d) -> p h d", h=4)
            D3 = DPS.t[:].rearrange("p (h d) -> p h d", h=4)
            for h in range(4):
                sc.add("pe", _mm(DPS.t[:, h * 256:h * 256 + 129], ktok.t[:, h, :], v1.t[:, h, 0:129], True, True), r=[ktok.b, v1.b], w=[DPS.b])
            for h in range(4):
                sc.add("pe", _mm(UPS.t[:, h * 256:h * 256 + 129], mm.t[:, h, :], v1.t[:, h, 0:129], True, False), r=[mm.b, v1.b], w=[UPS.b])
                sc.add("pe", _mm(UPS.t[:, h * 256:h * 256 + 129], sl.t[:, h, c4:c4 + 128], c1b.t[:, h, 0:129], False, True),
                       r=[sl.b, c1b.b], w=[UPS.b])
            sc.add("dve", _tt(c1.t[:, :, 0:129], D3[:, :, 0:129], tc_.t[:, :, 0:129], ALU.add), r=[DPS.b, tc_.b], w=[c1.b])
            sc.add("dve", _tt(tc_.t[:, :, 0:129], c1.t[:, :, 0:129], bc(eg.t[:, 4:8], 129), ALU.mult), r=[c1.b, eg.b], w=[tc_.b])
            sc.add("act", _act(c1b.t[:, :, 0:129], tc_.t[:, :, 0:129], AF.Copy), r=[tc_.b], w=[c1b.b])
            sc.add("dve", _tt(dn.t[:], U3[:, :, 128], eg.t[:, 0:4], ALU.mult), r=[UPS.b, eg.b], w=[dn.b])
            sc.add("act", _act(dn.t[:], dn.t[:], AF.Abs), r=[dn.b], w=[dn.b])
            sc.add("dve", _ts(dn.t[:], dn.t[:], 1.0, None, ALU.max), r=[dn.b], w=[dn.b])
            sc.add("dve", lambda e: e.reciprocal(rr.t[:], dn.t[:]), r=[dn.b], w=[rr.b])
            sc.add("dve", _tt(rr.t[:], rr.t[:], eg.t[:, 0:4], ALU.mult), r=[rr.b, eg.b], w=[rr.b])
            sc.add("dve", _tt(ho.t[:].rearrange("p (h d) -> p h d", h=4), U3[:, :, 0:128], bc(rr.t[:], 128), ALU.mult),
                   r=[UPS.b, rr.b], w=[ho.b])
            sc.add("pool", _dma(hdst[d][r0:r0 + 128, :], ho.t[:]), r=[ho.b], chan=ho.c)

        orders = (list(range(NB)), list(range(NB - 1, -1, -1)))
        nxt = [None, None]
        for d in range(2):
            sc.add("pool", _memset(C1[d].t[:], 0.0), w=[C1[d].b])
            sc.add("pool", _memset(tmpC[d].t[:], 0.0), w=[tmpC[d].b])
            sc.add("pool", _memset(C1b[d].t[:], 0.0), w=[C1b[d].b])
            nxt[d] = pre(orders[d][0], d, 0)
        for j in range(NB):
            cur = list(nxt)
            if j + 1 < NB:
                for d in range(2):
                    nxt[d] = pre(orders[d][j + 1], d, j + 1)
            for d in range(2):
                main(orders[d][j], d, j, *cur[d])
        sc.emit()

    KT_s = k.dram_tmp("KT_s", [512, S], BF16)
    KR_s = k.dram_tmp("KR_s", [64, S], BF16)
    V_s = k.dram_tmp("V_s", [S, 512], BF16)
    QN_s = k.dram_tmp("QN_s", [512, S], BF16)
    QR_s = k.dram_tmp("QR_s", [4, 65, S], BF16)
    kmax_s = k.dram_tmp("kmax_s", [128, 8])
    TWO_PI = 6.283185307179586

    with contextlib.ExitStack() as st:
        stage = [Slot(k.sb(st, f"wstagec{i}", [128, 1024]), ld_ch[i]) for i in range(2)]
        gq = load_cols(st, "gq", q_norm_g, 2)
        gkv = load_cols(st, "gkv", kv_norm_g, 1)
        Wuq, Wuqb = load_w(st, stage, "Wuq", w_uq, 256, 768, gq)
        Wkv, Wkvb = load_w(st, stage, "Wkv", w_ukv, 128, 1024, gkv)
        Wkv4 = Wkv[:, 0, :].rearrange("p (h t d) -> p h t d", h=4, t=2)
        cos2 = k.sb(st, "cos2", [128, NB, 64])
        sin1 = k.sb(st, "sin1", [128, NB, 32])
        tb_ = Buf("ropetab")
        pos = k.sb(st, "pos", [128, NB])
        invf = k.sb(st, "invf", [128, 32])
        ang = k.sb(st, "ang", [128, NB, 32])
        angi = k.sb(st, "angi", [128, NB, 32], mybir.dt.int32)
        angf = k.sb(st, "angf", [128, NB, 32])
        msk = k.sb(st, "msk", [128, NB, 32])
        sc.add("pool", lambda e: e.iota(pos[:], [[128, NB]], base=0, channel_multiplier=1, allow_small_or_imprecise_dtypes=True), w=[tb_])
        sc.add("pool", lambda e: e.iota(invf[:], [[1, 32]], base=0, channel_multiplier=0, allow_small_or_imprecise_dtypes=True), r=[tb_], w=[tb_])
        sc.add("act", _act(invf[:], invf[:], AF.Exp, scale=-float(np.log(10000.0)) / 32.0), r=[tb_], w=[tb_])
        sc.add("dve", _tt(ang[:], bc(pos[:], 32), bc_mid(invf[:], NB), ALU.mult), r=[tb_], w=[tb_])
        sc.add("dve", _ts(ang[:], ang[:], 1.0 / TWO_PI, None, ALU.mult), r=[tb_], w=[tb_])
        for which in range(2):
            if which == 1:
                sc.add("dve", _ts(ang[:], ang[:], 0.25, None, ALU.add), r=[tb_], w=[tb_])
            sc.add("dve", _cp(angi[:], ang[:]), r=[tb_], w=[tb_])
            sc.add("dve", _cp(angf[:], angi[:]), r=[tb_], w=[tb_])
            sc.add("dve", _tt(angf[:], ang[:], angf[:], ALU.subtract), r=[tb_], w=[tb_])
            sc.add("dve", _ts(msk[:], angf[:], 0.5, None, ALU.is_gt), r=[tb_], w=[tb_])
            sc.add("dve", _tt(angf[:], angf[:], msk[:], ALU.subtract), r=[tb_], w=[tb_])
            sc.add("dve", _ts(msk[:], angf[:], -0.5, None, ALU.is_lt), r=[tb_], w=[tb_])
            sc.add("dve", _tt(angf[:], angf[:], msk[:], ALU.add), r=[tb_], w=[tb_])
            if which == 0:
                sc.add("act", _act(sin1[:], angf[:], AF.Sin, scale=TWO_PI * (1.0 - 1e-6)), r=[tb_], w=[tb_])
            else:
                sc.add("act", _act(cos2[:, :, 0:32], angf[:], AF.Sin, scale=TWO_PI * (1.0 - 1e-6)), r=[tb_], w=[tb_])
                sc.add("act", _act(cos2[:, :, 32:64], angf[:], AF.Sin, scale=TWO_PI * (1.0 - 1e-6)), r=[tb_], w=[tb_])
        sc.barrier()

        G3T = [Slot(k.sb(st, f"g3t{i}", [128, 4, 464]), sc.chan(f"c_g3t{i}")) for i in range(2)]
        junkc = Slot(k.sb(st, "junkc", [128, 3072], BF16))
        ssq = Slot(k.sb(st, "ssq", [128, 8]))
        ssq2 = Slot(k.sb(st, "ssq2", [128, 8]))
        rst = Slot(k.sb(st, "rst", [128, 8]))
        cqn = Slot(k.sb(st, "cqn", [128, 4, 256], BF16))
        ckvn = Slot(k.sb(st, "ckvn", [128, 4, 128], BF16))
        tA = Slot(k.sb(st, "tA", [128, 4, 64]))
        tB = Slot(k.sb(st, "tB", [128, 4, 64]))
        krb = Slot(k.sb(st, "krb", [128, 4, 64], BF16))
        sqr = Slot(k.sb(st, "sqr", [128, 4, 64]))
        KR2 = [Slot(k.sb(st, f"kr2_{i}", [128, 4])) for i in range(2)]
        CQT = [Slot(k.sb(st, f"cqT{i}", [128, 2, 512], BF16)) for i in range(2)]
        CKVT = [Slot(k.sb(st, f"ckvT{i}", [128, 512], BF16)) for i in range(2)]
        krT = Slot(k.sb(st, "krT", [64, 512], BF16), sc.chan("c_krT"))
        KTS = [Slot(k.sb(st, f"kts{i}", [128, 512], BF16), sc.chan(f"c_kts{i}")) for i in range(2)]
        VS = [Slot(k.sb(st, f"vs{i}", [128, 512], BF16), sc.chan(f"c_vs{i}")) for i in range(2)]
        sqk = Slot(k.sb(st, "sqk", [128, 512]))
        kn2 = Slot(k.sb(st, "kn2", [128, 4, 4]))
        kmax = Slot(k.sb(st, "kmax", [128, 4]))
        kmt = Slot(k.sb(st, "kmt", [128, 4]))
        q_sb = Slot(k.sb(st, "q_sb", [128, 4, 768]))
        qtA = Slot(k.sb(st, "qtA", [128, 4, 4, 64]))
        qtB = Slot(k.sb(st, "qtB", [128, 4, 4, 64]))
        qbn = Slot(k.sb(st, "qbn", [128, 4, 4, 128], BF16))
        qbr = Slot(k.sb(st, "qbr", [128, 4, 4, 66], BF16))
        qn2 = Slot(k.sb(st, "qn2", [128, 16]))
        qn1 = Slot(k.sb(st, "qn1", [128, 16]))
        QS = [Slot(k.sb(st, f"qs{i}", [128, 2, 512], BF16), sc.chan(f"c_qs{i}")) for i in range(2)]
        QRS = [Slot(k.sb(st, f"qrs{i}", [65, 2, 512], BF16), sc.chan(f"c_qrs{i}")) for i in range(2)]
        PB = [Slot(k.ps(st, f"pb{i}", [128, 512])) for i in range(8)]
        pbi = {"i": 0}

        def bank():
            pbi["i"] += 1
            return PB[pbi["i"] % 8]

        def bfv(slot):
            return slot.t[:].bitcast(BF16)

        sc.add("pool", _memset(kmax.t[:], 0.0), w=[kmax.b])
        qmx = Slot(k.sb(st, "qmx", [128, 4]))
        qmt = Slot(k.sb(st, "qmt", [128, 4]))
        sc.add("pool", _memset(qmx.t[:], 0.0), w=[qmx.b])
        sc.add("pool", _memset(qbr.t[:], 0.0), w=[qbr.b])
        q4 = q_sb.t[:].rearrange("p b (h d) -> p b h d", h=4)
        cntC = {"kti": 0}

        def c_x(i):
            t0 = i * 512
            g3 = G3T[i % 2]
            kr2, cqT, ckvT = KR2[i % 2], CQT[i % 2], CKVT[i % 2]
            kti = cntC["kti"]
            sc.add("sp", _dma(g3.t[:], g3_s[t0:t0 + 512, :].rearrange("(b p) c -> p b c", p=128)), w=[g3.b], chan=g3.c)
            for b in range(4):
                sc.add("act", _act(junkc.t[:, 0:256], g3.t[:, b, 16:272], AF.Square, scale=1.0 / 16.0, accum_out=ssq.t[:, b:b + 1]),
                       r=[g3.b], w=[junkc.b, ssq.b])
                sc.add("act", _act(junkc.t[:, 0:128], g3.t[:, b, 272:400], AF.Square, scale=128.0 ** -0.5, accum_out=ssq.t[:, 4 + b:5 + b]),
                       r=[g3.b], w=[junkc.b, ssq.b])
            sc.add("dve", _ts(ssq2.t[:], ssq.t[:], EPS, None, ALU.add), r=[ssq.b], w=[ssq2.b])
            sc.add("pool", _tt(rst.t[:], ssq2.t[:], mhalf[:, 0:8], ALU.pow), r=[ssq2.b], w=[rst.b])
            sc.add("dve", _tt(cqn.t[:], g3.t[:, :, 16:272], bc(rst.t[:, 0:4], 256), ALU.mult), r=[g3.b, rst.b], w=[cqn.b])
            sc.add("dve", _tt(ckvn.t[:], g3.t[:, :, 272:400], bc(rst.t[:, 4:8], 128), ALU.mult), r=[g3.b, rst.b], w=[ckvn.b])
            xk = g3.t[:, :, 400:464]
            cs, sn = cos2[:, 4 * i:4 * i + 4, :], sin1[:, 4 * i:4 * i + 4, :]
            sc.add("dve", _tt(tA.t[:], xk, cs, ALU.mult), r=[g3.b], w=[tA.b])
            sc.add("dve", _tt(tB.t[:, :, 0:32], g3.t[:, :, 432:464], sn, ALU.mult), r=[g3.b], w=[tB.b])
            sc.add("dve", _tt(tB.t[:, :, 32:64], g3.t[:, :, 400:432], sn, ALU.mult), r=[g3.b], w=[tB.b])
            sc.add("dve", _tt(krb.t[:, :, 0:32], tA.t[:, :, 0:32], tB.t[:, :, 0:32], ALU.subtract), r=[tA.b, tB.b], w=[krb.b])
            sc.add("dve", _tt(krb.t[:, :, 32:64], tA.t[:, :, 32:64], tB.t[:, :, 32:64], ALU.add), r=[tA.b, tB.b], w=[krb.b])
            sc.add("act", _act(sqr.t[:], xk, AF.Square), r=[g3.b], w=[sqr.b])
            sc.add("dve", lambda e: e.tensor_reduce(kr2.t[:], sqr.t[:], AX.X, ALU.add), r=[sqr.b], w=[kr2.b])
            pa, pb2 = bank(), bank()
            for b in range(4):
                for kc in range(2):
                    sc.add("pe", _tr(bfv(pa)[:, kc * 512 + b * 128:kc * 512 + (b + 1) * 128], cqn.t[:, b, kc * 128:(kc + 1) * 128], ident[:]),
                           r=[cqn.b], w=[pa.b])
                sc.add("pe", _tr(bfv(pb2)[:, b * 128:(b + 1) * 128], ckvn.t[:, b, :], ident[:]), r=[ckvn.b], w=[pb2.b])
                sc.add("pe", _tr(bfv(pb2)[0:64, 512 + b * 128:512 + (b + 1) * 128], krb.t[:, b, :], ident[:]), r=[krb.b], w=[pb2.b])
            sc.add("act", _act(cqT.t[:], bfv(pa).rearrange("p (a b) -> p a b", a=2), AF.Copy), r=[pa.b], w=[cqT.b])
            sc.add("dve", _cp(ckvT.t[:], bfv(pb2)[:, 0:512]), r=[pb2.b], w=[ckvT.b])
            sc.add("act", _act(krT.t[:], bfv(pb2)[0:64, 512:1024], AF.Copy), r=[pb2.b], w=[krT.b])
            sc.add("pool", _dma(KR_s[:, t0:t0 + 512], krT.t[:]), r=[krT.b], chan=krT.c)
            for h in range(4):
                pk = bank()
                sc.add("pe", _mm(pk.t[:], Wkv[:, 0, h * 256:h * 256 + 128], ckvT.t[:], True, True), r=[ckvT.b, Wkvb], w=[pk.b])
                kts = KTS[kti % 2]
                kti += 1
                e = "act" if h % 2 else "dve"
                sc.add(e, scale_cast(e, kts.t[:], pk.t[:]), r=[pk.b], w=[kts.b])
                sc.add("pool", _dma(KT_s[h * 128:(h + 1) * 128, t0:t0 + 512], kts.t[:]), r=[kts.b], chan=kts.c)
            cntC["kti"] = kti

        def c_y(i):
            t0 = i * 512
            kr2, cqT, ckvT = KR2[i % 2], CQT[i % 2], CKVT[i % 2]
            cs, sn = cos2[:, 4 * i:4 * i + 4, :], sin1[:, 4 * i:4 * i + 4, :]
            for b in range(4):
                tok = slice(b * 128, (b + 1) * 128)
                pv = bank()
                sc.add("pe", _mm(pv.t[:].rearrange("p (h d) -> p h d", h=4), ckvT.t[:, tok], Wkv4[:, :, 1, :], True, True),
                       r=[ckvT.b, Wkvb], w=[pv.b])
                vs = VS[b % 2]
                sc.add("act", _act(vs.t[:], pv.t[:], AF.Copy), r=[pv.b], w=[vs.b])
                sc.add("pool", _dma(V_s[t0 + b * 128:t0 + (b + 1) * 128, :], vs.t[:]), r=[vs.b], chan=vs.c)
                pk = bank()
                sc.add("pe", _mm(pk.t[:].rearrange("p (h d) -> p h d", h=4), ckvT.t[:, tok], Wkv4[:, :, 0, :], True, True),
                       r=[ckvT.b, Wkvb], w=[pk.b])
                sc.add("act", _act(sqk.t[:], pk.t[:], AF.Square), r=[pk.b], w=[sqk.b])
                sc.add("dve", lambda e, b=b: e.tensor_reduce(kn2.t[:, b, :], sqk.t[:].rearrange("p (h d) -> p h d", h=4), AX.X, ALU.add),
                       r=[sqk.b], w=[kn2.b])
                pq0, pq1 = bank(), bank()
                for kc in range(2):
                    sc.add("pe", _mm(pq0.t[:], cqT.t[:, kc, tok], Wuq[:, kc, 0:512], kc == 0, kc == 1), r=[cqT.b, Wuqb], w=[pq0.b])
                for kc in range(2):
                    sc.add("pe", _mm(pq1.t[:, 0:256], cqT.t[:, kc, tok], Wuq[:, kc, 512:768], kc == 0, kc == 1), r=[cqT.b, Wuqb], w=[pq1.b])
                sc.add("act", _act(q_sb.t[:, b, 0:512], pq0.t[:], AF.Copy), r=[pq0.b], w=[q_sb.b])
                sc.add("dve", _cp(q_sb.t[:, b, 512:768], pq1.t[:, 0:256]), r=[pq1.b], w=[q_sb.b])
            sc.add("dve", _tt(kn2.t[:], kn2.t[:], bc(kr2.t[:], 4), ALU.add), r=[kn2.b, kr2.b], w=[kn2.b])
            sc.add("dve", lambda e: e.tensor_reduce(kmt.t[:], kn2.t[:].rearrange("p b h -> p h b"), AX.X, ALU.max), r=[kn2.b], w=[kmt.b])
            sc.add("dve", _tt(kmax.t[:], kmax.t[:], kmt.t[:], ALU.max), r=[kmax.b, kmt.b], w=[kmax.b])
            cs4 = bass.AP(cs.tensor, cs.offset, [list(cs.ap[0]), list(cs.ap[1]), [0, 4], list(cs.ap[2])])
            sn4 = bass.AP(sn.tensor, sn.offset, [list(sn.ap[0]), list(sn.ap[1]), [0, 4], list(sn.ap[2])])
            sc.add("dve", _tt(qtA.t[:], q4[:, :, :, 128:192], cs4, ALU.mult), r=[q_sb.b], w=[qtA.b])
            sc.add("dve", _tt(qtB.t[:, :, :, 0:32], q4[:, :, :, 160:192], sn4, ALU.mult), r=[q_sb.b], w=[qtB.b])
            sc.add("dve", _tt(qtB.t[:, :, :, 32:64], q4[:, :, :, 128:160], sn4, ALU.mult), r=[q_sb.b], w=[qtB.b])
            sc.add("dve", _tt(qbr.t[:, :, :, 0:32], qtA.t[:, :, :, 0:32], qtB.t[:, :, :, 0:32], ALU.subtract), r=[qtA.b, qtB.b], w=[qbr.b])
            sc.add("dve", _tt(qbr.t[:, :, :, 32:64], qtA.t[:, :, :, 32:64], qtB.t[:, :, :, 32:64], ALU.add), r=[qtA.b, qtB.b], w=[qbr.b])
            sc.add("dve", _cp(qbn.t[:], q4[:, :, :, 0:128]), r=[q_sb.b], w=[qbn.b])
            sc.add("act", _act(junkc.t[:], q_sb.t[:].rearrange("p b c -> p (b c)"), AF.Square), r=[q_sb.b], w=[junkc.b])
            sc.add("dve", lambda e: e.tensor_reduce(qn2.t[:], junkc.t[:].rearrange("p (g d) -> p g d", g=16), AX.X, ALU.add),
                   r=[junkc.b], w=[qn2.b])
            sc.add("dve", lambda e: e.tensor_reduce(qmt.t[:], qn2.t[:].rearrange("p (b h) -> p h b", b=4), AX.X, ALU.max), r=[qn2.b], w=[qmt.b])
            sc.add("dve", _tt(qmx.t[:], qmx.t[:], qmt.t[:], ALU.max), r=[qmx.b, qmt.b], w=[qmx.b])
            sc.add("pool", _tt(qn1.t[:], qn2.t[:], mhalf[:, 0:16], ALU.pow), r=[qn2.b], w=[qn1.b])
            sc.add("dve", _tt(qn1.t[:], qn1.t[:], qn2.t[:], ALU.mult), r=[qn1.b, qn2.b], w=[qn1.b])
            sc.add("dve", _ts(qbr.t[:, :, :, 64], qn1.t[:].rearrange("p (b h) -> p b h", b=4), -1.01, None, ALU.mult),
                   r=[qn1.b], w=[qbr.b])
            for hp in range(2):
                pn, pr = bank(), bank()
                for hh in range(2):
                    h = 2 * hp + hh
                    for b in range(4):
                        sc.add("pe", _tr(bfv(pn)[:, hh * 512 + b * 128:hh * 512 + (b + 1) * 128], qbn.t[:, b, h, :], ident[:]), r=[qbn.b], w=[pn.b])
                        sc.add("pe", _tr(bfv(pr)[0:65, hh * 512 + b * 128:hh * 512 + (b + 1) * 128], qbr.t[:, b, h, 0:65], ident[:]), r=[qbr.b], w=[pr.b])
                qs, qrs = QS[hp], QRS[hp]
                sc.add("act", _act(qs.t[:], bfv(pn).rearrange("p (a b) -> p a b", a=2), AF.Copy), r=[pn.b], w=[qs.b])
                sc.add("dve", _cp(qrs.t[:], bfv(pr)[0:65, :].rearrange("p (a b) -> p a b", a=2)), r=[pr.b], w=[qrs.b])
                sc.add("pool", _dma(QN_s.rearrange("(h p) s -> p h s", p=128)[:, 2 * hp:2 * hp + 2, t0:t0 + 512], qs.t[:]), r=[qs.b], chan=qs.c)
                sc.add("pool", _dma(QR_s.rearrange("h p s -> p h s")[:, 2 * hp:2 * hp + 2, t0:t0 + 512], qrs.t[:]), r=[qrs.b], chan=qrs.c)
        c_x(0)
        for i in range(NT):
            if i + 1 < NT:
                c_x(i + 1)
            c_y(i)
        kmo = Slot(k.sb(st, "kmo", [128, 8]), sc.chan("c_kmo"))
        sc.add("dve", _cp(kmo.t[:, 0:4], kmax.t[:]), r=[kmax.b], w=[kmo.b])
        sc.add("dve", _cp(kmo.t[:, 4:8], qmx.t[:]), r=[qmx.b], w=[kmo.b])
        sc.add("pool", _dma(kmax_s[:, :], kmo.t[:]), r=[kmo.b], chan=kmo.c)
        sc.emit()

    with contextlib.ExitStack() as st:
        KT = Slot(k.sb(st, "KT", [128, 4, S], BF16), sc.chan("c_KT"))
        KR = Slot(k.sb(st, "KR", [128, S], BF16), sc.chan("c_KR"))
        VR = Slot(k.sb(st, "VR", [128, NB, 512], BF16), sc.chan("c_VR"))
        kml = Slot(k.sb(st, "kml", [128, 8]), sc.chan("c_kml"))
        km1 = Slot(k.sb(st, "km1", [1, 8]))
        kmx = Slot(k.sb(st, "kmx", [128, 8]))
        cbias_ = Slot(k.sb(st, "cbias_", [128, 4]))
        phalf = Slot(k.sb(st, "phalf", [128, 4]))
        SPB = [Slot(k.ps(st, f"spb{i}", [128, 512])) for i in range(4)]
        OPB = [Slot(k.ps(st, f"opb{i}", [128, 512])) for i in range(2)]
        RSB = Slot(k.ps(st, "rsb", [128, 512]))
        RS = [Slot(k.ps(st, f"rs{i}", [128, 512])) for i in range(1)]
        NPT = 10
        PT = [Slot(k.sb(st, f"pt{i}", [128, 512], BF16)) for i in range(NPT)]
        rs_sb = Slot(k.sb(st, "rs_sb", [128, 512]))
        ones_b = Slot(k.sb(st, "ones_b", [128, 32], BF16))
        inv32 = Slot(k.sb(st, "inv32", [128, 128]))
        sc.add("pool", _memset(ones_b.t[:], 1.0), w=[ones_b.b])
        sc.add("pool", _memset(inv32.t[:], 1.0 / 32.0), w=[inv32.b])
        QN = [Slot(k.sb(st, f"qnt{i}", [128, 512], BF16), sc.chan(f"c_qn{i}")) for i in range(2)]
        QR = [Slot(k.sb(st, f"qrt{i}", [128, 512], BF16), sc.chan(f"c_qr{i}")) for i in range(2)]
        rinv = Slot(k.sb(st, "rinv", [128, 512]))
        YO = [Slot(k.sb(st, f"yo{i}", [128, 512], BF16), sc.chan(f"c_yo{i}")) for i in range(2)]
        sc.add("sp", _dma(KT.t[:], KT_s.rearrange("(h p) s -> p h s", p=128)), w=[KT.b], chan=KT.c)
        sc.add("sp", _dma(KR.t[0:64, :], KR_s[:, :]), w=[KR.b], chan=KR.c)
        sc.add("sp", _dma(KR.t[64:128, :], KR_s[:, :]), w=[KR.b], chan=sc.chan("c_KR2"))
        sc.add("sp", _dma(VR.t[:], V_s.rearrange("(c p) d -> p c d", p=128)), w=[VR.b], chan=VR.c)
        sc.add("sp", _dma(kml.t[:], kmax_s[:, :]), w=[kml.b], chan=kml.c)
        sc.add("pool", _memset(phalf.t[:], 0.5), w=[phalf.b])
        scale = 192.0 ** -0.5
        sc.add("pool", lambda e: e.tensor_reduce(km1.t[:], kml.t[:], AX.C, ALU.max), r=[kml.b], w=[km1.b])
        sc.add("pe", _mm(RSB.t[:, 0:8], ones_f[0:1, :], km1.t[:], True, True), r=[km1.b], w=[RSB.b])
        sc.add("dve", _cp(kmx.t[:], RSB.t[:, 0:8]), r=[RSB.b], w=[kmx.b])
        sc.add("dve", _tt(cbias_.t[:], kmx.t[:, 0:4], kmx.t[:, 4:8], ALU.mult), r=[kmx.b], w=[cbias_.b])
        sc.add("pool", _tt(cbias_.t[:], cbias_.t[:], phalf.t[:], ALU.pow), r=[cbias_.b, phalf.b], w=[cbias_.b])
        sc.add("dve", _ts(cbias_.t[:], cbias_.t[:], -1.01 * scale, None, ALU.mult), r=[cbias_.b], w=[cbias_.b])
        if "cb_dbg" in k.dbg:
            cb_dbg = k.dram_tmp("cb_dbg", [128, 12])
            dbt = Slot(k.sb(st, "dbt", [128, 12]), sc.chan("c_dbt"))
            sc.add("dve", _cp(dbt.t[:, 0:4], cbias_.t[:]), r=[cbias_.b], w=[dbt.b])
            sc.add("dve", _cp(dbt.t[:, 4:12], kmx.t[:]), r=[kmx.b], w=[dbt.b])
            sc.add("pool", _dma(cb_dbg[:, :], dbt.t[:]), r=[dbt.b], chan=dbt.c)
        it = 0
        pti = 0
        for h in range(4):
            for j in range(NT):
                qn, qr = QN[it % 2], QR[it % 2]
                opb, yo, rs = OPB[it % 2], YO[it % 2], RS[0]
                it += 1
                sc.add("sp", _dma(qn.t[:], QN_s[h * 128:(h + 1) * 128, j * 512:(j + 1) * 512]), w=[qn.b], chan=qn.c)
                sc.add("sp", _dma(qr.t[0:64, :], QR_s[h, 0:64, j * 512:(j + 1) * 512]), w=[qr.b], chan=qr.c)
                sc.add("sp", _dma(qr.t[64:128, :], QR_s[h, 0:64, j * 512:(j + 1) * 512]), w=[qr.b], chan=qr.c)

                def qk2(kc):
                    for r_ in range(2):
                        sp_ = SPB[(kc + r_) % 4]
                        ks = slice((kc + r_) * 128, (kc + r_ + 1) * 128)
                        sc.add("pe", _mm(sp_.t[:], KT.t[:, h, ks], qn.t[:], True, False), r=[KT.b, qn.b], w=[sp_.b])
                    for r_ in range(2):
                        sp_ = SPB[(kc + r_) % 4]
                        ks = slice((kc + r_) * 128, (kc + r_ + 1) * 128)
                        rows = slice(64 * r_, 64 * r_ + 64)
                        sc.add("pe", lambda e, sp_=sp_, ks=ks, rows=rows, r_=r_, qr=qr: e.matmul(sp_.t[:], KR.t[rows, ks], qr.t[rows, :], start=False, stop=True,
                                                                                             tile_position=(64 * r_, 0)),
                               r=[KR.b, qr.b], w=[sp_.b])

                qk2(0)
                grp = []
                for kc in range(NB):
                    if kc % 2 == 0 and kc + 2 < NB:
                        qk2(kc + 2)
                    sp_ = SPB[kc % 4]
                    pt = PT[pti % NPT]
                    pti += 1
                    sc.add("act", _act(pt.t[:], sp_.t[:], AF.Exp, scale=scale, bias=cbias_.t[:, h:h + 1]), r=[sp_.b, cbias_.b], w=[pt.b])
                    sc.add("pe", _mm(opb.t[:], VR.t[:, kc, h * 128:(h + 1) * 128], pt.t[:], kc == 0, kc == NB - 1),
                           r=[VR.b, pt.b], w=[opb.b])
                    grp.append(pt)
                    if len(grp) == 4:
                        for r_, ptr in enumerate(grp):
                            sc.add("pe", lambda e, r_=r_, ptr=ptr, kc=kc: e.matmul(rs.t[32 * r_:32 * r_ + 32, :], ones_b.t[:, 0:32], ptr.t[:],
                                                                               start=(kc == 3), stop=(kc == NB - 1),
                                                                               tile_position=(0, 32 * r_)),
                                   r=[ptr.b, ones_b.b], w=[rs.b])
                        grp = []
                sc.add("dve", _cp(rs_sb.t[:], rs.t[:]), r=[rs.b], w=[rs_sb.b])
                sc.add("pe", _mm(RSB.t[:], inv32.t[:], rs_sb.t[:], True, True), r=[rs_sb.b, inv32.b], w=[RSB.b])
                sc.add("dve", lambda e: e.reciprocal(rinv.t[:], RSB.t[:]), r=[RSB.b], w=[rinv.b])
                sc.add("dve", _tt(yo.t[:], opb.t[:], rinv.t[:], ALU.mult), r=[opb.b, rinv.b], w=[yo.b])
                sc.add("pool", _dma(yT_s[512 + h * 128:512 + (h + 1) * 128, j * 512:(j + 1) * 512], yo.t[:]), r=[yo.b], chan=yo.c)
        sc.emit()

    h1_s = k.dram_tmp("h1_s", [S, D])
    xn2T_s = k.dram_tmp("xn2T_s", [D, S + 2], BF16)
    h2_s = k.dram_tmp("h2_s", [S, D])
    yT3 = yT_s.rearrange("(g p) s -> p g s", p=128)
    xn2T3 = xn2T_s.rearrange("(g p) s -> p g s", p=128)

    def rms_transpose(xt, ss, ss2, rstd, junk, XB, xn, TBs, gain_scale=1.0 / 32.0):
        for b in range(4):
            sc.add("act", _act(junk.t[:], xt.t[:, b, :], AF.Square, scale=gain_scale, accum_out=ss.t[:, b:b + 1]),
                   r=[xt.b], w=[junk.b, ss.b])
        sc.add("dve", _ts(ss2.t[:], ss.t[:], EPS, None, ALU.add), r=[ss.b], w=[ss2.b])
        sc.add("pool", _tt(rstd.t[:], ss2.t[:], mhalf[:, 0:4], ALU.pow), r=[ss2.b], w=[rstd.b])
        for b in range(4):
            e = "dve" if b % 2 else "act"
            sc.add(e, scale_cast(e, XB.t[:, b, :], xt.t[:, b, :], rstd.t[:, b:b + 1]), r=[xt.b, rstd.b], w=[XB.b])
        for j in range(4):
            tb = TBs[j % 2]
            for kk in range(2):
                kc = 2 * j + kk
                for b in range(4):
                    sc.add("pe", _tr(tb.t[:, kk * 512 + b * 128:kk * 512 + (b + 1) * 128],
                                     XB.t[:, b, kc * 128:(kc + 1) * 128], ident[:]), r=[XB.b], w=[tb.b])
            e = "dve" if j % 2 else "act"
            sc.add(e, scale_cast(e, xn.t[:, 2 * j:2 * j + 2, :], tb.t[:].rearrange("p (a b) -> p a b", a=2)),
                   r=[tb.b], w=[xn.b])

    with contextlib.ExitStack() as st:
        stage = [Slot(k.sb(st, f"wstaged{i}", [128, 1024]), ld_ch[i]) for i in range(2)]
        Wout, Woutb = load_w(st, stage, "Wout", w_out, D, D)
        normg = load_bcast(st, "normg", mlstm_norm_g, 512)
        zt = Slot(k.sb(st, "zt", [128, 8, 2], BF16), sc.chan("c_zt"))
        sc.add("pool", _memset(zt.t[:], 0.0), w=[zt.b])
        sc.add("pool", _dma(xn2T3[:, :, 0:1], zt.t[:, :, 0:1], allow_slow_non_contiguous=True), r=[zt.b], chan=zt.c)
        sc.add("pool", _dma(xn2T3[:, :, S + 1:S + 2], zt.t[:, :, 1:2], allow_slow_non_contiguous=True), r=[zt.b], chan=zt.c)
        HFT = [Slot(k.sb(st, f"hft{i}", [128, 4, 512], BF16), sc.chan(f"c_hft{i}")) for i in range(2)]
        HBT = [Slot(k.sb(st, f"hbt{i}", [128, 4, 512], BF16), sc.chan(f"c_hbt{i}")) for i in range(2)]
        SOT = [Slot(k.sb(st, f"sot{i}", [128, 4, 512], BF16), sc.chan(f"c_sot{i}")) for i in range(2)]
        HS = Slot(k.sb(st, "hsd", [128, 4, 512]))
        SG = Slot(k.sb(st, "sgd", [128, 4, 512]))
        sqd = Slot(k.sb(st, "sqd", [128, 4, 512], BF16))
        ssn = Slot(k.sb(st, "ssnd", [128, 16]))
        rsn = Slot(k.sb(st, "rsnd", [128, 16]))
        YB = [Slot(k.sb(st, f"ybd{i}", [128, 4, 512], BF16)) for i in range(2)]
        YAT = [Slot(k.sb(st, f"yat{i}", [128, 4, 512], BF16)) for i in range(2)]
        YTT = [Slot(k.sb(st, f"ytt{i}", [128, 4, 512], BF16), sc.chan(f"c_ytt{i}")) for i in range(2)]
        XT = [Slot(k.sb(st, f"xtd{i}", [128, 4, D]), sc.chan(f"c_xtd{i}")) for i in range(3)]
        XB = Slot(k.sb(st, "xbd", [128, 4, D], BF16))
        XN = [Slot(k.sb(st, f"xnd{i}", [128, 8, 512], BF16), sc.chan(f"c_xnd{i}")) for i in range(2)]
        junk = Slot(k.sb(st, "junkd", [128, D], BF16))
        ss = Slot(k.sb(st, "ssd", [128, 4]))
        ss2 = Slot(k.sb(st, "ss2d", [128, 4]))
        rstd = Slot(k.sb(st, "rstdd", [128, 4]))
        TBs = [Slot(k.ps(st, f"tbd{i}", [128, 1024], BF16)) for i in range(2)]
        MB = [Slot(k.ps(st, f"mbd{i}", [128, 512])) for i in range(6)]
        cntD = {"mbi": 0}

        def loads(i):
            t0 = i * 512
            hft, hbt, sot, ytt, xt = HFT[i % 2], HBT[i % 2], SOT[i % 2], YTT[i % 2], XT[i % 3]
            tv = lambda ap: ap[t0:t0 + 512, :].rearrange("(b p) d -> p b d", p=128)
            sc.add("sp", _dma(hft.t[:], tv(hf_s)), w=[hft.b], chan=hft.c)
            sc.add("sp", _dma(hbt.t[:], tv(hb_s)), w=[hbt.b], chan=hbt.c)
            sc.add("sp", _dma(sot.t[:], tv(so_s)), w=[sot.b], chan=sot.c)
            sc.add("sp", _dma(ytt.t[:], yT3[:, 4:8, t0:t0 + 512]), w=[ytt.b], chan=ytt.c)
            sc.add("sp", _dma(xt.t[:], tv(x)), w=[xt.b], chan=xt.c)

        def comb1(i):
            hft, hbt, sot = HFT[i % 2], HBT[i % 2], SOT[i % 2]
            sc.add("dve", _tt(HS.t[:], hft.t[:], hbt.t[:], ALU.add), r=[hft.b, hbt.b], w=[HS.b])
            sc.add("act", _act(sqd.t[:], HS.t[:], AF.Square, scale=128.0 ** -0.5), r=[HS.b], w=[sqd.b])
            sc.add("pool", _tt(SG.t[:], sot.t[:], bc_mid(normg[0][:], 4), ALU.mult), r=[sot.b, normg[1]], w=[SG.b])
            sc.add("dve", lambda e: e.tensor_reduce(ssn.t[:], sqd.t[:].rearrange("p b (h d) -> p (b h) d", h=4), AX.X, ALU.add),
                   r=[sqd.b], w=[ssn.b])
            sc.add("dve", _ts(ssn.t[:], ssn.t[:], EPS, None, ALU.add), r=[ssn.b], w=[ssn.b])
            sc.add("pool", _tt(rsn.t[:], ssn.t[:], mhalf[:, 0:16], ALU.pow), r=[ssn.b], w=[rsn.b])

        def comb2(i):
            yb = YB[i % 2]
            h16 = HS.t[:].rearrange("p b (h d) -> p (b h) d", h=4)
            sc.add("dve", _tt(h16, h16, bc(rsn.t[:], 128), ALU.mult), r=[HS.b, rsn.b], w=[HS.b])
            sc.add("dve", _tt(yb.t[:], HS.t[:], SG.t[:], ALU.mult), r=[HS.b, SG.b], w=[yb.b])

        def norm1(i):
            t0 = i * 512
            xt = XT[i % 3]
            sc.add("sp", _dma(h1_s[t0:t0 + 512, :].rearrange("(b p) d -> p b d", p=128), xt.t[:]), r=[xt.b], chan=xt.c)
            for b in range(4):
                sc.add("act", _act(junk.t[:], xt.t[:, b, :], AF.Square, scale=1.0 / 32.0, accum_out=ss.t[:, b:b + 1]),
                       r=[xt.b], w=[junk.b, ss.b])
            sc.add("dve", _ts(ss2.t[:], ss.t[:], EPS, None, ALU.add), r=[ss.b], w=[ss2.b])
            sc.add("pool", _tt(rstd.t[:], ss2.t[:], mhalf[:, 0:4], ALU.pow), r=[ss2.b], w=[rstd.b])

        def norm2(i):
            xt = XT[i % 3]
            for b in range(4):
                e = "dve" if b % 2 else "act"
                sc.add(e, scale_cast(e, XB.t[:, b, :], xt.t[:, b, :], rstd.t[:, b:b + 1]), r=[xt.b, rstd.b], w=[XB.b])

        def mmstage(i):
            yb, yat, ytt, xt = YB[i % 2], YAT[i % 2], YTT[i % 2], XT[i % 3]
            for hp in range(2):
                tb = TBs[hp]
                for hh in range(2):
                    h = 2 * hp + hh
                    for b in range(4):
                        sc.add("pe", _tr(tb.t[:, hh * 512 + b * 128:hh * 512 + (b + 1) * 128], yb.t[:, b, h * 128:(h + 1) * 128], ident[:]),
                               r=[yb.b], w=[tb.b])
                e = "dve" if hp else "act"
                sc.add(e, scale_cast(e, yat.t[:, 2 * hp:2 * hp + 2, :], tb.t[:].rearrange("p (a b) -> p a b", a=2)), r=[tb.b], w=[yat.b])
            mbi = cntD["mbi"]
            for b in range(4):
                for half in range(2):
                    pm = MB[mbi % 6]
                    mbi += 1
                    for kc in range(8):
                        src = yat if kc < 4 else ytt
                        sc.add("pe", _mm(pm.t[:], src.t[:, kc % 4, b * 128:(b + 1) * 128], Wout[:, kc, half * 512:(half + 1) * 512],
                                         kc == 0, kc == 7), r=[src.b, Woutb], w=[pm.b])
                    sc.add("dve", _tt(xt.t[:, b, half * 512:(half + 1) * 512], pm.t[:], xt.t[:, b, half * 512:(half + 1) * 512], ALU.add),
                           r=[pm.b, xt.b], w=[xt.b])
            cntD["mbi"] = mbi

        def trstage(i):
            t0 = i * 512
            xn = XN[i % 2]
            for j in range(4):
                tb = TBs[j % 2]
                for kk in range(2):
                    kc = 2 * j + kk
                    for b in range(4):
                        sc.add("pe", _tr(tb.t[:, kk * 512 + b * 128:kk * 512 + (b + 1) * 128],
                                         XB.t[:, b, kc * 128:(kc + 1) * 128], ident[:]), r=[XB.b], w=[tb.b])
                e = "dve" if j % 2 else "act"
                sc.add(e, scale_cast(e, xn.t[:, 2 * j:2 * j + 2, :], tb.t[:].rearrange("p (a b) -> p a b", a=2)),
                       r=[tb.b], w=[xn.b])
            sc.add("sp", _dma(xn2T3[:, :, 1 + t0:1 + t0 + 512], xn.t[:]), r=[xn.b], chan=xn.c)

        loads(0)
        if NT > 1:
            loads(1)
        comb1(0)
        comb2(0)
        for i in range(NT + 1):
            if i >= 1:
                norm1(i - 1)
            if i + 1 < NT:
                comb1(i + 1)
            if i < NT:
                mmstage(i)
            if i >= 1:
                norm2(i - 1)
                trstage(i - 1)
            if i + 1 < NT:
                comb2(i + 1)
            if i + 2 < NT:
                loads(i + 2)
        sc.emit()

    TT_ = 256
    with contextlib.ExitStack() as st:
        stage = [Slot(k.sb(st, f"wstagee{i}", [128, 1408]), ld_ch[i]) for i in range(2)]
        gffn = load_cols(st, "gffn", ln_ffn_g, 8)
        Wup, Wupb = load_w(st, stage, "Wup", w_up, D, 2 * D_FF, gffn)
        Wdn, Wdnb = load_w(st, stage, "Wdn", w_down, D_FF, D)
        fw = k.sb(st, "fw", [128, 44, 3])
        fwb = Buf("fw")
        for tap in range(3):
            sc.add("sp", _dma(fw[:, :, tap], conv_ffn_w[tap].rearrange("(g p) -> p g", p=128),
                              allow_slow_non_contiguous=True), w=[Buf()], chan=sc.chan(f"c_fw{tap}"))
        fb = load_cols(st, "fb", conv_ffn_b, 44)
        sc.barrier()
        XS = [Slot(k.sb(st, f"xs{i}", [128, 8, TT_ + 2], BF16), sc.chan(f"c_xs{i}")) for i in range(2)]
        H1 = [Slot(k.sb(st, f"h1t{i}", [128, 2, D]), sc.chan(f"c_h1t{i}")) for i in range(2)]
        AT = [Slot(k.sb(st, f"at{i}", [128, 22, TT_], BF16)) for i in range(2)]
        CG = [Slot(k.sb(st, f"cg{i}", [128, TT_])) for i in range(2)]
        CV = [Slot(k.sb(st, f"cv{i}", [128, TT_])) for i in range(2)]
        SG = [Slot(k.sb(st, f"sg{i}", [128, TT_])) for i in range(2)]
        MB = [Slot(k.ps(st, f"mbe{i}", [128, 512])) for i in range(8)]
        mbi = 0
        for i in range(S // TT_):
            t0 = i * TT_
            xs, h1, at = XS[i % 2], H1[i % 2], AT[i % 2]
            sc.add("sp", _dma(xs.t[:], xn2T3[:, :, t0:t0 + TT_ + 2]), w=[xs.b], chan=xs.c)
            sc.add("sp", _dma(h1.t[:], h1_s[t0:t0 + TT_, :].rearrange("(b p) d -> p b d", p=128)), w=[h1.b], chan=h1.c)
            for g in range(22):
                res = []
                for which, (gi, dst) in enumerate(((g, CG[g % 2]), (22 + g, CV[g % 2]))):
                    pm = MB[mbi % 8]
                    mbi += 1
                    for kc in range(8):
                        sc.add("pe", _mm(pm.t[:, 0:TT_ + 2], Wup[:, kc, gi * 128:(gi + 1) * 128], xs.t[:, kc, :], kc == 0, kc == 7),
                               r=[xs.b, Wupb], w=[pm.b])
                    sc.add("act", _act(dst.t[:], pm.t[:, 0:TT_], AF.Identity, scale=fw[:, gi, 0:1], bias=fb[0][:, gi:gi + 1]),
                           r=[pm.b], w=[dst.b])
                    sc.add("dve", _stt(dst.t[:], pm.t[:, 1:TT_ + 1], fw[:, gi, 1:2], dst.t[:], ALU.mult, ALU.add), r=[pm.b, dst.b], w=[dst.b])
                    sc.add("dve", _stt(dst.t[:], pm.t[:, 2:TT_ + 2], fw[:, gi, 2:3], dst.t[:], ALU.mult, ALU.add), r=[pm.b, dst.b], w=[dst.b])
                cg, cv, sg = CG[g % 2], CV[g % 2], SG[g % 2]
                sc.add("act", _act(sg.t[:], cg.t[:], AF.Silu), r=[cg.b], w=[sg.b])
                sc.add("pool", _tt(at.t[:, g, :], sg.t[:], cv.t[:], ALU.mult), r=[sg.b, cv.b], w=[at.b])
            for b in range(TT_ // 128):
                for half in range(2):
                    pm = MB[mbi % 8]
                    mbi += 1
                    for g in range(22):
                        sc.add("pe", _mm(pm.t[:], at.t[:, g, b * 128:(b + 1) * 128], Wdn[:, g, half * 512:(half + 1) * 512], g == 0, g == 21),
                               r=[at.b, Wdnb], w=[pm.b])
                    sc.add("dve", _tt(h1.t[:, b, half * 512:(half + 1) * 512], pm.t[:], h1.t[:, b, half * 512:(half + 1) * 512], ALU.add),
                           r=[pm.b, h1.b], w=[h1.b])
            sc.add("pool", _dma(h2_s[t0:t0 + TT_, :].rearrange("(b p) d -> p b d", p=128), h1.t[:]), r=[h1.b], chan=h1.c)
        sc.emit()

    with contextlib.ExitStack() as st:
        stage = [Slot(k.sb(st, f"wstagef{i}", [128, 1024]), ld_ch[i]) for i in range(2)]
        gple = load_cols(st, "gple", ple_norm_g, 8)
        Wg, Wgb = load_w(st, stage, "Wg", w_ple_gate, D, D, gple)
        Wp, Wpb = load_w(st, stage, "Wp", w_ple_proj, 256, D)
        postg = load_bcast(st, "postg", ple_post_g, D)
        fing = load_bcast(st, "fing", final_g, D)
        sc.barrier()
        XT = [Slot(k.sb(st, f"xtf{i}", [128, 4, D]), sc.chan(f"c_xtf{i}")) for i in range(3)]
        PTL = [Slot(k.sb(st, f"ptl{i}", [128, 4, 256]), sc.chan(f"c_ptl{i}")) for i in range(2)]
        XBF = [Slot(k.sb(st, f"xbf{i}", [128, 4, D], BF16)) for i in range(2)]
        PBF = [Slot(k.sb(st, f"pbf{i}", [128, 4, 256], BF16)) for i in range(2)]
        XNF = [Slot(k.sb(st, f"xnf{i}", [128, 8, 512], BF16)) for i in range(2)]
        PTTF = [Slot(k.sb(st, f"ptt{i}", [128, 2, 512], BF16)) for i in range(2)]
        junk = Slot(k.sb(st, "junkf", [128, D], BF16))
        ss = Slot(k.sb(st, "ssf", [128, 4]))
        ss2 = Slot(k.sb(st, "ss2f", [128, 4]))
        rstd = Slot(k.sb(st, "rstdf", [128, 4]))
        ssb = Slot(k.sb(st, "ssb", [128, 2]))
        ssb2 = Slot(k.sb(st, "ssb2", [128, 2]))
        rsb2 = Slot(k.sb(st, "rsb2", [128, 2]))
        SGM = [Slot(k.sb(st, f"sgm{i}", [128, D])) for i in range(2)]
        PJ = [Slot(k.sb(st, f"pj{i}", [128, D])) for i in range(2)]
        OT = [Slot(k.sb(st, f"ot{i}", [128, D]), sc.chan(f"c_ot{i}")) for i in range(3)]
        TBs = [Slot(k.ps(st, f"tbf{i}", [128, 1024], BF16)) for i in range(2)]
        MB = [Slot(k.ps(st, f"mbf{i}", [128, 512])) for i in range(6)]
        cntF = {"mbi": 0, "bi": 0}

        def f_stage1a(i):
            t0 = i * 512
            xt, ptl, XB, PBf = XT[i % 3], PTL[i % 2], XBF[i % 2], PBF[i % 2]
            sc.add("sp", _dma(xt.t[:], h2_s[t0:t0 + 512, :].rearrange("(b p) d -> p b d", p=128)), w=[xt.b], chan=xt.c)
            sc.add("sp", _dma(ptl.t[:], p_in[t0:t0 + 512, :].rearrange("(b p) d -> p b d", p=128)), w=[ptl.b], chan=ptl.c)
            for b in range(4):
                sc.add("act", _act(junk.t[:], xt.t[:, b, :], AF.Square, scale=1.0 / 32.0, accum_out=ss.t[:, b:b + 1]),
                       r=[xt.b], w=[junk.b, ss.b])
            sc.add("dve", _ts(ss2.t[:], ss.t[:], EPS, None, ALU.add), r=[ss.b], w=[ss2.b])
            sc.add("pool", _tt(rstd.t[:], ss2.t[:], mhalf[:, 0:4], ALU.pow), r=[ss2.b], w=[rstd.b])
            for b in range(4):
                e = "dve" if b % 2 else "act"
                sc.add(e, scale_cast(e, XB.t[:, b, :], xt.t[:, b, :], rstd.t[:, b:b + 1]), r=[xt.b, rstd.b], w=[XB.b])
            sc.add("act", _act(PBf.t[:], ptl.t[:], AF.Copy), r=[ptl.b], w=[PBf.b])

        def f_stage1b(i):
            XN, PTT, XB, PBf = XNF[i % 2], PTTF[i % 2], XBF[i % 2], PBF[i % 2]
            for j in range(4):
                tb = TBs[j % 2]
                for kk in range(2):
                    kc = 2 * j + kk
                    for b in range(4):
                        sc.add("pe", _tr(tb.t[:, kk * 512 + b * 128:kk * 512 + (b + 1) * 128],
                                         XB.t[:, b, kc * 128:(kc + 1) * 128], ident[:]), r=[XB.b], w=[tb.b])
                e = "dve" if j % 2 else "act"
                sc.add(e, scale_cast(e, XN.t[:, 2 * j:2 * j + 2, :], tb.t[:].rearrange("p (a b) -> p a b", a=2)),
                       r=[tb.b], w=[XN.b])
            tb = TBs[0]
            for kc in range(2):
                for b in range(4):
                    sc.add("pe", _tr(tb.t[:, kc * 512 + b * 128:kc * 512 + (b + 1) * 128], PBf.t[:, b, kc * 128:(kc + 1) * 128], ident[:]),
                           r=[PBf.b], w=[tb.b])
            sc.add("act", _act(PTT.t[:], tb.t[:].rearrange("p (a b) -> p a b", a=2), AF.Copy), r=[tb.b], w=[PTT.b])

        RN = 4
        SGM3 = SGM + [Slot(k.sb(st, f"sgm{i}", [128, D])) for i in range(2, RN)]
        PJ3 = PJ + [Slot(k.sb(st, f"pj{i}", [128, D])) for i in range(2, RN)]
        SSB = [Slot(k.sb(st, f"ssbr{i}", [128, 2])) for i in range(RN)]
        RSB2 = [Slot(k.sb(st, f"rsbr{i}", [128, 2])) for i in range(RN)]
        junk2 = Slot(k.sb(st, "junkf2", [128, D], BF16))

        def blk(n):
            i, b = divmod(n, 4)
            return i, b, XT[i % 3], SGM3[n % RN], PJ3[n % RN], SSB[n % RN], RSB2[n % RN], OT[n % 3]

        def f_a12(n):
            i, b, xt, sgm, pj, ssb_, rsb_, ot = blk(n)
            XN, PTT = XNF[i % 2], PTTF[i % 2]
            mbi = cntF["mbi"]
            tok = slice(b * 128, (b + 1) * 128)
            for half in range(2):
                hs = slice(half * 512, (half + 1) * 512)
                pm = MB[mbi % 6]
                mbi += 1
                for kc in range(8):
                    sc.add("pe", _mm(pm.t[:], XN.t[:, kc, tok], Wg[:, kc, hs], kc == 0, kc == 7), r=[XN.b, Wgb], w=[pm.b])
                sc.add("act", _act(sgm.t[:, hs], pm.t[:], AF.Sigmoid), r=[pm.b], w=[sgm.b])
                pm = MB[mbi % 6]
                mbi += 1
                for kc in range(2):
                    sc.add("pe", _mm(pm.t[:], PTT.t[:, kc, tok], Wp[:, kc, hs], kc == 0, kc == 1), r=[PTT.b, Wpb], w=[pm.b])
                sc.add("act", _act(pj.t[:, hs], pm.t[:], AF.Copy), r=[pm.b], w=[pj.b])
            cntF["mbi"] = mbi
            sc.add("act", _act(junk.t[:], pj.t[:], AF.Square, scale=1.0 / 32.0, accum_out=ssb_.t[:, 0:1]), r=[pj.b], w=[junk.b, ssb_.b])

        def f_d12(n):
            i, b, xt, sgm, pj, ssb_, rsb_, ot = blk(n)
            sc.add("dve", _ts(ssb_.t[:, 0:1], ssb_.t[:, 0:1], EPS, None, ALU.add), r=[ssb_.b], w=[ssb_.b])
            sc.add("pool", _tt(rsb_.t[:, 0:1], ssb_.t[:, 0:1], mhalf[:, 0:1], ALU.pow), r=[ssb_.b], w=[rsb_.b])
            sc.add("dve", _stt(sgm.t[:], sgm.t[:], rsb_.t[:, 0:1], postg[0][:], ALU.mult, ALU.mult), r=[sgm.b, rsb_.b, postg[1]], w=[sgm.b])
            sc.add("dve", _tt(pj.t[:], pj.t[:], sgm.t[:], ALU.mult), r=[pj.b, sgm.b], w=[pj.b])
            sc.add("dve", _tt(pj.t[:], pj.t[:], xt.t[:, b, :], ALU.add), r=[pj.b, xt.b], w=[pj.b])

        def f_a3(n):
            i, b, xt, sgm, pj, ssb_, rsb_, ot = blk(n)
            sc.add("act", _act(junk2.t[:], pj.t[:], AF.Square, scale=1.0 / 32.0, accum_out=ssb_.t[:, 1:2]), r=[pj.b], w=[junk2.b, ssb_.b])

        def f_d3(n):
            i, b, xt, sgm, pj, ssb_, rsb_, ot = blk(n)
            t0 = i * 512
            sc.add("dve", _ts(ssb_.t[:, 1:2], ssb_.t[:, 1:2], EPS, None, ALU.add), r=[ssb_.b], w=[ssb_.b])
            sc.add("pool", _tt(rsb_.t[:, 1:2], ssb_.t[:, 1:2], mhalf[:, 0:1], ALU.pow), r=[ssb_.b], w=[rsb_.b])
            sc.add("dve", _stt(ot.t[:], pj.t[:], rsb_.t[:, 1:2], fing[0][:], ALU.mult, ALU.mult), r=[pj.b, rsb_.b, fing[1]], w=[ot.b])
            sc.add("sp", _dma(out[t0 + b * 128:t0 + (b + 1) * 128, :], ot.t[:]), r=[ot.b], chan=ot.c)

        f_stage1a(0)
        f_stage1b(0)
        if NT > 1:
            f_stage1a(1)
        NBLK = NT * 4
        for n in range(-2, NBLK + 1):
            if 0 <= n + 2 < NBLK:
                f_a12(n + 2)
                i2, b2 = divmod(n + 2, 4)
                if b2 == 1:
                    if i2 + 1 < NT:
                        f_stage1b(i2 + 1)
                    if i2 + 2 < NT:
                        f_stage1a(i2 + 2)
            if 0 <= n + 1 < NBLK:
                f_d12(n + 1)
            if 0 <= n < NBLK:
                f_a3(n)
            if 0 <= n - 1 < NBLK:
                f_d3(n - 1)
        sc.emit()

    k.final_wait = None
    return k


def finish(k):
    return k.nc


_W_NAMES = ["ln_mix_g", "w_in", "b_gates", "conv_qk_w", "conv_qk_b", "mlstm_norm_g", "q_norm_g", "w_uq", "kv_norm_g",
            "w_ukv", "w_out", "ln_ffn_g", "w_up", "conv_ffn_w", "conv_ffn_b", "w_down", "ple_norm_g", "w_ple_gate",
            "w_ple_proj", "ple_post_g"]


def kernel(**inputs):
    x = np.asarray(inputs["x"])
    p = np.asarray(inputs["p"])
    B, S, _ = x.shape
    nc = finish(build(S))
    shared = {n: np.ascontiguousarray(np.asarray(inputs[n])[0], dtype=np.float32) for n in _W_NAMES}
    shared["final_g"] = np.ascontiguousarray(np.asarray(inputs["final_g"]), dtype=np.float32)
    in_maps = []
    for b in range(B):
        m = dict(shared)
        m["x"] = np.ascontiguousarray(x[b], dtype=np.float32)
        m["p"] = np.ascontiguousarray(p[0, b], dtype=np.float32)
        in_maps.append(m)
    res = run_bass_kernel_spmd(nc, in_maps, core_ids=list(range(B)))
    return np.stack([np.asarray(r["out"]) for r in res.results], axis=0).astype(np.float32)
```

```python
import contextlib
import numpy as np
import concourse.bass as bass
import concourse.mybir as mybir
from concourse.bass_utils import run_bass_kernel_spmd

F32 = mybir.dt.float32
BF16 = mybir.dt.bfloat16
AF = mybir.ActivationFunctionType
ALU = mybir.AluOpType
AX = mybir.AxisListType

D = 1024
NH = 4
IN_COLS = 2512
D_FF = 2816
EPS = 1e-6
SEM_MAX = 30000


class Buf:
    __slots__ = ("name", "lw", "rd")

    def __init__(self, name=""):
        self.name = name
        self.lw = None
        self.rd = []


class Chan:
    __slots__ = ("sem", "count", "last")

    def __init__(self, sem):
        self.sem = sem
        self.count = 0
        self.last = None


class Op:
    __slots__ = ("eng", "fn", "deps", "sig", "signo", "chan", "cval", "done")


class Sched:
    ENGS = ("pe", "act", "dve", "pool", "sp")

    def __init__(self, nc, stack):
        self.nc = nc
        self.stack = stack
        self.ops = []
        self.last_on = {e: None for e in self.ENGS}
        self.pending_bar = {e: [] for e in self.ENGS}
        self.chans = []
        self.free_chans = []
        self.phase_chans = []
        self.cnt = {e: 0 for e in self.ENGS}
        self.sems = {e: [] for e in self.ENGS}
        self.waited = {e: {} for e in self.ENGS}

    def chan(self, name, keep=False):
        if self.free_chans and not keep:
            c = self.free_chans.pop()
        else:
            c = Chan(self.stack.enter_context(self.nc.semaphore(name)))
            self.chans.append(c)
        if not keep:
            self.phase_chans.append(c)
        return c

    def add(self, eng, fn, r=(), w=(), chan=None):
        op = Op()
        op.eng, op.fn, op.deps, op.sig, op.signo, op.chan, op.cval = eng, fn, {}, False, 0, chan, 0
        op.done = False
        for b in r:
            if b.lw is not None:
                op.deps[b.lw] = True
        for b in w:
            if b.lw is not None:
                op.deps.setdefault(b.lw, False)
            for q in b.rd:
                op.deps.setdefault(q, False)
        for b in r:
            b.rd.append(op)
        for b in w:
            b.lw = op
            b.rd = []
        if self.pending_bar[eng]:
            for d in self.pending_bar[eng]:
                op.deps[d] = True
            self.pending_bar[eng] = []
        if chan is not None:
            if chan.last is not None:
                op.deps[chan.last] = True
            chan.count += 16
            op.cval = chan.count
            chan.last = op
        op.deps.pop(op, None)
        self.ops.append(op)
        self.last_on[eng] = op
        return op

    def barrier(self):
        lasts = [o for o in self.last_on.values() if o is not None]
        lasts += [c.last for c in self.chans if c.last is not None]
        for e in self.ENGS:
            self.pending_bar[e] = list(lasts)

    def emit(self):
        nc = self.nc
        fin = self.add("sp", lambda e: e.nop())
        for c in self.chans:
            if c.last is not None and not c.last.done:
                fin.deps[c.last] = True
        for e in self.ENGS:
            self.pending_bar[e] = []
        for op in self.ops:
            for d in [d for d in op.deps if d.done]:
                del op.deps[d]
            for d, raw in op.deps.items():
                if d.chan is not None:
                    continue
                if d.eng == op.eng and (op.eng == "pe" or not raw):
                    continue
                d.sig = True
        cnt = self.cnt
        for op in self.ops:
            if op.chan is None and op.sig:
                cnt[op.eng] += 1
                op.signo = cnt[op.eng]
        sems = self.sems
        for e in self.ENGS:
            n = cnt[e] // SEM_MAX + 1
            while len(sems[e]) < n:
                sems[e].append(self.stack.enter_context(nc.semaphore(f"s_{e}{len(sems[e])}")))
        per = {e: [o for o in self.ops if o.eng == e] for e in self.ENGS}
        handles = {"pe": "tensor", "act": "scalar", "dve": "vector", "pool": "gpsimd", "sp": "sync"}

        def run(e, eng):
            waited = self.waited[e]
            for op in per[e]:
                for d, raw in op.deps.items():
                    if d.chan is not None:
                        key, val, sem = ("c", id(d.chan)), d.cval, d.chan.sem
                    else:
                        if d.eng == e and (e == "pe" or not raw):
                            continue
                        j = (d.signo - 1) // SEM_MAX
                        key, val, sem = (d.eng, j), d.signo - j * SEM_MAX, sems[d.eng][j]
                    if waited.get(key, 0) >= val:
                        continue
                    waited[key] = val
                    eng.wait_ge(sem, val)
                ins = op.fn(eng)
                if op.chan is not None:
                    ins.then_inc(op.chan.sem, 16)
                elif op.sig:
                    j = (op.signo - 1) // SEM_MAX
                    ins.then_inc(sems[e][j], 1)

        with nc.Block() as block:
            for e in self.ENGS:
                if per[e]:
                    getattr(block, handles[e])(lambda eng, e=e: run(e, eng))
        for op in self.ops:
            op.done = True
            op.fn = None
            op.deps = {}
        self.ops = []
        self.last_on = {e: None for e in self.ENGS}
        for c in self.phase_chans:
            c.last = None
        self.free_chans.extend(self.phase_chans)
        self.phase_chans = []


def _act(out, in_, func, **kw):
    return lambda e: e.activation(out, in_, func, **kw)


def _ts(out, in0, s1, s2, op0, op1=None):
    if op1 is None:
        return lambda e: e.tensor_scalar(out, in0, s1, None, op0)
    return lambda e: e.tensor_scalar(out, in0, s1, s2, op0, op1)


def _stt(out, in0, sc, in1, op0, op1):
    return lambda e: e.scalar_tensor_tensor(out, in0, sc, in1, op0, op1)


def _tt(out, in0, in1, op):
    return lambda e: e.tensor_tensor(out, in0, in1, op)


def _cp(out, in_):
    return lambda e: e.tensor_copy(out, in_)


def _mm(out, lhsT, rhs, start, stop):
    return lambda e: e.matmul(out, lhsT, rhs, start=start, stop=stop)


def _tr(out, in_, ident):
    return lambda e: e.transpose(out, in_, ident)


def _dma(out, in_, **kw):
    return lambda e: e.dma_start(out=out, in_=in_, **kw)


def _memset(ap, v):
    return lambda e: e.memset(ap, v)


class K:
    def __init__(self, S, dbg=()):
        self.S = S
        self.dbg = dbg
        self.nc = bass.Bass("TRN2", target_bir_lowering=False)
        self.stack = contextlib.ExitStack()
        self.sc = Sched(self.nc, self.stack)

    def sb(self, st, name, shape, dt=F32):
        return st.enter_context(self.nc.sbuf_tensor(name, list(shape), dt))

    def ps(self, st, name, shape, dt=F32):
        return st.enter_context(self.nc.psum_tensor(name, list(shape), dt))

    def dram_in(self, name, shape, dt=F32):
        return self.nc.dram_tensor(name, list(shape), dt, kind="ExternalInput").ap()

    def dram_out(self, name, shape, dt=F32):
        return self.nc.dram_tensor(name, list(shape), dt, kind="ExternalOutput").ap()

    def dram_tmp(self, name, shape, dt=F32):
        if name in self.dbg:
            return self.nc.dram_tensor(name, list(shape), dt, kind="ExternalOutput").ap()
        return self.nc.dram_tensor(name, list(shape), dt).ap()


class Slot:
    def __init__(self, t, chan=None):
        self.t = t
        self.b = Buf()
        self.c = chan


def build(S, dbg=()):
    k = K(S, dbg)
    nc, sc = k.nc, k.sc
    NT, NB = S // 512, S // 128
    top = k.stack

    x = k.dram_in("x", [S, D])
    p_in = k.dram_in("p", [S, 256])
    ln_mix_g = k.dram_in("ln_mix_g", [D])
    w_in = k.dram_in("w_in", [D, IN_COLS])
    b_gates = k.dram_in("b_gates", [16])
    conv_qk_w = k.dram_in("conv_qk_w", [3, 1024])
    conv_qk_b = k.dram_in("conv_qk_b", [1024])
    mlstm_norm_g = k.dram_in("mlstm_norm_g", [512])
    q_norm_g = k.dram_in("q_norm_g", [256])
    w_uq = k.dram_in("w_uq", [256, 768])
    kv_norm_g = k.dram_in("kv_norm_g", [128])
    w_ukv = k.dram_in("w_ukv", [128, 1024])
    w_out = k.dram_in("w_out", [1024, 1024])
    ln_ffn_g = k.dram_in("ln_ffn_g", [D])
    w_up = k.dram_in("w_up", [D, 2 * D_FF])
    conv_ffn_w = k.dram_in("conv_ffn_w", [3, 2 * D_FF])
    conv_ffn_b = k.dram_in("conv_ffn_b", [2 * D_FF])
    w_down = k.dram_in("w_down", [D_FF, D])
    ple_norm_g = k.dram_in("ple_norm_g", [D])
    w_ple_gate = k.dram_in("w_ple_gate", [D, D])
    w_ple_proj = k.dram_in("w_ple_proj", [256, D])
    ple_post_g = k.dram_in("ple_post_g", [D])
    final_g = k.dram_in("final_g", [D])
    out = k.dram_out("out", [S, D])

    qkT = k.dram_tmp("qkT", [1024, S], BF16)
    vo_s = k.dram_tmp("vo_s", [S, 512])
    so_s = k.dram_tmp("so_s", [S, 512], BF16)
    g3_s = k.dram_tmp("g3_s", [S, 464])

    ident = k.sb(top, "ident", [128, 128], BF16)
    ones_f = k.sb(top, "ones_f", [128, 128], F32)
    mhalf = k.sb(top, "mhalf", [128, 16], F32)
    cb = Buf("consts")
    sc.add("pool", _memset(ones_f[:], 1.0), w=[cb])
    sc.add("pool", _memset(mhalf[:], -0.5), w=[cb])
    sc.add("pool", lambda e: e.affine_select(ident[:], ones_f[:], [[-1, 128]], ALU.is_equal, 0.0,
                                             base=0, channel_multiplier=1), r=[cb], w=[cb])
    sc.emit()

    ld_ch = [sc.chan(f"ldw{i}", keep=True) for i in range(2)]
    cnt = {"w": 0, "e": 0}

    def alt():
        cnt["e"] += 1
        return "dve" if cnt["e"] % 2 else "act"

    def scale_cast(eng, out_ap, in_ap, sc_ap=None):
        if eng == "act":
            if sc_ap is None:
                return _act(out_ap, in_ap, AF.Copy)
            return _act(out_ap, in_ap, AF.Copy, scale=sc_ap)
        if sc_ap is None:
            return _cp(out_ap, in_ap)
        return _ts(out_ap, in_ap, sc_ap, None, ALU.mult)

    def load_cols(st, name, src, G):
        t = k.sb(st, name, [128, G])
        b = Buf(name)
        ch = sc.chan("c_" + name)
        sc.add("sp", _dma(t[:], src.rearrange("(g p) -> p g", p=128), allow_slow_non_contiguous=True),
               w=[b], chan=ch)
        return t, b

    def load_bcast(st, name, src, n):
        t = k.sb(st, name, [128, n])
        b = Buf(name)
        ch = sc.chan("c_" + name)
        sc.add("sp", _dma(t[:], bass.AP(src.tensor, src.offset, [[0, 128], [1, n]])), w=[b], chan=ch)
        return t, b

    def load_w(st, stage, name, src, Kdim, cols, gain=None):
        kcn = Kdim // 128
        t = k.sb(st, name, [128, kcn, cols], BF16)
        b = Buf(name)
        for kc in range(kcn):
            sw = stage[0].t.shape[1]
            for c0 in range(0, cols, sw):
                w = min(sw, cols - c0)
                s = stage[cnt["w"] % 2]
                cnt["w"] += 1
                sc.add("sp", _dma(s.t[:, 0:w], src[kc * 128:(kc + 1) * 128, c0:c0 + w]), w=[s.b], chan=s.c)
                g = None if gain is None else gain[0][:, kc:kc + 1]
                rr = [s.b] + ([] if gain is None else [gain[1]])
                e = alt()
                sc.add(e, scale_cast(e, t[:, kc, c0:c0 + w], s.t[:, 0:w], g), r=rr, w=[b])
        return t, b

    with contextlib.ExitStack() as st:
        stage = [Slot(k.sb(st, f"wstage{i}", [128, 2816]), ld_ch[i]) for i in range(2)]
        gmix = load_cols(st, "gmix", ln_mix_g, 8)
        Win, Winb = load_w(st, stage, "Win", w_in, D, IN_COLS, gmix)
        cw = k.sb(st, "cw", [128, 8, 3])
        cwb = Buf("cw")
        for tap in range(3):
            sc.add("sp", _dma(cw[:, :, tap], conv_qk_w[tap].rearrange("(g p) -> p g", p=128),
                              allow_slow_non_contiguous=True), w=[Buf()], chan=sc.chan(f"c_cw{tap}"))
        cbias = load_cols(st, "cbias", conv_qk_b, 8)
        bg = load_bcast(st, "bg", b_gates, 16)
        sc.barrier()

        XT = [Slot(k.sb(st, f"xt{i}", [128, 4, D]), sc.chan(f"c_xt{i}")) for i in range(3)]
        XBA = [Slot(k.sb(st, f"xb{i}", [128, 4, D], BF16)) for i in range(2)]
        XN = [Slot(k.sb(st, f"xn{i}", [128, 8, 512], BF16)) for i in range(2)]
        junk = Slot(k.sb(st, "junk", [128, D], BF16))
        ss = Slot(k.sb(st, "ss", [128, 4]))
        ss2 = Slot(k.sb(st, "ss2", [128, 4]))
        rstd = Slot(k.sb(st, "rstd", [128, 4]))
        PRE = [Slot(k.sb(st, f"pre{g}", [128, 514])) for g in range(8)]
        ACC = [Slot(k.sb(st, f"acc{i}", [128, 512])) for i in range(2)]
        QKB = [Slot(k.sb(st, f"qkb{i}", [128, 512], BF16), sc.chan(f"c_qkb{i}")) for i in range(3)]
        VO = [Slot(k.sb(st, f"vo{i}", [128, 1024]), sc.chan(f"c_vo{i}")) for i in range(2)]
        G3 = [Slot(k.sb(st, f"g3{i}", [128, 464]), sc.chan(f"c_g3{i}")) for i in range(2)]
        SOB = [Slot(k.sb(st, f"sob{i}", [128, 512], BF16), sc.chan(f"c_sob{i}")) for i in range(2)]
        TB = [Slot(k.ps(st, f"tb{i}", [128, 1024], BF16)) for i in range(2)]
        MB = [Slot(k.ps(st, f"mb{i}", [128, 512])) for i in range(6)]
        last8 = Slot(k.sb(st, "last8", [128, 8]))
        last8b = Slot(k.sb(st, "last8b", [128, 8], BF16), sc.chan("c_last8"))
        for g in range(8):
            sc.add("pool", _memset(PRE[g].t[:, 0:2], 0.0), w=[PRE[g].b])
        cntA = {"mbi": 0, "qi": 0, "voi": 0}

        def stage1a(i):
            t0 = i * 512
            xt, XB = XT[i % 3], XBA[i % 2]
            sc.add("sp", _dma(xt.t[:], x[t0:t0 + 512, :].rearrange("(b p) d -> p b d", p=128)), w=[xt.b], chan=xt.c)
            for b in range(4):
                sc.add("act", _act(junk.t[:], xt.t[:, b, :], AF.Square, scale=1.0 / 32.0, accum_out=ss.t[:, b:b + 1]),
                       r=[xt.b], w=[junk.b, ss.b])
            sc.add("dve", _ts(ss2.t[:], ss.t[:], EPS, None, ALU.add), r=[ss.b], w=[ss2.b])
            sc.add("pool", _tt(rstd.t[:], ss2.t[:], mhalf[:, 0:4], ALU.pow), r=[ss2.b], w=[rstd.b])
            for b in range(4):
                e = "dve" if b % 2 else "act"
                sc.add(e, scale_cast(e, XB.t[:, b, :], xt.t[:, b, :], rstd.t[:, b:b + 1]), r=[xt.b, rstd.b], w=[XB.b])

        def stage1b(i):
            xn, XB = XN[i % 2], XBA[i % 2]
            for j in range(4):
                tb = TB[j % 2]
                for kk in range(2):
                    kc = 2 * j + kk
                    for b in range(4):
                        sc.add("pe", _tr(tb.t[:, kk * 512 + b * 128:kk * 512 + (b + 1) * 128],
                                         XB.t[:, b, kc * 128:(kc + 1) * 128], ident[:]), r=[XB.b], w=[tb.b])
                e = "dve" if j % 2 else "act"
                sc.add(e, scale_cast(e, xn.t[:, 2 * j:2 * j + 2, :], tb.t[:].rearrange("p (a b) -> p a b", a=2)),
                       r=[tb.b], w=[xn.b])

        def stage2fm(i):
            t0 = i * 512
            xn = XN[i % 2]
            mbi, qi, voi = cntA["mbi"], cntA["qi"], cntA["voi"]
            def tail(g, qi):
                pre = PRE[g]
                acc = ACC[g % 2]
                qb = QKB[qi % 3]
                sc.add("dve", _ts(acc.t[:], pre.t[:, 2:514], cw[:, g, 2:3], None, ALU.mult), r=[pre.b], w=[acc.b])
                sc.add("dve", _stt(acc.t[:], pre.t[:, 1:513], cw[:, g, 1:2], acc.t[:], ALU.mult, ALU.add),
                       r=[pre.b, acc.b], w=[acc.b])
                sc.add("dve", _stt(acc.t[:], pre.t[:, 0:512], cw[:, g, 0:1], acc.t[:], ALU.mult, ALU.add),
                       r=[pre.b, acc.b], w=[acc.b])
                sc.add("act", _act(qb.t[:], acc.t[:], AF.Silu, bias=cbias[0][:, g:g + 1]), r=[acc.b], w=[qb.b])
                if i == 0:
                    sc.add("pool", _dma(qkT[g * 128:(g + 1) * 128, 0:511], qb.t[:, 1:512]), r=[qb.b], chan=qb.c)
                else:
                    sc.add("pool", _dma(qkT[g * 128:(g + 1) * 128, t0 - 1:t0 + 511], qb.t[:]), r=[qb.b], chan=qb.c)
                sc.add("dve", _cp(pre.t[:, 0:2], pre.t[:, 512:514]), r=[pre.b], w=[pre.b])

            for g in range(8):
                pm = MB[mbi % 6]
                mbi += 1
                for kc in range(8):
                    sc.add("pe", _mm(pm.t[:], Win[:, kc, g * 128:(g + 1) * 128], xn.t[:, kc, :], kc == 0, kc == 7),
                           r=[xn.b, Winb], w=[pm.b])
                sc.add("act", _act(PRE[g].t[:, 2:514], pm.t[:], AF.Copy), r=[pm.b], w=[PRE[g].b])
                if g >= 1:
                    tail(g - 1, qi)
                    qi += 1
            tail(7, qi)
            qi += 1
            cntA["mbi"], cntA["qi"], cntA["voi"] = mbi, qi, voi

        def stage2tm(i):
            t0 = i * 512
            xn = XN[i % 2]
            mbi, qi, voi = cntA["mbi"], cntA["qi"], cntA["voi"]
            for b in range(4):
                vo = VO[voi % 2]
                g3 = G3[voi % 2]
                sob = SOB[voi % 2]
                voi += 1
                for part, (c0, c1) in enumerate(((1024, 1536), (1536, 2048), (2048, 2512))):
                    pm = MB[mbi % 6]
                    mbi += 1
                    for kc in range(8):
                        sc.add("pe", _mm(pm.t[:, 0:c1 - c0], xn.t[:, kc, b * 128:(b + 1) * 128], Win[:, kc, c0:c1],
                                         kc == 0, kc == 7), r=[xn.b, Winb], w=[pm.b])
                    if part == 0:
                        sc.add("dve", _cp(vo.t[:, 0:512], pm.t[:]), r=[pm.b], w=[vo.b])
                    elif part == 1:
                        sc.add("act", _act(vo.t[:, 512:1024], pm.t[:], AF.Tanh, scale=0.5), r=[pm.b], w=[vo.b])
                        sc.add("dve", _ts(sob.t[:], vo.t[:, 512:1024], 0.5, 0.5, ALU.mult, ALU.add),
                               r=[vo.b], w=[sob.b])
                    else:
                        sc.add("act", _act(g3.t[:], pm.t[:, 0:464], AF.Copy), r=[pm.b], w=[g3.b])
                        sc.add("dve", _tt(g3.t[:, 0:16], g3.t[:, 0:16], bg[0][:], ALU.add), r=[g3.b], w=[g3.b])
                r0 = t0 + b * 128
                sc.add("pool", _dma(vo_s[r0:r0 + 128, :], vo.t[:, 0:512]), r=[vo.b], chan=vo.c)
                sc.add("pool", _dma(so_s[r0:r0 + 128, :], sob.t[:]), r=[sob.b], chan=sob.c)
                sc.add("pool", _dma(g3_s[r0:r0 + 128, :], g3.t[:]), r=[g3.b], chan=g3.c)
            cntA["mbi"], cntA["qi"], cntA["voi"] = mbi, qi, voi

        stage1a(0)
        stage1b(0)
        if NT > 1:
            stage1a(1)
        for i in range(NT):
            stage2fm(i)
            if i + 1 < NT:
                stage1b(i + 1)
            if i + 2 < NT:
                stage1a(i + 2)
            stage2tm(i)
        prb = [PRE[g].b for g in range(8)]
        for g in range(8):
            sc.add("dve", _ts(last8.t[:, g:g + 1], PRE[g].t[:, 0:1], cw[:, g, 0:1], None, ALU.mult), r=[PRE[g].b], w=[last8.b])
            sc.add("dve", _stt(last8.t[:, g:g + 1], PRE[g].t[:, 1:2], cw[:, g, 1:2], last8.t[:, g:g + 1], ALU.mult, ALU.add),
                   r=[PRE[g].b, last8.b], w=[last8.b])
        sc.add("dve", _tt(last8.t[:], last8.t[:], cbias[0][:], ALU.add), r=[last8.b], w=[last8.b])
        sc.add("act", _act(last8b.t[:], last8.t[:], AF.Silu), r=[last8.b], w=[last8b.b])
        sc.add("pool", _dma(qkT.rearrange("(g p) s -> p g s", p=128)[:, :, S - 1], last8b.t[:],
                            allow_slow_non_contiguous=True), r=[last8b.b], chan=last8b.c)
        sc.emit()

    hf_s = k.dram_tmp("hf_s", [S, 512], BF16)
    hb_s = k.dram_tmp("hb_s", [S, 512], BF16)
    yT_s = k.dram_tmp("yT_s", [1024, S], BF16)

    def bc(ap, m):
        a = [list(d) for d in ap.ap]
        return bass.AP(ap.tensor, ap.offset, a + [[0, m]])

    def bc_mid(ap, m):
        a = [list(d) for d in ap.ap]
        return bass.AP(ap.tensor, ap.offset, [a[0], [0, m]] + a[1:])

    with contextlib.ExitStack() as st:
        maskF = k.sb(st, "maskF", [128, 128])
        maskB = k.sb(st, "maskB", [128, 128])
        mb_ = Buf("masks")
        sc.add("pool", lambda e: e.affine_select(maskF[:], ones_f[:], [[1, 128]], ALU.is_ge, 0.0,
                                                 base=0, channel_multiplier=-1), w=[mb_])
        sc.add("pool", lambda e: e.affine_select(maskB[:], ones_f[:], [[-1, 128]], ALU.is_ge, 0.0,
                                                 base=0, channel_multiplier=1), w=[mb_])
        hdst = (hf_s, hb_s)
        tris = (maskF, maskB)

        def ring(name, shape, dt=F32, chan=False, n=2):
            return [[Slot(k.sb(st, f"{name}{d}_{i}", shape, dt), sc.chan(f"c_{name}{d}_{i}") if chan else None) for i in range(n)]
                    for d in range(2)]

        SL = ring("sl", [128, 8, 512], BF16, True)
        VOT = ring("vot", [128, 512], F32, True)
        GT = ring("gt", [128, 16], F32, True)
        HO = ring("ho", [128, 512], BF16, True)
        AA = ring("aa", [128, 4])
        EG = ring("eg", [128, 8])
        V1 = ring("v1", [128, 4, 130], BF16)
        KTOK = ring("ktok", [128, 4, 128], BF16)
        MM = ring("mm", [128, 4, 128], BF16)
        e1 = [Slot(k.sb(st, f"e1_{d}", [128, 4])) for d in range(2)]
        lsp = [Slot(k.sb(st, f"lsp_{d}", [128, 4])) for d in range(2)]
        tmpa = [Slot(k.sb(st, f"tmpa_{d}", [128, 4])) for d in range(2)]
        C1 = [Slot(k.sb(st, f"c1_{d}", [128, 4, 130])) for d in range(2)]
        C1b = [Slot(k.sb(st, f"c1b_{d}", [128, 4, 130], BF16)) for d in range(2)]
        tmpC = [Slot(k.sb(st, f"tmpc_{d}", [128, 4, 130])) for d in range(2)]
        den = [Slot(k.sb(st, f"den_{d}", [128, 4])) for d in range(2)]
        rr_ = [Slot(k.sb(st, f"rr_{d}", [128, 4])) for d in range(2)]
        TBK = Slot(k.ps(st, "tbk", [128, 1024], BF16))
        SPS = [Slot(k.ps(st, f"sps{i}", [128, 512])) for i in range(2)]
        UPS = Slot(k.ps(st, "ups", [128, 1024]))
        DPS = Slot(k.ps(st, "dps", [128, 1024]))
        GPS = Slot(k.ps(st, "gps", [128, 512]))
        qkT3 = qkT.rearrange("(g p) s -> p g s", p=128)
        slab = [{"cg": None, "n": 0} for _ in range(2)]

        def pre(c, d, n):
            cg = c // 4
            if slab[d]["cg"] != cg:
                slab[d]["cg"] = cg
                slab[d]["n"] += 1
                sl = SL[d][slab[d]["n"] % 2]
                sc.add("sp", _dma(sl.t[:], qkT3[:, :, cg * 512:(cg + 1) * 512]), w=[sl.b], chan=sl.c)
            sl = SL[d][slab[d]["n"] % 2]
            vot, gt, aa, eg, v1, ktok, mm = (VOT[d][n % 2], GT[d][n % 2], AA[d][n % 2], EG[d][n % 2], V1[d][n % 2],
                                             KTOK[d][n % 2], MM[d][n % 2])
            sps = SPS[d]
            r0 = c * 128
            sc.add("sp", _dma(vot.t[:], vo_s[r0:r0 + 128, :]), w=[vot.b], chan=vot.c)
            sc.add("sp", _dma(gt.t[:], g3_s[r0:r0 + 128, 0:16]), w=[gt.b], chan=gt.c)
            io, fo = d * 8, d * 8 + 4
            tri = tris[d]
            sc.add("act", _act(e1[d].t[:], gt.t[:, fo:fo + 4], AF.Exp, scale=-1.0), r=[gt.b], w=[e1[d].b])
            sc.add("act", _act(lsp[d].t[:], e1[d].t[:], AF.Ln, bias=1.0), r=[e1[d].b], w=[lsp[d].b])
            gcol = d * 8
            sc.add("pe", _mm(GPS.t[:, gcol:gcol + 4], tri[:], lsp[d].t[:], True, True), r=[lsp[d].b, mb_], w=[GPS.b])
            sc.add("pe", _mm(GPS.t[:, gcol + 4:gcol + 8], ones_f[:], lsp[d].t[:], True, True), r=[lsp[d].b], w=[GPS.b])
            sc.add("dve", _tt(tmpa[d].t[:], gt.t[:, io:io + 4], GPS.t[:, gcol:gcol + 4], ALU.add), r=[gt.b, GPS.b], w=[tmpa[d].b])
            sc.add("act", _act(aa.t[:], tmpa[d].t[:], AF.Exp), r=[tmpa[d].b], w=[aa.b])
            sc.add("act", _act(eg.t[:], GPS.t[:, gcol:gcol + 8], AF.Exp, scale=-1.0), r=[GPS.b], w=[eg.b])
            sc.add("dve", _tt(v1.t[:, :, 0:128], vot.t[:].rearrange("p (h d) -> p h d", h=4), bc(aa.t[:], 128), ALU.mult),
                   r=[vot.b, aa.b], w=[v1.b])
            sc.add("dve", _cp(v1.t[:, :, 128], aa.t[:]), r=[aa.b], w=[v1.b])
            c4 = (c % 4) * 128
            tcol = d * 512
            for h in range(4):
                sc.add("pe", _tr(TBK.t[:, tcol + h * 128:tcol + (h + 1) * 128], sl.t[:, 4 + h, c4:c4 + 128], ident[:]), r=[sl.b], w=[TBK.b])
            sc.add("act", _act(ktok.t[:], TBK.t[:, tcol:tcol + 512].rearrange("p (h d) -> p h d", h=4), AF.Copy, scale=128.0 ** -0.5),
                   r=[TBK.b], w=[ktok.b])
            for h in range(4):
                sc.add("pe", _mm(sps.t[:, h * 128:(h + 1) * 128], sl.t[:, 4 + h, c4:c4 + 128], sl.t[:, h, c4:c4 + 128], True, True),
                       r=[sl.b], w=[sps.b])
            sc.add("dve", _stt(mm.t[:], sps.t[:].rearrange("p (h d) -> p h d", h=4), 128.0 ** -0.5, bc_mid(tri[:], 4), ALU.mult, ALU.mult),
                   r=[sps.b, mb_], w=[mm.b])
            return sl, c4

        def main(c, d, n, sl, c4):
            eg, v1, ktok, mm, ho = EG[d][n % 2], V1[d][n % 2], KTOK[d][n % 2], MM[d][n % 2], HO[d][n % 2]
            c1, c1b, tc_, dn, rr = C1[d], C1b[d], tmpC[d], den[d], rr_[d]
            r0 = c * 128
            U3 = UPS.t[:].rearrange("p (h d) -> p h d", h=4)
            D3 = DPS.t[:].rearrange("p (h d) -> p h d", h=4)
            for h in range(4):
                sc.add("pe", _mm(DPS.t[:, h * 256:h * 256 + 129], ktok.t[:, h, :], v1.t[:, h, 0:129], True, True), r=[ktok.b, v1.b], w=[DPS.b])
            for h in range(4):
                sc.add("pe", _mm(UPS.t[:, h * 256:h * 256 + 129], mm.t[:, h, :], v1.t[:, h, 0:129], True, False), r=[mm.b, v1.b], w=[UPS.b])
                sc.add("pe", _mm(UPS.t[:, h * 256:h * 256 + 129], sl.t[:, h, c4:c4 + 128], c1b.t[:, h, 0:129], False, True),
                       r=[sl.b, c1b.b], w=[UPS.b])
            sc.add("dve", _tt(c1.t[:, :, 0:129], D3[:, :, 0:129], tc_.t[:, :, 0:129], ALU.add), r=[DPS.b, tc_.b], w=[c1.b])
            sc.add("dve", _tt(tc_.t[:, :, 0:129], c1.t[:, :, 0:129], bc(eg.t[:, 4:8], 129), ALU.mult), r=[c1.b, eg.b], w=[tc_.b])
            sc.add("act", _act(c1b.t[:, :, 0:129], tc_.t[:, :, 0:129], AF.Copy), r=[tc_.b], w=[c1b.b])
            sc.add("dve", _tt(dn.t[:], U3[:, :, 128], eg.t[:, 0:4], ALU.mult), r=[UPS.b, eg.b], w=[dn.b])
            sc.add("act", _act(dn.t[:], dn.t[:], AF.Abs), r=[dn.b], w=[dn.b])
            sc.add("dve", _ts(dn.t[:], dn.t[:], 1.0, None, ALU.max), r=[dn.b], w=[dn.b])
            sc.add("dve", lambda e: e.reciprocal(rr.t[:], dn.t[:]), r=[dn.b], w=[rr.b])
            sc.add("dve", _tt(rr.t[:], rr.t[:], eg.t[:, 0:4], ALU.mult), r=[rr.b, eg.b], w=[rr.b])
            sc.add("dve", _tt(ho.t[:].rearrange("p (h d) -> p h d", h=4), U3[:, :, 0:128], bc(rr.t[:], 128), ALU.mult),
                   r=[UPS.b, rr.b], w=[ho.b])
            sc.add("pool", _dma(hdst[d][r0:r0 + 128, :], ho.t[:]), r=[ho.b], chan=ho.c)

        orders = (list(range(NB)), list(range(NB - 1, -1, -1)))
        nxt = [None, None]
        for d in range(2):
            sc.add("pool", _memset(C1[d].t[:], 0.0), w=[C1[d].b])
            sc.add("pool", _memset(tmpC[d].t[:], 0.0), w=[tmpC[d].b])
            sc.add("pool", _memset(C1b[d].t[:], 0.0), w=[C1b[d].b])
            nxt[d] = pre(orders[d][0], d, 0)
        for j in range(NB):
            cur = list(nxt)
            if j + 1 < NB:
                for d in range(2):
                    nxt[d] = pre(orders[d][j + 1], d, j + 1)
            for d in range(2):
                main(orders[d][j], d, j, *cur[d])
        sc.emit()

    KT_s = k.dram_tmp("KT_s", [512, S], BF16)
    KR_s = k.dram_tmp("KR_s", [64, S], BF16)
    V_s = k.dram_tmp("V_s", [S, 512], BF16)
    QN_s = k.dram_tmp("QN_s", [512, S], BF16)
    QR_s = k.dram_tmp("QR_s", [4, 65, S], BF16)
    kmax_s = k.dram_tmp("kmax_s", [128, 8])
    TWO_PI = 6.283185307179586

    with contextlib.ExitStack() as st:
        stage = [Slot(k.sb(st, f"wstagec{i}", [128, 1024]), ld_ch[i]) for i in range(2)]
        gq = load_cols(st, "gq", q_norm_g, 2)
        gkv = load_cols(st, "gkv", kv_norm_g, 1)
        Wuq, Wuqb = load_w(st, stage, "Wuq", w_uq, 256, 768, gq)
        Wkv, Wkvb = load_w(st, stage, "Wkv", w_ukv, 128, 1024, gkv)
        Wkv4 = Wkv[:, 0, :].rearrange("p (h t d) -> p h t d", h=4, t=2)
        cos2 = k.sb(st, "cos2", [128, NB, 64])
        sin1 = k.sb(st, "sin1", [128, NB, 32])
        tb_ = Buf("ropetab")
        pos = k.sb(st, "pos", [128, NB])
        invf = k.sb(st, "invf", [128, 32])
        ang = k.sb(st, "ang", [128, NB, 32])
        angi = k.sb(st, "angi", [128, NB, 32], mybir.dt.int32)
        angf = k.sb(st, "angf", [128, NB, 32])
        msk = k.sb(st, "msk", [128, NB, 32])
        sc.add("pool", lambda e: e.iota(pos[:], [[128, NB]], base=0, channel_multiplier=1, allow_small_or_imprecise_dtypes=True), w=[tb_])
        sc.add("pool", lambda e: e.iota(invf[:], [[1, 32]], base=0, channel_multiplier=0, allow_small_or_imprecise_dtypes=True), r=[tb_], w=[tb_])
        sc.add("act", _act(invf[:], invf[:], AF.Exp, scale=-float(np.log(10000.0)) / 32.0), r=[tb_], w=[tb_])
        sc.add("dve", _tt(ang[:], bc(pos[:], 32), bc_mid(invf[:], NB), ALU.mult), r=[tb_], w=[tb_])
        sc.add("dve", _ts(ang[:], ang[:], 1.0 / TWO_PI, None, ALU.mult), r=[tb_], w=[tb_])
        for which in range(2):
            if which == 1:
                sc.add("dve", _ts(ang[:], ang[:], 0.25, None, ALU.add), r=[tb_], w=[tb_])
            sc.add("dve", _cp(angi[:], ang[:]), r=[tb_], w=[tb_])
            sc.add("dve", _cp(angf[:], angi[:]), r=[tb_], w=[tb_])
            sc.add("dve", _tt(angf[:], ang[:], angf[:], ALU.subtract), r=[tb_], w=[tb_])
            sc.add("dve", _ts(msk[:], angf[:], 0.5, None, ALU.is_gt), r=[tb_], w=[tb_])
            sc.add("dve", _tt(angf[:], angf[:], msk[:], ALU.subtract), r=[tb_], w=[tb_])
            sc.add("dve", _ts(msk[:], angf[:], -0.5, None, ALU.is_lt), r=[tb_], w=[tb_])
            sc.add("dve", _tt(angf[:], angf[:], msk[:], ALU.add), r=[tb_], w=[tb_])
            if which == 0:
                sc.add("act", _act(sin1[:], angf[:], AF.Sin, scale=TWO_PI * (1.0 - 1e-6)), r=[tb_], w=[tb_])
            else:
                sc.add("act", _act(cos2[:, :, 0:32], angf[:], AF.Sin, scale=TWO_PI * (1.0 - 1e-6)), r=[tb_], w=[tb_])
                sc.add("act", _act(cos2[:, :, 32:64], angf[:], AF.Sin, scale=TWO_PI * (1.0 - 1e-6)), r=[tb_], w=[tb_])
        sc.barrier()

        G3T = [Slot(k.sb(st, f"g3t{i}", [128, 4, 464]), sc.chan(f"c_g3t{i}")) for i in range(2)]
        junkc = Slot(k.sb(st, "junkc", [128, 3072], BF16))
        ssq = Slot(k.sb(st, "ssq", [128, 8]))
        ssq2 = Slot(k.sb(st, "ssq2", [128, 8]))
        rst = Slot(k.sb(st, "rst", [128, 8]))
        cqn = Slot(k.sb(st, "cqn", [128, 4, 256], BF16))
        ckvn = Slot(k.sb(st, "ckvn", [128, 4, 128], BF16))
        tA = Slot(k.sb(st, "tA", [128, 4, 64]))
        tB = Slot(k.sb(st, "tB", [128, 4, 64]))
        krb = Slot(k.sb(st, "krb", [128, 4, 64], BF16))
        sqr = Slot(k.sb(st, "sqr", [128, 4, 64]))
        KR2 = [Slot(k.sb(st, f"kr2_{i}", [128, 4])) for i in range(2)]
        CQT = [Slot(k.sb(st, f"cqT{i}", [128, 2, 512], BF16)) for i in range(2)]
        CKVT = [Slot(k.sb(st, f"ckvT{i}", [128, 512], BF16)) for i in range(2)]
        krT = Slot(k.sb(st, "krT", [64, 512], BF16), sc.chan("c_krT"))
        KTS = [Slot(k.sb(st, f"kts{i}", [128, 512], BF16), sc.chan(f"c_kts{i}")) for i in range(2)]
        VS = [Slot(k.sb(st, f"vs{i}", [128, 512], BF16), sc.chan(f"c_vs{i}")) for i in range(2)]
        sqk = Slot(k.sb(st, "sqk", [128, 512]))
        kn2 = Slot(k.sb(st, "kn2", [128, 4, 4]))
        kmax = Slot(k.sb(st, "kmax", [128, 4]))
        kmt = Slot(k.sb(st, "kmt", [128, 4]))
        q_sb = Slot(k.sb(st, "q_sb", [128, 4, 768]))
        qtA = Slot(k.sb(st, "qtA", [128, 4, 4, 64]))
        qtB = Slot(k.sb(st, "qtB", [128, 4, 4, 64]))
        qbn = Slot(k.sb(st, "qbn", [128, 4, 4, 128], BF16))
        qbr = Slot(k.sb(st, "qbr", [128, 4, 4, 66], BF16))
        qn2 = Slot(k.sb(st, "qn2", [128, 16]))
        qn1 = Slot(k.sb(st, "qn1", [128, 16]))
        QS = [Slot(k.sb(st, f"qs{i}", [128, 2, 512], BF16), sc.chan(f"c_qs{i}")) for i in range(2)]
        QRS = [Slot(k.sb(st, f"qrs{i}", [65, 2, 512], BF16), sc.chan(f"c_qrs{i}")) for i in range(2)]
        PB = [Slot(k.ps(st, f"pb{i}", [128, 512])) for i in range(8)]
        pbi = {"i": 0}

        def bank():
            pbi["i"] += 1
            return PB[pbi["i"] % 8]

        def bfv(slot):
            return slot.t[:].bitcast(BF16)

        sc.add("pool", _memset(kmax.t[:], 0.0), w=[kmax.b])
        qmx = Slot(k.sb(st, "qmx", [128, 4]))
        qmt = Slot(k.sb(st, "qmt", [128, 4]))
        sc.add("pool", _memset(qmx.t[:], 0.0), w=[qmx.b])
        sc.add("pool", _memset(qbr.t[:], 0.0), w=[qbr.b])
        q4 = q_sb.t[:].rearrange("p b (h d) -> p b h d", h=4)
        cntC = {"kti": 0}

        def c_x(i):
            t0 = i * 512
            g3 = G3T[i % 2]
            kr2, cqT, ckvT = KR2[i % 2], CQT[i % 2], CKVT[i % 2]
            kti = cntC["kti"]
            sc.add("sp", _dma(g3.t[:], g3_s[t0:t0 + 512, :].rearrange("(b p) c -> p b c", p=128)), w=[g3.b], chan=g3.c)
            for b in range(4):
                sc.add("act", _act(junkc.t[:, 0:256], g3.t[:, b, 16:272], AF.Square, scale=1.0 / 16.0, accum_out=ssq.t[:, b:b + 1]),
                       r=[g3.b], w=[junkc.b, ssq.b])
                sc.add("act", _act(junkc.t[:, 0:128], g3.t[:, b, 272:400], AF.Square, scale=128.0 ** -0.5, accum_out=ssq.t[:, 4 + b:5 + b]),
                       r=[g3.b], w=[junkc.b, ssq.b])
            sc.add("dve", _ts(ssq2.t[:], ssq.t[:], EPS, None, ALU.add), r=[ssq.b], w=[ssq2.b])
            sc.add("pool", _tt(rst.t[:], ssq2.t[:], mhalf[:, 0:8], ALU.pow), r=[ssq2.b], w=[rst.b])
            sc.add("dve", _tt(cqn.t[:], g3.t[:, :, 16:272], bc(rst.t[:, 0:4], 256), ALU.mult), r=[g3.b, rst.b], w=[cqn.b])
            sc.add("dve", _tt(ckvn.t[:], g3.t[:, :, 272:400], bc(rst.t[:, 4:8], 128), ALU.mult), r=[g3.b, rst.b], w=[ckvn.b])
            xk = g3.t[:, :, 400:464]
            cs, sn = cos2[:, 4 * i:4 * i + 4, :], sin1[:, 4 * i:4 * i + 4, :]
            sc.add("dve", _tt(tA.t[:], xk, cs, ALU.mult), r=[g3.b], w=[tA.b])
            sc.add("dve", _tt(tB.t[:, :, 0:32], g3.t[:, :, 432:464], sn, ALU.mult), r=[g3.b], w=[tB.b])
            sc.add("dve", _tt(tB.t[:, :, 32:64], g3.t[:, :, 400:432], sn, ALU.mult), r=[g3.b], w=[tB.b])
            sc.add("dve", _tt(krb.t[:, :, 0:32], tA.t[:, :, 0:32], tB.t[:, :, 0:32], ALU.subtract), r=[tA.b, tB.b], w=[krb.b])
            sc.add("dve", _tt(krb.t[:, :, 32:64], tA.t[:, :, 32:64], tB.t[:, :, 32:64], ALU.add), r=[tA.b, tB.b], w=[krb.b])
            sc.add("act", _act(sqr.t[:], xk, AF.Square), r=[g3.b], w=[sqr.b])
            sc.add("dve", lambda e: e.tensor_reduce(kr2.t[:], sqr.t[:], AX.X, ALU.add), r=[sqr.b], w=[kr2.b])
            pa, pb2 = bank(), bank()
            for b in range(4):
                for kc in range(2):
                    sc.add("pe", _tr(bfv(pa)[:, kc * 512 + b * 128:kc * 512 + (b + 1) * 128], cqn.t[:, b, kc * 128:(kc + 1) * 128], ident[:]),
                           r=[cqn.b], w=[pa.b])
                sc.add("pe", _tr(bfv(pb2)[:, b * 128:(b + 1) * 128], ckvn.t[:, b, :], ident[:]), r=[ckvn.b], w=[pb2.b])
                sc.add("pe", _tr(bfv(pb2)[0:64, 512 + b * 128:512 + (b + 1) * 128], krb.t[:, b, :], ident[:]), r=[krb.b], w=[pb2.b])
            sc.add("act", _act(cqT.t[:], bfv(pa).rearrange("p (a b) -> p a b", a=2), AF.Copy), r=[pa.b], w=[cqT.b])
            sc.add("dve", _cp(ckvT.t[:], bfv(pb2)[:, 0:512]), r=[pb2.b], w=[ckvT.b])
            sc.add("act", _act(krT.t[:], bfv(pb2)[0:64, 512:1024], AF.Copy), r=[pb2.b], w=[krT.b])
            sc.add("pool", _dma(KR_s[:, t0:t0 + 512], krT.t[:]), r=[krT.b], chan=krT.c)
            for h in range(4):
                pk = bank()
                sc.add("pe", _mm(pk.t[:], Wkv[:, 0, h * 256:h * 256 + 128], ckvT.t[:], True, True), r=[ckvT.b, Wkvb], w=[pk.b])
                kts = KTS[kti % 2]
                kti += 1
                e = "act" if h % 2 else "dve"
                sc.add(e, scale_cast(e, kts.t[:], pk.t[:]), r=[pk.b], w=[kts.b])
                sc.add("pool", _dma(KT_s[h * 128:(h + 1) * 128, t0:t0 + 512], kts.t[:]), r=[kts.b], chan=kts.c)
            cntC["kti"] = kti

        def c_y(i):
            t0 = i * 512
            kr2, cqT, ckvT = KR2[i % 2], CQT[i % 2], CKVT[i % 2]
            cs, sn = cos2[:, 4 * i:4 * i + 4, :], sin1[:, 4 * i:4 * i + 4, :]
            for b in range(4):
                tok = slice(b * 128, (b + 1) * 128)
                pv = bank()
                sc.add("pe", _mm(pv.t[:].rearrange("p (h d) -> p h d", h=4), ckvT.t[:, tok], Wkv4[:, :, 1, :], True, True),
                       r=[ckvT.b, Wkvb], w=[pv.b])
                vs = VS[b % 2]
                sc.add("act", _act(vs.t[:], pv.t[:], AF.Copy), r=[pv.b], w=[vs.b])
                sc.add("pool", _dma(V_s[t0 + b * 128:t0 + (b + 1) * 128, :], vs.t[:]), r=[vs.b], chan=vs.c)
                pk = bank()
                sc.add("pe", _mm(pk.t[:].rearrange("p (h d) -> p h d", h=4), ckvT.t[:, tok], Wkv4[:, :, 0, :], True, True),
                       r=[ckvT.b, Wkvb], w=[pk.b])
                sc.add("act", _act(sqk.t[:], pk.t[:], AF.Square), r=[pk.b], w=[sqk.b])
                sc.add("dve", lambda e, b=b: e.tensor_reduce(kn2.t[:, b, :], sqk.t[:].rearrange("p (h d) -> p h d", h=4), AX.X, ALU.add),
                       r=[sqk.b], w=[kn2.b])
                pq0, pq1 = bank(), bank()
                for kc in range(2):
                    sc.add("pe", _mm(pq0.t[:], cqT.t[:, kc, tok], Wuq[:, kc, 0:512], kc == 0, kc == 1), r=[cqT.b, Wuqb], w=[pq0.b])
                for kc in range(2):
                    sc.add("pe", _mm(pq1.t[:, 0:256], cqT.t[:, kc, tok], Wuq[:, kc, 512:768], kc == 0, kc == 1), r=[cqT.b, Wuqb], w=[pq1.b])
                sc.add("act", _act(q_sb.t[:, b, 0:512], pq0.t[:], AF.Copy), r=[pq0.b], w=[q_sb.b])
                sc.add("dve", _cp(q_sb.t[:, b, 512:768], pq1.t[:, 0:256]), r=[pq1.b], w=[q_sb.b])
            sc.add("dve", _tt(kn2.t[:], kn2.t[:], bc(kr2.t[:], 4), ALU.add), r=[kn2.b, kr2.b], w=[kn2.b])
            sc.add("dve", lambda e: e.tensor_reduce(kmt.t[:], kn2.t[:].rearrange("p b h -> p h b"), AX.X, ALU.max), r=[kn2.b], w=[kmt.b])
            sc.add("dve", _tt(kmax.t[:], kmax.t[:], kmt.t[:], ALU.max), r=[kmax.b, kmt.b], w=[kmax.b])
            cs4 = bass.AP(cs.tensor, cs.offset, [list(cs.ap[0]), list(cs.ap[1]), [0, 4], list(cs.ap[2])])
            sn4 = bass.AP(sn.tensor, sn.offset, [list(sn.ap[0]), list(sn.ap[1]), [0, 4], list(sn.ap[2])])
            sc.add("dve", _tt(qtA.t[:], q4[:, :, :, 128:192], cs4, ALU.mult), r=[q_sb.b], w=[qtA.b])
            sc.add("dve", _tt(qtB.t[:, :, :, 0:32], q4[:, :, :, 160:192], sn4, ALU.mult), r=[q_sb.b], w=[qtB.b])
            sc.add("dve", _tt(qtB.t[:, :, :, 32:64], q4[:, :, :, 128:160], sn4, ALU.mult), r=[q_sb.b], w=[qtB.b])
            sc.add("dve", _tt(qbr.t[:, :, :, 0:32], qtA.t[:, :, :, 0:32], qtB.t[:, :, :, 0:32], ALU.subtract), r=[qtA.b, qtB.b], w=[qbr.b])
            sc.add("dve", _tt(qbr.t[:, :, :, 32:64], qtA.t[:, :, :, 32:64], qtB.t[:, :, :, 32:64], ALU.add), r=[qtA.b, qtB.b], w=[qbr.b])
            sc.add("dve", _cp(qbn.t[:], q4[:, :, :, 0:128]), r=[q_sb.b], w=[qbn.b])
            sc.add("act", _act(junkc.t[:], q_sb.t[:].rearrange("p b c -> p (b c)"), AF.Square), r=[q_sb.b], w=[junkc.b])
            sc.add("dve", lambda e: e.tensor_reduce(qn2.t[:], junkc.t[:].rearrange("p (g d) -> p g d", g=16), AX.X, ALU.add),
                   r=[junkc.b], w=[qn2.b])
            sc.add("dve", lambda e: e.tensor_reduce(qmt.t[:], qn2.t[:].rearrange("p (b h) -> p h b", b=4), AX.X, ALU.max), r=[qn2.b], w=[qmt.b])
            sc.add("dve", _tt(qmx.t[:], qmx.t[:], qmt.t[:], ALU.max), r=[qmx.b, qmt.b], w=[qmx.b])
            sc.add("pool", _tt(qn1.t[:], qn2.t[:], mhalf[:, 0:16], ALU.pow), r=[qn2.b], w=[qn1.b])
            sc.add("dve", _tt(qn1.t[:], qn1.t[:], qn2.t[:], ALU.mult), r=[qn1.b, qn2.b], w=[qn1.b])
            sc.add("dve", _ts(qbr.t[:, :, :, 64], qn1.t[:].rearrange("p (b h) -> p b h", b=4), -1.01, None, ALU.mult),
                   r=[qn1.b], w=[qbr.b])
            for hp in range(2):
                pn, pr = bank(), bank()
                for hh in range(2):
                    h = 2 * hp + hh
                    for b in range(4):
                        sc.add("pe", _tr(bfv(pn)[:, hh * 512 + b * 128:hh * 512 + (b + 1) * 128], qbn.t[:, b, h, :], ident[:]), r=[qbn.b], w=[pn.b])
                        sc.add("pe", _tr(bfv(pr)[0:65, hh * 512 + b * 128:hh * 512 + (b + 1) * 128], qbr.t[:, b, h, 0:65], ident[:]), r=[qbr.b], w=[pr.b])
                qs, qrs = QS[hp], QRS[hp]
                sc.add("act", _act(qs.t[:], bfv(pn).rearrange("p (a b) -> p a b", a=2), AF.Copy), r=[pn.b], w=[qs.b])
                sc.add("dve", _cp(qrs.t[:], bfv(pr)[0:65, :].rearrange("p (a b) -> p a b", a=2)), r=[pr.b], w=[qrs.b])
                sc.add("pool", _dma(QN_s.rearrange("(h p) s -> p h s", p=128)[:, 2 * hp:2 * hp + 2, t0:t0 + 512], qs.t[:]), r=[qs.b], chan=qs.c)
                sc.add("pool", _dma(QR_s.rearrange("h p s -> p h s")[:, 2 * hp:2 * hp + 2, t0:t0 + 512], qrs.t[:]), r=[qrs.b], chan=qrs.c)
        c_x(0)
        for i in range(NT):
            if i + 1 < NT:
                c_x(i + 1)
            c_y(i)
        kmo = Slot(k.sb(st, "kmo", [128, 8]), sc.chan("c_kmo"))
        sc.add("dve", _cp(kmo.t[:, 0:4], kmax.t[:]), r=[kmax.b], w=[kmo.b])
        sc.add("dve", _cp(kmo.t[:, 4:8], qmx.t[:]), r=[qmx.b], w=[kmo.b])
        sc.add("pool", _dma(kmax_s[:, :], kmo.t[:]), r=[kmo.b], chan=kmo.c)
        sc.emit()

    with contextlib.ExitStack() as st:
        KT = Slot(k.sb(st, "KT", [128, 4, S], BF16), sc.chan("c_KT"))
        KR = Slot(k.sb(st, "KR", [128, S], BF16), sc.chan("c_KR"))
        VR = Slot(k.sb(st, "VR", [128, NB, 512], BF16), sc.chan("c_VR"))
        kml = Slot(k.sb(st, "kml", [128, 8]), sc.chan("c_kml"))
        km1 = Slot(k.sb(st, "km1", [1, 8]))
        kmx = Slot(k.sb(st, "kmx", [128, 8]))
        cbias_ = Slot(k.sb(st, "cbias_", [128, 4]))
        phalf = Slot(k.sb(st, "phalf", [128, 4]))
        SPB = [Slot(k.ps(st, f"spb{i}", [128, 512])) for i in range(4)]
        OPB = [Slot(k.ps(st, f"opb{i}", [128, 512])) for i in range(2)]
        RSB = Slot(k.ps(st, "rsb", [128, 512]))
        RS = [Slot(k.ps(st, f"rs{i}", [128, 512])) for i in range(1)]
        NPT = 10
        PT = [Slot(k.sb(st, f"pt{i}", [128, 512], BF16)) for i in range(NPT)]
        rs_sb = Slot(k.sb(st, "rs_sb", [128, 512]))
        ones_b = Slot(k.sb(st, "ones_b", [128, 32], BF16))
        inv32 = Slot(k.sb(st, "inv32", [128, 128]))
        sc.add("pool", _memset(ones_b.t[:], 1.0), w=[ones_b.b])
        sc.add("pool", _memset(inv32.t[:], 1.0 / 32.0), w=[inv32.b])
        QN = [Slot(k.sb(st, f"qnt{i}", [128, 512], BF16), sc.chan(f"c_qn{i}")) for i in range(2)]
        QR = [Slot(k.sb(st, f"qrt{i}", [128, 512], BF16), sc.chan(f"c_qr{i}")) for i in range(2)]
        rinv = Slot(k.sb(st, "rinv", [128, 512]))
        YO = [Slot(k.sb(st, f"yo{i}", [128, 512], BF16), sc.chan(f"c_yo{i}")) for i in range(2)]
        sc.add("sp", _dma(KT.t[:], KT_s.rearrange("(h p) s -> p h s", p=128)), w=[KT.b], chan=KT.c)
        sc.add("sp", _dma(KR.t[0:64, :], KR_s[:, :]), w=[KR.b], chan=KR.c)
        sc.add("sp", _dma(KR.t[64:128, :], KR_s[:, :]), w=[KR.b], chan=sc.chan("c_KR2"))
        sc.add("sp", _dma(VR.t[:], V_s.rearrange("(c p) d -> p c d", p=128)), w=[VR.b], chan=VR.c)
        sc.add("sp", _dma(kml.t[:], kmax_s[:, :]), w=[kml.b], chan=kml.c)
        sc.add("pool", _memset(phalf.t[:], 0.5), w=[phalf.b])
        scale = 192.0 ** -0.5
        sc.add("pool", lambda e: e.tensor_reduce(km1.t[:], kml.t[:], AX.C, ALU.max), r=[kml.b], w=[km1.b])
        sc.add("pe", _mm(RSB.t[:, 0:8], ones_f[0:1, :], km1.t[:], True, True), r=[km1.b], w=[RSB.b])
        sc.add("dve", _cp(kmx.t[:], RSB.t[:, 0:8]), r=[RSB.b], w=[kmx.b])
        sc.add("dve", _tt(cbias_.t[:], kmx.t[:, 0:4], kmx.t[:, 4:8], ALU.mult), r=[kmx.b], w=[cbias_.b])
        sc.add("pool", _tt(cbias_.t[:], cbias_.t[:], phalf.t[:], ALU.pow), r=[cbias_.b, phalf.b], w=[cbias_.b])
        sc.add("dve", _ts(cbias_.t[:], cbias_.t[:], -1.01 * scale, None, ALU.mult), r=[cbias_.b], w=[cbias_.b])
        if "cb_dbg" in k.dbg:
            cb_dbg = k.dram_tmp("cb_dbg", [128, 12])
            dbt = Slot(k.sb(st, "dbt", [128, 12]), sc.chan("c_dbt"))
            sc.add("dve", _cp(dbt.t[:, 0:4], cbias_.t[:]), r=[cbias_.b], w=[dbt.b])
            sc.add("dve", _cp(dbt.t[:, 4:12], kmx.t[:]), r=[kmx.b], w=[dbt.b])
            sc.add("pool", _dma(cb_dbg[:, :], dbt.t[:]), r=[dbt.b], chan=dbt.c)
        it = 0
        pti = 0
        for h in range(4):
            for j in range(NT):
                qn, qr = QN[it % 2], QR[it % 2]
                opb, yo, rs = OPB[it % 2], YO[it % 2], RS[0]
                it += 1
                sc.add("sp", _dma(qn.t[:], QN_s[h * 128:(h + 1) * 128, j * 512:(j + 1) * 512]), w=[qn.b], chan=qn.c)
                sc.add("sp", _dma(qr.t[0:64, :], QR_s[h, 0:64, j * 512:(j + 1) * 512]), w=[qr.b], chan=qr.c)
                sc.add("sp", _dma(qr.t[64:128, :], QR_s[h, 0:64, j * 512:(j + 1) * 512]), w=[qr.b], chan=qr.c)

                def qk2(kc):
                    for r_ in range(2):
                        sp_ = SPB[(kc + r_) % 4]
                        ks = slice((kc + r_) * 128, (kc + r_ + 1) * 128)
                        sc.add("pe", _mm(sp_.t[:], KT.t[:, h, ks], qn.t[:], True, False), r=[KT.b, qn.b], w=[sp_.b])
                    for r_ in range(2):
                        sp_ = SPB[(kc + r_) % 4]
                        ks = slice((kc + r_) * 128, (kc + r_ + 1) * 128)
                        rows = slice(64 * r_, 64 * r_ + 64)
                        sc.add("pe", lambda e, sp_=sp_, ks=ks, rows=rows, r_=r_, qr=qr: e.matmul(sp_.t[:], KR.t[rows, ks], qr.t[rows, :], start=False, stop=True,
                                                                                             tile_position=(64 * r_, 0)),
                               r=[KR.b, qr.b], w=[sp_.b])

                qk2(0)
                grp = []
                for kc in range(NB):
                    if kc % 2 == 0 and kc + 2 < NB:
                        qk2(kc + 2)
                    sp_ = SPB[kc % 4]
                    pt = PT[pti % NPT]
                    pti += 1
                    sc.add("act", _act(pt.t[:], sp_.t[:], AF.Exp, scale=scale, bias=cbias_.t[:, h:h + 1]), r=[sp_.b, cbias_.b], w=[pt.b])
                    sc.add("pe", _mm(opb.t[:], VR.t[:, kc, h * 128:(h + 1) * 128], pt.t[:], kc == 0, kc == NB - 1),
                           r=[VR.b, pt.b], w=[opb.b])
                    grp.append(pt)
                    if len(grp) == 4:
                        for r_, ptr in enumerate(grp):
                            sc.add("pe", lambda e, r_=r_, ptr=ptr, kc=kc: e.matmul(rs.t[32 * r_:32 * r_ + 32, :], ones_b.t[:, 0:32], ptr.t[:],
                                                                               start=(kc == 3), stop=(kc == NB - 1),
                                                                               tile_position=(0, 32 * r_)),
                                   r=[ptr.b, ones_b.b], w=[rs.b])
                        grp = []
                sc.add("dve", _cp(rs_sb.t[:], rs.t[:]), r=[rs.b], w=[rs_sb.b])
                sc.add("pe", _mm(RSB.t[:], inv32.t[:], rs_sb.t[:], True, True), r=[rs_sb.b, inv32.b], w=[RSB.b])
                sc.add("dve", lambda e: e.reciprocal(rinv.t[:], RSB.t[:]), r=[RSB.b], w=[rinv.b])
                sc.add("dve", _tt(yo.t[:], opb.t[:], rinv.t[:], ALU.mult), r=[opb.b, rinv.b], w=[yo.b])
                sc.add("pool", _dma(yT_s[512 + h * 128:512 + (h + 1) * 128, j * 512:(j + 1) * 512], yo.t[:]), r=[yo.b], chan=yo.c)
        sc.emit()

    h1_s = k.dram_tmp("h1_s", [S, D])
    xn2T_s = k.dram_tmp("xn2T_s", [D, S + 2], BF16)
    h2_s = k.dram_tmp("h2_s", [S, D])
    yT3 = yT_s.rearrange("(g p) s -> p g s", p=128)
    xn2T3 = xn2T_s.rearrange("(g p) s -> p g s", p=128)

    def rms_transpose(xt, ss, ss2, rstd, junk, XB, xn, TBs, gain_scale=1.0 / 32.0):
        for b in range(4):
            sc.add("act", _act(junk.t[:], xt.t[:, b, :], AF.Square, scale=gain_scale, accum_out=ss.t[:, b:b + 1]),
                   r=[xt.b], w=[junk.b, ss.b])
        sc.add("dve", _ts(ss2.t[:], ss.t[:], EPS, None, ALU.add), r=[ss.b], w=[ss2.b])
        sc.add("pool", _tt(rstd.t[:], ss2.t[:], mhalf[:, 0:4], ALU.pow), r=[ss2.b], w=[rstd.b])
        for b in range(4):
            e = "dve" if b % 2 else "act"
            sc.add(e, scale_cast(e, XB.t[:, b, :], xt.t[:, b, :], rstd.t[:, b:b + 1]), r=[xt.b, rstd.b], w=[XB.b])
        for j in range(4):
            tb = TBs[j % 2]
            for kk in range(2):
                kc = 2 * j + kk
                for b in range(4):
                    sc.add("pe", _tr(tb.t[:, kk * 512 + b * 128:kk * 512 + (b + 1) * 128],
                                     XB.t[:, b, kc * 128:(kc + 1) * 128], ident[:]), r=[XB.b], w=[tb.b])
            e = "dve" if j % 2 else "act"
            sc.add(e, scale_cast(e, xn.t[:, 2 * j:2 * j + 2, :], tb.t[:].rearrange("p (a b) -> p a b", a=2)),
                   r=[tb.b], w=[xn.b])

    with contextlib.ExitStack() as st:
        stage = [Slot(k.sb(st, f"wstaged{i}", [128, 1024]), ld_ch[i]) for i in range(2)]
        Wout, Woutb = load_w(st, stage, "Wout", w_out, D, D)
        normg = load_bcast(st, "normg", mlstm_norm_g, 512)
        zt = Slot(k.sb(st, "zt", [128, 8, 2], BF16), sc.chan("c_zt"))
        sc.add("pool", _memset(zt.t[:], 0.0), w=[zt.b])
        sc.add("pool", _dma(xn2T3[:, :, 0:1], zt.t[:, :, 0:1], allow_slow_non_contiguous=True), r=[zt.b], chan=zt.c)
        sc.add("pool", _dma(xn2T3[:, :, S + 1:S + 2], zt.t[:, :, 1:2], allow_slow_non_contiguous=True), r=[zt.b], chan=zt.c)
        HFT = [Slot(k.sb(st, f"hft{i}", [128, 4, 512], BF16), sc.chan(f"c_hft{i}")) for i in range(2)]
        HBT = [Slot(k.sb(st, f"hbt{i}", [128, 4, 512], BF16), sc.chan(f"c_hbt{i}")) for i in range(2)]
        SOT = [Slot(k.sb(st, f"sot{i}", [128, 4, 512], BF16), sc.chan(f"c_sot{i}")) for i in range(2)]
        HS = Slot(k.sb(st, "hsd", [128, 4, 512]))
        SG = Slot(k.sb(st, "sgd", [128, 4, 512]))
        sqd = Slot(k.sb(st, "sqd", [128, 4, 512], BF16))
        ssn = Slot(k.sb(st, "ssnd", [128, 16]))
        rsn = Slot(k.sb(st, "rsnd", [128, 16]))
        YB = [Slot(k.sb(st, f"ybd{i}", [128, 4, 512], BF16)) for i in range(2)]
        YAT = [Slot(k.sb(st, f"yat{i}", [128, 4, 512], BF16)) for i in range(2)]
        YTT = [Slot(k.sb(st, f"ytt{i}", [128, 4, 512], BF16), sc.chan(f"c_ytt{i}")) for i in range(3)]
        XT = [Slot(k.sb(st, f"xtd{i}", [128, 4, D]), sc.chan(f"c_xtd{i}")) for i in range(3)]
        XB = Slot(k.sb(st, "xbd", [128, 4, D], BF16))
        XN = [Slot(k.sb(st, f"xnd{i}", [128, 8, 512], BF16), sc.chan(f"c_xnd{i}")) for i in range(2)]
        junk = Slot(k.sb(st, "junkd", [128, D], BF16))
        ss = Slot(k.sb(st, "ssd", [128, 4]))
        ss2 = Slot(k.sb(st, "ss2d", [128, 4]))
        rstd = Slot(k.sb(st, "rstdd", [128, 4]))
        TBs = [Slot(k.ps(st, f"tbd{i}", [128, 1024], BF16)) for i in range(2)]
        MB = [Slot(k.ps(st, f"mbd{i}", [128, 512])) for i in range(6)]
        cntD = {"mbi": 0}

        def loads(i):
            t0 = i * 512
            hft, hbt, sot, ytt = HFT[i % 2], HBT[i % 2], SOT[i % 2], YTT[i % 3]
            tv = lambda ap: ap[t0:t0 + 512, :].rearrange("(b p) d -> p b d", p=128)
            sc.add("sp", _dma(hft.t[:], tv(hf_s)), w=[hft.b], chan=hft.c)
            sc.add("sp", _dma(hbt.t[:], tv(hb_s)), w=[hbt.b], chan=hbt.c)
            sc.add("sp", _dma(sot.t[:], tv(so_s)), w=[sot.b], chan=sot.c)
            sc.add("sp", _dma(ytt.t[:], yT3[:, 4:8, t0:t0 + 512]), w=[ytt.b], chan=ytt.c)

        def loadx(i):
            t0 = i * 512
            xt = XT[i % 3]
            sc.add("sp", _dma(xt.t[:], x[t0:t0 + 512, :].rearrange("(b p) d -> p b d", p=128)), w=[xt.b], chan=xt.c)

        def comb1(i):
            hft, hbt, sot = HFT[i % 2], HBT[i % 2], SOT[i % 2]
            sc.add("dve", _tt(HS.t[:], hft.t[:], hbt.t[:], ALU.add), r=[hft.b, hbt.b], w=[HS.b])
            sc.add("act", _act(sqd.t[:], HS.t[:], AF.Square, scale=128.0 ** -0.5), r=[HS.b], w=[sqd.b])
            sc.add("pool", _tt(SG.t[:], sot.t[:], bc_mid(normg[0][:], 4), ALU.mult), r=[sot.b, normg[1]], w=[SG.b])
            sc.add("dve", lambda e: e.tensor_reduce(ssn.t[:], sqd.t[:].rearrange("p b (h d) -> p (b h) d", h=4), AX.X, ALU.add),
                   r=[sqd.b], w=[ssn.b])
            sc.add("dve", _ts(ssn.t[:], ssn.t[:], EPS, None, ALU.add), r=[ssn.b], w=[ssn.b])
            sc.add("pool", _tt(rsn.t[:], ssn.t[:], mhalf[:, 0:16], ALU.pow), r=[ssn.b], w=[rsn.b])

        def comb2(i):
            yb = YB[i % 2]
            h16 = HS.t[:].rearrange("p b (h d) -> p (b h) d", h=4)
            sc.add("dve", _tt(h16, h16, bc(rsn.t[:], 128), ALU.mult), r=[HS.b, rsn.b], w=[HS.b])
            sc.add("dve", _tt(yb.t[:], HS.t[:], SG.t[:], ALU.mult), r=[HS.b, SG.b], w=[yb.b])

        def norm1(i):
            t0 = i * 512
            xt = XT[i % 3]
            sc.add("sp", _dma(h1_s[t0:t0 + 512, :].rearrange("(b p) d -> p b d", p=128), xt.t[:]), r=[xt.b], chan=xt.c)
            for b in range(4):
                sc.add("act", _act(junk.t[:], xt.t[:, b, :], AF.Square, scale=1.0 / 32.0, accum_out=ss.t[:, b:b + 1]),
                       r=[xt.b], w=[junk.b, ss.b])
            sc.add("dve", _ts(ss2.t[:], ss.t[:], EPS, None, ALU.add), r=[ss.b], w=[ss2.b])
            sc.add("pool", _tt(rstd.t[:], ss2.t[:], mhalf[:, 0:4], ALU.pow), r=[ss2.b], w=[rstd.b])

        def norm2(i):
            xt = XT[i % 3]
            for b in range(4):
                e = "dve" if b % 2 else "act"
                sc.add(e, scale_cast(e, XB.t[:, b, :], xt.t[:, b, :], rstd.t[:, b:b + 1]), r=[xt.b, rstd.b], w=[XB.b])

        def mmstage(i):
            yb, yat, ytt, xt = YB[i % 2], YAT[i % 2], YTT[i % 3], XT[i % 3]
            for hp in range(2):
                tb = TBs[hp]
                for hh in range(2):
                    h = 2 * hp + hh
                    for b in range(4):
                        sc.add("pe", _tr(tb.t[:, hh * 512 + b * 128:hh * 512 + (b + 1) * 128], yb.t[:, b, h * 128:(h + 1) * 128], ident[:]),
                               r=[yb.b], w=[tb.b])
                e = "dve" if hp else "act"
                sc.add(e, scale_cast(e, yat.t[:, 2 * hp:2 * hp + 2, :], tb.t[:].rearrange("p (a b) -> p a b", a=2)), r=[tb.b], w=[yat.b])
            mbi = cntD["mbi"]
            for b in range(4):
                for half in range(2):
                    pm = MB[mbi % 6]
                    mbi += 1
                    for kc in range(8):
                        src = yat if kc < 4 else ytt
                        sc.add("pe", _mm(pm.t[:], src.t[:, kc % 4, b * 128:(b + 1) * 128], Wout[:, kc, half * 512:(half + 1) * 512],
                                         kc == 0, kc == 7), r=[src.b, Woutb], w=[pm.b])
                    sc.add("dve", _tt(xt.t[:, b, half * 512:(half + 1) * 512], pm.t[:], xt.t[:, b, half * 512:(half + 1) * 512], ALU.add),
                           r=[pm.b, xt.b], w=[xt.b])
            cntD["mbi"] = mbi

        def trstage(i):
            t0 = i * 512
            xn = XN[i % 2]
            for j in range(4):
                tb = TBs[j % 2]
                for kk in range(2):
                    kc = 2 * j + kk
                    for b in range(4):
                        sc.add("pe", _tr(tb.t[:, kk * 512 + b * 128:kk * 512 + (b + 1) * 128],
                                         XB.t[:, b, kc * 128:(kc + 1) * 128], ident[:]), r=[XB.b], w=[tb.b])
                e = "dve" if j % 2 else "act"
                sc.add(e, scale_cast(e, xn.t[:, 2 * j:2 * j + 2, :], tb.t[:].rearrange("p (a b) -> p a b", a=2)),
                       r=[tb.b], w=[xn.b])
            sc.add("sp", _dma(xn2T3[:, :, 1 + t0:1 + t0 + 512], xn.t[:]), r=[xn.b], chan=xn.c)

        loads(0)
        loadx(0)
        if NT > 1:
            loads(1)
            loadx(1)
        comb1(0)
        comb2(0)
        for i in range(NT + 1):
            if i + 2 < NT:
                loads(i + 2)
            if i >= 1:
                norm1(i - 1)
            if i + 1 < NT:
                comb1(i + 1)
            if i < NT:
                mmstage(i)
            if i >= 1:
                norm2(i - 1)
                trstage(i - 1)
            if i + 1 < NT:
                comb2(i + 1)
            if i + 2 < NT:
                loadx(i + 2)
        sc.emit()

    TT_ = 256
    with contextlib.ExitStack() as st:
        stage = [Slot(k.sb(st, f"wstagee{i}", [128, 1408]), ld_ch[i]) for i in range(2)]
        gffn = load_cols(st, "gffn", ln_ffn_g, 8)
        Wup, Wupb = load_w(st, stage, "Wup", w_up, D, 2 * D_FF, gffn)
        Wdn, Wdnb = load_w(st, stage, "Wdn", w_down, D_FF, D)
        fw = k.sb(st, "fw", [128, 44, 3])
        fwb = Buf("fw")
        for tap in range(3):
            sc.add("sp", _dma(fw[:, :, tap], conv_ffn_w[tap].rearrange("(g p) -> p g", p=128),
                              allow_slow_non_contiguous=True), w=[Buf()], chan=sc.chan(f"c_fw{tap}"))
        fb = load_cols(st, "fb", conv_ffn_b, 44)
        sc.barrier()
        XS = [Slot(k.sb(st, f"xs{i}", [128, 8, TT_ + 2], BF16), sc.chan(f"c_xs{i}")) for i in range(2)]
        H1 = [Slot(k.sb(st, f"h1t{i}", [128, 2, D]), sc.chan(f"c_h1t{i}")) for i in range(2)]
        AT = [Slot(k.sb(st, f"at{i}", [128, 22, TT_], BF16)) for i in range(2)]
        CG = [Slot(k.sb(st, f"cg{i}", [128, TT_])) for i in range(2)]
        CV = [Slot(k.sb(st, f"cv{i}", [128, TT_])) for i in range(2)]
        SG = [Slot(k.sb(st, f"sg{i}", [128, TT_])) for i in range(2)]
        MB = [Slot(k.ps(st, f"mbe{i}", [128, 512])) for i in range(8)]
        mbi = 0
        for i in range(S // TT_):
            t0 = i * TT_
            xs, h1, at = XS[i % 2], H1[i % 2], AT[i % 2]
            sc.add("sp", _dma(xs.t[:], xn2T3[:, :, t0:t0 + TT_ + 2]), w=[xs.b], chan=xs.c)
            sc.add("sp", _dma(h1.t[:], h1_s[t0:t0 + TT_, :].rearrange("(b p) d -> p b d", p=128)), w=[h1.b], chan=h1.c)
            for g in range(22):
                res = []
                for which, (gi, dst) in enumerate(((g, CG[g % 2]), (22 + g, CV[g % 2]))):
                    pm = MB[mbi % 8]
                    mbi += 1
                    for kc in range(8):
                        sc.add("pe", _mm(pm.t[:, 0:TT_ + 2], Wup[:, kc, gi * 128:(gi + 1) * 128], xs.t[:, kc, :], kc == 0, kc == 7),
                               r=[xs.b, Wupb], w=[pm.b])
                    sc.add("act", _act(dst.t[:], pm.t[:, 0:TT_], AF.Identity, scale=fw[:, gi, 0:1], bias=fb[0][:, gi:gi + 1]),
                           r=[pm.b], w=[dst.b])
                    sc.add("dve", _stt(dst.t[:], pm.t[:, 1:TT_ + 1], fw[:, gi, 1:2], dst.t[:], ALU.mult, ALU.add), r=[pm.b, dst.b], w=[dst.b])
                    sc.add("dve", _stt(dst.t[:], pm.t[:, 2:TT_ + 2], fw[:, gi, 2:3], dst.t[:], ALU.mult, ALU.add), r=[pm.b, dst.b], w=[dst.b])
                cg, cv, sg = CG[g % 2], CV[g % 2], SG[g % 2]
                sc.add("act", _act(sg.t[:], cg.t[:], AF.Silu), r=[cg.b], w=[sg.b])
                sc.add("pool", _tt(at.t[:, g, :], sg.t[:], cv.t[:], ALU.mult), r=[sg.b, cv.b], w=[at.b])
            for b in range(TT_ // 128):
                for half in range(2):
                    pm = MB[mbi % 8]
                    mbi += 1
                    for g in range(22):
                        sc.add("pe", _mm(pm.t[:], at.t[:, g, b * 128:(b + 1) * 128], Wdn[:, g, half * 512:(half + 1) * 512], g == 0, g == 21),
                               r=[at.b, Wdnb], w=[pm.b])
                    sc.add("dve", _tt(h1.t[:, b, half * 512:(half + 1) * 512], pm.t[:], h1.t[:, b, half * 512:(half + 1) * 512], ALU.add),
                           r=[pm.b, h1.b], w=[h1.b])
            sc.add("pool", _dma(h2_s[t0:t0 + TT_, :].rearrange("(b p) d -> p b d", p=128), h1.t[:]), r=[h1.b], chan=h1.c)
        sc.emit()

    with contextlib.ExitStack() as st:
        stage = [Slot(k.sb(st, f"wstagef{i}", [128, 1024]), ld_ch[i]) for i in range(2)]
        gple = load_cols(st, "gple", ple_norm_g, 8)
        Wg, Wgb = load_w(st, stage, "Wg", w_ple_gate, D, D, gple)
        Wp, Wpb = load_w(st, stage, "Wp", w_ple_proj, 256, D)
        postg = load_bcast(st, "postg", ple_post_g, D)
        fing = load_bcast(st, "fing", final_g, D)
        sc.barrier()
        XT = [Slot(k.sb(st, f"xtf{i}", [128, 4, D]), sc.chan(f"c_xtf{i}")) for i in range(3)]
        PTL = [Slot(k.sb(st, f"ptl{i}", [128, 4, 256]), sc.chan(f"c_ptl{i}")) for i in range(2)]
        XBF = [Slot(k.sb(st, f"xbf{i}", [128, 4, D], BF16)) for i in range(2)]
        PBF = [Slot(k.sb(st, f"pbf{i}", [128, 4, 256], BF16)) for i in range(2)]
        XNF = [Slot(k.sb(st, f"xnf{i}", [128, 8, 512], BF16)) for i in range(2)]
        PTTF = [Slot(k.sb(st, f"ptt{i}", [128, 2, 512], BF16)) for i in range(2)]
        junk = Slot(k.sb(st, "junkf", [128, D], BF16))
        ss = Slot(k.sb(st, "ssf", [128, 4]))
        ss2 = Slot(k.sb(st, "ss2f", [128, 4]))
        rstd = Slot(k.sb(st, "rstdf", [128, 4]))
        ssb = Slot(k.sb(st, "ssb", [128, 2]))
        ssb2 = Slot(k.sb(st, "ssb2", [128, 2]))
        rsb2 = Slot(k.sb(st, "rsb2", [128, 2]))
        SGM = [Slot(k.sb(st, f"sgm{i}", [128, D])) for i in range(2)]
        PJ = [Slot(k.sb(st, f"pj{i}", [128, D])) for i in range(2)]
        OT = [Slot(k.sb(st, f"ot{i}", [128, D]), sc.chan(f"c_ot{i}")) for i in range(3)]
        TBs = [Slot(k.ps(st, f"tbf{i}", [128, 1024], BF16)) for i in range(2)]
        MB = [Slot(k.ps(st, f"mbf{i}", [128, 512])) for i in range(6)]
        cntF = {"mbi": 0, "bi": 0}

        def f_stage1a(i):
            t0 = i * 512
            xt, ptl, XB, PBf = XT[i % 3], PTL[i % 2], XBF[i % 2], PBF[i % 2]
            sc.add("sp", _dma(xt.t[:], h2_s[t0:t0 + 512, :].rearrange("(b p) d -> p b d", p=128)), w=[xt.b], chan=xt.c)
            sc.add("sp", _dma(ptl.t[:], p_in[t0:t0 + 512, :].rearrange("(b p) d -> p b d", p=128)), w=[ptl.b], chan=ptl.c)
            for b in range(4):
                sc.add("act", _act(junk.t[:], xt.t[:, b, :], AF.Square, scale=1.0 / 32.0, accum_out=ss.t[:, b:b + 1]),
                       r=[xt.b], w=[junk.b, ss.b])
            sc.add("dve", _ts(ss2.t[:], ss.t[:], EPS, None, ALU.add), r=[ss.b], w=[ss2.b])
            sc.add("pool", _tt(rstd.t[:], ss2.t[:], mhalf[:, 0:4], ALU.pow), r=[ss2.b], w=[rstd.b])
            for b in range(4):
                e = "dve" if b % 2 else "act"
                sc.add(e, scale_cast(e, XB.t[:, b, :], xt.t[:, b, :], rstd.t[:, b:b + 1]), r=[xt.b, rstd.b], w=[XB.b])
            sc.add("act", _act(PBf.t[:], ptl.t[:], AF.Copy), r=[ptl.b], w=[PBf.b])

        def f_stage1b(i):
            XN, PTT, XB, PBf = XNF[i % 2], PTTF[i % 2], XBF[i % 2], PBF[i % 2]
            for j in range(4):
                tb = TBs[j % 2]
                for kk in range(2):
                    kc = 2 * j + kk
                    for b in range(4):
                        sc.add("pe", _tr(tb.t[:, kk * 512 + b * 128:kk * 512 + (b + 1) * 128],
                                         XB.t[:, b, kc * 128:(kc + 1) * 128], ident[:]), r=[XB.b], w=[tb.b])
                e = "dve" if j % 2 else "act"
                sc.add(e, scale_cast(e, XN.t[:, 2 * j:2 * j + 2, :], tb.t[:].rearrange("p (a b) -> p a b", a=2)),
                       r=[tb.b], w=[XN.b])
            tb = TBs[0]
            for kc in range(2):
                for b in range(4):
                    sc.add("pe", _tr(tb.t[:, kc * 512 + b * 128:kc * 512 + (b + 1) * 128], PBf.t[:, b, kc * 128:(kc + 1) * 128], ident[:]),
                           r=[PBf.b], w=[tb.b])
            sc.add("act", _act(PTT.t[:], tb.t[:].rearrange("p (a b) -> p a b", a=2), AF.Copy), r=[tb.b], w=[PTT.b])

        RN = 4
        SGM3 = SGM + [Slot(k.sb(st, f"sgm{i}", [128, D])) for i in range(2, RN)]
        PJ3 = PJ + [Slot(k.sb(st, f"pj{i}", [128, D])) for i in range(2, RN)]
        SSB = [Slot(k.sb(st, f"ssbr{i}", [128, 2])) for i in range(RN)]
        RSB2 = [Slot(k.sb(st, f"rsbr{i}", [128, 2])) for i in range(RN)]
        junk2 = Slot(k.sb(st, "junkf2", [128, D], BF16))

        def blk(n):
            i, b = divmod(n, 4)
            return i, b, XT[i % 3], SGM3[n % RN], PJ3[n % RN], SSB[n % RN], RSB2[n % RN], OT[n % 3]

        def f_a12(n):
            i, b, xt, sgm, pj, ssb_, rsb_, ot = blk(n)
            XN, PTT = XNF[i % 2], PTTF[i % 2]
            mbi = cntF["mbi"]
            tok = slice(b * 128, (b + 1) * 128)
            for half in range(2):
                hs = slice(half * 512, (half + 1) * 512)
                pm = MB[mbi % 6]
                mbi += 1
                for kc in range(8):
                    sc.add("pe", _mm(pm.t[:], XN.t[:, kc, tok], Wg[:, kc, hs], kc == 0, kc == 7), r=[XN.b, Wgb], w=[pm.b])
                sc.add("act", _act(sgm.t[:, hs], pm.t[:], AF.Sigmoid), r=[pm.b], w=[sgm.b])
                pm = MB[mbi % 6]
                mbi += 1
                for kc in range(2):
                    sc.add("pe", _mm(pm.t[:], PTT.t[:, kc, tok], Wp[:, kc, hs], kc == 0, kc == 1), r=[PTT.b, Wpb], w=[pm.b])
                sc.add("act", _act(pj.t[:, hs], pm.t[:], AF.Copy), r=[pm.b], w=[pj.b])
            cntF["mbi"] = mbi
            sc.add("act", _act(junk.t[:], pj.t[:], AF.Square, scale=1.0 / 32.0, accum_out=ssb_.t[:, 0:1]), r=[pj.b], w=[junk.b, ssb_.b])

        def f_d12(n):
            i, b, xt, sgm, pj, ssb_, rsb_, ot = blk(n)
            sc.add("dve", _ts(ssb_.t[:, 0:1], ssb_.t[:, 0:1], EPS, None, ALU.add), r=[ssb_.b], w=[ssb_.b])
            sc.add("pool", _tt(rsb_.t[:, 0:1], ssb_.t[:, 0:1], mhalf[:, 0:1], ALU.pow), r=[ssb_.b], w=[rsb_.b])
            sc.add("dve", _stt(sgm.t[:], sgm.t[:], rsb_.t[:, 0:1], postg[0][:], ALU.mult, ALU.mult), r=[sgm.b, rsb_.b, postg[1]], w=[sgm.b])
            sc.add("dve", _tt(pj.t[:], pj.t[:], sgm.t[:], ALU.mult), r=[pj.b, sgm.b], w=[pj.b])
            sc.add("dve", _tt(pj.t[:], pj.t[:], xt.t[:, b, :], ALU.add), r=[pj.b, xt.b], w=[pj.b])

        def f_a3(n):
            i, b, xt, sgm, pj, ssb_, rsb_, ot = blk(n)
            sc.add("act", _act(junk2.t[:], pj.t[:], AF.Square, scale=1.0 / 32.0, accum_out=ssb_.t[:, 1:2]), r=[pj.b], w=[junk2.b, ssb_.b])

        def f_d3(n):
            i, b, xt, sgm, pj, ssb_, rsb_, ot = blk(n)
            t0 = i * 512
            sc.add("dve", _ts(ssb_.t[:, 1:2], ssb_.t[:, 1:2], EPS, None, ALU.add), r=[ssb_.b], w=[ssb_.b])
            sc.add("pool", _tt(rsb_.t[:, 1:2], ssb_.t[:, 1:2], mhalf[:, 0:1], ALU.pow), r=[ssb_.b], w=[rsb_.b])
            sc.add("dve", _stt(ot.t[:], pj.t[:], rsb_.t[:, 1:2], fing[0][:], ALU.mult, ALU.mult), r=[pj.b, rsb_.b, fing[1]], w=[ot.b])
            sc.add("sp", _dma(out[t0 + b * 128:t0 + (b + 1) * 128, :], ot.t[:]), r=[ot.b], chan=ot.c)

        f_stage1a(0)
        f_stage1b(0)
        if NT > 1:
            f_stage1a(1)
        NBLK = NT * 4
        for n in range(-2, NBLK + 1):
            if 0 <= n + 2 < NBLK:
                f_a12(n + 2)
                i2, b2 = divmod(n + 2, 4)
                if b2 == 1:
                    if i2 + 1 < NT:
                        f_stage1b(i2 + 1)
                    if i2 + 2 < NT:
                        f_stage1a(i2 + 2)
            if 0 <= n + 1 < NBLK:
                f_d12(n + 1)
            if 0 <= n < NBLK:
                f_a3(n)
            if 0 <= n - 1 < NBLK:
                f_d3(n - 1)
        sc.emit()

    k.final_wait = None
    return k


def finish(k):
    return k.nc


_W_NAMES = ["ln_mix_g", "w_in", "b_gates", "conv_qk_w", "conv_qk_b", "mlstm_norm_g", "q_norm_g", "w_uq", "kv_norm_g",
            "w_ukv", "w_out", "ln_ffn_g", "w_up", "conv_ffn_w", "conv_ffn_b", "w_down", "ple_norm_g", "w_ple_gate",
            "w_ple_proj", "ple_post_g"]


def kernel(**inputs):
    x = np.asarray(inputs["x"])
    p = np.asarray(inputs["p"])
    B, S, _ = x.shape
    nc = finish(build(S))
    shared = {n: np.ascontiguousarray(np.asarray(inputs[n])[0], dtype=np.float32) for n in _W_NAMES}
    shared["final_g"] = np.ascontiguousarray(np.asarray(inputs["final_g"]), dtype=np.float32)
    in_maps = []
    for b in range(B):
        m = dict(shared)
        m["x"] = np.ascontiguousarray(x[b], dtype=np.float32)
        m["p"] = np.ascontiguousarray(p[0, b], dtype=np.float32)
        in_maps.append(m)
    res = run_bass_kernel_spmd(nc, in_maps, core_ids=list(range(B)))
    return np.stack([np.asarray(r["out"]) for r in res.results], axis=0).astype(np.float32)
```

```python
import contextlib
import numpy as np
import concourse.bass as bass
import concourse.mybir as mybir
from concourse.bass_utils import run_bass_kernel_spmd

F32 = mybir.dt.float32
BF16 = mybir.dt.bfloat16
AF = mybir.ActivationFunctionType
ALU = mybir.AluOpType
AX = mybir.AxisListType

D = 1024
NH = 4
IN_COLS = 2512
D_FF = 2816
EPS = 1e-6
SEM_MAX = 30000
STRICT_SAME_ENGINE = False


class Buf:
    __slots__ = ("name", "lw", "rd")

    def __init__(self, name=""):
        self.name = name
        self.lw = None
        self.rd = []


class Chan:
    __slots__ = ("sem", "count", "last")

    def __init__(self, sem):
        self.sem = sem
        self.count = 0
        self.last = None


class Op:
    __slots__ = ("eng", "fn", "deps", "sig", "signo", "chan", "cval", "done")


class Sched:
    ENGS = ("pe", "act", "dve", "pool", "sp")

    def __init__(self, nc, stack):
        self.nc = nc
        self.stack = stack
        self.ops = []
        self.last_on = {e: None for e in self.ENGS}
        self.pending_bar = {e: [] for e in self.ENGS}
        self.chans = []
        self.free_chans = []
        self.phase_chans = []
        self.cnt = {e: 0 for e in self.ENGS}
        self.sems = {e: [] for e in self.ENGS}
        self.waited = {e: {} for e in self.ENGS}

    def chan(self, name, keep=False):
        if self.free_chans and not keep:
            c = self.free_chans.pop()
        else:
            c = Chan(self.stack.enter_context(self.nc.semaphore(name)))
            self.chans.append(c)
        if not keep:
            self.phase_chans.append(c)
        return c

    def add(self, eng, fn, r=(), w=(), chan=None):
        op = Op()
        op.eng, op.fn, op.deps, op.sig, op.signo, op.chan, op.cval = eng, fn, {}, False, 0, chan, 0
        op.done = False
        for b in r:
            if b.lw is not None:
                op.deps[b.lw] = True
        for b in w:
            if b.lw is not None:
                op.deps.setdefault(b.lw, False)
            for q in b.rd:
                op.deps.setdefault(q, False)
        for b in r:
            b.rd.append(op)
        for b in w:
            b.lw = op
            b.rd = []
        if self.pending_bar[eng]:
            for d in self.pending_bar[eng]:
                op.deps[d] = True
            self.pending_bar[eng] = []
        if chan is not None:
            if chan.last is not None:
                op.deps[chan.last] = True
            chan.count += 16
            op.cval = chan.count
            chan.last = op
        op.deps.pop(op, None)
        self.ops.append(op)
        self.last_on[eng] = op
        return op

    def barrier(self):
        lasts = [o for o in self.last_on.values() if o is not None]
        lasts += [c.last for c in self.chans if c.last is not None]
        for e in self.ENGS:
            self.pending_bar[e] = list(lasts)

    def emit(self):
        nc = self.nc
        fin = self.add("sp", lambda e: e.nop())
        for c in self.chans:
            if c.last is not None and not c.last.done:
                fin.deps[c.last] = True
        for e in self.ENGS:
            self.pending_bar[e] = []
        for op in self.ops:
            for d in [d for d in op.deps if d.done]:
                del op.deps[d]
            for d, raw in op.deps.items():
                if d.chan is not None:
                    continue
                if d.eng == op.eng and (op.eng == "pe" or not (raw or STRICT_SAME_ENGINE)):
                    continue
                d.sig = True
        cnt = self.cnt
        for op in self.ops:
            if op.chan is None and op.sig:
                cnt[op.eng] += 1
                op.signo = cnt[op.eng]
        sems = self.sems
        for e in self.ENGS:
            n = cnt[e] // SEM_MAX + 1
            while len(sems[e]) < n:
                sems[e].append(self.stack.enter_context(nc.semaphore(f"s_{e}{len(sems[e])}")))
        per = {e: [o for o in self.ops if o.eng == e] for e in self.ENGS}
        handles = {"pe": "tensor", "act": "scalar", "dve": "vector", "pool": "gpsimd", "sp": "sync"}

        def run(e, eng):
            waited = self.waited[e]
            for op in per[e]:
                for d, raw in op.deps.items():
                    if d.chan is not None:
                        key, val, sem = ("c", id(d.chan)), d.cval, d.chan.sem
                    else:
                        if d.eng == e and (e == "pe" or not (raw or STRICT_SAME_ENGINE)):
                            continue
                        j = (d.signo - 1) // SEM_MAX
                        key, val, sem = (d.eng, j), d.signo - j * SEM_MAX, sems[d.eng][j]
                    if waited.get(key, 0) >= val:
                        continue
                    waited[key] = val
                    eng.wait_ge(sem, val)
                ins = op.fn(eng)
                if op.chan is not None:
                    ins.then_inc(op.chan.sem, 16)
                elif op.sig:
                    j = (op.signo - 1) // SEM_MAX
                    ins.then_inc(sems[e][j], 1)

        with nc.Block() as block:
            for e in self.ENGS:
                if per[e]:
                    getattr(block, handles[e])(lambda eng, e=e: run(e, eng))
        for op in self.ops:
            op.done = True
            op.fn = None
            op.deps = {}
        self.ops = []
        self.last_on = {e: None for e in self.ENGS}
        for c in self.phase_chans:
            c.last = None
        self.free_chans.extend(self.phase_chans)
        self.phase_chans = []


def _act(out, in_, func, **kw):
    return lambda e: e.activation(out, in_, func, **kw)


def _ts(out, in0, s1, s2, op0, op1=None):
    if op1 is None:
        return lambda e: e.tensor_scalar(out, in0, s1, None, op0)
    return lambda e: e.tensor_scalar(out, in0, s1, s2, op0, op1)


def _stt(out, in0, sc, in1, op0, op1):
    return lambda e: e.scalar_tensor_tensor(out, in0, sc, in1, op0, op1)


def _tt(out, in0, in1, op):
    return lambda e: e.tensor_tensor(out, in0, in1, op)


def _cp(out, in_):
    return lambda e: e.tensor_copy(out, in_)


def _mm(out, lhsT, rhs, start, stop):
    return lambda e: e.matmul(out, lhsT, rhs, start=start, stop=stop)


def _tr(out, in_, ident):
    return lambda e: e.transpose(out, in_, ident)


def _dma(out, in_, **kw):
    return lambda e: e.dma_start(out=out, in_=in_, **kw)


def _memset(ap, v):
    return lambda e: e.memset(ap, v)


class K:
    def __init__(self, S, dbg=()):
        self.S = S
        self.dbg = dbg
        self.nc = bass.Bass("TRN2", target_bir_lowering=False)
        self.stack = contextlib.ExitStack()
        self.sc = Sched(self.nc, self.stack)

    def sb(self, st, name, shape, dt=F32):
        return st.enter_context(self.nc.sbuf_tensor(name, list(shape), dt))

    def ps(self, st, name, shape, dt=F32):
        return st.enter_context(self.nc.psum_tensor(name, list(shape), dt))

    def dram_in(self, name, shape, dt=F32):
        return self.nc.dram_tensor(name, list(shape), dt, kind="ExternalInput").ap()

    def dram_out(self, name, shape, dt=F32):
        return self.nc.dram_tensor(name, list(shape), dt, kind="ExternalOutput").ap()

    def dram_tmp(self, name, shape, dt=F32):
        if name in self.dbg:
            return self.nc.dram_tensor(name, list(shape), dt, kind="ExternalOutput").ap()
        return self.nc.dram_tensor(name, list(shape), dt).ap()


class Slot:
    def __init__(self, t, chan=None):
        self.t = t
        self.b = Buf()
        self.c = chan


def build(S, dbg=()):
    k = K(S, dbg)
    nc, sc = k.nc, k.sc
    NT, NB = S // 512, S // 128
    top = k.stack

    x = k.dram_in("x", [S, D])
    p_in = k.dram_in("p", [S, 256])
    ln_mix_g = k.dram_in("ln_mix_g", [D])
    w_in = k.dram_in("w_in", [D, IN_COLS])
    b_gates = k.dram_in("b_gates", [16])
    conv_qk_w = k.dram_in("conv_qk_w", [3, 1024])
    conv_qk_b = k.dram_in("conv_qk_b", [1024])
    mlstm_norm_g = k.dram_in("mlstm_norm_g", [512])
    q_norm_g = k.dram_in("q_norm_g", [256])
    w_uq = k.dram_in("w_uq", [256, 768])
    kv_norm_g = k.dram_in("kv_norm_g", [128])
    w_ukv = k.dram_in("w_ukv", [128, 1024])
    w_out = k.dram_in("w_out", [1024, 1024])
    ln_ffn_g = k.dram_in("ln_ffn_g", [D])
    w_up = k.dram_in("w_up", [D, 2 * D_FF])
    conv_ffn_w = k.dram_in("conv_ffn_w", [3, 2 * D_FF])
    conv_ffn_b = k.dram_in("conv_ffn_b", [2 * D_FF])
    w_down = k.dram_in("w_down", [D_FF, D])
    ple_norm_g = k.dram_in("ple_norm_g", [D])
    w_ple_gate = k.dram_in("w_ple_gate", [D, D])
    w_ple_proj = k.dram_in("w_ple_proj", [256, D])
    ple_post_g = k.dram_in("ple_post_g", [D])
    final_g = k.dram_in("final_g", [D])
    out = k.dram_out("out", [S, D])

    qkT = k.dram_tmp("qkT", [1024, S], BF16)
    vo_s = k.dram_tmp("vo_s", [S, 512])
    so_s = k.dram_tmp("so_s", [S, 512], BF16)
    g3_s = k.dram_tmp("g3_s", [S, 464])

    ident = k.sb(top, "ident", [128, 128], BF16)
    ones_f = k.sb(top, "ones_f", [128, 128], F32)
    mhalf = k.sb(top, "mhalf", [128, 16], F32)
    cb = Buf("consts")
    sc.add("pool", _memset(ones_f[:], 1.0), w=[cb])
    sc.add("pool", _memset(mhalf[:], -0.5), w=[cb])
    sc.add("pool", lambda e: e.affine_select(ident[:], ones_f[:], [[-1, 128]], ALU.is_equal, 0.0,
                                             base=0, channel_multiplier=1), r=[cb], w=[cb])
    sc.emit()

    ld_ch = [sc.chan(f"ldw{i}", keep=True) for i in range(2)]
    cnt = {"w": 0, "e": 0}

    def alt():
        cnt["e"] += 1
        return "dve" if cnt["e"] % 2 else "act"

    def scale_cast(eng, out_ap, in_ap, sc_ap=None):
        if eng == "act":
            if sc_ap is None:
                return _act(out_ap, in_ap, AF.Copy)
            return _act(out_ap, in_ap, AF.Copy, scale=sc_ap)
        if sc_ap is None:
            return _cp(out_ap, in_ap)
        return _ts(out_ap, in_ap, sc_ap, None, ALU.mult)

    def load_cols(st, name, src, G):
        t = k.sb(st, name, [128, G])
        b = Buf(name)
        ch = sc.chan("c_" + name)
        sc.add("sp", _dma(t[:], src.rearrange("(g p) -> p g", p=128), allow_slow_non_contiguous=True),
               w=[b], chan=ch)
        return t, b

    def load_bcast(st, name, src, n):
        t = k.sb(st, name, [128, n])
        b = Buf(name)
        ch = sc.chan("c_" + name)
        sc.add("sp", _dma(t[:], bass.AP(src.tensor, src.offset, [[0, 128], [1, n]])), w=[b], chan=ch)
        return t, b

    def load_w(st, stage, name, src, Kdim, cols, gain=None):
        kcn = Kdim // 128
        t = k.sb(st, name, [128, kcn, cols], BF16)
        b = Buf(name)
        for kc in range(kcn):
            sw = stage[0].t.shape[1]
            for c0 in range(0, cols, sw):
                w = min(sw, cols - c0)
                s = stage[cnt["w"] % 2]
                cnt["w"] += 1
                sc.add("sp", _dma(s.t[:, 0:w], src[kc * 128:(kc + 1) * 128, c0:c0 + w]), w=[s.b], chan=s.c)
                g = None if gain is None else gain[0][:, kc:kc + 1]
                rr = [s.b] + ([] if gain is None else [gain[1]])
                e = alt()
                sc.add(e, scale_cast(e, t[:, kc, c0:c0 + w], s.t[:, 0:w], g), r=rr, w=[b])
        return t, b

    with contextlib.ExitStack() as st:
        stage = [Slot(k.sb(st, f"wstage{i}", [128, 2816]), ld_ch[i]) for i in range(2)]
        gmix = load_cols(st, "gmix", ln_mix_g, 8)
        Win, Winb = load_w(st, stage, "Win", w_in, D, IN_COLS, gmix)
        cw = k.sb(st, "cw", [128, 8, 3])
        cwb = Buf("cw")
        for tap in range(3):
            sc.add("sp", _dma(cw[:, :, tap], conv_qk_w[tap].rearrange("(g p) -> p g", p=128),
                              allow_slow_non_contiguous=True), w=[Buf()], chan=sc.chan(f"c_cw{tap}"))
        cbias = load_cols(st, "cbias", conv_qk_b, 8)
        bg = load_bcast(st, "bg", b_gates, 16)
        sc.barrier()

        XT = [Slot(k.sb(st, f"xt{i}", [128, 4, D]), sc.chan(f"c_xt{i}")) for i in range(3)]
        XBA = [Slot(k.sb(st, f"xb{i}", [128, 4, D], BF16)) for i in range(2)]
        XN = [Slot(k.sb(st, f"xn{i}", [128, 8, 512], BF16)) for i in range(2)]
        junk = Slot(k.sb(st, "junk", [128, D], BF16))
        ss = Slot(k.sb(st, "ss", [128, 4]))
        ss2 = Slot(k.sb(st, "ss2", [128, 4]))
        rstd = Slot(k.sb(st, "rstd", [128, 4]))
        PRE = [Slot(k.sb(st, f"pre{g}", [128, 514])) for g in range(8)]
        ACC = [Slot(k.sb(st, f"acc{i}", [128, 512])) for i in range(2)]
        QKB = [Slot(k.sb(st, f"qkb{i}", [128, 512], BF16), sc.chan(f"c_qkb{i}")) for i in range(3)]
        VO = [Slot(k.sb(st, f"vo{i}", [128, 1024]), sc.chan(f"c_vo{i}")) for i in range(2)]
        G3 = [Slot(k.sb(st, f"g3{i}", [128, 464]), sc.chan(f"c_g3{i}")) for i in range(2)]
        SOB = [Slot(k.sb(st, f"sob{i}", [128, 512], BF16), sc.chan(f"c_sob{i}")) for i in range(2)]
        TB = [Slot(k.ps(st, f"tb{i}", [128, 1024], BF16)) for i in range(2)]
        MB = [Slot(k.ps(st, f"mb{i}", [128, 512])) for i in range(6)]
        last8 = Slot(k.sb(st, "last8", [128, 8]))
        last8b = Slot(k.sb(st, "last8b", [128, 8], BF16), sc.chan("c_last8"))
        for g in range(8):
            sc.add("pool", _memset(PRE[g].t[:, 0:2], 0.0), w=[PRE[g].b])
        cntA = {"mbi": 0, "qi": 0, "voi": 0}

        def stage1a(i):
            t0 = i * 512
            xt, XB = XT[i % 3], XBA[i % 2]
            sc.add("sp", _dma(xt.t[:], x[t0:t0 + 512, :].rearrange("(b p) d -> p b d", p=128)), w=[xt.b], chan=xt.c)
            for b in range(4):
                sc.add("act", _act(junk.t[:], xt.t[:, b, :], AF.Square, scale=1.0 / 32.0, accum_out=ss.t[:, b:b + 1]),
                       r=[xt.b], w=[junk.b, ss.b])
            sc.add("dve", _ts(ss2.t[:], ss.t[:], EPS, None, ALU.add), r=[ss.b], w=[ss2.b])
            sc.add("pool", _tt(rstd.t[:], ss2.t[:], mhalf[:, 0:4], ALU.pow), r=[ss2.b], w=[rstd.b])
            for b in range(4):
                e = "dve" if b % 2 else "act"
                sc.add(e, scale_cast(e, XB.t[:, b, :], xt.t[:, b, :], rstd.t[:, b:b + 1]), r=[xt.b, rstd.b], w=[XB.b])

        def stage1b(i):
            xn, XB = XN[i % 2], XBA[i % 2]
            for j in range(4):
                tb = TB[j % 2]
                for kk in range(2):
                    kc = 2 * j + kk
                    for b in range(4):
                        sc.add("pe", _tr(tb.t[:, kk * 512 + b * 128:kk * 512 + (b + 1) * 128],
                                         XB.t[:, b, kc * 128:(kc + 1) * 128], ident[:]), r=[XB.b], w=[tb.b])
                e = "dve" if j % 2 else "act"
                sc.add(e, scale_cast(e, xn.t[:, 2 * j:2 * j + 2, :], tb.t[:].rearrange("p (a b) -> p a b", a=2)),
                       r=[tb.b], w=[xn.b])

        def stage2fm(i):
            t0 = i * 512
            xn = XN[i % 2]
            mbi, qi, voi = cntA["mbi"], cntA["qi"], cntA["voi"]
            def tail(g, qi):
                pre = PRE[g]
                acc = ACC[g % 2]
                qb = QKB[qi % 3]
                sc.add("dve", _ts(acc.t[:], pre.t[:, 2:514], cw[:, g, 2:3], None, ALU.mult), r=[pre.b], w=[acc.b])
                sc.add("dve", _stt(acc.t[:], pre.t[:, 1:513], cw[:, g, 1:2], acc.t[:], ALU.mult, ALU.add),
                       r=[pre.b, acc.b], w=[acc.b])
                sc.add("dve", _stt(acc.t[:], pre.t[:, 0:512], cw[:, g, 0:1], acc.t[:], ALU.mult, ALU.add),
                       r=[pre.b, acc.b], w=[acc.b])
                sc.add("act", _act(qb.t[:], acc.t[:], AF.Silu, bias=cbias[0][:, g:g + 1]), r=[acc.b], w=[qb.b])
                if i == 0:
                    sc.add("pool", _dma(qkT[g * 128:(g + 1) * 128, 0:511], qb.t[:, 1:512]), r=[qb.b], chan=qb.c)
                else:
                    sc.add("pool", _dma(qkT[g * 128:(g + 1) * 128, t0 - 1:t0 + 511], qb.t[:]), r=[qb.b], chan=qb.c)
                sc.add("dve", _cp(pre.t[:, 0:2], pre.t[:, 512:514]), r=[pre.b], w=[pre.b])

            for g in range(8):
                pm = MB[mbi % 6]
                mbi += 1
                for kc in range(8):
                    sc.add("pe", _mm(pm.t[:], Win[:, kc, g * 128:(g + 1) * 128], xn.t[:, kc, :], kc == 0, kc == 7),
                           r=[xn.b, Winb], w=[pm.b])
                sc.add("act", _act(PRE[g].t[:, 2:514], pm.t[:], AF.Copy), r=[pm.b], w=[PRE[g].b])
                if g >= 1:
                    tail(g - 1, qi)
                    qi += 1
            tail(7, qi)
            qi += 1
            cntA["mbi"], cntA["qi"], cntA["voi"] = mbi, qi, voi

        def stage2tm(i):
            t0 = i * 512
            xn = XN[i % 2]
            mbi, qi, voi = cntA["mbi"], cntA["qi"], cntA["voi"]
            for b in range(4):
                vo = VO[voi % 2]
                g3 = G3[voi % 2]
                sob = SOB[voi % 2]
                voi += 1
                for part, (c0, c1) in enumerate(((1024, 1536), (1536, 2048), (2048, 2512))):
                    pm = MB[mbi % 6]
                    mbi += 1
                    for kc in range(8):
                        sc.add("pe", _mm(pm.t[:, 0:c1 - c0], xn.t[:, kc, b * 128:(b + 1) * 128], Win[:, kc, c0:c1],
                                         kc == 0, kc == 7), r=[xn.b, Winb], w=[pm.b])
                    if part == 0:
                        sc.add("dve", _cp(vo.t[:, 0:512], pm.t[:]), r=[pm.b], w=[vo.b])
                    elif part == 1:
                        sc.add("act", _act(vo.t[:, 512:1024], pm.t[:], AF.Tanh, scale=0.5), r=[pm.b], w=[vo.b])
                        sc.add("dve", _ts(sob.t[:], vo.t[:, 512:1024], 0.5, 0.5, ALU.mult, ALU.add),
                               r=[vo.b], w=[sob.b])
                    else:
                        sc.add("act", _act(g3.t[:], pm.t[:, 0:464], AF.Copy), r=[pm.b], w=[g3.b])
                        sc.add("dve", _tt(g3.t[:, 0:16], g3.t[:, 0:16], bg[0][:], ALU.add), r=[g3.b], w=[g3.b])
                r0 = t0 + b * 128
                sc.add("pool", _dma(vo_s[r0:r0 + 128, :], vo.t[:, 0:512]), r=[vo.b], chan=vo.c)
                sc.add("pool", _dma(so_s[r0:r0 + 128, :], sob.t[:]), r=[sob.b], chan=sob.c)
                sc.add("pool", _dma(g3_s[r0:r0 + 128, :], g3.t[:]), r=[g3.b], chan=g3.c)
            cntA["mbi"], cntA["qi"], cntA["voi"] = mbi, qi, voi

        stage1a(0)
        stage1b(0)
        if NT > 1:
            stage1a(1)
        for i in range(NT):
            stage2fm(i)
            if i + 1 < NT:
                stage1b(i + 1)
            if i + 2 < NT:
                stage1a(i + 2)
            stage2tm(i)
        prb = [PRE[g].b for g in range(8)]
        for g in range(8):
            sc.add("dve", _ts(last8.t[:, g:g + 1], PRE[g].t[:, 0:1], cw[:, g, 0:1], None, ALU.mult), r=[PRE[g].b], w=[last8.b])
            sc.add("dve", _stt(last8.t[:, g:g + 1], PRE[g].t[:, 1:2], cw[:, g, 1:2], last8.t[:, g:g + 1], ALU.mult, ALU.add),
                   r=[PRE[g].b, last8.b], w=[last8.b])
        sc.add("dve", _tt(last8.t[:], last8.t[:], cbias[0][:], ALU.add), r=[last8.b], w=[last8.b])
        sc.add("act", _act(last8b.t[:], last8.t[:], AF.Silu), r=[last8.b], w=[last8b.b])
        sc.add("pool", _dma(qkT.rearrange("(g p) s -> p g s", p=128)[:, :, S - 1], last8b.t[:],
                            allow_slow_non_contiguous=True), r=[last8b.b], chan=last8b.c)
        sc.emit()

    hf_s = k.dram_tmp("hf_s", [S, 512], BF16)
    hb_s = k.dram_tmp("hb_s", [S, 512], BF16)
    yT_s = k.dram_tmp("yT_s", [1024, S], BF16)

    def bc(ap, m):
        a = [list(d) for d in ap.ap]
        return bass.AP(ap.tensor, ap.offset, a + [[0, m]])

    def bc_mid(ap, m):
        a = [list(d) for d in ap.ap]
        return bass.AP(ap.tensor, ap.offset, [a[0], [0, m]] + a[1:])

    with contextlib.ExitStack() as st:
        maskF = k.sb(st, "maskF", [128, 128])
        maskB = k.sb(st, "maskB", [128, 128])
        mb_ = Buf("masks")
        sc.add("pool", lambda e: e.affine_select(maskF[:], ones_f[:], [[1, 128]], ALU.is_ge, 0.0,
                                                 base=0, channel_multiplier=-1), w=[mb_])
        sc.add("pool", lambda e: e.affine_select(maskB[:], ones_f[:], [[-1, 128]], ALU.is_ge, 0.0,
                                                 base=0, channel_multiplier=1), w=[mb_])
        hdst = (hf_s, hb_s)
        tris = (maskF, maskB)

        def ring(name, shape, dt=F32, chan=False, n=2):
            return [[Slot(k.sb(st, f"{name}{d}_{i}", shape, dt), sc.chan(f"c_{name}{d}_{i}") if chan else None) for i in range(n)]
                    for d in range(2)]

        SL = ring("sl", [128, 8, 512], BF16, True)
        VOT = ring("vot", [128, 512], F32, True)
        GT = ring("gt", [128, 16], F32, True)
        HO = ring("ho", [128, 512], BF16, True)
        AA = ring("aa", [128, 4])
        EG = ring("eg", [128, 8])
        V1 = ring("v1", [128, 4, 130], BF16)
        KTOK = ring("ktok", [128, 4, 128], BF16)
        MM = ring("mm", [128, 4, 128], BF16)
        e1 = [Slot(k.sb(st, f"e1_{d}", [128, 4])) for d in range(2)]
        lsp = [Slot(k.sb(st, f"lsp_{d}", [128, 4])) for d in range(2)]
        tmpa = [Slot(k.sb(st, f"tmpa_{d}", [128, 4])) for d in range(2)]
        C1 = [Slot(k.sb(st, f"c1_{d}", [128, 4, 130])) for d in range(2)]
        C1b = [Slot(k.sb(st, f"c1b_{d}", [128, 4, 130], BF16)) for d in range(2)]
        tmpC = [Slot(k.sb(st, f"tmpc_{d}", [128, 4, 130])) for d in range(2)]
        den = [Slot(k.sb(st, f"den_{d}", [128, 4])) for d in range(2)]
        rr_ = [Slot(k.sb(st, f"rr_{d}", [128, 4])) for d in range(2)]
        TBK = Slot(k.ps(st, "tbk", [128, 1024], BF16))
        SPS = [Slot(k.ps(st, f"sps{i}", [128, 512])) for i in range(2)]
        UPS = Slot(k.ps(st, "ups", [128, 1024]))
        DPS = Slot(k.ps(st, "dps", [128, 1024]))
        GPS = Slot(k.ps(st, "gps", [128, 512]))
        qkT3 = qkT.rearrange("(g p) s -> p g s", p=128)
        slab = [{"cg": None, "n": 0} for _ in range(2)]

        def pre(c, d, n):
            cg = c // 4
            if slab[d]["cg"] != cg:
                slab[d]["cg"] = cg
                slab[d]["n"] += 1
                sl = SL[d][slab[d]["n"] % 2]
                sc.add("sp", _dma(sl.t[:], qkT3[:, :, cg * 512:(cg + 1) * 512]), w=[sl.b], chan=sl.c)
            sl = SL[d][slab[d]["n"] % 2]
            vot, gt, aa, eg, v1, ktok, mm = (VOT[d][n % 2], GT[d][n % 2], AA[d][n % 2], EG[d][n % 2], V1[d][n % 2],
                                             KTOK[d][n % 2], MM[d][n % 2])
            sps = SPS[d]
            r0 = c * 128
            sc.add("sp", _dma(vot.t[:], vo_s[r0:r0 + 128, :]), w=[vot.b], chan=vot.c)
            sc.add("sp", _dma(gt.t[:], g3_s[r0:r0 + 128, 0:16]), w=[gt.b], chan=gt.c)
            io, fo = d * 8, d * 8 + 4
            tri = tris[d]
            sc.add("act", _act(e1[d].t[:], gt.t[:, fo:fo + 4], AF.Exp, scale=-1.0), r=[gt.b], w=[e1[d].b])
            sc.add("act", _act(lsp[d].t[:], e1[d].t[:], AF.Ln, bias=1.0), r=[e1[d].b], w=[lsp[d].b])
            gcol = d * 8
            sc.add("pe", _mm(GPS.t[:, gcol:gcol + 4], tri[:], lsp[d].t[:], True, True), r=[lsp[d].b, mb_], w=[GPS.b])
            sc.add("pe", _mm(GPS.t[:, gcol + 4:gcol + 8], ones_f[:], lsp[d].t[:], True, True), r=[lsp[d].b], w=[GPS.b])
            sc.add("dve", _tt(tmpa[d].t[:], gt.t[:, io:io + 4], GPS.t[:, gcol:gcol + 4], ALU.add), r=[gt.b, GPS.b], w=[tmpa[d].b])
            sc.add("act", _act(aa.t[:], tmpa[d].t[:], AF.Exp), r=[tmpa[d].b], w=[aa.b])
            sc.add("act", _act(eg.t[:], GPS.t[:, gcol:gcol + 8], AF.Exp, scale=-1.0), r=[GPS.b], w=[eg.b])
            sc.add("dve", _tt(v1.t[:, :, 0:128], vot.t[:].rearrange("p (h d) -> p h d", h=4), bc(aa.t[:], 128), ALU.mult),
                   r=[vot.b, aa.b], w=[v1.b])
            sc.add("dve", _cp(v1.t[:, :, 128], aa.t[:]), r=[aa.b], w=[v1.b])
            c4 = (c % 4) * 128
            tcol = d * 512
            for h in range(4):
                sc.add("pe", _tr(TBK.t[:, tcol + h * 128:tcol + (h + 1) * 128], sl.t[:, 4 + h, c4:c4 + 128], ident[:]), r=[sl.b], w=[TBK.b])
            sc.add("act", _act(ktok.t[:], TBK.t[:, tcol:tcol + 512].rearrange("p (h d) -> p h d", h=4), AF.Copy, scale=128.0 ** -0.5),
                   r=[TBK.b], w=[ktok.b])
            for h in range(4):
                sc.add("pe", _mm(sps.t[:, h * 128:(h + 1) * 128], sl.t[:, 4 + h, c4:c4 + 128], sl.t[:, h, c4:c4 + 128], True, True),
                       r=[sl.b], w=[sps.b])
            sc.add("dve", _stt(mm.t[:], sps.t[:].rearrange("p (h d) -> p h d", h=4), 128.0 ** -0.5, bc_mid(tri[:], 4), ALU.mult, ALU.mult),
                   r=[sps.b, mb_], w=[mm.b])
            return sl, c4

        def main(c, d, n, sl, c4):
            eg, v1, ktok, mm, ho = EG[d][n % 2], V1[d][n % 2], KTOK[d][n % 2], MM[d][n % 2], HO[d][n % 2]
            c1, c1b, tc_, dn, rr = C1[d], C1b[d], tmpC[d], den[d], rr_[d]
            r0 = c * 128
            U3 = UPS.t[:].rearrange("p (h d) -> p h d", h=4)
            D3 = DPS.t[:].rearrange("p (h d) -> p h d", h=4)
            for h in range(4):
                sc.add("pe", _mm(DPS.t[:, h * 256:h * 256 + 129], ktok.t[:, h, :], v1.t[:, h, 0:129], True, True), r=[ktok.b, v1.b], w=[DPS.b])
            for h in range(4):
                sc.add("pe", _mm(UPS.t[:, h * 256:h * 256 + 129], mm.t[:, h, :], v1.t[:, h, 0:129], True, False), r=[mm.b, v1.b], w=[UPS.b])
                sc.add("pe", _mm(UPS.t[:, h * 256:h * 256 + 129], sl.t[:, h, c4:c4 + 128], c1b.t[:, h, 0:129], False, True),
                       r=[sl.b, c1b.b], w=[UPS.b])
            sc.add("dve", _tt(c1.t[:, :, 0:129], D3[:, :, 0:129], tc_.t[:, :, 0:129], ALU.add), r=[DPS.b, tc_.b], w=[c1.b])
            sc.add("dve", _tt(tc_.t[:, :, 0:129], c1.t[:, :, 0:129], bc(eg.t[:, 4:8], 129), ALU.mult), r=[c1.b, eg.b], w=[tc_.b])
            sc.add("act", _act(c1b.t[:, :, 0:129], tc_.t[:, :, 0:129], AF.Copy), r=[tc_.b], w=[c1b.b])
            sc.add("dve", _tt(dn.t[:], U3[:, :, 128], eg.t[:, 0:4], ALU.mult), r=[UPS.b, eg.b], w=[dn.b])
            sc.add("act", _act(dn.t[:], dn.t[:], AF.Abs), r=[dn.b], w=[dn.b])
            sc.add("dve", _ts(dn.t[:], dn.t[:], 1.0, None, ALU.max), r=[dn.b], w=[dn.b])
            sc.add("dve", lambda e: e.reciprocal(rr.t[:], dn.t[:]), r=[dn.b], w=[rr.b])
            sc.add("dve", _tt(rr.t[:], rr.t[:], eg.t[:, 0:4], ALU.mult), r=[rr.b, eg.b], w=[rr.b])
            sc.add("dve", _tt(ho.t[:].rearrange("p (h d) -> p h d", h=4), U3[:, :, 0:128], bc(rr.t[:], 128), ALU.mult),
                   r=[UPS.b, rr.b], w=[ho.b])
            sc.add("pool", _dma(hdst[d][r0:r0 + 128, :], ho.t[:]), r=[ho.b], chan=ho.c)

        orders = (list(range(NB)), list(range(NB - 1, -1, -1)))
        nxt = [None, None]
        for d in range(2):
            sc.add("pool", _memset(C1[d].t[:], 0.0), w=[C1[d].b])
            sc.add("pool", _memset(tmpC[d].t[:], 0.0), w=[tmpC[d].b])
            sc.add("pool", _memset(C1b[d].t[:], 0.0), w=[C1b[d].b])
            nxt[d] = pre(orders[d][0], d, 0)
        for j in range(NB):
            cur = list(nxt)
            if j + 1 < NB:
                for d in range(2):
                    nxt[d] = pre(orders[d][j + 1], d, j + 1)
            for d in range(2):
                main(orders[d][j], d, j, *cur[d])
        sc.emit()

    KT_s = k.dram_tmp("KT_s", [512, S], BF16)
    KR_s = k.dram_tmp("KR_s", [64, S], BF16)
    V_s = k.dram_tmp("V_s", [S, 512], BF16)
    QN_s = k.dram_tmp("QN_s", [512, S], BF16)
    QR_s = k.dram_tmp("QR_s", [4, 65, S], BF16)
    kmax_s = k.dram_tmp("kmax_s", [128, 8])
    TWO_PI = 6.283185307179586

    with contextlib.ExitStack() as st:
        stage = [Slot(k.sb(st, f"wstagec{i}", [128, 1024]), ld_ch[i]) for i in range(2)]
        gq = load_cols(st, "gq", q_norm_g, 2)
        gkv = load_cols(st, "gkv", kv_norm_g, 1)
        Wuq, Wuqb = load_w(st, stage, "Wuq", w_uq, 256, 768, gq)
        Wkv, Wkvb = load_w(st, stage, "Wkv", w_ukv, 128, 1024, gkv)
        Wkv4 = Wkv[:, 0, :].rearrange("p (h t d) -> p h t d", h=4, t=2)
        cos2 = k.sb(st, "cos2", [128, NB, 64])
        sin1 = k.sb(st, "sin1", [128, NB, 32])
        tb_ = Buf("ropetab")
        pos = k.sb(st, "pos", [128, NB])
        invf = k.sb(st, "invf", [128, 32])
        ang = k.sb(st, "ang", [128, NB, 32])
        angi = k.sb(st, "angi", [128, NB, 32], mybir.dt.int32)
        angf = k.sb(st, "angf", [128, NB, 32])
        msk = k.sb(st, "msk", [128, NB, 32])
        sc.add("pool", lambda e: e.iota(pos[:], [[128, NB]], base=0, channel_multiplier=1, allow_small_or_imprecise_dtypes=True), w=[tb_])
        sc.add("pool", lambda e: e.iota(invf[:], [[1, 32]], base=0, channel_multiplier=0, allow_small_or_imprecise_dtypes=True), r=[tb_], w=[tb_])
        sc.add("act", _act(invf[:], invf[:], AF.Exp, scale=-float(np.log(10000.0)) / 32.0), r=[tb_], w=[tb_])
        sc.add("dve", _tt(ang[:], bc(pos[:], 32), bc_mid(invf[:], NB), ALU.mult), r=[tb_], w=[tb_])
        sc.add("dve", _ts(ang[:], ang[:], 1.0 / TWO_PI, None, ALU.mult), r=[tb_], w=[tb_])
        for which in range(2):
            if which == 1:
                sc.add("dve", _ts(ang[:], ang[:], 0.25, None, ALU.add), r=[tb_], w=[tb_])
            sc.add("dve", _cp(angi[:], ang[:]), r=[tb_], w=[tb_])
            sc.add("dve", _cp(angf[:], angi[:]), r=[tb_], w=[tb_])
            sc.add("dve", _tt(angf[:], ang[:], angf[:], ALU.subtract), r=[tb_], w=[tb_])
            sc.add("dve", _ts(msk[:], angf[:], 0.5, None, ALU.is_gt), r=[tb_], w=[tb_])
            sc.add("dve", _tt(angf[:], angf[:], msk[:], ALU.subtract), r=[tb_], w=[tb_])
            sc.add("dve", _ts(msk[:], angf[:], -0.5, None, ALU.is_lt), r=[tb_], w=[tb_])
            sc.add("dve", _tt(angf[:], angf[:], msk[:], ALU.add), r=[tb_], w=[tb_])
            if which == 0:
                sc.add("act", _act(sin1[:], angf[:], AF.Sin, scale=TWO_PI * (1.0 - 1e-6)), r=[tb_], w=[tb_])
            else:
                sc.add("act", _act(cos2[:, :, 0:32], angf[:], AF.Sin, scale=TWO_PI * (1.0 - 1e-6)), r=[tb_], w=[tb_])
                sc.add("act", _act(cos2[:, :, 32:64], angf[:], AF.Sin, scale=TWO_PI * (1.0 - 1e-6)), r=[tb_], w=[tb_])
        sc.barrier()

        G3T = [Slot(k.sb(st, f"g3t{i}", [128, 4, 464]), sc.chan(f"c_g3t{i}")) for i in range(2)]
        junkc = Slot(k.sb(st, "junkc", [128, 3072], BF16))
        ssq = Slot(k.sb(st, "ssq", [128, 8]))
        ssq2 = Slot(k.sb(st, "ssq2", [128, 8]))
        rst = Slot(k.sb(st, "rst", [128, 8]))
        cqn = Slot(k.sb(st, "cqn", [128, 4, 256], BF16))
        ckvn = Slot(k.sb(st, "ckvn", [128, 4, 128], BF16))
        tA = Slot(k.sb(st, "tA", [128, 4, 64]))
        tB = Slot(k.sb(st, "tB", [128, 4, 64]))
        krb = Slot(k.sb(st, "krb", [128, 4, 64], BF16))
        sqr = Slot(k.sb(st, "sqr", [128, 4, 64]))
        KR2 = [Slot(k.sb(st, f"kr2_{i}", [128, 4])) for i in range(2)]
        CQT = [Slot(k.sb(st, f"cqT{i}", [128, 2, 512], BF16)) for i in range(2)]
        CKVT = [Slot(k.sb(st, f"ckvT{i}", [128, 512], BF16)) for i in range(2)]
        krT = Slot(k.sb(st, "krT", [64, 512], BF16), sc.chan("c_krT"))
        KTS = [Slot(k.sb(st, f"kts{i}", [128, 512], BF16), sc.chan(f"c_kts{i}")) for i in range(2)]
        VS = [Slot(k.sb(st, f"vs{i}", [128, 512], BF16), sc.chan(f"c_vs{i}")) for i in range(2)]
        sqk = Slot(k.sb(st, "sqk", [128, 512]))
        kn2 = Slot(k.sb(st, "kn2", [128, 4, 4]))
        kmax = Slot(k.sb(st, "kmax", [128, 4]))
        kmt = Slot(k.sb(st, "kmt", [128, 4]))
        q_sb = Slot(k.sb(st, "q_sb", [128, 4, 768]))
        qtA = Slot(k.sb(st, "qtA", [128, 4, 4, 64]))
        qtB = Slot(k.sb(st, "qtB", [128, 4, 4, 64]))
        qbn = Slot(k.sb(st, "qbn", [128, 4, 4, 128], BF16))
        qbr = Slot(k.sb(st, "qbr", [128, 4, 4, 66], BF16))
        qn2 = Slot(k.sb(st, "qn2", [128, 16]))
        qn1 = Slot(k.sb(st, "qn1", [128, 16]))
        QS = [Slot(k.sb(st, f"qs{i}", [128, 2, 512], BF16), sc.chan(f"c_qs{i}")) for i in range(2)]
        QRS = [Slot(k.sb(st, f"qrs{i}", [65, 2, 512], BF16), sc.chan(f"c_qrs{i}")) for i in range(2)]
        PB = [Slot(k.ps(st, f"pb{i}", [128, 512])) for i in range(8)]
        pbi = {"i": 0}

        def bank():
            pbi["i"] += 1
            return PB[pbi["i"] % 8]

        def bfv(slot):
            return slot.t[:].bitcast(BF16)

        sc.add("pool", _memset(kmax.t[:], 0.0), w=[kmax.b])
        qmx = Slot(k.sb(st, "qmx", [128, 4]))
        qmt = Slot(k.sb(st, "qmt", [128, 4]))
        sc.add("pool", _memset(qmx.t[:], 0.0), w=[qmx.b])
        sc.add("pool", _memset(qbr.t[:], 0.0), w=[qbr.b])
        q4 = q_sb.t[:].rearrange("p b (h d) -> p b h d", h=4)
        cntC = {"kti": 0}

        def c_x(i):
            t0 = i * 512
            g3 = G3T[i % 2]
            kr2, cqT, ckvT = KR2[i % 2], CQT[i % 2], CKVT[i % 2]
            kti = cntC["kti"]
            sc.add("sp", _dma(g3.t[:], g3_s[t0:t0 + 512, :].rearrange("(b p) c -> p b c", p=128)), w=[g3.b], chan=g3.c)
            for b in range(4):
                sc.add("act", _act(junkc.t[:, 0:256], g3.t[:, b, 16:272], AF.Square, scale=1.0 / 16.0, accum_out=ssq.t[:, b:b + 1]),
                       r=[g3.b], w=[junkc.b, ssq.b])
                sc.add("act", _act(junkc.t[:, 0:128], g3.t[:, b, 272:400], AF.Square, scale=128.0 ** -0.5, accum_out=ssq.t[:, 4 + b:5 + b]),
                       r=[g3.b], w=[junkc.b, ssq.b])
            sc.add("dve", _ts(ssq2.t[:], ssq.t[:], EPS, None, ALU.add), r=[ssq.b], w=[ssq2.b])
            sc.add("pool", _tt(rst.t[:], ssq2.t[:], mhalf[:, 0:8], ALU.pow), r=[ssq2.b], w=[rst.b])
            sc.add("dve", _tt(cqn.t[:], g3.t[:, :, 16:272], bc(rst.t[:, 0:4], 256), ALU.mult), r=[g3.b, rst.b], w=[cqn.b])
            sc.add("dve", _tt(ckvn.t[:], g3.t[:, :, 272:400], bc(rst.t[:, 4:8], 128), ALU.mult), r=[g3.b, rst.b], w=[ckvn.b])
            xk = g3.t[:, :, 400:464]
            cs, sn = cos2[:, 4 * i:4 * i + 4, :], sin1[:, 4 * i:4 * i + 4, :]
            sc.add("dve", _tt(tA.t[:], xk, cs, ALU.mult), r=[g3.b], w=[tA.b])
            sc.add("dve", _tt(tB.t[:, :, 0:32], g3.t[:, :, 432:464], sn, ALU.mult), r=[g3.b], w=[tB.b])
            sc.add("dve", _tt(tB.t[:, :, 32:64], g3.t[:, :, 400:432], sn, ALU.mult), r=[g3.b], w=[tB.b])
            sc.add("dve", _tt(krb.t[:, :, 0:32], tA.t[:, :, 0:32], tB.t[:, :, 0:32], ALU.subtract), r=[tA.b, tB.b], w=[krb.b])
            sc.add("dve", _tt(krb.t[:, :, 32:64], tA.t[:, :, 32:64], tB.t[:, :, 32:64], ALU.add), r=[tA.b, tB.b], w=[krb.b])
            sc.add("act", _act(sqr.t[:], xk, AF.Square), r=[g3.b], w=[sqr.b])
            sc.add("dve", lambda e: e.tensor_reduce(kr2.t[:], sqr.t[:], AX.X, ALU.add), r=[sqr.b], w=[kr2.b])
            pa, pb2 = bank(), bank()
            for b in range(4):
                for kc in range(2):
                    sc.add("pe", _tr(bfv(pa)[:, kc * 512 + b * 128:kc * 512 + (b + 1) * 128], cqn.t[:, b, kc * 128:(kc + 1) * 128], ident[:]),
                           r=[cqn.b], w=[pa.b])
                sc.add("pe", _tr(bfv(pb2)[:, b * 128:(b + 1) * 128], ckvn.t[:, b, :], ident[:]), r=[ckvn.b], w=[pb2.b])
                sc.add("pe", _tr(bfv(pb2)[0:64, 512 + b * 128:512 + (b + 1) * 128], krb.t[:, b, :], ident[:]), r=[krb.b], w=[pb2.b])
            sc.add("act", _act(cqT.t[:], bfv(pa).rearrange("p (a b) -> p a b", a=2), AF.Copy), r=[pa.b], w=[cqT.b])
            sc.add("dve", _cp(ckvT.t[:], bfv(pb2)[:, 0:512]), r=[pb2.b], w=[ckvT.b])
            sc.add("act", _act(krT.t[:], bfv(pb2)[0:64, 512:1024], AF.Copy), r=[pb2.b], w=[krT.b])
            sc.add("pool", _dma(KR_s[:, t0:t0 + 512], krT.t[:]), r=[krT.b], chan=krT.c)
            for h in range(4):
                pk = bank()
                sc.add("pe", _mm(pk.t[:], Wkv[:, 0, h * 256:h * 256 + 128], ckvT.t[:], True, True), r=[ckvT.b, Wkvb], w=[pk.b])
                kts = KTS[kti % 2]
                kti += 1
                e = "act" if h % 2 else "dve"
                sc.add(e, scale_cast(e, kts.t[:], pk.t[:]), r=[pk.b], w=[kts.b])
                sc.add("pool", _dma(KT_s[h * 128:(h + 1) * 128, t0:t0 + 512], kts.t[:]), r=[kts.b], chan=kts.c)
            cntC["kti"] = kti

        def c_y(i):
            t0 = i * 512
            kr2, cqT, ckvT = KR2[i % 2], CQT[i % 2], CKVT[i % 2]
            cs, sn = cos2[:, 4 * i:4 * i + 4, :], sin1[:, 4 * i:4 * i + 4, :]
            for b in range(4):
                tok = slice(b * 128, (b + 1) * 128)
                pv = bank()
                sc.add("pe", _mm(pv.t[:].rearrange("p (h d) -> p h d", h=4), ckvT.t[:, tok], Wkv4[:, :, 1, :], True, True),
                       r=[ckvT.b, Wkvb], w=[pv.b])
                vs = VS[b % 2]
                sc.add("act", _act(vs.t[:], pv.t[:], AF.Copy), r=[pv.b], w=[vs.b])
                sc.add("pool", _dma(V_s[t0 + b * 128:t0 + (b + 1) * 128, :], vs.t[:]), r=[vs.b], chan=vs.c)
                pk = bank()
                sc.add("pe", _mm(pk.t[:].rearrange("p (h d) -> p h d", h=4), ckvT.t[:, tok], Wkv4[:, :, 0, :], True, True),
                       r=[ckvT.b, Wkvb], w=[pk.b])
                sc.add("act", _act(sqk.t[:], pk.t[:], AF.Square), r=[pk.b], w=[sqk.b])
                sc.add("dve", lambda e, b=b: e.tensor_reduce(kn2.t[:, b, :], sqk.t[:].rearrange("p (h d) -> p h d", h=4), AX.X, ALU.add),
                       r=[sqk.b], w=[kn2.b])
                pq0, pq1 = bank(), bank()
                for kc in range(2):
                    sc.add("pe", _mm(pq0.t[:], cqT.t[:, kc, tok], Wuq[:, kc, 0:512], kc == 0, kc == 1), r=[cqT.b, Wuqb], w=[pq0.b])
                for kc in range(2):
                    sc.add("pe", _mm(pq1.t[:, 0:256], cqT.t[:, kc, tok], Wuq[:, kc, 512:768], kc == 0, kc == 1), r=[cqT.b, Wuqb], w=[pq1.b])
                sc.add("act", _act(q_sb.t[:, b, 0:512], pq0.t[:], AF.Copy), r=[pq0.b], w=[q_sb.b])
                sc.add("dve", _cp(q_sb.t[:, b, 512:768], pq1.t[:, 0:256]), r=[pq1.b], w=[q_sb.b])
            sc.add("dve", _tt(kn2.t[:], kn2.t[:], bc(kr2.t[:], 4), ALU.add), r=[kn2.b, kr2.b], w=[kn2.b])
            sc.add("dve", lambda e: e.tensor_reduce(kmt.t[:], kn2.t[:].rearrange("p b h -> p h b"), AX.X, ALU.max), r=[kn2.b], w=[kmt.b])
            sc.add("dve", _tt(kmax.t[:], kmax.t[:], kmt.t[:], ALU.max), r=[kmax.b, kmt.b], w=[kmax.b])
            cs4 = bass.AP(cs.tensor, cs.offset, [list(cs.ap[0]), list(cs.ap[1]), [0, 4], list(cs.ap[2])])
            sn4 = bass.AP(sn.tensor, sn.offset, [list(sn.ap[0]), list(sn.ap[1]), [0, 4], list(sn.ap[2])])
            sc.add("dve", _tt(qtA.t[:], q4[:, :, :, 128:192], cs4, ALU.mult), r=[q_sb.b], w=[qtA.b])
            sc.add("dve", _tt(qtB.t[:, :, :, 0:32], q4[:, :, :, 160:192], sn4, ALU.mult), r=[q_sb.b], w=[qtB.b])
            sc.add("dve", _tt(qtB.t[:, :, :, 32:64], q4[:, :, :, 128:160], sn4, ALU.mult), r=[q_sb.b], w=[qtB.b])
            sc.add("dve", _tt(qbr.t[:, :, :, 0:32], qtA.t[:, :, :, 0:32], qtB.t[:, :, :, 0:32], ALU.subtract), r=[qtA.b, qtB.b], w=[qbr.b])
            sc.add("dve", _tt(qbr.t[:, :, :, 32:64], qtA.t[:, :, :, 32:64], qtB.t[:, :, :, 32:64], ALU.add), r=[qtA.b, qtB.b], w=[qbr.b])
            sc.add("dve", _cp(qbn.t[:], q4[:, :, :, 0:128]), r=[q_sb.b], w=[qbn.b])
            sc.add("act", _act(junkc.t[:], q_sb.t[:].rearrange("p b c -> p (b c)"), AF.Square), r=[q_sb.b], w=[junkc.b])
            sc.add("dve", lambda e: e.tensor_reduce(qn2.t[:], junkc.t[:].rearrange("p (g d) -> p g d", g=16), AX.X, ALU.add),
                   r=[junkc.b], w=[qn2.b])
            sc.add("dve", lambda e: e.tensor_reduce(qmt.t[:], qn2.t[:].rearrange("p (b h) -> p h b", b=4), AX.X, ALU.max), r=[qn2.b], w=[qmt.b])
            sc.add("dve", _tt(qmx.t[:], qmx.t[:], qmt.t[:], ALU.max), r=[qmx.b, qmt.b], w=[qmx.b])
            sc.add("pool", _tt(qn1.t[:], qn2.t[:], mhalf[:, 0:16], ALU.pow), r=[qn2.b], w=[qn1.b])
            sc.add("dve", _tt(qn1.t[:], qn1.t[:], qn2.t[:], ALU.mult), r=[qn1.b, qn2.b], w=[qn1.b])
            sc.add("dve", _ts(qbr.t[:, :, :, 64], qn1.t[:].rearrange("p (b h) -> p b h", b=4), -1.01, None, ALU.mult),
                   r=[qn1.b], w=[qbr.b])
            for hp in range(2):
                pn, pr = bank(), bank()
                for hh in range(2):
                    h = 2 * hp + hh
                    for b in range(4):
                        sc.add("pe", _tr(bfv(pn)[:, hh * 512 + b * 128:hh * 512 + (b + 1) * 128], qbn.t[:, b, h, :], ident[:]), r=[qbn.b], w=[pn.b])
                        sc.add("pe", _tr(bfv(pr)[0:65, hh * 512 + b * 128:hh * 512 + (b + 1) * 128], qbr.t[:, b, h, 0:65], ident[:]), r=[qbr.b], w=[pr.b])
                qs, qrs = QS[hp], QRS[hp]
                sc.add("act", _act(qs.t[:], bfv(pn).rearrange("p (a b) -> p a b", a=2), AF.Copy), r=[pn.b], w=[qs.b])
                sc.add("dve", _cp(qrs.t[:], bfv(pr)[0:65, :].rearrange("p (a b) -> p a b", a=2)), r=[pr.b], w=[qrs.b])
                sc.add("pool", _dma(QN_s.rearrange("(h p) s -> p h s", p=128)[:, 2 * hp:2 * hp + 2, t0:t0 + 512], qs.t[:]), r=[qs.b], chan=qs.c)
                sc.add("pool", _dma(QR_s.rearrange("h p s -> p h s")[:, 2 * hp:2 * hp + 2, t0:t0 + 512], qrs.t[:]), r=[qrs.b], chan=qrs.c)
        c_x(0)
        for i in range(NT):
            if i + 1 < NT:
                c_x(i + 1)
            c_y(i)
        kmo = Slot(k.sb(st, "kmo", [128, 8]), sc.chan("c_kmo"))
        sc.add("dve", _cp(kmo.t[:, 0:4], kmax.t[:]), r=[kmax.b], w=[kmo.b])
        sc.add("dve", _cp(kmo.t[:, 4:8], qmx.t[:]), r=[qmx.b], w=[kmo.b])
        sc.add("pool", _dma(kmax_s[:, :], kmo.t[:]), r=[kmo.b], chan=kmo.c)
        sc.emit()

    with contextlib.ExitStack() as st:
        KT = Slot(k.sb(st, "KT", [128, 4, S], BF16), sc.chan("c_KT"))
        KR = Slot(k.sb(st, "KR", [128, S], BF16), sc.chan("c_KR"))
        VR = Slot(k.sb(st, "VR", [128, NB, 512], BF16), sc.chan("c_VR"))
        kml = Slot(k.sb(st, "kml", [128, 8]), sc.chan("c_kml"))
        km1 = Slot(k.sb(st, "km1", [1, 8]))
        kmx = Slot(k.sb(st, "kmx", [128, 8]))
        cbias_ = Slot(k.sb(st, "cbias_", [128, 4]))
        phalf = Slot(k.sb(st, "phalf", [128, 4]))
        SPB = [Slot(k.ps(st, f"spb{i}", [128, 512])) for i in range(4)]
        OPB = [Slot(k.ps(st, f"opb{i}", [128, 512])) for i in range(2)]
        RSB = Slot(k.ps(st, "rsb", [128, 512]))
        RS = [Slot(k.ps(st, f"rs{i}", [128, 512])) for i in range(1)]
        NPT = 10
        PT = [Slot(k.sb(st, f"pt{i}", [128, 512], BF16)) for i in range(NPT)]
        rs_sb = Slot(k.sb(st, "rs_sb", [128, 512]))
        ones_b = Slot(k.sb(st, "ones_b", [128, 32], BF16))
        inv32 = Slot(k.sb(st, "inv32", [128, 128]))
        sc.add("pool", _memset(ones_b.t[:], 1.0), w=[ones_b.b])
        sc.add("pool", _memset(inv32.t[:], 1.0 / 32.0), w=[inv32.b])
        QN = [Slot(k.sb(st, f"qnt{i}", [128, 512], BF16), sc.chan(f"c_qn{i}")) for i in range(2)]
        QR = [Slot(k.sb(st, f"qrt{i}", [128, 512], BF16), sc.chan(f"c_qr{i}")) for i in range(2)]
        rinv = Slot(k.sb(st, "rinv", [128, 512]))
        YO = [Slot(k.sb(st, f"yo{i}", [128, 512], BF16), sc.chan(f"c_yo{i}")) for i in range(2)]
        sc.add("sp", _dma(KT.t[:], KT_s.rearrange("(h p) s -> p h s", p=128)), w=[KT.b], chan=KT.c)
        sc.add("sp", _dma(KR.t[0:64, :], KR_s[:, :]), w=[KR.b], chan=KR.c)
        sc.add("sp", _dma(KR.t[64:128, :], KR_s[:, :]), w=[KR.b], chan=sc.chan("c_KR2"))
        sc.add("sp", _dma(VR.t[:], V_s.rearrange("(c p) d -> p c d", p=128)), w=[VR.b], chan=VR.c)
        sc.add("sp", _dma(kml.t[:], kmax_s[:, :]), w=[kml.b], chan=kml.c)
        sc.add("pool", _memset(phalf.t[:], 0.5), w=[phalf.b])
        scale = 192.0 ** -0.5
        sc.add("pool", lambda e: e.tensor_reduce(km1.t[:], kml.t[:], AX.C, ALU.max), r=[kml.b], w=[km1.b])
        sc.add("pe", _mm(RSB.t[:, 0:8], ones_f[0:1, :], km1.t[:], True, True), r=[km1.b], w=[RSB.b])
        sc.add("dve", _cp(kmx.t[:], RSB.t[:, 0:8]), r=[RSB.b], w=[kmx.b])
        sc.add("dve", _tt(cbias_.t[:], kmx.t[:, 0:4], kmx.t[:, 4:8], ALU.mult), r=[kmx.b], w=[cbias_.b])
        sc.add("pool", _tt(cbias_.t[:], cbias_.t[:], phalf.t[:], ALU.pow), r=[cbias_.b, phalf.b], w=[cbias_.b])
        sc.add("dve", _ts(cbias_.t[:], cbias_.t[:], -1.01 * scale, None, ALU.mult), r=[cbias_.b], w=[cbias_.b])
        if "cb_dbg" in k.dbg:
            cb_dbg = k.dram_tmp("cb_dbg", [128, 12])
            dbt = Slot(k.sb(st, "dbt", [128, 12]), sc.chan("c_dbt"))
            sc.add("dve", _cp(dbt.t[:, 0:4], cbias_.t[:]), r=[cbias_.b], w=[dbt.b])
            sc.add("dve", _cp(dbt.t[:, 4:12], kmx.t[:]), r=[kmx.b], w=[dbt.b])
            sc.add("pool", _dma(cb_dbg[:, :], dbt.t[:]), r=[dbt.b], chan=dbt.c)
        it = 0
        pti = 0
        for h in range(4):
            for j in range(NT):
                qn, qr = QN[it % 2], QR[it % 2]
                opb, yo, rs = OPB[it % 2], YO[it % 2], RS[0]
                it += 1
                sc.add("sp", _dma(qn.t[:], QN_s[h * 128:(h + 1) * 128, j * 512:(j + 1) * 512]), w=[qn.b], chan=qn.c)
                sc.add("sp", _dma(qr.t[0:64, :], QR_s[h, 0:64, j * 512:(j + 1) * 512]), w=[qr.b], chan=qr.c)
                sc.add("sp", _dma(qr.t[64:128, :], QR_s[h, 0:64, j * 512:(j + 1) * 512]), w=[qr.b], chan=qr.c)

                def qk2(kc):
                    for r_ in range(2):
                        sp_ = SPB[(kc + r_) % 4]
                        ks = slice((kc + r_) * 128, (kc + r_ + 1) * 128)
                        sc.add("pe", _mm(sp_.t[:], KT.t[:, h, ks], qn.t[:], True, False), r=[KT.b, qn.b], w=[sp_.b])
                    for r_ in range(2):
                        sp_ = SPB[(kc + r_) % 4]
                        ks = slice((kc + r_) * 128, (kc + r_ + 1) * 128)
                        rows = slice(64 * r_, 64 * r_ + 64)
                        sc.add("pe", lambda e, sp_=sp_, ks=ks, rows=rows, r_=r_, qr=qr: e.matmul(sp_.t[:], KR.t[rows, ks], qr.t[rows, :], start=False, stop=True,
                                                                                             tile_position=(64 * r_, 0)),
                               r=[KR.b, qr.b], w=[sp_.b])

                qk2(0)
                grp = []
                for kc in range(NB):
                    if kc % 2 == 0 and kc + 2 < NB:
                        qk2(kc + 2)
                    sp_ = SPB[kc % 4]
                    pt = PT[pti % NPT]
                    pti += 1
                    sc.add("act", _act(pt.t[:], sp_.t[:], AF.Exp, scale=scale, bias=cbias_.t[:, h:h + 1]), r=[sp_.b, cbias_.b], w=[pt.b])
                    sc.add("pe", _mm(opb.t[:], VR.t[:, kc, h * 128:(h + 1) * 128], pt.t[:], kc == 0, kc == NB - 1),
                           r=[VR.b, pt.b], w=[opb.b])
                    grp.append(pt)
                    if len(grp) == 4:
                        for r_, ptr in enumerate(grp):
                            sc.add("pe", lambda e, r_=r_, ptr=ptr, kc=kc: e.matmul(rs.t[32 * r_:32 * r_ + 32, :], ones_b.t[:, 0:32], ptr.t[:],
                                                                               start=(kc == 3), stop=(kc == NB - 1),
                                                                               tile_position=(0, 32 * r_)),
                                   r=[ptr.b, ones_b.b], w=[rs.b])
                        grp = []
                sc.add("dve", _cp(rs_sb.t[:], rs.t[:]), r=[rs.b], w=[rs_sb.b])
                sc.add("pe", _mm(RSB.t[:], inv32.t[:], rs_sb.t[:], True, True), r=[rs_sb.b, inv32.b], w=[RSB.b])
                sc.add("dve", lambda e: e.reciprocal(rinv.t[:], RSB.t[:]), r=[RSB.b], w=[rinv.b])
                sc.add("dve", _tt(yo.t[:], opb.t[:], rinv.t[:], ALU.mult), r=[opb.b, rinv.b], w=[yo.b])
                sc.add("pool", _dma(yT_s[512 + h * 128:512 + (h + 1) * 128, j * 512:(j + 1) * 512], yo.t[:]), r=[yo.b], chan=yo.c)
        sc.emit()

    h1_s = k.dram_tmp("h1_s", [S, D])
    xn2T_s = k.dram_tmp("xn2T_s", [D, S + 2], BF16)
    h2_s = k.dram_tmp("h2_s", [S, D])
    yT3 = yT_s.rearrange("(g p) s -> p g s", p=128)
    xn2T3 = xn2T_s.rearrange("(g p) s -> p g s", p=128)

    def rms_transpose(xt, ss, ss2, rstd, junk, XB, xn, TBs, gain_scale=1.0 / 32.0):
        for b in range(4):
            sc.add("act", _act(junk.t[:], xt.t[:, b, :], AF.Square, scale=gain_scale, accum_out=ss.t[:, b:b + 1]),
                   r=[xt.b], w=[junk.b, ss.b])
        sc.add("dve", _ts(ss2.t[:], ss.t[:], EPS, None, ALU.add), r=[ss.b], w=[ss2.b])
        sc.add("pool", _tt(rstd.t[:], ss2.t[:], mhalf[:, 0:4], ALU.pow), r=[ss2.b], w=[rstd.b])
        for b in range(4):
            e = "dve" if b % 2 else "act"
            sc.add(e, scale_cast(e, XB.t[:, b, :], xt.t[:, b, :], rstd.t[:, b:b + 1]), r=[xt.b, rstd.b], w=[XB.b])
        for j in range(4):
            tb = TBs[j % 2]
            for kk in range(2):
                kc = 2 * j + kk
                for b in range(4):
                    sc.add("pe", _tr(tb.t[:, kk * 512 + b * 128:kk * 512 + (b + 1) * 128],
                                     XB.t[:, b, kc * 128:(kc + 1) * 128], ident[:]), r=[XB.b], w=[tb.b])
            e = "dve" if j % 2 else "act"
            sc.add(e, scale_cast(e, xn.t[:, 2 * j:2 * j + 2, :], tb.t[:].rearrange("p (a b) -> p a b", a=2)),
                   r=[tb.b], w=[xn.b])

    with contextlib.ExitStack() as st:
        stage = [Slot(k.sb(st, f"wstaged{i}", [128, 1024]), ld_ch[i]) for i in range(2)]
        Wout, Woutb = load_w(st, stage, "Wout", w_out, D, D)
        normg = load_bcast(st, "normg", mlstm_norm_g, 512)
        zt = Slot(k.sb(st, "zt", [128, 8, 2], BF16), sc.chan("c_zt"))
        sc.add("pool", _memset(zt.t[:], 0.0), w=[zt.b])
        sc.add("pool", _dma(xn2T3[:, :, 0:1], zt.t[:, :, 0:1], allow_slow_non_contiguous=True), r=[zt.b], chan=zt.c)
        sc.add("pool", _dma(xn2T3[:, :, S + 1:S + 2], zt.t[:, :, 1:2], allow_slow_non_contiguous=True), r=[zt.b], chan=zt.c)
        HFT = [Slot(k.sb(st, f"hft{i}", [128, 4, 512], BF16), sc.chan(f"c_hft{i}")) for i in range(2)]
        HBT = [Slot(k.sb(st, f"hbt{i}", [128, 4, 512], BF16), sc.chan(f"c_hbt{i}")) for i in range(2)]
        SOT = [Slot(k.sb(st, f"sot{i}", [128, 4, 512], BF16), sc.chan(f"c_sot{i}")) for i in range(2)]
        HS = Slot(k.sb(st, "hsd", [128, 4, 512]))
        SG = Slot(k.sb(st, "sgd", [128, 4, 512]))
        sqd = Slot(k.sb(st, "sqd", [128, 4, 512], BF16))
        ssn = Slot(k.sb(st, "ssnd", [128, 16]))
        rsn = Slot(k.sb(st, "rsnd", [128, 16]))
        YB = [Slot(k.sb(st, f"ybd{i}", [128, 4, 512], BF16)) for i in range(2)]
        YAT = [Slot(k.sb(st, f"yat{i}", [128, 4, 512], BF16)) for i in range(2)]
        YTT = [Slot(k.sb(st, f"ytt{i}", [128, 4, 512], BF16), sc.chan(f"c_ytt{i}")) for i in range(3)]
        XT = [Slot(k.sb(st, f"xtd{i}", [128, 4, D]), sc.chan(f"c_xtd{i}")) for i in range(3)]
        XB = Slot(k.sb(st, "xbd", [128, 4, D], BF16))
        XN = [Slot(k.sb(st, f"xnd{i}", [128, 8, 512], BF16), sc.chan(f"c_xnd{i}")) for i in range(2)]
        junk = Slot(k.sb(st, "junkd", [128, D], BF16))
        ss = Slot(k.sb(st, "ssd", [128, 4]))
        ss2 = Slot(k.sb(st, "ss2d", [128, 4]))
        rstd = Slot(k.sb(st, "rstdd", [128, 4]))
        TBs = [Slot(k.ps(st, f"tbd{i}", [128, 1024], BF16)) for i in range(2)]
        MB = [Slot(k.ps(st, f"mbd{i}", [128, 512])) for i in range(6)]
        cntD = {"mbi": 0}

        def loads(i):
            t0 = i * 512
            hft, hbt, sot, ytt = HFT[i % 2], HBT[i % 2], SOT[i % 2], YTT[i % 3]
            tv = lambda ap: ap[t0:t0 + 512, :].rearrange("(b p) d -> p b d", p=128)
            sc.add("sp", _dma(hft.t[:], tv(hf_s)), w=[hft.b], chan=hft.c)
            sc.add("sp", _dma(hbt.t[:], tv(hb_s)), w=[hbt.b], chan=hbt.c)
            sc.add("sp", _dma(sot.t[:], tv(so_s)), w=[sot.b], chan=sot.c)
            sc.add("sp", _dma(ytt.t[:], yT3[:, 4:8, t0:t0 + 512]), w=[ytt.b], chan=ytt.c)

        def loadx(i):
            t0 = i * 512
            xt = XT[i % 3]
            sc.add("sp", _dma(xt.t[:], x[t0:t0 + 512, :].rearrange("(b p) d -> p b d", p=128)), w=[xt.b], chan=xt.c)

        def comb1(i):
            hft, hbt, sot = HFT[i % 2], HBT[i % 2], SOT[i % 2]
            sc.add("dve", _tt(HS.t[:], hft.t[:], hbt.t[:], ALU.add), r=[hft.b, hbt.b], w=[HS.b])
            sc.add("act", _act(sqd.t[:], HS.t[:], AF.Square, scale=128.0 ** -0.5), r=[HS.b], w=[sqd.b])
            sc.add("pool", _tt(SG.t[:], sot.t[:], bc_mid(normg[0][:], 4), ALU.mult), r=[sot.b, normg[1]], w=[SG.b])
            sc.add("dve", lambda e: e.tensor_reduce(ssn.t[:], sqd.t[:].rearrange("p b (h d) -> p (b h) d", h=4), AX.X, ALU.add),
                   r=[sqd.b], w=[ssn.b])
            sc.add("dve", _ts(ssn.t[:], ssn.t[:], EPS, None, ALU.add), r=[ssn.b], w=[ssn.b])
            sc.add("pool", _tt(rsn.t[:], ssn.t[:], mhalf[:, 0:16], ALU.pow), r=[ssn.b], w=[rsn.b])

        def comb2(i):
            yb = YB[i % 2]
            h16 = HS.t[:].rearrange("p b (h d) -> p (b h) d", h=4)
            sc.add("dve", _tt(h16, h16, bc(rsn.t[:], 128), ALU.mult), r=[HS.b, rsn.b], w=[HS.b])
            sc.add("dve", _tt(yb.t[:], HS.t[:], SG.t[:], ALU.mult), r=[HS.b, SG.b], w=[yb.b])

        def norm1(i):
            t0 = i * 512
            xt = XT[i % 3]
            sc.add("sp", _dma(h1_s[t0:t0 + 512, :].rearrange("(b p) d -> p b d", p=128), xt.t[:]), r=[xt.b], chan=xt.c)
            for b in range(4):
                sc.add("act", _act(junk.t[:], xt.t[:, b, :], AF.Square, scale=1.0 / 32.0, accum_out=ss.t[:, b:b + 1]),
                       r=[xt.b], w=[junk.b, ss.b])
            sc.add("dve", _ts(ss2.t[:], ss.t[:], EPS, None, ALU.add), r=[ss.b], w=[ss2.b])
            sc.add("pool", _tt(rstd.t[:], ss2.t[:], mhalf[:, 0:4], ALU.pow), r=[ss2.b], w=[rstd.b])

        def norm2(i):
            xt = XT[i % 3]
            for b in range(4):
                e = "dve" if b % 2 else "act"
                sc.add(e, scale_cast(e, XB.t[:, b, :], xt.t[:, b, :], rstd.t[:, b:b + 1]), r=[xt.b, rstd.b], w=[XB.b])

        def mmstage(i):
            yb, yat, ytt, xt = YB[i % 2], YAT[i % 2], YTT[i % 3], XT[i % 3]
            for hp in range(2):
                tb = TBs[hp]
                for hh in range(2):
                    h = 2 * hp + hh
                    for b in range(4):
                        sc.add("pe", _tr(tb.t[:, hh * 512 + b * 128:hh * 512 + (b + 1) * 128], yb.t[:, b, h * 128:(h + 1) * 128], ident[:]),
                               r=[yb.b], w=[tb.b])
                e = "dve" if hp else "act"
                sc.add(e, scale_cast(e, yat.t[:, 2 * hp:2 * hp + 2, :], tb.t[:].rearrange("p (a b) -> p a b", a=2)), r=[tb.b], w=[yat.b])
            mbi = cntD["mbi"]
            for b in range(4):
                for half in range(2):
                    pm = MB[mbi % 6]
                    mbi += 1
                    for kc in range(8):
                        src = yat if kc < 4 else ytt
                        sc.add("pe", _mm(pm.t[:], src.t[:, kc % 4, b * 128:(b + 1) * 128], Wout[:, kc, half * 512:(half + 1) * 512],
                                         kc == 0, kc == 7), r=[src.b, Woutb], w=[pm.b])
                    sc.add("dve", _tt(xt.t[:, b, half * 512:(half + 1) * 512], pm.t[:], xt.t[:, b, half * 512:(half + 1) * 512], ALU.add),
                           r=[pm.b, xt.b], w=[xt.b])
            cntD["mbi"] = mbi

        def trstage(i):
            t0 = i * 512
            xn = XN[i % 2]
            for j in range(4):
                tb = TBs[j % 2]
                for kk in range(2):
                    kc = 2 * j + kk
                    for b in range(4):
                        sc.add("pe", _tr(tb.t[:, kk * 512 + b * 128:kk * 512 + (b + 1) * 128],
                                         XB.t[:, b, kc * 128:(kc + 1) * 128], ident[:]), r=[XB.b], w=[tb.b])
                e = "dve" if j % 2 else "act"
                sc.add(e, scale_cast(e, xn.t[:, 2 * j:2 * j + 2, :], tb.t[:].rearrange("p (a b) -> p a b", a=2)),
                       r=[tb.b], w=[xn.b])
            sc.add("sp", _dma(xn2T3[:, :, 1 + t0:1 + t0 + 512], xn.t[:]), r=[xn.b], chan=xn.c)

        loads(0)
        loadx(0)
        if NT > 1:
            loads(1)
            loadx(1)
        comb1(0)
        comb2(0)
        for i in range(NT + 1):
            if i + 2 < NT:
                loads(i + 2)
            if i >= 1:
                norm1(i - 1)
            if i + 1 < NT:
                comb1(i + 1)
            if i < NT:
                mmstage(i)
            if i >= 1:
                norm2(i - 1)
                trstage(i - 1)
            if i + 1 < NT:
                comb2(i + 1)
            if i + 2 < NT:
                loadx(i + 2)
        sc.emit()

    TT_ = 256
    with contextlib.ExitStack() as st:
        stage = [Slot(k.sb(st, f"wstagee{i}", [128, 1408]), ld_ch[i]) for i in range(2)]
        gffn = load_cols(st, "gffn", ln_ffn_g, 8)
        Wup, Wupb = load_w(st, stage, "Wup", w_up, D, 2 * D_FF, gffn)
        Wdn, Wdnb = load_w(st, stage, "Wdn", w_down, D_FF, D)
        fw = k.sb(st, "fw", [128, 44, 3])
        fwb = Buf("fw")
        for tap in range(3):
            sc.add("sp", _dma(fw[:, :, tap], conv_ffn_w[tap].rearrange("(g p) -> p g", p=128),
                              allow_slow_non_contiguous=True), w=[Buf()], chan=sc.chan(f"c_fw{tap}"))
        fb = load_cols(st, "fb", conv_ffn_b, 44)
        sc.barrier()
        XS = [Slot(k.sb(st, f"xs{i}", [128, 8, TT_ + 2], BF16), sc.chan(f"c_xs{i}")) for i in range(2)]
        H1 = [Slot(k.sb(st, f"h1t{i}", [128, 2, D]), sc.chan(f"c_h1t{i}")) for i in range(2)]
        AT = [Slot(k.sb(st, f"at{i}", [128, 22, TT_], BF16)) for i in range(2)]
        CG = [Slot(k.sb(st, f"cg{i}", [128, TT_])) for i in range(2)]
        CV = [Slot(k.sb(st, f"cv{i}", [128, TT_])) for i in range(2)]
        SG = [Slot(k.sb(st, f"sg{i}", [128, TT_])) for i in range(2)]
        MB = [Slot(k.ps(st, f"mbe{i}", [128, 512])) for i in range(8)]
        cntE = {"mbi": 0}

        def down_group(pend, dj):
            pi, pt0, ph1, pat = pend
            b, half = divmod(dj, 2)
            mbi = cntE["mbi"]
            pm = MB[mbi % 8]
            cntE["mbi"] = mbi + 1
            for g in range(22):
                sc.add("pe", _mm(pm.t[:], pat.t[:, g, b * 128:(b + 1) * 128], Wdn[:, g, half * 512:(half + 1) * 512], g == 0, g == 21),
                       r=[pat.b, Wdnb], w=[pm.b])
            sc.add("dve", _tt(ph1.t[:, b, half * 512:(half + 1) * 512], pm.t[:], ph1.t[:, b, half * 512:(half + 1) * 512], ALU.add),
                   r=[pm.b, ph1.b], w=[ph1.b])

        def down_store(pend):
            pi, pt0, ph1, pat = pend
            sc.add("pool", _dma(h2_s[pt0:pt0 + TT_, :].rearrange("(b p) d -> p b d", p=128), ph1.t[:]), r=[ph1.b], chan=ph1.c)

        pending = None
        for i in range(S // TT_):
            t0 = i * TT_
            xs, h1, at = XS[i % 2], H1[i % 2], AT[i % 2]
            mbi = cntE["mbi"]
            sc.add("sp", _dma(xs.t[:], xn2T3[:, :, t0:t0 + TT_ + 2]), w=[xs.b], chan=xs.c)
            sc.add("sp", _dma(h1.t[:], h1_s[t0:t0 + TT_, :].rearrange("(b p) d -> p b d", p=128)), w=[h1.b], chan=h1.c)
            for g in range(22):
                res = []
                for which, (gi, dst) in enumerate(((g, CG[g % 2]), (22 + g, CV[g % 2]))):
                    pm = MB[mbi % 8]
                    mbi += 1
                    for kc in range(8):
                        sc.add("pe", _mm(pm.t[:, 0:TT_ + 2], Wup[:, kc, gi * 128:(gi + 1) * 128], xs.t[:, kc, :], kc == 0, kc == 7),
                               r=[xs.b, Wupb], w=[pm.b])
                    sc.add("act", _act(dst.t[:], pm.t[:, 0:TT_], AF.Identity, scale=fw[:, gi, 0:1], bias=fb[0][:, gi:gi + 1]),
                           r=[pm.b], w=[dst.b])
                    sc.add("dve", _stt(dst.t[:], pm.t[:, 1:TT_ + 1], fw[:, gi, 1:2], dst.t[:], ALU.mult, ALU.add), r=[pm.b, dst.b], w=[dst.b])
                    sc.add("dve", _stt(dst.t[:], pm.t[:, 2:TT_ + 2], fw[:, gi, 2:3], dst.t[:], ALU.mult, ALU.add), r=[pm.b, dst.b], w=[dst.b])
                cg, cv, sg = CG[g % 2], CV[g % 2], SG[g % 2]
                sc.add("act", _act(sg.t[:], cg.t[:], AF.Silu), r=[cg.b], w=[sg.b])
                sc.add("pool", _tt(at.t[:, g, :], sg.t[:], cv.t[:], ALU.mult), r=[sg.b, cv.b], w=[at.b])
                if pending is not None and g in (4, 9, 14, 19):
                    cntE["mbi"] = mbi
                    down_group(pending, (g - 4) // 5)
                    mbi = cntE["mbi"]
                    if g == 19:
                        down_store(pending)
            cntE["mbi"] = mbi
            pending = (i, t0, h1, at)
        for dj in range(4):
            down_group(pending, dj)
        down_store(pending)
        sc.emit()

    with contextlib.ExitStack() as st:
        stage = [Slot(k.sb(st, f"wstagef{i}", [128, 1024]), ld_ch[i]) for i in range(2)]
        gple = load_cols(st, "gple", ple_norm_g, 8)
        Wg, Wgb = load_w(st, stage, "Wg", w_ple_gate, D, D, gple)
        Wp, Wpb = load_w(st, stage, "Wp", w_ple_proj, 256, D)
        postg = load_bcast(st, "postg", ple_post_g, D)
        fing = load_bcast(st, "fing", final_g, D)
        sc.barrier()
        XT = [Slot(k.sb(st, f"xtf{i}", [128, 4, D]), sc.chan(f"c_xtf{i}")) for i in range(3)]
        PTL = [Slot(k.sb(st, f"ptl{i}", [128, 4, 256]), sc.chan(f"c_ptl{i}")) for i in range(2)]
        XBF = [Slot(k.sb(st, f"xbf{i}", [128, 4, D], BF16)) for i in range(2)]
        PBF = [Slot(k.sb(st, f"pbf{i}", [128, 4, 256], BF16)) for i in range(2)]
        XNF = [Slot(k.sb(st, f"xnf{i}", [128, 8, 512], BF16)) for i in range(2)]
        PTTF = [Slot(k.sb(st, f"ptt{i}", [128, 2, 512], BF16)) for i in range(2)]
        junk = Slot(k.sb(st, "junkf", [128, D], BF16))
        ss = Slot(k.sb(st, "ssf", [128, 4]))
        ss2 = Slot(k.sb(st, "ss2f", [128, 4]))
        rstd = Slot(k.sb(st, "rstdf", [128, 4]))
        ssb = Slot(k.sb(st, "ssb", [128, 2]))
        ssb2 = Slot(k.sb(st, "ssb2", [128, 2]))
        rsb2 = Slot(k.sb(st, "rsb2", [128, 2]))
        SGM = [Slot(k.sb(st, f"sgm{i}", [128, D])) for i in range(2)]
        PJ = [Slot(k.sb(st, f"pj{i}", [128, D])) for i in range(2)]
        OT = [Slot(k.sb(st, f"ot{i}", [128, D]), sc.chan(f"c_ot{i}")) for i in range(3)]
        TBs = [Slot(k.ps(st, f"tbf{i}", [128, 1024], BF16)) for i in range(2)]
        MB = [Slot(k.ps(st, f"mbf{i}", [128, 512])) for i in range(6)]
        cntF = {"mbi": 0, "bi": 0}

        def f_stage1a(i):
            t0 = i * 512
            xt, ptl, XB, PBf = XT[i % 3], PTL[i % 2], XBF[i % 2], PBF[i % 2]
            sc.add("sp", _dma(xt.t[:], h2_s[t0:t0 + 512, :].rearrange("(b p) d -> p b d", p=128)), w=[xt.b], chan=xt.c)
            sc.add("sp", _dma(ptl.t[:], p_in[t0:t0 + 512, :].rearrange("(b p) d -> p b d", p=128)), w=[ptl.b], chan=ptl.c)
            for b in range(4):
                sc.add("act", _act(junk.t[:], xt.t[:, b, :], AF.Square, scale=1.0 / 32.0, accum_out=ss.t[:, b:b + 1]),
                       r=[xt.b], w=[junk.b, ss.b])
            sc.add("dve", _ts(ss2.t[:], ss.t[:], EPS, None, ALU.add), r=[ss.b], w=[ss2.b])
            sc.add("pool", _tt(rstd.t[:], ss2.t[:], mhalf[:, 0:4], ALU.pow), r=[ss2.b], w=[rstd.b])
            for b in range(4):
                e = "dve" if b % 2 else "act"
                sc.add(e, scale_cast(e, XB.t[:, b, :], xt.t[:, b, :], rstd.t[:, b:b + 1]), r=[xt.b, rstd.b], w=[XB.b])
            sc.add("act", _act(PBf.t[:], ptl.t[:], AF.Copy), r=[ptl.b], w=[PBf.b])

        def f_stage1b(i):
            XN, PTT, XB, PBf = XNF[i % 2], PTTF[i % 2], XBF[i % 2], PBF[i % 2]
            for j in range(4):
                tb = TBs[j % 2]
                for kk in range(2):
                    kc = 2 * j + kk
                    for b in range(4):
                        sc.add("pe", _tr(tb.t[:, kk * 512 + b * 128:kk * 512 + (b + 1) * 128],
                                         XB.t[:, b, kc * 128:(kc + 1) * 128], ident[:]), r=[XB.b], w=[tb.b])
                e = "dve" if j % 2 else "act"
                sc.add(e, scale_cast(e, XN.t[:, 2 * j:2 * j + 2, :], tb.t[:].rearrange("p (a b) -> p a b", a=2)),
                       r=[tb.b], w=[XN.b])
            tb = TBs[0]
            for kc in range(2):
                for b in range(4):
                    sc.add("pe", _tr(tb.t[:, kc * 512 + b * 128:kc * 512 + (b + 1) * 128], PBf.t[:, b, kc * 128:(kc + 1) * 128], ident[:]),
                           r=[PBf.b], w=[tb.b])
            sc.add("act", _act(PTT.t[:], tb.t[:].rearrange("p (a b) -> p a b", a=2), AF.Copy), r=[tb.b], w=[PTT.b])

        RN = 4
        SGM3 = SGM + [Slot(k.sb(st, f"sgm{i}", [128, D])) for i in range(2, RN)]
        PJ3 = PJ + [Slot(k.sb(st, f"pj{i}", [128, D])) for i in range(2, RN)]
        SSB = [Slot(k.sb(st, f"ssbr{i}", [128, 2])) for i in range(RN)]
        RSB2 = [Slot(k.sb(st, f"rsbr{i}", [128, 2])) for i in range(RN)]
        junk2 = Slot(k.sb(st, "junkf2", [128, D], BF16))

        def blk(n):
            i, b = divmod(n, 4)
            return i, b, XT[i % 3], SGM3[n % RN], PJ3[n % RN], SSB[n % RN], RSB2[n % RN], OT[n % 3]

        def f_a12(n):
            i, b, xt, sgm, pj, ssb_, rsb_, ot = blk(n)
            XN, PTT = XNF[i % 2], PTTF[i % 2]
            mbi = cntF["mbi"]
            tok = slice(b * 128, (b + 1) * 128)
            for half in range(2):
                hs = slice(half * 512, (half + 1) * 512)
                pm = MB[mbi % 6]
                mbi += 1
                for kc in range(8):
                    sc.add("pe", _mm(pm.t[:], XN.t[:, kc, tok], Wg[:, kc, hs], kc == 0, kc == 7), r=[XN.b, Wgb], w=[pm.b])
                sc.add("act", _act(sgm.t[:, hs], pm.t[:], AF.Sigmoid), r=[pm.b], w=[sgm.b])
                pm = MB[mbi % 6]
                mbi += 1
                for kc in range(2):
                    sc.add("pe", _mm(pm.t[:], PTT.t[:, kc, tok], Wp[:, kc, hs], kc == 0, kc == 1), r=[PTT.b, Wpb], w=[pm.b])
                sc.add("act", _act(pj.t[:, hs], pm.t[:], AF.Copy), r=[pm.b], w=[pj.b])
            cntF["mbi"] = mbi
            sc.add("act", _act(junk.t[:], pj.t[:], AF.Square, scale=1.0 / 32.0, accum_out=ssb_.t[:, 0:1]), r=[pj.b], w=[junk.b, ssb_.b])

        def f_d12(n):
            i, b, xt, sgm, pj, ssb_, rsb_, ot = blk(n)
            sc.add("dve", _ts(ssb_.t[:, 0:1], ssb_.t[:, 0:1], EPS, None, ALU.add), r=[ssb_.b], w=[ssb_.b])
            sc.add("pool", _tt(rsb_.t[:, 0:1], ssb_.t[:, 0:1], mhalf[:, 0:1], ALU.pow), r=[ssb_.b], w=[rsb_.b])
            sc.add("dve", _stt(sgm.t[:], sgm.t[:], rsb_.t[:, 0:1], postg[0][:], ALU.mult, ALU.mult), r=[sgm.b, rsb_.b, postg[1]], w=[sgm.b])
            sc.add("dve", _tt(pj.t[:], pj.t[:], sgm.t[:], ALU.mult), r=[pj.b, sgm.b], w=[pj.b])
            sc.add("dve", _tt(pj.t[:], pj.t[:], xt.t[:, b, :], ALU.add), r=[pj.b, xt.b], w=[pj.b])

        def f_a3(n):
            i, b, xt, sgm, pj, ssb_, rsb_, ot = blk(n)
            sc.add("act", _act(junk2.t[:], pj.t[:], AF.Square, scale=1.0 / 32.0, accum_out=ssb_.t[:, 1:2]), r=[pj.b], w=[junk2.b, ssb_.b])

        def f_d3(n):
            i, b, xt, sgm, pj, ssb_, rsb_, ot = blk(n)
            t0 = i * 512
            sc.add("dve", _ts(ssb_.t[:, 1:2], ssb_.t[:, 1:2], EPS, None, ALU.add), r=[ssb_.b], w=[ssb_.b])
            sc.add("pool", _tt(rsb_.t[:, 1:2], ssb_.t[:, 1:2], mhalf[:, 0:1], ALU.pow), r=[ssb_.b], w=[rsb_.b])
            sc.add("dve", _stt(ot.t[:], pj.t[:], rsb_.t[:, 1:2], fing[0][:], ALU.mult, ALU.mult), r=[pj.b, rsb_.b, fing[1]], w=[ot.b])
            sc.add("sp", _dma(out[t0 + b * 128:t0 + (b + 1) * 128, :], ot.t[:]), r=[ot.b], chan=ot.c)

        f_stage1a(0)
        f_stage1b(0)
        if NT > 1:
            f_stage1a(1)
        NBLK = NT * 4
        for n in range(-2, NBLK + 1):
            if 0 <= n + 2 < NBLK:
                f_a12(n + 2)
                i2, b2 = divmod(n + 2, 4)
                if b2 == 1:
                    if i2 + 1 < NT:
                        f_stage1b(i2 + 1)
                    if i2 + 2 < NT:
                        f_stage1a(i2 + 2)
            if 0 <= n + 1 < NBLK:
                f_d12(n + 1)
            if 0 <= n < NBLK:
                f_a3(n)
            if 0 <= n - 1 < NBLK:
                f_d3(n - 1)
        sc.emit()

    k.final_wait = None
    return k


def finish(k):
    return k.nc


_W_NAMES = ["ln_mix_g", "w_in", "b_gates", "conv_qk_w", "conv_qk_b", "mlstm_norm_g", "q_norm_g", "w_uq", "kv_norm_g",
            "w_ukv", "w_out", "ln_ffn_g", "w_up", "conv_ffn_w", "conv_ffn_b", "w_down", "ple_norm_g", "w_ple_gate",
            "w_ple_proj", "ple_post_g"]


def kernel(**inputs):
    x = np.asarray(inputs["x"])
    p = np.asarray(inputs["p"])
    B, S, _ = x.shape
    nc = finish(build(S))
    shared = {n: np.ascontiguousarray(np.asarray(inputs[n])[0], dtype=np.float32) for n in _W_NAMES}
    shared["final_g"] = np.ascontiguousarray(np.asarray(inputs["final_g"]), dtype=np.float32)
    in_maps = []
    for b in range(B):
        m = dict(shared)
        m["x"] = np.ascontiguousarray(x[b], dtype=np.float32)
        m["p"] = np.ascontiguousarray(p[0, b], dtype=np.float32)
        in_maps.append(m)
    res = run_bass_kernel_spmd(nc, in_maps, core_ids=list(range(B)))
    return np.stack([np.asarray(r["out"]) for r in res.results], axis=0).astype(np.float32)
```

```python
import contextlib
import numpy as np
import concourse.bass as bass
import concourse.mybir as mybir
from concourse.bass_utils import run_bass_kernel_spmd

F32 = mybir.dt.float32
BF16 = mybir.dt.bfloat16
AF = mybir.ActivationFunctionType
ALU = mybir.AluOpType
AX = mybir.AxisListType

D = 1024
NH = 4
IN_COLS = 2512
D_FF = 2816
EPS = 1e-6
SEM_MAX = 30000
STRICT_SAME_ENGINE = True


class Buf:
    __slots__ = ("name", "lw", "rd")

    def __init__(self, name=""):
        self.name = name
        self.lw = None
        self.rd = []


class Chan:
    __slots__ = ("sem", "count", "last")

    def __init__(self, sem):
        self.sem = sem
        self.count = 0
        self.last = None


class Op:
    __slots__ = ("eng", "fn", "deps", "sig", "signo", "chan", "cval", "done")


class Sched:
    ENGS = ("pe", "act", "dve", "pool", "sp")

    def __init__(self, nc, stack):
        self.nc = nc
        self.stack = stack
        self.ops = []
        self.last_on = {e: None for e in self.ENGS}
        self.pending_bar = {e: [] for e in self.ENGS}
        self.chans = []
        self.free_chans = []
        self.phase_chans = []
        self.cnt = {e: 0 for e in self.ENGS}
        self.sems = {e: [] for e in self.ENGS}
        self.waited = {e: {} for e in self.ENGS}

    def chan(self, name, keep=False):
        if self.free_chans and not keep:
            c = self.free_chans.pop()
        else:
            c = Chan(self.stack.enter_context(self.nc.semaphore(name)))
            self.chans.append(c)
        if not keep:
            self.phase_chans.append(c)
        return c

    def add(self, eng, fn, r=(), w=(), chan=None):
        op = Op()
        op.eng, op.fn, op.deps, op.sig, op.signo, op.chan, op.cval = eng, fn, {}, False, 0, chan, 0
        op.done = False
        for b in r:
            if b.lw is not None:
                op.deps[b.lw] = True
        for b in w:
            if b.lw is not None:
                op.deps.setdefault(b.lw, False)
            for q in b.rd:
                op.deps.setdefault(q, False)
        for b in r:
            b.rd.append(op)
        for b in w:
            b.lw = op
            b.rd = []
        if self.pending_bar[eng]:
            for d in self.pending_bar[eng]:
                op.deps[d] = True
            self.pending_bar[eng] = []
        if chan is not None:
            if chan.last is not None:
                op.deps[chan.last] = True
            chan.count += 16
            op.cval = chan.count
            chan.last = op
        op.deps.pop(op, None)
        self.ops.append(op)
        self.last_on[eng] = op
        return op

    def barrier(self):
        lasts = [o for o in self.last_on.values() if o is not None]
        lasts += [c.last for c in self.chans if c.last is not None]
        for e in self.ENGS:
            self.pending_bar[e] = list(lasts)

    def emit(self):
        nc = self.nc
        fin = self.add("sp", lambda e: e.nop())
        for c in self.chans:
            if c.last is not None and not c.last.done:
                fin.deps[c.last] = True
        for e in self.ENGS:
            self.pending_bar[e] = []
        for op in self.ops:
            for d in [d for d in op.deps if d.done]:
                del op.deps[d]
            for d, raw in op.deps.items():
                if d.chan is not None:
                    continue
                if d.eng == op.eng and (op.eng == "pe" or not (raw or STRICT_SAME_ENGINE)):
                    continue
                d.sig = True
        cnt = self.cnt
        for op in self.ops:
            if op.chan is None and op.sig:
                cnt[op.eng] += 1
                op.signo = cnt[op.eng]
        sems = self.sems
        for e in self.ENGS:
            n = cnt[e] // SEM_MAX + 1
            while len(sems[e]) < n:
                sems[e].append(self.stack.enter_context(nc.semaphore(f"s_{e}{len(sems[e])}")))
        per = {e: [o for o in self.ops if o.eng == e] for e in self.ENGS}
        handles = {"pe": "tensor", "act": "scalar", "dve": "vector", "pool": "gpsimd", "sp": "sync"}

        def run(e, eng):
            waited = self.waited[e]
            for op in per[e]:
                for d, raw in op.deps.items():
                    if d.chan is not None:
                        key, val, sem = ("c", id(d.chan)), d.cval, d.chan.sem
                    else:
                        if d.eng == e and (e == "pe" or not (raw or STRICT_SAME_ENGINE)):
                            continue
                        j = (d.signo - 1) // SEM_MAX
                        key, val, sem = (d.eng, j), d.signo - j * SEM_MAX, sems[d.eng][j]
                    if waited.get(key, 0) >= val:
                        continue
                    waited[key] = val
                    eng.wait_ge(sem, val)
                ins = op.fn(eng)
                if op.chan is not None:
                    ins.then_inc(op.chan.sem, 16)
                elif op.sig:
                    j = (op.signo - 1) // SEM_MAX
                    ins.then_inc(sems[e][j], 1)

        with nc.Block() as block:
            for e in self.ENGS:
                if per[e]:
                    getattr(block, handles[e])(lambda eng, e=e: run(e, eng))
        for op in self.ops:
            op.done = True
            op.fn = None
            op.deps = {}
        self.ops = []
        self.last_on = {e: None for e in self.ENGS}
        for c in self.phase_chans:
            c.last = None
        self.free_chans.extend(self.phase_chans)
        self.phase_chans = []


def _act(out, in_, func, **kw):
    return lambda e: e.activation(out, in_, func, **kw)


def _ts(out, in0, s1, s2, op0, op1=None):
    if op1 is None:
        return lambda e: e.tensor_scalar(out, in0, s1, None, op0)
    return lambda e: e.tensor_scalar(out, in0, s1, s2, op0, op1)


def _stt(out, in0, sc, in1, op0, op1):
    return lambda e: e.scalar_tensor_tensor(out, in0, sc, in1, op0, op1)


def _tt(out, in0, in1, op):
    return lambda e: e.tensor_tensor(out, in0, in1, op)


def _cp(out, in_):
    return lambda e: e.tensor_copy(out, in_)


def _mm(out, lhsT, rhs, start, stop):
    return lambda e: e.matmul(out, lhsT, rhs, start=start, stop=stop)


def _tr(out, in_, ident):
    return lambda e: e.transpose(out, in_, ident)


def _dma(out, in_, **kw):
    return lambda e: e.dma_start(out=out, in_=in_, **kw)


def _memset(ap, v):
    return lambda e: e.memset(ap, v)


class K:
    def __init__(self, S, dbg=()):
        self.S = S
        self.dbg = dbg
        self.nc = bass.Bass("TRN2", target_bir_lowering=False)
        self.stack = contextlib.ExitStack()
        self.sc = Sched(self.nc, self.stack)

    def sb(self, st, name, shape, dt=F32):
        return st.enter_context(self.nc.sbuf_tensor(name, list(shape), dt))

    def ps(self, st, name, shape, dt=F32):
        return st.enter_context(self.nc.psum_tensor(name, list(shape), dt))

    def dram_in(self, name, shape, dt=F32):
        return self.nc.dram_tensor(name, list(shape), dt, kind="ExternalInput").ap()

    def dram_out(self, name, shape, dt=F32):
        return self.nc.dram_tensor(name, list(shape), dt, kind="ExternalOutput").ap()

    def dram_tmp(self, name, shape, dt=F32):
        if name in self.dbg:
            return self.nc.dram_tensor(name, list(shape), dt, kind="ExternalOutput").ap()
        return self.nc.dram_tensor(name, list(shape), dt).ap()


class Slot:
    def __init__(self, t, chan=None):
        self.t = t
        self.b = Buf()
        self.c = chan


def build(S, dbg=()):
    k = K(S, dbg)
    nc, sc = k.nc, k.sc
    NT, NB = S // 512, S // 128
    top = k.stack

    x = k.dram_in("x", [S, D])
    p_in = k.dram_in("p", [S, 256])
    ln_mix_g = k.dram_in("ln_mix_g", [D])
    w_in = k.dram_in("w_in", [D, IN_COLS])
    b_gates = k.dram_in("b_gates", [16])
    conv_qk_w = k.dram_in("conv_qk_w", [3, 1024])
    conv_qk_b = k.dram_in("conv_qk_b", [1024])
    mlstm_norm_g = k.dram_in("mlstm_norm_g", [512])
    q_norm_g = k.dram_in("q_norm_g", [256])
    w_uq = k.dram_in("w_uq", [256, 768])
    kv_norm_g = k.dram_in("kv_norm_g", [128])
    w_ukv = k.dram_in("w_ukv", [128, 1024])
    w_out = k.dram_in("w_out", [1024, 1024])
    ln_ffn_g = k.dram_in("ln_ffn_g", [D])
    w_up = k.dram_in("w_up", [D, 2 * D_FF])
    conv_ffn_w = k.dram_in("conv_ffn_w", [3, 2 * D_FF])
    conv_ffn_b = k.dram_in("conv_ffn_b", [2 * D_FF])
    w_down = k.dram_in("w_down", [D_FF, D])
    ple_norm_g = k.dram_in("ple_norm_g", [D])
    w_ple_gate = k.dram_in("w_ple_gate", [D, D])
    w_ple_proj = k.dram_in("w_ple_proj", [256, D])
    ple_post_g = k.dram_in("ple_post_g", [D])
    final_g = k.dram_in("final_g", [D])
    out = k.dram_out("out", [S, D])

    qkT = k.dram_tmp("qkT", [1024, S], BF16)
    vo_s = k.dram_tmp("vo_s", [S, 512])
    so_s = k.dram_tmp("so_s", [S, 512], BF16)
    g3_s = k.dram_tmp("g3_s", [S, 464])

    ident = k.sb(top, "ident", [128, 128], BF16)
    ones_f = k.sb(top, "ones_f", [128, 128], F32)
    mhalf = k.sb(top, "mhalf", [128, 16], F32)
    cb = Buf("consts")
    sc.add("pool", _memset(ones_f[:], 1.0), w=[cb])
    sc.add("pool", _memset(mhalf[:], -0.5), w=[cb])
    sc.add("pool", lambda e: e.affine_select(ident[:], ones_f[:], [[-1, 128]], ALU.is_equal, 0.0,
                                             base=0, channel_multiplier=1), r=[cb], w=[cb])
    sc.emit()

    ld_ch = [sc.chan(f"ldw{i}", keep=True) for i in range(2)]
    cnt = {"w": 0, "e": 0}

    def alt():
        cnt["e"] += 1
        return "dve" if cnt["e"] % 2 else "act"

    def scale_cast(eng, out_ap, in_ap, sc_ap=None):
        if eng == "act":
            if sc_ap is None:
                return _act(out_ap, in_ap, AF.Copy)
            return _act(out_ap, in_ap, AF.Copy, scale=sc_ap)
        if sc_ap is None:
            return _cp(out_ap, in_ap)
        return _ts(out_ap, in_ap, sc_ap, None, ALU.mult)

    def load_cols(st, name, src, G):
        t = k.sb(st, name, [128, G])
        b = Buf(name)
        ch = sc.chan("c_" + name)
        sc.add("sp", _dma(t[:], src.rearrange("(g p) -> p g", p=128), allow_slow_non_contiguous=True),
               w=[b], chan=ch)
        return t, b

    def load_bcast(st, name, src, n):
        t = k.sb(st, name, [128, n])
        b = Buf(name)
        ch = sc.chan("c_" + name)
        sc.add("sp", _dma(t[:], bass.AP(src.tensor, src.offset, [[0, 128], [1, n]])), w=[b], chan=ch)
        return t, b

    def load_w(st, stage, name, src, Kdim, cols, gain=None):
        kcn = Kdim // 128
        t = k.sb(st, name, [128, kcn, cols], BF16)
        b = Buf(name)
        for kc in range(kcn):
            sw = stage[0].t.shape[1]
            for c0 in range(0, cols, sw):
                w = min(sw, cols - c0)
                s = stage[cnt["w"] % 2]
                cnt["w"] += 1
                sc.add("sp", _dma(s.t[:, 0:w], src[kc * 128:(kc + 1) * 128, c0:c0 + w]), w=[s.b], chan=s.c)
                g = None if gain is None else gain[0][:, kc:kc + 1]
                rr = [s.b] + ([] if gain is None else [gain[1]])
                e = alt()
                sc.add(e, scale_cast(e, t[:, kc, c0:c0 + w], s.t[:, 0:w], g), r=rr, w=[b])
        return t, b

    with contextlib.ExitStack() as st:
        stage = [Slot(k.sb(st, f"wstage{i}", [128, 2816]), ld_ch[i]) for i in range(2)]
        gmix = load_cols(st, "gmix", ln_mix_g, 8)
        Win, Winb = load_w(st, stage, "Win", w_in, D, IN_COLS, gmix)
        cw = k.sb(st, "cw", [128, 8, 3])
        cwb = Buf("cw")
        for tap in range(3):
            sc.add("sp", _dma(cw[:, :, tap], conv_qk_w[tap].rearrange("(g p) -> p g", p=128),
                              allow_slow_non_contiguous=True), w=[Buf()], chan=sc.chan(f"c_cw{tap}"))
        cbias = load_cols(st, "cbias", conv_qk_b, 8)
        bg = load_bcast(st, "bg", b_gates, 16)
        sc.barrier()

        XT = [Slot(k.sb(st, f"xt{i}", [128, 4, D]), sc.chan(f"c_xt{i}")) for i in range(3)]
        XBA = [Slot(k.sb(st, f"xb{i}", [128, 4, D], BF16)) for i in range(2)]
        XN = [Slot(k.sb(st, f"xn{i}", [128, 8, 512], BF16)) for i in range(2)]
        junk = Slot(k.sb(st, "junk", [128, D], BF16))
        ss = Slot(k.sb(st, "ss", [128, 4]))
        ss2 = Slot(k.sb(st, "ss2", [128, 4]))
        rstd = Slot(k.sb(st, "rstd", [128, 4]))
        PRE = [Slot(k.sb(st, f"pre{g}", [128, 514])) for g in range(8)]
        ACC = [Slot(k.sb(st, f"acc{i}", [128, 512])) for i in range(2)]
        QKB = [Slot(k.sb(st, f"qkb{i}", [128, 512], BF16), sc.chan(f"c_qkb{i}")) for i in range(3)]
        VO = [Slot(k.sb(st, f"vo{i}", [128, 1024]), sc.chan(f"c_vo{i}")) for i in range(2)]
        G3 = [Slot(k.sb(st, f"g3{i}", [128, 464]), sc.chan(f"c_g3{i}")) for i in range(2)]
        SOB = [Slot(k.sb(st, f"sob{i}", [128, 512], BF16), sc.chan(f"c_sob{i}")) for i in range(2)]
        TB = [Slot(k.ps(st, f"tb{i}", [128, 1024], BF16)) for i in range(2)]
        MB = [Slot(k.ps(st, f"mb{i}", [128, 512])) for i in range(6)]
        last8 = Slot(k.sb(st, "last8", [128, 8]))
        last8b = Slot(k.sb(st, "last8b", [128, 8], BF16), sc.chan("c_last8"))
        for g in range(8):
            sc.add("pool", _memset(PRE[g].t[:, 0:2], 0.0), w=[PRE[g].b])
        cntA = {"mbi": 0, "qi": 0, "voi": 0}

        def stage1a(i):
            t0 = i * 512
            xt, XB = XT[i % 3], XBA[i % 2]
            sc.add("sp", _dma(xt.t[:], x[t0:t0 + 512, :].rearrange("(b p) d -> p b d", p=128)), w=[xt.b], chan=xt.c)
            for b in range(4):
                sc.add("act", _act(junk.t[:], xt.t[:, b, :], AF.Square, scale=1.0 / 32.0, accum_out=ss.t[:, b:b + 1]),
                       r=[xt.b], w=[junk.b, ss.b])
            sc.add("dve", _ts(ss2.t[:], ss.t[:], EPS, None, ALU.add), r=[ss.b], w=[ss2.b])
            sc.add("pool", _tt(rstd.t[:], ss2.t[:], mhalf[:, 0:4], ALU.pow), r=[ss2.b], w=[rstd.b])
            for b in range(4):
                e = "dve" if b % 2 else "act"
                sc.add(e, scale_cast(e, XB.t[:, b, :], xt.t[:, b, :], rstd.t[:, b:b + 1]), r=[xt.b, rstd.b], w=[XB.b])

        def stage1b(i):
            xn, XB = XN[i % 2], XBA[i % 2]
            for j in range(4):
                tb = TB[j % 2]
                for kk in range(2):
                    kc = 2 * j + kk
                    for b in range(4):
                        sc.add("pe", _tr(tb.t[:, kk * 512 + b * 128:kk * 512 + (b + 1) * 128],
                                         XB.t[:, b, kc * 128:(kc + 1) * 128], ident[:]), r=[XB.b], w=[tb.b])
                e = "dve" if j % 2 else "act"
                sc.add(e, scale_cast(e, xn.t[:, 2 * j:2 * j + 2, :], tb.t[:].rearrange("p (a b) -> p a b", a=2)),
                       r=[tb.b], w=[xn.b])

        def stage2fm(i):
            t0 = i * 512
            xn = XN[i % 2]
            mbi, qi, voi = cntA["mbi"], cntA["qi"], cntA["voi"]
            def tail(g, qi):
                pre = PRE[g]
                acc = ACC[g % 2]
                qb = QKB[qi % 3]
                sc.add("dve", _ts(acc.t[:], pre.t[:, 2:514], cw[:, g, 2:3], None, ALU.mult), r=[pre.b], w=[acc.b])
                sc.add("dve", _stt(acc.t[:], pre.t[:, 1:513], cw[:, g, 1:2], acc.t[:], ALU.mult, ALU.add),
                       r=[pre.b, acc.b], w=[acc.b])
                sc.add("dve", _stt(acc.t[:], pre.t[:, 0:512], cw[:, g, 0:1], acc.t[:], ALU.mult, ALU.add),
                       r=[pre.b, acc.b], w=[acc.b])
                sc.add("act", _act(qb.t[:], acc.t[:], AF.Silu, bias=cbias[0][:, g:g + 1]), r=[acc.b], w=[qb.b])
                if i == 0:
                    sc.add("pool", _dma(qkT[g * 128:(g + 1) * 128, 0:511], qb.t[:, 1:512]), r=[qb.b], chan=qb.c)
                else:
                    sc.add("pool", _dma(qkT[g * 128:(g + 1) * 128, t0 - 1:t0 + 511], qb.t[:]), r=[qb.b], chan=qb.c)
                sc.add("dve", _cp(pre.t[:, 0:2], pre.t[:, 512:514]), r=[pre.b], w=[pre.b])

            for g in range(8):
                pm = MB[mbi % 6]
                mbi += 1
                for kc in range(8):
                    sc.add("pe", _mm(pm.t[:], Win[:, kc, g * 128:(g + 1) * 128], xn.t[:, kc, :], kc == 0, kc == 7),
                           r=[xn.b, Winb], w=[pm.b])
                sc.add("act", _act(PRE[g].t[:, 2:514], pm.t[:], AF.Copy), r=[pm.b], w=[PRE[g].b])
                if g >= 1:
                    tail(g - 1, qi)
                    qi += 1
            tail(7, qi)
            qi += 1
            cntA["mbi"], cntA["qi"], cntA["voi"] = mbi, qi, voi

        def stage2tm(i):
            t0 = i * 512
            xn = XN[i % 2]
            mbi, qi, voi = cntA["mbi"], cntA["qi"], cntA["voi"]
            for b in range(4):
                vo = VO[voi % 2]
                g3 = G3[voi % 2]
                sob = SOB[voi % 2]
                voi += 1
                for part, (c0, c1) in enumerate(((1024, 1536), (1536, 2048), (2048, 2512))):
                    pm = MB[mbi % 6]
                    mbi += 1
                    for kc in range(8):
                        sc.add("pe", _mm(pm.t[:, 0:c1 - c0], xn.t[:, kc, b * 128:(b + 1) * 128], Win[:, kc, c0:c1],
                                         kc == 0, kc == 7), r=[xn.b, Winb], w=[pm.b])
                    if part == 0:
                        sc.add("dve", _cp(vo.t[:, 0:512], pm.t[:]), r=[pm.b], w=[vo.b])
                    elif part == 1:
                        sc.add("act", _act(vo.t[:, 512:1024], pm.t[:], AF.Tanh, scale=0.5), r=[pm.b], w=[vo.b])
                        sc.add("dve", _ts(sob.t[:], vo.t[:, 512:1024], 0.5, 0.5, ALU.mult, ALU.add),
                               r=[vo.b], w=[sob.b])
                    else:
                        sc.add("act", _act(g3.t[:], pm.t[:, 0:464], AF.Copy), r=[pm.b], w=[g3.b])
                        sc.add("dve", _tt(g3.t[:, 0:16], g3.t[:, 0:16], bg[0][:], ALU.add), r=[g3.b], w=[g3.b])
                r0 = t0 + b * 128
                sc.add("pool", _dma(vo_s[r0:r0 + 128, :], vo.t[:, 0:512]), r=[vo.b], chan=vo.c)
                sc.add("pool", _dma(so_s[r0:r0 + 128, :], sob.t[:]), r=[sob.b], chan=sob.c)
                sc.add("pool", _dma(g3_s[r0:r0 + 128, :], g3.t[:]), r=[g3.b], chan=g3.c)
            cntA["mbi"], cntA["qi"], cntA["voi"] = mbi, qi, voi

        stage1a(0)
        stage1b(0)
        if NT > 1:
            stage1a(1)
        for i in range(NT):
            stage2fm(i)
            if i + 1 < NT:
                stage1b(i + 1)
            if i + 2 < NT:
                stage1a(i + 2)
            stage2tm(i)
        prb = [PRE[g].b for g in range(8)]
        for g in range(8):
            sc.add("dve", _ts(last8.t[:, g:g + 1], PRE[g].t[:, 0:1], cw[:, g, 0:1], None, ALU.mult), r=[PRE[g].b], w=[last8.b])
            sc.add("dve", _stt(last8.t[:, g:g + 1], PRE[g].t[:, 1:2], cw[:, g, 1:2], last8.t[:, g:g + 1], ALU.mult, ALU.add),
                   r=[PRE[g].b, last8.b], w=[last8.b])
        sc.add("dve", _tt(last8.t[:], last8.t[:], cbias[0][:], ALU.add), r=[last8.b], w=[last8.b])
        sc.add("act", _act(last8b.t[:], last8.t[:], AF.Silu), r=[last8.b], w=[last8b.b])
        sc.add("pool", _dma(qkT.rearrange("(g p) s -> p g s", p=128)[:, :, S - 1], last8b.t[:],
                            allow_slow_non_contiguous=True), r=[last8b.b], chan=last8b.c)
        sc.emit()

    hf_s = k.dram_tmp("hf_s", [S, 512], BF16)
    hb_s = k.dram_tmp("hb_s", [S, 512], BF16)
    yT_s = k.dram_tmp("yT_s", [1024, S], BF16)

    def bc(ap, m):
        a = [list(d) for d in ap.ap]
        return bass.AP(ap.tensor, ap.offset, a + [[0, m]])

    def bc_mid(ap, m):
        a = [list(d) for d in ap.ap]
        return bass.AP(ap.tensor, ap.offset, [a[0], [0, m]] + a[1:])

    with contextlib.ExitStack() as st:
        maskF = k.sb(st, "maskF", [128, 128])
        maskB = k.sb(st, "maskB", [128, 128])
        mb_ = Buf("masks")
        sc.add("pool", lambda e: e.affine_select(maskF[:], ones_f[:], [[1, 128]], ALU.is_ge, 0.0,
                                                 base=0, channel_multiplier=-1), w=[mb_])
        sc.add("pool", lambda e: e.affine_select(maskB[:], ones_f[:], [[-1, 128]], ALU.is_ge, 0.0,
                                                 base=0, channel_multiplier=1), w=[mb_])
        hdst = (hf_s, hb_s)
        tris = (maskF, maskB)

        def ring(name, shape, dt=F32, chan=False, n=2):
            return [[Slot(k.sb(st, f"{name}{d}_{i}", shape, dt), sc.chan(f"c_{name}{d}_{i}") if chan else None) for i in range(n)]
                    for d in range(2)]

        SL = ring("sl", [128, 8, 512], BF16, True)
        VOT = ring("vot", [128, 512], F32, True)
        GT = ring("gt", [128, 16], F32, True)
        HO = ring("ho", [128, 512], BF16, True)
        AA = ring("aa", [128, 4])
        EG = ring("eg", [128, 8])
        V1 = ring("v1", [128, 4, 130], BF16)
        KTOK = ring("ktok", [128, 4, 128], BF16)
        MM = ring("mm", [128, 4, 128], BF16)
        e1 = [Slot(k.sb(st, f"e1_{d}", [128, 4])) for d in range(2)]
        lsp = [Slot(k.sb(st, f"lsp_{d}", [128, 4])) for d in range(2)]
        tmpa = [Slot(k.sb(st, f"tmpa_{d}", [128, 4])) for d in range(2)]
        C1 = [Slot(k.sb(st, f"c1_{d}", [128, 4, 130])) for d in range(2)]
        C1b = [Slot(k.sb(st, f"c1b_{d}", [128, 4, 130], BF16)) for d in range(2)]
        tmpC = [Slot(k.sb(st, f"tmpc_{d}", [128, 4, 130])) for d in range(2)]
        den = [Slot(k.sb(st, f"den_{d}", [128, 4])) for d in range(2)]
        rr_ = [Slot(k.sb(st, f"rr_{d}", [128, 4])) for d in range(2)]
        TBK = Slot(k.ps(st, "tbk", [128, 1024], BF16))
        SPS = [Slot(k.ps(st, f"sps{i}", [128, 512])) for i in range(2)]
        UPS = Slot(k.ps(st, "ups", [128, 1024]))
        DPS = Slot(k.ps(st, "dps", [128, 1024]))
        GPS = Slot(k.ps(st, "gps", [128, 512]))
        qkT3 = qkT.rearrange("(g p) s -> p g s", p=128)
        slab = [{"cg": None, "n": 0} for _ in range(2)]

        def pre(c, d, n):
            cg = c // 4
            if slab[d]["cg"] != cg:
                slab[d]["cg"] = cg
                slab[d]["n"] += 1
                sl = SL[d][slab[d]["n"] % 2]
                sc.add("sp", _dma(sl.t[:], qkT3[:, :, cg * 512:(cg + 1) * 512]), w=[sl.b], chan=sl.c)
            sl = SL[d][slab[d]["n"] % 2]
            vot, gt, aa, eg, v1, ktok, mm = (VOT[d][n % 2], GT[d][n % 2], AA[d][n % 2], EG[d][n % 2], V1[d][n % 2],
                                             KTOK[d][n % 2], MM[d][n % 2])
            sps = SPS[d]
            r0 = c * 128
            sc.add("sp", _dma(vot.t[:], vo_s[r0:r0 + 128, :]), w=[vot.b], chan=vot.c)
            sc.add("sp", _dma(gt.t[:], g3_s[r0:r0 + 128, 0:16]), w=[gt.b], chan=gt.c)
            io, fo = d * 8, d * 8 + 4
            tri = tris[d]
            sc.add("act", _act(e1[d].t[:], gt.t[:, fo:fo + 4], AF.Exp, scale=-1.0), r=[gt.b], w=[e1[d].b])
            sc.add("act", _act(lsp[d].t[:], e1[d].t[:], AF.Ln, bias=1.0), r=[e1[d].b], w=[lsp[d].b])
            gcol = d * 8
            sc.add("pe", _mm(GPS.t[:, gcol:gcol + 4], tri[:], lsp[d].t[:], True, True), r=[lsp[d].b, mb_], w=[GPS.b])
            sc.add("pe", _mm(GPS.t[:, gcol + 4:gcol + 8], ones_f[:], lsp[d].t[:], True, True), r=[lsp[d].b], w=[GPS.b])
            sc.add("dve", _tt(tmpa[d].t[:], gt.t[:, io:io + 4], GPS.t[:, gcol:gcol + 4], ALU.add), r=[gt.b, GPS.b], w=[tmpa[d].b])
            sc.add("act", _act(aa.t[:], tmpa[d].t[:], AF.Exp), r=[tmpa[d].b], w=[aa.b])
            sc.add("act", _act(eg.t[:], GPS.t[:, gcol:gcol + 8], AF.Exp, scale=-1.0), r=[GPS.b], w=[eg.b])
            sc.add("dve", _tt(v1.t[:, :, 0:128], vot.t[:].rearrange("p (h d) -> p h d", h=4), bc(aa.t[:], 128), ALU.mult),
                   r=[vot.b, aa.b], w=[v1.b])
            sc.add("dve", _cp(v1.t[:, :, 128], aa.t[:]), r=[aa.b], w=[v1.b])
            c4 = (c % 4) * 128
            tcol = d * 512
            for h in range(4):
                sc.add("pe", _tr(TBK.t[:, tcol + h * 128:tcol + (h + 1) * 128], sl.t[:, 4 + h, c4:c4 + 128], ident[:]), r=[sl.b], w=[TBK.b])
            sc.add("act", _act(ktok.t[:], TBK.t[:, tcol:tcol + 512].rearrange("p (h d) -> p h d", h=4), AF.Copy, scale=128.0 ** -0.5),
                   r=[TBK.b], w=[ktok.b])
            for h in range(4):
                sc.add("pe", _mm(sps.t[:, h * 128:(h + 1) * 128], sl.t[:, 4 + h, c4:c4 + 128], sl.t[:, h, c4:c4 + 128], True, True),
                       r=[sl.b], w=[sps.b])
            sc.add("dve", _stt(mm.t[:], sps.t[:].rearrange("p (h d) -> p h d", h=4), 128.0 ** -0.5, bc_mid(tri[:], 4), ALU.mult, ALU.mult),
                   r=[sps.b, mb_], w=[mm.b])
            return sl, c4

        def main(c, d, n, sl, c4):
            eg, v1, ktok, mm, ho = EG[d][n % 2], V1[d][n % 2], KTOK[d][n % 2], MM[d][n % 2], HO[d][n % 2]
            c1, c1b, tc_, dn, rr = C1[d], C1b[d], tmpC[d], den[d], rr_[d]
            r0 = c * 128
            U3 = UPS.t[:].rearrange("p (h d) -> p h d", h=4)
            D3 = DPS.t[:].rearrange("p (h d) -> p h d", h=4)
            for h in range(4):
                sc.add("pe", _mm(DPS.t[:, h * 256:h * 256 + 129], ktok.t[:, h, :], v1.t[:, h, 0:129], True, True), r=[ktok.b, v1.b], w=[DPS.b])
            for h in range(4):
                sc.add("pe", _mm(UPS.t[:, h * 256:h * 256 + 129], mm.t[:, h, :], v1.t[:, h, 0:129], True, False), r=[mm.b, v1.b], w=[UPS.b])
                sc.add("pe", _mm(UPS.t[:, h * 256:h * 256 + 129], sl.t[:, h, c4:c4 + 128], c1b.t[:, h, 0:129], False, True),
                       r=[sl.b, c1b.b], w=[UPS.b])
            sc.add("dve", _tt(c1.t[:, :, 0:129], D3[:, :, 0:129], tc_.t[:, :, 0:129], ALU.add), r=[DPS.b, tc_.b], w=[c1.b])
            sc.add("dve", _tt(tc_.t[:, :, 0:129], c1.t[:, :, 0:129], bc(eg.t[:, 4:8], 129), ALU.mult), r=[c1.b, eg.b], w=[tc_.b])
            sc.add("act", _act(c1b.t[:, :, 0:129], tc_.t[:, :, 0:129], AF.Copy), r=[tc_.b], w=[c1b.b])
            sc.add("dve", _tt(dn.t[:], U3[:, :, 128], eg.t[:, 0:4], ALU.mult), r=[UPS.b, eg.b], w=[dn.b])
            sc.add("act", _act(dn.t[:], dn.t[:], AF.Abs), r=[dn.b], w=[dn.b])
            sc.add("dve", _ts(dn.t[:], dn.t[:], 1.0, None, ALU.max), r=[dn.b], w=[dn.b])
            sc.add("dve", lambda e: e.reciprocal(rr.t[:], dn.t[:]), r=[dn.b], w=[rr.b])
            sc.add("dve", _tt(rr.t[:], rr.t[:], eg.t[:, 0:4], ALU.mult), r=[rr.b, eg.b], w=[rr.b])
            sc.add("dve", _tt(ho.t[:].rearrange("p (h d) -> p h d", h=4), U3[:, :, 0:128], bc(rr.t[:], 128), ALU.mult),
                   r=[UPS.b, rr.b], w=[ho.b])
            sc.add("pool", _dma(hdst[d][r0:r0 + 128, :], ho.t[:]), r=[ho.b], chan=ho.c)

        orders = (list(range(NB)), list(range(NB - 1, -1, -1)))
        nxt = [None, None]
        for d in range(2):
            sc.add("pool", _memset(C1[d].t[:], 0.0), w=[C1[d].b])
            sc.add("pool", _memset(tmpC[d].t[:], 0.0), w=[tmpC[d].b])
            sc.add("pool", _memset(C1b[d].t[:], 0.0), w=[C1b[d].b])
            nxt[d] = pre(orders[d][0], d, 0)
        for j in range(NB):
            cur = list(nxt)
            if j + 1 < NB:
                for d in range(2):
                    nxt[d] = pre(orders[d][j + 1], d, j + 1)
            for d in range(2):
                main(orders[d][j], d, j, *cur[d])
        sc.emit()

    KT_s = k.dram_tmp("KT_s", [512, S], BF16)
    KR_s = k.dram_tmp("KR_s", [64, S], BF16)
    V_s = k.dram_tmp("V_s", [S, 512], BF16)
    QN_s = k.dram_tmp("QN_s", [512, S], BF16)
    QR_s = k.dram_tmp("QR_s", [4, 65, S], BF16)
    kmax_s = k.dram_tmp("kmax_s", [128, 8])
    TWO_PI = 6.283185307179586

    with contextlib.ExitStack() as st:
        stage = [Slot(k.sb(st, f"wstagec{i}", [128, 1024]), ld_ch[i]) for i in range(2)]
        gq = load_cols(st, "gq", q_norm_g, 2)
        gkv = load_cols(st, "gkv", kv_norm_g, 1)
        Wuq, Wuqb = load_w(st, stage, "Wuq", w_uq, 256, 768, gq)
        Wkv, Wkvb = load_w(st, stage, "Wkv", w_ukv, 128, 1024, gkv)
        Wkv4 = Wkv[:, 0, :].rearrange("p (h t d) -> p h t d", h=4, t=2)
        cos2 = k.sb(st, "cos2", [128, NB, 64])
        sin1 = k.sb(st, "sin1", [128, NB, 32])
        tb_ = Buf("ropetab")
        pos = k.sb(st, "pos", [128, NB])
        invf = k.sb(st, "invf", [128, 32])
        ang = k.sb(st, "ang", [128, NB, 32])
        angi = k.sb(st, "angi", [128, NB, 32], mybir.dt.int32)
        angf = k.sb(st, "angf", [128, NB, 32])
        msk = k.sb(st, "msk", [128, NB, 32])
        sc.add("pool", lambda e: e.iota(pos[:], [[128, NB]], base=0, channel_multiplier=1, allow_small_or_imprecise_dtypes=True), w=[tb_])
        sc.add("pool", lambda e: e.iota(invf[:], [[1, 32]], base=0, channel_multiplier=0, allow_small_or_imprecise_dtypes=True), r=[tb_], w=[tb_])
        sc.add("act", _act(invf[:], invf[:], AF.Exp, scale=-float(np.log(10000.0)) / 32.0), r=[tb_], w=[tb_])
        sc.add("dve", _tt(ang[:], bc(pos[:], 32), bc_mid(invf[:], NB), ALU.mult), r=[tb_], w=[tb_])
        sc.add("dve", _ts(ang[:], ang[:], 1.0 / TWO_PI, None, ALU.mult), r=[tb_], w=[tb_])
        for which in range(2):
            if which == 1:
                sc.add("dve", _ts(ang[:], ang[:], 0.25, None, ALU.add), r=[tb_], w=[tb_])
            sc.add("dve", _cp(angi[:], ang[:]), r=[tb_], w=[tb_])
            sc.add("dve", _cp(angf[:], angi[:]), r=[tb_], w=[tb_])
            sc.add("dve", _tt(angf[:], ang[:], angf[:], ALU.subtract), r=[tb_], w=[tb_])
            sc.add("dve", _ts(msk[:], angf[:], 0.5, None, ALU.is_gt), r=[tb_], w=[tb_])
            sc.add("dve", _tt(angf[:], angf[:], msk[:], ALU.subtract), r=[tb_], w=[tb_])
            sc.add("dve", _ts(msk[:], angf[:], -0.5, None, ALU.is_lt), r=[tb_], w=[tb_])
            sc.add("dve", _tt(angf[:], angf[:], msk[:], ALU.add), r=[tb_], w=[tb_])
            if which == 0:
                sc.add("act", _act(sin1[:], angf[:], AF.Sin, scale=TWO_PI * (1.0 - 1e-6)), r=[tb_], w=[tb_])
            else:
                sc.add("act", _act(cos2[:, :, 0:32], angf[:], AF.Sin, scale=TWO_PI * (1.0 - 1e-6)), r=[tb_], w=[tb_])
                sc.add("act", _act(cos2[:, :, 32:64], angf[:], AF.Sin, scale=TWO_PI * (1.0 - 1e-6)), r=[tb_], w=[tb_])
        sc.barrier()

        G3T = [Slot(k.sb(st, f"g3t{i}", [128, 4, 464]), sc.chan(f"c_g3t{i}")) for i in range(2)]
        junkc = Slot(k.sb(st, "junkc", [128, 3072], BF16))
        ssq = Slot(k.sb(st, "ssq", [128, 8]))
        ssq2 = Slot(k.sb(st, "ssq2", [128, 8]))
        rst = Slot(k.sb(st, "rst", [128, 8]))
        cqn = Slot(k.sb(st, "cqn", [128, 4, 256], BF16))
        ckvn = Slot(k.sb(st, "ckvn", [128, 4, 128], BF16))
        tA = Slot(k.sb(st, "tA", [128, 4, 64]))
        tB = Slot(k.sb(st, "tB", [128, 4, 64]))
        krb = Slot(k.sb(st, "krb", [128, 4, 64], BF16))
        sqr = Slot(k.sb(st, "sqr", [128, 4, 64]))
        KR2 = [Slot(k.sb(st, f"kr2_{i}", [128, 4])) for i in range(2)]
        CQT = [Slot(k.sb(st, f"cqT{i}", [128, 2, 512], BF16)) for i in range(2)]
        CKVT = [Slot(k.sb(st, f"ckvT{i}", [128, 512], BF16)) for i in range(2)]
        krT = Slot(k.sb(st, "krT", [64, 512], BF16), sc.chan("c_krT"))
        KTS = [Slot(k.sb(st, f"kts{i}", [128, 512], BF16), sc.chan(f"c_kts{i}")) for i in range(2)]
        VS = [Slot(k.sb(st, f"vs{i}", [128, 512], BF16), sc.chan(f"c_vs{i}")) for i in range(2)]
        sqk = Slot(k.sb(st, "sqk", [128, 512]))
        kn2 = Slot(k.sb(st, "kn2", [128, 4, 4]))
        kmax = Slot(k.sb(st, "kmax", [128, 4]))
        kmt = Slot(k.sb(st, "kmt", [128, 4]))
        q_sb = Slot(k.sb(st, "q_sb", [128, 4, 768]))
        qtA = Slot(k.sb(st, "qtA", [128, 4, 4, 64]))
        qtB = Slot(k.sb(st, "qtB", [128, 4, 4, 64]))
        qbn = Slot(k.sb(st, "qbn", [128, 4, 4, 128], BF16))
        qbr = Slot(k.sb(st, "qbr", [128, 4, 4, 66], BF16))
        qn2 = Slot(k.sb(st, "qn2", [128, 16]))
        qn1 = Slot(k.sb(st, "qn1", [128, 16]))
        QS = [Slot(k.sb(st, f"qs{i}", [128, 2, 512], BF16), sc.chan(f"c_qs{i}")) for i in range(2)]
        QRS = [Slot(k.sb(st, f"qrs{i}", [65, 2, 512], BF16), sc.chan(f"c_qrs{i}")) for i in range(2)]
        PB = [Slot(k.ps(st, f"pb{i}", [128, 512])) for i in range(8)]
        pbi = {"i": 0}

        def bank():
            pbi["i"] += 1
            return PB[pbi["i"] % 8]

        def bfv(slot):
            return slot.t[:].bitcast(BF16)

        sc.add("pool", _memset(kmax.t[:], 0.0), w=[kmax.b])
        qmx = Slot(k.sb(st, "qmx", [128, 4]))
        qmt = Slot(k.sb(st, "qmt", [128, 4]))
        sc.add("pool", _memset(qmx.t[:], 0.0), w=[qmx.b])
        sc.add("pool", _memset(qbr.t[:], 0.0), w=[qbr.b])
        q4 = q_sb.t[:].rearrange("p b (h d) -> p b h d", h=4)
        cntC = {"kti": 0}

        def c_x(i):
            t0 = i * 512
            g3 = G3T[i % 2]
            kr2, cqT, ckvT = KR2[i % 2], CQT[i % 2], CKVT[i % 2]
            kti = cntC["kti"]
            sc.add("sp", _dma(g3.t[:], g3_s[t0:t0 + 512, :].rearrange("(b p) c -> p b c", p=128)), w=[g3.b], chan=g3.c)
            for b in range(4):
                sc.add("act", _act(junkc.t[:, 0:256], g3.t[:, b, 16:272], AF.Square, scale=1.0 / 16.0, accum_out=ssq.t[:, b:b + 1]),
                       r=[g3.b], w=[junkc.b, ssq.b])
                sc.add("act", _act(junkc.t[:, 0:128], g3.t[:, b, 272:400], AF.Square, scale=128.0 ** -0.5, accum_out=ssq.t[:, 4 + b:5 + b]),
                       r=[g3.b], w=[junkc.b, ssq.b])
            sc.add("dve", _ts(ssq2.t[:], ssq.t[:], EPS, None, ALU.add), r=[ssq.b], w=[ssq2.b])
            sc.add("pool", _tt(rst.t[:], ssq2.t[:], mhalf[:, 0:8], ALU.pow), r=[ssq2.b], w=[rst.b])
            sc.add("dve", _tt(cqn.t[:], g3.t[:, :, 16:272], bc(rst.t[:, 0:4], 256), ALU.mult), r=[g3.b, rst.b], w=[cqn.b])
            sc.add("dve", _tt(ckvn.t[:], g3.t[:, :, 272:400], bc(rst.t[:, 4:8], 128), ALU.mult), r=[g3.b, rst.b], w=[ckvn.b])
            xk = g3.t[:, :, 400:464]
            cs, sn = cos2[:, 4 * i:4 * i + 4, :], sin1[:, 4 * i:4 * i + 4, :]
            sc.add("dve", _tt(tA.t[:], xk, cs, ALU.mult), r=[g3.b], w=[tA.b])
            sc.add("dve", _tt(tB.t[:, :, 0:32], g3.t[:, :, 432:464], sn, ALU.mult), r=[g3.b], w=[tB.b])
            sc.add("dve", _tt(tB.t[:, :, 32:64], g3.t[:, :, 400:432], sn, ALU.mult), r=[g3.b], w=[tB.b])
            sc.add("dve", _tt(krb.t[:, :, 0:32], tA.t[:, :, 0:32], tB.t[:, :, 0:32], ALU.subtract), r=[tA.b, tB.b], w=[krb.b])
            sc.add("dve", _tt(krb.t[:, :, 32:64], tA.t[:, :, 32:64], tB.t[:, :, 32:64], ALU.add), r=[tA.b, tB.b], w=[krb.b])
            sc.add("act", _act(sqr.t[:], xk, AF.Square), r=[g3.b], w=[sqr.b])
            sc.add("dve", lambda e: e.tensor_reduce(kr2.t[:], sqr.t[:], AX.X, ALU.add), r=[sqr.b], w=[kr2.b])
            pa, pb2 = bank(), bank()
            for b in range(4):
                for kc in range(2):
                    sc.add("pe", _tr(bfv(pa)[:, kc * 512 + b * 128:kc * 512 + (b + 1) * 128], cqn.t[:, b, kc * 128:(kc + 1) * 128], ident[:]),
                           r=[cqn.b], w=[pa.b])
                sc.add("pe", _tr(bfv(pb2)[:, b * 128:(b + 1) * 128], ckvn.t[:, b, :], ident[:]), r=[ckvn.b], w=[pb2.b])
                sc.add("pe", _tr(bfv(pb2)[0:64, 512 + b * 128:512 + (b + 1) * 128], krb.t[:, b, :], ident[:]), r=[krb.b], w=[pb2.b])
            sc.add("act", _act(cqT.t[:], bfv(pa).rearrange("p (a b) -> p a b", a=2), AF.Copy), r=[pa.b], w=[cqT.b])
            sc.add("dve", _cp(ckvT.t[:], bfv(pb2)[:, 0:512]), r=[pb2.b], w=[ckvT.b])
            sc.add("act", _act(krT.t[:], bfv(pb2)[0:64, 512:1024], AF.Copy), r=[pb2.b], w=[krT.b])
            sc.add("pool", _dma(KR_s[:, t0:t0 + 512], krT.t[:]), r=[krT.b], chan=krT.c)
            for h in range(4):
                pk = bank()
                sc.add("pe", _mm(pk.t[:], Wkv[:, 0, h * 256:h * 256 + 128], ckvT.t[:], True, True), r=[ckvT.b, Wkvb], w=[pk.b])
                kts = KTS[kti % 2]
                kti += 1
                e = "act" if h % 2 else "dve"
                sc.add(e, scale_cast(e, kts.t[:], pk.t[:]), r=[pk.b], w=[kts.b])
                sc.add("pool", _dma(KT_s[h * 128:(h + 1) * 128, t0:t0 + 512], kts.t[:]), r=[kts.b], chan=kts.c)
            cntC["kti"] = kti

        def c_y(i):
            t0 = i * 512
            kr2, cqT, ckvT = KR2[i % 2], CQT[i % 2], CKVT[i % 2]
            cs, sn = cos2[:, 4 * i:4 * i + 4, :], sin1[:, 4 * i:4 * i + 4, :]
            for b in range(4):
                tok = slice(b * 128, (b + 1) * 128)
                pv = bank()
                sc.add("pe", _mm(pv.t[:].rearrange("p (h d) -> p h d", h=4), ckvT.t[:, tok], Wkv4[:, :, 1, :], True, True),
                       r=[ckvT.b, Wkvb], w=[pv.b])
                vs = VS[b % 2]
                sc.add("act", _act(vs.t[:], pv.t[:], AF.Copy), r=[pv.b], w=[vs.b])
                sc.add("pool", _dma(V_s[t0 + b * 128:t0 + (b + 1) * 128, :], vs.t[:]), r=[vs.b], chan=vs.c)
                pk = bank()
                sc.add("pe", _mm(pk.t[:].rearrange("p (h d) -> p h d", h=4), ckvT.t[:, tok], Wkv4[:, :, 0, :], True, True),
                       r=[ckvT.b, Wkvb], w=[pk.b])
                sc.add("act", _act(sqk.t[:], pk.t[:], AF.Square), r=[pk.b], w=[sqk.b])
                sc.add("dve", lambda e, b=b: e.tensor_reduce(kn2.t[:, b, :], sqk.t[:].rearrange("p (h d) -> p h d", h=4), AX.X, ALU.add),
                       r=[sqk.b], w=[kn2.b])
                pq0, pq1 = bank(), bank()
                for kc in range(2):
                    sc.add("pe", _mm(pq0.t[:], cqT.t[:, kc, tok], Wuq[:, kc, 0:512], kc == 0, kc == 1), r=[cqT.b, Wuqb], w=[pq0.b])
                for kc in range(2):
                    sc.add("pe", _mm(pq1.t[:, 0:256], cqT.t[:, kc, tok], Wuq[:, kc, 512:768], kc == 0, kc == 1), r=[cqT.b, Wuqb], w=[pq1.b])
                sc.add("act", _act(q_sb.t[:, b, 0:512], pq0.t[:], AF.Copy), r=[pq0.b], w=[q_sb.b])
                sc.add("dve", _cp(q_sb.t[:, b, 512:768], pq1.t[:, 0:256]), r=[pq1.b], w=[q_sb.b])
            sc.add("dve", _tt(kn2.t[:], kn2.t[:], bc(kr2.t[:], 4), ALU.add), r=[kn2.b, kr2.b], w=[kn2.b])
            sc.add("dve", lambda e: e.tensor_reduce(kmt.t[:], kn2.t[:].rearrange("p b h -> p h b"), AX.X, ALU.max), r=[kn2.b], w=[kmt.b])
            sc.add("dve", _tt(kmax.t[:], kmax.t[:], kmt.t[:], ALU.max), r=[kmax.b, kmt.b], w=[kmax.b])
            cs4 = bass.AP(cs.tensor, cs.offset, [list(cs.ap[0]), list(cs.ap[1]), [0, 4], list(cs.ap[2])])
            sn4 = bass.AP(sn.tensor, sn.offset, [list(sn.ap[0]), list(sn.ap[1]), [0, 4], list(sn.ap[2])])
            sc.add("dve", _tt(qtA.t[:], q4[:, :, :, 128:192], cs4, ALU.mult), r=[q_sb.b], w=[qtA.b])
            sc.add("dve", _tt(qtB.t[:, :, :, 0:32], q4[:, :, :, 160:192], sn4, ALU.mult), r=[q_sb.b], w=[qtB.b])
            sc.add("dve", _tt(qtB.t[:, :, :, 32:64], q4[:, :, :, 128:160], sn4, ALU.mult), r=[q_sb.b], w=[qtB.b])
            sc.add("dve", _tt(qbr.t[:, :, :, 0:32], qtA.t[:, :, :, 0:32], qtB.t[:, :, :, 0:32], ALU.subtract), r=[qtA.b, qtB.b], w=[qbr.b])
            sc.add("dve", _tt(qbr.t[:, :, :, 32:64], qtA.t[:, :, :, 32:64], qtB.t[:, :, :, 32:64], ALU.add), r=[qtA.b, qtB.b], w=[qbr.b])
            sc.add("dve", _cp(qbn.t[:], q4[:, :, :, 0:128]), r=[q_sb.b], w=[qbn.b])
            sc.add("act", _act(junkc.t[:], q_sb.t[:].rearrange("p b c -> p (b c)"), AF.Square), r=[q_sb.b], w=[junkc.b])
            sc.add("dve", lambda e: e.tensor_reduce(qn2.t[:], junkc.t[:].rearrange("p (g d) -> p g d", g=16), AX.X, ALU.add),
                   r=[junkc.b], w=[qn2.b])
            sc.add("dve", lambda e: e.tensor_reduce(qmt.t[:], qn2.t[:].rearrange("p (b h) -> p h b", b=4), AX.X, ALU.max), r=[qn2.b], w=[qmt.b])
            sc.add("dve", _tt(qmx.t[:], qmx.t[:], qmt.t[:], ALU.max), r=[qmx.b, qmt.b], w=[qmx.b])
            sc.add("pool", _tt(qn1.t[:], qn2.t[:], mhalf[:, 0:16], ALU.pow), r=[qn2.b], w=[qn1.b])
            sc.add("dve", _tt(qn1.t[:], qn1.t[:], qn2.t[:], ALU.mult), r=[qn1.b, qn2.b], w=[qn1.b])
            sc.add("dve", _ts(qbr.t[:, :, :, 64], qn1.t[:].rearrange("p (b h) -> p b h", b=4), -1.01, None, ALU.mult),
                   r=[qn1.b], w=[qbr.b])
            for hp in range(2):
                pn, pr = bank(), bank()
                for hh in range(2):
                    h = 2 * hp + hh
                    for b in range(4):
                        sc.add("pe", _tr(bfv(pn)[:, hh * 512 + b * 128:hh * 512 + (b + 1) * 128], qbn.t[:, b, h, :], ident[:]), r=[qbn.b], w=[pn.b])
                        sc.add("pe", _tr(bfv(pr)[0:65, hh * 512 + b * 128:hh * 512 + (b + 1) * 128], qbr.t[:, b, h, 0:65], ident[:]), r=[qbr.b], w=[pr.b])
                qs, qrs = QS[hp], QRS[hp]
                sc.add("act", _act(qs.t[:], bfv(pn).rearrange("p (a b) -> p a b", a=2), AF.Copy), r=[pn.b], w=[qs.b])
                sc.add("dve", _cp(qrs.t[:], bfv(pr)[0:65, :].rearrange("p (a b) -> p a b", a=2)), r=[pr.b], w=[qrs.b])
                sc.add("pool", _dma(QN_s.rearrange("(h p) s -> p h s", p=128)[:, 2 * hp:2 * hp + 2, t0:t0 + 512], qs.t[:]), r=[qs.b], chan=qs.c)
                sc.add("pool", _dma(QR_s.rearrange("h p s -> p h s")[:, 2 * hp:2 * hp + 2, t0:t0 + 512], qrs.t[:]), r=[qrs.b], chan=qrs.c)
        c_x(0)
        for i in range(NT):
            if i + 1 < NT:
                c_x(i + 1)
            c_y(i)
        kmo = Slot(k.sb(st, "kmo", [128, 8]), sc.chan("c_kmo"))
        sc.add("dve", _cp(kmo.t[:, 0:4], kmax.t[:]), r=[kmax.b], w=[kmo.b])
        sc.add("dve", _cp(kmo.t[:, 4:8], qmx.t[:]), r=[qmx.b], w=[kmo.b])
        sc.add("pool", _dma(kmax_s[:, :], kmo.t[:]), r=[kmo.b], chan=kmo.c)
        sc.emit()

    with contextlib.ExitStack() as st:
        KT = Slot(k.sb(st, "KT", [128, 4, S], BF16), sc.chan("c_KT"))
        KR = Slot(k.sb(st, "KR", [128, S], BF16), sc.chan("c_KR"))
        VR = Slot(k.sb(st, "VR", [128, NB, 512], BF16), sc.chan("c_VR"))
        kml = Slot(k.sb(st, "kml", [128, 8]), sc.chan("c_kml"))
        km1 = Slot(k.sb(st, "km1", [1, 8]))
        kmx = Slot(k.sb(st, "kmx", [128, 8]))
        cbias_ = Slot(k.sb(st, "cbias_", [128, 4]))
        phalf = Slot(k.sb(st, "phalf", [128, 4]))
        SPB = [Slot(k.ps(st, f"spb{i}", [128, 512])) for i in range(4)]
        OPB = [Slot(k.ps(st, f"opb{i}", [128, 512])) for i in range(2)]
        RSB = Slot(k.ps(st, "rsb", [128, 512]))
        RS = [Slot(k.ps(st, f"rs{i}", [128, 512])) for i in range(1)]
        NPT = 10
        PT = [Slot(k.sb(st, f"pt{i}", [128, 512], BF16)) for i in range(NPT)]
        rs_sb = Slot(k.sb(st, "rs_sb", [128, 512]))
        ones_b = Slot(k.sb(st, "ones_b", [128, 32], BF16))
        inv32 = Slot(k.sb(st, "inv32", [128, 128]))
        sc.add("pool", _memset(ones_b.t[:], 1.0), w=[ones_b.b])
        sc.add("pool", _memset(inv32.t[:], 1.0 / 32.0), w=[inv32.b])
        QN = [Slot(k.sb(st, f"qnt{i}", [128, 512], BF16), sc.chan(f"c_qn{i}")) for i in range(2)]
        QR = [Slot(k.sb(st, f"qrt{i}", [128, 512], BF16), sc.chan(f"c_qr{i}")) for i in range(2)]
        rinv = Slot(k.sb(st, "rinv", [128, 512]))
        YO = [Slot(k.sb(st, f"yo{i}", [128, 512], BF16), sc.chan(f"c_yo{i}")) for i in range(2)]
        sc.add("sp", _dma(KT.t[:], KT_s.rearrange("(h p) s -> p h s", p=128)), w=[KT.b], chan=KT.c)
        sc.add("sp", _dma(KR.t[0:64, :], KR_s[:, :]), w=[KR.b], chan=KR.c)
        sc.add("sp", _dma(KR.t[64:128, :], KR_s[:, :]), w=[KR.b], chan=sc.chan("c_KR2"))
        sc.add("sp", _dma(VR.t[:], V_s.rearrange("(c p) d -> p c d", p=128)), w=[VR.b], chan=VR.c)
        sc.add("sp", _dma(kml.t[:], kmax_s[:, :]), w=[kml.b], chan=kml.c)
        sc.add("pool", _memset(phalf.t[:], 0.5), w=[phalf.b])
        scale = 192.0 ** -0.5
        sc.add("pool", lambda e: e.tensor_reduce(km1.t[:], kml.t[:], AX.C, ALU.max), r=[kml.b], w=[km1.b])
        sc.add("pe", _mm(RSB.t[:, 0:8], ones_f[0:1, :], km1.t[:], True, True), r=[km1.b], w=[RSB.b])
        sc.add("dve", _cp(kmx.t[:], RSB.t[:, 0:8]), r=[RSB.b], w=[kmx.b])
        sc.add("dve", _tt(cbias_.t[:], kmx.t[:, 0:4], kmx.t[:, 4:8], ALU.mult), r=[kmx.b], w=[cbias_.b])
        sc.add("pool", _tt(cbias_.t[:], cbias_.t[:], phalf.t[:], ALU.pow), r=[cbias_.b, phalf.b], w=[cbias_.b])
        sc.add("dve", _ts(cbias_.t[:], cbias_.t[:], -1.01 * scale, None, ALU.mult), r=[cbias_.b], w=[cbias_.b])
        if "cb_dbg" in k.dbg:
            cb_dbg = k.dram_tmp("cb_dbg", [128, 12])
            dbt = Slot(k.sb(st, "dbt", [128, 12]), sc.chan("c_dbt"))
            sc.add("dve", _cp(dbt.t[:, 0:4], cbias_.t[:]), r=[cbias_.b], w=[dbt.b])
            sc.add("dve", _cp(dbt.t[:, 4:12], kmx.t[:]), r=[kmx.b], w=[dbt.b])
            sc.add("pool", _dma(cb_dbg[:, :], dbt.t[:]), r=[dbt.b], chan=dbt.c)
        it = 0
        pti = 0
        for h in range(4):
            for j in range(NT):
                qn, qr = QN[it % 2], QR[it % 2]
                opb, yo, rs = OPB[it % 2], YO[it % 2], RS[0]
                it += 1
                sc.add("sp", _dma(qn.t[:], QN_s[h * 128:(h + 1) * 128, j * 512:(j + 1) * 512]), w=[qn.b], chan=qn.c)
                sc.add("sp", _dma(qr.t[0:64, :], QR_s[h, 0:64, j * 512:(j + 1) * 512]), w=[qr.b], chan=qr.c)
                sc.add("sp", _dma(qr.t[64:128, :], QR_s[h, 0:64, j * 512:(j + 1) * 512]), w=[qr.b], chan=qr.c)

                def qk2(kc):
                    for r_ in range(2):
                        sp_ = SPB[(kc + r_) % 4]
                        ks = slice((kc + r_) * 128, (kc + r_ + 1) * 128)
                        sc.add("pe", _mm(sp_.t[:], KT.t[:, h, ks], qn.t[:], True, False), r=[KT.b, qn.b], w=[sp_.b])
                    for r_ in range(2):
                        sp_ = SPB[(kc + r_) % 4]
                        ks = slice((kc + r_) * 128, (kc + r_ + 1) * 128)
                        rows = slice(64 * r_, 64 * r_ + 64)
                        sc.add("pe", lambda e, sp_=sp_, ks=ks, rows=rows, r_=r_, qr=qr: e.matmul(sp_.t[:], KR.t[rows, ks], qr.t[rows, :], start=False, stop=True,
                                                                                             tile_position=(64 * r_, 0)),
                               r=[KR.b, qr.b], w=[sp_.b])

                qk2(0)
                grp = []
                for kc in range(NB):
                    if kc % 2 == 0 and kc + 2 < NB:
                        qk2(kc + 2)
                    sp_ = SPB[kc % 4]
                    pt = PT[pti % NPT]
                    pti += 1
                    sc.add("act", _act(pt.t[:], sp_.t[:], AF.Exp, scale=scale, bias=cbias_.t[:, h:h + 1]), r=[sp_.b, cbias_.b], w=[pt.b])
                    sc.add("pe", _mm(opb.t[:], VR.t[:, kc, h * 128:(h + 1) * 128], pt.t[:], kc == 0, kc == NB - 1),
                           r=[VR.b, pt.b], w=[opb.b])
                    grp.append(pt)
                    if len(grp) == 4:
                        for r_, ptr in enumerate(grp):
                            sc.add("pe", lambda e, r_=r_, ptr=ptr, kc=kc: e.matmul(rs.t[32 * r_:32 * r_ + 32, :], ones_b.t[:, 0:32], ptr.t[:],
                                                                               start=(kc == 3), stop=(kc == NB - 1),
                                                                               tile_position=(0, 32 * r_)),
                                   r=[ptr.b, ones_b.b], w=[rs.b])
                        grp = []
                sc.add("dve", _cp(rs_sb.t[:], rs.t[:]), r=[rs.b], w=[rs_sb.b])
                sc.add("pe", _mm(RSB.t[:], inv32.t[:], rs_sb.t[:], True, True), r=[rs_sb.b, inv32.b], w=[RSB.b])
                sc.add("dve", lambda e: e.reciprocal(rinv.t[:], RSB.t[:]), r=[RSB.b], w=[rinv.b])
                sc.add("dve", _tt(yo.t[:], opb.t[:], rinv.t[:], ALU.mult), r=[opb.b, rinv.b], w=[yo.b])
                sc.add("pool", _dma(yT_s[512 + h * 128:512 + (h + 1) * 128, j * 512:(j + 1) * 512], yo.t[:]), r=[yo.b], chan=yo.c)
        sc.emit()

    h1_s = k.dram_tmp("h1_s", [S, D])
    xn2T_s = k.dram_tmp("xn2T_s", [D, S + 2], BF16)
    h2_s = k.dram_tmp("h2_s", [S, D])
    yT3 = yT_s.rearrange("(g p) s -> p g s", p=128)
    xn2T3 = xn2T_s.rearrange("(g p) s -> p g s", p=128)

    def rms_transpose(xt, ss, ss2, rstd, junk, XB, xn, TBs, gain_scale=1.0 / 32.0):
        for b in range(4):
            sc.add("act", _act(junk.t[:], xt.t[:, b, :], AF.Square, scale=gain_scale, accum_out=ss.t[:, b:b + 1]),
                   r=[xt.b], w=[junk.b, ss.b])
        sc.add("dve", _ts(ss2.t[:], ss.t[:], EPS, None, ALU.add), r=[ss.b], w=[ss2.b])
        sc.add("pool", _tt(rstd.t[:], ss2.t[:], mhalf[:, 0:4], ALU.pow), r=[ss2.b], w=[rstd.b])
        for b in range(4):
            e = "dve" if b % 2 else "act"
            sc.add(e, scale_cast(e, XB.t[:, b, :], xt.t[:, b, :], rstd.t[:, b:b + 1]), r=[xt.b, rstd.b], w=[XB.b])
        for j in range(4):
            tb = TBs[j % 2]
            for kk in range(2):
                kc = 2 * j + kk
                for b in range(4):
                    sc.add("pe", _tr(tb.t[:, kk * 512 + b * 128:kk * 512 + (b + 1) * 128],
                                     XB.t[:, b, kc * 128:(kc + 1) * 128], ident[:]), r=[XB.b], w=[tb.b])
            e = "dve" if j % 2 else "act"
            sc.add(e, scale_cast(e, xn.t[:, 2 * j:2 * j + 2, :], tb.t[:].rearrange("p (a b) -> p a b", a=2)),
                   r=[tb.b], w=[xn.b])

    with contextlib.ExitStack() as st:
        stage = [Slot(k.sb(st, f"wstaged{i}", [128, 1024]), ld_ch[i]) for i in range(2)]
        Wout, Woutb = load_w(st, stage, "Wout", w_out, D, D)
        normg = load_bcast(st, "normg", mlstm_norm_g, 512)
        zt = Slot(k.sb(st, "zt", [128, 8, 2], BF16), sc.chan("c_zt"))
        sc.add("pool", _memset(zt.t[:], 0.0), w=[zt.b])
        sc.add("pool", _dma(xn2T3[:, :, 0:1], zt.t[:, :, 0:1], allow_slow_non_contiguous=True), r=[zt.b], chan=zt.c)
        sc.add("pool", _dma(xn2T3[:, :, S + 1:S + 2], zt.t[:, :, 1:2], allow_slow_non_contiguous=True), r=[zt.b], chan=zt.c)
        HFT = [Slot(k.sb(st, f"hft{i}", [128, 4, 512], BF16), sc.chan(f"c_hft{i}")) for i in range(2)]
        HBT = [Slot(k.sb(st, f"hbt{i}", [128, 4, 512], BF16), sc.chan(f"c_hbt{i}")) for i in range(2)]
        SOT = [Slot(k.sb(st, f"sot{i}", [128, 4, 512], BF16), sc.chan(f"c_sot{i}")) for i in range(2)]
        HS = Slot(k.sb(st, "hsd", [128, 4, 512]))
        SG = Slot(k.sb(st, "sgd", [128, 4, 512]))
        sqd = Slot(k.sb(st, "sqd", [128, 4, 512], BF16))
        ssn = Slot(k.sb(st, "ssnd", [128, 16]))
        rsn = Slot(k.sb(st, "rsnd", [128, 16]))
        YB = [Slot(k.sb(st, f"ybd{i}", [128, 4, 512], BF16)) for i in range(2)]
        YAT = [Slot(k.sb(st, f"yat{i}", [128, 4, 512], BF16)) for i in range(2)]
        YTT = [Slot(k.sb(st, f"ytt{i}", [128, 4, 512], BF16), sc.chan(f"c_ytt{i}")) for i in range(3)]
        XT = [Slot(k.sb(st, f"xtd{i}", [128, 4, D]), sc.chan(f"c_xtd{i}")) for i in range(3)]
        XB = Slot(k.sb(st, "xbd", [128, 4, D], BF16))
        XN = [Slot(k.sb(st, f"xnd{i}", [128, 8, 512], BF16), sc.chan(f"c_xnd{i}")) for i in range(2)]
        junk = Slot(k.sb(st, "junkd", [128, D], BF16))
        ss = Slot(k.sb(st, "ssd", [128, 4]))
        ss2 = Slot(k.sb(st, "ss2d", [128, 4]))
        rstd = Slot(k.sb(st, "rstdd", [128, 4]))
        TBs = [Slot(k.ps(st, f"tbd{i}", [128, 1024], BF16)) for i in range(2)]
        MB = [Slot(k.ps(st, f"mbd{i}", [128, 512])) for i in range(6)]
        cntD = {"mbi": 0}

        def loads(i):
            t0 = i * 512
            hft, hbt, sot, ytt = HFT[i % 2], HBT[i % 2], SOT[i % 2], YTT[i % 3]
            tv = lambda ap: ap[t0:t0 + 512, :].rearrange("(b p) d -> p b d", p=128)
            sc.add("sp", _dma(hft.t[:], tv(hf_s)), w=[hft.b], chan=hft.c)
            sc.add("sp", _dma(hbt.t[:], tv(hb_s)), w=[hbt.b], chan=hbt.c)
            sc.add("sp", _dma(sot.t[:], tv(so_s)), w=[sot.b], chan=sot.c)
            sc.add("sp", _dma(ytt.t[:], yT3[:, 4:8, t0:t0 + 512]), w=[ytt.b], chan=ytt.c)

        def loadx(i):
            t0 = i * 512
            xt = XT[i % 3]
            sc.add("sp", _dma(xt.t[:], x[t0:t0 + 512, :].rearrange("(b p) d -> p b d", p=128)), w=[xt.b], chan=xt.c)

        def comb1(i):
            hft, hbt, sot = HFT[i % 2], HBT[i % 2], SOT[i % 2]
            sc.add("dve", _tt(HS.t[:], hft.t[:], hbt.t[:], ALU.add), r=[hft.b, hbt.b], w=[HS.b])
            sc.add("act", _act(sqd.t[:], HS.t[:], AF.Square, scale=128.0 ** -0.5), r=[HS.b], w=[sqd.b])
            sc.add("pool", _tt(SG.t[:], sot.t[:], bc_mid(normg[0][:], 4), ALU.mult), r=[sot.b, normg[1]], w=[SG.b])
            sc.add("dve", lambda e: e.tensor_reduce(ssn.t[:], sqd.t[:].rearrange("p b (h d) -> p (b h) d", h=4), AX.X, ALU.add),
                   r=[sqd.b], w=[ssn.b])
            sc.add("dve", _ts(ssn.t[:], ssn.t[:], EPS, None, ALU.add), r=[ssn.b], w=[ssn.b])
            sc.add("pool", _tt(rsn.t[:], ssn.t[:], mhalf[:, 0:16], ALU.pow), r=[ssn.b], w=[rsn.b])

        def comb2(i):
            yb = YB[i % 2]
            h16 = HS.t[:].rearrange("p b (h d) -> p (b h) d", h=4)
            sc.add("dve", _tt(h16, h16, bc(rsn.t[:], 128), ALU.mult), r=[HS.b, rsn.b], w=[HS.b])
            sc.add("dve", _tt(yb.t[:], HS.t[:], SG.t[:], ALU.mult), r=[HS.b, SG.b], w=[yb.b])

        def norm1(i):
            t0 = i * 512
            xt = XT[i % 3]
            sc.add("sp", _dma(h1_s[t0:t0 + 512, :].rearrange("(b p) d -> p b d", p=128), xt.t[:]), r=[xt.b], chan=xt.c)
            for b in range(4):
                sc.add("act", _act(junk.t[:], xt.t[:, b, :], AF.Square, scale=1.0 / 32.0, accum_out=ss.t[:, b:b + 1]),
                       r=[xt.b], w=[junk.b, ss.b])
            sc.add("dve", _ts(ss2.t[:], ss.t[:], EPS, None, ALU.add), r=[ss.b], w=[ss2.b])
            sc.add("pool", _tt(rstd.t[:], ss2.t[:], mhalf[:, 0:4], ALU.pow), r=[ss2.b], w=[rstd.b])

        def norm2(i):
            xt = XT[i % 3]
            for b in range(4):
                e = "dve" if b % 2 else "act"
                sc.add(e, scale_cast(e, XB.t[:, b, :], xt.t[:, b, :], rstd.t[:, b:b + 1]), r=[xt.b, rstd.b], w=[XB.b])

        def mmstage(i):
            yb, yat, ytt, xt = YB[i % 2], YAT[i % 2], YTT[i % 3], XT[i % 3]
            for hp in range(2):
                tb = TBs[hp]
                for hh in range(2):
                    h = 2 * hp + hh
                    for b in range(4):
                        sc.add("pe", _tr(tb.t[:, hh * 512 + b * 128:hh * 512 + (b + 1) * 128], yb.t[:, b, h * 128:(h + 1) * 128], ident[:]),
                               r=[yb.b], w=[tb.b])
                e = "dve" if hp else "act"
                sc.add(e, scale_cast(e, yat.t[:, 2 * hp:2 * hp + 2, :], tb.t[:].rearrange("p (a b) -> p a b", a=2)), r=[tb.b], w=[yat.b])
            mbi = cntD["mbi"]
            for b in range(4):
                for half in range(2):
                    pm = MB[mbi % 6]
                    mbi += 1
                    for kc in range(8):
                        src = yat if kc < 4 else ytt
                        sc.add("pe", _mm(pm.t[:], src.t[:, kc % 4, b * 128:(b + 1) * 128], Wout[:, kc, half * 512:(half + 1) * 512],
                                         kc == 0, kc == 7), r=[src.b, Woutb], w=[pm.b])
                    sc.add("dve", _tt(xt.t[:, b, half * 512:(half + 1) * 512], pm.t[:], xt.t[:, b, half * 512:(half + 1) * 512], ALU.add),
                           r=[pm.b, xt.b], w=[xt.b])
            cntD["mbi"] = mbi

        def trstage(i):
            t0 = i * 512
            xn = XN[i % 2]
            for j in range(4):
                tb = TBs[j % 2]
                for kk in range(2):
                    kc = 2 * j + kk
                    for b in range(4):
                        sc.add("pe", _tr(tb.t[:, kk * 512 + b * 128:kk * 512 + (b + 1) * 128],
                                         XB.t[:, b, kc * 128:(kc + 1) * 128], ident[:]), r=[XB.b], w=[tb.b])
                e = "dve" if j % 2 else "act"
                sc.add(e, scale_cast(e, xn.t[:, 2 * j:2 * j + 2, :], tb.t[:].rearrange("p (a b) -> p a b", a=2)),
                       r=[tb.b], w=[xn.b])
            sc.add("sp", _dma(xn2T3[:, :, 1 + t0:1 + t0 + 512], xn.t[:]), r=[xn.b], chan=xn.c)

        loads(0)
        loadx(0)
        if NT > 1:
            loads(1)
            loadx(1)
        comb1(0)
        comb2(0)
        for i in range(NT + 1):
            if i + 2 < NT:
                loads(i + 2)
            if i >= 1:
                norm1(i - 1)
            if i + 1 < NT:
                comb1(i + 1)
            if i < NT:
                mmstage(i)
            if i >= 1:
                norm2(i - 1)
                trstage(i - 1)
            if i + 1 < NT:
                comb2(i + 1)
            if i + 2 < NT:
                loadx(i + 2)
        sc.emit()

    TT_ = 256
    with contextlib.ExitStack() as st:
        stage = [Slot(k.sb(st, f"wstagee{i}", [128, 1408]), ld_ch[i]) for i in range(2)]
        gffn = load_cols(st, "gffn", ln_ffn_g, 8)
        Wup, Wupb = load_w(st, stage, "Wup", w_up, D, 2 * D_FF, gffn)
        Wdn, Wdnb = load_w(st, stage, "Wdn", w_down, D_FF, D)
        fw = k.sb(st, "fw", [128, 44, 3])
        fwb = Buf("fw")
        for tap in range(3):
            sc.add("sp", _dma(fw[:, :, tap], conv_ffn_w[tap].rearrange("(g p) -> p g", p=128),
                              allow_slow_non_contiguous=True), w=[Buf()], chan=sc.chan(f"c_fw{tap}"))
        fb = load_cols(st, "fb", conv_ffn_b, 44)
        sc.barrier()
        XS = [Slot(k.sb(st, f"xs{i}", [128, 8, TT_ + 2], BF16), sc.chan(f"c_xs{i}")) for i in range(2)]
        H1 = [Slot(k.sb(st, f"h1t{i}", [128, 2, D]), sc.chan(f"c_h1t{i}")) for i in range(2)]
        AT = [Slot(k.sb(st, f"at{i}", [128, 22, TT_], BF16)) for i in range(2)]
        CG = [Slot(k.sb(st, f"cg{i}", [128, TT_])) for i in range(2)]
        CV = [Slot(k.sb(st, f"cv{i}", [128, TT_])) for i in range(2)]
        SG = [Slot(k.sb(st, f"sg{i}", [128, TT_])) for i in range(2)]
        MB = [Slot(k.ps(st, f"mbe{i}", [128, 512])) for i in range(8)]
        cntE = {"mbi": 0}

        def down_group(pend, dj):
            pi, pt0, ph1, pat = pend
            b, half = divmod(dj, 2)
            mbi = cntE["mbi"]
            pm = MB[mbi % 8]
            cntE["mbi"] = mbi + 1
            for g in range(22):
                sc.add("pe", _mm(pm.t[:], pat.t[:, g, b * 128:(b + 1) * 128], Wdn[:, g, half * 512:(half + 1) * 512], g == 0, g == 21),
                       r=[pat.b, Wdnb], w=[pm.b])
            sc.add("dve", _tt(ph1.t[:, b, half * 512:(half + 1) * 512], pm.t[:], ph1.t[:, b, half * 512:(half + 1) * 512], ALU.add),
                   r=[pm.b, ph1.b], w=[ph1.b])

        def down_store(pend):
            pi, pt0, ph1, pat = pend
            sc.add("pool", _dma(h2_s[pt0:pt0 + TT_, :].rearrange("(b p) d -> p b d", p=128), ph1.t[:]), r=[ph1.b], chan=ph1.c)

        pending = None
        for i in range(S // TT_):
            t0 = i * TT_
            xs, h1, at = XS[i % 2], H1[i % 2], AT[i % 2]
            mbi = cntE["mbi"]
            sc.add("sp", _dma(xs.t[:], xn2T3[:, :, t0:t0 + TT_ + 2]), w=[xs.b], chan=xs.c)
            sc.add("sp", _dma(h1.t[:], h1_s[t0:t0 + TT_, :].rearrange("(b p) d -> p b d", p=128)), w=[h1.b], chan=h1.c)
            for g in range(22):
                res = []
                for which, (gi, dst) in enumerate(((g, CG[g % 2]), (22 + g, CV[g % 2]))):
                    pm = MB[mbi % 8]
                    mbi += 1
                    for kc in range(8):
                        sc.add("pe", _mm(pm.t[:, 0:TT_ + 2], Wup[:, kc, gi * 128:(gi + 1) * 128], xs.t[:, kc, :], kc == 0, kc == 7),
                               r=[xs.b, Wupb], w=[pm.b])
                    sc.add("act", _act(dst.t[:], pm.t[:, 0:TT_], AF.Identity, scale=fw[:, gi, 0:1], bias=fb[0][:, gi:gi + 1]),
                           r=[pm.b], w=[dst.b])
                    sc.add("dve", _stt(dst.t[:], pm.t[:, 1:TT_ + 1], fw[:, gi, 1:2], dst.t[:], ALU.mult, ALU.add), r=[pm.b, dst.b], w=[dst.b])
                    sc.add("dve", _stt(dst.t[:], pm.t[:, 2:TT_ + 2], fw[:, gi, 2:3], dst.t[:], ALU.mult, ALU.add), r=[pm.b, dst.b], w=[dst.b])
                cg, cv, sg = CG[g % 2], CV[g % 2], SG[g % 2]
                sc.add("act", _act(sg.t[:], cg.t[:], AF.Silu), r=[cg.b], w=[sg.b])
                sc.add("pool", _tt(at.t[:, g, :], sg.t[:], cv.t[:], ALU.mult), r=[sg.b, cv.b], w=[at.b])
                if pending is not None and g in (4, 9, 14, 19):
                    cntE["mbi"] = mbi
                    down_group(pending, (g - 4) // 5)
                    mbi = cntE["mbi"]
                    if g == 19:
                        down_store(pending)
            cntE["mbi"] = mbi
            pending = (i, t0, h1, at)
        for dj in range(4):
            down_group(pending, dj)
        down_store(pending)
        sc.emit()

    with contextlib.ExitStack() as st:
        stage = [Slot(k.sb(st, f"wstagef{i}", [128, 1024]), ld_ch[i]) for i in range(2)]
        gple = load_cols(st, "gple", ple_norm_g, 8)
        Wg, Wgb = load_w(st, stage, "Wg", w_ple_gate, D, D, gple)
        Wp, Wpb = load_w(st, stage, "Wp", w_ple_proj, 256, D)
        postg = load_bcast(st, "postg", ple_post_g, D)
        fing = load_bcast(st, "fing", final_g, D)
        sc.barrier()
        XT = [Slot(k.sb(st, f"xtf{i}", [128, 4, D]), sc.chan(f"c_xtf{i}")) for i in range(3)]
        PTL = [Slot(k.sb(st, f"ptl{i}", [128, 4, 256]), sc.chan(f"c_ptl{i}")) for i in range(2)]
        XBF = [Slot(k.sb(st, f"xbf{i}", [128, 4, D], BF16)) for i in range(2)]
        PBF = [Slot(k.sb(st, f"pbf{i}", [128, 4, 256], BF16)) for i in range(2)]
        XNF = [Slot(k.sb(st, f"xnf{i}", [128, 8, 512], BF16)) for i in range(2)]
        PTTF = [Slot(k.sb(st, f"ptt{i}", [128, 2, 512], BF16)) for i in range(2)]
        junk = Slot(k.sb(st, "junkf", [128, D], BF16))
        ss = Slot(k.sb(st, "ssf", [128, 4]))
        ss2 = Slot(k.sb(st, "ss2f", [128, 4]))
        rstd = Slot(k.sb(st, "rstdf", [128, 4]))
        ssb = Slot(k.sb(st, "ssb", [128, 2]))
        ssb2 = Slot(k.sb(st, "ssb2", [128, 2]))
        rsb2 = Slot(k.sb(st, "rsb2", [128, 2]))
        SGM = [Slot(k.sb(st, f"sgm{i}", [128, D])) for i in range(2)]
        PJ = [Slot(k.sb(st, f"pj{i}", [128, D])) for i in range(2)]
        OT = [Slot(k.sb(st, f"ot{i}", [128, D]), sc.chan(f"c_ot{i}")) for i in range(3)]
        TBs = [Slot(k.ps(st, f"tbf{i}", [128, 1024], BF16)) for i in range(2)]
        MB = [Slot(k.ps(st, f"mbf{i}", [128, 512])) for i in range(6)]
        cntF = {"mbi": 0, "bi": 0}

        def f_stage1a(i):
            t0 = i * 512
            xt, ptl, XB, PBf = XT[i % 3], PTL[i % 2], XBF[i % 2], PBF[i % 2]
            sc.add("sp", _dma(xt.t[:], h2_s[t0:t0 + 512, :].rearrange("(b p) d -> p b d", p=128)), w=[xt.b], chan=xt.c)
            sc.add("sp", _dma(ptl.t[:], p_in[t0:t0 + 512, :].rearrange("(b p) d -> p b d", p=128)), w=[ptl.b], chan=ptl.c)
            for b in range(4):
                sc.add("act", _act(junk.t[:], xt.t[:, b, :], AF.Square, scale=1.0 / 32.0, accum_out=ss.t[:, b:b + 1]),
                       r=[xt.b], w=[junk.b, ss.b])
            sc.add("dve", _ts(ss2.t[:], ss.t[:], EPS, None, ALU.add), r=[ss.b], w=[ss2.b])
            sc.add("pool", _tt(rstd.t[:], ss2.t[:], mhalf[:, 0:4], ALU.pow), r=[ss2.b], w=[rstd.b])
            for b in range(4):
                e = "dve" if b % 2 else "act"
                sc.add(e, scale_cast(e, XB.t[:, b, :], xt.t[:, b, :], rstd.t[:, b:b + 1]), r=[xt.b, rstd.b], w=[XB.b])
            sc.add("act", _act(PBf.t[:], ptl.t[:], AF.Copy), r=[ptl.b], w=[PBf.b])

        def f_stage1b(i):
            XN, PTT, XB, PBf = XNF[i % 2], PTTF[i % 2], XBF[i % 2], PBF[i % 2]
            for j in range(4):
                tb = TBs[j % 2]
                for kk in range(2):
                    kc = 2 * j + kk
                    for b in range(4):
                        sc.add("pe", _tr(tb.t[:, kk * 512 + b * 128:kk * 512 + (b + 1) * 128],
                                         XB.t[:, b, kc * 128:(kc + 1) * 128], ident[:]), r=[XB.b], w=[tb.b])
                e = "dve" if j % 2 else "act"
                sc.add(e, scale_cast(e, XN.t[:, 2 * j:2 * j + 2, :], tb.t[:].rearrange("p (a b) -> p a b", a=2)),
                       r=[tb.b], w=[XN.b])
            tb = TBs[0]
            for kc in range(2):
                for b in range(4):
                    sc.add("pe", _tr(tb.t[:, kc * 512 + b * 128:kc * 512 + (b + 1) * 128], PBf.t[:, b, kc * 128:(kc + 1) * 128], ident[:]),
                           r=[PBf.b], w=[tb.b])
            sc.add("act", _act(PTT.t[:], tb.t[:].rearrange("p (a b) -> p a b", a=2), AF.Copy), r=[tb.b], w=[PTT.b])

        RN = 4
        SGM3 = SGM + [Slot(k.sb(st, f"sgm{i}", [128, D])) for i in range(2, RN)]
        PJ3 = PJ + [Slot(k.sb(st, f"pj{i}", [128, D])) for i in range(2, RN)]
        SSB = [Slot(k.sb(st, f"ssbr{i}", [128, 2])) for i in range(RN)]
        RSB2 = [Slot(k.sb(st, f"rsbr{i}", [128, 2])) for i in range(RN)]
        junk2 = Slot(k.sb(st, "junkf2", [128, D], BF16))

        def blk(n):
            i, b = divmod(n, 4)
            return i, b, XT[i % 3], SGM3[n % RN], PJ3[n % RN], SSB[n % RN], RSB2[n % RN], OT[n % 3]

        def f_a12(n):
            i, b, xt, sgm, pj, ssb_, rsb_, ot = blk(n)
            XN, PTT = XNF[i % 2], PTTF[i % 2]
            mbi = cntF["mbi"]
            tok = slice(b * 128, (b + 1) * 128)
            for half in range(2):
                hs = slice(half * 512, (half + 1) * 512)
                pm = MB[mbi % 6]
                mbi += 1
                for kc in range(8):
                    sc.add("pe", _mm(pm.t[:], XN.t[:, kc, tok], Wg[:, kc, hs], kc == 0, kc == 7), r=[XN.b, Wgb], w=[pm.b])
                sc.add("act", _act(sgm.t[:, hs], pm.t[:], AF.Sigmoid), r=[pm.b], w=[sgm.b])
                pm = MB[mbi % 6]
                mbi += 1
                for kc in range(2):
                    sc.add("pe", _mm(pm.t[:], PTT.t[:, kc, tok], Wp[:, kc, hs], kc == 0, kc == 1), r=[PTT.b, Wpb], w=[pm.b])
                sc.add("act", _act(pj.t[:, hs], pm.t[:], AF.Copy), r=[pm.b], w=[pj.b])
            cntF["mbi"] = mbi
            sc.add("act", _act(junk.t[:], pj.t[:], AF.Square, scale=1.0 / 32.0, accum_out=ssb_.t[:, 0:1]), r=[pj.b], w=[junk.b, ssb_.b])

        def f_d12(n):
            i, b, xt, sgm, pj, ssb_, rsb_, ot = blk(n)
            sc.add("dve", _ts(ssb_.t[:, 0:1], ssb_.t[:, 0:1], EPS, None, ALU.add), r=[ssb_.b], w=[ssb_.b])
            sc.add("pool", _tt(rsb_.t[:, 0:1], ssb_.t[:, 0:1], mhalf[:, 0:1], ALU.pow), r=[ssb_.b], w=[rsb_.b])
            sc.add("dve", _stt(sgm.t[:], sgm.t[:], rsb_.t[:, 0:1], postg[0][:], ALU.mult, ALU.mult), r=[sgm.b, rsb_.b, postg[1]], w=[sgm.b])
            sc.add("dve", _tt(pj.t[:], pj.t[:], sgm.t[:], ALU.mult), r=[pj.b, sgm.b], w=[pj.b])
            sc.add("dve", _tt(pj.t[:], pj.t[:], xt.t[:, b, :], ALU.add), r=[pj.b, xt.b], w=[pj.b])

        def f_a3(n):
            i, b, xt, sgm, pj, ssb_, rsb_, ot = blk(n)
            sc.add("act", _act(junk2.t[:], pj.t[:], AF.Square, scale=1.0 / 32.0, accum_out=ssb_.t[:, 1:2]), r=[pj.b], w=[junk2.b, ssb_.b])

        def f_d3(n):
            i, b, xt, sgm, pj, ssb_, rsb_, ot = blk(n)
            t0 = i * 512
            sc.add("dve", _ts(ssb_.t[:, 1:2], ssb_.t[:, 1:2], EPS, None, ALU.add), r=[ssb_.b], w=[ssb_.b])
            sc.add("pool", _tt(rsb_.t[:, 1:2], ssb_.t[:, 1:2], mhalf[:, 0:1], ALU.pow), r=[ssb_.b], w=[rsb_.b])
            sc.add("dve", _stt(ot.t[:], pj.t[:], rsb_.t[:, 1:2], fing[0][:], ALU.mult, ALU.mult), r=[pj.b, rsb_.b, fing[1]], w=[ot.b])
            sc.add("sp", _dma(out[t0 + b * 128:t0 + (b + 1) * 128, :], ot.t[:]), r=[ot.b], chan=ot.c)

        f_stage1a(0)
        f_stage1b(0)
        if NT > 1:
            f_stage1a(1)
        NBLK = NT * 4
        for n in range(-2, NBLK + 1):
            if 0 <= n + 2 < NBLK:
                f_a12(n + 2)
                i2, b2 = divmod(n + 2, 4)
                if b2 == 1:
                    if i2 + 1 < NT:
                        f_stage1b(i2 + 1)
                    if i2 + 2 < NT:
                        f_stage1a(i2 + 2)
            if 0 <= n + 1 < NBLK:
                f_d12(n + 1)
            if 0 <= n < NBLK:
                f_a3(n)
            if 0 <= n - 1 < NBLK:
                f_d3(n - 1)
        sc.emit()

    k.final_wait = None
    return k


def finish(k):
    return k.nc


_W_NAMES = ["ln_mix_g", "w_in", "b_gates", "conv_qk_w", "conv_qk_b", "mlstm_norm_g", "q_norm_g", "w_uq", "kv_norm_g",
            "w_ukv", "w_out", "ln_ffn_g", "w_up", "conv_ffn_w", "conv_ffn_b", "w_down", "ple_norm_g", "w_ple_gate",
            "w_ple_proj", "ple_post_g"]


def kernel(**inputs):
    x = np.asarray(inputs["x"])
    p = np.asarray(inputs["p"])
    B, S, _ = x.shape
    nc = finish(build(S))
    shared = {n: np.ascontiguousarray(np.asarray(inputs[n])[0], dtype=np.float32) for n in _W_NAMES}
    shared["final_g"] = np.ascontiguousarray(np.asarray(inputs["final_g"]), dtype=np.float32)
    in_maps = []
    for b in range(B):
        m = dict(shared)
        m["x"] = np.ascontiguousarray(x[b], dtype=np.float32)
        m["p"] = np.ascontiguousarray(p[0, b], dtype=np.float32)
        in_maps.append(m)
    res = run_bass_kernel_spmd(nc, in_maps, core_ids=list(range(B)))
    return np.stack([np.asarray(r["out"]) for r in res.results], axis=0).astype(np.float32)
```
